# Optimizing a Trainium2 kernel written in Bass

```python
import math
import jax, jax.numpy as jnp
from jax import lax
import numpy as np

D_MODEL = 1024
BATCH = 8
SEQ = 2048
DEPTH = 2
DEC_BATCH = 128
DEC_SEQ = 4
PAST_LEN = 16384
PAGE_SIZE = 128

N_MIXERS = 2
N_RET = (DEPTH + 1) // 2
N_GDN = DEPTH // 2
RET_HEADS = 4
RET_DK = D_MODEL // RET_HEADS
RET_DV = 2 * RET_DK
RET_QK_DIM = RET_HEADS * RET_DK
RET_V_DIM = RET_HEADS * RET_DV
RET_IN = 2 * RET_QK_DIM + 2 * RET_V_DIM
RET_CHUNK = 64
ROPE_BASE = 10000.0
GDN_K_HEADS = D_MODEL // 128
GDN_V_HEADS = 2 * GDN_K_HEADS
GDN_DK = 128
GDN_DV = 128
GDN_KEY_DIM = GDN_K_HEADS * GDN_DK
GDN_VAL_DIM = GDN_V_HEADS * GDN_DV
GDN_CONV_DIM = 2 * GDN_KEY_DIM + GDN_VAL_DIM
GDN_CONV_W = 4
GDN_IN = GDN_CONV_DIM + GDN_VAL_DIM + 2 * GDN_V_HEADS
GDN_CHUNK = 64
D_FF = 2816
FFN_CONV_W = 3
EPS = 1e-6

kernel_name = 'hybrid_retention_gdn_convffn_adaln_step'


def ada_rmsnorm(x, g, shift, scale):
    xf = x.astype(jnp.float32)
    xn = xf * lax.rsqrt(jnp.mean(xf * xf, axis=-1, keepdims=True) + EPS) * g.astype(jnp.float32)
    return (xn * (1.0 + scale[:, None, :]) + shift[:, None, :]).astype(x.dtype)


def causal_dwconv(x, buf, w):
    W = w.shape[0]
    T = x.shape[1]
    xp = jnp.concatenate([buf.astype(x.dtype), x], axis=1)
    out = sum(xp[:, i:i + T, :] * w[i].astype(x.dtype) for i in range(W))
    return out, xp[:, T:, :]


def rope(x, pos0):
    T, d = x.shape[2], x.shape[3]
    half = d // 2
    inv_freq = ROPE_BASE ** (-jnp.arange(half, dtype=jnp.float32) / half)
    pos = pos0 + jnp.arange(T, dtype=jnp.float32)
    ang = pos[:, None] * inv_freq[None, :]
    cos, sin = jnp.cos(ang), jnp.sin(ang)
    x1, x2 = x[..., :half], x[..., half:]
    return jnp.concatenate([x1 * cos - x2 * sin, x1 * sin + x2 * cos], axis=-1)


def l2norm(x):
    return x * lax.rsqrt(jnp.sum(x * x, axis=-1, keepdims=True) + EPS)


def to_chunks(a, N, C):
    B, H = a.shape[0], a.shape[1]
    return jnp.moveaxis(a.reshape((B, H, N, C) + a.shape[3:]), 2, 0)


def from_chunks(o):
    N, B, H, C, d = o.shape
    return jnp.moveaxis(o, 0, 2).reshape(B, H, N * C, d)


def retention_chunked(q, k, v, s0):
    B, H, T, dk = q.shape
    C = math.gcd(T, RET_CHUNK)
    N = T // C
    log_gamma = jnp.log(1.0 - 2.0 ** (-5.0 - jnp.arange(H, dtype=jnp.float32)))
    idx = jnp.arange(C, dtype=jnp.float32)
    diff = idx[:, None] - idx[None, :]
    causal = diff >= 0
    decay = jnp.where(causal[None], jnp.exp(jnp.where(causal, diff, 0.0)[None] * log_gamma[:, None, None]), 0.0)
    q_decay = jnp.exp((idx + 1.0)[None, :] * log_gamma[:, None])[None, :, :, None]
    k_decay = jnp.exp((C - 1.0 - idx)[None, :] * log_gamma[:, None])[None, :, :, None]
    chunk_decay = jnp.exp(C * log_gamma)[None, :, None, None]

    def step(S, inp):
        qc, kc, vc = inp
        intra = jnp.einsum('bhid,bhjd->bhij', qc, kc) * decay[None]
        o = jnp.einsum('bhij,bhjv->bhiv', intra, vc) + jnp.einsum('bhid,bhdv->bhiv', qc, S) * q_decay
        S = S * chunk_decay + jnp.einsum('bhjd,bhjv->bhdv', kc * k_decay, vc)
        return S, o

    S, o = lax.scan(step, s0, (to_chunks(q, N, C), to_chunks(k, N, C), to_chunks(v, N, C)))
    return from_chunks(o), S


def retention_mixer(h, pos0, s0, w_in, w_out):
    B, T, _ = h.shape
    proj = jnp.einsum('btd,de->bte', h, w_in).astype(jnp.float32)
    q = proj[..., :RET_QK_DIM]
    k = proj[..., RET_QK_DIM:2 * RET_QK_DIM]
    v = proj[..., 2 * RET_QK_DIM:2 * RET_QK_DIM + RET_V_DIM]
    g = proj[..., 2 * RET_QK_DIM + RET_V_DIM:]
    heads = lambda a, d: a.reshape(B, T, RET_HEADS, d).transpose(0, 2, 1, 3)
    q = rope(heads(q, RET_DK), pos0)
    k = rope(heads(k, RET_DK), pos0) * RET_DK ** -0.5
    v = heads(v, RET_DV)
    o, S = retention_chunked(q, k, v, s0.astype(jnp.float32))
    o = o * lax.rsqrt(jnp.mean(o * o, axis=-1, keepdims=True) + EPS)
    o = o.transpose(0, 2, 1, 3).reshape(B, T, RET_V_DIM) * jax.nn.silu(g)
    return jnp.einsum('bte,ed->btd', o.astype(h.dtype), w_out), S


def gated_delta_chunked(q, k, v, g, beta, s0):
    B, H, T, dk = q.shape
    dv = v.shape[-1]
    C = math.gcd(T, GDN_CHUNK)
    N = T // C
    idx = jnp.arange(C)
    tri_incl = idx[:, None] >= idx[None, :]
    tri_strict = idx[:, None] > idx[None, :]
    eye = jnp.eye(C, dtype=jnp.float32)

    def step(S, inp):
        qc, kc, vc, gc, bc = inp
        G = jnp.cumsum(gc, axis=-1)
        dG = G[..., :, None] - G[..., None, :]
        decay = jnp.exp(jnp.where(tri_incl, dG, -jnp.inf))
        kb = kc * bc[..., None]
        L = jnp.where(tri_strict, jnp.einsum('bhid,bhjd->bhij', kb, kc) * decay, 0.0)
        rhs = jnp.concatenate([vc * bc[..., None], kb * jnp.exp(G)[..., None]], axis=-1)
        sol = lax.linalg.triangular_solve(eye + L, rhs, left_side=True, lower=True, unit_diagonal=True)
        u, w = sol[..., :dv], sol[..., dv:]
        v_new = u - jnp.einsum('bhid,bhdv->bhiv', w, S)
        attn = jnp.einsum('bhid,bhjd->bhij', qc, kc) * decay
        o = jnp.einsum('bhid,bhdv->bhiv', qc * jnp.exp(G)[..., None], S) + jnp.einsum('bhij,bhjv->bhiv', attn, v_new)
        G_last = G[..., -1]
        S = S * jnp.exp(G_last)[..., None, None] + jnp.einsum('bhjd,bhjv->bhdv', kc * jnp.exp(G_last[..., None] - G)[..., None], v_new)
        return S, o

    S, o = lax.scan(step, s0, (to_chunks(q, N, C), to_chunks(k, N, C), to_chunks(v, N, C),
                               to_chunks(g, N, C), to_chunks(beta, N, C)))
    return from_chunks(o), S


def gdn_mixer(h, conv_buf, s0, w_in, w_conv, a_log, dt_bias, norm_w, w_out):
    B, T, _ = h.shape
    proj = jnp.einsum('btd,de->bte', h, w_in)
    mixed = proj[..., :GDN_CONV_DIM]
    z = proj[..., GDN_CONV_DIM:GDN_CONV_DIM + GDN_VAL_DIM]
    b = proj[..., GDN_CONV_DIM + GDN_VAL_DIM:GDN_CONV_DIM + GDN_VAL_DIM + GDN_V_HEADS]
    a = proj[..., GDN_CONV_DIM + GDN_VAL_DIM + GDN_V_HEADS:]
    conv, new_buf = causal_dwconv(mixed, conv_buf, w_conv)
    conv = jax.nn.silu(conv.astype(jnp.float32))
    q = conv[..., :GDN_KEY_DIM].reshape(B, T, GDN_K_HEADS, GDN_DK)
    k = conv[..., GDN_KEY_DIM:2 * GDN_KEY_DIM].reshape(B, T, GDN_K_HEADS, GDN_DK)
    v = conv[..., 2 * GDN_KEY_DIM:].reshape(B, T, GDN_V_HEADS, GDN_DV)
    rep = GDN_V_HEADS // GDN_K_HEADS
    q = jnp.repeat(l2norm(q), rep, axis=2) * GDN_DK ** -0.5
    k = jnp.repeat(l2norm(k), rep, axis=2)
    beta = jax.nn.sigmoid(b.astype(jnp.float32))
    g = -jnp.exp(a_log.astype(jnp.float32)) * jax.nn.softplus(a.astype(jnp.float32) + dt_bias.astype(jnp.float32))
    o, S = gated_delta_chunked(q.transpose(0, 2, 1, 3), k.transpose(0, 2, 1, 3), v.transpose(0, 2, 1, 3),
                               g.transpose(0, 2, 1), beta.transpose(0, 2, 1), s0.astype(jnp.float32))
    o = o.transpose(0, 2, 1, 3)
    zf = z.astype(jnp.float32).reshape(B, T, GDN_V_HEADS, GDN_DV)
    o = o * lax.rsqrt(jnp.mean(o * o, axis=-1, keepdims=True) + EPS) * norm_w.astype(jnp.float32) * jax.nn.silu(zf)
    out = jnp.einsum('bte,ed->btd', o.reshape(B, T, GDN_VAL_DIM).astype(h.dtype), w_out)
    return out, S, new_buf


def conv_ffn(h, buf, w_up, w_dw, b_dw, w_down):
    up = jnp.einsum('btd,df->btf', h, w_up)
    gate_br, val = up[..., :D_FF], up[..., D_FF:]
    conv, new_buf = causal_dwconv(gate_br, buf, w_dw)
    act = jax.nn.silu(conv + b_dw.astype(conv.dtype)) * val
    return jnp.einsum('btf,fd->btd', act, w_down), new_buf


def trunk(x, c, pos0, s_ret, s_gdn, s_gconv, s_fconv, p):
    cs = jax.nn.silu(c.astype(jnp.float32))
    new_ret, new_gdn, new_gconv, new_fconv = [], [], [], []
    for l in range(DEPTH):
        mod = cs @ p['w_ada'][l].astype(jnp.float32) + p['b_ada'][l].astype(jnp.float32)
        sh1, sc1, g1, sh2, sc2, g2 = jnp.split(mod, 6, axis=-1)
        h = ada_rmsnorm(x, p['norm_mix'][l], sh1, sc1)
        i = l // N_MIXERS
        if l % N_MIXERS == 0:
            out, S = retention_mixer(h, pos0, s_ret[i], p['w_ret_in'][i], p['w_ret_out'][i])
            new_ret.append(S.astype(s_ret.dtype))
        else:
            out, S, buf = gdn_mixer(h, s_gconv[i], s_gdn[i], p['w_gdn_in'][i], p['w_gdn_conv'][i],
                                    p['gdn_a_log'][i], p['gdn_dt_bias'][i], p['gdn_norm'][i], p['w_gdn_out'][i])
            new_gdn.append(S.astype(s_gdn.dtype))
            new_gconv.append(buf.astype(s_gconv.dtype))
        x = (x.astype(jnp.float32) + g1[:, None, :] * out.astype(jnp.float32)).astype(x.dtype)
        h = ada_rmsnorm(x, p['norm_ffn'][l], sh2, sc2)
        out, buf = conv_ffn(h, s_fconv[l], p['w_ffn_up'][l], p['w_ffn_dw'][l], p['b_ffn_dw'][l], p['w_ffn_down'][l])
        new_fconv.append(buf.astype(s_fconv.dtype))
        x = (x.astype(jnp.float32) + g2[:, None, :] * out.astype(jnp.float32)).astype(x.dtype)
    mod = cs @ p['w_ada_final'].astype(jnp.float32) + p['b_ada_final'].astype(jnp.float32)
    shf, scf = jnp.split(mod, 2, axis=-1)
    y = ada_rmsnorm(x, p['norm_final'], shf, scf)
    return y, jnp.stack(new_ret), jnp.stack(new_gdn), jnp.stack(new_gconv), jnp.stack(new_fconv)


def setup_inputs(seed: int = 0) -> dict:
    key = jax.random.key(seed)
    ks = jax.random.split(key, 32)
    f32 = jnp.float32
    nrm = lambda k, shape, s: s * jax.random.normal(k, shape, f32)
    inp = {}
    inp['x_prompt'] = nrm(ks[0], (BATCH, SEQ, D_MODEL), 1.0)
    inp['x_sample'] = nrm(ks[1], (DEC_BATCH, DEC_SEQ, D_MODEL), 1.0)
    inp['state_ret'] = nrm(ks[2], (N_RET, DEC_BATCH, RET_HEADS, RET_DK, RET_DV), 0.1)
    inp['state_gdn'] = nrm(ks[3], (N_GDN, DEC_BATCH, GDN_V_HEADS, GDN_DK, GDN_DV), 0.1)
    inp['state_gdn_conv'] = nrm(ks[4], (N_GDN, DEC_BATCH, GDN_CONV_W - 1, GDN_CONV_DIM), 1.0)
    inp['state_ffn_conv'] = nrm(ks[5], (DEPTH, DEC_BATCH, FFN_CONV_W - 1, D_FF), 1.0)
    inp['c_prompt'] = nrm(ks[6], (BATCH, D_MODEL), 1.0)
    inp['c_sample'] = nrm(ks[7], (DEC_BATCH, D_MODEL), 1.0)
    inp['w_ada'] = nrm(ks[8], (DEPTH, D_MODEL, 6 * D_MODEL), 0.5 * D_MODEL ** -0.5)
    inp['b_ada'] = nrm(ks[9], (DEPTH, 6 * D_MODEL), 0.02)
    inp['w_ada_final'] = nrm(ks[10], (D_MODEL, 2 * D_MODEL), 0.5 * D_MODEL ** -0.5)
    inp['b_ada_final'] = nrm(ks[11], (2 * D_MODEL,), 0.02)
    inp['norm_mix'] = 1.0 + nrm(ks[12], (DEPTH, D_MODEL), 0.02)
    inp['norm_ffn'] = 1.0 + nrm(ks[13], (DEPTH, D_MODEL), 0.02)
    inp['norm_final'] = 1.0 + nrm(ks[14], (D_MODEL,), 0.02)
    inp['w_ret_in'] = nrm(ks[15], (N_RET, D_MODEL, RET_IN), D_MODEL ** -0.5)
    inp['w_ret_out'] = nrm(ks[16], (N_RET, RET_V_DIM, D_MODEL), RET_V_DIM ** -0.5)
    inp['w_gdn_in'] = nrm(ks[17], (N_GDN, D_MODEL, GDN_IN), D_MODEL ** -0.5)
    inp['w_gdn_conv'] = nrm(ks[18], (N_GDN, GDN_CONV_W, GDN_CONV_DIM), GDN_CONV_W ** -0.5)
    inp['gdn_a_log'] = jnp.log(jax.random.uniform(ks[19], (N_GDN, GDN_V_HEADS), f32, 1.0, 16.0))
    dt = jnp.exp(jax.random.uniform(ks[20], (N_GDN, GDN_V_HEADS), f32, math.log(1e-3), math.log(1e-1)))
    inp['gdn_dt_bias'] = dt + jnp.log(-jnp.expm1(-dt))
    inp['gdn_norm'] = 1.0 + nrm(ks[21], (N_GDN, GDN_DV), 0.02)
    inp['w_gdn_out'] = nrm(ks[22], (N_GDN, GDN_VAL_DIM, D_MODEL), GDN_VAL_DIM ** -0.5)
    inp['w_ffn_up'] = nrm(ks[23], (DEPTH, D_MODEL, 2 * D_FF), D_MODEL ** -0.5)
    inp['w_ffn_dw'] = nrm(ks[24], (DEPTH, FFN_CONV_W, D_FF), FFN_CONV_W ** -0.5)
    inp['b_ffn_dw'] = nrm(ks[25], (DEPTH, D_FF), 0.02)
    inp['w_ffn_down'] = nrm(ks[26], (DEPTH, D_FF, D_MODEL), D_FF ** -0.5)
    return inp


def reference(x_prompt, x_sample, state_ret, state_gdn, state_gdn_conv, state_ffn_conv,
              c_prompt, c_sample, w_ada, b_ada, w_ada_final, b_ada_final,
              norm_mix, norm_ffn, norm_final, w_ret_in, w_ret_out,
              w_gdn_in, w_gdn_conv, gdn_a_log, gdn_dt_bias, gdn_norm, w_gdn_out,
              w_ffn_up, w_ffn_dw, b_ffn_dw, w_ffn_down):
    p = {'w_ada': w_ada, 'b_ada': b_ada, 'w_ada_final': w_ada_final, 'b_ada_final': b_ada_final,
         'norm_mix': norm_mix, 'norm_ffn': norm_ffn, 'norm_final': norm_final,
         'w_ret_in': w_ret_in, 'w_ret_out': w_ret_out,
         'w_gdn_in': w_gdn_in, 'w_gdn_conv': w_gdn_conv, 'gdn_a_log': gdn_a_log,
         'gdn_dt_bias': gdn_dt_bias, 'gdn_norm': gdn_norm, 'w_gdn_out': w_gdn_out,
         'w_ffn_up': w_ffn_up, 'w_ffn_dw': w_ffn_dw, 'b_ffn_dw': b_ffn_dw, 'w_ffn_down': w_ffn_down}
    z_ret = jnp.zeros((N_RET, BATCH, RET_HEADS, RET_DK, RET_DV), state_ret.dtype)
    z_gdn = jnp.zeros((N_GDN, BATCH, GDN_V_HEADS, GDN_DK, GDN_DV), state_gdn.dtype)
    z_gconv = jnp.zeros((N_GDN, BATCH, GDN_CONV_W - 1, GDN_CONV_DIM), state_gdn_conv.dtype)
    z_fconv = jnp.zeros((DEPTH, BATCH, FFN_CONV_W - 1, D_FF), state_ffn_conv.dtype)
    y_prompt, ret_p, gdn_p, gconv_p, fconv_p = trunk(x_prompt, c_prompt, 0, z_ret, z_gdn, z_gconv, z_fconv, p)
    y_sample, ret_s, gdn_s, gconv_s, fconv_s = trunk(x_sample, c_sample, PAST_LEN, state_ret, state_gdn,
                                                     state_gdn_conv, state_ffn_conv, p)
    return (y_prompt, y_sample, ret_p, gdn_p, gconv_p, fconv_p, ret_s, gdn_s, gconv_s, fconv_s)
```

```python
import math
import bisect
import numpy as np
from contextlib import ExitStack
import concourse.bass as bass
import concourse.mybir as mybir
from concourse.bass_utils import run_bass_kernel_spmd

F32 = mybir.dt.float32
BF16 = mybir.dt.bfloat16
AF = mybir.ActivationFunctionType
ALU = mybir.AluOpType

NCORES = 8
D = 1024
KC = 8
TP = 2048
NSEQ = 16
TS = 64
TT = TP + TS
DFF = 2816
NF = 22
EPS = 1e-6
RET_H = 4
GDN_HV = 16
GDN_HK = 8
PAST = 16384
SAME_ENGINE_SYNC = True
DEBUG = {}


class _Op:
    __slots__ = ("eng", "fn", "deps", "dma_key", "sig", "cnt", "idx")

    def __init__(self, eng, fn, deps, dma_key, idx):
        self.eng = eng
        self.fn = fn
        self.deps = deps
        self.dma_key = dma_key
        self.sig = False
        self.cnt = 0
        self.idx = idx


class Sched:
    ENGS = ("pe", "act", "dve", "pool", "sp")

    def __init__(self, nc):
        self.nc = nc
        self.ops = []
        self.last_w = {}
        self.readers = {}
        self.last_eng = {}
        self.dmas_since_bar = []

    def op(self, eng, fn, reads=(), writes=(), dma_key=None, extra=()):
        deps = set(extra)
        for r in reads:
            w = self.last_w.get(r)
            if w is not None:
                deps.add(w)
        for w_ in writes:
            w = self.last_w.get(w_)
            if w is not None:
                deps.add(w)
            deps |= self.readers.get(w_, set())
        idx = len(self.ops)
        deps.discard(idx)
        self.ops.append(_Op(eng, fn, deps, dma_key, idx))
        for r in reads:
            self.readers.setdefault(r, set()).add(idx)
        for w_ in writes:
            self.last_w[w_] = idx
            self.readers[w_] = set()
        self.last_eng[eng] = idx
        if dma_key is not None:
            self.dmas_since_bar.append(idx)
        return idx

    def pe(self, fn, reads=(), writes=()):
        return self.op("pe", fn, reads, writes)

    def act(self, fn, reads=(), writes=()):
        return self.op("act", fn, reads, writes)

    def dve(self, fn, reads=(), writes=()):
        return self.op("dve", fn, reads, writes)

    def pool(self, fn, reads=(), writes=()):
        return self.op("pool", fn, reads, writes)

    def dma(self, q, key, fn, reads=(), writes=()):
        return self.op(q, fn, reads, writes, dma_key=key)

    def barrier(self):
        deps = set(self.last_eng.values()) | set(self.dmas_since_bar)
        self.dmas_since_bar = []
        for e in self.ENGS:
            self.op(e, None, extra=deps)
        self.last_w = {}
        self.readers = {}

    def emit(self, stack):
        nc = self.nc
        ops = self.ops

        def needs(c, p):
            if p.fn is None:
                return False
            if p.dma_key is not None:
                return True
            if p.eng == c.eng:
                if p.eng == "pe":
                    return False
                return SAME_ENGINE_SYNC
            return True

        for c in ops:
            for d in c.deps:
                p = ops[d]
                if needs(c, p):
                    p.sig = True
        eng_cnt = {e: 0 for e in self.ENGS}
        dma_cnt = {}
        dma_keys = []
        dma_issue_idx = {}
        for o in ops:
            if o.dma_key is not None:
                if o.dma_key not in dma_cnt:
                    dma_cnt[o.dma_key] = 0
                    dma_keys.append(o.dma_key)
                    dma_issue_idx[o.dma_key] = []
                dma_cnt[o.dma_key] += 1
                dma_issue_idx[o.dma_key].append(o.idx)
            elif o.sig:
                eng_cnt[o.eng] += 1
                o.cnt = eng_cnt[o.eng]
        sems = {}
        for e in self.ENGS:
            sems[("e", e)] = stack.enter_context(nc.semaphore("s_" + e))
        for k in dma_keys:
            sems[("d", k)] = stack.enter_context(nc.semaphore("d_" + str(k)))
        per_eng = {e: [o for o in ops if o.eng == e] for e in self.ENGS}
        block = stack.enter_context(nc.Block())

        def run_engine(ename, eobj):
            waited = {}
            for o in per_eng[ename]:
                need = {}
                for d in o.deps:
                    p = ops[d]
                    if not needs(o, p):
                        continue
                    if p.dma_key is not None:
                        key = ("d", p.dma_key)
                        val = 16 * bisect.bisect_left(dma_issue_idx[p.dma_key], o.idx)
                    else:
                        key = ("e", p.eng)
                        val = p.cnt
                    if need.get(key, 0) < val:
                        need[key] = val
                for key, val in need.items():
                    if waited.get(key, 0) >= val:
                        continue
                    eobj.wait_ge(sems[key], val)
                    waited[key] = val
                if o.fn is None:
                    continue
                ins = o.fn(eobj)
                if o.dma_key is not None:
                    ins.then_inc(sems[("d", o.dma_key)], 16)
                elif o.sig:
                    ins.then_inc(sems[("e", ename)], 1)
            for k in dma_keys:
                if any(o.dma_key == k for o in per_eng[ename]):
                    eobj.wait_ge(sems[("d", k)], 16 * dma_cnt[k])

        @block.tensor
        def _(e):
            run_engine("pe", e)

        @block.scalar
        def _(e):
            run_engine("act", e)

        @block.vector
        def _(e):
            run_engine("dve", e)

        @block.gpsimd
        def _(e):
            run_engine("pool", e)

        @block.sync
        def _(e):
            run_engine("sp", e)


def _gammas():
    return (1.0 - 2.0 ** (-5.0 - np.arange(RET_H, dtype=np.float64)))


def host_consts():
    c = {}
    c["c_ident"] = np.eye(128, dtype=np.float32)
    half = 128
    inv_freq = (np.float32(10000.0) ** (-(np.arange(half, dtype=np.float32)) / np.float32(half))).astype(np.float32)
    pos = np.concatenate([np.arange(TP, dtype=np.float32), (PAST + (np.arange(TS) % 4)).astype(np.float32)])
    ang = (pos[None, :] * inv_freq[:, None]).astype(np.float32)
    cos = np.cos(ang.astype(np.float64))
    sin = np.sin(ang.astype(np.float64))
    g = _gammas()
    pin = np.concatenate([np.arange(TP) % 128, np.arange(TS) % 4]).astype(np.float64)
    rope = np.zeros((RET_H + 1, 2, 128, TT), np.float32)
    for h in range(RET_H):
        dec = g[h] ** (pin + 1.0)
        rope[h, 0] = cos * dec[None, :]
        rope[h, 1] = sin * dec[None, :]
    rope[RET_H, 0] = cos * (256.0 ** -0.5)
    rope[RET_H, 1] = sin * (256.0 ** -0.5)
    c["rope"] = rope
    mP = np.zeros((RET_H, 128, 128), np.float32)
    mS = np.zeros((RET_H, 128, 128), np.float32)
    ks = np.zeros((128, 2 * RET_H), np.float32)
    jj = np.arange(128)
    for h in range(RET_H):
        mP[h] = np.where(jj[None, :] >= jj[:, None], g[h] ** (-(jj[:, None] + 1.0)), 0.0)
        j4 = jj[:64] % 4
        same = (jj[:64, None] // 4) == (jj[None, :64] // 4)
        mS[h, :64, :64] = np.where(same & (jj[None, :64] >= jj[:64, None]), g[h] ** (-(j4[:, None] + 1.0)), 0.0)
        ks[:, h] = g[h] ** (127.0 - jj)
        ks[:64, 4 + h] = g[h] ** (3.0 - j4)
    c["retmask"] = np.concatenate([mP, mS], axis=0)
    c["kscale"] = ks
    seg = np.zeros((128, NSEQ, 64), np.float32)
    for s in range(NSEQ):
        seg[:, s, 4 * s:4 * s + 4] = 1.0
    c["segmask"] = seg.reshape(128, NSEQ * 64)
    segcol = np.zeros((128, NSEQ), np.float32)
    for s in range(NSEQ):
        segcol[4 * s:4 * s + 4, s] = 1.0
    c["segcol"] = segcol
    gm = np.zeros((8, 128, 128), np.float32)
    a = np.arange(128)
    gm[0] = (a[:, None] <= a[None, :])
    gm[1] = (a[:, None] > a[None, :])
    gm[2] = (a[None, :] >= a[:, None])
    gm[3] = (a[None, :] > a[:, None])
    b = np.arange(64)
    same = (b[:, None] // 4) == (b[None, :] // 4)
    gm[4, :64, :64] = (b[:, None] <= b[None, :]) & same
    gm[5, :64, :64] = (b[:, None] > b[None, :]) & same
    gm[6, :64, :64] = (b[None, :] >= b[:, None]) & same
    gm[7, :64, :64] = (b[None, :] > b[:, None]) & same
    c["gmask"] = np.ascontiguousarray(gm.transpose(1, 0, 2)).reshape(128, 8 * 128)
    bmk = np.zeros((4, 128, 128), np.float32)
    bmk[0] = (a[:, None] // 16) == (a[None, :] // 16)
    for mi, m in enumerate([32, 64, 128]):
        bmk[mi + 1] = ((a[:, None] // m) == (a[None, :] // m)) & ((a[:, None] // (m // 2)) != (a[None, :] // (m // 2)))
    c["bmask"] = np.ascontiguousarray(bmk.transpose(1, 0, 2)).reshape(128, 4 * 128)
    return c


class Ctx:
    pass


def build_program(dbg=()):
    nc = bass.Bass("TRN2", target_bir_lowering=False)
    K = Ctx()
    K.nc = nc
    ins = {}

    def din(name, shape):
        ins[name] = nc.dram_tensor(name, list(shape), F32, kind="ExternalInput").ap()
        return ins[name]

    def dout(name, shape):
        return nc.dram_tensor(name, list(shape), F32, kind="ExternalOutput").ap()

    xp = din("xp", [TP, D]); xs = din("xs", [TS, D])
    s_ret = din("s_ret", [NSEQ, RET_H, 256, 512])
    s_gdn = din("s_gdn", [NSEQ, GDN_HV, 128, 128])
    s_gconv = din("s_gconv", [NSEQ * 3, 4096])
    s_fconv = din("s_fconv", [2, NSEQ * 2, DFF])
    vec22 = din("vec22", [22, D])
    w_ada = din("w_ada", [2, D, 6 * D]); b_ada = din("b_ada", [2, 6 * D])
    w_adaf = din("w_ada_final", [D, 2 * D]); b_adaf = din("b_ada_final", [1, 2 * D])
    w_ret_in = din("w_ret_in", [D, 6144]); w_ret_out = din("w_ret_out", [2048, D])
    w_gdn_in = din("w_gdn_in", [D, 6176]); w_gdn_out = din("w_gdn_out", [2048, D])
    w_gconv = din("w_gdn_conv", [4, 4096])
    gdn_vec = din("gdn_vec", [1, 32])
    gdn_norm = din("gdn_norm", [1, 128])
    w_up = din("w_ffn_up", [2, D, 2 * DFF]); w_down = din("w_ffn_down", [2, DFF, D])
    ffn_vec = din("ffn_vec", [8, DFF])
    c_ident = din("c_ident", [128, 128])
    c_rope = din("rope", [RET_H + 1, 2, 128, TT])
    c_retmask = din("retmask", [2 * RET_H, 128, 128])
    c_kscale = din("kscale", [128, 2 * RET_H])
    c_segmask = din("segmask", [128, NSEQ * 64])
    c_segcol = din("segcol", [128, NSEQ])
    c_gmask = din("gmask", [128, 8 * 128])
    c_bmask = din("bmask", [128, 4 * 128])

    y_p = dout("y_p", [TP, D]); y_s = dout("y_s", [TS, D])
    ret_p = dout("ret_p", [RET_H, 256, 512]); gdn_p = dout("gdn_p", [GDN_HV, 128, 128])
    gconv_p = dout("gconv_p", [3, 4096]); fconv_p = dout("fconv_p", [2, 2, DFF])
    ret_s = dout("ret_s", [NSEQ, RET_H, 256, 512]); gdn_s = dout("gdn_s", [NSEQ, GDN_HV, 128, 128])
    gconv_s = dout("gconv_s", [NSEQ * 3, 4096]); fconv_s = dout("fconv_s", [2, NSEQ * 2, DFF])
    dbg_out = {n: dout("dbg_" + n, shp) for n, shp in dbg}

    with ExitStack() as st:
        def T(name, shape, dt):
            return st.enter_context(nc.sbuf_tensor(name, list(shape), dt))

        S = Sched(nc)
        xT = T("xT", [128, KC, TT], F32)
        hT = T("hT", [128, KC, TT], BF16)
        wring = T("wring", [128, 4, 4096], BF16)
        modT = T("modT", [128, 112, 17], F32)
        vecT = T("vecT", [128, KC, 22], F32)
        Amod = T("Amod", [128, 5, KC, 17], F32)
        csT = T("csT", [128, KC, 17], F32)
        ident_f = T("ident_f", [128, 128], F32)
        ident_b = T("ident_b", [128, 128], BF16)
        ones_b = T("ones_b", [128, 128], BF16)
        ones_f = T("ones_f", [128, 32], F32)
        ffnvT = T("ffnvT", [128, NF, 8], F32)
        fcarry = T("fcarry", [128, NF, 2], F32)
        ARENA = 15000
        arena = T("arena", [128, ARENA], F32)
        ps = st.enter_context(nc.psum_tensor("ps", [128, 8, 512], F32))

        A = Ctx()
        A.off = 0

        def a_reset():
            A.off = 0

        def a_f32(*free, parts=128):
            n = int(np.prod(free))
            assert A.off + n <= ARENA, ("arena overflow", A.off, n)
            ap = arena[0:parts, A.off:A.off + n]
            A.off += n
            if len(free) == 2:
                ap = ap.rearrange("p (a b) -> p a b", a=free[0])
            elif len(free) == 3:
                ap = ap.rearrange("p (a b c) -> p a b c", a=free[0], b=free[1])
            return ap

        def a_bf16(*free, parts=128):
            n = int(np.prod(free))
            nf = (n + 1) // 2
            assert A.off + nf <= ARENA, ("arena overflow", A.off, nf)
            ap = arena[0:parts, A.off:A.off + nf].bitcast(BF16)[:, 0:n]
            A.off += nf
            if len(free) == 2:
                ap = ap.rearrange("p (a b) -> p a b", a=free[0])
            elif len(free) == 3:
                ap = ap.rearrange("p (a b c) -> p a b c", a=free[0], b=free[1])
            return ap

        P = Ctx()
        P.i = 0

        P.reserved = set()

        def ps_next(n=1):
            while True:
                if P.i + n > 8:
                    P.i = 0
                b = P.i
                P.i = (P.i + n) % 8
                if not any((b + i) in P.reserved for i in range(n)):
                    return b

        def psk(b, n=1):
            return [("ps", b + i) for i in range(n)]

        W = Ctx()
        W.i = 0

        def wslot():
            i = W.i
            W.i = (W.i + 1) % 4
            return i

        def load_w_bf16(src2d, kc, ncols, row0=0, col0=0, slot=None, off=0, key=None):
            i = wslot() if slot is None else slot
            view = wring[:, i, off:off + kc * ncols].rearrange("p (k c) -> p k c", k=kc)
            src = src2d[row0:row0 + kc * 128, col0:col0 + ncols].rearrange("(k p) c -> p k c", p=128)
            k_ = ("w", i) if key is None else key
            S.dma("pool", "w%d" % i, lambda e: e.dma_start(out=view, in_=src), writes=[k_])
            return view, k_

        def load_w_f32(src2d, kc, ncols, row0=0, col0=0):
            i = wslot()
            view = wring[:, i, :].bitcast(F32)[:, 0:kc * ncols].rearrange("p (k c) -> p k c", k=kc)
            src = src2d[row0:row0 + kc * 128, col0:col0 + ncols].rearrange("(k p) c -> p k c", p=128)
            S.dma("sp", "wf%d" % i, lambda e: e.dma_start(out=view, in_=src), writes=[("w", i)])
            return view, ("w", i)

        BLOCKS_P = [(0, 512), (512, 512), (1024, 512), (1536, 512)]
        BLOCK_S = (TP, TS)
        ALLBLOCKS = BLOCKS_P + [BLOCK_S]

        S.dma("sp", "c_id", lambda e: e.dma_start(out=ident_f[:], in_=c_ident), writes=["ident_f"])
        S.dve(lambda e: e.tensor_copy(out=ident_b[:], in_=ident_f[:]), reads=["ident_f"], writes=["ident_b"])
        S.dve(lambda e: e.memset(ones_b[:], 1.0), writes=["ones_b"])
        S.dve(lambda e: e.memset(ones_f[:], 1.0), writes=["ones_f"])
        a_reset()
        stage22 = a_f32(D, parts=22)
        S.dma("sp", "c_s22", lambda e: e.dma_start(out=stage22, in_=vec22), writes=["stage22"])
        b0 = ps_next()
        for kc in range(KC):
            S.pe(lambda e, kc=kc: e.transpose(ps[:, b0, kc * 22:(kc + 1) * 22], stage22[:, kc * 128:(kc + 1) * 128], ident_f[0:22, 0:22]),
                 reads=["stage22", "ident_f"], writes=psk(b0))
        S.dve(lambda e: e.tensor_copy(out=vecT[:].rearrange("p a b -> p (a b)"), in_=ps[:, b0, 0:KC * 22]), reads=psk(b0), writes=["vecT"])
        S.act(lambda e: e.activation(out=csT[:], in_=vecT[:, :, 0:17], func=AF.Silu), reads=["vecT"], writes=["csT"])
        stage8 = a_f32(DFF, parts=8)
        S.dma("sp", "c_s8", lambda e: e.dma_start(out=stage8, in_=ins["ffn_vec"]), writes=["stage8"])
        b1 = ps_next()
        for fc in range(NF):
            S.pe(lambda e, fc=fc: e.transpose(ps[:, b1, fc * 8:(fc + 1) * 8], stage8[:, fc * 128:(fc + 1) * 128], ident_f[0:8, 0:8]),
                 reads=["stage8", "ident_f"], writes=psk(b1))
        S.dve(lambda e: e.tensor_copy(out=ffnvT[:].rearrange("p a b -> p (a b)"), in_=ps[:, b1, 0:NF * 8]), reads=psk(b1), writes=["ffnvT"])

        browb = [a_bf16(512, parts=1) for _ in range(3)]
        mstage = [a_f32(512, parts=17) for _ in range(2)]
        csb = a_bf16(KC, 17)
        S.dve(lambda e: e.tensor_copy(out=csb, in_=csT[:]), reads=["csT"], writes=["csb"])
        bri = [0]

        def ada_layer(wsrc, bsrc, ncols, mod_base):
            for pcs in range(ncols // 512):
                wv, wk = load_w_bf16(wsrc, KC, 512, col0=pcs * 512)
                bi = bri[0] % 3
                m2 = bri[0] % 2
                bri[0] += 1
                S.dma("pool", "brb%d" % bi, lambda e, bi=bi, pcs=pcs: e.dma_start(out=browb[bi], in_=bsrc[0:1, pcs * 512:(pcs + 1) * 512]), writes=[("browb", bi)])
                bank = ps_next()
                for kc in range(KC):
                    S.pe(lambda e, bank=bank, kc=kc, wv=wv: e.matmul(ps[0:17, bank, :], lhsT=csb[:, kc, :], rhs=wv[:, kc, :], start=(kc == 0), stop=False), reads=[wk, "csb"], writes=psk(bank))
                S.pe(lambda e, bank=bank, bi=bi: e.matmul(ps[0:17, bank, :], lhsT=ones_b[0:1, 0:17], rhs=browb[bi][0:1, :], start=False, stop=True), reads=[("browb", bi), "ones_b"], writes=psk(bank))
                S.act(lambda e, bank=bank, m2=m2: e.copy(out=mstage[m2], in_=ps[0:17, bank, :]), reads=psk(bank), writes=[("mstage", m2)])
                bank2 = ps_next()
                for q in range(4):
                    S.pe(lambda e, bank2=bank2, q=q, m2=m2: e.transpose(ps[:, bank2, q * 17:(q + 1) * 17], mstage[m2][:, q * 128:(q + 1) * 128], ident_f[0:17, 0:17]), reads=[("mstage", m2), "ident_f"], writes=psk(bank2))
                m0 = mod_base + 4 * pcs
                S.dve(lambda e, bank2=bank2, m0=m0: e.tensor_copy(out=modT[:, m0:m0 + 4, :].rearrange("p a b -> p (a b)"), in_=ps[:, bank2, 0:68]), reads=psk(bank2), writes=["modT"])

        ada_layer(w_ada[0], b_ada[0:1, :], 6 * D, 0)
        ada_layer(w_ada[1], b_ada[1:2, :], 6 * D, 48)
        ada_layer(w_adaf, b_adaf, 2 * D, 96)
        norm_specs = [(0 * 48 + 8, 17), (0 * 48 + 32, 19), (1 * 48 + 8, 18), (1 * 48 + 32, 20), (96 + 8, 21)]
        for n, (scb, col) in enumerate(norm_specs):
            for kc in range(KC):
                S.dve(lambda e, n=n, kc=kc, scb=scb, col=col: e.tensor_scalar(out=Amod[:, n, kc, :], in0=modT[:, scb + kc, :], scalar1=1.0, scalar2=vecT[:, kc, col:col + 1], op0=ALU.add, op1=ALU.mult),
                      reads=["modT", "vecT"], writes=["Amod"])
        norm_shift = [0, 24, 48, 72, 96]
        S.barrier()

        a_reset()
        xst = [a_f32(D) for _ in range(2)]
        tiles = [(xp, t * 128, 128, t * 128) for t in range(16)] + [(xs, 0, TS, TP)]
        for ti, (src, r0, rows, t0) in enumerate(tiles):
            sb = ti % 2
            S.dma("sp", "xst%d" % sb, lambda e, sb=sb, src=src, r0=r0, rows=rows: e.dma_start(out=xst[sb][0:rows, :], in_=src[r0:r0 + rows, :]), writes=[("xst", sb)])
            for half in range(2):
                b = ps_next()
                for q in range(4):
                    kc = half * 4 + q
                    S.pe(lambda e, b=b, q=q, kc=kc, sb=sb, rows=rows: e.transpose(ps[:, b, q * 128:q * 128 + rows], xst[sb][0:rows, kc * 128:(kc + 1) * 128], ident_f[0:rows, 0:rows]),
                         reads=[("xst", sb), "ident_f"], writes=psk(b))
                src_ap = ps[:, b, :].rearrange("p (q c) -> p q c", q=4)[:, :, 0:rows]
                dst_ap = xT[:, half * 4:half * 4 + 4, t0:t0 + rows]
                if half == 0:
                    S.act(lambda e, dst_ap=dst_ap, src_ap=src_ap: e.copy(out=dst_ap, in_=src_ap), reads=psk(b), writes=[("xT", ti)])
                else:
                    S.dve(lambda e, dst_ap=dst_ap, src_ap=src_ap: e.tensor_copy(out=dst_ap, in_=src_ap), reads=psk(b), writes=[("xT", ti)])
        S.barrier()

        def do_norm(n, out_hT=True, out_f32=None):
            a_reset()
            sq = [a_bf16(KC, 512) for _ in range(2)]
            rstd = [a_f32(512) for _ in range(2)]
            tmp = [a_f32(KC, 512) for _ in range(2)]
            shb = norm_shift[n]
            for bi, (t0, nt) in enumerate(ALLBLOCKS):
                i2 = bi % 2
                S.act(lambda e, i2=i2, t0=t0, nt=nt: e.activation(out=sq[i2][:, :, 0:nt], in_=xT[:, :, t0:t0 + nt], func=AF.Square),
                      reads=[], writes=[("sq", i2)])
                b = ps_next()
                for kc in range(KC):
                    S.pe(lambda e, b=b, kc=kc, i2=i2, nt=nt: e.matmul(ps[:, b, 0:nt], lhsT=ones_b[:], rhs=sq[i2][:, kc, 0:nt], start=(kc == 0), stop=(kc == KC - 1)),
                         reads=[("sq", i2), "ones_b"], writes=psk(b))
                S.act(lambda e, b=b, i2=i2, nt=nt: e.activation(out=rstd[i2][:, 0:nt], in_=ps[:, b, 0:nt], func=AF.Sqrt, bias=EPS, scale=1.0 / D),
                      reads=psk(b), writes=[("rstd", i2)])
                S.dve(lambda e, i2=i2, nt=nt: e.reciprocal(out=rstd[i2][:, 0:nt], in_=rstd[i2][:, 0:nt]), reads=[("rstd", i2)], writes=[("rstd", i2)])
                S.dve(lambda e, i2=i2, t0=t0, nt=nt: e.tensor_tensor(out=tmp[i2][:, :, 0:nt], in0=xT[:, :, t0:t0 + nt], in1=rstd[i2][:, 0:nt].unsqueeze(1).to_broadcast([128, KC, nt]), op=ALU.mult),
                      reads=[("rstd", i2)], writes=[("tmp", i2)])
                for kc in range(KC):
                    dst = hT[:, kc, t0:t0 + nt] if out_f32 is None else out_f32(bi, kc)
                    if t0 < TP:
                        S.act(lambda e, dst=dst, i2=i2, kc=kc, nt=nt: e.activation(out=dst, in_=tmp[i2][:, kc, 0:nt], func=AF.Identity, scale=Amod[:, n, kc, 0:1], bias=modT[:, shb + kc, 0:1]),
                              reads=[("tmp", i2)], writes=[("h", bi, kc)])
                    else:
                        S.dve(lambda e, i2=i2, kc=kc: e.tensor_tensor(out=tmp[i2][:, kc, 0:TS].rearrange("p (s j) -> p s j", j=4), in0=tmp[i2][:, kc, 0:TS].rearrange("p (s j) -> p s j", j=4),
                                                                     in1=Amod[:, n, kc, 1:17].unsqueeze(2).to_broadcast([128, NSEQ, 4]), op=ALU.mult),
                              reads=[("tmp", i2)], writes=[("tmp", i2)])
                        S.dve(lambda e, dst=dst, i2=i2, kc=kc: e.tensor_tensor(out=dst.rearrange("p (s j) -> p s j", j=4), in0=tmp[i2][:, kc, 0:TS].rearrange("p (s j) -> p s j", j=4),
                                                                              in1=modT[:, shb + kc, 1:17].unsqueeze(2).to_broadcast([128, NSEQ, 4]), op=ALU.add),
                              reads=[("tmp", i2)], writes=[("h", bi, kc)])
            S.barrier()

        def gate_ap(gbase, kc, bi, nt):
            if bi != "S":
                return modT[:, gbase + kc, 0:1].to_broadcast([128, nt])
            return modT[:, gbase + kc, 1:17].unsqueeze(2).to_broadcast([128, NSEQ, 4])

        def resid_update(psrc, gbase, kc, bi, t0, nt, reads):
            if t0 < TP:
                S.dve(lambda e: e.scalar_tensor_tensor(out=xT[:, kc, t0:t0 + nt], in0=psrc, scalar=modT[:, gbase + kc, 0:1], in1=xT[:, kc, t0:t0 + nt], op0=ALU.mult, op1=ALU.add),
                      reads=list(reads) + [("x", kc, t0)], writes=[("x", kc, t0)])
            else:
                tmpg = K.tmpg
                S.dve(lambda e: e.tensor_tensor(out=tmpg.rearrange("p (s j) -> p s j", j=4), in0=psrc.rearrange("p (s j) -> p s j", j=4), in1=gate_ap(gbase, kc, "S", nt), op=ALU.mult),
                      reads=list(reads), writes=["tmpg"])
                S.dve(lambda e: e.tensor_tensor(out=xT[:, kc, t0:t0 + nt], in0=xT[:, kc, t0:t0 + nt], in1=tmpg, op=ALU.add),
                      reads=["tmpg", ("x", kc, t0)], writes=[("x", kc, t0)])

        def do_ffn(l):
            a_reset()
            gbase = l * 48 + 40
            K.tmpg = a_f32(TS)
            actb = a_bf16(NF, 704)
            gx = [a_f32(2 + 512) for _ in range(2)]
            gxs = a_f32(NSEQ, 6)
            cv = [a_f32(512) for _ in range(2)]
            tailP = a_f32(NF, 2)
            tailS = a_f32(NF, NSEQ, 2)
            sstT = a_f32(NF, 32)
            sstg = [a_f32(512, parts=32) for _ in range(2)]
            ostg = [a_f32(512, parts=32) for _ in range(2)]
            ostgP = [a_f32(512, parts=2) for _ in range(2)]
            for gi, g4 in enumerate(range(0, NF, 4)):
                b = ps_next()
                n = min(4, NF - g4)
                s2 = gi % 2
                S.dma("sp", "fst%d" % s2, lambda e, s2=s2, g4=g4, n=n: e.dma_start(out=sstg[s2][:, 0:n * 128], in_=s_fconv[l][:, g4 * 128:(g4 + n) * 128]), writes=[("sstg", s2)])
                for q in range(n):
                    S.pe(lambda e, b=b, q=q, s2=s2: e.transpose(ps[:, b, q * 32:(q + 1) * 32], sstg[s2][:, q * 128:(q + 1) * 128], ident_f[0:32, 0:32]),
                         reads=[("sstg", s2), "ident_f"], writes=psk(b))
                S.dve(lambda e, b=b, n=n, g4=g4: e.tensor_copy(out=sstT[:, g4:g4 + n, :].rearrange("p a b -> p (a b)"), in_=ps[:, b, 0:n * 32]), reads=psk(b), writes=["sstT"])
            S.dve(lambda e: e.memset(fcarry[:], 0.0), writes=[("fcarry", fc_) for fc_ in range(NF)])
            wcol = l * 3
            passes = [[(0, 0, 352, 0), (1, 352, 352, 352)], [(2, 704, 352, 0), (3, 1056, 352, 352)], [(4, 1408, 320, 0), (5, 1728, 320, 320), (6, TP, TS, 640)]]
            for pi, blks in enumerate(passes):
                for f0 in range(0, NF, 4):
                    nf = min(4, NF - f0)
                    wg, wgk = load_w_bf16(w_up[l], KC, nf * 128, col0=f0 * 128)
                    wv, wvk = load_w_bf16(w_up[l], KC, nf * 128, col0=DFF + f0 * 128)
                    for q in range(nf):
                        fc = f0 + q
                        for (bi, t0, nt, a0) in blks:
                            bg = ps_next()
                            for kc in range(KC):
                                S.pe(lambda e, bg=bg, kc=kc, q=q, t0=t0, nt=nt, wg=wg: e.matmul(ps[:, bg, 0:nt], lhsT=wg[:, kc, q * 128:(q + 1) * 128], rhs=hT[:, kc, t0:t0 + nt], start=(kc == 0), stop=(kc == KC - 1)),
                                     reads=[wgk], writes=psk(bg))
                            bv = ps_next()
                            for kc in range(KC):
                                S.pe(lambda e, bv=bv, kc=kc, q=q, t0=t0, nt=nt, wv=wv: e.matmul(ps[:, bv, 0:nt], lhsT=wv[:, kc, q * 128:(q + 1) * 128], rhs=hT[:, kc, t0:t0 + nt], start=(kc == 0), stop=(kc == KC - 1)),
                                     reads=[wvk], writes=psk(bv))
                            w0 = ffnvT[:, fc, wcol + 0:wcol + 1]
                            w1 = ffnvT[:, fc, wcol + 1:wcol + 2]
                            w2 = ffnvT[:, fc, wcol + 2:wcol + 3]
                            bb = ffnvT[:, fc, 6 + l:7 + l]
                            if bi < 6:
                                i2 = bi % 2
                                G = gx[i2]
                                c_ = cv[i2]
                                S.dve(lambda e, G=G, fc=fc: e.tensor_copy(out=G[:, 0:2], in_=fcarry[:, fc, :]), reads=[("fcarry", fc)], writes=[("gx", i2)])
                                S.act(lambda e, G=G, bg=bg, nt=nt: e.copy(out=G[:, 2:2 + nt], in_=ps[:, bg, 0:nt]), reads=psk(bg), writes=[("gx", i2)])
                                S.dve(lambda e, G=G, fc=fc, nt=nt: e.tensor_copy(out=fcarry[:, fc, :], in_=G[:, nt:nt + 2]), reads=[("gx", i2)], writes=[("fcarry", fc)])
                                if bi == 5:
                                    S.dve(lambda e, G=G, fc=fc, nt=nt: e.tensor_copy(out=tailP[:, fc, :], in_=G[:, nt:nt + 2]), reads=[("gx", i2)], writes=["tailP"])
                                S.act(lambda e, G=G, c_=c_, nt=nt, w2=w2: e.activation(out=c_[:, 0:nt], in_=G[:, 2:2 + nt], func=AF.Identity, scale=w2), reads=[("gx", i2), "ffnvT"], writes=[("cv", i2)])
                                S.dve(lambda e, G=G, c_=c_, nt=nt, w1=w1: e.scalar_tensor_tensor(out=c_[:, 0:nt], in0=G[:, 1:1 + nt], scalar=w1, in1=c_[:, 0:nt], op0=ALU.mult, op1=ALU.add), reads=[("gx", i2), ("cv", i2)], writes=[("cv", i2)])
                                S.dve(lambda e, G=G, c_=c_, nt=nt, w0=w0: e.scalar_tensor_tensor(out=c_[:, 0:nt], in0=G[:, 0:nt], scalar=w0, in1=c_[:, 0:nt], op0=ALU.mult, op1=ALU.add), reads=[("gx", i2), ("cv", i2)], writes=[("cv", i2)])
                                S.act(lambda e, c_=c_, nt=nt, bb=bb: e.activation(out=c_[:, 0:nt], in_=c_[:, 0:nt], func=AF.Silu, bias=bb), reads=[("cv", i2)], writes=[("cv", i2)])
                                S.dve(lambda e, c_=c_, nt=nt, bv=bv, fc=fc, a0=a0: e.tensor_tensor(out=actb[:, fc, a0:a0 + nt], in0=c_[:, 0:nt], in1=ps[:, bv, 0:nt], op=ALU.mult), reads=[("cv", i2)] + psk(bv), writes=[("act", fc, bi)])
                            else:
                                c_ = cv[0][:, 0:TS].rearrange("p (s j) -> p s j", j=4)
                                S.dve(lambda e, fc=fc: e.tensor_copy(out=gxs[:, :, 0:2], in_=sstT[:, fc, :].rearrange("p (s r) -> p s r", r=2)), reads=["sstT"], writes=["gxs"])
                                S.act(lambda e, bg=bg: e.copy(out=gxs[:, :, 2:6], in_=ps[:, bg, 0:TS].rearrange("p (s j) -> p s j", j=4)), reads=psk(bg), writes=["gxs"])
                                S.dve(lambda e, fc=fc: e.tensor_copy(out=tailS[:, fc, :, :], in_=gxs[:, :, 4:6]), reads=["gxs"], writes=["tailS"])
                                S.act(lambda e, c_=c_, w2=w2: e.activation(out=c_, in_=gxs[:, :, 2:6], func=AF.Identity, scale=w2), reads=["gxs", "ffnvT"], writes=[("cv", 0)])
                                S.dve(lambda e, c_=c_, w1=w1: e.scalar_tensor_tensor(out=c_, in0=gxs[:, :, 1:5], scalar=w1, in1=c_, op0=ALU.mult, op1=ALU.add), reads=["gxs", ("cv", 0)], writes=[("cv", 0)])
                                S.dve(lambda e, c_=c_, w0=w0: e.scalar_tensor_tensor(out=c_, in0=gxs[:, :, 0:4], scalar=w0, in1=c_, op0=ALU.mult, op1=ALU.add), reads=["gxs", ("cv", 0)], writes=[("cv", 0)])
                                S.act(lambda e, bb=bb: e.activation(out=cv[0][:, 0:TS], in_=cv[0][:, 0:TS], func=AF.Silu, bias=bb), reads=[("cv", 0)], writes=[("cv", 0)])
                                S.dve(lambda e, bv=bv, fc=fc, a0=a0: e.tensor_tensor(out=actb[:, fc, a0:a0 + TS], in0=cv[0][:, 0:TS], in1=ps[:, bv, 0:TS], op=ALU.mult), reads=[("cv", 0)] + psk(bv), writes=[("act", fc, bi)])
                for dh in range(2):
                    banks = {}
                    for dc in range(4):
                        for (bi, t0, nt, a0) in blks:
                            if len(blks) * 4 > 8 and bi == 6:
                                continue
                            banks[(dc, bi)] = ps_next()
                    for f0 in range(0, NF, 4):
                        nf = min(4, NF - f0)
                        wd, wdk = load_w_bf16(w_down[l], nf, 512, row0=f0 * 128, col0=dh * 512)
                        for (dc, bi), b in banks.items():
                            t0, nt, a0 = [(x[1], x[2], x[3]) for x in blks if x[0] == bi][0]
                            for q in range(nf):
                                fc = f0 + q
                                S.pe(lambda e, b=b, q=q, dc=dc, fc=fc, a0=a0, nt=nt, wd=wd: e.matmul(ps[:, b, 0:nt], lhsT=wd[:, q, dc * 128:(dc + 1) * 128], rhs=actb[:, fc, a0:a0 + nt], start=(fc == 0), stop=(fc == NF - 1)),
                                     reads=[wdk, ("act", fc, bi)], writes=psk(b))
                    for (dc, bi), b in banks.items():
                        t0, nt, a0 = [(x[1], x[2], x[3]) for x in blks if x[0] == bi][0]
                        resid_update(ps[:, b, 0:nt], gbase, dh * 4 + dc, bi, t0, nt, psk(b))
                if len(blks) * 4 > 8:
                    (bi, t0, nt, a0) = blks[2]
                    for dh in range(2):
                        banks = {dc: ps_next() for dc in range(4)}
                        for f0 in range(0, NF, 4):
                            nf = min(4, NF - f0)
                            wd, wdk = load_w_bf16(w_down[l], nf, 512, row0=f0 * 128, col0=dh * 512)
                            for dc, b in banks.items():
                                for q in range(nf):
                                    fc = f0 + q
                                    S.pe(lambda e, b=b, q=q, dc=dc, fc=fc, wd=wd, nt=nt, a0=a0: e.matmul(ps[:, b, 0:nt], lhsT=wd[:, q, dc * 128:(dc + 1) * 128], rhs=actb[:, fc, a0:a0 + nt], start=(fc == 0), stop=(fc == NF - 1)),
                                         reads=[wdk, ("act", fc, bi)], writes=psk(b))
                        for dc, b in banks.items():
                            resid_update(ps[:, b, 0:nt], gbase, dh * 4 + dc, bi, t0, nt, psk(b))
            for gi, g4 in enumerate(range(0, NF, 4)):
                n = min(4, NF - g4)
                o2 = gi % 2
                b = ps_next()
                b2 = ps_next()
                for q in range(n):
                    fc = g4 + q
                    S.pe(lambda e, b=b, q=q, fc=fc: e.transpose(ps[0:2, b, q * 128:(q + 1) * 128], tailP[:, fc, :], ident_f[:]), reads=["tailP", "ident_f"], writes=psk(b))
                    S.pe(lambda e, b2=b2, q=q, fc=fc: e.transpose(ps[0:32, b2, q * 128:(q + 1) * 128], tailS[:, fc, :, :].rearrange("p s r -> p (s r)"), ident_f[:]), reads=["tailS", "ident_f"], writes=psk(b2))
                S.dve(lambda e, b=b, n=n, o2=o2: e.tensor_copy(out=ostgP[o2][:, 0:n * 128], in_=ps[0:2, b, 0:n * 128]), reads=psk(b), writes=[("ostgP", o2)])
                S.dve(lambda e, b2=b2, n=n, o2=o2: e.tensor_copy(out=ostg[o2][:, 0:n * 128], in_=ps[0:32, b2, 0:n * 128]), reads=psk(b2), writes=[("ostg", o2)])
                S.dma("sp", "foutP%d" % o2, lambda e, o2=o2, n=n, g4=g4: e.dma_start(out=fconv_p[l][:, g4 * 128:(g4 + n) * 128], in_=ostgP[o2][:, 0:n * 128]), reads=[("ostgP", o2)])
                S.dma("sp", "foutS%d" % o2, lambda e, o2=o2, n=n, g4=g4: e.dma_start(out=fconv_s[l][:, g4 * 128:(g4 + n) * 128], in_=ostg[o2][:, 0:n * 128]), reads=[("ostg", o2)])
            S.barrier()

        def do_final():
            a_reset()
            yT = a_f32(KC, 512)
            ytok = [a_f32(D) for _ in range(2)]
            n = 4
            sq = a_bf16(KC, 512)
            rstd = a_f32(512)
            tmp = a_f32(KC, 512)
            shb = norm_shift[n]
            oi = 0
            for bi, (t0, nt) in enumerate(ALLBLOCKS):
                S.act(lambda e, t0=t0, nt=nt: e.activation(out=sq[:, :, 0:nt], in_=xT[:, :, t0:t0 + nt], func=AF.Square), reads=[], writes=["sq"])
                b = ps_next()
                for kc in range(KC):
                    S.pe(lambda e, b=b, kc=kc, nt=nt: e.matmul(ps[:, b, 0:nt], lhsT=ones_b[:], rhs=sq[:, kc, 0:nt], start=(kc == 0), stop=(kc == KC - 1)), reads=["sq", "ones_b"], writes=psk(b))
                S.act(lambda e, b=b, nt=nt: e.activation(out=rstd[:, 0:nt], in_=ps[:, b, 0:nt], func=AF.Sqrt, bias=EPS, scale=1.0 / D), reads=psk(b), writes=["rstd"])
                S.dve(lambda e, nt=nt: e.reciprocal(out=rstd[:, 0:nt], in_=rstd[:, 0:nt]), reads=["rstd"], writes=["rstd"])
                S.dve(lambda e, t0=t0, nt=nt: e.tensor_tensor(out=tmp[:, :, 0:nt], in0=xT[:, :, t0:t0 + nt], in1=rstd[:, 0:nt].unsqueeze(1).to_broadcast([128, KC, nt]), op=ALU.mult), reads=["rstd"], writes=["tmp"])
                for kc in range(KC):
                    if t0 < TP:
                        S.act(lambda e, kc=kc, nt=nt: e.activation(out=yT[:, kc, 0:nt], in_=tmp[:, kc, 0:nt], func=AF.Identity, scale=Amod[:, n, kc, 0:1], bias=modT[:, shb + kc, 0:1]), reads=["tmp"], writes=["yT"])
                    else:
                        S.dve(lambda e, kc=kc: e.tensor_tensor(out=tmp[:, kc, 0:TS].rearrange("p (s j) -> p s j", j=4), in0=tmp[:, kc, 0:TS].rearrange("p (s j) -> p s j", j=4), in1=Amod[:, n, kc, 1:17].unsqueeze(2).to_broadcast([128, NSEQ, 4]), op=ALU.mult), reads=["tmp"], writes=["tmp"])
                        S.dve(lambda e, kc=kc: e.tensor_tensor(out=yT[:, kc, 0:TS].rearrange("p (s j) -> p s j", j=4), in0=tmp[:, kc, 0:TS].rearrange("p (s j) -> p s j", j=4), in1=modT[:, shb + kc, 1:17].unsqueeze(2).to_broadcast([128, NSEQ, 4]), op=ALU.add), reads=["tmp"], writes=["yT"])
                for sub in range(0, nt, 128):
                    rows = min(128, nt - sub)
                    o2 = oi % 2
                    oi += 1
                    for half in range(2):
                        b = ps_next()
                        for q in range(4):
                            kc = half * 4 + q
                            S.pe(lambda e, b=b, q=q, kc=kc, sub=sub, rows=rows: e.transpose(ps[0:rows, b, q * 128:(q + 1) * 128], yT[:, kc, sub:sub + rows], ident_f[:]), reads=["yT", "ident_f"], writes=psk(b))
                        if half == 0:
                            S.act(lambda e, b=b, o2=o2, rows=rows, half=half: e.copy(out=ytok[o2][0:rows, half * 512:(half + 1) * 512], in_=ps[0:rows, b, :]), reads=psk(b), writes=[("ytok", o2, half)])
                        else:
                            S.dve(lambda e, b=b, o2=o2, rows=rows, half=half: e.tensor_copy(out=ytok[o2][0:rows, half * 512:(half + 1) * 512], in_=ps[0:rows, b, :]), reads=psk(b), writes=[("ytok", o2, half)])
                    if t0 < TP:
                        dst = y_p[t0 + sub:t0 + sub + rows, :]
                    else:
                        dst = y_s[sub:sub + rows, :]
                    S.dma("sp", "yo%d" % o2, lambda e, dst=dst, o2=o2, rows=rows: e.dma_start(out=dst, in_=ytok[o2][0:rows, :]), reads=[("ytok", o2, 0), ("ytok", o2, 1)])
            S.barrier()

        K.__dict__.update(locals())
        G_ = globals()
        do_norm(0)
        import os
        if "do_retention" in G_ and not os.environ.get("SKIP_RET"):
            G_["do_retention"](K)
        do_norm(1)
        do_ffn(0)
        do_norm(2)
        if "do_gdn" in G_:
            G_["do_gdn"](K)
        do_norm(3)
        do_ffn(1)
        do_final()
        if "modT" in dbg_out:
            S.dma("sp", "dbg", lambda e: e.dma_start(out=dbg_out["modT"], in_=modT[:].rearrange("p a b -> p (a b)")))
        if "Amod" in dbg_out:
            S.dma("sp", "dbg", lambda e: e.dma_start(out=dbg_out["Amod"], in_=Amod[:].rearrange("p a b c -> p (a b c)")))
        S.emit(st)
    return nc


def do_retention(K):
    S = K.S; ps = K.ps; nc = K.nc
    a_f32 = K.a_f32; a_bf16 = K.a_bf16; ps_next = K.ps_next; psk = K.psk
    hT = K.hT; xT = K.xT; modT = K.modT
    ident_b = K.ident_b
    g = _gammas()
    K.a_reset()
    K.tmpg = a_f32(TS)
    tabs = a_f32(4, 512)
    qb = a_bf16(2, 512)
    kb = a_bf16(2, 512)
    t1 = a_f32(512)
    t2 = a_f32(512)
    vb = a_bf16(512)
    sg = a_f32(512)
    kh = a_bf16(256)
    im = a_bf16(128)
    og = a_bf16(512)
    ogT = a_bf16(4, 512)
    ss = a_f32(2)
    maskP = a_f32(128)
    maskS = a_f32(128)
    ksc = a_f32(2 * RET_H)
    segcol = a_f32(NSEQ)
    S32 = a_f32(2, 512)
    Sbf = a_bf16(2, 512)
    qf = a_f32(2, TS)
    qX = a_f32(2, NSEQ, TS)
    khm = [a_bf16(256) for _ in range(2)]
    Sin = [a_f32(2, 512) for _ in range(2)]
    Sout = a_f32(2, 512)
    segmask = a_f32(NSEQ, TS)
    S.dma("sp", "rc", lambda e: e.dma_start(out=ksc, in_=K.c_kscale), writes=["ksc"])
    S.dma("sp", "rc", lambda e: e.dma_start(out=segcol, in_=K.c_segcol), writes=["segcol"])
    S.dma("sp", "rc", lambda e: e.dma_start(out=segmask.rearrange("p a b -> p (a b)"), in_=K.c_segmask), writes=["segmask"])
    gbase = 16
    w_in = K.w_ret_in
    w_out = K.w_ret_out
    sctr = [0]
    for h in range(RET_H):
        wq, wqk = K.load_w_bf16(w_in, KC, 256, col0=h * 256, slot=0, off=0, key=("w", 0, "q"))
        wk, wkk = K.load_w_bf16(w_in, KC, 256, col0=1024 + h * 256, slot=0, off=2048, key=("w", 0, "k"))
        wv, wvk = K.load_w_bf16(w_in, KC, 512, col0=2048 + h * 512, slot=1)
        wg, wgk = K.load_w_bf16(w_in, KC, 512, col0=4096 + h * 512, slot=2)
        wo, wok = K.load_w_bf16(w_out, 4, 1024, row0=h * 512, slot=3)
        S.dma("sp", "rm", lambda e, h=h: e.dma_start(out=maskP, in_=K.c_retmask[h]), writes=["maskP"])
        S.dma("sp", "rm", lambda e, h=h: e.dma_start(out=maskS, in_=K.c_retmask[RET_H + h]), writes=["maskS"])
        for bi, (t0, nt) in enumerate(K.ALLBLOCKS):
            isS = t0 >= TP
            C = 64 if isS else 128
            for ti, (hh, cs_) in enumerate([(h, 0), (h, 1), (RET_H, 0), (RET_H, 1)]):
                S.dma("sp", "rt", lambda e, ti=ti, hh=hh, cs_=cs_, t0=t0, nt=nt: e.dma_start(out=tabs[:, ti, 0:nt], in_=K.c_rope[hh, cs_, :, t0:t0 + nt]), writes=[("tabs", ti)])
            for which, (wt, wkey, dst, tb) in enumerate([(wq, wqk, qb, 0), (wk, wkk, kb, 2)]):
                pb = [ps_next(), ps_next()]
                for dc in range(2):
                    for kc in range(KC):
                        S.pe(lambda e, b=pb[dc], kc=kc, dc=dc, wt=wt, t0=t0, nt=nt: e.matmul(ps[:, b, 0:nt], lhsT=wt[:, kc, dc * 128:(dc + 1) * 128], rhs=hT[:, kc, t0:t0 + nt], start=(kc == 0), stop=(kc == KC - 1)),
                             reads=[wkey], writes=psk(pb[dc]))
                p1 = ps[:, pb[0], 0:nt]
                p2 = ps[:, pb[1], 0:nt]
                cc = tabs[:, tb, 0:nt]
                sn = tabs[:, tb + 1, 0:nt]
                to_f = isS and which == 0
                d1 = qf[:, 0, :] if to_f else dst[:, 0, 0:nt]
                d2 = qf[:, 1, :] if to_f else dst[:, 1, 0:nt]
                S.dve(lambda e, p1=p1, cc=cc, nt=nt: e.tensor_tensor(out=t1[:, 0:nt], in0=p1, in1=cc, op=ALU.mult), reads=psk(pb[0]) + [("tabs", tb)], writes=["t1"])
                S.dve(lambda e, p2=p2, sn=sn, nt=nt: e.tensor_tensor(out=t2[:, 0:nt], in0=p2, in1=sn, op=ALU.mult), reads=psk(pb[1]) + [("tabs", tb + 1)], writes=["t2"])
                S.pool(lambda e, d1=d1, nt=nt: e.tensor_tensor(out=d1, in0=t1[:, 0:nt], in1=t2[:, 0:nt], op=ALU.subtract), reads=["t1", "t2"], writes=[("rd", which, 0)])
                S.dve(lambda e, p1=p1, sn=sn, nt=nt: e.tensor_tensor(out=t1[:, 0:nt], in0=p1, in1=sn, op=ALU.mult), reads=psk(pb[0]) + [("tabs", tb + 1)], writes=["t1"])
                S.dve(lambda e, p2=p2, cc=cc, nt=nt: e.tensor_tensor(out=t2[:, 0:nt], in0=p2, in1=cc, op=ALU.mult), reads=psk(pb[1]) + [("tabs", tb)], writes=["t2"])
                S.pool(lambda e, d2=d2, nt=nt: e.tensor_tensor(out=d2, in0=t1[:, 0:nt], in1=t2[:, 0:nt], op=ALU.add), reads=["t1", "t2"], writes=[("rd", which, 1)])
                if to_f:
                    S.act(lambda e: e.copy(out=qb[:, :, 0:TS], in_=qf[:, :, :]), reads=[("rd", 0, 0), ("rd", 0, 1)], writes=["qbS"])
            qkeys = [("rd", 0, 0), ("rd", 0, 1)] + (["qbS"] if isS else [])
            kkeys = [("rd", 1, 0), ("rd", 1, 1)]
            for c0 in range(0, nt, C):
                first = (not isS) and t0 == 0 and c0 == 0
                last = (not isS) and (t0 + c0 + C == TP)
                bv = ps_next()
                for kc in range(KC):
                    S.pe(lambda e, bv=bv, kc=kc, t0=t0, c0=c0, C=C: e.matmul(ps[0:C, bv, :], lhsT=hT[:, kc, t0 + c0:t0 + c0 + C], rhs=wv[:, kc, :], start=(kc == 0), stop=(kc == KC - 1)), reads=[wvk], writes=psk(bv))
                S.act(lambda e, bv=bv, C=C: e.copy(out=vb[0:C, :], in_=ps[0:C, bv, :]), reads=psk(bv), writes=["vb"])
                bg = ps_next()
                for kc in range(KC):
                    S.pe(lambda e, bg=bg, kc=kc, t0=t0, c0=c0, C=C: e.matmul(ps[0:C, bg, :], lhsT=hT[:, kc, t0 + c0:t0 + c0 + C], rhs=wg[:, kc, :], start=(kc == 0), stop=(kc == KC - 1)), reads=[wgk], writes=psk(bg))
                S.act(lambda e, bg=bg, C=C: e.activation(out=sg[0:C, :], in_=ps[0:C, bg, :], func=AF.Silu), reads=psk(bg), writes=["sg"])
                bk = ps_next()
                psb = ps[:, bk, :].bitcast(BF16)
                for dc in range(2):
                    S.pe(lambda e, psb=psb, dc=dc, c0=c0, C=C: e.transpose(psb[0:C, dc * 128:(dc + 1) * 128], kb[:, dc, c0:c0 + C], ident_b[:]), reads=kkeys + ["ident_b"], writes=psk(bk))
                kcol = (RET_H + h) if isS else h
                S.act(lambda e, psb=psb, C=C, kcol=kcol: e.activation(out=kh[0:C, :], in_=psb[0:C, 0:256], func=AF.Identity, scale=ksc[0:C, kcol:kcol + 1]), reads=psk(bk) + ["ksc"], writes=["kh"])
                bi_ = ps_next()
                for dc in range(2):
                    S.pe(lambda e, bi_=bi_, dc=dc, c0=c0, C=C: e.matmul(ps[0:C, bi_, 0:C], lhsT=kb[:, dc, c0:c0 + C], rhs=qb[:, dc, c0:c0 + C], start=(dc == 0), stop=(dc == 1)), reads=kkeys + qkeys, writes=psk(bi_))
                mk = maskS if isS else maskP
                S.dve(lambda e, bi_=bi_, C=C, mk=mk: e.tensor_tensor(out=im[0:C, 0:C], in0=ps[0:C, bi_, 0:C], in1=mk[0:C, 0:C], op=ALU.mult), reads=psk(bi_) + ["maskP", "maskS"], writes=["im"])
                bo = ps_next()
                has_inter = isS or not first
                S.pe(lambda e, bo=bo, C=C, has_inter=has_inter: e.matmul(ps[0:C, bo, :], lhsT=im[0:C, 0:C], rhs=vb[0:C, :], start=True, stop=not has_inter), reads=["im", "vb"], writes=psk(bo))
                if not isS:
                    if not first:
                        for dc in range(2):
                            S.pe(lambda e, bo=bo, dc=dc, c0=c0, C=C: e.matmul(ps[0:C, bo, :], lhsT=qb[:, dc, c0:c0 + C], rhs=Sbf[:, dc, :], start=False, stop=(dc == 1)), reads=qkeys + [("Sbf", dc)], writes=psk(bo))
                    for dc in range(2):
                        bs = ps_next()
                        S.pe(lambda e, bs=bs, dc=dc, C=C: e.matmul(ps[:, bs, :], lhsT=kh[0:C, dc * 128:(dc + 1) * 128], rhs=vb[0:C, :], start=True, stop=True), reads=["kh", "vb"], writes=psk(bs))
                        if first:
                            S.dve(lambda e, bs=bs, dc=dc: e.tensor_copy(out=S32[:, dc, :], in_=ps[:, bs, :]), reads=psk(bs), writes=[("S32", dc)])
                        else:
                            S.dve(lambda e, bs=bs, dc=dc, gc=float(g[h] ** 128): e.scalar_tensor_tensor(out=S32[:, dc, :], in0=S32[:, dc, :], scalar=gc, in1=ps[:, bs, :], op0=ALU.mult, op1=ALU.add), reads=psk(bs) + [("S32", dc)], writes=[("S32", dc)])
                        if last:
                            S.dma("sp", "rpo", lambda e, dc=dc, h=h: e.dma_start(out=K.ret_p[h, dc * 128:(dc + 1) * 128, :], in_=S32[:, dc, :]), reads=[("S32", dc)])
                        else:
                            S.act(lambda e, dc=dc: e.copy(out=Sbf[:, dc, :], in_=S32[:, dc, :]), reads=[("S32", dc)], writes=[("Sbf", dc)])
                else:
                    K.P.reserved = {bo}
                    for dc in range(2):
                        S.dve(lambda e, dc=dc: e.tensor_tensor(out=qX[:, dc, :, :], in0=qf[:, dc, :].unsqueeze(1).to_broadcast([128, NSEQ, TS]), in1=segmask[:, :, :], op=ALU.mult), reads=[("rd", 0, dc), "segmask"], writes=[("qX", dc)])
                    for s_ in range(NSEQ):
                        i2 = s_ % 2
                        S.dma("sp", "sin%d" % i2, lambda e, i2=i2, s_=s_, h=h: e.dma_start(out=Sin[i2], in_=K.s_ret[s_, h].rearrange("(dc p) v -> p dc v", p=128)), writes=[("Sin", i2)])
                        for dc in range(2):
                            S.pe(lambda e, bo=bo, dc=dc, s_=s_, i2=i2: e.matmul(ps[0:TS, bo, :], lhsT=qX[:, dc, s_, :], rhs=Sin[i2][:, dc, :], start=False, stop=(s_ == NSEQ - 1 and dc == 1)), reads=[("qX", dc), ("Sin", i2)], writes=psk(bo))
                        S.dve(lambda e, i2=i2, s_=s_: e.tensor_scalar(out=khm[i2][0:TS, :], in0=kh[0:TS, :], scalar1=segcol[0:TS, s_:s_ + 1], scalar2=None, op0=ALU.mult), reads=["kh", "segcol"], writes=[("khm", i2)])
                        for dc in range(2):
                            bs = ps_next()
                            S.pe(lambda e, bs=bs, dc=dc, i2=i2: e.matmul(ps[:, bs, :], lhsT=khm[i2][0:TS, dc * 128:(dc + 1) * 128], rhs=vb[0:TS, :], start=True, stop=True), reads=[("khm", i2), "vb"], writes=psk(bs))
                            S.dve(lambda e, bs=bs, dc=dc, i2=i2, gc=float(g[h] ** 4): e.scalar_tensor_tensor(out=Sout[:, dc, :], in0=Sin[i2][:, dc, :], scalar=gc, in1=ps[:, bs, :], op0=ALU.mult, op1=ALU.add), reads=psk(bs) + [("Sin", i2)], writes=[("Sout", dc)])
                        S.dma("sp", "sout", lambda e, s_=s_, h=h: e.dma_start(out=K.ret_s[s_, h].rearrange("(dc p) v -> p dc v", p=128), in_=Sout), reads=[("Sout", 0), ("Sout", 1)], writes=[])
                    K.P.reserved = set()
                _ret_tail(K, h, bo, C, c0, sg, og, ogT, ss, t1)
            for dc8 in range(KC):
                b = ps_next()
                for ec in range(4):
                    S.pe(lambda e, b=b, ec=ec, dc8=dc8, nt=nt: e.matmul(ps[:, b, 0:nt], lhsT=wo[:, ec, dc8 * 128:(dc8 + 1) * 128], rhs=ogT[:, ec, 0:nt], start=(ec == 0), stop=(ec == 3)), reads=[wok, "ogT"], writes=psk(b))
                K.resid_update(ps[:, b, 0:nt], gbase, dc8, ("S" if isS else bi), t0, nt, psk(b))
        S.barrier()
    S.barrier()


def _ret_tail(K, h, bo, C, c0, sg, og, ogT, ss, junk):
    S = K.S; ps = K.ps
    S.act(lambda e: e.activation(out=junk[0:C, :], in_=ps[0:C, bo, :], func=AF.Square, accum_out=ss[0:C, 0:1]), reads=K.psk(bo), writes=["t1", "ss"])
    S.act(lambda e: e.activation(out=ss[0:C, 1:2], in_=ss[0:C, 0:1], func=AF.Sqrt, bias=EPS, scale=1.0 / 512), reads=["ss"], writes=["ss2"])
    S.dve(lambda e: e.reciprocal(out=ss[0:C, 1:2], in_=ss[0:C, 1:2]), reads=["ss2"], writes=["ss2"])
    S.dve(lambda e: e.scalar_tensor_tensor(out=og[0:C, :], in0=ps[0:C, bo, :], scalar=ss[0:C, 1:2], in1=sg[0:C, :], op0=ALU.mult, op1=ALU.mult), reads=K.psk(bo) + ["ss2", "sg"], writes=["og"])
    bt = K.ps_next()
    psb = ps[:, bt, :].bitcast(BF16)
    for ec in range(4):
        S.pe(lambda e, ec=ec: e.transpose(psb[:, ec * C:(ec + 1) * C], og[0:C, ec * 128:(ec + 1) * 128], K.ident_b[0:C, 0:C]), reads=["og", "ident_b"], writes=K.psk(bt))
    S.act(lambda e: e.copy(out=ogT[:, :, c0:c0 + C], in_=psb[:, 0:4 * C].rearrange("p (a b) -> p a b", a=4)), reads=K.psk(bt), writes=["ogT"])


def do_gdn(K):
    S = K.S; ps = K.ps
    a_f32 = K.a_f32; a_bf16 = K.a_bf16; ps_next = K.ps_next; psk = K.psk
    hT = K.hT; ident_f = K.ident_f; ident_b = K.ident_b; ones_b = K.ones_b
    w_in = K.w_gdn_in; w_out = K.w_gdn_out
    K.a_reset()
    K.tmpg = a_f32(TS)
    gm = a_f32(8, 128)
    segmask = a_f32(NSEQ, TS)
    segcol = a_f32(NSEQ)
    beta_all = a_f32(17, 16); g_all = a_f32(17, 16); negeG_all = a_f32(17, 16); kdec_all = a_f32(17, 16)
    gv = a_f32(32); negA = a_f32(16); gnw = a_f32(128)
    wcT = a_f32(32, 4)
    ones128 = a_f32(128)
    wba = a_bf16(KC, 32)
    Gx = a_f32(3 + 512); acc = a_f32(512); acc2 = a_f32(512); cch = a_f32(4, 3); gsT = a_f32(4, 48); Gxs = a_f32(NSEQ, 7)
    tailP = a_f32(4, 3); tailS = a_f32(4, NSEQ, 3)
    vs = a_f32(2, 512)
    sqb = a_bf16(512); rs = a_f32(512)
    qn = a_bf16(512); kn = a_bf16(512); qnf = a_f32(TS); knf = a_f32(TS)
    Bm = a_f32(2, 128); E = a_f32(2, 128); E2 = a_f32(2, 128); DTm = a_f32(2, 128)
    U = a_bf16(2, 128); UT = a_bf16(2, 128)
    UoM = [a_bf16(2, 128) for _ in range(3)]; UoTM = [a_bf16(2, 128) for _ in range(3)]
    Nb = [a_bf16(2, 128) for _ in range(2)]; Pb = [a_bf16(2, 128) for _ in range(2)]; PTb = [a_bf16(2, 128) for _ in range(2)]
    NTb = [a_bf16(2, 128) for _ in range(2)]; bm = a_bf16(4, 128)
    ktok = a_bf16(128); xv = a_bf16(2, 128); vnew = a_bf16(2, 128)
    Sbf = a_bf16(2, 128); og = a_bf16(256); ogT = a_bf16(2, 512)
    S32 = a_f32(2, 128); ss = a_f32(8)
    X = a_f32(NSEQ, TS); qgf = a_f32(TS)
    kdm = a_bf16(128); Sin = [a_f32(128) for _ in range(2)]; Sout = a_f32(128)
    ostg = acc
    eGl = [a_f32(2), a_f32(2)]
    w3f = K.wring[:, 3, :].bitcast(F32)
    w3b = K.wring[:, 3, :]
    szn2 = [w3f[:, 0:256], w3f[:, 256:512]]
    vtok2 = [w3f[:, 512:768].rearrange("p (a b) -> p a b", a=2), w3f[:, 768:1024].rearrange("p (a b) -> p a b", a=2)]

    def _b3(i):
        return w3b[:, 2048 + i * 256:2048 + (i + 1) * 256].rearrange("p (a b) -> p a b", a=2)
    Nfin = [_b3(0), _b3(1)]; attnT2 = [_b3(2), _b3(3)]; qg2 = [_b3(4), _b3(5)]; kd2 = [_b3(6), _b3(7)]
    mark = K.A.off

    S.dma("sp", "gc", lambda e: e.dma_start(out=gm.rearrange("p a b -> p (a b)"), in_=K.c_gmask), writes=["gm"])
    S.dma("sp", "gc", lambda e: e.dma_start(out=segmask.rearrange("p a b -> p (a b)"), in_=K.c_segmask), writes=["segmask"])
    S.dma("sp", "gc", lambda e: e.dma_start(out=segcol, in_=K.c_segcol), writes=["segcol"])
    S.dma("pool", "gcb", lambda e: e.dma_start(out=bm.rearrange("p a b -> p (a b)"), in_=K.c_bmask), writes=["bm"])
    S.dma("sp", "gc", lambda e: e.dma_start(out=gv, in_=K.gdn_vec.partition_broadcast(128)), writes=["gv"])
    S.dma("sp", "gc", lambda e: e.dma_start(out=gnw, in_=K.gdn_norm.partition_broadcast(128)), writes=["gnw"])
    S.dve(lambda e: e.memset(ones128, 1.0), writes=["ones128"])
    S.act(lambda e: e.activation(out=negA, in_=gv[:, 0:16], func=AF.Exp), reads=["gv"], writes=["negA"])
    S.dve(lambda e: e.tensor_scalar(out=negA, in0=negA, scalar1=-1.0, scalar2=None, op0=ALU.mult), reads=["negA"], writes=["negA"])
    TRIU = {False: gm[:, 0, :], True: gm[:, 4, :]}
    SU = {False: gm[:, 1, :], True: gm[:, 5, :]}
    INCL = {False: gm[:, 2, :], True: gm[:, 6, :]}
    STRICT = {False: gm[:, 3, :], True: gm[:, 7, :]}
    import os
    STG = int(os.environ.get('GDN_STAGE', 99))
    if STG == 0:
        S.barrier(); return
    Xflat = X.rearrange("p a b -> p (a b)")
    wst = [Xflat[0:4, 0:512], Xflat[0:4, 512:1024]]
    bw = ps_next()
    for pc in range(8):
        S.dma("sp", "wst%d" % (pc % 2), lambda e, pc=pc: e.dma_start(out=wst[pc % 2], in_=K.w_gconv[:, pc * 512:(pc + 1) * 512]), writes=[("wst", pc % 2)])
        for q in range(4):
            cidx = pc * 4 + q
            S.pe(lambda e, pc=pc, q=q, cidx=cidx: e.transpose(ps[:, bw, cidx * 4:(cidx + 1) * 4], wst[pc % 2][:, q * 128:(q + 1) * 128], ident_f[0:4, 0:4]), reads=[("wst", pc % 2), "ident_f"], writes=psk(bw))
    S.dve(lambda e: e.tensor_copy(out=wcT.rearrange("p a b -> p (a b)"), in_=ps[:, bw, 0:128]), reads=psk(bw), writes=["wcT"])
    if STG == 1:
        S.barrier(); return
    src = w_in[:, 6144:6176].rearrange("(k p) c -> p k c", p=128)
    S.dma("pool", "wba", lambda e: e.dma_start(out=wba, in_=src), writes=["wba"])
    if STG == 2:
        S.barrier(); return
    tiles = [(t * 128, 128, False) for t in range(16)] + [(TP, TS, True)]
    for tl, (t0, C, isS) in enumerate(tiles):
        pb = ps_next()
        for kc in range(KC):
            S.pe(lambda e, pb=pb, kc=kc, t0=t0, C=C: e.matmul(ps[0:C, pb, 0:32], lhsT=hT[:, kc, t0:t0 + C], rhs=wba[:, kc, :], start=(kc == 0), stop=(kc == KC - 1)), reads=["wba"], writes=psk(pb))
        S.act(lambda e, pb=pb, C=C, tl=tl: e.activation(out=beta_all[0:C, tl, :], in_=ps[0:C, pb, 0:16], func=AF.Sigmoid), reads=psk(pb), writes=[("beta", tl)])
        S.dve(lambda e, pb=pb, C=C, tl=tl: e.tensor_tensor(out=g_all[0:C, tl, :], in0=ps[0:C, pb, 16:32], in1=gv[0:C, 16:32], op=ALU.add), reads=psk(pb) + ["gv"], writes=[("g", tl)])
        S.act(lambda e, C=C, tl=tl: e.activation(out=g_all[0:C, tl, :], in_=g_all[0:C, tl, :], func=AF.Exp), reads=[("g", tl)], writes=[("g", tl)])
        S.act(lambda e, C=C, tl=tl: e.activation(out=g_all[0:C, tl, :], in_=g_all[0:C, tl, :], func=AF.Ln, bias=1.0), reads=[("g", tl)], writes=[("g", tl)])
        S.dve(lambda e, C=C, tl=tl: e.tensor_tensor(out=g_all[0:C, tl, :], in0=g_all[0:C, tl, :], in1=negA[0:C, :], op=ALU.mult), reads=[("g", tl), "negA"], writes=[("g", tl)])
        pg = ps_next()
        S.pe(lambda e, pg=pg, C=C, tl=tl, isS=isS: e.matmul(ps[0:C, pg, 0:16], lhsT=TRIU[isS][0:C, 0:C], rhs=g_all[0:C, tl, :], start=True, stop=True), reads=[("g", tl), "gm"], writes=psk(pg))
        S.pe(lambda e, pg=pg, C=C, tl=tl, isS=isS: e.matmul(ps[0:C, pg, 16:32], lhsT=SU[isS][0:C, 0:C], rhs=g_all[0:C, tl, :], start=True, stop=True), reads=[("g", tl), "gm"], writes=psk(pg))
        S.act(lambda e, pg=pg, C=C, tl=tl: e.activation(out=negeG_all[0:C, tl, :], in_=ps[0:C, pg, 0:16], func=AF.Exp), reads=psk(pg), writes=[("negeG", tl)])
        S.dve(lambda e, C=C, tl=tl: e.tensor_scalar(out=negeG_all[0:C, tl, :], in0=negeG_all[0:C, tl, :], scalar1=-1.0, scalar2=None, op0=ALU.mult), reads=[("negeG", tl)], writes=[("negeG", tl)])
        S.act(lambda e, pg=pg, C=C, tl=tl: e.activation(out=kdec_all[0:C, tl, :], in_=ps[0:C, pg, 16:32], func=AF.Exp), reads=psk(pg), writes=[("kdec", tl)])
    S.barrier()
    K.A.off = mark
    gbase = 48 + 16

    import os
    for hk in range(int(os.environ.get("GDN_HK0", 0)), int(os.environ.get("GDN_NHK", GDN_HK))):
        hv0 = 2 * hk
        wq, wqk = K.load_w_bf16(w_in, KC, 128, col0=hk * 128, slot=0, off=0, key=("w", 0, "q"))
        wk, wkk = K.load_w_bf16(w_in, KC, 128, col0=1024 + hk * 128, slot=0, off=1024, key=("w", 0, "k"))
        wv, wvk = K.load_w_bf16(w_in, KC, 256, col0=2048 + hk * 256, slot=1, off=0, key=("w", 1, "v"))
        wz, wzk = K.load_w_bf16(w_in, KC, 256, col0=4096 + hk * 256, slot=1, off=2048, key=("w", 1, "z"))
        wo, wok = K.load_w_bf16(w_out, 2, 1024, row0=hk * 256, slot=2)
        cids = [hk, 8 + hk, 16 + 2 * hk, 17 + 2 * hk]
        CUT = os.environ.get('GDN_CUT', '')
        if hk >= 1 and 'D' in CUT:
            S.barrier(); continue
        gst = a_f32(512, parts=48)
        K.A.off = mark
        if not (hk >= 1 and 'H' in CUT):
            for ci, cid in enumerate(cids):
                S.dma("sp", "gst", lambda e, ci=ci, cid=cid, gst=gst: e.dma_start(out=gst[:, ci * 128:(ci + 1) * 128], in_=K.s_gconv[:, cid * 128:(cid + 1) * 128]), writes=["gst"])
            bq = ps_next()
            for ci in range(4):
                S.pe(lambda e, ci=ci, bq=bq, gst=gst: e.transpose(ps[:, bq, ci * 48:(ci + 1) * 48], gst[:, ci * 128:(ci + 1) * 128], ident_f[0:48, 0:48]), reads=["gst", "ident_f"], writes=psk(bq))
            S.dve(lambda e, bq=bq: e.tensor_copy(out=gsT.rearrange("p a b -> p (a b)"), in_=ps[:, bq, 0:192]), reads=psk(bq), writes=["gsT"])
        S.dve(lambda e: e.memset(cch, 0.0), writes=["cch"])
        if hk >= 1 and 'E' in CUT:
            S.barrier(); continue
        for bi, (t0, nt) in enumerate(K.ALLBLOCKS):
            isS = t0 >= TP
            C = 64 if isS else 128
            lastblk = (t0 + nt == TP)
            for ci, (wt, wkey, col) in enumerate([(wq, wqk, 0), (wk, wkk, 0), (wv, wvk, 0), (wv, wvk, 128)]):
                pp = ps_next()
                for kc in range(KC):
                    S.pe(lambda e, pp=pp, kc=kc, wt=wt, col=col, t0=t0, nt=nt: e.matmul(ps[:, pp, 0:nt], lhsT=wt[:, kc, col:col + 128], rhs=hT[:, kc, t0:t0 + nt], start=(kc == 0), stop=(kc == KC - 1)), reads=[wkey], writes=psk(pp))
                cid = cids[ci]
                wc = [wcT[:, cid, i:i + 1] for i in range(4)]
                accb = acc if ci % 2 == 0 else acc2
                ak = ("acc", ci % 2)
                dst = accb[:, 0:nt] if ci < 2 else vs[:, ci - 2, 0:nt]
                if not isS:
                    S.dve(lambda e, ci=ci: e.tensor_copy(out=Gx[:, 0:3], in_=cch[:, ci, :]), reads=["cch"], writes=["Gx"])
                    S.act(lambda e, pp=pp, nt=nt: e.copy(out=Gx[:, 3:3 + nt], in_=ps[:, pp, 0:nt]), reads=psk(pp), writes=["Gx"])
                    S.dve(lambda e, ci=ci, nt=nt: e.tensor_copy(out=cch[:, ci, :], in_=Gx[:, nt:nt + 3]), reads=["Gx"], writes=["cch"])
                    if lastblk:
                        S.dve(lambda e, ci=ci, nt=nt: e.tensor_copy(out=tailP[:, ci, :], in_=Gx[:, nt:nt + 3]), reads=["Gx"], writes=["tailP"])
                    x3 = [Gx[:, i:i + nt] for i in range(4)]
                    a_ = accb[:, 0:nt]
                    d_ = dst
                    gk = ["Gx"]
                else:
                    S.dve(lambda e, ci=ci: e.tensor_copy(out=Gxs[:, :, 0:3], in_=gsT[:, ci, :].rearrange("p (s r) -> p s r", r=3)), reads=["gsT"], writes=["Gxs"])
                    S.act(lambda e, pp=pp: e.copy(out=Gxs[:, :, 3:7], in_=ps[:, pp, 0:TS].rearrange("p (s j) -> p s j", j=4)), reads=psk(pp), writes=["Gxs"])
                    S.dve(lambda e, ci=ci: e.tensor_copy(out=tailS[:, ci, :, :], in_=Gxs[:, :, 4:7]), reads=["Gxs"], writes=["tailS"])
                    x3 = [Gxs[:, :, i:i + 4] for i in range(4)]
                    a_ = accb[:, 0:TS].rearrange("p (s j) -> p s j", j=4)
                    d_ = dst.rearrange("p (s j) -> p s j", j=4)
                    gk = ["Gxs"]
                S.act(lambda e, a_=a_, x3=x3, wc=wc: e.activation(out=a_, in_=x3[3], func=AF.Identity, scale=wc[3]), reads=gk + ["wcT"], writes=[ak])
                for i in (2, 1, 0):
                    S.dve(lambda e, a_=a_, x3=x3, wc=wc, i=i: e.scalar_tensor_tensor(out=a_, in0=x3[i], scalar=wc[i], in1=a_, op0=ALU.mult, op1=ALU.add), reads=gk + [ak], writes=[ak])
                S.act(lambda e, a_=a_, d_=d_: e.activation(out=d_, in_=a_, func=AF.Silu), reads=[ak], writes=([ak, ("cv", ci)] if ci < 2 else [("cv", ci)]))
                if ci < 2:
                    S.act(lambda e, nt=nt, accb=accb: e.activation(out=sqb[:, 0:nt], in_=accb[:, 0:nt], func=AF.Square), reads=[ak], writes=["sqb"])
                    pn = ps_next()
                    S.pe(lambda e, pn=pn, nt=nt: e.matmul(ps[:, pn, 0:nt], lhsT=ones_b[:], rhs=sqb[:, 0:nt], start=True, stop=True), reads=["sqb", "ones_b"], writes=psk(pn))
                    S.act(lambda e, pn=pn, nt=nt: e.activation(out=rs[:, 0:nt], in_=ps[:, pn, 0:nt], func=AF.Sqrt, bias=EPS, scale=1.0), reads=psk(pn), writes=["rs"])
                    S.dve(lambda e, nt=nt: e.reciprocal(out=rs[:, 0:nt], in_=rs[:, 0:nt]), reads=["rs"], writes=["rs"])
                    dn = qn if ci == 0 else kn
                    dnf = qnf if ci == 0 else knf
                    sc = (128.0 ** -0.5) if ci == 0 else 1.0
                    if isS:
                        S.dve(lambda e, dnf=dnf, sc=sc, accb=accb: e.scalar_tensor_tensor(out=dnf[:, :], in0=accb[:, 0:TS], scalar=sc, in1=rs[:, 0:TS], op0=ALU.mult, op1=ALU.mult), reads=[ak, "rs"], writes=[("nf", ci)])
                        S.act(lambda e, dn=dn, dnf=dnf: e.copy(out=dn[:, 0:TS], in_=dnf[:, :]), reads=[("nf", ci)], writes=[("n", ci)])
                    else:
                        S.dve(lambda e, dn=dn, sc=sc, nt=nt, accb=accb: e.scalar_tensor_tensor(out=dn[:, 0:nt], in0=accb[:, 0:nt], scalar=sc, in1=rs[:, 0:nt], op0=ALU.mult, op1=ALU.mult), reads=[ak, "rs"], writes=[("n", ci)])
            def chunk_gen(c0, pb):
                tl = (t0 + c0) // 128
                first = (not isS) and t0 == 0 and c0 == 0
                last = (not isS) and (t0 + c0 + C == TP)
                L = 1 if isS else 6
                bvt = ps_next()
                for e_ in range(2):
                    S.pe(lambda e, e_=e_, bvt=bvt, c0=c0, C=C: e.transpose(ps[0:C, bvt, e_ * 128:(e_ + 1) * 128], vs[:, e_, c0:c0 + C], ident_f[:]), reads=[("cv", 2 + e_), "ident_f"], writes=psk(bvt))
                    yield
                S.act(lambda e, bvt=bvt, C=C: e.copy(out=vtok2[pb][0:C].rearrange("p a b -> p (a b)"), in_=ps[0:C, bvt, 0:256]), reads=psk(bvt), writes=[("vtok", pb)])
                yield
                bz = ps_next()
                for kc in range(KC):
                    S.pe(lambda e, bz=bz, kc=kc, t0=t0, c0=c0, C=C: e.matmul(ps[0:C, bz, 0:256], lhsT=hT[:, kc, t0 + c0:t0 + c0 + C], rhs=wz[:, kc, :], start=(kc == 0), stop=(kc == KC - 1)), reads=[wzk], writes=psk(bz))
                    yield
                S.act(lambda e, bz=bz, C=C: e.activation(out=szn2[pb][0:C, :], in_=ps[0:C, bz, 0:256], func=AF.Silu), reads=psk(bz), writes=[("szn", pb)])
                yield
                S.dve(lambda e, C=C: e.tensor_tensor(out=szn2[pb][0:C, :].rearrange("p (a b) -> p a b", a=2), in0=szn2[pb][0:C, :].rearrange("p (a b) -> p a b", a=2), in1=gnw[0:C, :].unsqueeze(1).to_broadcast([C, 2, 128]), op=ALU.mult), reads=[("szn", pb), "gnw"], writes=[("szn", pb)])
                yield
                bkt = ps_next()
                pkb = ps[:, bkt, :].bitcast(BF16)
                S.pe(lambda e, pkb=pkb, c0=c0, C=C: e.transpose(pkb[0:C, 0:128], kn[:, c0:c0 + C], ident_b[:]), reads=[("n", 1), "ident_b"], writes=psk(bkt))
                yield
                S.act(lambda e, pkb=pkb, C=C: e.copy(out=ktok[0:C, :], in_=pkb[0:C, 0:128]), reads=psk(bkt), writes=["ktok"])
                yield
                bkk = ps_next()
                S.pe(lambda e, bkk=bkk, c0=c0, C=C: e.matmul(ps[0:C, bkk, 0:C], lhsT=kn[:, c0:c0 + C], rhs=kn[:, c0:c0 + C], start=True, stop=True), reads=[("n", 1)], writes=psk(bkk))
                yield
                S.pe(lambda e, bkk=bkk, c0=c0, C=C: e.matmul(ps[0:C, bkk, 128:128 + C], lhsT=kn[:, c0:c0 + C], rhs=qn[:, c0:c0 + C], start=True, stop=True), reads=[("n", 0), ("n", 1)], writes=psk(bkk))
                yield
                bd = ps_next()
                bd2 = ps_next()
                for e_ in range(2):
                    hv = hv0 + e_
                    S.dve(lambda e, e_=e_, hv=hv, C=C, tl=tl, isS=isS: e.tensor_scalar(out=Bm[0:C, e_, 0:C], in0=TRIU[isS][0:C, 0:C], scalar1=g_all[0:C, tl, hv:hv + 1], scalar2=None, op0=ALU.mult), reads=["gm"], writes=[("Bm", e_)])
                    yield
                    S.pe(lambda e, e_=e_, bd=bd, C=C, isS=isS: e.matmul(ps[0:C, bd, e_ * 128:e_ * 128 + C], lhsT=SU[isS][0:C, 0:C], rhs=Bm[0:C, e_, 0:C], start=True, stop=True), reads=[("Bm", e_), "gm"], writes=psk(bd))
                    yield
                    S.pe(lambda e, e_=e_, bd2=bd2, C=C: e.matmul(ps[:, bd2, e_ * 128:e_ * 128 + C], lhsT=ones128[0:C, :], rhs=Bm[0:C, e_, 0:C], start=True, stop=True), reads=[("Bm", e_), "ones128"], writes=psk(bd2))
                    yield
                S.act(lambda e, bd=bd, C=C: e.activation(out=E[0:C, :, 0:C], in_=ps[0:C, bd, 0:256].rearrange("p (a b) -> p a b", a=2)[:, :, 0:C], func=AF.Exp), reads=psk(bd), writes=["E"])
                yield
                S.act(lambda e, bd2=bd2, C=C: e.activation(out=E2[:, :, 0:C], in_=ps[:, bd2, 0:256].rearrange("p (a b) -> p a b", a=2)[:, :, 0:C], func=AF.Exp), reads=psk(bd2), writes=["E2"])
                yield
                S.dve(lambda e, C=C, isS=isS: e.tensor_tensor(out=DTm[0:C, :, 0:C], in0=E[0:C, :, 0:C], in1=INCL[isS][0:C, 0:C].unsqueeze(1).to_broadcast([C, 2, C]), op=ALU.mult), reads=["E", "gm"], writes=["DTm"])
                yield
                S.dve(lambda e, C=C, isS=isS: e.tensor_tensor(out=E[0:C, :, 0:C], in0=E[0:C, :, 0:C], in1=STRICT[isS][0:C, 0:C].unsqueeze(1).to_broadcast([C, 2, C]), op=ALU.mult), reads=["E", "gm", "DTm"], writes=["E"])
                yield
                for e_ in range(2):
                    hv = hv0 + e_
                    S.dve(lambda e, e_=e_, hv=hv, bkk=bkk, C=C, tl=tl: e.scalar_tensor_tensor(out=U[0:C, e_, 0:C], in0=ps[0:C, bkk, 0:C], scalar=beta_all[0:C, tl, hv:hv + 1], in1=E[0:C, e_, 0:C], op0=ALU.mult, op1=ALU.mult), reads=psk(bkk) + ["E"], writes=[("U", e_)])
                    yield
                    S.dve(lambda e, e_=e_, bkk=bkk, C=C: e.tensor_tensor(out=attnT2[pb][0:C, e_, 0:C], in0=ps[0:C, bkk, 128:128 + C], in1=DTm[0:C, e_, 0:C], op=ALU.mult), reads=psk(bkk) + ["DTm"], writes=[("attnT", e_, pb)])
                    yield
                    if isS:
                        pass
                    else:
                        S.pool(lambda e, e_=e_, c0=c0, C=C: e.tensor_tensor(out=qg2[pb][:, e_, 0:C], in0=qn[:, c0:c0 + C], in1=E2[:, e_, 0:C], op=ALU.mult), reads=[("n", 0), "E2"], writes=[("qg", e_, pb)])
                        yield
                    S.dve(lambda e, e_=e_, hv=hv, C=C, tl=tl: e.tensor_scalar(out=kd2[pb][0:C, e_, :], in0=ktok[0:C, :], scalar1=kdec_all[0:C, tl, hv:hv + 1], scalar2=None, op0=ALU.mult), reads=["ktok"], writes=[("kd", e_, pb)])
                    yield
                but = ps_next()
                pub = ps[:, but, :].bitcast(BF16)
                for e_ in range(2):
                    S.pe(lambda e, e_=e_, pub=pub, C=C: e.transpose(pub[0:C, e_ * 128:e_ * 128 + C], U[0:C, e_, 0:C], ident_b[0:C, 0:C]), reads=[("U", e_), "ident_b"], writes=psk(but))
                    yield
                S.act(lambda e, pub=pub, C=C: e.copy(out=UT[0:C, :, 0:C], in_=pub[0:C, 0:256].rearrange("p (a b) -> p a b", a=2)[:, :, 0:C]), reads=psk(but), writes=["UT"])
                yield
                if isS:
                    S.dve(lambda e, C=C: e.tensor_tensor(out=Nb[0][0:C, :, 0:C], in0=ident_f[0:C, 0:C].unsqueeze(1).to_broadcast([C, 2, C]), in1=U[0:C, :, 0:C], op=ALU.subtract), reads=[("U", 0), ("U", 1), "ident_f"], writes=[("N", 0)])
                    yield
                    Pprev, PTprev, Pk_, PTk_ = U, UT, ["U0", "U1"], ["UT"]
                    Pkeys_prev = [("U", 0), ("U", 1)]
                    PTkeys_prev = ["UT"]
                    ni = 0
                    for lv in range(1, L + 1):
                        pi = lv % 2
                        need_P = lv < L
                        if need_P:
                            b1 = ps_next()
                            for e_ in range(2):
                                S.pe(lambda e, e_=e_, b1=b1, C=C, PTprev=PTprev, Pprev=Pprev: e.matmul(ps[0:C, b1, e_ * 128:e_ * 128 + C], lhsT=PTprev[0:C, e_, 0:C], rhs=Pprev[0:C, e_, 0:C], start=True, stop=True), reads=Pkeys_prev + PTkeys_prev, writes=psk(b1))
                                yield
                            S.act(lambda e, b1=b1, C=C, pi=pi: e.copy(out=Pb[pi][0:C, :, 0:C], in_=ps[0:C, b1, 0:256].rearrange("p (a b) -> p a b", a=2)[:, :, 0:C]), reads=psk(b1), writes=[("P", pi)])
                            yield
                        b2 = ps_next()
                        for e_ in range(2):
                            S.pe(lambda e, e_=e_, b2=b2, C=C, PTprev=PTprev, Pprev=Pprev: e.matmul(ps[0:C, b2, e_ * 128:e_ * 128 + C], lhsT=Pprev[0:C, e_, 0:C], rhs=PTprev[0:C, e_, 0:C], start=True, stop=True), reads=Pkeys_prev + PTkeys_prev, writes=psk(b2))
                            yield
                        S.act(lambda e, b2=b2, C=C, pi=pi: e.copy(out=PTb[pi][0:C, :, 0:C], in_=ps[0:C, b2, 0:256].rearrange("p (a b) -> p a b", a=2)[:, :, 0:C]), reads=psk(b2), writes=[("PT", pi)])
                        yield
                        b3 = ps_next()
                        for e_ in range(2):
                            S.pe(lambda e, e_=e_, b3=b3, C=C, pi=pi, ni=ni: e.matmul(ps[0:C, b3, e_ * 128:e_ * 128 + C], lhsT=PTb[pi][0:C, e_, 0:C], rhs=Nb[ni][0:C, e_, 0:C], start=True, stop=True), reads=[("PT", pi), ("N", ni)], writes=psk(b3))
                            yield
                        S.dve(lambda e, b3=b3, C=C, ni=ni: e.tensor_tensor(out=Nb[1 - ni][0:C, :, 0:C], in0=ps[0:C, b3, 0:256].rearrange("p (a b) -> p a b", a=2)[:, :, 0:C], in1=Nb[ni][0:C, :, 0:C], op=ALU.add), reads=psk(b3) + [("N", ni)], writes=[("N", 1 - ni)])
                        yield
                        ni = 1 - ni
                        Pprev, PTprev = Pb[pi], PTb[pi]
                        Pkeys_prev = [("P", pi)]
                        PTkeys_prev = [("PT", pi)]

                else:
                    def _ev2(b, C=C):
                        return ps[0:C, b, 0:256].rearrange("p (a b) -> p a b", a=2)[:, :, 0:C]
                    idb = ident_f[0:C, 0:C].unsqueeze(1).to_broadcast([C, 2, C])
                    S.dve(lambda e: e.tensor_tensor(out=Pb[0][:, :, :], in0=U[:, :, :], in1=bm[:, 0, :].unsqueeze(1).to_broadcast([128, 2, 128]), op=ALU.mult), reads=[("U", 0), ("U", 1), "bm"], writes=[("P", 0)])
                    yield
                    S.dve(lambda e: e.tensor_tensor(out=PTb[0][:, :, :], in0=UT[:, :, :], in1=bm[:, 0, :].unsqueeze(1).to_broadcast([128, 2, 128]), op=ALU.mult), reads=["UT", "bm"], writes=[("PT", 0)])
                    yield
                    S.dve(lambda e, idb=idb: e.tensor_tensor(out=Nb[0][:, :, :], in0=idb, in1=Pb[0][:, :, :], op=ALU.subtract), reads=[("P", 0), "ident_f"], writes=[("N", 0)])
                    yield
                    S.dve(lambda e, idb=idb: e.tensor_tensor(out=NTb[0][:, :, :], in0=idb, in1=PTb[0][:, :, :], op=ALU.subtract), reads=[("PT", 0), "ident_f"], writes=[("NT", 0)])
                    yield
                    for mi in range(3):
                        S.pool(lambda e, mi=mi: e.tensor_tensor(out=UoM[mi][:, :, :], in0=U[:, :, :], in1=bm[:, mi + 1, :].unsqueeze(1).to_broadcast([128, 2, 128]), op=ALU.mult), reads=[("U", 0), ("U", 1), "bm"], writes=[("UoM", mi)])
                        yield
                        S.pool(lambda e, mi=mi: e.tensor_tensor(out=UoTM[mi][:, :, :], in0=UT[:, :, :], in1=bm[:, mi + 1, :].unsqueeze(1).to_broadcast([128, 2, 128]), op=ALU.mult), reads=["UT", "bm"], writes=[("UoTM", mi)])
                        yield
                    ni = 0
                    pprev = 0
                    for lv in range(1, 4):
                        pi = lv % 2
                        b1 = ps_next(); b2 = ps_next(); b3 = ps_next(); b4 = ps_next()
                        for e_ in range(2):
                            S.pe(lambda e, e_=e_, b1=b1, pprev=pprev: e.matmul(ps[:, b1, e_ * 128:(e_ + 1) * 128], lhsT=PTb[pprev][:, e_, :], rhs=Pb[pprev][:, e_, :], start=True, stop=True), reads=[("P", pprev), ("PT", pprev)], writes=psk(b1))
                            yield
                        for e_ in range(2):
                            S.pe(lambda e, e_=e_, b2=b2, pprev=pprev: e.matmul(ps[:, b2, e_ * 128:(e_ + 1) * 128], lhsT=Pb[pprev][:, e_, :], rhs=PTb[pprev][:, e_, :], start=True, stop=True), reads=[("P", pprev), ("PT", pprev)], writes=psk(b2))
                            yield
                        S.act(lambda e, b1=b1, pi=pi: e.copy(out=Pb[pi][:, :, :], in_=_ev2(b1)), reads=psk(b1), writes=[("P", pi)])
                        yield
                        S.act(lambda e, b2=b2, pi=pi: e.copy(out=PTb[pi][:, :, :], in_=_ev2(b2)), reads=psk(b2), writes=[("PT", pi)])
                        yield
                        for e_ in range(2):
                            S.pe(lambda e, e_=e_, b3=b3, pi=pi, ni=ni: e.matmul(ps[:, b3, e_ * 128:(e_ + 1) * 128], lhsT=PTb[pi][:, e_, :], rhs=Nb[ni][:, e_, :], start=True, stop=True), reads=[("PT", pi), ("N", ni)], writes=psk(b3))
                            yield
                        for e_ in range(2):
                            S.pe(lambda e, e_=e_, b4=b4, pi=pi, ni=ni: e.matmul(ps[:, b4, e_ * 128:(e_ + 1) * 128], lhsT=Pb[pi][:, e_, :], rhs=NTb[ni][:, e_, :], start=True, stop=True), reads=[("P", pi), ("NT", ni)], writes=psk(b4))
                            yield
                        S.dve(lambda e, b3=b3, ni=ni: e.tensor_tensor(out=Nb[1 - ni][:, :, :], in0=_ev2(b3), in1=Nb[ni][:, :, :], op=ALU.add), reads=psk(b3) + [("N", ni)], writes=[("N", 1 - ni)])
                        yield
                        S.dve(lambda e, b4=b4, ni=ni: e.tensor_tensor(out=NTb[1 - ni][:, :, :], in0=_ev2(b4), in1=NTb[ni][:, :, :], op=ALU.add), reads=psk(b4) + [("NT", ni)], writes=[("NT", 1 - ni)])
                        yield
                        ni = 1 - ni
                        pprev = pi
                    for mi in range(3):
                        lastm = (mi == 2)
                        b1 = ps_next()
                        for e_ in range(2):
                            S.pe(lambda e, e_=e_, b1=b1, ni=ni, mi=mi: e.matmul(ps[:, b1, e_ * 128:(e_ + 1) * 128], lhsT=UoTM[mi][:, e_, :], rhs=Nb[ni][:, e_, :], start=True, stop=True), reads=[("UoTM", mi), ("N", ni)], writes=psk(b1))
                            yield
                        S.act(lambda e, b1=b1: e.copy(out=Pb[1][:, :, :], in_=_ev2(b1)), reads=psk(b1), writes=[("P", 1)])
                        yield
                        if not lastm:
                            b3 = ps_next()
                            for e_ in range(2):
                                S.pe(lambda e, e_=e_, b3=b3, ni=ni, mi=mi: e.matmul(ps[:, b3, e_ * 128:(e_ + 1) * 128], lhsT=UoM[mi][:, e_, :], rhs=NTb[ni][:, e_, :], start=True, stop=True), reads=[("UoM", mi), ("NT", ni)], writes=psk(b3))
                                yield
                            S.act(lambda e, b3=b3: e.copy(out=PTb[1][:, :, :], in_=_ev2(b3)), reads=psk(b3), writes=[("PT", 1)])
                            yield
                        b2 = ps_next()
                        for e_ in range(2):
                            S.pe(lambda e, e_=e_, b2=b2, ni=ni: e.matmul(ps[:, b2, e_ * 128:(e_ + 1) * 128], lhsT=NTb[ni][:, e_, :], rhs=Pb[1][:, e_, :], start=True, stop=True), reads=[("NT", ni), ("P", 1)], writes=psk(b2))
                            yield
                        S.dve(lambda e, b2=b2, ni=ni, lastm=lastm: e.tensor_tensor(out=(Nfin[pb] if lastm else Nb[1 - ni])[:, :, :], in0=Nb[ni][:, :, :], in1=_ev2(b2), op=ALU.subtract), reads=psk(b2) + [("N", ni)], writes=[(("Nfin", pb) if lastm else ("N", 1 - ni))])
                        yield
                        if not lastm:
                            b4 = ps_next()
                            for e_ in range(2):
                                S.pe(lambda e, e_=e_, b4=b4, ni=ni: e.matmul(ps[:, b4, e_ * 128:(e_ + 1) * 128], lhsT=Nb[ni][:, e_, :], rhs=PTb[1][:, e_, :], start=True, stop=True), reads=[("N", ni), ("PT", 1)], writes=psk(b4))
                                yield
                            S.dve(lambda e, b4=b4, ni=ni: e.tensor_tensor(out=NTb[1 - ni][:, :, :], in0=NTb[ni][:, :, :], in1=_ev2(b4), op=ALU.subtract), reads=psk(b4) + [("NT", ni)], writes=[("NT", 1 - ni)])
                            yield
                        ni = 1 - ni
                if isS:
                    S.pool(lambda e, ni=ni, C=C: e.tensor_copy(out=Nfin[pb][0:C, :, 0:C], in_=Nb[ni][0:C, :, 0:C]), reads=[("N", ni)], writes=[("Nfin", pb)])
                    yield
                Nf = Nfin[pb]
                Nkey = ("Nfin", pb)
                if not isS:
                    S.pool(lambda e, C=C: e.tensor_copy(out=eGl[pb][:, 0:2], in_=E2[:, :, C - 1:C].rearrange("p a b -> p (a b)")), reads=["E2"], writes=[("eGl", pb)])
                    yield
                yield "SPLIT"
                if not isS:
                    if first:
                        S.act(lambda e, C=C: e.copy(out=xv[0:C].rearrange("p a b -> p (a b)"), in_=vtok2[pb][0:C].rearrange("p a b -> p (a b)")), reads=[("vtok", pb)], writes=["xv"])
                        yield
                    else:
                        pk_ = ps_next()
                        S.pe(lambda e, pk_=pk_, c0=c0, C=C: e.matmul(ps[0:C, pk_, 0:256], lhsT=kn[:, c0:c0 + C], rhs=Sbf[:].rearrange("p a b -> p (a b)"), start=True, stop=True), reads=[("n", 1), "Sbf"], writes=psk(pk_))
                        yield
                        for e_ in range(2):
                            hv = hv0 + e_
                            S.dve(lambda e, e_=e_, hv=hv, pk_=pk_, C=C, tl=tl: e.scalar_tensor_tensor(out=xv[0:C, e_, :], in0=ps[0:C, pk_, e_ * 128:(e_ + 1) * 128], scalar=negeG_all[0:C, tl, hv:hv + 1], in1=vtok2[pb][0:C, e_, :], op0=ALU.mult, op1=ALU.add), reads=psk(pk_) + [("vtok", pb)], writes=["xv"])
                            yield
                else:
                    S.dve(lambda e: e.tensor_tensor(out=X[:, :, :], in0=knf[:, :].unsqueeze(1).to_broadcast([128, NSEQ, TS]), in1=segmask[:, :, :], op=ALU.mult), reads=[("nf", 1), "segmask"], writes=["X"])
                    yield
                    pks = [ps_next(), ps_next()]
                    K.P.reserved = set(pks)
                    for e_ in range(2):
                        hv = hv0 + e_
                        for s_ in range(NSEQ):
                            i2 = s_ % 2
                            S.dma("sp", "gsin%d" % i2, lambda e, i2=i2, s_=s_, hv=hv: e.dma_start(out=Sin[i2], in_=K.s_gdn[s_, hv]), writes=[("Sin", i2)])
                            yield
                            S.pe(lambda e, e_=e_, s_=s_, i2=i2, pks=pks: e.matmul(ps[0:TS, pks[e_], 0:128], lhsT=X[:, s_, :], rhs=Sin[i2], start=(s_ == 0), stop=(s_ == NSEQ - 1)), reads=["X", ("Sin", i2)], writes=psk(pks[e_]))
                            yield
                        S.dve(lambda e, e_=e_, hv=hv, tl=tl, pks=pks: e.scalar_tensor_tensor(out=xv[0:TS, e_, :], in0=ps[0:TS, pks[e_], 0:128], scalar=negeG_all[0:TS, tl, hv:hv + 1], in1=vtok2[pb][0:TS, e_, :], op0=ALU.mult, op1=ALU.add), reads=psk(pks[e_]) + [("vtok", pb)], writes=["xv"])
                        yield
                    K.P.reserved = set()
                pv = ps_next()
                for e_ in range(2):
                    S.pe(lambda e, e_=e_, pv=pv, C=C, Nf=Nf: e.matmul(ps[0:C, pv, e_ * 128:(e_ + 1) * 128], lhsT=Nf[0:C, e_, 0:C], rhs=xv[0:C, e_, :], start=True, stop=True), reads=[Nkey, "xv"], writes=psk(pv))
                    yield
                for e_ in range(2):
                    hv = hv0 + e_
                    S.act(lambda e, e_=e_, hv=hv, pv=pv, C=C, tl=tl: e.activation(out=vnew[0:C, e_, :], in_=ps[0:C, pv, e_ * 128:(e_ + 1) * 128], func=AF.Identity, scale=beta_all[0:C, tl, hv:hv + 1]), reads=psk(pv), writes=[("vnew", e_)])
                    yield
                if not isS:
                    po = ps_next()
                    po_aps = [ps[0:C, po, 0:128], ps[0:C, po, 128:256]]
                    po_keys = [psk(po), psk(po)]
                    for e_ in range(2):
                        if not first:
                            S.pe(lambda e, e_=e_, C=C, po_aps=po_aps: e.matmul(po_aps[e_], lhsT=qg2[pb][:, e_, 0:C], rhs=Sbf[:, e_, :], start=True, stop=False), reads=[("qg", e_, pb), "Sbf"], writes=po_keys[e_])
                            yield
                        S.pe(lambda e, e_=e_, C=C, po_aps=po_aps, first=first: e.matmul(po_aps[e_], lhsT=attnT2[pb][0:C, e_, 0:C], rhs=vnew[0:C, e_, :], start=first, stop=True), reads=[("attnT", e_, pb), ("vnew", e_)], writes=po_keys[e_])
                        yield
                    pS_ = ps_next()
                    for e_ in range(2):
                        S.pe(lambda e, e_=e_, pS_=pS_, C=C: e.matmul(ps[:, pS_, e_ * 128:(e_ + 1) * 128], lhsT=kd2[pb][0:C, e_, :], rhs=vnew[0:C, e_, :], start=True, stop=True), reads=[("kd", e_, pb), ("vnew", e_)], writes=psk(pS_))
                        yield
                    for e_ in range(2):
                        hv = hv0 + e_
                        if first:
                            S.dve(lambda e, e_=e_, pS_=pS_: e.tensor_copy(out=S32[:, e_, :], in_=ps[:, pS_, e_ * 128:(e_ + 1) * 128]), reads=psk(pS_), writes=[("S32", e_)])
                            yield
                        else:
                            S.dve(lambda e, e_=e_, pS_=pS_, C=C: e.scalar_tensor_tensor(out=S32[:, e_, :], in0=S32[:, e_, :], scalar=eGl[pb][:, e_:e_ + 1], in1=ps[:, pS_, e_ * 128:(e_ + 1) * 128], op0=ALU.mult, op1=ALU.add), reads=psk(pS_) + [("S32", e_), ("eGl", pb)], writes=[("S32", e_)])
                            yield
                        if last:
                            S.dma("sp", "gpo", lambda e, e_=e_, hv=hv: e.dma_start(out=K.gdn_p[hv], in_=S32[:, e_, :]), reads=[("S32", e_)])
                            yield
                    if not last:
                        S.act(lambda e: e.copy(out=Sbf[:].rearrange("p a b -> p (a b)"), in_=S32[:].rearrange("p a b -> p (a b)")), reads=[("S32", 0), ("S32", 1)], writes=["Sbf"])
                        yield
                else:
                    pos = [ps_next(), ps_next()]
                    K.P.reserved = set(pos)
                    po_aps = [ps[0:TS, pos[0], 0:128], ps[0:TS, pos[1], 0:128]]
                    po_keys = [psk(pos[0]), psk(pos[1])]
                    for e_ in range(2):
                        hv = hv0 + e_
                        S.pe(lambda e, e_=e_, po_aps=po_aps: e.matmul(po_aps[e_], lhsT=attnT2[pb][0:TS, e_, 0:TS], rhs=vnew[0:TS, e_, :], start=True, stop=False), reads=[("attnT", e_, pb), ("vnew", e_)], writes=po_keys[e_])
                        yield
                        S.pool(lambda e, e_=e_: e.tensor_tensor(out=qgf[:, :], in0=qnf[:, :], in1=E2[:, e_, 0:TS], op=ALU.mult), reads=[("nf", 0), "E2"], writes=["qgf"])
                        yield
                        S.dve(lambda e: e.tensor_tensor(out=X[:, :, :], in0=qgf[:, :].unsqueeze(1).to_broadcast([128, NSEQ, TS]), in1=segmask[:, :, :], op=ALU.mult), reads=["qgf", "segmask"], writes=["X"])
                        yield
                        for s_ in range(NSEQ):
                            i2 = s_ % 2
                            S.dma("sp", "gsin%d" % i2, lambda e, i2=i2, s_=s_, hv=hv: e.dma_start(out=Sin[i2], in_=K.s_gdn[s_, hv]), writes=[("Sin", i2)])
                            yield
                            S.pe(lambda e, e_=e_, s_=s_, i2=i2, po_aps=po_aps: e.matmul(po_aps[e_], lhsT=X[:, s_, :], rhs=Sin[i2], start=False, stop=(s_ == NSEQ - 1)), reads=["X", ("Sin", i2)], writes=po_keys[e_])
                            yield
                            S.dve(lambda e, e_=e_, s_=s_: e.tensor_scalar(out=kdm[0:TS, :], in0=kd2[pb][0:TS, e_, :], scalar1=segcol[0:TS, s_:s_ + 1], scalar2=None, op0=ALU.mult), reads=[("kd", e_, pb), "segcol"], writes=["kdm"])
                            yield
                            pS_ = ps_next()
                            S.pe(lambda e, e_=e_, pS_=pS_: e.matmul(ps[:, pS_, 0:128], lhsT=kdm[0:TS, :], rhs=vnew[0:TS, e_, :], start=True, stop=True), reads=["kdm", ("vnew", e_)], writes=psk(pS_))
                            yield
                            S.dve(lambda e, e_=e_, s_=s_, i2=i2, pS_=pS_: e.scalar_tensor_tensor(out=Sout, in0=Sin[i2], scalar=E2[:, e_, 4 * s_ + 3:4 * s_ + 4], in1=ps[:, pS_, 0:128], op0=ALU.mult, op1=ALU.add), reads=psk(pS_) + [("Sin", i2), "E2"], writes=["Sout"])
                            yield
                            S.dma("sp", "gsout", lambda e, s_=s_, hv=hv: e.dma_start(out=K.gdn_s[s_, hv], in_=Sout), reads=["Sout"])
                            yield
                    K.P.reserved = set()
                for e_ in range(2):
                    S.act(lambda e, e_=e_, C=C, po_aps=po_aps: e.activation(out=acc[0:C, e_ * 128:(e_ + 1) * 128], in_=po_aps[e_], func=AF.Square, accum_out=ss[0:C, e_:e_ + 1]), reads=po_keys[e_], writes=[("acc", 0), ("ss", e_)])
                    yield
                S.act(lambda e, C=C: e.activation(out=ss[0:C, 2:4], in_=ss[0:C, 0:2], func=AF.Sqrt, bias=EPS, scale=1.0 / 128), reads=[("ss", 0), ("ss", 1)], writes=["ss2"])
                yield
                S.dve(lambda e, C=C: e.reciprocal(out=ss[0:C, 2:4], in_=ss[0:C, 2:4]), reads=["ss2"], writes=["ss2"])
                yield
                for e_ in range(2):
                    S.dve(lambda e, e_=e_, C=C, po_aps=po_aps: e.scalar_tensor_tensor(out=og[0:C, e_ * 128:(e_ + 1) * 128], in0=po_aps[e_], scalar=ss[0:C, 2 + e_:3 + e_], in1=szn2[pb][0:C, e_ * 128:(e_ + 1) * 128], op0=ALU.mult, op1=ALU.mult), reads=po_keys[e_] + ["ss2", ("szn", pb)], writes=[("og", e_)])
                    yield
                bt = ps_next()
                ptb = ps[:, bt, :].bitcast(BF16)
                for e_ in range(2):
                    S.pe(lambda e, e_=e_, ptb=ptb, C=C: e.transpose(ptb[:, e_ * C:(e_ + 1) * C], og[0:C, e_ * 128:(e_ + 1) * 128], ident_b[0:C, 0:C]), reads=[("og", e_), "ident_b"], writes=psk(bt))
                    yield
                S.act(lambda e, ptb=ptb, C=C, c0=c0: e.copy(out=ogT[:, :, c0:c0 + C], in_=ptb[:, 0:2 * C].rearrange("p (a b) -> p a b", a=2)), reads=psk(bt), writes=["ogT"])
                yield
            chunks_ = list(range(0, nt, C))
            gens = [chunk_gen(c0_, i_ % 2) for i_, c0_ in enumerate(chunks_)]
            PIPE = (not isS) and os.environ.get("GDN_PIPE", "1") == "1"

            def _adv_split(g):
                for x_ in g:
                    if x_ == "SPLIT":
                        return
            if not PIPE:
                for g in gens:
                    for _ in g:
                        pass
            else:
                _adv_split(gens[0])
                for i_ in range(len(gens)):
                    nxt = gens[i_ + 1] if i_ + 1 < len(gens) else None
                    a_done = False
                    b_done = nxt is None
                    while not (a_done and b_done):
                        if not a_done:
                            try:
                                next(gens[i_])
                            except StopIteration:
                                a_done = True
                        if not b_done:
                            try:
                                if next(nxt) == "SPLIT":
                                    b_done = True
                            except StopIteration:
                                b_done = True
            for dc8 in range(KC):
                if hk >= 1 and 'F' in CUT:
                    continue
                b = ps_next()
                for e_ in range(2):
                    S.pe(lambda e, b=b, e_=e_, dc8=dc8, nt=nt: e.matmul(ps[:, b, 0:nt], lhsT=wo[:, e_, dc8 * 128:(dc8 + 1) * 128], rhs=ogT[:, e_, 0:nt], start=(e_ == 0), stop=(e_ == 1)), reads=[wok, "ogT"], writes=psk(b))
                K.resid_update(ps[:, b, 0:nt], gbase, dc8, ("S" if isS else bi), t0, nt, psk(b))
        if hk >= 1 and 'G' in CUT:
            S.barrier(); continue
        bo1 = ps_next()
        for ci in range(4):
            S.pe(lambda e, ci=ci, bo1=bo1: e.transpose(ps[0:3, bo1, ci * 128:(ci + 1) * 128], tailP[:, ci, :], ident_f[:]), reads=["tailP", "ident_f"], writes=psk(bo1))
        S.dve(lambda e, bo1=bo1: e.tensor_copy(out=ostg[0:3, :], in_=ps[0:3, bo1, :]), reads=psk(bo1), writes=[("acc", 0)])
        for ci, cid in enumerate(cids):
            S.dma("sp", "gco", lambda e, ci=ci, cid=cid: e.dma_start(out=K.gconv_p[:, cid * 128:(cid + 1) * 128], in_=ostg[0:3, ci * 128:(ci + 1) * 128]), reads=[("acc", 0)])
        bo2 = ps_next()
        for ci in range(4):
            S.pe(lambda e, ci=ci, bo2=bo2: e.transpose(ps[0:48, bo2, ci * 128:(ci + 1) * 128], tailS[:, ci, :, :].rearrange("p s r -> p (s r)"), ident_f[:]), reads=["tailS", "ident_f"], writes=psk(bo2))
        S.dve(lambda e, bo2=bo2: e.tensor_copy(out=ostg[0:48, :], in_=ps[0:48, bo2, :]), reads=psk(bo2), writes=[("acc", 0)])
        for ci, cid in enumerate(cids):
            S.dma("sp", "gco", lambda e, ci=ci, cid=cid: e.dma_start(out=K.gconv_s[:, cid * 128:(cid + 1) * 128], in_=ostg[0:48, ci * 128:(ci + 1) * 128]), reads=[("acc", 0)])
        S.barrier()
    S.barrier()

_CACHE = {}


def make_in_maps(inp):
    f = lambda a: np.ascontiguousarray(np.asarray(a, dtype=np.float32))
    consts = host_consts()
    shared = {
        "w_ada": f(inp["w_ada"]), "b_ada": f(inp["b_ada"]),
        "w_ada_final": f(inp["w_ada_final"]), "b_ada_final": f(inp["b_ada_final"]).reshape(1, -1),
        "w_ret_in": f(inp["w_ret_in"][0]), "w_ret_out": f(inp["w_ret_out"][0]),
        "w_gdn_in": f(inp["w_gdn_in"][0]), "w_gdn_out": f(inp["w_gdn_out"][0]),
        "w_gdn_conv": f(inp["w_gdn_conv"][0]),
        "gdn_vec": f(np.concatenate([inp["gdn_a_log"][0], inp["gdn_dt_bias"][0]])).reshape(1, 32),
        "gdn_norm": f(inp["gdn_norm"]).reshape(1, 128),
        "w_ffn_up": f(inp["w_ffn_up"]), "w_ffn_down": f(inp["w_ffn_down"]),
        "ffn_vec": f(np.concatenate([np.asarray(inp["w_ffn_dw"]).reshape(6, DFF), np.asarray(inp["b_ffn_dw"]).reshape(2, DFF)], axis=0)),
    }
    shared.update(consts)
    maps = []
    for c in range(NCORES):
        sl = slice(NSEQ * c, NSEQ * (c + 1))
        m = dict(shared)
        m["xp"] = f(inp["x_prompt"][c])
        m["xs"] = f(np.asarray(inp["x_sample"][sl]).reshape(TS, D))
        m["s_ret"] = f(inp["state_ret"][0, sl])
        m["s_gdn"] = f(inp["state_gdn"][0, sl])
        m["s_gconv"] = f(np.asarray(inp["state_gdn_conv"][0, sl]).reshape(NSEQ * 3, 4096))
        m["s_fconv"] = f(np.asarray(inp["state_ffn_conv"][:, sl]).reshape(2, NSEQ * 2, DFF))
        m["vec22"] = f(np.concatenate([np.asarray(inp["c_prompt"][c:c + 1]), np.asarray(inp["c_sample"][sl]),
                                        np.asarray(inp["norm_mix"]), np.asarray(inp["norm_ffn"]), np.asarray(inp["norm_final"]).reshape(1, D)], axis=0))
        maps.append(m)
    return maps


def kernel(**inp):
    if "nc" not in _CACHE:
        _CACHE["nc"] = build_program()
    nc = _CACHE["nc"]
    maps = make_in_maps(inp)
    res = run_bass_kernel_spmd(nc, maps, core_ids=list(range(NCORES)))
    R = res.results
    cat = lambda k: np.stack([np.asarray(R[c][k]) for c in range(NCORES)], axis=0)
    y_prompt = cat("y_p")
    y_sample = cat("y_s").reshape(128, 4, D)
    ret_p = cat("ret_p")[None]
    gdn_p = cat("gdn_p")[None]
    gconv_p = cat("gconv_p")[None]
    fconv_p = np.transpose(cat("fconv_p"), (1, 0, 2, 3))
    ret_s = cat("ret_s").reshape(1, 128, RET_H, 256, 512)
    gdn_s = cat("gdn_s").reshape(1, 128, GDN_HV, 128, 128)
    gconv_s = cat("gconv_s").reshape(1, 128, 3, 4096)
    fconv_s = np.transpose(cat("fconv_s").reshape(NCORES, 2, NSEQ, 2, DFF), (1, 0, 2, 3, 4)).reshape(2, 128, 2, DFF)
    return (y_prompt.astype(np.float32), y_sample.astype(np.float32), ret_p.astype(np.float32), gdn_p.astype(np.float32),
            gconv_p.astype(np.float32), fconv_p.astype(np.float32), ret_s.astype(np.float32), gdn_s.astype(np.float32),
            gconv_s.astype(np.float32), fconv_s.astype(np.float32))
```

```python
import math
import bisect
import numpy as np
from contextlib import ExitStack
import concourse.bass as bass
import concourse.mybir as mybir
from concourse.bass_utils import run_bass_kernel_spmd

F32 = mybir.dt.float32
BF16 = mybir.dt.bfloat16
AF = mybir.ActivationFunctionType
ALU = mybir.AluOpType

NCORES = 8
D = 1024
KC = 8
TP = 2048
NSEQ = 16
TS = 64
TT = TP + TS
DFF = 2816
NF = 22
EPS = 1e-6
RET_H = 4
GDN_HV = 16
GDN_HK = 8
PAST = 16384
SAME_ENGINE_SYNC = True
DEBUG = {}


class _Op:
    __slots__ = ("eng", "fn", "deps", "dma_key", "sig", "cnt", "idx")

    def __init__(self, eng, fn, deps, dma_key, idx):
        self.eng = eng
        self.fn = fn
        self.deps = deps
        self.dma_key = dma_key
        self.sig = False
        self.cnt = 0
        self.idx = idx


class Sched:
    ENGS = ("pe", "act", "dve", "pool", "sp")

    def __init__(self, nc):
        self.nc = nc
        self.ops = []
        self.last_w = {}
        self.readers = {}
        self.last_eng = {}
        self.dmas_since_bar = []

    def op(self, eng, fn, reads=(), writes=(), dma_key=None, extra=()):
        deps = set(extra)
        for r in reads:
            w = self.last_w.get(r)
            if w is not None:
                deps.add(w)
        for w_ in writes:
            w = self.last_w.get(w_)
            if w is not None:
                deps.add(w)
            deps |= self.readers.get(w_, set())
        idx = len(self.ops)
        deps.discard(idx)
        self.ops.append(_Op(eng, fn, deps, dma_key, idx))
        for r in reads:
            self.readers.setdefault(r, set()).add(idx)
        for w_ in writes:
            self.last_w[w_] = idx
            self.readers[w_] = set()
        self.last_eng[eng] = idx
        if dma_key is not None:
            self.dmas_since_bar.append(idx)
        return idx

    def pe(self, fn, reads=(), writes=()):
        return self.op("pe", fn, reads, writes)

    def act(self, fn, reads=(), writes=()):
        return self.op("act", fn, reads, writes)

    def dve(self, fn, reads=(), writes=()):
        return self.op("dve", fn, reads, writes)

    def pool(self, fn, reads=(), writes=()):
        return self.op("pool", fn, reads, writes)

    def dma(self, q, key, fn, reads=(), writes=()):
        return self.op(q, fn, reads, writes, dma_key=key)

    def barrier(self):
        deps = set(self.last_eng.values()) | set(self.dmas_since_bar)
        self.dmas_since_bar = []
        for e in self.ENGS:
            self.op(e, None, extra=deps)
        self.last_w = {}
        self.readers = {}

    def emit(self, stack):
        nc = self.nc
        ops = self.ops

        def needs(c, p):
            if p.fn is None:
                return False
            if p.dma_key is not None:
                return True
            if p.eng == c.eng:
                if p.eng == "pe":
                    return False
                return SAME_ENGINE_SYNC
            return True

        for c in ops:
            for d in c.deps:
                p = ops[d]
                if needs(c, p):
                    p.sig = True
        eng_cnt = {e: 0 for e in self.ENGS}
        dma_cnt = {}
        dma_keys = []
        dma_issue_idx = {}
        for o in ops:
            if o.dma_key is not None:
                if o.dma_key not in dma_cnt:
                    dma_cnt[o.dma_key] = 0
                    dma_keys.append(o.dma_key)
                    dma_issue_idx[o.dma_key] = []
                dma_cnt[o.dma_key] += 1
                dma_issue_idx[o.dma_key].append(o.idx)
            elif o.sig:
                eng_cnt[o.eng] += 1
                o.cnt = eng_cnt[o.eng]
        sems = {}
        for e in self.ENGS:
            sems[("e", e)] = stack.enter_context(nc.semaphore("s_" + e))
        for k in dma_keys:
            sems[("d", k)] = stack.enter_context(nc.semaphore("d_" + str(k)))
        per_eng = {e: [o for o in ops if o.eng == e] for e in self.ENGS}
        block = stack.enter_context(nc.Block())

        def run_engine(ename, eobj):
            waited = {}
            for o in per_eng[ename]:
                need = {}
                for d in o.deps:
                    p = ops[d]
                    if not needs(o, p):
                        continue
                    if p.dma_key is not None:
                        key = ("d", p.dma_key)
                        val = 16 * bisect.bisect_left(dma_issue_idx[p.dma_key], o.idx)
                    else:
                        key = ("e", p.eng)
                        val = p.cnt
                    if need.get(key, 0) < val:
                        need[key] = val
                for key, val in need.items():
                    if waited.get(key, 0) >= val:
                        continue
                    eobj.wait_ge(sems[key], val)
                    waited[key] = val
                if o.fn is None:
                    continue
                ins = o.fn(eobj)
                if o.dma_key is not None:
                    ins.then_inc(sems[("d", o.dma_key)], 16)
                elif o.sig:
                    ins.then_inc(sems[("e", ename)], 1)
            for k in dma_keys:
                if any(o.dma_key == k for o in per_eng[ename]):
                    eobj.wait_ge(sems[("d", k)], 16 * dma_cnt[k])

        @block.tensor
        def _(e):
            run_engine("pe", e)

        @block.scalar
        def _(e):
            run_engine("act", e)

        @block.vector
        def _(e):
            run_engine("dve", e)

        @block.gpsimd
        def _(e):
            run_engine("pool", e)

        @block.sync
        def _(e):
            run_engine("sp", e)


def _gammas():
    return (1.0 - 2.0 ** (-5.0 - np.arange(RET_H, dtype=np.float64)))


def host_consts():
    c = {}
    c["c_ident"] = np.eye(128, dtype=np.float32)
    half = 128
    inv_freq = (np.float32(10000.0) ** (-(np.arange(half, dtype=np.float32)) / np.float32(half))).astype(np.float32)
    pos = np.concatenate([np.arange(TP, dtype=np.float32), (PAST + (np.arange(TS) % 4)).astype(np.float32)])
    ang = (pos[None, :] * inv_freq[:, None]).astype(np.float32)
    cos = np.cos(ang.astype(np.float64))
    sin = np.sin(ang.astype(np.float64))
    g = _gammas()
    pin = np.concatenate([np.arange(TP) % 128, np.arange(TS) % 4]).astype(np.float64)
    rope = np.zeros((RET_H + 1, 2, 128, TT), np.float32)
    for h in range(RET_H):
        dec = g[h] ** (pin + 1.0)
        rope[h, 0] = cos * dec[None, :]
        rope[h, 1] = sin * dec[None, :]
    rope[RET_H, 0] = cos * (256.0 ** -0.5)
    rope[RET_H, 1] = sin * (256.0 ** -0.5)
    c["rope"] = rope
    mP = np.zeros((RET_H, 128, 128), np.float32)
    mS = np.zeros((RET_H, 128, 128), np.float32)
    ks = np.zeros((128, 2 * RET_H), np.float32)
    jj = np.arange(128)
    for h in range(RET_H):
        mP[h] = np.where(jj[None, :] >= jj[:, None], g[h] ** (-(jj[:, None] + 1.0)), 0.0)
        j4 = jj[:64] % 4
        same = (jj[:64, None] // 4) == (jj[None, :64] // 4)
        mS[h, :64, :64] = np.where(same & (jj[None, :64] >= jj[:64, None]), g[h] ** (-(j4[:, None] + 1.0)), 0.0)
        ks[:, h] = g[h] ** (127.0 - jj)
        ks[:64, 4 + h] = g[h] ** (3.0 - j4)
    c["retmask"] = np.concatenate([mP, mS], axis=0)
    c["kscale"] = ks
    seg = np.zeros((128, NSEQ, 64), np.float32)
    for s in range(NSEQ):
        seg[:, s, 4 * s:4 * s + 4] = 1.0
    c["segmask"] = seg.reshape(128, NSEQ * 64)
    segcol = np.zeros((128, NSEQ), np.float32)
    for s in range(NSEQ):
        segcol[4 * s:4 * s + 4, s] = 1.0
    c["segcol"] = segcol
    gm = np.zeros((8, 128, 128), np.float32)
    a = np.arange(128)
    gm[0] = (a[:, None] <= a[None, :])
    gm[1] = (a[:, None] > a[None, :])
    gm[2] = (a[None, :] >= a[:, None])
    gm[3] = (a[None, :] > a[:, None])
    b = np.arange(64)
    same = (b[:, None] // 4) == (b[None, :] // 4)
    gm[4, :64, :64] = (b[:, None] <= b[None, :]) & same
    gm[5, :64, :64] = (b[:, None] > b[None, :]) & same
    gm[6, :64, :64] = (b[None, :] >= b[:, None]) & same
    gm[7, :64, :64] = (b[None, :] > b[:, None]) & same
    c["gmask"] = np.ascontiguousarray(gm.transpose(1, 0, 2)).reshape(128, 8 * 128)
    bmk = np.zeros((4, 128, 128), np.float32)
    bmk[0] = (a[:, None] // 16) == (a[None, :] // 16)
    for mi, m in enumerate([32, 64, 128]):
        bmk[mi + 1] = ((a[:, None] // m) == (a[None, :] // m)) & ((a[:, None] // (m // 2)) != (a[None, :] // (m // 2)))
    c["bmask"] = np.ascontiguousarray(bmk.transpose(1, 0, 2)).reshape(128, 4 * 128)
    return c


class Ctx:
    pass


def build_program(dbg=()):
    nc = bass.Bass("TRN2", target_bir_lowering=False)
    K = Ctx()
    K.nc = nc
    ins = {}

    def din(name, shape):
        ins[name] = nc.dram_tensor(name, list(shape), F32, kind="ExternalInput").ap()
        return ins[name]

    def dout(name, shape):
        return nc.dram_tensor(name, list(shape), F32, kind="ExternalOutput").ap()

    xp = din("xp", [TP, D]); xs = din("xs", [TS, D])
    s_ret = din("s_ret", [NSEQ, RET_H, 256, 512])
    s_gdn = din("s_gdn", [NSEQ, GDN_HV, 128, 128])
    s_gconv = din("s_gconv", [NSEQ * 3, 4096])
    s_fconv = din("s_fconv", [2, NSEQ * 2, DFF])
    vec22 = din("vec22", [22, D])
    w_ada = din("w_ada", [2, D, 6 * D]); b_ada = din("b_ada", [2, 6 * D])
    w_adaf = din("w_ada_final", [D, 2 * D]); b_adaf = din("b_ada_final", [1, 2 * D])
    w_ret_in = din("w_ret_in", [D, 6144]); w_ret_out = din("w_ret_out", [2048, D])
    w_gdn_in = din("w_gdn_in", [D, 6176]); w_gdn_out = din("w_gdn_out", [2048, D])
    w_gconv = din("w_gdn_conv", [4, 4096])
    gdn_vec = din("gdn_vec", [1, 32])
    gdn_norm = din("gdn_norm", [1, 128])
    w_up = din("w_ffn_up", [2, D, 2 * DFF]); w_down = din("w_ffn_down", [2, DFF, D])
    ffn_vec = din("ffn_vec", [8, DFF])
    c_ident = din("c_ident", [128, 128])
    c_rope = din("rope", [RET_H + 1, 2, 128, TT])
    c_retmask = din("retmask", [2 * RET_H, 128, 128])
    c_kscale = din("kscale", [128, 2 * RET_H])
    c_segmask = din("segmask", [128, NSEQ * 64])
    c_segcol = din("segcol", [128, NSEQ])
    c_gmask = din("gmask", [128, 8 * 128])
    c_bmask = din("bmask", [128, 4 * 128])

    y_p = dout("y_p", [TP, D]); y_s = dout("y_s", [TS, D])
    ret_p = dout("ret_p", [RET_H, 256, 512]); gdn_p = dout("gdn_p", [GDN_HV, 128, 128])
    gconv_p = dout("gconv_p", [3, 4096]); fconv_p = dout("fconv_p", [2, 2, DFF])
    ret_s = dout("ret_s", [NSEQ, RET_H, 256, 512]); gdn_s = dout("gdn_s", [NSEQ, GDN_HV, 128, 128])
    gconv_s = dout("gconv_s", [NSEQ * 3, 4096]); fconv_s = dout("fconv_s", [2, NSEQ * 2, DFF])
    dbg_out = {n: dout("dbg_" + n, shp) for n, shp in dbg}

    with ExitStack() as st:
        def T(name, shape, dt):
            return st.enter_context(nc.sbuf_tensor(name, list(shape), dt))

        S = Sched(nc)
        xT = T("xT", [128, KC, TT], F32)
        hT = T("hT", [128, KC, TT], BF16)
        wring = T("wring", [128, 4, 4096], BF16)
        modT = T("modT", [128, 112, 17], F32)
        vecT = T("vecT", [128, KC, 22], F32)
        Amod = T("Amod", [128, 5, KC, 17], F32)
        csT = T("csT", [128, KC, 17], F32)
        ident_f = T("ident_f", [128, 128], F32)
        ident_b = T("ident_b", [128, 128], BF16)
        ones_b = T("ones_b", [128, 128], BF16)
        ones_f = T("ones_f", [128, 32], F32)
        ffnvT = T("ffnvT", [128, NF, 8], F32)
        fcarry = T("fcarry", [128, NF, 2], F32)
        ARENA = 15000
        arena = T("arena", [128, ARENA], F32)
        ps = st.enter_context(nc.psum_tensor("ps", [128, 8, 512], F32))

        A = Ctx()
        A.off = 0

        def a_reset():
            A.off = 0

        def a_f32(*free, parts=128):
            n = int(np.prod(free))
            assert A.off + n <= ARENA, ("arena overflow", A.off, n)
            ap = arena[0:parts, A.off:A.off + n]
            A.off += n
            if len(free) == 2:
                ap = ap.rearrange("p (a b) -> p a b", a=free[0])
            elif len(free) == 3:
                ap = ap.rearrange("p (a b c) -> p a b c", a=free[0], b=free[1])
            return ap

        def a_bf16(*free, parts=128):
            n = int(np.prod(free))
            nf = (n + 1) // 2
            assert A.off + nf <= ARENA, ("arena overflow", A.off, nf)
            ap = arena[0:parts, A.off:A.off + nf].bitcast(BF16)[:, 0:n]
            A.off += nf
            if len(free) == 2:
                ap = ap.rearrange("p (a b) -> p a b", a=free[0])
            elif len(free) == 3:
                ap = ap.rearrange("p (a b c) -> p a b c", a=free[0], b=free[1])
            return ap

        P = Ctx()
        P.i = 0

        P.reserved = set()

        def ps_next(n=1):
            while True:
                if P.i + n > 8:
                    P.i = 0
                b = P.i
                P.i = (P.i + n) % 8
                if not any((b + i) in P.reserved for i in range(n)):
                    return b

        def psk(b, n=1):
            return [("ps", b + i) for i in range(n)]

        W = Ctx()
        W.i = 0

        def wslot():
            i = W.i
            W.i = (W.i + 1) % 4
            return i

        def load_w_bf16(src2d, kc, ncols, row0=0, col0=0, slot=None, off=0, key=None):
            i = wslot() if slot is None else slot
            view = wring[:, i, off:off + kc * ncols].rearrange("p (k c) -> p k c", k=kc)
            src = src2d[row0:row0 + kc * 128, col0:col0 + ncols].rearrange("(k p) c -> p k c", p=128)
            k_ = ("w", i) if key is None else key
            S.dma("pool", "w%d" % i, lambda e: e.dma_start(out=view, in_=src), writes=[k_])
            return view, k_

        def load_w_f32(src2d, kc, ncols, row0=0, col0=0):
            i = wslot()
            view = wring[:, i, :].bitcast(F32)[:, 0:kc * ncols].rearrange("p (k c) -> p k c", k=kc)
            src = src2d[row0:row0 + kc * 128, col0:col0 + ncols].rearrange("(k p) c -> p k c", p=128)
            S.dma("sp", "wf%d" % i, lambda e: e.dma_start(out=view, in_=src), writes=[("w", i)])
            return view, ("w", i)

        BLOCKS_P = [(0, 512), (512, 512), (1024, 512), (1536, 512)]
        BLOCK_S = (TP, TS)
        ALLBLOCKS = BLOCKS_P + [BLOCK_S]

        S.dma("sp", "c_id", lambda e: e.dma_start(out=ident_f[:], in_=c_ident), writes=["ident_f"])
        S.dve(lambda e: e.tensor_copy(out=ident_b[:], in_=ident_f[:]), reads=["ident_f"], writes=["ident_b"])
        S.dve(lambda e: e.memset(ones_b[:], 1.0), writes=["ones_b"])
        S.dve(lambda e: e.memset(ones_f[:], 1.0), writes=["ones_f"])
        a_reset()
        stage22 = a_f32(D, parts=22)
        S.dma("sp", "c_s22", lambda e: e.dma_start(out=stage22, in_=vec22), writes=["stage22"])
        b0 = ps_next()
        for kc in range(KC):
            S.pe(lambda e, kc=kc: e.transpose(ps[:, b0, kc * 22:(kc + 1) * 22], stage22[:, kc * 128:(kc + 1) * 128], ident_f[0:22, 0:22]),
                 reads=["stage22", "ident_f"], writes=psk(b0))
        S.dve(lambda e: e.tensor_copy(out=vecT[:].rearrange("p a b -> p (a b)"), in_=ps[:, b0, 0:KC * 22]), reads=psk(b0), writes=["vecT"])
        S.act(lambda e: e.activation(out=csT[:], in_=vecT[:, :, 0:17], func=AF.Silu), reads=["vecT"], writes=["csT"])
        stage8 = a_f32(DFF, parts=8)
        S.dma("sp", "c_s8", lambda e: e.dma_start(out=stage8, in_=ins["ffn_vec"]), writes=["stage8"])
        b1 = ps_next()
        for fc in range(NF):
            S.pe(lambda e, fc=fc: e.transpose(ps[:, b1, fc * 8:(fc + 1) * 8], stage8[:, fc * 128:(fc + 1) * 128], ident_f[0:8, 0:8]),
                 reads=["stage8", "ident_f"], writes=psk(b1))
        S.dve(lambda e: e.tensor_copy(out=ffnvT[:].rearrange("p a b -> p (a b)"), in_=ps[:, b1, 0:NF * 8]), reads=psk(b1), writes=["ffnvT"])

        browb = [a_bf16(512, parts=1) for _ in range(3)]
        mstage = [a_f32(512, parts=17) for _ in range(2)]
        csb = a_bf16(KC, 17)
        S.dve(lambda e: e.tensor_copy(out=csb, in_=csT[:]), reads=["csT"], writes=["csb"])
        bri = [0]

        def ada_layer(wsrc, bsrc, ncols, mod_base):
            for pcs in range(ncols // 512):
                wv, wk = load_w_bf16(wsrc, KC, 512, col0=pcs * 512)
                bi = bri[0] % 3
                m2 = bri[0] % 2
                bri[0] += 1
                S.dma("pool", "brb%d" % bi, lambda e, bi=bi, pcs=pcs: e.dma_start(out=browb[bi], in_=bsrc[0:1, pcs * 512:(pcs + 1) * 512]), writes=[("browb", bi)])
                bank = ps_next()
                for kc in range(KC):
                    S.pe(lambda e, bank=bank, kc=kc, wv=wv: e.matmul(ps[0:17, bank, :], lhsT=csb[:, kc, :], rhs=wv[:, kc, :], start=(kc == 0), stop=False), reads=[wk, "csb"], writes=psk(bank))
                S.pe(lambda e, bank=bank, bi=bi: e.matmul(ps[0:17, bank, :], lhsT=ones_b[0:1, 0:17], rhs=browb[bi][0:1, :], start=False, stop=True), reads=[("browb", bi), "ones_b"], writes=psk(bank))
                S.act(lambda e, bank=bank, m2=m2: e.copy(out=mstage[m2], in_=ps[0:17, bank, :]), reads=psk(bank), writes=[("mstage", m2)])
                bank2 = ps_next()
                for q in range(4):
                    S.pe(lambda e, bank2=bank2, q=q, m2=m2: e.transpose(ps[:, bank2, q * 17:(q + 1) * 17], mstage[m2][:, q * 128:(q + 1) * 128], ident_f[0:17, 0:17]), reads=[("mstage", m2), "ident_f"], writes=psk(bank2))
                m0 = mod_base + 4 * pcs
                S.dve(lambda e, bank2=bank2, m0=m0: e.tensor_copy(out=modT[:, m0:m0 + 4, :].rearrange("p a b -> p (a b)"), in_=ps[:, bank2, 0:68]), reads=psk(bank2), writes=["modT"])

        ada_layer(w_ada[0], b_ada[0:1, :], 6 * D, 0)
        ada_layer(w_ada[1], b_ada[1:2, :], 6 * D, 48)
        ada_layer(w_adaf, b_adaf, 2 * D, 96)
        norm_specs = [(0 * 48 + 8, 17), (0 * 48 + 32, 19), (1 * 48 + 8, 18), (1 * 48 + 32, 20), (96 + 8, 21)]
        for n, (scb, col) in enumerate(norm_specs):
            for kc in range(KC):
                S.dve(lambda e, n=n, kc=kc, scb=scb, col=col: e.tensor_scalar(out=Amod[:, n, kc, :], in0=modT[:, scb + kc, :], scalar1=1.0, scalar2=vecT[:, kc, col:col + 1], op0=ALU.add, op1=ALU.mult),
                      reads=["modT", "vecT"], writes=["Amod"])
        norm_shift = [0, 24, 48, 72, 96]
        S.barrier()

        a_reset()
        xst = [a_f32(D) for _ in range(2)]
        tiles = [(xp, t * 128, 128, t * 128) for t in range(16)] + [(xs, 0, TS, TP)]
        for ti, (src, r0, rows, t0) in enumerate(tiles):
            sb = ti % 2
            S.dma("sp", "xst%d" % sb, lambda e, sb=sb, src=src, r0=r0, rows=rows: e.dma_start(out=xst[sb][0:rows, :], in_=src[r0:r0 + rows, :]), writes=[("xst", sb)])
            for half in range(2):
                b = ps_next()
                for q in range(4):
                    kc = half * 4 + q
                    S.pe(lambda e, b=b, q=q, kc=kc, sb=sb, rows=rows: e.transpose(ps[:, b, q * 128:q * 128 + rows], xst[sb][0:rows, kc * 128:(kc + 1) * 128], ident_f[0:rows, 0:rows]),
                         reads=[("xst", sb), "ident_f"], writes=psk(b))
                src_ap = ps[:, b, :].rearrange("p (q c) -> p q c", q=4)[:, :, 0:rows]
                dst_ap = xT[:, half * 4:half * 4 + 4, t0:t0 + rows]
                if half == 0:
                    S.act(lambda e, dst_ap=dst_ap, src_ap=src_ap: e.copy(out=dst_ap, in_=src_ap), reads=psk(b), writes=[("xT", ti)])
                else:
                    S.dve(lambda e, dst_ap=dst_ap, src_ap=src_ap: e.tensor_copy(out=dst_ap, in_=src_ap), reads=psk(b), writes=[("xT", ti)])
        S.barrier()

        def do_norm(n, out_hT=True, out_f32=None):
            a_reset()
            sq = [a_bf16(KC, 512) for _ in range(2)]
            rstd = [a_f32(512) for _ in range(2)]
            tmp = [a_f32(KC, 512) for _ in range(2)]
            shb = norm_shift[n]
            for bi, (t0, nt) in enumerate(ALLBLOCKS):
                i2 = bi % 2
                S.act(lambda e, i2=i2, t0=t0, nt=nt: e.activation(out=sq[i2][:, :, 0:nt], in_=xT[:, :, t0:t0 + nt], func=AF.Square),
                      reads=[], writes=[("sq", i2)])
                b = ps_next()
                for kc in range(KC):
                    S.pe(lambda e, b=b, kc=kc, i2=i2, nt=nt: e.matmul(ps[:, b, 0:nt], lhsT=ones_b[:], rhs=sq[i2][:, kc, 0:nt], start=(kc == 0), stop=(kc == KC - 1)),
                         reads=[("sq", i2), "ones_b"], writes=psk(b))
                S.act(lambda e, b=b, i2=i2, nt=nt: e.activation(out=rstd[i2][:, 0:nt], in_=ps[:, b, 0:nt], func=AF.Sqrt, bias=EPS, scale=1.0 / D),
                      reads=psk(b), writes=[("rstd", i2)])
                S.dve(lambda e, i2=i2, nt=nt: e.reciprocal(out=rstd[i2][:, 0:nt], in_=rstd[i2][:, 0:nt]), reads=[("rstd", i2)], writes=[("rstd", i2)])
                S.dve(lambda e, i2=i2, t0=t0, nt=nt: e.tensor_tensor(out=tmp[i2][:, :, 0:nt], in0=xT[:, :, t0:t0 + nt], in1=rstd[i2][:, 0:nt].unsqueeze(1).to_broadcast([128, KC, nt]), op=ALU.mult),
                      reads=[("rstd", i2)], writes=[("tmp", i2)])
                for kc in range(KC):
                    dst = hT[:, kc, t0:t0 + nt] if out_f32 is None else out_f32(bi, kc)
                    if t0 < TP:
                        S.act(lambda e, dst=dst, i2=i2, kc=kc, nt=nt: e.activation(out=dst, in_=tmp[i2][:, kc, 0:nt], func=AF.Identity, scale=Amod[:, n, kc, 0:1], bias=modT[:, shb + kc, 0:1]),
                              reads=[("tmp", i2)], writes=[("h", bi, kc)])
                    else:
                        S.dve(lambda e, i2=i2, kc=kc: e.tensor_tensor(out=tmp[i2][:, kc, 0:TS].rearrange("p (s j) -> p s j", j=4), in0=tmp[i2][:, kc, 0:TS].rearrange("p (s j) -> p s j", j=4),
                                                                     in1=Amod[:, n, kc, 1:17].unsqueeze(2).to_broadcast([128, NSEQ, 4]), op=ALU.mult),
                              reads=[("tmp", i2)], writes=[("tmp", i2)])
                        S.dve(lambda e, dst=dst, i2=i2, kc=kc: e.tensor_tensor(out=dst.rearrange("p (s j) -> p s j", j=4), in0=tmp[i2][:, kc, 0:TS].rearrange("p (s j) -> p s j", j=4),
                                                                              in1=modT[:, shb + kc, 1:17].unsqueeze(2).to_broadcast([128, NSEQ, 4]), op=ALU.add),
                              reads=[("tmp", i2)], writes=[("h", bi, kc)])
            S.barrier()

        def gate_ap(gbase, kc, bi, nt):
            if bi != "S":
                return modT[:, gbase + kc, 0:1].to_broadcast([128, nt])
            return modT[:, gbase + kc, 1:17].unsqueeze(2).to_broadcast([128, NSEQ, 4])

        def resid_update(psrc, gbase, kc, bi, t0, nt, reads):
            if t0 < TP:
                S.dve(lambda e: e.scalar_tensor_tensor(out=xT[:, kc, t0:t0 + nt], in0=psrc, scalar=modT[:, gbase + kc, 0:1], in1=xT[:, kc, t0:t0 + nt], op0=ALU.mult, op1=ALU.add),
                      reads=list(reads) + [("x", kc, t0)], writes=[("x", kc, t0)])
            else:
                tmpg = K.tmpg
                S.dve(lambda e: e.tensor_tensor(out=tmpg.rearrange("p (s j) -> p s j", j=4), in0=psrc.rearrange("p (s j) -> p s j", j=4), in1=gate_ap(gbase, kc, "S", nt), op=ALU.mult),
                      reads=list(reads), writes=["tmpg"])
                S.dve(lambda e: e.tensor_tensor(out=xT[:, kc, t0:t0 + nt], in0=xT[:, kc, t0:t0 + nt], in1=tmpg, op=ALU.add),
                      reads=["tmpg", ("x", kc, t0)], writes=[("x", kc, t0)])

        def do_ffn(l):
            a_reset()
            gbase = l * 48 + 40
            K.tmpg = a_f32(TS)
            actb = a_bf16(NF, 704)
            gx = [a_f32(2 + 512) for _ in range(2)]
            gxs = a_f32(NSEQ, 6)
            cv = [a_f32(512) for _ in range(2)]
            tailP = a_f32(NF, 2)
            tailS = a_f32(NF, NSEQ, 2)
            sstT = a_f32(NF, 32)
            sstg = [a_f32(512, parts=32) for _ in range(2)]
            ostg = [a_f32(512, parts=32) for _ in range(2)]
            ostgP = [a_f32(512, parts=2) for _ in range(2)]
            for gi, g4 in enumerate(range(0, NF, 4)):
                b = ps_next()
                n = min(4, NF - g4)
                s2 = gi % 2
                S.dma("sp", "fst%d" % s2, lambda e, s2=s2, g4=g4, n=n: e.dma_start(out=sstg[s2][:, 0:n * 128], in_=s_fconv[l][:, g4 * 128:(g4 + n) * 128]), writes=[("sstg", s2)])
                for q in range(n):
                    S.pe(lambda e, b=b, q=q, s2=s2: e.transpose(ps[:, b, q * 32:(q + 1) * 32], sstg[s2][:, q * 128:(q + 1) * 128], ident_f[0:32, 0:32]),
                         reads=[("sstg", s2), "ident_f"], writes=psk(b))
                S.dve(lambda e, b=b, n=n, g4=g4: e.tensor_copy(out=sstT[:, g4:g4 + n, :].rearrange("p a b -> p (a b)"), in_=ps[:, b, 0:n * 32]), reads=psk(b), writes=["sstT"])
            S.dve(lambda e: e.memset(fcarry[:], 0.0), writes=[("fcarry", fc_) for fc_ in range(NF)])
            wcol = l * 3
            passes = [[(0, 0, 352, 0), (1, 352, 352, 352)], [(2, 704, 352, 0), (3, 1056, 352, 352)], [(4, 1408, 320, 0), (5, 1728, 320, 320), (6, TP, TS, 640)]]
            for pi, blks in enumerate(passes):
                for f0 in range(0, NF, 4):
                    nf = min(4, NF - f0)
                    wg, wgk = load_w_bf16(w_up[l], KC, nf * 128, col0=f0 * 128)
                    wv, wvk = load_w_bf16(w_up[l], KC, nf * 128, col0=DFF + f0 * 128)
                    for q in range(nf):
                        fc = f0 + q
                        for (bi, t0, nt, a0) in blks:
                            bg = ps_next()
                            for kc in range(KC):
                                S.pe(lambda e, bg=bg, kc=kc, q=q, t0=t0, nt=nt, wg=wg: e.matmul(ps[:, bg, 0:nt], lhsT=wg[:, kc, q * 128:(q + 1) * 128], rhs=hT[:, kc, t0:t0 + nt], start=(kc == 0), stop=(kc == KC - 1)),
                                     reads=[wgk], writes=psk(bg))
                            bv = ps_next()
                            for kc in range(KC):
                                S.pe(lambda e, bv=bv, kc=kc, q=q, t0=t0, nt=nt, wv=wv: e.matmul(ps[:, bv, 0:nt], lhsT=wv[:, kc, q * 128:(q + 1) * 128], rhs=hT[:, kc, t0:t0 + nt], start=(kc == 0), stop=(kc == KC - 1)),
                                     reads=[wvk], writes=psk(bv))
                            w0 = ffnvT[:, fc, wcol + 0:wcol + 1]
                            w1 = ffnvT[:, fc, wcol + 1:wcol + 2]
                            w2 = ffnvT[:, fc, wcol + 2:wcol + 3]
                            bb = ffnvT[:, fc, 6 + l:7 + l]
                            if bi < 6:
                                i2 = bi % 2
                                G = gx[i2]
                                c_ = cv[i2]
                                S.dve(lambda e, G=G, fc=fc: e.tensor_copy(out=G[:, 0:2], in_=fcarry[:, fc, :]), reads=[("fcarry", fc)], writes=[("gx", i2)])
                                S.act(lambda e, G=G, bg=bg, nt=nt: e.copy(out=G[:, 2:2 + nt], in_=ps[:, bg, 0:nt]), reads=psk(bg), writes=[("gx", i2)])
                                S.dve(lambda e, G=G, fc=fc, nt=nt: e.tensor_copy(out=fcarry[:, fc, :], in_=G[:, nt:nt + 2]), reads=[("gx", i2)], writes=[("fcarry", fc)])
                                if bi == 5:
                                    S.dve(lambda e, G=G, fc=fc, nt=nt: e.tensor_copy(out=tailP[:, fc, :], in_=G[:, nt:nt + 2]), reads=[("gx", i2)], writes=["tailP"])
                                S.act(lambda e, G=G, c_=c_, nt=nt, w2=w2: e.activation(out=c_[:, 0:nt], in_=G[:, 2:2 + nt], func=AF.Identity, scale=w2), reads=[("gx", i2), "ffnvT"], writes=[("cv", i2)])
                                S.dve(lambda e, G=G, c_=c_, nt=nt, w1=w1: e.scalar_tensor_tensor(out=c_[:, 0:nt], in0=G[:, 1:1 + nt], scalar=w1, in1=c_[:, 0:nt], op0=ALU.mult, op1=ALU.add), reads=[("gx", i2), ("cv", i2)], writes=[("cv", i2)])
                                S.dve(lambda e, G=G, c_=c_, nt=nt, w0=w0: e.scalar_tensor_tensor(out=c_[:, 0:nt], in0=G[:, 0:nt], scalar=w0, in1=c_[:, 0:nt], op0=ALU.mult, op1=ALU.add), reads=[("gx", i2), ("cv", i2)], writes=[("cv", i2)])
                                S.act(lambda e, c_=c_, nt=nt, bb=bb: e.activation(out=c_[:, 0:nt], in_=c_[:, 0:nt], func=AF.Silu, bias=bb), reads=[("cv", i2)], writes=[("cv", i2)])
                                S.dve(lambda e, c_=c_, nt=nt, bv=bv, fc=fc, a0=a0: e.tensor_tensor(out=actb[:, fc, a0:a0 + nt], in0=c_[:, 0:nt], in1=ps[:, bv, 0:nt], op=ALU.mult), reads=[("cv", i2)] + psk(bv), writes=[("act", fc, bi)])
                            else:
                                c_ = cv[0][:, 0:TS].rearrange("p (s j) -> p s j", j=4)
                                S.dve(lambda e, fc=fc: e.tensor_copy(out=gxs[:, :, 0:2], in_=sstT[:, fc, :].rearrange("p (s r) -> p s r", r=2)), reads=["sstT"], writes=["gxs"])
                                S.act(lambda e, bg=bg: e.copy(out=gxs[:, :, 2:6], in_=ps[:, bg, 0:TS].rearrange("p (s j) -> p s j", j=4)), reads=psk(bg), writes=["gxs"])
                                S.dve(lambda e, fc=fc: e.tensor_copy(out=tailS[:, fc, :, :], in_=gxs[:, :, 4:6]), reads=["gxs"], writes=["tailS"])
                                S.act(lambda e, c_=c_, w2=w2: e.activation(out=c_, in_=gxs[:, :, 2:6], func=AF.Identity, scale=w2), reads=["gxs", "ffnvT"], writes=[("cv", 0)])
                                S.dve(lambda e, c_=c_, w1=w1: e.scalar_tensor_tensor(out=c_, in0=gxs[:, :, 1:5], scalar=w1, in1=c_, op0=ALU.mult, op1=ALU.add), reads=["gxs", ("cv", 0)], writes=[("cv", 0)])
                                S.dve(lambda e, c_=c_, w0=w0: e.scalar_tensor_tensor(out=c_, in0=gxs[:, :, 0:4], scalar=w0, in1=c_, op0=ALU.mult, op1=ALU.add), reads=["gxs", ("cv", 0)], writes=[("cv", 0)])
                                S.act(lambda e, bb=bb: e.activation(out=cv[0][:, 0:TS], in_=cv[0][:, 0:TS], func=AF.Silu, bias=bb), reads=[("cv", 0)], writes=[("cv", 0)])
                                S.dve(lambda e, bv=bv, fc=fc, a0=a0: e.tensor_tensor(out=actb[:, fc, a0:a0 + TS], in0=cv[0][:, 0:TS], in1=ps[:, bv, 0:TS], op=ALU.mult), reads=[("cv", 0)] + psk(bv), writes=[("act", fc, bi)])
                for dh in range(2):
                    banks = {}
                    for dc in range(4):
                        for (bi, t0, nt, a0) in blks:
                            if len(blks) * 4 > 8 and bi == 6:
                                continue
                            banks[(dc, bi)] = ps_next()
                    for f0 in range(0, NF, 4):
                        nf = min(4, NF - f0)
                        wd, wdk = load_w_bf16(w_down[l], nf, 512, row0=f0 * 128, col0=dh * 512)
                        for (dc, bi), b in banks.items():
                            t0, nt, a0 = [(x[1], x[2], x[3]) for x in blks if x[0] == bi][0]
                            for q in range(nf):
                                fc = f0 + q
                                S.pe(lambda e, b=b, q=q, dc=dc, fc=fc, a0=a0, nt=nt, wd=wd: e.matmul(ps[:, b, 0:nt], lhsT=wd[:, q, dc * 128:(dc + 1) * 128], rhs=actb[:, fc, a0:a0 + nt], start=(fc == 0), stop=(fc == NF - 1)),
                                     reads=[wdk, ("act", fc, bi)], writes=psk(b))
                    for (dc, bi), b in banks.items():
                        t0, nt, a0 = [(x[1], x[2], x[3]) for x in blks if x[0] == bi][0]
                        resid_update(ps[:, b, 0:nt], gbase, dh * 4 + dc, bi, t0, nt, psk(b))
                if len(blks) * 4 > 8:
                    (bi, t0, nt, a0) = blks[2]
                    for dh in range(2):
                        banks = {dc: ps_next() for dc in range(4)}
                        for f0 in range(0, NF, 4):
                            nf = min(4, NF - f0)
                            wd, wdk = load_w_bf16(w_down[l], nf, 512, row0=f0 * 128, col0=dh * 512)
                            for dc, b in banks.items():
                                for q in range(nf):
                                    fc = f0 + q
                                    S.pe(lambda e, b=b, q=q, dc=dc, fc=fc, wd=wd, nt=nt, a0=a0: e.matmul(ps[:, b, 0:nt], lhsT=wd[:, q, dc * 128:(dc + 1) * 128], rhs=actb[:, fc, a0:a0 + nt], start=(fc == 0), stop=(fc == NF - 1)),
                                         reads=[wdk, ("act", fc, bi)], writes=psk(b))
                        for dc, b in banks.items():
                            resid_update(ps[:, b, 0:nt], gbase, dh * 4 + dc, bi, t0, nt, psk(b))
            for gi, g4 in enumerate(range(0, NF, 4)):
                n = min(4, NF - g4)
                o2 = gi % 2
                b = ps_next()
                b2 = ps_next()
                for q in range(n):
                    fc = g4 + q
                    S.pe(lambda e, b=b, q=q, fc=fc: e.transpose(ps[0:2, b, q * 128:(q + 1) * 128], tailP[:, fc, :], ident_f[:]), reads=["tailP", "ident_f"], writes=psk(b))
                    S.pe(lambda e, b2=b2, q=q, fc=fc: e.transpose(ps[0:32, b2, q * 128:(q + 1) * 128], tailS[:, fc, :, :].rearrange("p s r -> p (s r)"), ident_f[:]), reads=["tailS", "ident_f"], writes=psk(b2))
                S.dve(lambda e, b=b, n=n, o2=o2: e.tensor_copy(out=ostgP[o2][:, 0:n * 128], in_=ps[0:2, b, 0:n * 128]), reads=psk(b), writes=[("ostgP", o2)])
                S.dve(lambda e, b2=b2, n=n, o2=o2: e.tensor_copy(out=ostg[o2][:, 0:n * 128], in_=ps[0:32, b2, 0:n * 128]), reads=psk(b2), writes=[("ostg", o2)])
                S.dma("sp", "foutP%d" % o2, lambda e, o2=o2, n=n, g4=g4: e.dma_start(out=fconv_p[l][:, g4 * 128:(g4 + n) * 128], in_=ostgP[o2][:, 0:n * 128]), reads=[("ostgP", o2)])
                S.dma("sp", "foutS%d" % o2, lambda e, o2=o2, n=n, g4=g4: e.dma_start(out=fconv_s[l][:, g4 * 128:(g4 + n) * 128], in_=ostg[o2][:, 0:n * 128]), reads=[("ostg", o2)])
            S.barrier()

        def do_final():
            a_reset()
            yT = a_f32(KC, 512)
            ytok = [a_f32(D) for _ in range(2)]
            n = 4
            sq = a_bf16(KC, 512)
            rstd = a_f32(512)
            tmp = a_f32(KC, 512)
            shb = norm_shift[n]
            oi = 0
            for bi, (t0, nt) in enumerate(ALLBLOCKS):
                S.act(lambda e, t0=t0, nt=nt: e.activation(out=sq[:, :, 0:nt], in_=xT[:, :, t0:t0 + nt], func=AF.Square), reads=[], writes=["sq"])
                b = ps_next()
                for kc in range(KC):
                    S.pe(lambda e, b=b, kc=kc, nt=nt: e.matmul(ps[:, b, 0:nt], lhsT=ones_b[:], rhs=sq[:, kc, 0:nt], start=(kc == 0), stop=(kc == KC - 1)), reads=["sq", "ones_b"], writes=psk(b))
                S.act(lambda e, b=b, nt=nt: e.activation(out=rstd[:, 0:nt], in_=ps[:, b, 0:nt], func=AF.Sqrt, bias=EPS, scale=1.0 / D), reads=psk(b), writes=["rstd"])
                S.dve(lambda e, nt=nt: e.reciprocal(out=rstd[:, 0:nt], in_=rstd[:, 0:nt]), reads=["rstd"], writes=["rstd"])
                S.dve(lambda e, t0=t0, nt=nt: e.tensor_tensor(out=tmp[:, :, 0:nt], in0=xT[:, :, t0:t0 + nt], in1=rstd[:, 0:nt].unsqueeze(1).to_broadcast([128, KC, nt]), op=ALU.mult), reads=["rstd"], writes=["tmp"])
                for kc in range(KC):
                    if t0 < TP:
                        S.act(lambda e, kc=kc, nt=nt: e.activation(out=yT[:, kc, 0:nt], in_=tmp[:, kc, 0:nt], func=AF.Identity, scale=Amod[:, n, kc, 0:1], bias=modT[:, shb + kc, 0:1]), reads=["tmp"], writes=["yT"])
                    else:
                        S.dve(lambda e, kc=kc: e.tensor_tensor(out=tmp[:, kc, 0:TS].rearrange("p (s j) -> p s j", j=4), in0=tmp[:, kc, 0:TS].rearrange("p (s j) -> p s j", j=4), in1=Amod[:, n, kc, 1:17].unsqueeze(2).to_broadcast([128, NSEQ, 4]), op=ALU.mult), reads=["tmp"], writes=["tmp"])
                        S.dve(lambda e, kc=kc: e.tensor_tensor(out=yT[:, kc, 0:TS].rearrange("p (s j) -> p s j", j=4), in0=tmp[:, kc, 0:TS].rearrange("p (s j) -> p s j", j=4), in1=modT[:, shb + kc, 1:17].unsqueeze(2).to_broadcast([128, NSEQ, 4]), op=ALU.add), reads=["tmp"], writes=["yT"])
                for sub in range(0, nt, 128):
                    rows = min(128, nt - sub)
                    o2 = oi % 2
                    oi += 1
                    for half in range(2):
                        b = ps_next()
                        for q in range(4):
                            kc = half * 4 + q
                            S.pe(lambda e, b=b, q=q, kc=kc, sub=sub, rows=rows: e.transpose(ps[0:rows, b, q * 128:(q + 1) * 128], yT[:, kc, sub:sub + rows], ident_f[:]), reads=["yT", "ident_f"], writes=psk(b))
                        if half == 0:
                            S.act(lambda e, b=b, o2=o2, rows=rows, half=half: e.copy(out=ytok[o2][0:rows, half * 512:(half + 1) * 512], in_=ps[0:rows, b, :]), reads=psk(b), writes=[("ytok", o2, half)])
                        else:
                            S.dve(lambda e, b=b, o2=o2, rows=rows, half=half: e.tensor_copy(out=ytok[o2][0:rows, half * 512:(half + 1) * 512], in_=ps[0:rows, b, :]), reads=psk(b), writes=[("ytok", o2, half)])
                    if t0 < TP:
                        dst = y_p[t0 + sub:t0 + sub + rows, :]
                    else:
                        dst = y_s[sub:sub + rows, :]
                    S.dma("sp", "yo%d" % o2, lambda e, dst=dst, o2=o2, rows=rows: e.dma_start(out=dst, in_=ytok[o2][0:rows, :]), reads=[("ytok", o2, 0), ("ytok", o2, 1)])
            S.barrier()

        K.__dict__.update(locals())
        G_ = globals()
        do_norm(0)
        import os
        if "do_retention" in G_ and not os.environ.get("SKIP_RET"):
            G_["do_retention"](K)
        do_norm(1)
        do_ffn(0)
        do_norm(2)
        if "do_gdn" in G_:
            G_["do_gdn"](K)
        do_norm(3)
        do_ffn(1)
        do_final()
        if "modT" in dbg_out:
            S.dma("sp", "dbg", lambda e: e.dma_start(out=dbg_out["modT"], in_=modT[:].rearrange("p a b -> p (a b)")))
        if "Amod" in dbg_out:
            S.dma("sp", "dbg", lambda e: e.dma_start(out=dbg_out["Amod"], in_=Amod[:].rearrange("p a b c -> p (a b c)")))
        S.emit(st)
    return nc


def do_retention(K):
    S = K.S; ps = K.ps; nc = K.nc
    a_f32 = K.a_f32; a_bf16 = K.a_bf16; ps_next = K.ps_next; psk = K.psk
    hT = K.hT; xT = K.xT; modT = K.modT
    ident_b = K.ident_b
    g = _gammas()
    K.a_reset()
    K.tmpg = a_f32(TS)
    tabs = a_f32(4, 512)
    qb = a_bf16(2, 512)
    kb = a_bf16(2, 512)
    t1 = a_f32(512)
    t2 = a_f32(512)
    vb = a_bf16(512)
    sg = a_f32(512)
    kh = a_bf16(256)
    im = a_bf16(128)
    og = a_bf16(512)
    ogT = a_bf16(4, 512)
    ss = a_f32(2)
    maskP = a_f32(128)
    maskS = a_f32(128)
    ksc = a_f32(2 * RET_H)
    segcol = a_f32(NSEQ)
    S32 = a_f32(2, 512)
    Sbf = a_bf16(2, 512)
    qf = a_f32(2, TS)
    qX = a_f32(2, NSEQ, TS)
    khm = [a_bf16(256) for _ in range(2)]
    Sin = [a_f32(2, 512) for _ in range(2)]
    Sout = a_f32(2, 512)
    segmask = a_f32(NSEQ, TS)
    S.dma("sp", "rc", lambda e: e.dma_start(out=ksc, in_=K.c_kscale), writes=["ksc"])
    S.dma("sp", "rc", lambda e: e.dma_start(out=segcol, in_=K.c_segcol), writes=["segcol"])
    S.dma("sp", "rc", lambda e: e.dma_start(out=segmask.rearrange("p a b -> p (a b)"), in_=K.c_segmask), writes=["segmask"])
    gbase = 16
    w_in = K.w_ret_in
    w_out = K.w_ret_out
    sctr = [0]
    for h in range(RET_H):
        wq, wqk = K.load_w_bf16(w_in, KC, 256, col0=h * 256, slot=0, off=0, key=("w", 0, "q"))
        wk, wkk = K.load_w_bf16(w_in, KC, 256, col0=1024 + h * 256, slot=0, off=2048, key=("w", 0, "k"))
        wv, wvk = K.load_w_bf16(w_in, KC, 512, col0=2048 + h * 512, slot=1)
        wg, wgk = K.load_w_bf16(w_in, KC, 512, col0=4096 + h * 512, slot=2)
        wo, wok = K.load_w_bf16(w_out, 4, 1024, row0=h * 512, slot=3)
        S.dma("sp", "rm", lambda e, h=h: e.dma_start(out=maskP, in_=K.c_retmask[h]), writes=["maskP"])
        S.dma("sp", "rm", lambda e, h=h: e.dma_start(out=maskS, in_=K.c_retmask[RET_H + h]), writes=["maskS"])
        for bi, (t0, nt) in enumerate(K.ALLBLOCKS):
            isS = t0 >= TP
            C = 64 if isS else 128
            for ti, (hh, cs_) in enumerate([(h, 0), (h, 1), (RET_H, 0), (RET_H, 1)]):
                S.dma("sp", "rt", lambda e, ti=ti, hh=hh, cs_=cs_, t0=t0, nt=nt: e.dma_start(out=tabs[:, ti, 0:nt], in_=K.c_rope[hh, cs_, :, t0:t0 + nt]), writes=[("tabs", ti)])
            for which, (wt, wkey, dst, tb) in enumerate([(wq, wqk, qb, 0), (wk, wkk, kb, 2)]):
                pb = [ps_next(), ps_next()]
                for dc in range(2):
                    for kc in range(KC):
                        S.pe(lambda e, b=pb[dc], kc=kc, dc=dc, wt=wt, t0=t0, nt=nt: e.matmul(ps[:, b, 0:nt], lhsT=wt[:, kc, dc * 128:(dc + 1) * 128], rhs=hT[:, kc, t0:t0 + nt], start=(kc == 0), stop=(kc == KC - 1)),
                             reads=[wkey], writes=psk(pb[dc]))
                p1 = ps[:, pb[0], 0:nt]
                p2 = ps[:, pb[1], 0:nt]
                cc = tabs[:, tb, 0:nt]
                sn = tabs[:, tb + 1, 0:nt]
                to_f = isS and which == 0
                d1 = qf[:, 0, :] if to_f else dst[:, 0, 0:nt]
                d2 = qf[:, 1, :] if to_f else dst[:, 1, 0:nt]
                S.dve(lambda e, p1=p1, cc=cc, nt=nt: e.tensor_tensor(out=t1[:, 0:nt], in0=p1, in1=cc, op=ALU.mult), reads=psk(pb[0]) + [("tabs", tb)], writes=["t1"])
                S.dve(lambda e, p2=p2, sn=sn, nt=nt: e.tensor_tensor(out=t2[:, 0:nt], in0=p2, in1=sn, op=ALU.mult), reads=psk(pb[1]) + [("tabs", tb + 1)], writes=["t2"])
                S.pool(lambda e, d1=d1, nt=nt: e.tensor_tensor(out=d1, in0=t1[:, 0:nt], in1=t2[:, 0:nt], op=ALU.subtract), reads=["t1", "t2"], writes=[("rd", which, 0)])
                S.dve(lambda e, p1=p1, sn=sn, nt=nt: e.tensor_tensor(out=t1[:, 0:nt], in0=p1, in1=sn, op=ALU.mult), reads=psk(pb[0]) + [("tabs", tb + 1)], writes=["t1"])
                S.dve(lambda e, p2=p2, cc=cc, nt=nt: e.tensor_tensor(out=t2[:, 0:nt], in0=p2, in1=cc, op=ALU.mult), reads=psk(pb[1]) + [("tabs", tb)], writes=["t2"])
                S.pool(lambda e, d2=d2, nt=nt: e.tensor_tensor(out=d2, in0=t1[:, 0:nt], in1=t2[:, 0:nt], op=ALU.add), reads=["t1", "t2"], writes=[("rd", which, 1)])
                if to_f:
                    S.act(lambda e: e.copy(out=qb[:, :, 0:TS], in_=qf[:, :, :]), reads=[("rd", 0, 0), ("rd", 0, 1)], writes=["qbS"])
            qkeys = [("rd", 0, 0), ("rd", 0, 1)] + (["qbS"] if isS else [])
            kkeys = [("rd", 1, 0), ("rd", 1, 1)]
            for c0 in range(0, nt, C):
                first = (not isS) and t0 == 0 and c0 == 0
                last = (not isS) and (t0 + c0 + C == TP)
                bv = ps_next()
                for kc in range(KC):
                    S.pe(lambda e, bv=bv, kc=kc, t0=t0, c0=c0, C=C: e.matmul(ps[0:C, bv, :], lhsT=hT[:, kc, t0 + c0:t0 + c0 + C], rhs=wv[:, kc, :], start=(kc == 0), stop=(kc == KC - 1)), reads=[wvk], writes=psk(bv))
                S.act(lambda e, bv=bv, C=C: e.copy(out=vb[0:C, :], in_=ps[0:C, bv, :]), reads=psk(bv), writes=["vb"])
                bg = ps_next()
                for kc in range(KC):
                    S.pe(lambda e, bg=bg, kc=kc, t0=t0, c0=c0, C=C: e.matmul(ps[0:C, bg, :], lhsT=hT[:, kc, t0 + c0:t0 + c0 + C], rhs=wg[:, kc, :], start=(kc == 0), stop=(kc == KC - 1)), reads=[wgk], writes=psk(bg))
                S.act(lambda e, bg=bg, C=C: e.activation(out=sg[0:C, :], in_=ps[0:C, bg, :], func=AF.Silu), reads=psk(bg), writes=["sg"])
                bk = ps_next()
                psb = ps[:, bk, :].bitcast(BF16)
                for dc in range(2):
                    S.pe(lambda e, psb=psb, dc=dc, c0=c0, C=C: e.transpose(psb[0:C, dc * 128:(dc + 1) * 128], kb[:, dc, c0:c0 + C], ident_b[:]), reads=kkeys + ["ident_b"], writes=psk(bk))
                kcol = (RET_H + h) if isS else h
                S.act(lambda e, psb=psb, C=C, kcol=kcol: e.activation(out=kh[0:C, :], in_=psb[0:C, 0:256], func=AF.Identity, scale=ksc[0:C, kcol:kcol + 1]), reads=psk(bk) + ["ksc"], writes=["kh"])
                bi_ = ps_next()
                for dc in range(2):
                    S.pe(lambda e, bi_=bi_, dc=dc, c0=c0, C=C: e.matmul(ps[0:C, bi_, 0:C], lhsT=kb[:, dc, c0:c0 + C], rhs=qb[:, dc, c0:c0 + C], start=(dc == 0), stop=(dc == 1)), reads=kkeys + qkeys, writes=psk(bi_))
                mk = maskS if isS else maskP
                S.dve(lambda e, bi_=bi_, C=C, mk=mk: e.tensor_tensor(out=im[0:C, 0:C], in0=ps[0:C, bi_, 0:C], in1=mk[0:C, 0:C], op=ALU.mult), reads=psk(bi_) + ["maskP", "maskS"], writes=["im"])
                bo = ps_next()
                has_inter = isS or not first
                S.pe(lambda e, bo=bo, C=C, has_inter=has_inter: e.matmul(ps[0:C, bo, :], lhsT=im[0:C, 0:C], rhs=vb[0:C, :], start=True, stop=not has_inter), reads=["im", "vb"], writes=psk(bo))
                if not isS:
                    if not first:
                        for dc in range(2):
                            S.pe(lambda e, bo=bo, dc=dc, c0=c0, C=C: e.matmul(ps[0:C, bo, :], lhsT=qb[:, dc, c0:c0 + C], rhs=Sbf[:, dc, :], start=False, stop=(dc == 1)), reads=qkeys + [("Sbf", dc)], writes=psk(bo))
                    for dc in range(2):
                        bs = ps_next()
                        S.pe(lambda e, bs=bs, dc=dc, C=C: e.matmul(ps[:, bs, :], lhsT=kh[0:C, dc * 128:(dc + 1) * 128], rhs=vb[0:C, :], start=True, stop=True), reads=["kh", "vb"], writes=psk(bs))
                        if first:
                            S.dve(lambda e, bs=bs, dc=dc: e.tensor_copy(out=S32[:, dc, :], in_=ps[:, bs, :]), reads=psk(bs), writes=[("S32", dc)])
                        else:
                            S.dve(lambda e, bs=bs, dc=dc, gc=float(g[h] ** 128): e.scalar_tensor_tensor(out=S32[:, dc, :], in0=S32[:, dc, :], scalar=gc, in1=ps[:, bs, :], op0=ALU.mult, op1=ALU.add), reads=psk(bs) + [("S32", dc)], writes=[("S32", dc)])
                        if last:
                            S.dma("sp", "rpo", lambda e, dc=dc, h=h: e.dma_start(out=K.ret_p[h, dc * 128:(dc + 1) * 128, :], in_=S32[:, dc, :]), reads=[("S32", dc)])
                        else:
                            S.act(lambda e, dc=dc: e.copy(out=Sbf[:, dc, :], in_=S32[:, dc, :]), reads=[("S32", dc)], writes=[("Sbf", dc)])
                else:
                    K.P.reserved = {bo}
                    for dc in range(2):
                        S.dve(lambda e, dc=dc: e.tensor_tensor(out=qX[:, dc, :, :], in0=qf[:, dc, :].unsqueeze(1).to_broadcast([128, NSEQ, TS]), in1=segmask[:, :, :], op=ALU.mult), reads=[("rd", 0, dc), "segmask"], writes=[("qX", dc)])
                    for s_ in range(NSEQ):
                        i2 = s_ % 2
                        S.dma("sp", "sin%d" % i2, lambda e, i2=i2, s_=s_, h=h: e.dma_start(out=Sin[i2], in_=K.s_ret[s_, h].rearrange("(dc p) v -> p dc v", p=128)), writes=[("Sin", i2)])
                        for dc in range(2):
                            S.pe(lambda e, bo=bo, dc=dc, s_=s_, i2=i2: e.matmul(ps[0:TS, bo, :], lhsT=qX[:, dc, s_, :], rhs=Sin[i2][:, dc, :], start=False, stop=(s_ == NSEQ - 1 and dc == 1)), reads=[("qX", dc), ("Sin", i2)], writes=psk(bo))
                        S.dve(lambda e, i2=i2, s_=s_: e.tensor_scalar(out=khm[i2][0:TS, :], in0=kh[0:TS, :], scalar1=segcol[0:TS, s_:s_ + 1], scalar2=None, op0=ALU.mult), reads=["kh", "segcol"], writes=[("khm", i2)])
                        for dc in range(2):
                            bs = ps_next()
                            S.pe(lambda e, bs=bs, dc=dc, i2=i2: e.matmul(ps[:, bs, :], lhsT=khm[i2][0:TS, dc * 128:(dc + 1) * 128], rhs=vb[0:TS, :], start=True, stop=True), reads=[("khm", i2), "vb"], writes=psk(bs))
                            S.dve(lambda e, bs=bs, dc=dc, i2=i2, gc=float(g[h] ** 4): e.scalar_tensor_tensor(out=Sout[:, dc, :], in0=Sin[i2][:, dc, :], scalar=gc, in1=ps[:, bs, :], op0=ALU.mult, op1=ALU.add), reads=psk(bs) + [("Sin", i2)], writes=[("Sout", dc)])
                        S.dma("sp", "sout", lambda e, s_=s_, h=h: e.dma_start(out=K.ret_s[s_, h].rearrange("(dc p) v -> p dc v", p=128), in_=Sout), reads=[("Sout", 0), ("Sout", 1)], writes=[])
                    K.P.reserved = set()
                _ret_tail(K, h, bo, C, c0, sg, og, ogT, ss, t1)
            for dc8 in range(KC):
                b = ps_next()
                for ec in range(4):
                    S.pe(lambda e, b=b, ec=ec, dc8=dc8, nt=nt: e.matmul(ps[:, b, 0:nt], lhsT=wo[:, ec, dc8 * 128:(dc8 + 1) * 128], rhs=ogT[:, ec, 0:nt], start=(ec == 0), stop=(ec == 3)), reads=[wok, "ogT"], writes=psk(b))
                K.resid_update(ps[:, b, 0:nt], gbase, dc8, ("S" if isS else bi), t0, nt, psk(b))
        S.barrier()
    S.barrier()


def _ret_tail(K, h, bo, C, c0, sg, og, ogT, ss, junk):
    S = K.S; ps = K.ps
    S.act(lambda e: e.activation(out=junk[0:C, :], in_=ps[0:C, bo, :], func=AF.Square, accum_out=ss[0:C, 0:1]), reads=K.psk(bo), writes=["t1", "ss"])
    S.act(lambda e: e.activation(out=ss[0:C, 1:2], in_=ss[0:C, 0:1], func=AF.Sqrt, bias=EPS, scale=1.0 / 512), reads=["ss"], writes=["ss2"])
    S.dve(lambda e: e.reciprocal(out=ss[0:C, 1:2], in_=ss[0:C, 1:2]), reads=["ss2"], writes=["ss2"])
    S.dve(lambda e: e.scalar_tensor_tensor(out=og[0:C, :], in0=ps[0:C, bo, :], scalar=ss[0:C, 1:2], in1=sg[0:C, :], op0=ALU.mult, op1=ALU.mult), reads=K.psk(bo) + ["ss2", "sg"], writes=["og"])
    bt = K.ps_next()
    psb = ps[:, bt, :].bitcast(BF16)
    for ec in range(4):
        S.pe(lambda e, ec=ec: e.transpose(psb[:, ec * C:(ec + 1) * C], og[0:C, ec * 128:(ec + 1) * 128], K.ident_b[0:C, 0:C]), reads=["og", "ident_b"], writes=K.psk(bt))
    S.act(lambda e: e.copy(out=ogT[:, :, c0:c0 + C], in_=psb[:, 0:4 * C].rearrange("p (a b) -> p a b", a=4)), reads=K.psk(bt), writes=["ogT"])


def do_gdn(K):
    S = K.S; ps = K.ps
    a_f32 = K.a_f32; a_bf16 = K.a_bf16; ps_next = K.ps_next; psk = K.psk
    hT = K.hT; ident_f = K.ident_f; ident_b = K.ident_b; ones_b = K.ones_b
    w_in = K.w_gdn_in; w_out = K.w_gdn_out
    K.a_reset()
    K.tmpg = a_f32(TS)
    gm = a_f32(8, 128)
    segmask = a_f32(NSEQ, TS)
    segcol = a_f32(NSEQ)
    beta_all = a_f32(17, 16); g_all = a_f32(17, 16); negeG_all = a_f32(17, 16); kdec_all = a_f32(17, 16)
    gv = a_f32(32); negA = a_f32(16); gnw = a_f32(128)
    wcT = a_f32(32, 4)
    ones128 = a_f32(128)
    wba = a_bf16(KC, 32)
    Gx = a_f32(3 + 512); acc = a_f32(512); acc2 = a_f32(512); cch = a_f32(4, 3); gsT = a_f32(4, 48); Gxs = a_f32(NSEQ, 7)
    tailP = a_f32(4, 3); tailS = a_f32(4, NSEQ, 3)
    vs = a_f32(2, 512)
    sqb = a_bf16(512); rs = a_f32(512)
    qn = a_bf16(512); kn = a_bf16(512); qnf = a_f32(TS); knf = a_f32(TS)
    Bm = a_f32(2, 128); E = a_f32(2, 128); E2 = a_f32(2, 128); DTm = a_f32(2, 128)
    U = a_bf16(2, 128); UT = a_bf16(2, 128)
    UoM = [a_bf16(2, 128) for _ in range(3)]; UoTM = [a_bf16(2, 128) for _ in range(3)]
    Nb = [a_bf16(2, 128) for _ in range(2)]; Pb = [a_bf16(2, 128) for _ in range(2)]; PTb = [a_bf16(2, 128) for _ in range(2)]
    NTb = [a_bf16(2, 128) for _ in range(2)]; bm = a_bf16(4, 128)
    ktok = a_bf16(128); xv = a_bf16(2, 128); vnew = a_bf16(2, 128)
    Sbf = a_bf16(2, 128); og = a_bf16(256); ogT = a_bf16(2, 512)
    S32 = a_f32(2, 128); ss = a_f32(8)
    X = a_f32(NSEQ, TS); qgf = a_f32(TS)
    kdm = a_bf16(128); Sin = [a_f32(128) for _ in range(2)]; Sout = a_f32(128)
    ostg = acc
    eGl = [a_f32(2), a_f32(2)]
    w3f = K.wring[:, 3, :].bitcast(F32)
    w3b = K.wring[:, 3, :]
    szn2 = [w3f[:, 0:256], w3f[:, 256:512]]
    vtok2 = [w3f[:, 512:768].rearrange("p (a b) -> p a b", a=2), w3f[:, 768:1024].rearrange("p (a b) -> p a b", a=2)]

    def _b3(i):
        return w3b[:, 2048 + i * 256:2048 + (i + 1) * 256].rearrange("p (a b) -> p a b", a=2)
    Nfin = [_b3(0), _b3(1)]; attnT2 = [_b3(2), _b3(3)]; qg2 = [_b3(4), _b3(5)]; kd2 = [_b3(6), _b3(7)]

    def _f32v(b):
        return b.rearrange("p a b -> p (a b)").bitcast(F32)
    SinR = [Sin[0], Sin[1]] + [_f32v(x) for x in UoM + UoTM]
    SoutR = [Sout, _f32v(NTb[0]), _f32v(NTb[1])]
    RDEPTH = 7
    NLD = 2 * NSEQ
    mark = K.A.off

    S.dma("sp", "gc", lambda e: e.dma_start(out=gm.rearrange("p a b -> p (a b)"), in_=K.c_gmask), writes=["gm"])
    S.dma("sp", "gc", lambda e: e.dma_start(out=segmask.rearrange("p a b -> p (a b)"), in_=K.c_segmask), writes=["segmask"])
    S.dma("sp", "gc", lambda e: e.dma_start(out=segcol, in_=K.c_segcol), writes=["segcol"])
    S.dma("pool", "gcb", lambda e: e.dma_start(out=bm.rearrange("p a b -> p (a b)"), in_=K.c_bmask), writes=["bm"])
    S.dma("sp", "gc", lambda e: e.dma_start(out=gv, in_=K.gdn_vec.partition_broadcast(128)), writes=["gv"])
    S.dma("sp", "gc", lambda e: e.dma_start(out=gnw, in_=K.gdn_norm.partition_broadcast(128)), writes=["gnw"])
    S.dve(lambda e: e.memset(ones128, 1.0), writes=["ones128"])
    S.act(lambda e: e.activation(out=negA, in_=gv[:, 0:16], func=AF.Exp), reads=["gv"], writes=["negA"])
    S.dve(lambda e: e.tensor_scalar(out=negA, in0=negA, scalar1=-1.0, scalar2=None, op0=ALU.mult), reads=["negA"], writes=["negA"])
    TRIU = {False: gm[:, 0, :], True: gm[:, 4, :]}
    SU = {False: gm[:, 1, :], True: gm[:, 5, :]}
    INCL = {False: gm[:, 2, :], True: gm[:, 6, :]}
    STRICT = {False: gm[:, 3, :], True: gm[:, 7, :]}
    import os
    STG = int(os.environ.get('GDN_STAGE', 99))
    if STG == 0:
        S.barrier(); return
    Xflat = X.rearrange("p a b -> p (a b)")
    wst = [Xflat[0:4, 0:512], Xflat[0:4, 512:1024]]
    bw = ps_next()
    for pc in range(8):
        S.dma("sp", "wst%d" % (pc % 2), lambda e, pc=pc: e.dma_start(out=wst[pc % 2], in_=K.w_gconv[:, pc * 512:(pc + 1) * 512]), writes=[("wst", pc % 2)])
        for q in range(4):
            cidx = pc * 4 + q
            S.pe(lambda e, pc=pc, q=q, cidx=cidx: e.transpose(ps[:, bw, cidx * 4:(cidx + 1) * 4], wst[pc % 2][:, q * 128:(q + 1) * 128], ident_f[0:4, 0:4]), reads=[("wst", pc % 2), "ident_f"], writes=psk(bw))
    S.dve(lambda e: e.tensor_copy(out=wcT.rearrange("p a b -> p (a b)"), in_=ps[:, bw, 0:128]), reads=psk(bw), writes=["wcT"])
    if STG == 1:
        S.barrier(); return
    src = w_in[:, 6144:6176].rearrange("(k p) c -> p k c", p=128)
    S.dma("pool", "wba", lambda e: e.dma_start(out=wba, in_=src), writes=["wba"])
    if STG == 2:
        S.barrier(); return
    tiles = [(t * 128, 128, False) for t in range(16)] + [(TP, TS, True)]
    for tl, (t0, C, isS) in enumerate(tiles):
        pb = ps_next()
        for kc in range(KC):
            S.pe(lambda e, pb=pb, kc=kc, t0=t0, C=C: e.matmul(ps[0:C, pb, 0:32], lhsT=hT[:, kc, t0:t0 + C], rhs=wba[:, kc, :], start=(kc == 0), stop=(kc == KC - 1)), reads=["wba"], writes=psk(pb))
        S.act(lambda e, pb=pb, C=C, tl=tl: e.activation(out=beta_all[0:C, tl, :], in_=ps[0:C, pb, 0:16], func=AF.Sigmoid), reads=psk(pb), writes=[("beta", tl)])
        S.dve(lambda e, pb=pb, C=C, tl=tl: e.tensor_tensor(out=g_all[0:C, tl, :], in0=ps[0:C, pb, 16:32], in1=gv[0:C, 16:32], op=ALU.add), reads=psk(pb) + ["gv"], writes=[("g", tl)])
        S.act(lambda e, C=C, tl=tl: e.activation(out=g_all[0:C, tl, :], in_=g_all[0:C, tl, :], func=AF.Exp), reads=[("g", tl)], writes=[("g", tl)])
        S.act(lambda e, C=C, tl=tl: e.activation(out=g_all[0:C, tl, :], in_=g_all[0:C, tl, :], func=AF.Ln, bias=1.0), reads=[("g", tl)], writes=[("g", tl)])
        S.dve(lambda e, C=C, tl=tl: e.tensor_tensor(out=g_all[0:C, tl, :], in0=g_all[0:C, tl, :], in1=negA[0:C, :], op=ALU.mult), reads=[("g", tl), "negA"], writes=[("g", tl)])
        pg = ps_next()
        S.pe(lambda e, pg=pg, C=C, tl=tl, isS=isS: e.matmul(ps[0:C, pg, 0:16], lhsT=TRIU[isS][0:C, 0:C], rhs=g_all[0:C, tl, :], start=True, stop=True), reads=[("g", tl), "gm"], writes=psk(pg))
        S.pe(lambda e, pg=pg, C=C, tl=tl, isS=isS: e.matmul(ps[0:C, pg, 16:32], lhsT=SU[isS][0:C, 0:C], rhs=g_all[0:C, tl, :], start=True, stop=True), reads=[("g", tl), "gm"], writes=psk(pg))
        S.act(lambda e, pg=pg, C=C, tl=tl: e.activation(out=negeG_all[0:C, tl, :], in_=ps[0:C, pg, 0:16], func=AF.Exp), reads=psk(pg), writes=[("negeG", tl)])
        S.dve(lambda e, C=C, tl=tl: e.tensor_scalar(out=negeG_all[0:C, tl, :], in0=negeG_all[0:C, tl, :], scalar1=-1.0, scalar2=None, op0=ALU.mult), reads=[("negeG", tl)], writes=[("negeG", tl)])
        S.act(lambda e, pg=pg, C=C, tl=tl: e.activation(out=kdec_all[0:C, tl, :], in_=ps[0:C, pg, 16:32], func=AF.Exp), reads=psk(pg), writes=[("kdec", tl)])
    S.barrier()
    K.A.off = mark
    gbase = 48 + 16

    import os
    for hk in range(int(os.environ.get("GDN_HK0", 0)), int(os.environ.get("GDN_NHK", GDN_HK))):
        hv0 = 2 * hk
        wq, wqk = K.load_w_bf16(w_in, KC, 128, col0=hk * 128, slot=0, off=0, key=("w", 0, "q"))
        wk, wkk = K.load_w_bf16(w_in, KC, 128, col0=1024 + hk * 128, slot=0, off=1024, key=("w", 0, "k"))
        wv, wvk = K.load_w_bf16(w_in, KC, 256, col0=2048 + hk * 256, slot=1, off=0, key=("w", 1, "v"))
        wz, wzk = K.load_w_bf16(w_in, KC, 256, col0=4096 + hk * 256, slot=1, off=2048, key=("w", 1, "z"))
        wo, wok = K.load_w_bf16(w_out, 2, 1024, row0=hk * 256, slot=2)
        cids = [hk, 8 + hk, 16 + 2 * hk, 17 + 2 * hk]
        CUT = os.environ.get('GDN_CUT', '')
        if hk >= 1 and 'D' in CUT:
            S.barrier(); continue
        gst = a_f32(512, parts=48)
        K.A.off = mark
        if not (hk >= 1 and 'H' in CUT):
            for ci, cid in enumerate(cids):
                S.dma("sp", "gst", lambda e, ci=ci, cid=cid, gst=gst: e.dma_start(out=gst[:, ci * 128:(ci + 1) * 128], in_=K.s_gconv[:, cid * 128:(cid + 1) * 128]), writes=["gst"])
            bq = ps_next()
            for ci in range(4):
                S.pe(lambda e, ci=ci, bq=bq, gst=gst: e.transpose(ps[:, bq, ci * 48:(ci + 1) * 48], gst[:, ci * 128:(ci + 1) * 128], ident_f[0:48, 0:48]), reads=["gst", "ident_f"], writes=psk(bq))
            S.dve(lambda e, bq=bq: e.tensor_copy(out=gsT.rearrange("p a b -> p (a b)"), in_=ps[:, bq, 0:192]), reads=psk(bq), writes=["gsT"])
        S.dve(lambda e: e.memset(cch, 0.0), writes=["cch"])
        if hk >= 1 and 'E' in CUT:
            S.barrier(); continue
        for bi, (t0, nt) in enumerate(K.ALLBLOCKS):
            isS = t0 >= TP
            C = 64 if isS else 128
            lastblk = (t0 + nt == TP)
            if isS:
                S.barrier()
            for ci, (wt, wkey, col) in enumerate([(wq, wqk, 0), (wk, wkk, 0), (wv, wvk, 0), (wv, wvk, 128)]):
                pp = ps_next()
                for kc in range(KC):
                    S.pe(lambda e, pp=pp, kc=kc, wt=wt, col=col, t0=t0, nt=nt: e.matmul(ps[:, pp, 0:nt], lhsT=wt[:, kc, col:col + 128], rhs=hT[:, kc, t0:t0 + nt], start=(kc == 0), stop=(kc == KC - 1)), reads=[wkey], writes=psk(pp))
                cid = cids[ci]
                wc = [wcT[:, cid, i:i + 1] for i in range(4)]
                accb = acc if ci % 2 == 0 else acc2
                ak = ("acc", ci % 2)
                dst = accb[:, 0:nt] if ci < 2 else vs[:, ci - 2, 0:nt]
                if not isS:
                    S.dve(lambda e, ci=ci: e.tensor_copy(out=Gx[:, 0:3], in_=cch[:, ci, :]), reads=["cch"], writes=["Gx"])
                    S.act(lambda e, pp=pp, nt=nt: e.copy(out=Gx[:, 3:3 + nt], in_=ps[:, pp, 0:nt]), reads=psk(pp), writes=["Gx"])
                    S.dve(lambda e, ci=ci, nt=nt: e.tensor_copy(out=cch[:, ci, :], in_=Gx[:, nt:nt + 3]), reads=["Gx"], writes=["cch"])
                    if lastblk:
                        S.dve(lambda e, ci=ci, nt=nt: e.tensor_copy(out=tailP[:, ci, :], in_=Gx[:, nt:nt + 3]), reads=["Gx"], writes=["tailP"])
                    x3 = [Gx[:, i:i + nt] for i in range(4)]
                    a_ = accb[:, 0:nt]
                    d_ = dst
                    gk = ["Gx"]
                else:
                    S.dve(lambda e, ci=ci: e.tensor_copy(out=Gxs[:, :, 0:3], in_=gsT[:, ci, :].rearrange("p (s r) -> p s r", r=3)), reads=["gsT"], writes=["Gxs"])
                    S.act(lambda e, pp=pp: e.copy(out=Gxs[:, :, 3:7], in_=ps[:, pp, 0:TS].rearrange("p (s j) -> p s j", j=4)), reads=psk(pp), writes=["Gxs"])
                    S.dve(lambda e, ci=ci: e.tensor_copy(out=tailS[:, ci, :, :], in_=Gxs[:, :, 4:7]), reads=["Gxs"], writes=["tailS"])
                    x3 = [Gxs[:, :, i:i + 4] for i in range(4)]
                    a_ = accb[:, 0:TS].rearrange("p (s j) -> p s j", j=4)
                    d_ = dst.rearrange("p (s j) -> p s j", j=4)
                    gk = ["Gxs"]
                S.act(lambda e, a_=a_, x3=x3, wc=wc: e.activation(out=a_, in_=x3[3], func=AF.Identity, scale=wc[3]), reads=gk + ["wcT"], writes=[ak])
                for i in (2, 1, 0):
                    S.dve(lambda e, a_=a_, x3=x3, wc=wc, i=i: e.scalar_tensor_tensor(out=a_, in0=x3[i], scalar=wc[i], in1=a_, op0=ALU.mult, op1=ALU.add), reads=gk + [ak], writes=[ak])
                S.act(lambda e, a_=a_, d_=d_: e.activation(out=d_, in_=a_, func=AF.Silu), reads=[ak], writes=([ak, ("cv", ci)] if ci < 2 else [("cv", ci)]))
                if ci < 2:
                    S.act(lambda e, nt=nt, accb=accb: e.activation(out=sqb[:, 0:nt], in_=accb[:, 0:nt], func=AF.Square), reads=[ak], writes=["sqb"])
                    pn = ps_next()
                    S.pe(lambda e, pn=pn, nt=nt: e.matmul(ps[:, pn, 0:nt], lhsT=ones_b[:], rhs=sqb[:, 0:nt], start=True, stop=True), reads=["sqb", "ones_b"], writes=psk(pn))
                    S.act(lambda e, pn=pn, nt=nt: e.activation(out=rs[:, 0:nt], in_=ps[:, pn, 0:nt], func=AF.Sqrt, bias=EPS, scale=1.0), reads=psk(pn), writes=["rs"])
                    S.dve(lambda e, nt=nt: e.reciprocal(out=rs[:, 0:nt], in_=rs[:, 0:nt]), reads=["rs"], writes=["rs"])
                    dn = qn if ci == 0 else kn
                    dnf = qnf if ci == 0 else knf
                    sc = (128.0 ** -0.5) if ci == 0 else 1.0
                    if isS:
                        S.dve(lambda e, dnf=dnf, sc=sc, accb=accb: e.scalar_tensor_tensor(out=dnf[:, :], in0=accb[:, 0:TS], scalar=sc, in1=rs[:, 0:TS], op0=ALU.mult, op1=ALU.mult), reads=[ak, "rs"], writes=[("nf", ci)])
                        S.act(lambda e, dn=dn, dnf=dnf: e.copy(out=dn[:, 0:TS], in_=dnf[:, :]), reads=[("nf", ci)], writes=[("n", ci)])
                    else:
                        S.dve(lambda e, dn=dn, sc=sc, nt=nt, accb=accb: e.scalar_tensor_tensor(out=dn[:, 0:nt], in0=accb[:, 0:nt], scalar=sc, in1=rs[:, 0:nt], op0=ALU.mult, op1=ALU.mult), reads=[ak, "rs"], writes=[("n", ci)])
            def chunk_gen(c0, pb):
                tl = (t0 + c0) // 128
                first = (not isS) and t0 == 0 and c0 == 0
                last = (not isS) and (t0 + c0 + C == TP)
                L = 1 if isS else 6
                bvt = ps_next()
                for e_ in range(2):
                    S.pe(lambda e, e_=e_, bvt=bvt, c0=c0, C=C: e.transpose(ps[0:C, bvt, e_ * 128:(e_ + 1) * 128], vs[:, e_, c0:c0 + C], ident_f[:]), reads=[("cv", 2 + e_), "ident_f"], writes=psk(bvt))
                    yield
                S.act(lambda e, bvt=bvt, C=C: e.copy(out=vtok2[pb][0:C].rearrange("p a b -> p (a b)"), in_=ps[0:C, bvt, 0:256]), reads=psk(bvt), writes=[("vtok", pb)])
                yield
                bz = ps_next()
                for kc in range(KC):
                    S.pe(lambda e, bz=bz, kc=kc, t0=t0, c0=c0, C=C: e.matmul(ps[0:C, bz, 0:256], lhsT=hT[:, kc, t0 + c0:t0 + c0 + C], rhs=wz[:, kc, :], start=(kc == 0), stop=(kc == KC - 1)), reads=[wzk], writes=psk(bz))
                    yield
                S.act(lambda e, bz=bz, C=C: e.activation(out=szn2[pb][0:C, :], in_=ps[0:C, bz, 0:256], func=AF.Silu), reads=psk(bz), writes=[("szn", pb)])
                yield
                S.dve(lambda e, C=C: e.tensor_tensor(out=szn2[pb][0:C, :].rearrange("p (a b) -> p a b", a=2), in0=szn2[pb][0:C, :].rearrange("p (a b) -> p a b", a=2), in1=gnw[0:C, :].unsqueeze(1).to_broadcast([C, 2, 128]), op=ALU.mult), reads=[("szn", pb), "gnw"], writes=[("szn", pb)])
                yield
                bkt = ps_next()
                pkb = ps[:, bkt, :].bitcast(BF16)
                S.pe(lambda e, pkb=pkb, c0=c0, C=C: e.transpose(pkb[0:C, 0:128], kn[:, c0:c0 + C], ident_b[:]), reads=[("n", 1), "ident_b"], writes=psk(bkt))
                yield
                S.act(lambda e, pkb=pkb, C=C: e.copy(out=ktok[0:C, :], in_=pkb[0:C, 0:128]), reads=psk(bkt), writes=["ktok"])
                yield
                bkk = ps_next()
                S.pe(lambda e, bkk=bkk, c0=c0, C=C: e.matmul(ps[0:C, bkk, 0:C], lhsT=kn[:, c0:c0 + C], rhs=kn[:, c0:c0 + C], start=True, stop=True), reads=[("n", 1)], writes=psk(bkk))
                yield
                S.pe(lambda e, bkk=bkk, c0=c0, C=C: e.matmul(ps[0:C, bkk, 128:128 + C], lhsT=kn[:, c0:c0 + C], rhs=qn[:, c0:c0 + C], start=True, stop=True), reads=[("n", 0), ("n", 1)], writes=psk(bkk))
                yield
                bd = ps_next()
                bd2 = ps_next()
                for e_ in range(2):
                    hv = hv0 + e_
                    S.dve(lambda e, e_=e_, hv=hv, C=C, tl=tl, isS=isS: e.tensor_scalar(out=Bm[0:C, e_, 0:C], in0=TRIU[isS][0:C, 0:C], scalar1=g_all[0:C, tl, hv:hv + 1], scalar2=None, op0=ALU.mult), reads=["gm"], writes=[("Bm", e_)])
                    yield
                    S.pe(lambda e, e_=e_, bd=bd, C=C, isS=isS: e.matmul(ps[0:C, bd, e_ * 128:e_ * 128 + C], lhsT=SU[isS][0:C, 0:C], rhs=Bm[0:C, e_, 0:C], start=True, stop=True), reads=[("Bm", e_), "gm"], writes=psk(bd))
                    yield
                    S.pe(lambda e, e_=e_, bd2=bd2, C=C: e.matmul(ps[:, bd2, e_ * 128:e_ * 128 + C], lhsT=ones128[0:C, :], rhs=Bm[0:C, e_, 0:C], start=True, stop=True), reads=[("Bm", e_), "ones128"], writes=psk(bd2))
                    yield
                S.act(lambda e, bd=bd, C=C: e.activation(out=E[0:C, :, 0:C], in_=ps[0:C, bd, 0:256].rearrange("p (a b) -> p a b", a=2)[:, :, 0:C], func=AF.Exp), reads=psk(bd), writes=["E"])
                yield
                S.act(lambda e, bd2=bd2, C=C: e.activation(out=E2[:, :, 0:C], in_=ps[:, bd2, 0:256].rearrange("p (a b) -> p a b", a=2)[:, :, 0:C], func=AF.Exp), reads=psk(bd2), writes=["E2"])
                yield
                S.dve(lambda e, C=C, isS=isS: e.tensor_tensor(out=DTm[0:C, :, 0:C], in0=E[0:C, :, 0:C], in1=INCL[isS][0:C, 0:C].unsqueeze(1).to_broadcast([C, 2, C]), op=ALU.mult), reads=["E", "gm"], writes=["DTm"])
                yield
                S.dve(lambda e, C=C, isS=isS: e.tensor_tensor(out=E[0:C, :, 0:C], in0=E[0:C, :, 0:C], in1=STRICT[isS][0:C, 0:C].unsqueeze(1).to_broadcast([C, 2, C]), op=ALU.mult), reads=["E", "gm", "DTm"], writes=["E"])
                yield
                for e_ in range(2):
                    hv = hv0 + e_
                    S.dve(lambda e, e_=e_, hv=hv, bkk=bkk, C=C, tl=tl: e.scalar_tensor_tensor(out=U[0:C, e_, 0:C], in0=ps[0:C, bkk, 0:C], scalar=beta_all[0:C, tl, hv:hv + 1], in1=E[0:C, e_, 0:C], op0=ALU.mult, op1=ALU.mult), reads=psk(bkk) + ["E"], writes=[("U", e_)])
                    yield
                    S.dve(lambda e, e_=e_, bkk=bkk, C=C: e.tensor_tensor(out=attnT2[pb][0:C, e_, 0:C], in0=ps[0:C, bkk, 128:128 + C], in1=DTm[0:C, e_, 0:C], op=ALU.mult), reads=psk(bkk) + ["DTm"], writes=[("attnT", e_, pb)])
                    yield
                    if isS:
                        pass
                    else:
                        S.pool(lambda e, e_=e_, c0=c0, C=C: e.tensor_tensor(out=qg2[pb][:, e_, 0:C], in0=qn[:, c0:c0 + C], in1=E2[:, e_, 0:C], op=ALU.mult), reads=[("n", 0), "E2"], writes=[("qg", e_, pb)])
                        yield
                    S.dve(lambda e, e_=e_, hv=hv, C=C, tl=tl: e.tensor_scalar(out=kd2[pb][0:C, e_, :], in0=ktok[0:C, :], scalar1=kdec_all[0:C, tl, hv:hv + 1], scalar2=None, op0=ALU.mult), reads=["ktok"], writes=[("kd", e_, pb)])
                    yield
                but = ps_next()
                pub = ps[:, but, :].bitcast(BF16)
                for e_ in range(2):
                    S.pe(lambda e, e_=e_, pub=pub, C=C: e.transpose(pub[0:C, e_ * 128:e_ * 128 + C], U[0:C, e_, 0:C], ident_b[0:C, 0:C]), reads=[("U", e_), "ident_b"], writes=psk(but))
                    yield
                S.act(lambda e, pub=pub, C=C: e.copy(out=UT[0:C, :, 0:C], in_=pub[0:C, 0:256].rearrange("p (a b) -> p a b", a=2)[:, :, 0:C]), reads=psk(but), writes=["UT"])
                yield
                if isS:
                    S.dve(lambda e, C=C: e.tensor_tensor(out=Nb[0][0:C, :, 0:C], in0=ident_f[0:C, 0:C].unsqueeze(1).to_broadcast([C, 2, C]), in1=U[0:C, :, 0:C], op=ALU.subtract), reads=[("U", 0), ("U", 1), "ident_f"], writes=[("N", 0)])
                    yield
                    Pprev, PTprev, Pk_, PTk_ = U, UT, ["U0", "U1"], ["UT"]
                    Pkeys_prev = [("U", 0), ("U", 1)]
                    PTkeys_prev = ["UT"]
                    ni = 0
                    for lv in range(1, L + 1):
                        pi = lv % 2
                        need_P = lv < L
                        if need_P:
                            b1 = ps_next()
                            for e_ in range(2):
                                S.pe(lambda e, e_=e_, b1=b1, C=C, PTprev=PTprev, Pprev=Pprev: e.matmul(ps[0:C, b1, e_ * 128:e_ * 128 + C], lhsT=PTprev[0:C, e_, 0:C], rhs=Pprev[0:C, e_, 0:C], start=True, stop=True), reads=Pkeys_prev + PTkeys_prev, writes=psk(b1))
                                yield
                            S.act(lambda e, b1=b1, C=C, pi=pi: e.copy(out=Pb[pi][0:C, :, 0:C], in_=ps[0:C, b1, 0:256].rearrange("p (a b) -> p a b", a=2)[:, :, 0:C]), reads=psk(b1), writes=[("P", pi)])
                            yield
                        b2 = ps_next()
                        for e_ in range(2):
                            S.pe(lambda e, e_=e_, b2=b2, C=C, PTprev=PTprev, Pprev=Pprev: e.matmul(ps[0:C, b2, e_ * 128:e_ * 128 + C], lhsT=Pprev[0:C, e_, 0:C], rhs=PTprev[0:C, e_, 0:C], start=True, stop=True), reads=Pkeys_prev + PTkeys_prev, writes=psk(b2))
                            yield
                        S.act(lambda e, b2=b2, C=C, pi=pi: e.copy(out=PTb[pi][0:C, :, 0:C], in_=ps[0:C, b2, 0:256].rearrange("p (a b) -> p a b", a=2)[:, :, 0:C]), reads=psk(b2), writes=[("PT", pi)])
                        yield
                        b3 = ps_next()
                        for e_ in range(2):
                            S.pe(lambda e, e_=e_, b3=b3, C=C, pi=pi, ni=ni: e.matmul(ps[0:C, b3, e_ * 128:e_ * 128 + C], lhsT=PTb[pi][0:C, e_, 0:C], rhs=Nb[ni][0:C, e_, 0:C], start=True, stop=True), reads=[("PT", pi), ("N", ni)], writes=psk(b3))
                            yield
                        S.dve(lambda e, b3=b3, C=C, ni=ni: e.tensor_tensor(out=Nb[1 - ni][0:C, :, 0:C], in0=ps[0:C, b3, 0:256].rearrange("p (a b) -> p a b", a=2)[:, :, 0:C], in1=Nb[ni][0:C, :, 0:C], op=ALU.add), reads=psk(b3) + [("N", ni)], writes=[("N", 1 - ni)])
                        yield
                        ni = 1 - ni
                        Pprev, PTprev = Pb[pi], PTb[pi]
                        Pkeys_prev = [("P", pi)]
                        PTkeys_prev = [("PT", pi)]

                else:
                    def _ev2(b, C=C):
                        return ps[0:C, b, 0:256].rearrange("p (a b) -> p a b", a=2)[:, :, 0:C]
                    idb = ident_f[0:C, 0:C].unsqueeze(1).to_broadcast([C, 2, C])
                    S.dve(lambda e: e.tensor_tensor(out=Pb[0][:, :, :], in0=U[:, :, :], in1=bm[:, 0, :].unsqueeze(1).to_broadcast([128, 2, 128]), op=ALU.mult), reads=[("U", 0), ("U", 1), "bm"], writes=[("P", 0)])
                    yield
                    S.dve(lambda e: e.tensor_tensor(out=PTb[0][:, :, :], in0=UT[:, :, :], in1=bm[:, 0, :].unsqueeze(1).to_broadcast([128, 2, 128]), op=ALU.mult), reads=["UT", "bm"], writes=[("PT", 0)])
                    yield
                    S.dve(lambda e, idb=idb: e.tensor_tensor(out=Nb[0][:, :, :], in0=idb, in1=Pb[0][:, :, :], op=ALU.subtract), reads=[("P", 0), "ident_f"], writes=[("N", 0)])
                    yield
                    S.dve(lambda e, idb=idb: e.tensor_tensor(out=NTb[0][:, :, :], in0=idb, in1=PTb[0][:, :, :], op=ALU.subtract), reads=[("PT", 0), "ident_f"], writes=[("NT", 0)])
                    yield
                    for mi in range(3):
                        S.pool(lambda e, mi=mi: e.tensor_tensor(out=UoM[mi][:, :, :], in0=U[:, :, :], in1=bm[:, mi + 1, :].unsqueeze(1).to_broadcast([128, 2, 128]), op=ALU.mult), reads=[("U", 0), ("U", 1), "bm"], writes=[("UoM", mi)])
                        yield
                        S.pool(lambda e, mi=mi: e.tensor_tensor(out=UoTM[mi][:, :, :], in0=UT[:, :, :], in1=bm[:, mi + 1, :].unsqueeze(1).to_broadcast([128, 2, 128]), op=ALU.mult), reads=["UT", "bm"], writes=[("UoTM", mi)])
                        yield
                    ni = 0
                    pprev = 0
                    for lv in range(1, 4):
                        pi = lv % 2
                        b1 = ps_next(); b2 = ps_next(); b3 = ps_next(); b4 = ps_next()
                        for e_ in range(2):
                            S.pe(lambda e, e_=e_, b1=b1, pprev=pprev: e.matmul(ps[:, b1, e_ * 128:(e_ + 1) * 128], lhsT=PTb[pprev][:, e_, :], rhs=Pb[pprev][:, e_, :], start=True, stop=True), reads=[("P", pprev), ("PT", pprev)], writes=psk(b1))
                            yield
                        for e_ in range(2):
                            S.pe(lambda e, e_=e_, b2=b2, pprev=pprev: e.matmul(ps[:, b2, e_ * 128:(e_ + 1) * 128], lhsT=Pb[pprev][:, e_, :], rhs=PTb[pprev][:, e_, :], start=True, stop=True), reads=[("P", pprev), ("PT", pprev)], writes=psk(b2))
                            yield
                        S.act(lambda e, b1=b1, pi=pi: e.copy(out=Pb[pi][:, :, :], in_=_ev2(b1)), reads=psk(b1), writes=[("P", pi)])
                        yield
                        S.act(lambda e, b2=b2, pi=pi: e.copy(out=PTb[pi][:, :, :], in_=_ev2(b2)), reads=psk(b2), writes=[("PT", pi)])
                        yield
                        for e_ in range(2):
                            S.pe(lambda e, e_=e_, b3=b3, pi=pi, ni=ni: e.matmul(ps[:, b3, e_ * 128:(e_ + 1) * 128], lhsT=PTb[pi][:, e_, :], rhs=Nb[ni][:, e_, :], start=True, stop=True), reads=[("PT", pi), ("N", ni)], writes=psk(b3))
                            yield
                        for e_ in range(2):
                            S.pe(lambda e, e_=e_, b4=b4, pi=pi, ni=ni: e.matmul(ps[:, b4, e_ * 128:(e_ + 1) * 128], lhsT=Pb[pi][:, e_, :], rhs=NTb[ni][:, e_, :], start=True, stop=True), reads=[("P", pi), ("NT", ni)], writes=psk(b4))
                            yield
                        S.dve(lambda e, b3=b3, ni=ni: e.tensor_tensor(out=Nb[1 - ni][:, :, :], in0=_ev2(b3), in1=Nb[ni][:, :, :], op=ALU.add), reads=psk(b3) + [("N", ni)], writes=[("N", 1 - ni)])
                        yield
                        S.dve(lambda e, b4=b4, ni=ni: e.tensor_tensor(out=NTb[1 - ni][:, :, :], in0=_ev2(b4), in1=NTb[ni][:, :, :], op=ALU.add), reads=psk(b4) + [("NT", ni)], writes=[("NT", 1 - ni)])
                        yield
                        ni = 1 - ni
                        pprev = pi
                    for mi in range(3):
                        lastm = (mi == 2)
                        b1 = ps_next()
                        for e_ in range(2):
                            S.pe(lambda e, e_=e_, b1=b1, ni=ni, mi=mi: e.matmul(ps[:, b1, e_ * 128:(e_ + 1) * 128], lhsT=UoTM[mi][:, e_, :], rhs=Nb[ni][:, e_, :], start=True, stop=True), reads=[("UoTM", mi), ("N", ni)], writes=psk(b1))
                            yield
                        S.act(lambda e, b1=b1: e.copy(out=Pb[1][:, :, :], in_=_ev2(b1)), reads=psk(b1), writes=[("P", 1)])
                        yield
                        if not lastm:
                            b3 = ps_next()
                            for e_ in range(2):
                                S.pe(lambda e, e_=e_, b3=b3, ni=ni, mi=mi: e.matmul(ps[:, b3, e_ * 128:(e_ + 1) * 128], lhsT=UoM[mi][:, e_, :], rhs=NTb[ni][:, e_, :], start=True, stop=True), reads=[("UoM", mi), ("NT", ni)], writes=psk(b3))
                                yield
                            S.act(lambda e, b3=b3: e.copy(out=PTb[1][:, :, :], in_=_ev2(b3)), reads=psk(b3), writes=[("PT", 1)])
                            yield
                        b2 = ps_next()
                        for e_ in range(2):
                            S.pe(lambda e, e_=e_, b2=b2, ni=ni: e.matmul(ps[:, b2, e_ * 128:(e_ + 1) * 128], lhsT=NTb[ni][:, e_, :], rhs=Pb[1][:, e_, :], start=True, stop=True), reads=[("NT", ni), ("P", 1)], writes=psk(b2))
                            yield
                        S.dve(lambda e, b2=b2, ni=ni, lastm=lastm: e.tensor_tensor(out=(Nfin[pb] if lastm else Nb[1 - ni])[:, :, :], in0=Nb[ni][:, :, :], in1=_ev2(b2), op=ALU.subtract), reads=psk(b2) + [("N", ni)], writes=[(("Nfin", pb) if lastm else ("N", 1 - ni))])
                        yield
                        if not lastm:
                            b4 = ps_next()
                            for e_ in range(2):
                                S.pe(lambda e, e_=e_, b4=b4, ni=ni: e.matmul(ps[:, b4, e_ * 128:(e_ + 1) * 128], lhsT=Nb[ni][:, e_, :], rhs=PTb[1][:, e_, :], start=True, stop=True), reads=[("N", ni), ("PT", 1)], writes=psk(b4))
                                yield
                            S.dve(lambda e, b4=b4, ni=ni: e.tensor_tensor(out=NTb[1 - ni][:, :, :], in0=NTb[ni][:, :, :], in1=_ev2(b4), op=ALU.subtract), reads=psk(b4) + [("NT", ni)], writes=[("NT", 1 - ni)])
                            yield
                        ni = 1 - ni
                if isS:
                    S.pool(lambda e, ni=ni, C=C: e.tensor_copy(out=Nfin[pb][0:C, :, 0:C], in_=Nb[ni][0:C, :, 0:C]), reads=[("N", ni)], writes=[("Nfin", pb)])
                    yield
                Nf = Nfin[pb]
                Nkey = ("Nfin", pb)
                if not isS:
                    S.pool(lambda e, C=C: e.tensor_copy(out=eGl[pb][:, 0:2], in_=E2[:, :, C - 1:C].rearrange("p a b -> p (a b)")), reads=["E2"], writes=[("eGl", pb)])
                    yield
                yield "SPLIT"
                if not isS:
                    if first:
                        S.act(lambda e, C=C: e.copy(out=xv[0:C].rearrange("p a b -> p (a b)"), in_=vtok2[pb][0:C].rearrange("p a b -> p (a b)")), reads=[("vtok", pb)], writes=["xv"])
                        yield
                    else:
                        pk_ = ps_next()
                        S.pe(lambda e, pk_=pk_, c0=c0, C=C: e.matmul(ps[0:C, pk_, 0:256], lhsT=kn[:, c0:c0 + C], rhs=Sbf[:].rearrange("p a b -> p (a b)"), start=True, stop=True), reads=[("n", 1), "Sbf"], writes=psk(pk_))
                        yield
                        for e_ in range(2):
                            hv = hv0 + e_
                            S.dve(lambda e, e_=e_, hv=hv, pk_=pk_, C=C, tl=tl: e.scalar_tensor_tensor(out=xv[0:C, e_, :], in0=ps[0:C, pk_, e_ * 128:(e_ + 1) * 128], scalar=negeG_all[0:C, tl, hv:hv + 1], in1=vtok2[pb][0:C, e_, :], op0=ALU.mult, op1=ALU.add), reads=psk(pk_) + [("vtok", pb)], writes=["xv"])
                            yield
                else:
                    S.dve(lambda e: e.tensor_tensor(out=X[:, :, :], in0=knf[:, :].unsqueeze(1).to_broadcast([128, NSEQ, TS]), in1=segmask[:, :, :], op=ALU.mult), reads=[("nf", 1), "segmask"], writes=["X"])
                    yield
                    pks = [ps_next(), ps_next()]
                    K.P.reserved = set(pks)
                    def _ldA(n):
                        k_ = n % 8
                        S.dma("sp", "gsin%d" % k_, lambda e, k_=k_, s2=n % NSEQ, hv2=hv0 + n // NSEQ: e.dma_start(out=SinR[k_], in_=K.s_gdn[s2, hv2]), writes=[("SinR", k_)])
                    for n_ in range(RDEPTH):
                        _ldA(n_)
                        yield
                    for e_ in range(2):
                        hv = hv0 + e_
                        for s_ in range(NSEQ):
                            n_ = e_ * NSEQ + s_
                            k_ = n_ % 8
                            S.pe(lambda e, e_=e_, s_=s_, k_=k_, pks=pks: e.matmul(ps[0:TS, pks[e_], 0:128], lhsT=X[:, s_, :], rhs=SinR[k_], start=(s_ == 0), stop=(s_ == NSEQ - 1)), reads=["X", ("SinR", k_)], writes=psk(pks[e_]))
                            yield
                            if n_ + RDEPTH < NLD:
                                _ldA(n_ + RDEPTH)
                                yield
                        S.dve(lambda e, e_=e_, hv=hv, tl=tl, pks=pks: e.scalar_tensor_tensor(out=xv[0:TS, e_, :], in0=ps[0:TS, pks[e_], 0:128], scalar=negeG_all[0:TS, tl, hv:hv + 1], in1=vtok2[pb][0:TS, e_, :], op0=ALU.mult, op1=ALU.add), reads=psk(pks[e_]) + [("vtok", pb)], writes=["xv"])
                        yield
                    K.P.reserved = set()
                pv = ps_next()
                for e_ in range(2):
                    S.pe(lambda e, e_=e_, pv=pv, C=C, Nf=Nf: e.matmul(ps[0:C, pv, e_ * 128:(e_ + 1) * 128], lhsT=Nf[0:C, e_, 0:C], rhs=xv[0:C, e_, :], start=True, stop=True), reads=[Nkey, "xv"], writes=psk(pv))
                    yield
                for e_ in range(2):
                    hv = hv0 + e_
                    S.act(lambda e, e_=e_, hv=hv, pv=pv, C=C, tl=tl: e.activation(out=vnew[0:C, e_, :], in_=ps[0:C, pv, e_ * 128:(e_ + 1) * 128], func=AF.Identity, scale=beta_all[0:C, tl, hv:hv + 1]), reads=psk(pv), writes=[("vnew", e_)])
                    yield
                if not isS:
                    po = ps_next()
                    po_aps = [ps[0:C, po, 0:128], ps[0:C, po, 128:256]]
                    po_keys = [psk(po), psk(po)]
                    for e_ in range(2):
                        if not first:
                            S.pe(lambda e, e_=e_, C=C, po_aps=po_aps: e.matmul(po_aps[e_], lhsT=qg2[pb][:, e_, 0:C], rhs=Sbf[:, e_, :], start=True, stop=False), reads=[("qg", e_, pb), "Sbf"], writes=po_keys[e_])
                            yield
                        S.pe(lambda e, e_=e_, C=C, po_aps=po_aps, first=first: e.matmul(po_aps[e_], lhsT=attnT2[pb][0:C, e_, 0:C], rhs=vnew[0:C, e_, :], start=first, stop=True), reads=[("attnT", e_, pb), ("vnew", e_)], writes=po_keys[e_])
                        yield
                    pS_ = ps_next()
                    for e_ in range(2):
                        S.pe(lambda e, e_=e_, pS_=pS_, C=C: e.matmul(ps[:, pS_, e_ * 128:(e_ + 1) * 128], lhsT=kd2[pb][0:C, e_, :], rhs=vnew[0:C, e_, :], start=True, stop=True), reads=[("kd", e_, pb), ("vnew", e_)], writes=psk(pS_))
                        yield
                    for e_ in range(2):
                        hv = hv0 + e_
                        if first:
                            S.dve(lambda e, e_=e_, pS_=pS_: e.tensor_copy(out=S32[:, e_, :], in_=ps[:, pS_, e_ * 128:(e_ + 1) * 128]), reads=psk(pS_), writes=[("S32", e_)])
                            yield
                        else:
                            S.dve(lambda e, e_=e_, pS_=pS_, C=C: e.scalar_tensor_tensor(out=S32[:, e_, :], in0=S32[:, e_, :], scalar=eGl[pb][:, e_:e_ + 1], in1=ps[:, pS_, e_ * 128:(e_ + 1) * 128], op0=ALU.mult, op1=ALU.add), reads=psk(pS_) + [("S32", e_), ("eGl", pb)], writes=[("S32", e_)])
                            yield
                        if last:
                            S.dma("sp", "gpo", lambda e, e_=e_, hv=hv: e.dma_start(out=K.gdn_p[hv], in_=S32[:, e_, :]), reads=[("S32", e_)])
                            yield
                    if not last:
                        S.act(lambda e: e.copy(out=Sbf[:].rearrange("p a b -> p (a b)"), in_=S32[:].rearrange("p a b -> p (a b)")), reads=[("S32", 0), ("S32", 1)], writes=["Sbf"])
                        yield
                else:
                    pos = [ps_next(), ps_next()]
                    K.P.reserved = set(pos)
                    po_aps = [ps[0:TS, pos[0], 0:128], ps[0:TS, pos[1], 0:128]]
                    po_keys = [psk(pos[0]), psk(pos[1])]
                    def _ldB(n):
                        k_ = n % 8
                        S.dma("sp", "gsin%d" % k_, lambda e, k_=k_, s2=n % NSEQ, hv2=hv0 + n // NSEQ: e.dma_start(out=SinR[k_], in_=K.s_gdn[s2, hv2]), writes=[("SinR", k_)])
                    for n_ in range(RDEPTH):
                        _ldB(n_)
                        yield
                    for e_ in range(2):
                        hv = hv0 + e_
                        S.pe(lambda e, e_=e_, po_aps=po_aps: e.matmul(po_aps[e_], lhsT=attnT2[pb][0:TS, e_, 0:TS], rhs=vnew[0:TS, e_, :], start=True, stop=False), reads=[("attnT", e_, pb), ("vnew", e_)], writes=po_keys[e_])
                        yield
                        S.pool(lambda e, e_=e_: e.tensor_tensor(out=qgf[:, :], in0=qnf[:, :], in1=E2[:, e_, 0:TS], op=ALU.mult), reads=[("nf", 0), "E2"], writes=["qgf"])
                        yield
                        S.dve(lambda e: e.tensor_tensor(out=X[:, :, :], in0=qgf[:, :].unsqueeze(1).to_broadcast([128, NSEQ, TS]), in1=segmask[:, :, :], op=ALU.mult), reads=["qgf", "segmask"], writes=["X"])
                        yield
                        for s_ in range(NSEQ):
                            n_ = e_ * NSEQ + s_
                            k_ = n_ % 8
                            j_ = n_ % 3
                            S.pe(lambda e, e_=e_, s_=s_, k_=k_, po_aps=po_aps: e.matmul(po_aps[e_], lhsT=X[:, s_, :], rhs=SinR[k_], start=False, stop=(s_ == NSEQ - 1)), reads=["X", ("SinR", k_)], writes=po_keys[e_])
                            yield
                            S.dve(lambda e, e_=e_, s_=s_: e.tensor_scalar(out=kdm[0:TS, :], in0=kd2[pb][0:TS, e_, :], scalar1=segcol[0:TS, s_:s_ + 1], scalar2=None, op0=ALU.mult), reads=[("kd", e_, pb), "segcol"], writes=["kdm"])
                            yield
                            pS_ = ps_next()
                            S.pe(lambda e, e_=e_, pS_=pS_: e.matmul(ps[:, pS_, 0:128], lhsT=kdm[0:TS, :], rhs=vnew[0:TS, e_, :], start=True, stop=True), reads=["kdm", ("vnew", e_)], writes=psk(pS_))
                            yield
                            S.dve(lambda e, e_=e_, s_=s_, k_=k_, j_=j_, pS_=pS_: e.scalar_tensor_tensor(out=SoutR[j_], in0=SinR[k_], scalar=E2[:, e_, 4 * s_ + 3:4 * s_ + 4], in1=ps[:, pS_, 0:128], op0=ALU.mult, op1=ALU.add), reads=psk(pS_) + [("SinR", k_), "E2"], writes=[("SoutR", j_)])
                            yield
                            if n_ + RDEPTH < NLD:
                                _ldB(n_ + RDEPTH)
                                yield
                            S.dma("sp", "gsout%d" % j_, lambda e, s_=s_, hv=hv, j_=j_: e.dma_start(out=K.gdn_s[s_, hv], in_=SoutR[j_]), reads=[("SoutR", j_)])
                            yield
                    K.P.reserved = set()
                for e_ in range(2):
                    S.act(lambda e, e_=e_, C=C, po_aps=po_aps: e.activation(out=acc[0:C, e_ * 128:(e_ + 1) * 128], in_=po_aps[e_], func=AF.Square, accum_out=ss[0:C, e_:e_ + 1]), reads=po_keys[e_], writes=[("acc", 0), ("ss", e_)])
                    yield
                S.act(lambda e, C=C: e.activation(out=ss[0:C, 2:4], in_=ss[0:C, 0:2], func=AF.Sqrt, bias=EPS, scale=1.0 / 128), reads=[("ss", 0), ("ss", 1)], writes=["ss2"])
                yield
                S.dve(lambda e, C=C: e.reciprocal(out=ss[0:C, 2:4], in_=ss[0:C, 2:4]), reads=["ss2"], writes=["ss2"])
                yield
                for e_ in range(2):
                    S.dve(lambda e, e_=e_, C=C, po_aps=po_aps: e.scalar_tensor_tensor(out=og[0:C, e_ * 128:(e_ + 1) * 128], in0=po_aps[e_], scalar=ss[0:C, 2 + e_:3 + e_], in1=szn2[pb][0:C, e_ * 128:(e_ + 1) * 128], op0=ALU.mult, op1=ALU.mult), reads=po_keys[e_] + ["ss2", ("szn", pb)], writes=[("og", e_)])
                    yield
                bt = ps_next()
                ptb = ps[:, bt, :].bitcast(BF16)
                for e_ in range(2):
                    S.pe(lambda e, e_=e_, ptb=ptb, C=C: e.transpose(ptb[:, e_ * C:(e_ + 1) * C], og[0:C, e_ * 128:(e_ + 1) * 128], ident_b[0:C, 0:C]), reads=[("og", e_), "ident_b"], writes=psk(bt))
                    yield
                S.act(lambda e, ptb=ptb, C=C, c0=c0: e.copy(out=ogT[:, :, c0:c0 + C], in_=ptb[:, 0:2 * C].rearrange("p (a b) -> p a b", a=2)), reads=psk(bt), writes=["ogT"])
                yield
            chunks_ = list(range(0, nt, C))
            gens = [chunk_gen(c0_, i_ % 2) for i_, c0_ in enumerate(chunks_)]
            PIPE = (not isS) and os.environ.get("GDN_PIPE", "1") == "1"

            def _adv_split(g):
                for x_ in g:
                    if x_ == "SPLIT":
                        return
            if not PIPE:
                for g in gens:
                    for _ in g:
                        pass
            else:
                _adv_split(gens[0])
                for i_ in range(len(gens)):
                    nxt = gens[i_ + 1] if i_ + 1 < len(gens) else None
                    a_done = False
                    b_done = nxt is None
                    while not (a_done and b_done):
                        if not a_done:
                            try:
                                next(gens[i_])
                            except StopIteration:
                                a_done = True
                        if not b_done:
                            try:
                                if next(nxt) == "SPLIT":
                                    b_done = True
                            except StopIteration:
                                b_done = True
            for dc8 in range(KC):
                if hk >= 1 and 'F' in CUT:
                    continue
                b = ps_next()
                for e_ in range(2):
                    S.pe(lambda e, b=b, e_=e_, dc8=dc8, nt=nt: e.matmul(ps[:, b, 0:nt], lhsT=wo[:, e_, dc8 * 128:(dc8 + 1) * 128], rhs=ogT[:, e_, 0:nt], start=(e_ == 0), stop=(e_ == 1)), reads=[wok, "ogT"], writes=psk(b))
                K.resid_update(ps[:, b, 0:nt], gbase, dc8, ("S" if isS else bi), t0, nt, psk(b))
        if hk >= 1 and 'G' in CUT:
            S.barrier(); continue
        bo1 = ps_next()
        for ci in range(4):
            S.pe(lambda e, ci=ci, bo1=bo1: e.transpose(ps[0:3, bo1, ci * 128:(ci + 1) * 128], tailP[:, ci, :], ident_f[:]), reads=["tailP", "ident_f"], writes=psk(bo1))
        S.dve(lambda e, bo1=bo1: e.tensor_copy(out=ostg[0:3, :], in_=ps[0:3, bo1, :]), reads=psk(bo1), writes=[("acc", 0)])
        for ci, cid in enumerate(cids):
            S.dma("sp", "gco", lambda e, ci=ci, cid=cid: e.dma_start(out=K.gconv_p[:, cid * 128:(cid + 1) * 128], in_=ostg[0:3, ci * 128:(ci + 1) * 128]), reads=[("acc", 0)])
        bo2 = ps_next()
        for ci in range(4):
            S.pe(lambda e, ci=ci, bo2=bo2: e.transpose(ps[0:48, bo2, ci * 128:(ci + 1) * 128], tailS[:, ci, :, :].rearrange("p s r -> p (s r)"), ident_f[:]), reads=["tailS", "ident_f"], writes=psk(bo2))
        S.dve(lambda e, bo2=bo2: e.tensor_copy(out=ostg[0:48, :], in_=ps[0:48, bo2, :]), reads=psk(bo2), writes=[("acc", 0)])
        for ci, cid in enumerate(cids):
            S.dma("sp", "gco", lambda e, ci=ci, cid=cid: e.dma_start(out=K.gconv_s[:, cid * 128:(cid + 1) * 128], in_=ostg[0:48, ci * 128:(ci + 1) * 128]), reads=[("acc", 0)])
        S.barrier()
    S.barrier()

_CACHE = {}


def make_in_maps(inp):
    f = lambda a: np.ascontiguousarray(np.asarray(a, dtype=np.float32))
    consts = host_consts()
    shared = {
        "w_ada": f(inp["w_ada"]), "b_ada": f(inp["b_ada"]),
        "w_ada_final": f(inp["w_ada_final"]), "b_ada_final": f(inp["b_ada_final"]).reshape(1, -1),
        "w_ret_in": f(inp["w_ret_in"][0]), "w_ret_out": f(inp["w_ret_out"][0]),
        "w_gdn_in": f(inp["w_gdn_in"][0]), "w_gdn_out": f(inp["w_gdn_out"][0]),
        "w_gdn_conv": f(inp["w_gdn_conv"][0]),
        "gdn_vec": f(np.concatenate([inp["gdn_a_log"][0], inp["gdn_dt_bias"][0]])).reshape(1, 32),
        "gdn_norm": f(inp["gdn_norm"]).reshape(1, 128),
        "w_ffn_up": f(inp["w_ffn_up"]), "w_ffn_down": f(inp["w_ffn_down"]),
        "ffn_vec": f(np.concatenate([np.asarray(inp["w_ffn_dw"]).reshape(6, DFF), np.asarray(inp["b_ffn_dw"]).reshape(2, DFF)], axis=0)),
    }
    shared.update(consts)
    maps = []
    for c in range(NCORES):
        sl = slice(NSEQ * c, NSEQ * (c + 1))
        m = dict(shared)
        m["xp"] = f(inp["x_prompt"][c])
        m["xs"] = f(np.asarray(inp["x_sample"][sl]).reshape(TS, D))
        m["s_ret"] = f(inp["state_ret"][0, sl])
        m["s_gdn"] = f(inp["state_gdn"][0, sl])
        m["s_gconv"] = f(np.asarray(inp["state_gdn_conv"][0, sl]).reshape(NSEQ * 3, 4096))
        m["s_fconv"] = f(np.asarray(inp["state_ffn_conv"][:, sl]).reshape(2, NSEQ * 2, DFF))
        m["vec22"] = f(np.concatenate([np.asarray(inp["c_prompt"][c:c + 1]), np.asarray(inp["c_sample"][sl]),
                                        np.asarray(inp["norm_mix"]), np.asarray(inp["norm_ffn"]), np.asarray(inp["norm_final"]).reshape(1, D)], axis=0))
        maps.append(m)
    return maps


def kernel(**inp):
    if "nc" not in _CACHE:
        _CACHE["nc"] = build_program()
    nc = _CACHE["nc"]
    maps = make_in_maps(inp)
    res = run_bass_kernel_spmd(nc, maps, core_ids=list(range(NCORES)))
    R = res.results
    cat = lambda k: np.stack([np.asarray(R[c][k]) for c in range(NCORES)], axis=0)
    y_prompt = cat("y_p")
    y_sample = cat("y_s").reshape(128, 4, D)
    ret_p = cat("ret_p")[None]
    gdn_p = cat("gdn_p")[None]
    gconv_p = cat("gconv_p")[None]
    fconv_p = np.transpose(cat("fconv_p"), (1, 0, 2, 3))
    ret_s = cat("ret_s").reshape(1, 128, RET_H, 256, 512)
    gdn_s = cat("gdn_s").reshape(1, 128, GDN_HV, 128, 128)
    gconv_s = cat("gconv_s").reshape(1, 128, 3, 4096)
    fconv_s = np.transpose(cat("fconv_s").reshape(NCORES, 2, NSEQ, 2, DFF), (1, 0, 2, 3, 4)).reshape(2, 128, 2, DFF)
    return (y_prompt.astype(np.float32), y_sample.astype(np.float32), ret_p.astype(np.float32), gdn_p.astype(np.float32),
            gconv_p.astype(np.float32), fconv_p.astype(np.float32), ret_s.astype(np.float32), gdn_s.astype(np.float32),
            gconv_s.astype(np.float32), fconv_s.astype(np.float32))
```

```python
import math
import bisect
import numpy as np
from contextlib import ExitStack
import concourse.bass as bass
import concourse.mybir as mybir
from concourse.bass_utils import run_bass_kernel_spmd

F32 = mybir.dt.float32
BF16 = mybir.dt.bfloat16
AF = mybir.ActivationFunctionType
ALU = mybir.AluOpType

NCORES = 8
D = 1024
KC = 8
TP = 2048
NSEQ = 16
TS = 64
TT = TP + TS
DFF = 2816
NF = 22
EPS = 1e-6
RET_H = 4
GDN_HV = 16
GDN_HK = 8
PAST = 16384
SAME_ENGINE_SYNC = True
DEBUG = {}


class _Op:
    __slots__ = ("eng", "fn", "deps", "dma_key", "sig", "cnt", "idx")

    def __init__(self, eng, fn, deps, dma_key, idx):
        self.eng = eng
        self.fn = fn
        self.deps = deps
        self.dma_key = dma_key
        self.sig = False
        self.cnt = 0
        self.idx = idx


class Sched:
    ENGS = ("pe", "act", "dve", "pool", "sp")

    def __init__(self, nc):
        self.nc = nc
        self.ops = []
        self.last_w = {}
        self.readers = {}
        self.last_eng = {}
        self.dmas_since_bar = []

    def op(self, eng, fn, reads=(), writes=(), dma_key=None, extra=()):
        deps = set(extra)
        for r in reads:
            w = self.last_w.get(r)
            if w is not None:
                deps.add(w)
        for w_ in writes:
            w = self.last_w.get(w_)
            if w is not None:
                deps.add(w)
            deps |= self.readers.get(w_, set())
        idx = len(self.ops)
        deps.discard(idx)
        self.ops.append(_Op(eng, fn, deps, dma_key, idx))
        for r in reads:
            self.readers.setdefault(r, set()).add(idx)
        for w_ in writes:
            self.last_w[w_] = idx
            self.readers[w_] = set()
        self.last_eng[eng] = idx
        if dma_key is not None:
            self.dmas_since_bar.append(idx)
        return idx

    def pe(self, fn, reads=(), writes=()):
        return self.op("pe", fn, reads, writes)

    def act(self, fn, reads=(), writes=()):
        return self.op("act", fn, reads, writes)

    def dve(self, fn, reads=(), writes=()):
        return self.op("dve", fn, reads, writes)

    def pool(self, fn, reads=(), writes=()):
        return self.op("pool", fn, reads, writes)

    def dma(self, q, key, fn, reads=(), writes=()):
        return self.op(q, fn, reads, writes, dma_key=key)

    def barrier(self):
        deps = set(self.last_eng.values()) | set(self.dmas_since_bar)
        self.dmas_since_bar = []
        for e in self.ENGS:
            self.op(e, None, extra=deps)
        self.last_w = {}
        self.readers = {}

    def emit(self, stack):
        nc = self.nc
        ops = self.ops

        def needs(c, p):
            if p.fn is None:
                return False
            if p.dma_key is not None:
                return True
            if p.eng == c.eng:
                if p.eng == "pe":
                    return False
                return SAME_ENGINE_SYNC
            return True

        for c in ops:
            for d in c.deps:
                p = ops[d]
                if needs(c, p):
                    p.sig = True
        eng_cnt = {e: 0 for e in self.ENGS}
        dma_cnt = {}
        dma_keys = []
        dma_issue_idx = {}
        for o in ops:
            if o.dma_key is not None:
                if o.dma_key not in dma_cnt:
                    dma_cnt[o.dma_key] = 0
                    dma_keys.append(o.dma_key)
                    dma_issue_idx[o.dma_key] = []
                dma_cnt[o.dma_key] += 1
                dma_issue_idx[o.dma_key].append(o.idx)
            elif o.sig:
                eng_cnt[o.eng] += 1
                o.cnt = eng_cnt[o.eng]
        sems = {}
        for e in self.ENGS:
            sems[("e", e)] = stack.enter_context(nc.semaphore("s_" + e))
        for k in dma_keys:
            sems[("d", k)] = stack.enter_context(nc.semaphore("d_" + str(k)))
        per_eng = {e: [o for o in ops if o.eng == e] for e in self.ENGS}
        block = stack.enter_context(nc.Block())

        def run_engine(ename, eobj):
            waited = {}
            for o in per_eng[ename]:
                need = {}
                for d in o.deps:
                    p = ops[d]
                    if not needs(o, p):
                        continue
                    if p.dma_key is not None:
                        key = ("d", p.dma_key)
                        val = 16 * bisect.bisect_left(dma_issue_idx[p.dma_key], o.idx)
                    else:
                        key = ("e", p.eng)
                        val = p.cnt
                    if need.get(key, 0) < val:
                        need[key] = val
                for key, val in need.items():
                    if waited.get(key, 0) >= val:
                        continue
                    eobj.wait_ge(sems[key], val)
                    waited[key] = val
                if o.fn is None:
                    continue
                ins = o.fn(eobj)
                if o.dma_key is not None:
                    ins.then_inc(sems[("d", o.dma_key)], 16)
                elif o.sig:
                    ins.then_inc(sems[("e", ename)], 1)
            for k in dma_keys:
                if any(o.dma_key == k for o in per_eng[ename]):
                    eobj.wait_ge(sems[("d", k)], 16 * dma_cnt[k])

        @block.tensor
        def _(e):
            run_engine("pe", e)

        @block.scalar
        def _(e):
            run_engine("act", e)

        @block.vector
        def _(e):
            run_engine("dve", e)

        @block.gpsimd
        def _(e):
            run_engine("pool", e)

        @block.sync
        def _(e):
            run_engine("sp", e)


def _gammas():
    return (1.0 - 2.0 ** (-5.0 - np.arange(RET_H, dtype=np.float64)))


def host_consts():
    c = {}
    c["c_ident"] = np.eye(128, dtype=np.float32)
    half = 128
    inv_freq = (np.float32(10000.0) ** (-(np.arange(half, dtype=np.float32)) / np.float32(half))).astype(np.float32)
    pos = np.concatenate([np.arange(TP, dtype=np.float32), (PAST + (np.arange(TS) % 4)).astype(np.float32)])
    ang = (pos[None, :] * inv_freq[:, None]).astype(np.float32)
    cos = np.cos(ang.astype(np.float64))
    sin = np.sin(ang.astype(np.float64))
    g = _gammas()
    pin = np.concatenate([np.arange(TP) % 128, np.arange(TS) % 4]).astype(np.float64)
    rope = np.zeros((RET_H + 1, 2, 128, TT), np.float32)
    for h in range(RET_H):
        dec = g[h] ** (pin + 1.0)
        rope[h, 0] = cos * dec[None, :]
        rope[h, 1] = sin * dec[None, :]
    rope[RET_H, 0] = cos * (256.0 ** -0.5)
    rope[RET_H, 1] = sin * (256.0 ** -0.5)
    c["rope"] = rope
    mP = np.zeros((RET_H, 128, 128), np.float32)
    mS = np.zeros((RET_H, 128, 128), np.float32)
    ks = np.zeros((128, 2 * RET_H), np.float32)
    jj = np.arange(128)
    for h in range(RET_H):
        mP[h] = np.where(jj[None, :] >= jj[:, None], g[h] ** (-(jj[:, None] + 1.0)), 0.0)
        j4 = jj[:64] % 4
        same = (jj[:64, None] // 4) == (jj[None, :64] // 4)
        mS[h, :64, :64] = np.where(same & (jj[None, :64] >= jj[:64, None]), g[h] ** (-(j4[:, None] + 1.0)), 0.0)
        ks[:, h] = g[h] ** (127.0 - jj)
        ks[:64, 4 + h] = g[h] ** (3.0 - j4)
    c["retmask"] = np.concatenate([mP, mS], axis=0)
    c["kscale"] = ks
    seg = np.zeros((128, NSEQ, 64), np.float32)
    for s in range(NSEQ):
        seg[:, s, 4 * s:4 * s + 4] = 1.0
    c["segmask"] = seg.reshape(128, NSEQ * 64)
    segcol = np.zeros((128, NSEQ), np.float32)
    for s in range(NSEQ):
        segcol[4 * s:4 * s + 4, s] = 1.0
    c["segcol"] = segcol
    gm = np.zeros((8, 128, 128), np.float32)
    a = np.arange(128)
    gm[0] = (a[:, None] <= a[None, :])
    gm[1] = (a[:, None] > a[None, :])
    gm[2] = (a[None, :] >= a[:, None])
    gm[3] = (a[None, :] > a[:, None])
    b = np.arange(64)
    same = (b[:, None] // 4) == (b[None, :] // 4)
    gm[4, :64, :64] = (b[:, None] <= b[None, :]) & same
    gm[5, :64, :64] = (b[:, None] > b[None, :]) & same
    gm[6, :64, :64] = (b[None, :] >= b[:, None]) & same
    gm[7, :64, :64] = (b[None, :] > b[:, None]) & same
    c["gmask"] = np.ascontiguousarray(gm.transpose(1, 0, 2)).reshape(128, 8 * 128)
    bmk = np.zeros((4, 128, 128), np.float32)
    bmk[0] = (a[:, None] // 16) == (a[None, :] // 16)
    for mi, m in enumerate([32, 64, 128]):
        bmk[mi + 1] = ((a[:, None] // m) == (a[None, :] // m)) & ((a[:, None] // (m // 2)) != (a[None, :] // (m // 2)))
    c["bmask"] = np.ascontiguousarray(bmk.transpose(1, 0, 2)).reshape(128, 4 * 128)
    return c


class Ctx:
    pass


def build_program(dbg=()):
    nc = bass.Bass("TRN2", target_bir_lowering=False)
    K = Ctx()
    K.nc = nc
    ins = {}

    def din(name, shape):
        ins[name] = nc.dram_tensor(name, list(shape), F32, kind="ExternalInput").ap()
        return ins[name]

    def dout(name, shape):
        return nc.dram_tensor(name, list(shape), F32, kind="ExternalOutput").ap()

    xp = din("xp", [TP, D]); xs = din("xs", [TS, D])
    s_ret = din("s_ret", [NSEQ, RET_H, 256, 512])
    s_gdn = din("s_gdn", [NSEQ, GDN_HV, 128, 128])
    s_gconv = din("s_gconv", [NSEQ * 3, 4096])
    s_fconv = din("s_fconv", [2, NSEQ * 2, DFF])
    vec22 = din("vec22", [22, D])
    w_ada = din("w_ada", [2, D, 6 * D]); b_ada = din("b_ada", [2, 6 * D])
    w_adaf = din("w_ada_final", [D, 2 * D]); b_adaf = din("b_ada_final", [1, 2 * D])
    w_ret_in = din("w_ret_in", [D, 6144]); w_ret_out = din("w_ret_out", [2048, D])
    w_gdn_in = din("w_gdn_in", [D, 6176]); w_gdn_out = din("w_gdn_out", [2048, D])
    w_gconv = din("w_gdn_conv", [4, 4096])
    gdn_vec = din("gdn_vec", [1, 32])
    gdn_norm = din("gdn_norm", [1, 128])
    w_up = din("w_ffn_up", [2, D, 2 * DFF]); w_down = din("w_ffn_down", [2, DFF, D])
    ffn_vec = din("ffn_vec", [8, DFF])
    c_ident = din("c_ident", [128, 128])
    c_rope = din("rope", [RET_H + 1, 2, 128, TT])
    c_retmask = din("retmask", [2 * RET_H, 128, 128])
    c_kscale = din("kscale", [128, 2 * RET_H])
    c_segmask = din("segmask", [128, NSEQ * 64])
    c_segcol = din("segcol", [128, NSEQ])
    c_gmask = din("gmask", [128, 8 * 128])
    c_bmask = din("bmask", [128, 4 * 128])

    y_p = dout("y_p", [TP, D]); y_s = dout("y_s", [TS, D])
    ret_p = dout("ret_p", [RET_H, 256, 512]); gdn_p = dout("gdn_p", [GDN_HV, 128, 128])
    gconv_p = dout("gconv_p", [3, 4096]); fconv_p = dout("fconv_p", [2, 2, DFF])
    ret_s = dout("ret_s", [NSEQ, RET_H, 256, 512]); gdn_s = dout("gdn_s", [NSEQ, GDN_HV, 128, 128])
    gconv_s = dout("gconv_s", [NSEQ * 3, 4096]); fconv_s = dout("fconv_s", [2, NSEQ * 2, DFF])
    dbg_out = {n: dout("dbg_" + n, shp) for n, shp in dbg}

    with ExitStack() as st:
        def T(name, shape, dt):
            return st.enter_context(nc.sbuf_tensor(name, list(shape), dt))

        S = Sched(nc)
        xT = T("xT", [128, KC, TT], F32)
        hT = T("hT", [128, KC, TT], BF16)
        wring = T("wring", [128, 4, 4096], BF16)
        modT = T("modT", [128, 112, 17], F32)
        vecT = T("vecT", [128, KC, 22], F32)
        Amod = T("Amod", [128, 5, KC, 17], F32)
        csT = T("csT", [128, KC, 17], F32)
        ident_f = T("ident_f", [128, 128], F32)
        ident_b = T("ident_b", [128, 128], BF16)
        ones_b = T("ones_b", [128, 128], BF16)
        ones_f = T("ones_f", [128, 32], F32)
        ffnvT = T("ffnvT", [128, NF, 8], F32)
        fcarry = T("fcarry", [128, NF, 2], F32)
        ARENA = 15000
        arena = T("arena", [128, ARENA], F32)
        ps = st.enter_context(nc.psum_tensor("ps", [128, 8, 512], F32))

        A = Ctx()
        A.off = 0

        def a_reset():
            A.off = 0

        def a_f32(*free, parts=128):
            n = int(np.prod(free))
            assert A.off + n <= ARENA, ("arena overflow", A.off, n)
            ap = arena[0:parts, A.off:A.off + n]
            A.off += n
            if len(free) == 2:
                ap = ap.rearrange("p (a b) -> p a b", a=free[0])
            elif len(free) == 3:
                ap = ap.rearrange("p (a b c) -> p a b c", a=free[0], b=free[1])
            return ap

        def a_bf16(*free, parts=128):
            n = int(np.prod(free))
            nf = (n + 1) // 2
            assert A.off + nf <= ARENA, ("arena overflow", A.off, nf)
            ap = arena[0:parts, A.off:A.off + nf].bitcast(BF16)[:, 0:n]
            A.off += nf
            if len(free) == 2:
                ap = ap.rearrange("p (a b) -> p a b", a=free[0])
            elif len(free) == 3:
                ap = ap.rearrange("p (a b c) -> p a b c", a=free[0], b=free[1])
            return ap

        P = Ctx()
        P.i = 0

        P.reserved = set()

        def ps_next(n=1):
            while True:
                if P.i + n > 8:
                    P.i = 0
                b = P.i
                P.i = (P.i + n) % 8
                if not any((b + i) in P.reserved for i in range(n)):
                    return b

        def psk(b, n=1):
            return [("ps", b + i) for i in range(n)]

        W = Ctx()
        W.i = 0

        def wslot():
            i = W.i
            W.i = (W.i + 1) % 4
            return i

        def load_w_bf16(src2d, kc, ncols, row0=0, col0=0, slot=None, off=0, key=None):
            i = wslot() if slot is None else slot
            view = wring[:, i, off:off + kc * ncols].rearrange("p (k c) -> p k c", k=kc)
            src = src2d[row0:row0 + kc * 128, col0:col0 + ncols].rearrange("(k p) c -> p k c", p=128)
            k_ = ("w", i) if key is None else key
            S.dma("pool", "w%d" % i, lambda e: e.dma_start(out=view, in_=src), writes=[k_])
            return view, k_

        def load_w_f32(src2d, kc, ncols, row0=0, col0=0):
            i = wslot()
            view = wring[:, i, :].bitcast(F32)[:, 0:kc * ncols].rearrange("p (k c) -> p k c", k=kc)
            src = src2d[row0:row0 + kc * 128, col0:col0 + ncols].rearrange("(k p) c -> p k c", p=128)
            S.dma("sp", "wf%d" % i, lambda e: e.dma_start(out=view, in_=src), writes=[("w", i)])
            return view, ("w", i)

        BLOCKS_P = [(0, 512), (512, 512), (1024, 512), (1536, 512)]
        BLOCK_S = (TP, TS)
        ALLBLOCKS = BLOCKS_P + [BLOCK_S]

        S.dma("sp", "c_id", lambda e: e.dma_start(out=ident_f[:], in_=c_ident), writes=["ident_f"])
        S.dve(lambda e: e.tensor_copy(out=ident_b[:], in_=ident_f[:]), reads=["ident_f"], writes=["ident_b"])
        S.dve(lambda e: e.memset(ones_b[:], 1.0), writes=["ones_b"])
        S.dve(lambda e: e.memset(ones_f[:], 1.0), writes=["ones_f"])
        a_reset()
        stage22 = a_f32(D, parts=22)
        S.dma("sp", "c_s22", lambda e: e.dma_start(out=stage22, in_=vec22), writes=["stage22"])
        b0 = ps_next()
        for kc in range(KC):
            S.pe(lambda e, kc=kc: e.transpose(ps[:, b0, kc * 22:(kc + 1) * 22], stage22[:, kc * 128:(kc + 1) * 128], ident_f[0:22, 0:22]),
                 reads=["stage22", "ident_f"], writes=psk(b0))
        S.dve(lambda e: e.tensor_copy(out=vecT[:].rearrange("p a b -> p (a b)"), in_=ps[:, b0, 0:KC * 22]), reads=psk(b0), writes=["vecT"])
        S.act(lambda e: e.activation(out=csT[:], in_=vecT[:, :, 0:17], func=AF.Silu), reads=["vecT"], writes=["csT"])
        stage8 = a_f32(DFF, parts=8)
        S.dma("sp", "c_s8", lambda e: e.dma_start(out=stage8, in_=ins["ffn_vec"]), writes=["stage8"])
        b1 = ps_next()
        for fc in range(NF):
            S.pe(lambda e, fc=fc: e.transpose(ps[:, b1, fc * 8:(fc + 1) * 8], stage8[:, fc * 128:(fc + 1) * 128], ident_f[0:8, 0:8]),
                 reads=["stage8", "ident_f"], writes=psk(b1))
        S.dve(lambda e: e.tensor_copy(out=ffnvT[:].rearrange("p a b -> p (a b)"), in_=ps[:, b1, 0:NF * 8]), reads=psk(b1), writes=["ffnvT"])

        browb = [a_bf16(512, parts=1) for _ in range(3)]
        mstage = [a_f32(512, parts=17) for _ in range(2)]
        csb = a_bf16(KC, 17)
        S.dve(lambda e: e.tensor_copy(out=csb, in_=csT[:]), reads=["csT"], writes=["csb"])
        bri = [0]

        def ada_layer(wsrc, bsrc, ncols, mod_base):
            for pcs in range(ncols // 512):
                wv, wk = load_w_bf16(wsrc, KC, 512, col0=pcs * 512)
                bi = bri[0] % 3
                m2 = bri[0] % 2
                bri[0] += 1
                S.dma("pool", "brb%d" % bi, lambda e, bi=bi, pcs=pcs: e.dma_start(out=browb[bi], in_=bsrc[0:1, pcs * 512:(pcs + 1) * 512]), writes=[("browb", bi)])
                bank = ps_next()
                for kc in range(KC):
                    S.pe(lambda e, bank=bank, kc=kc, wv=wv: e.matmul(ps[0:17, bank, :], lhsT=csb[:, kc, :], rhs=wv[:, kc, :], start=(kc == 0), stop=False), reads=[wk, "csb"], writes=psk(bank))
                S.pe(lambda e, bank=bank, bi=bi: e.matmul(ps[0:17, bank, :], lhsT=ones_b[0:1, 0:17], rhs=browb[bi][0:1, :], start=False, stop=True), reads=[("browb", bi), "ones_b"], writes=psk(bank))
                S.act(lambda e, bank=bank, m2=m2: e.copy(out=mstage[m2], in_=ps[0:17, bank, :]), reads=psk(bank), writes=[("mstage", m2)])
                bank2 = ps_next()
                for q in range(4):
                    S.pe(lambda e, bank2=bank2, q=q, m2=m2: e.transpose(ps[:, bank2, q * 17:(q + 1) * 17], mstage[m2][:, q * 128:(q + 1) * 128], ident_f[0:17, 0:17]), reads=[("mstage", m2), "ident_f"], writes=psk(bank2))
                m0 = mod_base + 4 * pcs
                S.dve(lambda e, bank2=bank2, m0=m0: e.tensor_copy(out=modT[:, m0:m0 + 4, :].rearrange("p a b -> p (a b)"), in_=ps[:, bank2, 0:68]), reads=psk(bank2), writes=["modT"])

        ada_layer(w_ada[0], b_ada[0:1, :], 6 * D, 0)
        ada_layer(w_ada[1], b_ada[1:2, :], 6 * D, 48)
        ada_layer(w_adaf, b_adaf, 2 * D, 96)
        norm_specs = [(0 * 48 + 8, 17), (0 * 48 + 32, 19), (1 * 48 + 8, 18), (1 * 48 + 32, 20), (96 + 8, 21)]
        for n, (scb, col) in enumerate(norm_specs):
            for kc in range(KC):
                S.dve(lambda e, n=n, kc=kc, scb=scb, col=col: e.tensor_scalar(out=Amod[:, n, kc, :], in0=modT[:, scb + kc, :], scalar1=1.0, scalar2=vecT[:, kc, col:col + 1], op0=ALU.add, op1=ALU.mult),
                      reads=["modT", "vecT"], writes=["Amod"])
        norm_shift = [0, 24, 48, 72, 96]
        S.barrier()

        a_reset()
        xst = [a_f32(D) for _ in range(2)]
        tiles = [(xp, t * 128, 128, t * 128) for t in range(16)] + [(xs, 0, TS, TP)]
        for ti, (src, r0, rows, t0) in enumerate(tiles):
            sb = ti % 2
            S.dma("sp", "xst%d" % sb, lambda e, sb=sb, src=src, r0=r0, rows=rows: e.dma_start(out=xst[sb][0:rows, :], in_=src[r0:r0 + rows, :]), writes=[("xst", sb)])
            for half in range(2):
                b = ps_next()
                for q in range(4):
                    kc = half * 4 + q
                    S.pe(lambda e, b=b, q=q, kc=kc, sb=sb, rows=rows: e.transpose(ps[:, b, q * 128:q * 128 + rows], xst[sb][0:rows, kc * 128:(kc + 1) * 128], ident_f[0:rows, 0:rows]),
                         reads=[("xst", sb), "ident_f"], writes=psk(b))
                src_ap = ps[:, b, :].rearrange("p (q c) -> p q c", q=4)[:, :, 0:rows]
                dst_ap = xT[:, half * 4:half * 4 + 4, t0:t0 + rows]
                if half == 0:
                    S.act(lambda e, dst_ap=dst_ap, src_ap=src_ap: e.copy(out=dst_ap, in_=src_ap), reads=psk(b), writes=[("xT", ti)])
                else:
                    S.dve(lambda e, dst_ap=dst_ap, src_ap=src_ap: e.tensor_copy(out=dst_ap, in_=src_ap), reads=psk(b), writes=[("xT", ti)])
        S.barrier()

        def do_norm(n, out_hT=True, out_f32=None):
            a_reset()
            sq = [a_bf16(KC, 512) for _ in range(2)]
            rstd = [a_f32(512) for _ in range(2)]
            tmp = [a_f32(KC, 512) for _ in range(2)]
            shb = norm_shift[n]
            for bi, (t0, nt) in enumerate(ALLBLOCKS):
                i2 = bi % 2
                S.act(lambda e, i2=i2, t0=t0, nt=nt: e.activation(out=sq[i2][:, :, 0:nt], in_=xT[:, :, t0:t0 + nt], func=AF.Square),
                      reads=[], writes=[("sq", i2)])
                b = ps_next()
                for kc in range(KC):
                    S.pe(lambda e, b=b, kc=kc, i2=i2, nt=nt: e.matmul(ps[:, b, 0:nt], lhsT=ones_b[:], rhs=sq[i2][:, kc, 0:nt], start=(kc == 0), stop=(kc == KC - 1)),
                         reads=[("sq", i2), "ones_b"], writes=psk(b))
                S.act(lambda e, b=b, i2=i2, nt=nt: e.activation(out=rstd[i2][:, 0:nt], in_=ps[:, b, 0:nt], func=AF.Sqrt, bias=EPS, scale=1.0 / D),
                      reads=psk(b), writes=[("rstd", i2)])
                S.dve(lambda e, i2=i2, nt=nt: e.reciprocal(out=rstd[i2][:, 0:nt], in_=rstd[i2][:, 0:nt]), reads=[("rstd", i2)], writes=[("rstd", i2)])
                S.dve(lambda e, i2=i2, t0=t0, nt=nt: e.tensor_tensor(out=tmp[i2][:, :, 0:nt], in0=xT[:, :, t0:t0 + nt], in1=rstd[i2][:, 0:nt].unsqueeze(1).to_broadcast([128, KC, nt]), op=ALU.mult),
                      reads=[("rstd", i2)], writes=[("tmp", i2)])
                for kc in range(KC):
                    dst = hT[:, kc, t0:t0 + nt] if out_f32 is None else out_f32(bi, kc)
                    if t0 < TP:
                        S.act(lambda e, dst=dst, i2=i2, kc=kc, nt=nt: e.activation(out=dst, in_=tmp[i2][:, kc, 0:nt], func=AF.Identity, scale=Amod[:, n, kc, 0:1], bias=modT[:, shb + kc, 0:1]),
                              reads=[("tmp", i2)], writes=[("h", bi, kc)])
                    else:
                        S.dve(lambda e, i2=i2, kc=kc: e.tensor_tensor(out=tmp[i2][:, kc, 0:TS].rearrange("p (s j) -> p s j", j=4), in0=tmp[i2][:, kc, 0:TS].rearrange("p (s j) -> p s j", j=4),
                                                                     in1=Amod[:, n, kc, 1:17].unsqueeze(2).to_broadcast([128, NSEQ, 4]), op=ALU.mult),
                              reads=[("tmp", i2)], writes=[("tmp", i2)])
                        S.dve(lambda e, dst=dst, i2=i2, kc=kc: e.tensor_tensor(out=dst.rearrange("p (s j) -> p s j", j=4), in0=tmp[i2][:, kc, 0:TS].rearrange("p (s j) -> p s j", j=4),
                                                                              in1=modT[:, shb + kc, 1:17].unsqueeze(2).to_broadcast([128, NSEQ, 4]), op=ALU.add),
                              reads=[("tmp", i2)], writes=[("h", bi, kc)])
            S.barrier()

        def gate_ap(gbase, kc, bi, nt):
            if bi != "S":
                return modT[:, gbase + kc, 0:1].to_broadcast([128, nt])
            return modT[:, gbase + kc, 1:17].unsqueeze(2).to_broadcast([128, NSEQ, 4])

        def resid_update(psrc, gbase, kc, bi, t0, nt, reads):
            if t0 < TP:
                S.dve(lambda e: e.scalar_tensor_tensor(out=xT[:, kc, t0:t0 + nt], in0=psrc, scalar=modT[:, gbase + kc, 0:1], in1=xT[:, kc, t0:t0 + nt], op0=ALU.mult, op1=ALU.add),
                      reads=list(reads) + [("x", kc, t0)], writes=[("x", kc, t0)])
            else:
                tmpg = K.tmpg
                S.dve(lambda e: e.tensor_tensor(out=tmpg.rearrange("p (s j) -> p s j", j=4), in0=psrc.rearrange("p (s j) -> p s j", j=4), in1=gate_ap(gbase, kc, "S", nt), op=ALU.mult),
                      reads=list(reads), writes=["tmpg"])
                S.dve(lambda e: e.tensor_tensor(out=xT[:, kc, t0:t0 + nt], in0=xT[:, kc, t0:t0 + nt], in1=tmpg, op=ALU.add),
                      reads=["tmpg", ("x", kc, t0)], writes=[("x", kc, t0)])

        def do_ffn(l):
            a_reset()
            gbase = l * 48 + 40
            K.tmpg = a_f32(TS)
            actb = a_bf16(NF, 704)
            gx = [a_f32(2 + 512) for _ in range(2)]
            gxs = a_f32(NSEQ, 6)
            cv = [a_f32(512) for _ in range(2)]
            tailP = a_f32(NF, 2)
            tailS = a_f32(NF, NSEQ, 2)
            sstT = a_f32(NF, 32)
            sstg = [a_f32(512, parts=32) for _ in range(2)]
            ostg = [a_f32(512, parts=32) for _ in range(2)]
            ostgP = [a_f32(512, parts=2) for _ in range(2)]
            for gi, g4 in enumerate(range(0, NF, 4)):
                b = ps_next()
                n = min(4, NF - g4)
                s2 = gi % 2
                S.dma("sp", "fst%d" % s2, lambda e, s2=s2, g4=g4, n=n: e.dma_start(out=sstg[s2][:, 0:n * 128], in_=s_fconv[l][:, g4 * 128:(g4 + n) * 128]), writes=[("sstg", s2)])
                for q in range(n):
                    S.pe(lambda e, b=b, q=q, s2=s2: e.transpose(ps[:, b, q * 32:(q + 1) * 32], sstg[s2][:, q * 128:(q + 1) * 128], ident_f[0:32, 0:32]),
                         reads=[("sstg", s2), "ident_f"], writes=psk(b))
                S.dve(lambda e, b=b, n=n, g4=g4: e.tensor_copy(out=sstT[:, g4:g4 + n, :].rearrange("p a b -> p (a b)"), in_=ps[:, b, 0:n * 32]), reads=psk(b), writes=["sstT"])
            S.dve(lambda e: e.memset(fcarry[:], 0.0), writes=[("fcarry", fc_) for fc_ in range(NF)])
            wcol = l * 3
            passes = [[(0, 0, 352, 0), (1, 352, 352, 352)], [(2, 704, 352, 0), (3, 1056, 352, 352)], [(4, 1408, 320, 0), (5, 1728, 320, 320), (6, TP, TS, 640)]]
            for pi, blks in enumerate(passes):
                for f0 in range(0, NF, 4):
                    nf = min(4, NF - f0)
                    wg, wgk = load_w_bf16(w_up[l], KC, nf * 128, col0=f0 * 128)
                    wv, wvk = load_w_bf16(w_up[l], KC, nf * 128, col0=DFF + f0 * 128)
                    for q in range(nf):
                        fc = f0 + q
                        for (bi, t0, nt, a0) in blks:
                            bg = ps_next()
                            for kc in range(KC):
                                S.pe(lambda e, bg=bg, kc=kc, q=q, t0=t0, nt=nt, wg=wg: e.matmul(ps[:, bg, 0:nt], lhsT=wg[:, kc, q * 128:(q + 1) * 128], rhs=hT[:, kc, t0:t0 + nt], start=(kc == 0), stop=(kc == KC - 1)),
                                     reads=[wgk], writes=psk(bg))
                            bv = ps_next()
                            for kc in range(KC):
                                S.pe(lambda e, bv=bv, kc=kc, q=q, t0=t0, nt=nt, wv=wv: e.matmul(ps[:, bv, 0:nt], lhsT=wv[:, kc, q * 128:(q + 1) * 128], rhs=hT[:, kc, t0:t0 + nt], start=(kc == 0), stop=(kc == KC - 1)),
                                     reads=[wvk], writes=psk(bv))
                            w0 = ffnvT[:, fc, wcol + 0:wcol + 1]
                            w1 = ffnvT[:, fc, wcol + 1:wcol + 2]
                            w2 = ffnvT[:, fc, wcol + 2:wcol + 3]
                            bb = ffnvT[:, fc, 6 + l:7 + l]
                            if bi < 6:
                                i2 = bi % 2
                                G = gx[i2]
                                c_ = cv[i2]
                                S.dve(lambda e, G=G, fc=fc: e.tensor_copy(out=G[:, 0:2], in_=fcarry[:, fc, :]), reads=[("fcarry", fc)], writes=[("gx", i2)])
                                S.act(lambda e, G=G, bg=bg, nt=nt: e.copy(out=G[:, 2:2 + nt], in_=ps[:, bg, 0:nt]), reads=psk(bg), writes=[("gx", i2)])
                                S.dve(lambda e, G=G, fc=fc, nt=nt: e.tensor_copy(out=fcarry[:, fc, :], in_=G[:, nt:nt + 2]), reads=[("gx", i2)], writes=[("fcarry", fc)])
                                if bi == 5:
                                    S.dve(lambda e, G=G, fc=fc, nt=nt: e.tensor_copy(out=tailP[:, fc, :], in_=G[:, nt:nt + 2]), reads=[("gx", i2)], writes=["tailP"])
                                S.act(lambda e, G=G, c_=c_, nt=nt, w2=w2: e.activation(out=c_[:, 0:nt], in_=G[:, 2:2 + nt], func=AF.Identity, scale=w2), reads=[("gx", i2), "ffnvT"], writes=[("cv", i2)])
                                S.dve(lambda e, G=G, c_=c_, nt=nt, w1=w1: e.scalar_tensor_tensor(out=c_[:, 0:nt], in0=G[:, 1:1 + nt], scalar=w1, in1=c_[:, 0:nt], op0=ALU.mult, op1=ALU.add), reads=[("gx", i2), ("cv", i2)], writes=[("cv", i2)])
                                S.dve(lambda e, G=G, c_=c_, nt=nt, w0=w0: e.scalar_tensor_tensor(out=c_[:, 0:nt], in0=G[:, 0:nt], scalar=w0, in1=c_[:, 0:nt], op0=ALU.mult, op1=ALU.add), reads=[("gx", i2), ("cv", i2)], writes=[("cv", i2)])
                                S.act(lambda e, c_=c_, nt=nt, bb=bb: e.activation(out=c_[:, 0:nt], in_=c_[:, 0:nt], func=AF.Silu, bias=bb), reads=[("cv", i2)], writes=[("cv", i2)])
                                S.dve(lambda e, c_=c_, nt=nt, bv=bv, fc=fc, a0=a0: e.tensor_tensor(out=actb[:, fc, a0:a0 + nt], in0=c_[:, 0:nt], in1=ps[:, bv, 0:nt], op=ALU.mult), reads=[("cv", i2)] + psk(bv), writes=[("act", fc, bi)])
                            else:
                                c_ = cv[0][:, 0:TS].rearrange("p (s j) -> p s j", j=4)
                                S.dve(lambda e, fc=fc: e.tensor_copy(out=gxs[:, :, 0:2], in_=sstT[:, fc, :].rearrange("p (s r) -> p s r", r=2)), reads=["sstT"], writes=["gxs"])
                                S.act(lambda e, bg=bg: e.copy(out=gxs[:, :, 2:6], in_=ps[:, bg, 0:TS].rearrange("p (s j) -> p s j", j=4)), reads=psk(bg), writes=["gxs"])
                                S.dve(lambda e, fc=fc: e.tensor_copy(out=tailS[:, fc, :, :], in_=gxs[:, :, 4:6]), reads=["gxs"], writes=["tailS"])
                                S.act(lambda e, c_=c_, w2=w2: e.activation(out=c_, in_=gxs[:, :, 2:6], func=AF.Identity, scale=w2), reads=["gxs", "ffnvT"], writes=[("cv", 0)])
                                S.dve(lambda e, c_=c_, w1=w1: e.scalar_tensor_tensor(out=c_, in0=gxs[:, :, 1:5], scalar=w1, in1=c_, op0=ALU.mult, op1=ALU.add), reads=["gxs", ("cv", 0)], writes=[("cv", 0)])
                                S.dve(lambda e, c_=c_, w0=w0: e.scalar_tensor_tensor(out=c_, in0=gxs[:, :, 0:4], scalar=w0, in1=c_, op0=ALU.mult, op1=ALU.add), reads=["gxs", ("cv", 0)], writes=[("cv", 0)])
                                S.act(lambda e, bb=bb: e.activation(out=cv[0][:, 0:TS], in_=cv[0][:, 0:TS], func=AF.Silu, bias=bb), reads=[("cv", 0)], writes=[("cv", 0)])
                                S.dve(lambda e, bv=bv, fc=fc, a0=a0: e.tensor_tensor(out=actb[:, fc, a0:a0 + TS], in0=cv[0][:, 0:TS], in1=ps[:, bv, 0:TS], op=ALU.mult), reads=[("cv", 0)] + psk(bv), writes=[("act", fc, bi)])
                for dh in range(2):
                    banks = {}
                    for dc in range(4):
                        for (bi, t0, nt, a0) in blks:
                            if len(blks) * 4 > 8 and bi == 6:
                                continue
                            banks[(dc, bi)] = ps_next()
                    for f0 in range(0, NF, 4):
                        nf = min(4, NF - f0)
                        wd, wdk = load_w_bf16(w_down[l], nf, 512, row0=f0 * 128, col0=dh * 512)
                        for (dc, bi), b in banks.items():
                            t0, nt, a0 = [(x[1], x[2], x[3]) for x in blks if x[0] == bi][0]
                            for q in range(nf):
                                fc = f0 + q
                                S.pe(lambda e, b=b, q=q, dc=dc, fc=fc, a0=a0, nt=nt, wd=wd: e.matmul(ps[:, b, 0:nt], lhsT=wd[:, q, dc * 128:(dc + 1) * 128], rhs=actb[:, fc, a0:a0 + nt], start=(fc == 0), stop=(fc == NF - 1)),
                                     reads=[wdk, ("act", fc, bi)], writes=psk(b))
                    for (dc, bi), b in banks.items():
                        t0, nt, a0 = [(x[1], x[2], x[3]) for x in blks if x[0] == bi][0]
                        resid_update(ps[:, b, 0:nt], gbase, dh * 4 + dc, bi, t0, nt, psk(b))
                if len(blks) * 4 > 8:
                    (bi, t0, nt, a0) = blks[2]
                    for dh in range(2):
                        banks = {dc: ps_next() for dc in range(4)}
                        for f0 in range(0, NF, 4):
                            nf = min(4, NF - f0)
                            wd, wdk = load_w_bf16(w_down[l], nf, 512, row0=f0 * 128, col0=dh * 512)
                            for dc, b in banks.items():
                                for q in range(nf):
                                    fc = f0 + q
                                    S.pe(lambda e, b=b, q=q, dc=dc, fc=fc, wd=wd, nt=nt, a0=a0: e.matmul(ps[:, b, 0:nt], lhsT=wd[:, q, dc * 128:(dc + 1) * 128], rhs=actb[:, fc, a0:a0 + nt], start=(fc == 0), stop=(fc == NF - 1)),
                                         reads=[wdk, ("act", fc, bi)], writes=psk(b))
                        for dc, b in banks.items():
                            resid_update(ps[:, b, 0:nt], gbase, dh * 4 + dc, bi, t0, nt, psk(b))
            for gi, g4 in enumerate(range(0, NF, 4)):
                n = min(4, NF - g4)
                o2 = gi % 2
                b = ps_next()
                b2 = ps_next()
                for q in range(n):
                    fc = g4 + q
                    S.pe(lambda e, b=b, q=q, fc=fc: e.transpose(ps[0:2, b, q * 128:(q + 1) * 128], tailP[:, fc, :], ident_f[:]), reads=["tailP", "ident_f"], writes=psk(b))
                    S.pe(lambda e, b2=b2, q=q, fc=fc: e.transpose(ps[0:32, b2, q * 128:(q + 1) * 128], tailS[:, fc, :, :].rearrange("p s r -> p (s r)"), ident_f[:]), reads=["tailS", "ident_f"], writes=psk(b2))
                S.dve(lambda e, b=b, n=n, o2=o2: e.tensor_copy(out=ostgP[o2][:, 0:n * 128], in_=ps[0:2, b, 0:n * 128]), reads=psk(b), writes=[("ostgP", o2)])
                S.dve(lambda e, b2=b2, n=n, o2=o2: e.tensor_copy(out=ostg[o2][:, 0:n * 128], in_=ps[0:32, b2, 0:n * 128]), reads=psk(b2), writes=[("ostg", o2)])
                S.dma("sp", "foutP%d" % o2, lambda e, o2=o2, n=n, g4=g4: e.dma_start(out=fconv_p[l][:, g4 * 128:(g4 + n) * 128], in_=ostgP[o2][:, 0:n * 128]), reads=[("ostgP", o2)])
                S.dma("sp", "foutS%d" % o2, lambda e, o2=o2, n=n, g4=g4: e.dma_start(out=fconv_s[l][:, g4 * 128:(g4 + n) * 128], in_=ostg[o2][:, 0:n * 128]), reads=[("ostg", o2)])
            S.barrier()

        def do_final():
            a_reset()
            yT = a_f32(KC, 512)
            ytok = [a_f32(D) for _ in range(2)]
            n = 4
            sq = a_bf16(KC, 512)
            rstd = a_f32(512)
            tmp = a_f32(KC, 512)
            shb = norm_shift[n]
            oi = 0
            for bi, (t0, nt) in enumerate(ALLBLOCKS):
                S.act(lambda e, t0=t0, nt=nt: e.activation(out=sq[:, :, 0:nt], in_=xT[:, :, t0:t0 + nt], func=AF.Square), reads=[], writes=["sq"])
                b = ps_next()
                for kc in range(KC):
                    S.pe(lambda e, b=b, kc=kc, nt=nt: e.matmul(ps[:, b, 0:nt], lhsT=ones_b[:], rhs=sq[:, kc, 0:nt], start=(kc == 0), stop=(kc == KC - 1)), reads=["sq", "ones_b"], writes=psk(b))
                S.act(lambda e, b=b, nt=nt: e.activation(out=rstd[:, 0:nt], in_=ps[:, b, 0:nt], func=AF.Sqrt, bias=EPS, scale=1.0 / D), reads=psk(b), writes=["rstd"])
                S.dve(lambda e, nt=nt: e.reciprocal(out=rstd[:, 0:nt], in_=rstd[:, 0:nt]), reads=["rstd"], writes=["rstd"])
                S.dve(lambda e, t0=t0, nt=nt: e.tensor_tensor(out=tmp[:, :, 0:nt], in0=xT[:, :, t0:t0 + nt], in1=rstd[:, 0:nt].unsqueeze(1).to_broadcast([128, KC, nt]), op=ALU.mult), reads=["rstd"], writes=["tmp"])
                for kc in range(KC):
                    if t0 < TP:
                        S.act(lambda e, kc=kc, nt=nt: e.activation(out=yT[:, kc, 0:nt], in_=tmp[:, kc, 0:nt], func=AF.Identity, scale=Amod[:, n, kc, 0:1], bias=modT[:, shb + kc, 0:1]), reads=["tmp"], writes=["yT"])
                    else:
                        S.dve(lambda e, kc=kc: e.tensor_tensor(out=tmp[:, kc, 0:TS].rearrange("p (s j) -> p s j", j=4), in0=tmp[:, kc, 0:TS].rearrange("p (s j) -> p s j", j=4), in1=Amod[:, n, kc, 1:17].unsqueeze(2).to_broadcast([128, NSEQ, 4]), op=ALU.mult), reads=["tmp"], writes=["tmp"])
                        S.dve(lambda e, kc=kc: e.tensor_tensor(out=yT[:, kc, 0:TS].rearrange("p (s j) -> p s j", j=4), in0=tmp[:, kc, 0:TS].rearrange("p (s j) -> p s j", j=4), in1=modT[:, shb + kc, 1:17].unsqueeze(2).to_broadcast([128, NSEQ, 4]), op=ALU.add), reads=["tmp"], writes=["yT"])
                for sub in range(0, nt, 128):
                    rows = min(128, nt - sub)
                    o2 = oi % 2
                    oi += 1
                    for half in range(2):
                        b = ps_next()
                        for q in range(4):
                            kc = half * 4 + q
                            S.pe(lambda e, b=b, q=q, kc=kc, sub=sub, rows=rows: e.transpose(ps[0:rows, b, q * 128:(q + 1) * 128], yT[:, kc, sub:sub + rows], ident_f[:]), reads=["yT", "ident_f"], writes=psk(b))
                        if half == 0:
                            S.act(lambda e, b=b, o2=o2, rows=rows, half=half: e.copy(out=ytok[o2][0:rows, half * 512:(half + 1) * 512], in_=ps[0:rows, b, :]), reads=psk(b), writes=[("ytok", o2, half)])
                        else:
                            S.dve(lambda e, b=b, o2=o2, rows=rows, half=half: e.tensor_copy(out=ytok[o2][0:rows, half * 512:(half + 1) * 512], in_=ps[0:rows, b, :]), reads=psk(b), writes=[("ytok", o2, half)])
                    if t0 < TP:
                        dst = y_p[t0 + sub:t0 + sub + rows, :]
                    else:
                        dst = y_s[sub:sub + rows, :]
                    S.dma("sp", "yo%d" % o2, lambda e, dst=dst, o2=o2, rows=rows: e.dma_start(out=dst, in_=ytok[o2][0:rows, :]), reads=[("ytok", o2, 0), ("ytok", o2, 1)])
            S.barrier()

        K.__dict__.update(locals())
        G_ = globals()
        do_norm(0)
        import os
        if "do_retention" in G_ and not os.environ.get("SKIP_RET"):
            G_["do_retention"](K)
        do_norm(1)
        do_ffn(0)
        do_norm(2)
        if "do_gdn" in G_:
            G_["do_gdn"](K)
        do_norm(3)
        do_ffn(1)
        do_final()
        if "modT" in dbg_out:
            S.dma("sp", "dbg", lambda e: e.dma_start(out=dbg_out["modT"], in_=modT[:].rearrange("p a b -> p (a b)")))
        if "Amod" in dbg_out:
            S.dma("sp", "dbg", lambda e: e.dma_start(out=dbg_out["Amod"], in_=Amod[:].rearrange("p a b c -> p (a b c)")))
        S.emit(st)
    return nc


def do_retention(K):
    S = K.S; ps = K.ps; nc = K.nc
    a_f32 = K.a_f32; a_bf16 = K.a_bf16; ps_next = K.ps_next; psk = K.psk
    hT = K.hT; xT = K.xT; modT = K.modT
    ident_b = K.ident_b
    g = _gammas()
    K.a_reset()
    K.tmpg = a_f32(TS)
    tabs = a_f32(4, 512)
    qb = a_bf16(2, 512)
    kb = a_bf16(2, 512)
    t1 = a_f32(512)
    t2 = a_f32(512)
    vb = a_bf16(512)
    sg = a_f32(512)
    kh = a_bf16(256)
    im = a_bf16(128)
    og = a_bf16(512)
    ogT = a_bf16(4, 512)
    ss = a_f32(2)
    maskP = a_f32(128)
    maskS = a_f32(128)
    ksc = a_f32(2 * RET_H)
    segcol = a_f32(NSEQ)
    S32 = a_f32(2, 512)
    Sbf = a_bf16(2, 512)
    qf = a_f32(2, TS)
    qX = a_f32(2, NSEQ, TS)
    khm = [a_bf16(256) for _ in range(2)]
    Sin = [a_f32(2, 512) for _ in range(2)]
    Sout = a_f32(2, 512)
    segmask = a_f32(NSEQ, TS)
    S.dma("sp", "rc", lambda e: e.dma_start(out=ksc, in_=K.c_kscale), writes=["ksc"])
    S.dma("sp", "rc", lambda e: e.dma_start(out=segcol, in_=K.c_segcol), writes=["segcol"])
    S.dma("sp", "rc", lambda e: e.dma_start(out=segmask.rearrange("p a b -> p (a b)"), in_=K.c_segmask), writes=["segmask"])
    gbase = 16
    w_in = K.w_ret_in
    w_out = K.w_ret_out
    sctr = [0]
    for h in range(RET_H):
        wq, wqk = K.load_w_bf16(w_in, KC, 256, col0=h * 256, slot=0, off=0, key=("w", 0, "q"))
        wk, wkk = K.load_w_bf16(w_in, KC, 256, col0=1024 + h * 256, slot=0, off=2048, key=("w", 0, "k"))
        wv, wvk = K.load_w_bf16(w_in, KC, 512, col0=2048 + h * 512, slot=1)
        wg, wgk = K.load_w_bf16(w_in, KC, 512, col0=4096 + h * 512, slot=2)
        wo, wok = K.load_w_bf16(w_out, 4, 1024, row0=h * 512, slot=3)
        S.dma("sp", "rm", lambda e, h=h: e.dma_start(out=maskP, in_=K.c_retmask[h]), writes=["maskP"])
        S.dma("sp", "rm", lambda e, h=h: e.dma_start(out=maskS, in_=K.c_retmask[RET_H + h]), writes=["maskS"])
        for bi, (t0, nt) in enumerate(K.ALLBLOCKS):
            isS = t0 >= TP
            C = 64 if isS else 128
            for ti, (hh, cs_) in enumerate([(h, 0), (h, 1), (RET_H, 0), (RET_H, 1)]):
                S.dma("sp", "rt", lambda e, ti=ti, hh=hh, cs_=cs_, t0=t0, nt=nt: e.dma_start(out=tabs[:, ti, 0:nt], in_=K.c_rope[hh, cs_, :, t0:t0 + nt]), writes=[("tabs", ti)])
            for which, (wt, wkey, dst, tb) in enumerate([(wq, wqk, qb, 0), (wk, wkk, kb, 2)]):
                pb = [ps_next(), ps_next()]
                for dc in range(2):
                    for kc in range(KC):
                        S.pe(lambda e, b=pb[dc], kc=kc, dc=dc, wt=wt, t0=t0, nt=nt: e.matmul(ps[:, b, 0:nt], lhsT=wt[:, kc, dc * 128:(dc + 1) * 128], rhs=hT[:, kc, t0:t0 + nt], start=(kc == 0), stop=(kc == KC - 1)),
                             reads=[wkey], writes=psk(pb[dc]))
                p1 = ps[:, pb[0], 0:nt]
                p2 = ps[:, pb[1], 0:nt]
                cc = tabs[:, tb, 0:nt]
                sn = tabs[:, tb + 1, 0:nt]
                to_f = isS and which == 0
                d1 = qf[:, 0, :] if to_f else dst[:, 0, 0:nt]
                d2 = qf[:, 1, :] if to_f else dst[:, 1, 0:nt]
                S.dve(lambda e, p1=p1, cc=cc, nt=nt: e.tensor_tensor(out=t1[:, 0:nt], in0=p1, in1=cc, op=ALU.mult), reads=psk(pb[0]) + [("tabs", tb)], writes=["t1"])
                S.dve(lambda e, p2=p2, sn=sn, nt=nt: e.tensor_tensor(out=t2[:, 0:nt], in0=p2, in1=sn, op=ALU.mult), reads=psk(pb[1]) + [("tabs", tb + 1)], writes=["t2"])
                S.pool(lambda e, d1=d1, nt=nt: e.tensor_tensor(out=d1, in0=t1[:, 0:nt], in1=t2[:, 0:nt], op=ALU.subtract), reads=["t1", "t2"], writes=[("rd", which, 0)])
                S.dve(lambda e, p1=p1, sn=sn, nt=nt: e.tensor_tensor(out=t1[:, 0:nt], in0=p1, in1=sn, op=ALU.mult), reads=psk(pb[0]) + [("tabs", tb + 1)], writes=["t1"])
                S.dve(lambda e, p2=p2, cc=cc, nt=nt: e.tensor_tensor(out=t2[:, 0:nt], in0=p2, in1=cc, op=ALU.mult), reads=psk(pb[1]) + [("tabs", tb)], writes=["t2"])
                S.pool(lambda e, d2=d2, nt=nt: e.tensor_tensor(out=d2, in0=t1[:, 0:nt], in1=t2[:, 0:nt], op=ALU.add), reads=["t1", "t2"], writes=[("rd", which, 1)])
                if to_f:
                    S.act(lambda e: e.copy(out=qb[:, :, 0:TS], in_=qf[:, :, :]), reads=[("rd", 0, 0), ("rd", 0, 1)], writes=["qbS"])
            qkeys = [("rd", 0, 0), ("rd", 0, 1)] + (["qbS"] if isS else [])
            kkeys = [("rd", 1, 0), ("rd", 1, 1)]
            for c0 in range(0, nt, C):
                first = (not isS) and t0 == 0 and c0 == 0
                last = (not isS) and (t0 + c0 + C == TP)
                bv = ps_next()
                for kc in range(KC):
                    S.pe(lambda e, bv=bv, kc=kc, t0=t0, c0=c0, C=C: e.matmul(ps[0:C, bv, :], lhsT=hT[:, kc, t0 + c0:t0 + c0 + C], rhs=wv[:, kc, :], start=(kc == 0), stop=(kc == KC - 1)), reads=[wvk], writes=psk(bv))
                S.act(lambda e, bv=bv, C=C: e.copy(out=vb[0:C, :], in_=ps[0:C, bv, :]), reads=psk(bv), writes=["vb"])
                bg = ps_next()
                for kc in range(KC):
                    S.pe(lambda e, bg=bg, kc=kc, t0=t0, c0=c0, C=C: e.matmul(ps[0:C, bg, :], lhsT=hT[:, kc, t0 + c0:t0 + c0 + C], rhs=wg[:, kc, :], start=(kc == 0), stop=(kc == KC - 1)), reads=[wgk], writes=psk(bg))
                S.act(lambda e, bg=bg, C=C: e.activation(out=sg[0:C, :], in_=ps[0:C, bg, :], func=AF.Silu), reads=psk(bg), writes=["sg"])
                bk = ps_next()
                psb = ps[:, bk, :].bitcast(BF16)
                for dc in range(2):
                    S.pe(lambda e, psb=psb, dc=dc, c0=c0, C=C: e.transpose(psb[0:C, dc * 128:(dc + 1) * 128], kb[:, dc, c0:c0 + C], ident_b[:]), reads=kkeys + ["ident_b"], writes=psk(bk))
                kcol = (RET_H + h) if isS else h
                S.act(lambda e, psb=psb, C=C, kcol=kcol: e.activation(out=kh[0:C, :], in_=psb[0:C, 0:256], func=AF.Identity, scale=ksc[0:C, kcol:kcol + 1]), reads=psk(bk) + ["ksc"], writes=["kh"])
                bi_ = ps_next()
                for dc in range(2):
                    S.pe(lambda e, bi_=bi_, dc=dc, c0=c0, C=C: e.matmul(ps[0:C, bi_, 0:C], lhsT=kb[:, dc, c0:c0 + C], rhs=qb[:, dc, c0:c0 + C], start=(dc == 0), stop=(dc == 1)), reads=kkeys + qkeys, writes=psk(bi_))
                mk = maskS if isS else maskP
                S.dve(lambda e, bi_=bi_, C=C, mk=mk: e.tensor_tensor(out=im[0:C, 0:C], in0=ps[0:C, bi_, 0:C], in1=mk[0:C, 0:C], op=ALU.mult), reads=psk(bi_) + ["maskP", "maskS"], writes=["im"])
                bo = ps_next()
                has_inter = isS or not first
                S.pe(lambda e, bo=bo, C=C, has_inter=has_inter: e.matmul(ps[0:C, bo, :], lhsT=im[0:C, 0:C], rhs=vb[0:C, :], start=True, stop=not has_inter), reads=["im", "vb"], writes=psk(bo))
                if not isS:
                    if not first:
                        for dc in range(2):
                            S.pe(lambda e, bo=bo, dc=dc, c0=c0, C=C: e.matmul(ps[0:C, bo, :], lhsT=qb[:, dc, c0:c0 + C], rhs=Sbf[:, dc, :], start=False, stop=(dc == 1)), reads=qkeys + [("Sbf", dc)], writes=psk(bo))
                    for dc in range(2):
                        bs = ps_next()
                        S.pe(lambda e, bs=bs, dc=dc, C=C: e.matmul(ps[:, bs, :], lhsT=kh[0:C, dc * 128:(dc + 1) * 128], rhs=vb[0:C, :], start=True, stop=True), reads=["kh", "vb"], writes=psk(bs))
                        if first:
                            S.dve(lambda e, bs=bs, dc=dc: e.tensor_copy(out=S32[:, dc, :], in_=ps[:, bs, :]), reads=psk(bs), writes=[("S32", dc)])
                        else:
                            S.dve(lambda e, bs=bs, dc=dc, gc=float(g[h] ** 128): e.scalar_tensor_tensor(out=S32[:, dc, :], in0=S32[:, dc, :], scalar=gc, in1=ps[:, bs, :], op0=ALU.mult, op1=ALU.add), reads=psk(bs) + [("S32", dc)], writes=[("S32", dc)])
                        if last:
                            S.dma("sp", "rpo", lambda e, dc=dc, h=h: e.dma_start(out=K.ret_p[h, dc * 128:(dc + 1) * 128, :], in_=S32[:, dc, :]), reads=[("S32", dc)])
                        else:
                            S.act(lambda e, dc=dc: e.copy(out=Sbf[:, dc, :], in_=S32[:, dc, :]), reads=[("S32", dc)], writes=[("Sbf", dc)])
                else:
                    K.P.reserved = {bo}
                    for dc in range(2):
                        S.dve(lambda e, dc=dc: e.tensor_tensor(out=qX[:, dc, :, :], in0=qf[:, dc, :].unsqueeze(1).to_broadcast([128, NSEQ, TS]), in1=segmask[:, :, :], op=ALU.mult), reads=[("rd", 0, dc), "segmask"], writes=[("qX", dc)])
                    def _ldr(n, h=h):
                        i3 = n % 2
                        S.dma("sp", "sin%d" % i3, lambda e, i3=i3, n=n, h=h: e.dma_start(out=Sin[i3], in_=K.s_ret[n, h].rearrange("(dc p) v -> p dc v", p=128)), writes=[("Sin", i3)])
                    _ldr(0)
                    for s_ in range(NSEQ):
                        i2 = s_ % 2
                        if s_ + 1 < NSEQ:
                            _ldr(s_ + 1)
                        for dc in range(2):
                            S.pe(lambda e, bo=bo, dc=dc, s_=s_, i2=i2: e.matmul(ps[0:TS, bo, :], lhsT=qX[:, dc, s_, :], rhs=Sin[i2][:, dc, :], start=False, stop=(s_ == NSEQ - 1 and dc == 1)), reads=[("qX", dc), ("Sin", i2)], writes=psk(bo))
                        S.dve(lambda e, i2=i2, s_=s_: e.tensor_scalar(out=khm[i2][0:TS, :], in0=kh[0:TS, :], scalar1=segcol[0:TS, s_:s_ + 1], scalar2=None, op0=ALU.mult), reads=["kh", "segcol"], writes=[("khm", i2)])
                        for dc in range(2):
                            bs = ps_next()
                            S.pe(lambda e, bs=bs, dc=dc, i2=i2: e.matmul(ps[:, bs, :], lhsT=khm[i2][0:TS, dc * 128:(dc + 1) * 128], rhs=vb[0:TS, :], start=True, stop=True), reads=[("khm", i2), "vb"], writes=psk(bs))
                            S.dve(lambda e, bs=bs, dc=dc, i2=i2, gc=float(g[h] ** 4): e.scalar_tensor_tensor(out=Sout[:, dc, :], in0=Sin[i2][:, dc, :], scalar=gc, in1=ps[:, bs, :], op0=ALU.mult, op1=ALU.add), reads=psk(bs) + [("Sin", i2)], writes=[("Sout", dc)])
                        S.dma("sp", "sout", lambda e, s_=s_, h=h: e.dma_start(out=K.ret_s[s_, h].rearrange("(dc p) v -> p dc v", p=128), in_=Sout), reads=[("Sout", 0), ("Sout", 1)], writes=[])
                    K.P.reserved = set()
                _ret_tail(K, h, bo, C, c0, sg, og, ogT, ss, t1)
            for dc8 in range(KC):
                b = ps_next()
                for ec in range(4):
                    S.pe(lambda e, b=b, ec=ec, dc8=dc8, nt=nt: e.matmul(ps[:, b, 0:nt], lhsT=wo[:, ec, dc8 * 128:(dc8 + 1) * 128], rhs=ogT[:, ec, 0:nt], start=(ec == 0), stop=(ec == 3)), reads=[wok, "ogT"], writes=psk(b))
                K.resid_update(ps[:, b, 0:nt], gbase, dc8, ("S" if isS else bi), t0, nt, psk(b))
        S.barrier()
    S.barrier()


def _ret_tail(K, h, bo, C, c0, sg, og, ogT, ss, junk):
    S = K.S; ps = K.ps
    S.act(lambda e: e.activation(out=junk[0:C, :], in_=ps[0:C, bo, :], func=AF.Square, accum_out=ss[0:C, 0:1]), reads=K.psk(bo), writes=["t1", "ss"])
    S.act(lambda e: e.activation(out=ss[0:C, 1:2], in_=ss[0:C, 0:1], func=AF.Sqrt, bias=EPS, scale=1.0 / 512), reads=["ss"], writes=["ss2"])
    S.dve(lambda e: e.reciprocal(out=ss[0:C, 1:2], in_=ss[0:C, 1:2]), reads=["ss2"], writes=["ss2"])
    S.dve(lambda e: e.scalar_tensor_tensor(out=og[0:C, :], in0=ps[0:C, bo, :], scalar=ss[0:C, 1:2], in1=sg[0:C, :], op0=ALU.mult, op1=ALU.mult), reads=K.psk(bo) + ["ss2", "sg"], writes=["og"])
    bt = K.ps_next()
    psb = ps[:, bt, :].bitcast(BF16)
    for ec in range(4):
        S.pe(lambda e, ec=ec: e.transpose(psb[:, ec * C:(ec + 1) * C], og[0:C, ec * 128:(ec + 1) * 128], K.ident_b[0:C, 0:C]), reads=["og", "ident_b"], writes=K.psk(bt))
    S.act(lambda e: e.copy(out=ogT[:, :, c0:c0 + C], in_=psb[:, 0:4 * C].rearrange("p (a b) -> p a b", a=4)), reads=K.psk(bt), writes=["ogT"])


def do_gdn(K):
    S = K.S; ps = K.ps
    a_f32 = K.a_f32; a_bf16 = K.a_bf16; ps_next = K.ps_next; psk = K.psk
    hT = K.hT; ident_f = K.ident_f; ident_b = K.ident_b; ones_b = K.ones_b
    w_in = K.w_gdn_in; w_out = K.w_gdn_out
    K.a_reset()
    K.tmpg = a_f32(TS)
    gm = a_f32(8, 128)
    segmask = a_f32(NSEQ, TS)
    segcol = a_f32(NSEQ)
    beta_all = a_f32(17, 16); g_all = a_f32(17, 16); negeG_all = a_f32(17, 16); kdec_all = a_f32(17, 16)
    gv = a_f32(32); negA = a_f32(16); gnw = a_f32(128)
    wcT = a_f32(32, 4)
    ones128 = a_f32(128)
    wba = a_bf16(KC, 32)
    Gx = a_f32(3 + 512); acc = a_f32(512); acc2 = a_f32(512); cch = a_f32(4, 3); gsT = a_f32(4, 48); Gxs = a_f32(NSEQ, 7)
    tailP = a_f32(4, 3); tailS = a_f32(4, NSEQ, 3)
    vs = a_f32(2, 512)
    sqb = a_bf16(512); rs = a_f32(512)
    qn = a_bf16(512); kn = a_bf16(512); qnf = a_f32(TS); knf = a_f32(TS)
    Bm = a_f32(2, 128); E = a_f32(2, 128); E2 = a_f32(2, 128); DTm = a_f32(2, 128)
    U = a_bf16(2, 128); UT = a_bf16(2, 128)
    UoM = [a_bf16(2, 128) for _ in range(3)]; UoTM = [a_bf16(2, 128) for _ in range(3)]
    Nb = [a_bf16(2, 128) for _ in range(2)]; Pb = [a_bf16(2, 128) for _ in range(2)]; PTb = [a_bf16(2, 128) for _ in range(2)]
    NTb = [a_bf16(2, 128) for _ in range(2)]; bm = a_bf16(4, 128)
    ktok = a_bf16(128); xv = a_bf16(2, 128); vnew = a_bf16(2, 128)
    Sbf = a_bf16(2, 128); og = a_bf16(256); ogT = a_bf16(2, 512)
    S32 = a_f32(2, 128); ss = a_f32(8)
    X = a_f32(NSEQ, TS); qgf = a_f32(TS)
    kdm = a_bf16(128); Sin = [a_f32(128) for _ in range(2)]; Sout = a_f32(128)
    ostg = acc
    eGl = [a_f32(2), a_f32(2)]
    w3f = K.wring[:, 3, :].bitcast(F32)
    w3b = K.wring[:, 3, :]
    szn2 = [w3f[:, 0:256], w3f[:, 256:512]]
    vtok2 = [w3f[:, 512:768].rearrange("p (a b) -> p a b", a=2), w3f[:, 768:1024].rearrange("p (a b) -> p a b", a=2)]

    def _b3(i):
        return w3b[:, 2048 + i * 256:2048 + (i + 1) * 256].rearrange("p (a b) -> p a b", a=2)
    Nfin = [_b3(0), _b3(1)]; attnT2 = [_b3(2), _b3(3)]; qg2 = [_b3(4), _b3(5)]; kd2 = [_b3(6), _b3(7)]

    def _f32v(b):
        return b.rearrange("p a b -> p (a b)").bitcast(F32)
    SinR = [Sin[0], Sin[1]] + [_f32v(x) for x in UoM + UoTM]
    SoutR = [Sout, _f32v(NTb[0]), _f32v(NTb[1])]
    RDEPTH = 7
    NLD = 2 * NSEQ
    mark = K.A.off

    S.dma("sp", "gc", lambda e: e.dma_start(out=gm.rearrange("p a b -> p (a b)"), in_=K.c_gmask), writes=["gm"])
    S.dma("sp", "gc", lambda e: e.dma_start(out=segmask.rearrange("p a b -> p (a b)"), in_=K.c_segmask), writes=["segmask"])
    S.dma("sp", "gc", lambda e: e.dma_start(out=segcol, in_=K.c_segcol), writes=["segcol"])
    S.dma("pool", "gcb", lambda e: e.dma_start(out=bm.rearrange("p a b -> p (a b)"), in_=K.c_bmask), writes=["bm"])
    S.dma("sp", "gc", lambda e: e.dma_start(out=gv, in_=K.gdn_vec.partition_broadcast(128)), writes=["gv"])
    S.dma("sp", "gc", lambda e: e.dma_start(out=gnw, in_=K.gdn_norm.partition_broadcast(128)), writes=["gnw"])
    S.dve(lambda e: e.memset(ones128, 1.0), writes=["ones128"])
    S.act(lambda e: e.activation(out=negA, in_=gv[:, 0:16], func=AF.Exp), reads=["gv"], writes=["negA"])
    S.dve(lambda e: e.tensor_scalar(out=negA, in0=negA, scalar1=-1.0, scalar2=None, op0=ALU.mult), reads=["negA"], writes=["negA"])
    TRIU = {False: gm[:, 0, :], True: gm[:, 4, :]}
    SU = {False: gm[:, 1, :], True: gm[:, 5, :]}
    INCL = {False: gm[:, 2, :], True: gm[:, 6, :]}
    STRICT = {False: gm[:, 3, :], True: gm[:, 7, :]}
    import os
    STG = int(os.environ.get('GDN_STAGE', 99))
    if STG == 0:
        S.barrier(); return
    Xflat = X.rearrange("p a b -> p (a b)")
    wst = [Xflat[0:4, 0:512], Xflat[0:4, 512:1024]]
    bw = ps_next()
    for pc in range(8):
        S.dma("sp", "wst%d" % (pc % 2), lambda e, pc=pc: e.dma_start(out=wst[pc % 2], in_=K.w_gconv[:, pc * 512:(pc + 1) * 512]), writes=[("wst", pc % 2)])
        for q in range(4):
            cidx = pc * 4 + q
            S.pe(lambda e, pc=pc, q=q, cidx=cidx: e.transpose(ps[:, bw, cidx * 4:(cidx + 1) * 4], wst[pc % 2][:, q * 128:(q + 1) * 128], ident_f[0:4, 0:4]), reads=[("wst", pc % 2), "ident_f"], writes=psk(bw))
    S.dve(lambda e: e.tensor_copy(out=wcT.rearrange("p a b -> p (a b)"), in_=ps[:, bw, 0:128]), reads=psk(bw), writes=["wcT"])
    if STG == 1:
        S.barrier(); return
    src = w_in[:, 6144:6176].rearrange("(k p) c -> p k c", p=128)
    S.dma("pool", "wba", lambda e: e.dma_start(out=wba, in_=src), writes=["wba"])
    if STG == 2:
        S.barrier(); return
    tiles = [(t * 128, 128, False) for t in range(16)] + [(TP, TS, True)]
    for tl, (t0, C, isS) in enumerate(tiles):
        pb = ps_next()
        for kc in range(KC):
            S.pe(lambda e, pb=pb, kc=kc, t0=t0, C=C: e.matmul(ps[0:C, pb, 0:32], lhsT=hT[:, kc, t0:t0 + C], rhs=wba[:, kc, :], start=(kc == 0), stop=(kc == KC - 1)), reads=["wba"], writes=psk(pb))
        S.act(lambda e, pb=pb, C=C, tl=tl: e.activation(out=beta_all[0:C, tl, :], in_=ps[0:C, pb, 0:16], func=AF.Sigmoid), reads=psk(pb), writes=[("beta", tl)])
        S.dve(lambda e, pb=pb, C=C, tl=tl: e.tensor_tensor(out=g_all[0:C, tl, :], in0=ps[0:C, pb, 16:32], in1=gv[0:C, 16:32], op=ALU.add), reads=psk(pb) + ["gv"], writes=[("g", tl)])
        S.act(lambda e, C=C, tl=tl: e.activation(out=g_all[0:C, tl, :], in_=g_all[0:C, tl, :], func=AF.Exp), reads=[("g", tl)], writes=[("g", tl)])
        S.act(lambda e, C=C, tl=tl: e.activation(out=g_all[0:C, tl, :], in_=g_all[0:C, tl, :], func=AF.Ln, bias=1.0), reads=[("g", tl)], writes=[("g", tl)])
        S.dve(lambda e, C=C, tl=tl: e.tensor_tensor(out=g_all[0:C, tl, :], in0=g_all[0:C, tl, :], in1=negA[0:C, :], op=ALU.mult), reads=[("g", tl), "negA"], writes=[("g", tl)])
        pg = ps_next()
        S.pe(lambda e, pg=pg, C=C, tl=tl, isS=isS: e.matmul(ps[0:C, pg, 0:16], lhsT=TRIU[isS][0:C, 0:C], rhs=g_all[0:C, tl, :], start=True, stop=True), reads=[("g", tl), "gm"], writes=psk(pg))
        S.pe(lambda e, pg=pg, C=C, tl=tl, isS=isS: e.matmul(ps[0:C, pg, 16:32], lhsT=SU[isS][0:C, 0:C], rhs=g_all[0:C, tl, :], start=True, stop=True), reads=[("g", tl), "gm"], writes=psk(pg))
        S.act(lambda e, pg=pg, C=C, tl=tl: e.activation(out=negeG_all[0:C, tl, :], in_=ps[0:C, pg, 0:16], func=AF.Exp), reads=psk(pg), writes=[("negeG", tl)])
        S.dve(lambda e, C=C, tl=tl: e.tensor_scalar(out=negeG_all[0:C, tl, :], in0=negeG_all[0:C, tl, :], scalar1=-1.0, scalar2=None, op0=ALU.mult), reads=[("negeG", tl)], writes=[("negeG", tl)])
        S.act(lambda e, pg=pg, C=C, tl=tl: e.activation(out=kdec_all[0:C, tl, :], in_=ps[0:C, pg, 16:32], func=AF.Exp), reads=psk(pg), writes=[("kdec", tl)])
    S.barrier()
    K.A.off = mark
    gbase = 48 + 16

    import os
    for hk in range(int(os.environ.get("GDN_HK0", 0)), int(os.environ.get("GDN_NHK", GDN_HK))):
        hv0 = 2 * hk
        wq, wqk = K.load_w_bf16(w_in, KC, 128, col0=hk * 128, slot=0, off=0, key=("w", 0, "q"))
        wk, wkk = K.load_w_bf16(w_in, KC, 128, col0=1024 + hk * 128, slot=0, off=1024, key=("w", 0, "k"))
        wv, wvk = K.load_w_bf16(w_in, KC, 256, col0=2048 + hk * 256, slot=1, off=0, key=("w", 1, "v"))
        wz, wzk = K.load_w_bf16(w_in, KC, 256, col0=4096 + hk * 256, slot=1, off=2048, key=("w", 1, "z"))
        wo, wok = K.load_w_bf16(w_out, 2, 1024, row0=hk * 256, slot=2)
        cids = [hk, 8 + hk, 16 + 2 * hk, 17 + 2 * hk]
        CUT = os.environ.get('GDN_CUT', '')
        if hk >= 1 and 'D' in CUT:
            S.barrier(); continue
        gst = a_f32(512, parts=48)
        K.A.off = mark
        if not (hk >= 1 and 'H' in CUT):
            for ci, cid in enumerate(cids):
                S.dma("sp", "gst", lambda e, ci=ci, cid=cid, gst=gst: e.dma_start(out=gst[:, ci * 128:(ci + 1) * 128], in_=K.s_gconv[:, cid * 128:(cid + 1) * 128]), writes=["gst"])
            bq = ps_next()
            for ci in range(4):
                S.pe(lambda e, ci=ci, bq=bq, gst=gst: e.transpose(ps[:, bq, ci * 48:(ci + 1) * 48], gst[:, ci * 128:(ci + 1) * 128], ident_f[0:48, 0:48]), reads=["gst", "ident_f"], writes=psk(bq))
            S.dve(lambda e, bq=bq: e.tensor_copy(out=gsT.rearrange("p a b -> p (a b)"), in_=ps[:, bq, 0:192]), reads=psk(bq), writes=["gsT"])
        S.dve(lambda e: e.memset(cch, 0.0), writes=["cch"])
        if hk >= 1 and 'E' in CUT:
            S.barrier(); continue
        for bi, (t0, nt) in enumerate(K.ALLBLOCKS):
            isS = t0 >= TP
            C = 64 if isS else 128
            lastblk = (t0 + nt == TP)
            if isS:
                S.barrier()
            for ci, (wt, wkey, col) in enumerate([(wq, wqk, 0), (wk, wkk, 0), (wv, wvk, 0), (wv, wvk, 128)]):
                pp = ps_next()
                for kc in range(KC):
                    S.pe(lambda e, pp=pp, kc=kc, wt=wt, col=col, t0=t0, nt=nt: e.matmul(ps[:, pp, 0:nt], lhsT=wt[:, kc, col:col + 128], rhs=hT[:, kc, t0:t0 + nt], start=(kc == 0), stop=(kc == KC - 1)), reads=[wkey], writes=psk(pp))
                cid = cids[ci]
                wc = [wcT[:, cid, i:i + 1] for i in range(4)]
                accb = acc if ci % 2 == 0 else acc2
                ak = ("acc", ci % 2)
                dst = accb[:, 0:nt] if ci < 2 else vs[:, ci - 2, 0:nt]
                if not isS:
                    S.dve(lambda e, ci=ci: e.tensor_copy(out=Gx[:, 0:3], in_=cch[:, ci, :]), reads=["cch"], writes=["Gx"])
                    S.act(lambda e, pp=pp, nt=nt: e.copy(out=Gx[:, 3:3 + nt], in_=ps[:, pp, 0:nt]), reads=psk(pp), writes=["Gx"])
                    S.dve(lambda e, ci=ci, nt=nt: e.tensor_copy(out=cch[:, ci, :], in_=Gx[:, nt:nt + 3]), reads=["Gx"], writes=["cch"])
                    if lastblk:
                        S.dve(lambda e, ci=ci, nt=nt: e.tensor_copy(out=tailP[:, ci, :], in_=Gx[:, nt:nt + 3]), reads=["Gx"], writes=["tailP"])
                    x3 = [Gx[:, i:i + nt] for i in range(4)]
                    a_ = accb[:, 0:nt]
                    d_ = dst
                    gk = ["Gx"]
                else:
                    S.dve(lambda e, ci=ci: e.tensor_copy(out=Gxs[:, :, 0:3], in_=gsT[:, ci, :].rearrange("p (s r) -> p s r", r=3)), reads=["gsT"], writes=["Gxs"])
                    S.act(lambda e, pp=pp: e.copy(out=Gxs[:, :, 3:7], in_=ps[:, pp, 0:TS].rearrange("p (s j) -> p s j", j=4)), reads=psk(pp), writes=["Gxs"])
                    S.dve(lambda e, ci=ci: e.tensor_copy(out=tailS[:, ci, :, :], in_=Gxs[:, :, 4:7]), reads=["Gxs"], writes=["tailS"])
                    x3 = [Gxs[:, :, i:i + 4] for i in range(4)]
                    a_ = accb[:, 0:TS].rearrange("p (s j) -> p s j", j=4)
                    d_ = dst.rearrange("p (s j) -> p s j", j=4)
                    gk = ["Gxs"]
                S.act(lambda e, a_=a_, x3=x3, wc=wc: e.activation(out=a_, in_=x3[3], func=AF.Identity, scale=wc[3]), reads=gk + ["wcT"], writes=[ak])
                for i in (2, 1, 0):
                    S.dve(lambda e, a_=a_, x3=x3, wc=wc, i=i: e.scalar_tensor_tensor(out=a_, in0=x3[i], scalar=wc[i], in1=a_, op0=ALU.mult, op1=ALU.add), reads=gk + [ak], writes=[ak])
                S.act(lambda e, a_=a_, d_=d_: e.activation(out=d_, in_=a_, func=AF.Silu), reads=[ak], writes=([ak, ("cv", ci)] if ci < 2 else [("cv", ci)]))
                if ci < 2:
                    S.act(lambda e, nt=nt, accb=accb: e.activation(out=sqb[:, 0:nt], in_=accb[:, 0:nt], func=AF.Square), reads=[ak], writes=["sqb"])
                    pn = ps_next()
                    S.pe(lambda e, pn=pn, nt=nt: e.matmul(ps[:, pn, 0:nt], lhsT=ones_b[:], rhs=sqb[:, 0:nt], start=True, stop=True), reads=["sqb", "ones_b"], writes=psk(pn))
                    S.act(lambda e, pn=pn, nt=nt: e.activation(out=rs[:, 0:nt], in_=ps[:, pn, 0:nt], func=AF.Sqrt, bias=EPS, scale=1.0), reads=psk(pn), writes=["rs"])
                    S.dve(lambda e, nt=nt: e.reciprocal(out=rs[:, 0:nt], in_=rs[:, 0:nt]), reads=["rs"], writes=["rs"])
                    dn = qn if ci == 0 else kn
                    dnf = qnf if ci == 0 else knf
                    sc = (128.0 ** -0.5) if ci == 0 else 1.0
                    if isS:
                        S.dve(lambda e, dnf=dnf, sc=sc, accb=accb: e.scalar_tensor_tensor(out=dnf[:, :], in0=accb[:, 0:TS], scalar=sc, in1=rs[:, 0:TS], op0=ALU.mult, op1=ALU.mult), reads=[ak, "rs"], writes=[("nf", ci)])
                        S.act(lambda e, dn=dn, dnf=dnf: e.copy(out=dn[:, 0:TS], in_=dnf[:, :]), reads=[("nf", ci)], writes=[("n", ci)])
                    else:
                        S.dve(lambda e, dn=dn, sc=sc, nt=nt, accb=accb: e.scalar_tensor_tensor(out=dn[:, 0:nt], in0=accb[:, 0:nt], scalar=sc, in1=rs[:, 0:nt], op0=ALU.mult, op1=ALU.mult), reads=[ak, "rs"], writes=[("n", ci)])
            def chunk_gen(c0, pb):
                tl = (t0 + c0) // 128
                first = (not isS) and t0 == 0 and c0 == 0
                last = (not isS) and (t0 + c0 + C == TP)
                L = 1 if isS else 6
                bvt = ps_next()
                for e_ in range(2):
                    S.pe(lambda e, e_=e_, bvt=bvt, c0=c0, C=C: e.transpose(ps[0:C, bvt, e_ * 128:(e_ + 1) * 128], vs[:, e_, c0:c0 + C], ident_f[:]), reads=[("cv", 2 + e_), "ident_f"], writes=psk(bvt))
                    yield
                S.act(lambda e, bvt=bvt, C=C: e.copy(out=vtok2[pb][0:C].rearrange("p a b -> p (a b)"), in_=ps[0:C, bvt, 0:256]), reads=psk(bvt), writes=[("vtok", pb)])
                yield
                bz = ps_next()
                for kc in range(KC):
                    S.pe(lambda e, bz=bz, kc=kc, t0=t0, c0=c0, C=C: e.matmul(ps[0:C, bz, 0:256], lhsT=hT[:, kc, t0 + c0:t0 + c0 + C], rhs=wz[:, kc, :], start=(kc == 0), stop=(kc == KC - 1)), reads=[wzk], writes=psk(bz))
                    yield
                S.act(lambda e, bz=bz, C=C: e.activation(out=szn2[pb][0:C, :], in_=ps[0:C, bz, 0:256], func=AF.Silu), reads=psk(bz), writes=[("szn", pb)])
                yield
                S.dve(lambda e, C=C: e.tensor_tensor(out=szn2[pb][0:C, :].rearrange("p (a b) -> p a b", a=2), in0=szn2[pb][0:C, :].rearrange("p (a b) -> p a b", a=2), in1=gnw[0:C, :].unsqueeze(1).to_broadcast([C, 2, 128]), op=ALU.mult), reads=[("szn", pb), "gnw"], writes=[("szn", pb)])
                yield
                bkt = ps_next()
                pkb = ps[:, bkt, :].bitcast(BF16)
                S.pe(lambda e, pkb=pkb, c0=c0, C=C: e.transpose(pkb[0:C, 0:128], kn[:, c0:c0 + C], ident_b[:]), reads=[("n", 1), "ident_b"], writes=psk(bkt))
                yield
                S.act(lambda e, pkb=pkb, C=C: e.copy(out=ktok[0:C, :], in_=pkb[0:C, 0:128]), reads=psk(bkt), writes=["ktok"])
                yield
                bkk = ps_next()
                S.pe(lambda e, bkk=bkk, c0=c0, C=C: e.matmul(ps[0:C, bkk, 0:C], lhsT=kn[:, c0:c0 + C], rhs=kn[:, c0:c0 + C], start=True, stop=True), reads=[("n", 1)], writes=psk(bkk))
                yield
                S.pe(lambda e, bkk=bkk, c0=c0, C=C: e.matmul(ps[0:C, bkk, 128:128 + C], lhsT=kn[:, c0:c0 + C], rhs=qn[:, c0:c0 + C], start=True, stop=True), reads=[("n", 0), ("n", 1)], writes=psk(bkk))
                yield
                bd = ps_next()
                bd2 = ps_next()
                for e_ in range(2):
                    hv = hv0 + e_
                    S.dve(lambda e, e_=e_, hv=hv, C=C, tl=tl, isS=isS: e.tensor_scalar(out=Bm[0:C, e_, 0:C], in0=TRIU[isS][0:C, 0:C], scalar1=g_all[0:C, tl, hv:hv + 1], scalar2=None, op0=ALU.mult), reads=["gm"], writes=[("Bm", e_)])
                    yield
                    S.pe(lambda e, e_=e_, bd=bd, C=C, isS=isS: e.matmul(ps[0:C, bd, e_ * 128:e_ * 128 + C], lhsT=SU[isS][0:C, 0:C], rhs=Bm[0:C, e_, 0:C], start=True, stop=True), reads=[("Bm", e_), "gm"], writes=psk(bd))
                    yield
                    S.pe(lambda e, e_=e_, bd2=bd2, C=C: e.matmul(ps[:, bd2, e_ * 128:e_ * 128 + C], lhsT=ones128[0:C, :], rhs=Bm[0:C, e_, 0:C], start=True, stop=True), reads=[("Bm", e_), "ones128"], writes=psk(bd2))
                    yield
                S.act(lambda e, bd=bd, C=C: e.activation(out=E[0:C, :, 0:C], in_=ps[0:C, bd, 0:256].rearrange("p (a b) -> p a b", a=2)[:, :, 0:C], func=AF.Exp), reads=psk(bd), writes=["E"])
                yield
                S.act(lambda e, bd2=bd2, C=C: e.activation(out=E2[:, :, 0:C], in_=ps[:, bd2, 0:256].rearrange("p (a b) -> p a b", a=2)[:, :, 0:C], func=AF.Exp), reads=psk(bd2), writes=["E2"])
                yield
                S.dve(lambda e, C=C, isS=isS: e.tensor_tensor(out=DTm[0:C, :, 0:C], in0=E[0:C, :, 0:C], in1=INCL[isS][0:C, 0:C].unsqueeze(1).to_broadcast([C, 2, C]), op=ALU.mult), reads=["E", "gm"], writes=["DTm"])
                yield
                S.dve(lambda e, C=C, isS=isS: e.tensor_tensor(out=E[0:C, :, 0:C], in0=E[0:C, :, 0:C], in1=STRICT[isS][0:C, 0:C].unsqueeze(1).to_broadcast([C, 2, C]), op=ALU.mult), reads=["E", "gm", "DTm"], writes=["E"])
                yield
                for e_ in range(2):
                    hv = hv0 + e_
                    S.dve(lambda e, e_=e_, hv=hv, bkk=bkk, C=C, tl=tl: e.scalar_tensor_tensor(out=U[0:C, e_, 0:C], in0=ps[0:C, bkk, 0:C], scalar=beta_all[0:C, tl, hv:hv + 1], in1=E[0:C, e_, 0:C], op0=ALU.mult, op1=ALU.mult), reads=psk(bkk) + ["E"], writes=[("U", e_)])
                    yield
                    S.dve(lambda e, e_=e_, bkk=bkk, C=C: e.tensor_tensor(out=attnT2[pb][0:C, e_, 0:C], in0=ps[0:C, bkk, 128:128 + C], in1=DTm[0:C, e_, 0:C], op=ALU.mult), reads=psk(bkk) + ["DTm"], writes=[("attnT", e_, pb)])
                    yield
                    if isS:
                        pass
                    else:
                        S.pool(lambda e, e_=e_, c0=c0, C=C: e.tensor_tensor(out=qg2[pb][:, e_, 0:C], in0=qn[:, c0:c0 + C], in1=E2[:, e_, 0:C], op=ALU.mult), reads=[("n", 0), "E2"], writes=[("qg", e_, pb)])
                        yield
                    S.dve(lambda e, e_=e_, hv=hv, C=C, tl=tl: e.tensor_scalar(out=kd2[pb][0:C, e_, :], in0=ktok[0:C, :], scalar1=kdec_all[0:C, tl, hv:hv + 1], scalar2=None, op0=ALU.mult), reads=["ktok"], writes=[("kd", e_, pb)])
                    yield
                but = ps_next()
                pub = ps[:, but, :].bitcast(BF16)
                for e_ in range(2):
                    S.pe(lambda e, e_=e_, pub=pub, C=C: e.transpose(pub[0:C, e_ * 128:e_ * 128 + C], U[0:C, e_, 0:C], ident_b[0:C, 0:C]), reads=[("U", e_), "ident_b"], writes=psk(but))
                    yield
                S.act(lambda e, pub=pub, C=C: e.copy(out=UT[0:C, :, 0:C], in_=pub[0:C, 0:256].rearrange("p (a b) -> p a b", a=2)[:, :, 0:C]), reads=psk(but), writes=["UT"])
                yield
                if isS:
                    S.dve(lambda e, C=C: e.tensor_tensor(out=Nb[0][0:C, :, 0:C], in0=ident_f[0:C, 0:C].unsqueeze(1).to_broadcast([C, 2, C]), in1=U[0:C, :, 0:C], op=ALU.subtract), reads=[("U", 0), ("U", 1), "ident_f"], writes=[("N", 0)])
                    yield
                    Pprev, PTprev, Pk_, PTk_ = U, UT, ["U0", "U1"], ["UT"]
                    Pkeys_prev = [("U", 0), ("U", 1)]
                    PTkeys_prev = ["UT"]
                    ni = 0
                    for lv in range(1, L + 1):
                        pi = lv % 2
                        need_P = lv < L
                        if need_P:
                            b1 = ps_next()
                            for e_ in range(2):
                                S.pe(lambda e, e_=e_, b1=b1, C=C, PTprev=PTprev, Pprev=Pprev: e.matmul(ps[0:C, b1, e_ * 128:e_ * 128 + C], lhsT=PTprev[0:C, e_, 0:C], rhs=Pprev[0:C, e_, 0:C], start=True, stop=True), reads=Pkeys_prev + PTkeys_prev, writes=psk(b1))
                                yield
                            S.act(lambda e, b1=b1, C=C, pi=pi: e.copy(out=Pb[pi][0:C, :, 0:C], in_=ps[0:C, b1, 0:256].rearrange("p (a b) -> p a b", a=2)[:, :, 0:C]), reads=psk(b1), writes=[("P", pi)])
                            yield
                        b2 = ps_next()
                        for e_ in range(2):
                            S.pe(lambda e, e_=e_, b2=b2, C=C, PTprev=PTprev, Pprev=Pprev: e.matmul(ps[0:C, b2, e_ * 128:e_ * 128 + C], lhsT=Pprev[0:C, e_, 0:C], rhs=PTprev[0:C, e_, 0:C], start=True, stop=True), reads=Pkeys_prev + PTkeys_prev, writes=psk(b2))
                            yield
                        S.act(lambda e, b2=b2, C=C, pi=pi: e.copy(out=PTb[pi][0:C, :, 0:C], in_=ps[0:C, b2, 0:256].rearrange("p (a b) -> p a b", a=2)[:, :, 0:C]), reads=psk(b2), writes=[("PT", pi)])
                        yield
                        b3 = ps_next()
                        for e_ in range(2):
                            S.pe(lambda e, e_=e_, b3=b3, C=C, pi=pi, ni=ni: e.matmul(ps[0:C, b3, e_ * 128:e_ * 128 + C], lhsT=PTb[pi][0:C, e_, 0:C], rhs=Nb[ni][0:C, e_, 0:C], start=True, stop=True), reads=[("PT", pi), ("N", ni)], writes=psk(b3))
                            yield
                        S.dve(lambda e, b3=b3, C=C, ni=ni: e.tensor_tensor(out=Nb[1 - ni][0:C, :, 0:C], in0=ps[0:C, b3, 0:256].rearrange("p (a b) -> p a b", a=2)[:, :, 0:C], in1=Nb[ni][0:C, :, 0:C], op=ALU.add), reads=psk(b3) + [("N", ni)], writes=[("N", 1 - ni)])
                        yield
                        ni = 1 - ni
                        Pprev, PTprev = Pb[pi], PTb[pi]
                        Pkeys_prev = [("P", pi)]
                        PTkeys_prev = [("PT", pi)]

                else:
                    def _ev2(b, C=C):
                        return ps[0:C, b, 0:256].rearrange("p (a b) -> p a b", a=2)[:, :, 0:C]
                    idb = ident_f[0:C, 0:C].unsqueeze(1).to_broadcast([C, 2, C])
                    S.dve(lambda e: e.tensor_tensor(out=Pb[0][:, :, :], in0=U[:, :, :], in1=bm[:, 0, :].unsqueeze(1).to_broadcast([128, 2, 128]), op=ALU.mult), reads=[("U", 0), ("U", 1), "bm"], writes=[("P", 0)])
                    yield
                    S.dve(lambda e: e.tensor_tensor(out=PTb[0][:, :, :], in0=UT[:, :, :], in1=bm[:, 0, :].unsqueeze(1).to_broadcast([128, 2, 128]), op=ALU.mult), reads=["UT", "bm"], writes=[("PT", 0)])
                    yield
                    S.dve(lambda e, idb=idb: e.tensor_tensor(out=Nb[0][:, :, :], in0=idb, in1=Pb[0][:, :, :], op=ALU.subtract), reads=[("P", 0), "ident_f"], writes=[("N", 0)])
                    yield
                    S.dve(lambda e, idb=idb: e.tensor_tensor(out=NTb[0][:, :, :], in0=idb, in1=PTb[0][:, :, :], op=ALU.subtract), reads=[("PT", 0), "ident_f"], writes=[("NT", 0)])
                    yield
                    for mi in range(3):
                        S.pool(lambda e, mi=mi: e.tensor_tensor(out=UoM[mi][:, :, :], in0=U[:, :, :], in1=bm[:, mi + 1, :].unsqueeze(1).to_broadcast([128, 2, 128]), op=ALU.mult), reads=[("U", 0), ("U", 1), "bm"], writes=[("UoM", mi)])
                        yield
                        S.pool(lambda e, mi=mi: e.tensor_tensor(out=UoTM[mi][:, :, :], in0=UT[:, :, :], in1=bm[:, mi + 1, :].unsqueeze(1).to_broadcast([128, 2, 128]), op=ALU.mult), reads=["UT", "bm"], writes=[("UoTM", mi)])
                        yield
                    ni = 0
                    pprev = 0
                    for lv in range(1, 4):
                        pi = lv % 2
                        b1 = ps_next(); b2 = ps_next(); b3 = ps_next(); b4 = ps_next()
                        for e_ in range(2):
                            S.pe(lambda e, e_=e_, b1=b1, pprev=pprev: e.matmul(ps[:, b1, e_ * 128:(e_ + 1) * 128], lhsT=PTb[pprev][:, e_, :], rhs=Pb[pprev][:, e_, :], start=True, stop=True), reads=[("P", pprev), ("PT", pprev)], writes=psk(b1))
                            yield
                        for e_ in range(2):
                            S.pe(lambda e, e_=e_, b2=b2, pprev=pprev: e.matmul(ps[:, b2, e_ * 128:(e_ + 1) * 128], lhsT=Pb[pprev][:, e_, :], rhs=PTb[pprev][:, e_, :], start=True, stop=True), reads=[("P", pprev), ("PT", pprev)], writes=psk(b2))
                            yield
                        S.act(lambda e, b1=b1, pi=pi: e.copy(out=Pb[pi][:, :, :], in_=_ev2(b1)), reads=psk(b1), writes=[("P", pi)])
                        yield
                        S.act(lambda e, b2=b2, pi=pi: e.copy(out=PTb[pi][:, :, :], in_=_ev2(b2)), reads=psk(b2), writes=[("PT", pi)])
                        yield
                        for e_ in range(2):
                            S.pe(lambda e, e_=e_, b3=b3, pi=pi, ni=ni: e.matmul(ps[:, b3, e_ * 128:(e_ + 1) * 128], lhsT=PTb[pi][:, e_, :], rhs=Nb[ni][:, e_, :], start=True, stop=True), reads=[("PT", pi), ("N", ni)], writes=psk(b3))
                            yield
                        for e_ in range(2):
                            S.pe(lambda e, e_=e_, b4=b4, pi=pi, ni=ni: e.matmul(ps[:, b4, e_ * 128:(e_ + 1) * 128], lhsT=Pb[pi][:, e_, :], rhs=NTb[ni][:, e_, :], start=True, stop=True), reads=[("P", pi), ("NT", ni)], writes=psk(b4))
                            yield
                        S.dve(lambda e, b3=b3, ni=ni: e.tensor_tensor(out=Nb[1 - ni][:, :, :], in0=_ev2(b3), in1=Nb[ni][:, :, :], op=ALU.add), reads=psk(b3) + [("N", ni)], writes=[("N", 1 - ni)])
                        yield
                        S.dve(lambda e, b4=b4, ni=ni: e.tensor_tensor(out=NTb[1 - ni][:, :, :], in0=_ev2(b4), in1=NTb[ni][:, :, :], op=ALU.add), reads=psk(b4) + [("NT", ni)], writes=[("NT", 1 - ni)])
                        yield
                        ni = 1 - ni
                        pprev = pi
                    for mi in range(3):
                        lastm = (mi == 2)
                        b1 = ps_next()
                        for e_ in range(2):
                            S.pe(lambda e, e_=e_, b1=b1, ni=ni, mi=mi: e.matmul(ps[:, b1, e_ * 128:(e_ + 1) * 128], lhsT=UoTM[mi][:, e_, :], rhs=Nb[ni][:, e_, :], start=True, stop=True), reads=[("UoTM", mi), ("N", ni)], writes=psk(b1))
                            yield
                        S.act(lambda e, b1=b1: e.copy(out=Pb[1][:, :, :], in_=_ev2(b1)), reads=psk(b1), writes=[("P", 1)])
                        yield
                        if not lastm:
                            b3 = ps_next()
                            for e_ in range(2):
                                S.pe(lambda e, e_=e_, b3=b3, ni=ni, mi=mi: e.matmul(ps[:, b3, e_ * 128:(e_ + 1) * 128], lhsT=UoM[mi][:, e_, :], rhs=NTb[ni][:, e_, :], start=True, stop=True), reads=[("UoM", mi), ("NT", ni)], writes=psk(b3))
                                yield
                            S.act(lambda e, b3=b3: e.copy(out=PTb[1][:, :, :], in_=_ev2(b3)), reads=psk(b3), writes=[("PT", 1)])
                            yield
                        b2 = ps_next()
                        for e_ in range(2):
                            S.pe(lambda e, e_=e_, b2=b2, ni=ni: e.matmul(ps[:, b2, e_ * 128:(e_ + 1) * 128], lhsT=NTb[ni][:, e_, :], rhs=Pb[1][:, e_, :], start=True, stop=True), reads=[("NT", ni), ("P", 1)], writes=psk(b2))
                            yield
                        S.dve(lambda e, b2=b2, ni=ni, lastm=lastm: e.tensor_tensor(out=(Nfin[pb] if lastm else Nb[1 - ni])[:, :, :], in0=Nb[ni][:, :, :], in1=_ev2(b2), op=ALU.subtract), reads=psk(b2) + [("N", ni)], writes=[(("Nfin", pb) if lastm else ("N", 1 - ni))])
                        yield
                        if not lastm:
                            b4 = ps_next()
                            for e_ in range(2):
                                S.pe(lambda e, e_=e_, b4=b4, ni=ni: e.matmul(ps[:, b4, e_ * 128:(e_ + 1) * 128], lhsT=Nb[ni][:, e_, :], rhs=PTb[1][:, e_, :], start=True, stop=True), reads=[("N", ni), ("PT", 1)], writes=psk(b4))
                                yield
                            S.dve(lambda e, b4=b4, ni=ni: e.tensor_tensor(out=NTb[1 - ni][:, :, :], in0=NTb[ni][:, :, :], in1=_ev2(b4), op=ALU.subtract), reads=psk(b4) + [("NT", ni)], writes=[("NT", 1 - ni)])
                            yield
                        ni = 1 - ni
                if isS:
                    S.pool(lambda e, ni=ni, C=C: e.tensor_copy(out=Nfin[pb][0:C, :, 0:C], in_=Nb[ni][0:C, :, 0:C]), reads=[("N", ni)], writes=[("Nfin", pb)])
                    yield
                Nf = Nfin[pb]
                Nkey = ("Nfin", pb)
                if not isS:
                    S.pool(lambda e, C=C: e.tensor_copy(out=eGl[pb][:, 0:2], in_=E2[:, :, C - 1:C].rearrange("p a b -> p (a b)")), reads=["E2"], writes=[("eGl", pb)])
                    yield
                yield "SPLIT"
                if not isS:
                    if first:
                        S.act(lambda e, C=C: e.copy(out=xv[0:C].rearrange("p a b -> p (a b)"), in_=vtok2[pb][0:C].rearrange("p a b -> p (a b)")), reads=[("vtok", pb)], writes=["xv"])
                        yield
                    else:
                        pk_ = ps_next()
                        S.pe(lambda e, pk_=pk_, c0=c0, C=C: e.matmul(ps[0:C, pk_, 0:256], lhsT=kn[:, c0:c0 + C], rhs=Sbf[:].rearrange("p a b -> p (a b)"), start=True, stop=True), reads=[("n", 1), "Sbf"], writes=psk(pk_))
                        yield
                        for e_ in range(2):
                            hv = hv0 + e_
                            S.dve(lambda e, e_=e_, hv=hv, pk_=pk_, C=C, tl=tl: e.scalar_tensor_tensor(out=xv[0:C, e_, :], in0=ps[0:C, pk_, e_ * 128:(e_ + 1) * 128], scalar=negeG_all[0:C, tl, hv:hv + 1], in1=vtok2[pb][0:C, e_, :], op0=ALU.mult, op1=ALU.add), reads=psk(pk_) + [("vtok", pb)], writes=["xv"])
                            yield
                else:
                    S.dve(lambda e: e.tensor_tensor(out=X[:, :, :], in0=knf[:, :].unsqueeze(1).to_broadcast([128, NSEQ, TS]), in1=segmask[:, :, :], op=ALU.mult), reads=[("nf", 1), "segmask"], writes=["X"])
                    yield
                    pks = [ps_next(), ps_next()]
                    K.P.reserved = set(pks)
                    def _ldA(n):
                        k_ = n % 8
                        S.dma("sp", "gsin%d" % k_, lambda e, k_=k_, s2=n % NSEQ, hv2=hv0 + n // NSEQ: e.dma_start(out=SinR[k_], in_=K.s_gdn[s2, hv2]), writes=[("SinR", k_)])
                    for n_ in range(RDEPTH):
                        _ldA(n_)
                        yield
                    for e_ in range(2):
                        hv = hv0 + e_
                        for s_ in range(NSEQ):
                            n_ = e_ * NSEQ + s_
                            k_ = n_ % 8
                            S.pe(lambda e, e_=e_, s_=s_, k_=k_, pks=pks: e.matmul(ps[0:TS, pks[e_], 0:128], lhsT=X[:, s_, :], rhs=SinR[k_], start=(s_ == 0), stop=(s_ == NSEQ - 1)), reads=["X", ("SinR", k_)], writes=psk(pks[e_]))
                            yield
                            if n_ + RDEPTH < NLD:
                                _ldA(n_ + RDEPTH)
                                yield
                        S.dve(lambda e, e_=e_, hv=hv, tl=tl, pks=pks: e.scalar_tensor_tensor(out=xv[0:TS, e_, :], in0=ps[0:TS, pks[e_], 0:128], scalar=negeG_all[0:TS, tl, hv:hv + 1], in1=vtok2[pb][0:TS, e_, :], op0=ALU.mult, op1=ALU.add), reads=psk(pks[e_]) + [("vtok", pb)], writes=["xv"])
                        yield
                    K.P.reserved = set()
                pv = ps_next()
                for e_ in range(2):
                    S.pe(lambda e, e_=e_, pv=pv, C=C, Nf=Nf: e.matmul(ps[0:C, pv, e_ * 128:(e_ + 1) * 128], lhsT=Nf[0:C, e_, 0:C], rhs=xv[0:C, e_, :], start=True, stop=True), reads=[Nkey, "xv"], writes=psk(pv))
                    yield
                for e_ in range(2):
                    hv = hv0 + e_
                    S.act(lambda e, e_=e_, hv=hv, pv=pv, C=C, tl=tl: e.activation(out=vnew[0:C, e_, :], in_=ps[0:C, pv, e_ * 128:(e_ + 1) * 128], func=AF.Identity, scale=beta_all[0:C, tl, hv:hv + 1]), reads=psk(pv), writes=[("vnew", e_)])
                    yield
                if not isS:
                    po = ps_next()
                    po_aps = [ps[0:C, po, 0:128], ps[0:C, po, 128:256]]
                    po_keys = [psk(po), psk(po)]
                    for e_ in range(2):
                        if not first:
                            S.pe(lambda e, e_=e_, C=C, po_aps=po_aps: e.matmul(po_aps[e_], lhsT=qg2[pb][:, e_, 0:C], rhs=Sbf[:, e_, :], start=True, stop=False), reads=[("qg", e_, pb), "Sbf"], writes=po_keys[e_])
                            yield
                        S.pe(lambda e, e_=e_, C=C, po_aps=po_aps, first=first: e.matmul(po_aps[e_], lhsT=attnT2[pb][0:C, e_, 0:C], rhs=vnew[0:C, e_, :], start=first, stop=True), reads=[("attnT", e_, pb), ("vnew", e_)], writes=po_keys[e_])
                        yield
                    pS_ = ps_next()
                    for e_ in range(2):
                        S.pe(lambda e, e_=e_, pS_=pS_, C=C: e.matmul(ps[:, pS_, e_ * 128:(e_ + 1) * 128], lhsT=kd2[pb][0:C, e_, :], rhs=vnew[0:C, e_, :], start=True, stop=True), reads=[("kd", e_, pb), ("vnew", e_)], writes=psk(pS_))
                        yield
                    for e_ in range(2):
                        hv = hv0 + e_
                        if first:
                            S.dve(lambda e, e_=e_, pS_=pS_: e.tensor_copy(out=S32[:, e_, :], in_=ps[:, pS_, e_ * 128:(e_ + 1) * 128]), reads=psk(pS_), writes=[("S32", e_)])
                            yield
                        else:
                            S.dve(lambda e, e_=e_, pS_=pS_, C=C: e.scalar_tensor_tensor(out=S32[:, e_, :], in0=S32[:, e_, :], scalar=eGl[pb][:, e_:e_ + 1], in1=ps[:, pS_, e_ * 128:(e_ + 1) * 128], op0=ALU.mult, op1=ALU.add), reads=psk(pS_) + [("S32", e_), ("eGl", pb)], writes=[("S32", e_)])
                            yield
                        if last:
                            S.dma("sp", "gpo", lambda e, e_=e_, hv=hv: e.dma_start(out=K.gdn_p[hv], in_=S32[:, e_, :]), reads=[("S32", e_)])
                            yield
                    if not last:
                        S.act(lambda e: e.copy(out=Sbf[:].rearrange("p a b -> p (a b)"), in_=S32[:].rearrange("p a b -> p (a b)")), reads=[("S32", 0), ("S32", 1)], writes=["Sbf"])
                        yield
                else:
                    pos = [ps_next(), ps_next()]
                    K.P.reserved = set(pos)
                    po_aps = [ps[0:TS, pos[0], 0:128], ps[0:TS, pos[1], 0:128]]
                    po_keys = [psk(pos[0]), psk(pos[1])]
                    def _ldB(n):
                        k_ = n % 8
                        S.dma("sp", "gsin%d" % k_, lambda e, k_=k_, s2=n % NSEQ, hv2=hv0 + n // NSEQ: e.dma_start(out=SinR[k_], in_=K.s_gdn[s2, hv2]), writes=[("SinR", k_)])
                    for n_ in range(RDEPTH):
                        _ldB(n_)
                        yield
                    for e_ in range(2):
                        hv = hv0 + e_
                        S.pe(lambda e, e_=e_, po_aps=po_aps: e.matmul(po_aps[e_], lhsT=attnT2[pb][0:TS, e_, 0:TS], rhs=vnew[0:TS, e_, :], start=True, stop=False), reads=[("attnT", e_, pb), ("vnew", e_)], writes=po_keys[e_])
                        yield
                        S.pool(lambda e, e_=e_: e.tensor_tensor(out=qgf[:, :], in0=qnf[:, :], in1=E2[:, e_, 0:TS], op=ALU.mult), reads=[("nf", 0), "E2"], writes=["qgf"])
                        yield
                        S.dve(lambda e: e.tensor_tensor(out=X[:, :, :], in0=qgf[:, :].unsqueeze(1).to_broadcast([128, NSEQ, TS]), in1=segmask[:, :, :], op=ALU.mult), reads=["qgf", "segmask"], writes=["X"])
                        yield
                        for s_ in range(NSEQ):
                            n_ = e_ * NSEQ + s_
                            k_ = n_ % 8
                            j_ = n_ % 3
                            S.pe(lambda e, e_=e_, s_=s_, k_=k_, po_aps=po_aps: e.matmul(po_aps[e_], lhsT=X[:, s_, :], rhs=SinR[k_], start=False, stop=(s_ == NSEQ - 1)), reads=["X", ("SinR", k_)], writes=po_keys[e_])
                            yield
                            S.dve(lambda e, e_=e_, s_=s_: e.tensor_scalar(out=kdm[0:TS, :], in0=kd2[pb][0:TS, e_, :], scalar1=segcol[0:TS, s_:s_ + 1], scalar2=None, op0=ALU.mult), reads=[("kd", e_, pb), "segcol"], writes=["kdm"])
                            yield
                            pS_ = ps_next()
                            S.pe(lambda e, e_=e_, pS_=pS_: e.matmul(ps[:, pS_, 0:128], lhsT=kdm[0:TS, :], rhs=vnew[0:TS, e_, :], start=True, stop=True), reads=["kdm", ("vnew", e_)], writes=psk(pS_))
                            yield
                            S.dve(lambda e, e_=e_, s_=s_, k_=k_, j_=j_, pS_=pS_: e.scalar_tensor_tensor(out=SoutR[j_], in0=SinR[k_], scalar=E2[:, e_, 4 * s_ + 3:4 * s_ + 4], in1=ps[:, pS_, 0:128], op0=ALU.mult, op1=ALU.add), reads=psk(pS_) + [("SinR", k_), "E2"], writes=[("SoutR", j_)])
                            yield
                            if n_ + RDEPTH < NLD:
                                _ldB(n_ + RDEPTH)
                                yield
                            S.dma("sp", "gsout%d" % j_, lambda e, s_=s_, hv=hv, j_=j_: e.dma_start(out=K.gdn_s[s_, hv], in_=SoutR[j_]), reads=[("SoutR", j_)])
                            yield
                    K.P.reserved = set()
                for e_ in range(2):
                    S.act(lambda e, e_=e_, C=C, po_aps=po_aps: e.activation(out=acc[0:C, e_ * 128:(e_ + 1) * 128], in_=po_aps[e_], func=AF.Square, accum_out=ss[0:C, e_:e_ + 1]), reads=po_keys[e_], writes=[("acc", 0), ("ss", e_)])
                    yield
                S.act(lambda e, C=C: e.activation(out=ss[0:C, 2:4], in_=ss[0:C, 0:2], func=AF.Sqrt, bias=EPS, scale=1.0 / 128), reads=[("ss", 0), ("ss", 1)], writes=["ss2"])
                yield
                S.dve(lambda e, C=C: e.reciprocal(out=ss[0:C, 2:4], in_=ss[0:C, 2:4]), reads=["ss2"], writes=["ss2"])
                yield
                for e_ in range(2):
                    S.dve(lambda e, e_=e_, C=C, po_aps=po_aps: e.scalar_tensor_tensor(out=og[0:C, e_ * 128:(e_ + 1) * 128], in0=po_aps[e_], scalar=ss[0:C, 2 + e_:3 + e_], in1=szn2[pb][0:C, e_ * 128:(e_ + 1) * 128], op0=ALU.mult, op1=ALU.mult), reads=po_keys[e_] + ["ss2", ("szn", pb)], writes=[("og", e_)])
                    yield
                bt = ps_next()
                ptb = ps[:, bt, :].bitcast(BF16)
                for e_ in range(2):
                    S.pe(lambda e, e_=e_, ptb=ptb, C=C: e.transpose(ptb[:, e_ * C:(e_ + 1) * C], og[0:C, e_ * 128:(e_ + 1) * 128], ident_b[0:C, 0:C]), reads=[("og", e_), "ident_b"], writes=psk(bt))
                    yield
                S.act(lambda e, ptb=ptb, C=C, c0=c0: e.copy(out=ogT[:, :, c0:c0 + C], in_=ptb[:, 0:2 * C].rearrange("p (a b) -> p a b", a=2)), reads=psk(bt), writes=["ogT"])
                yield
            chunks_ = list(range(0, nt, C))
            gens = [chunk_gen(c0_, i_ % 2) for i_, c0_ in enumerate(chunks_)]
            PIPE = (not isS) and os.environ.get("GDN_PIPE", "1") == "1"

            def _adv_split(g):
                for x_ in g:
                    if x_ == "SPLIT":
                        return
            if not PIPE:
                for g in gens:
                    for _ in g:
                        pass
            else:
                _adv_split(gens[0])
                for i_ in range(len(gens)):
                    nxt = gens[i_ + 1] if i_ + 1 < len(gens) else None
                    a_done = False
                    b_done = nxt is None
                    while not (a_done and b_done):
                        if not a_done:
                            try:
                                next(gens[i_])
                            except StopIteration:
                                a_done = True
                        if not b_done:
                            try:
                                if next(nxt) == "SPLIT":
                                    b_done = True
                            except StopIteration:
                                b_done = True
            for dc8 in range(KC):
                if hk >= 1 and 'F' in CUT:
                    continue
                b = ps_next()
                for e_ in range(2):
                    S.pe(lambda e, b=b, e_=e_, dc8=dc8, nt=nt: e.matmul(ps[:, b, 0:nt], lhsT=wo[:, e_, dc8 * 128:(dc8 + 1) * 128], rhs=ogT[:, e_, 0:nt], start=(e_ == 0), stop=(e_ == 1)), reads=[wok, "ogT"], writes=psk(b))
                K.resid_update(ps[:, b, 0:nt], gbase, dc8, ("S" if isS else bi), t0, nt, psk(b))
        if hk >= 1 and 'G' in CUT:
            S.barrier(); continue
        bo1 = ps_next()
        for ci in range(4):
            S.pe(lambda e, ci=ci, bo1=bo1: e.transpose(ps[0:3, bo1, ci * 128:(ci + 1) * 128], tailP[:, ci, :], ident_f[:]), reads=["tailP", "ident_f"], writes=psk(bo1))
        S.dve(lambda e, bo1=bo1: e.tensor_copy(out=ostg[0:3, :], in_=ps[0:3, bo1, :]), reads=psk(bo1), writes=[("acc", 0)])
        for ci, cid in enumerate(cids):
            S.dma("sp", "gco", lambda e, ci=ci, cid=cid: e.dma_start(out=K.gconv_p[:, cid * 128:(cid + 1) * 128], in_=ostg[0:3, ci * 128:(ci + 1) * 128]), reads=[("acc", 0)])
        bo2 = ps_next()
        for ci in range(4):
            S.pe(lambda e, ci=ci, bo2=bo2: e.transpose(ps[0:48, bo2, ci * 128:(ci + 1) * 128], tailS[:, ci, :, :].rearrange("p s r -> p (s r)"), ident_f[:]), reads=["tailS", "ident_f"], writes=psk(bo2))
        S.dve(lambda e, bo2=bo2: e.tensor_copy(out=ostg[0:48, :], in_=ps[0:48, bo2, :]), reads=psk(bo2), writes=[("acc", 0)])
        for ci, cid in enumerate(cids):
            S.dma("sp", "gco", lambda e, ci=ci, cid=cid: e.dma_start(out=K.gconv_s[:, cid * 128:(cid + 1) * 128], in_=ostg[0:48, ci * 128:(ci + 1) * 128]), reads=[("acc", 0)])
        S.barrier()
    S.barrier()

_CACHE = {}


def make_in_maps(inp):
    f = lambda a: np.ascontiguousarray(np.asarray(a, dtype=np.float32))
    consts = host_consts()
    shared = {
        "w_ada": f(inp["w_ada"]), "b_ada": f(inp["b_ada"]),
        "w_ada_final": f(inp["w_ada_final"]), "b_ada_final": f(inp["b_ada_final"]).reshape(1, -1),
        "w_ret_in": f(inp["w_ret_in"][0]), "w_ret_out": f(inp["w_ret_out"][0]),
        "w_gdn_in": f(inp["w_gdn_in"][0]), "w_gdn_out": f(inp["w_gdn_out"][0]),
        "w_gdn_conv": f(inp["w_gdn_conv"][0]),
        "gdn_vec": f(np.concatenate([inp["gdn_a_log"][0], inp["gdn_dt_bias"][0]])).reshape(1, 32),
        "gdn_norm": f(inp["gdn_norm"]).reshape(1, 128),
        "w_ffn_up": f(inp["w_ffn_up"]), "w_ffn_down": f(inp["w_ffn_down"]),
        "ffn_vec": f(np.concatenate([np.asarray(inp["w_ffn_dw"]).reshape(6, DFF), np.asarray(inp["b_ffn_dw"]).reshape(2, DFF)], axis=0)),
    }
    shared.update(consts)
    maps = []
    for c in range(NCORES):
        sl = slice(NSEQ * c, NSEQ * (c + 1))
        m = dict(shared)
        m["xp"] = f(inp["x_prompt"][c])
        m["xs"] = f(np.asarray(inp["x_sample"][sl]).reshape(TS, D))
        m["s_ret"] = f(inp["state_ret"][0, sl])
        m["s_gdn"] = f(inp["state_gdn"][0, sl])
        m["s_gconv"] = f(np.asarray(inp["state_gdn_conv"][0, sl]).reshape(NSEQ * 3, 4096))
        m["s_fconv"] = f(np.asarray(inp["state_ffn_conv"][:, sl]).reshape(2, NSEQ * 2, DFF))
        m["vec22"] = f(np.concatenate([np.asarray(inp["c_prompt"][c:c + 1]), np.asarray(inp["c_sample"][sl]),
                                        np.asarray(inp["norm_mix"]), np.asarray(inp["norm_ffn"]), np.asarray(inp["norm_final"]).reshape(1, D)], axis=0))
        maps.append(m)
    return maps


def kernel(**inp):
    if "nc" not in _CACHE:
        _CACHE["nc"] = build_program()
    nc = _CACHE["nc"]
    maps = make_in_maps(inp)
    res = run_bass_kernel_spmd(nc, maps, core_ids=list(range(NCORES)))
    R = res.results
    cat = lambda k: np.stack([np.asarray(R[c][k]) for c in range(NCORES)], axis=0)
    y_prompt = cat("y_p")
    y_sample = cat("y_s").reshape(128, 4, D)
    ret_p = cat("ret_p")[None]
    gdn_p = cat("gdn_p")[None]
    gconv_p = cat("gconv_p")[None]
    fconv_p = np.transpose(cat("fconv_p"), (1, 0, 2, 3))
    ret_s = cat("ret_s").reshape(1, 128, RET_H, 256, 512)
    gdn_s = cat("gdn_s").reshape(1, 128, GDN_HV, 128, 128)
    gconv_s = cat("gconv_s").reshape(1, 128, 3, 4096)
    fconv_s = np.transpose(cat("fconv_s").reshape(NCORES, 2, NSEQ, 2, DFF), (1, 0, 2, 3, 4)).reshape(2, 128, 2, DFF)
    return (y_prompt.astype(np.float32), y_sample.astype(np.float32), ret_p.astype(np.float32), gdn_p.astype(np.float32),
            gconv_p.astype(np.float32), fconv_p.astype(np.float32), ret_s.astype(np.float32), gdn_s.astype(np.float32),
            gconv_s.astype(np.float32), fconv_s.astype(np.float32))
```

```python
import math
import bisect
import numpy as np
from contextlib import ExitStack
import concourse.bass as bass
import concourse.mybir as mybir
from concourse.bass_utils import run_bass_kernel_spmd

F32 = mybir.dt.float32
BF16 = mybir.dt.bfloat16
AF = mybir.ActivationFunctionType
ALU = mybir.AluOpType

NCORES = 8
D = 1024
KC = 8
TP = 2048
NSEQ = 16
TS = 64
TT = TP + TS
DFF = 2816
NF = 22
EPS = 1e-6
RET_H = 4
GDN_HV = 16
GDN_HK = 8
PAST = 16384
SAME_ENGINE_SYNC = True
ATTACH_WAITS = True
DEBUG = {}


class _Op:
    __slots__ = ("eng", "fn", "deps", "dma_key", "sig", "cnt", "idx")

    def __init__(self, eng, fn, deps, dma_key, idx):
        self.eng = eng
        self.fn = fn
        self.deps = deps
        self.dma_key = dma_key
        self.sig = False
        self.cnt = 0
        self.idx = idx


class Sched:
    ENGS = ("pe", "act", "dve", "pool", "sp")

    def __init__(self, nc):
        self.nc = nc
        self.ops = []
        self.last_w = {}
        self.readers = {}
        self.last_eng = {}
        self.dmas_since_bar = []

    def op(self, eng, fn, reads=(), writes=(), dma_key=None, extra=()):
        deps = set(extra)
        for r in reads:
            w = self.last_w.get(r)
            if w is not None:
                deps.add(w)
        for w_ in writes:
            w = self.last_w.get(w_)
            if w is not None:
                deps.add(w)
            deps |= self.readers.get(w_, set())
        idx = len(self.ops)
        deps.discard(idx)
        self.ops.append(_Op(eng, fn, deps, dma_key, idx))
        for r in reads:
            self.readers.setdefault(r, set()).add(idx)
        for w_ in writes:
            self.last_w[w_] = idx
            self.readers[w_] = set()
        self.last_eng[eng] = idx
        if dma_key is not None:
            self.dmas_since_bar.append(idx)
        return idx

    def pe(self, fn, reads=(), writes=()):
        return self.op("pe", fn, reads, writes)

    def act(self, fn, reads=(), writes=()):
        return self.op("act", fn, reads, writes)

    def dve(self, fn, reads=(), writes=()):
        return self.op("dve", fn, reads, writes)

    def pool(self, fn, reads=(), writes=()):
        return self.op("pool", fn, reads, writes)

    def dma(self, q, key, fn, reads=(), writes=()):
        return self.op(q, fn, reads, writes, dma_key=key)

    def barrier(self):
        deps = set(self.last_eng.values()) | set(self.dmas_since_bar)
        self.dmas_since_bar = []
        for e in self.ENGS:
            self.op(e, None, extra=deps)
        self.last_w = {}
        self.readers = {}

    def emit(self, stack):
        nc = self.nc
        ops = self.ops

        def needs(c, p):
            if p.fn is None:
                return False
            if p.dma_key is not None:
                return True
            if p.eng == c.eng:
                if p.eng == "pe":
                    return False
                return SAME_ENGINE_SYNC
            return True

        for c in ops:
            for d in c.deps:
                p = ops[d]
                if needs(c, p):
                    p.sig = True
        eng_cnt = {e: 0 for e in self.ENGS}
        dma_cnt = {}
        dma_keys = []
        dma_issue_idx = {}
        for o in ops:
            if o.dma_key is not None:
                if o.dma_key not in dma_cnt:
                    dma_cnt[o.dma_key] = 0
                    dma_keys.append(o.dma_key)
                    dma_issue_idx[o.dma_key] = []
                dma_cnt[o.dma_key] += 1
                dma_issue_idx[o.dma_key].append(o.idx)
            elif o.sig:
                eng_cnt[o.eng] += 1
                o.cnt = eng_cnt[o.eng]
        sems = {}
        for e in self.ENGS:
            sems[("e", e)] = stack.enter_context(nc.semaphore("s_" + e))
        for k in dma_keys:
            sems[("d", k)] = stack.enter_context(nc.semaphore("d_" + str(k)))
        per_eng = {e: [o for o in ops if o.eng == e] for e in self.ENGS}
        block = stack.enter_context(nc.Block())

        def run_engine(ename, eobj):
            waited = {}
            for o in per_eng[ename]:
                need = {}
                for d in o.deps:
                    p = ops[d]
                    if not needs(o, p):
                        continue
                    if p.dma_key is not None:
                        key = ("d", p.dma_key)
                        val = 16 * bisect.bisect_left(dma_issue_idx[p.dma_key], o.idx)
                    else:
                        key = ("e", p.eng)
                        val = p.cnt
                    if need.get(key, 0) < val:
                        need[key] = val
                pend = [(key, val) for key, val in need.items() if waited.get(key, 0) < val]
                for key, val in pend:
                    waited[key] = val
                fuse = (ATTACH_WAITS and o.fn is not None and o.dma_key is None and ename in ("act", "dve", "pool")
                        and not getattr(o.fn, "_multi", False) and len(pend) > 0)
                for key, val in (pend[:-1] if fuse else pend):
                    eobj.wait_ge(sems[key], val)
                if o.fn is None:
                    continue
                ins = o.fn(eobj)
                if fuse:
                    ins._wait_ge(sems[pend[-1][0]], pend[-1][1])
                if o.dma_key is not None:
                    ins.then_inc(sems[("d", o.dma_key)], 16)
                elif o.sig:
                    ins.then_inc(sems[("e", ename)], 1)
            for k in dma_keys:
                if any(o.dma_key == k for o in per_eng[ename]):
                    eobj.wait_ge(sems[("d", k)], 16 * dma_cnt[k])

        @block.tensor
        def _(e):
            run_engine("pe", e)

        @block.scalar
        def _(e):
            run_engine("act", e)

        @block.vector
        def _(e):
            run_engine("dve", e)

        @block.gpsimd
        def _(e):
            run_engine("pool", e)

        @block.sync
        def _(e):
            run_engine("sp", e)


def _gammas():
    return (1.0 - 2.0 ** (-5.0 - np.arange(RET_H, dtype=np.float64)))


def host_consts():
    c = {}
    c["c_ident"] = np.eye(128, dtype=np.float32)
    half = 128
    inv_freq = (np.float32(10000.0) ** (-(np.arange(half, dtype=np.float32)) / np.float32(half))).astype(np.float32)
    pos = np.concatenate([np.arange(TP, dtype=np.float32), (PAST + (np.arange(TS) % 4)).astype(np.float32)])
    ang = (pos[None, :] * inv_freq[:, None]).astype(np.float32)
    cos = np.cos(ang.astype(np.float64))
    sin = np.sin(ang.astype(np.float64))
    g = _gammas()
    pin = np.concatenate([np.arange(TP) % 128, np.arange(TS) % 4]).astype(np.float64)
    rope = np.zeros((RET_H + 1, 2, 128, TT), np.float32)
    for h in range(RET_H):
        dec = g[h] ** (pin + 1.0)
        rope[h, 0] = cos * dec[None, :]
        rope[h, 1] = sin * dec[None, :]
    rope[RET_H, 0] = cos * (256.0 ** -0.5)
    rope[RET_H, 1] = sin * (256.0 ** -0.5)
    c["rope"] = rope
    mP = np.zeros((RET_H, 128, 128), np.float32)
    mS = np.zeros((RET_H, 128, 128), np.float32)
    ks = np.zeros((128, 2 * RET_H), np.float32)
    jj = np.arange(128)
    for h in range(RET_H):
        mP[h] = np.where(jj[None, :] >= jj[:, None], g[h] ** (-(jj[:, None] + 1.0)), 0.0)
        j4 = jj[:64] % 4
        same = (jj[:64, None] // 4) == (jj[None, :64] // 4)
        mS[h, :64, :64] = np.where(same & (jj[None, :64] >= jj[:64, None]), g[h] ** (-(j4[:, None] + 1.0)), 0.0)
        ks[:, h] = g[h] ** (127.0 - jj)
        ks[:64, 4 + h] = g[h] ** (3.0 - j4)
    c["retmask"] = np.concatenate([mP, mS], axis=0)
    c["kscale"] = ks
    seg = np.zeros((128, NSEQ, 64), np.float32)
    for s in range(NSEQ):
        seg[:, s, 4 * s:4 * s + 4] = 1.0
    c["segmask"] = seg.reshape(128, NSEQ * 64)
    segcol = np.zeros((128, NSEQ), np.float32)
    for s in range(NSEQ):
        segcol[4 * s:4 * s + 4, s] = 1.0
    c["segcol"] = segcol
    gm = np.zeros((8, 128, 128), np.float32)
    a = np.arange(128)
    gm[0] = (a[:, None] <= a[None, :])
    gm[1] = (a[:, None] > a[None, :])
    gm[2] = (a[None, :] >= a[:, None])
    gm[3] = (a[None, :] > a[:, None])
    b = np.arange(64)
    same = (b[:, None] // 4) == (b[None, :] // 4)
    gm[4, :64, :64] = (b[:, None] <= b[None, :]) & same
    gm[5, :64, :64] = (b[:, None] > b[None, :]) & same
    gm[6, :64, :64] = (b[None, :] >= b[:, None]) & same
    gm[7, :64, :64] = (b[None, :] > b[:, None]) & same
    c["gmask"] = np.ascontiguousarray(gm.transpose(1, 0, 2)).reshape(128, 8 * 128)
    bmk = np.zeros((4, 128, 128), np.float32)
    bmk[0] = (a[:, None] // 16) == (a[None, :] // 16)
    for mi, m in enumerate([32, 64, 128]):
        bmk[mi + 1] = ((a[:, None] // m) == (a[None, :] // m)) & ((a[:, None] // (m // 2)) != (a[None, :] // (m // 2)))
    c["bmask"] = np.ascontiguousarray(bmk.transpose(1, 0, 2)).reshape(128, 4 * 128)
    return c


class Ctx:
    pass


def build_program(dbg=()):
    nc = bass.Bass("TRN2", target_bir_lowering=False)
    K = Ctx()
    K.nc = nc
    ins = {}

    def din(name, shape):
        ins[name] = nc.dram_tensor(name, list(shape), F32, kind="ExternalInput").ap()
        return ins[name]

    def dout(name, shape):
        return nc.dram_tensor(name, list(shape), F32, kind="ExternalOutput").ap()

    xp = din("xp", [TP, D]); xs = din("xs", [TS, D])
    s_ret = din("s_ret", [NSEQ, RET_H, 256, 512])
    s_gdn = din("s_gdn", [NSEQ, GDN_HV, 128, 128])
    s_gconv = din("s_gconv", [NSEQ * 3, 4096])
    s_fconv = din("s_fconv", [2, NSEQ * 2, DFF])
    vec22 = din("vec22", [22, D])
    w_ada = din("w_ada", [2, D, 6 * D]); b_ada = din("b_ada", [2, 6 * D])
    w_adaf = din("w_ada_final", [D, 2 * D]); b_adaf = din("b_ada_final", [1, 2 * D])
    w_ret_in = din("w_ret_in", [D, 6144]); w_ret_out = din("w_ret_out", [2048, D])
    w_gdn_in = din("w_gdn_in", [D, 6176]); w_gdn_out = din("w_gdn_out", [2048, D])
    w_gconv = din("w_gdn_conv", [4, 4096])
    gdn_vec = din("gdn_vec", [1, 32])
    gdn_norm = din("gdn_norm", [1, 128])
    w_up = din("w_ffn_up", [2, D, 2 * DFF]); w_down = din("w_ffn_down", [2, DFF, D])
    ffn_vec = din("ffn_vec", [8, DFF])
    c_ident = din("c_ident", [128, 128])
    c_rope = din("rope", [RET_H + 1, 2, 128, TT])
    c_retmask = din("retmask", [2 * RET_H, 128, 128])
    c_kscale = din("kscale", [128, 2 * RET_H])
    c_segmask = din("segmask", [128, NSEQ * 64])
    c_segcol = din("segcol", [128, NSEQ])
    c_gmask = din("gmask", [128, 8 * 128])
    c_bmask = din("bmask", [128, 4 * 128])

    y_p = dout("y_p", [TP, D]); y_s = dout("y_s", [TS, D])
    ret_p = dout("ret_p", [RET_H, 256, 512]); gdn_p = dout("gdn_p", [GDN_HV, 128, 128])
    gconv_p = dout("gconv_p", [3, 4096]); fconv_p = dout("fconv_p", [2, 2, DFF])
    ret_s = dout("ret_s", [NSEQ, RET_H, 256, 512]); gdn_s = dout("gdn_s", [NSEQ, GDN_HV, 128, 128])
    gconv_s = dout("gconv_s", [NSEQ * 3, 4096]); fconv_s = dout("fconv_s", [2, NSEQ * 2, DFF])
    dbg_out = {n: dout("dbg_" + n, shp) for n, shp in dbg}

    with ExitStack() as st:
        def T(name, shape, dt):
            return st.enter_context(nc.sbuf_tensor(name, list(shape), dt))

        S = Sched(nc)
        xT = T("xT", [128, KC, TT], F32)
        hT = T("hT", [128, KC, TT], BF16)
        wring = T("wring", [128, 4, 4096], BF16)
        modT = T("modT", [128, 112, 17], F32)
        vecT = T("vecT", [128, KC, 22], F32)
        Amod = T("Amod", [128, 5, KC, 17], F32)
        csT = T("csT", [128, KC, 17], F32)
        ident_f = T("ident_f", [128, 128], F32)
        ident_b = T("ident_b", [128, 128], BF16)
        ones_b = T("ones_b", [128, 128], BF16)
        ones_f = T("ones_f", [128, 32], F32)
        ffnvT = T("ffnvT", [128, NF, 8], F32)
        fcarry = T("fcarry", [128, NF, 2], F32)
        ARENA = 15000
        arena = T("arena", [128, ARENA], F32)
        ps = st.enter_context(nc.psum_tensor("ps", [128, 8, 512], F32))

        A = Ctx()
        A.off = 0

        def a_reset():
            A.off = 0

        def a_f32(*free, parts=128):
            n = int(np.prod(free))
            assert A.off + n <= ARENA, ("arena overflow", A.off, n)
            ap = arena[0:parts, A.off:A.off + n]
            A.off += n
            if len(free) == 2:
                ap = ap.rearrange("p (a b) -> p a b", a=free[0])
            elif len(free) == 3:
                ap = ap.rearrange("p (a b c) -> p a b c", a=free[0], b=free[1])
            return ap

        def a_bf16(*free, parts=128):
            n = int(np.prod(free))
            nf = (n + 1) // 2
            assert A.off + nf <= ARENA, ("arena overflow", A.off, nf)
            ap = arena[0:parts, A.off:A.off + nf].bitcast(BF16)[:, 0:n]
            A.off += nf
            if len(free) == 2:
                ap = ap.rearrange("p (a b) -> p a b", a=free[0])
            elif len(free) == 3:
                ap = ap.rearrange("p (a b c) -> p a b c", a=free[0], b=free[1])
            return ap

        P = Ctx()
        P.i = 0

        P.reserved = set()

        def ps_next(n=1):
            while True:
                if P.i + n > 8:
                    P.i = 0
                b = P.i
                P.i = (P.i + n) % 8
                if not any((b + i) in P.reserved for i in range(n)):
                    return b

        def psk(b, n=1):
            return [("ps", b + i) for i in range(n)]

        W = Ctx()
        W.i = 0

        def wslot():
            i = W.i
            W.i = (W.i + 1) % 4
            return i

        def load_w_bf16(src2d, kc, ncols, row0=0, col0=0, slot=None, off=0, key=None):
            i = wslot() if slot is None else slot
            view = wring[:, i, off:off + kc * ncols].rearrange("p (k c) -> p k c", k=kc)
            src = src2d[row0:row0 + kc * 128, col0:col0 + ncols].rearrange("(k p) c -> p k c", p=128)
            k_ = ("w", i) if key is None else key
            S.dma("pool", "w%d" % i, lambda e: e.dma_start(out=view, in_=src), writes=[k_])
            return view, k_

        def load_w_f32(src2d, kc, ncols, row0=0, col0=0):
            i = wslot()
            view = wring[:, i, :].bitcast(F32)[:, 0:kc * ncols].rearrange("p (k c) -> p k c", k=kc)
            src = src2d[row0:row0 + kc * 128, col0:col0 + ncols].rearrange("(k p) c -> p k c", p=128)
            S.dma("sp", "wf%d" % i, lambda e: e.dma_start(out=view, in_=src), writes=[("w", i)])
            return view, ("w", i)

        BLOCKS_P = [(0, 512), (512, 512), (1024, 512), (1536, 512)]
        BLOCK_S = (TP, TS)
        ALLBLOCKS = BLOCKS_P + [BLOCK_S]

        S.dma("sp", "c_id", lambda e: e.dma_start(out=ident_f[:], in_=c_ident), writes=["ident_f"])
        S.dve(lambda e: e.tensor_copy(out=ident_b[:], in_=ident_f[:]), reads=["ident_f"], writes=["ident_b"])
        S.dve(lambda e: e.memset(ones_b[:], 1.0), writes=["ones_b"])
        S.dve(lambda e: e.memset(ones_f[:], 1.0), writes=["ones_f"])
        a_reset()
        stage22 = a_f32(D, parts=22)
        S.dma("sp", "c_s22", lambda e: e.dma_start(out=stage22, in_=vec22), writes=["stage22"])
        b0 = ps_next()
        for kc in range(KC):
            S.pe(lambda e, kc=kc: e.transpose(ps[:, b0, kc * 22:(kc + 1) * 22], stage22[:, kc * 128:(kc + 1) * 128], ident_f[0:22, 0:22]),
                 reads=["stage22", "ident_f"], writes=psk(b0))
        S.dve(lambda e: e.tensor_copy(out=vecT[:].rearrange("p a b -> p (a b)"), in_=ps[:, b0, 0:KC * 22]), reads=psk(b0), writes=["vecT"])
        S.act(lambda e: e.activation(out=csT[:], in_=vecT[:, :, 0:17], func=AF.Silu), reads=["vecT"], writes=["csT"])
        stage8 = a_f32(DFF, parts=8)
        S.dma("sp", "c_s8", lambda e: e.dma_start(out=stage8, in_=ins["ffn_vec"]), writes=["stage8"])
        b1 = ps_next()
        for fc in range(NF):
            S.pe(lambda e, fc=fc: e.transpose(ps[:, b1, fc * 8:(fc + 1) * 8], stage8[:, fc * 128:(fc + 1) * 128], ident_f[0:8, 0:8]),
                 reads=["stage8", "ident_f"], writes=psk(b1))
        S.dve(lambda e: e.tensor_copy(out=ffnvT[:].rearrange("p a b -> p (a b)"), in_=ps[:, b1, 0:NF * 8]), reads=psk(b1), writes=["ffnvT"])

        browb = [a_bf16(512, parts=1) for _ in range(3)]
        mstage = [a_f32(512, parts=17) for _ in range(2)]
        csb = a_bf16(KC, 17)
        S.dve(lambda e: e.tensor_copy(out=csb, in_=csT[:]), reads=["csT"], writes=["csb"])
        bri = [0]

        def ada_layer(wsrc, bsrc, ncols, mod_base):
            for pcs in range(ncols // 512):
                wv, wk = load_w_bf16(wsrc, KC, 512, col0=pcs * 512)
                bi = bri[0] % 3
                m2 = bri[0] % 2
                bri[0] += 1
                S.dma("pool", "brb%d" % bi, lambda e, bi=bi, pcs=pcs: e.dma_start(out=browb[bi], in_=bsrc[0:1, pcs * 512:(pcs + 1) * 512]), writes=[("browb", bi)])
                bank = ps_next()
                for kc in range(KC):
                    S.pe(lambda e, bank=bank, kc=kc, wv=wv: e.matmul(ps[0:17, bank, :], lhsT=csb[:, kc, :], rhs=wv[:, kc, :], start=(kc == 0), stop=False), reads=[wk, "csb"], writes=psk(bank))
                S.pe(lambda e, bank=bank, bi=bi: e.matmul(ps[0:17, bank, :], lhsT=ones_b[0:1, 0:17], rhs=browb[bi][0:1, :], start=False, stop=True), reads=[("browb", bi), "ones_b"], writes=psk(bank))
                S.act(lambda e, bank=bank, m2=m2: e.copy(out=mstage[m2], in_=ps[0:17, bank, :]), reads=psk(bank), writes=[("mstage", m2)])
                bank2 = ps_next()
                for q in range(4):
                    S.pe(lambda e, bank2=bank2, q=q, m2=m2: e.transpose(ps[:, bank2, q * 17:(q + 1) * 17], mstage[m2][:, q * 128:(q + 1) * 128], ident_f[0:17, 0:17]), reads=[("mstage", m2), "ident_f"], writes=psk(bank2))
                m0 = mod_base + 4 * pcs
                S.dve(lambda e, bank2=bank2, m0=m0: e.tensor_copy(out=modT[:, m0:m0 + 4, :].rearrange("p a b -> p (a b)"), in_=ps[:, bank2, 0:68]), reads=psk(bank2), writes=["modT"])

        ada_layer(w_ada[0], b_ada[0:1, :], 6 * D, 0)
        ada_layer(w_ada[1], b_ada[1:2, :], 6 * D, 48)
        ada_layer(w_adaf, b_adaf, 2 * D, 96)
        norm_specs = [(0 * 48 + 8, 17), (0 * 48 + 32, 19), (1 * 48 + 8, 18), (1 * 48 + 32, 20), (96 + 8, 21)]
        for n, (scb, col) in enumerate(norm_specs):
            for kc in range(KC):
                S.dve(lambda e, n=n, kc=kc, scb=scb, col=col: e.tensor_scalar(out=Amod[:, n, kc, :], in0=modT[:, scb + kc, :], scalar1=1.0, scalar2=vecT[:, kc, col:col + 1], op0=ALU.add, op1=ALU.mult),
                      reads=["modT", "vecT"], writes=["Amod"])
        norm_shift = [0, 24, 48, 72, 96]
        S.barrier()

        a_reset()
        xst = [a_f32(D) for _ in range(2)]
        tiles = [(xp, t * 128, 128, t * 128) for t in range(16)] + [(xs, 0, TS, TP)]
        for ti, (src, r0, rows, t0) in enumerate(tiles):
            sb = ti % 2
            S.dma("sp", "xst%d" % sb, lambda e, sb=sb, src=src, r0=r0, rows=rows: e.dma_start(out=xst[sb][0:rows, :], in_=src[r0:r0 + rows, :]), writes=[("xst", sb)])
            for half in range(2):
                b = ps_next()
                for q in range(4):
                    kc = half * 4 + q
                    S.pe(lambda e, b=b, q=q, kc=kc, sb=sb, rows=rows: e.transpose(ps[:, b, q * 128:q * 128 + rows], xst[sb][0:rows, kc * 128:(kc + 1) * 128], ident_f[0:rows, 0:rows]),
                         reads=[("xst", sb), "ident_f"], writes=psk(b))
                src_ap = ps[:, b, :].rearrange("p (q c) -> p q c", q=4)[:, :, 0:rows]
                dst_ap = xT[:, half * 4:half * 4 + 4, t0:t0 + rows]
                if half == 0:
                    S.act(lambda e, dst_ap=dst_ap, src_ap=src_ap: e.copy(out=dst_ap, in_=src_ap), reads=psk(b), writes=[("xT", ti)])
                else:
                    S.dve(lambda e, dst_ap=dst_ap, src_ap=src_ap: e.tensor_copy(out=dst_ap, in_=src_ap), reads=psk(b), writes=[("xT", ti)])
        S.barrier()

        def do_norm(n, out_hT=True, out_f32=None):
            a_reset()
            sq = [a_bf16(KC, 512) for _ in range(2)]
            rstd = [a_f32(512) for _ in range(2)]
            tmp = [a_f32(KC, 512) for _ in range(2)]
            shb = norm_shift[n]
            for bi, (t0, nt) in enumerate(ALLBLOCKS):
                i2 = bi % 2
                S.act(lambda e, i2=i2, t0=t0, nt=nt: e.activation(out=sq[i2][:, :, 0:nt], in_=xT[:, :, t0:t0 + nt], func=AF.Square),
                      reads=[], writes=[("sq", i2)])
                b = ps_next()
                for kc in range(KC):
                    S.pe(lambda e, b=b, kc=kc, i2=i2, nt=nt: e.matmul(ps[:, b, 0:nt], lhsT=ones_b[:], rhs=sq[i2][:, kc, 0:nt], start=(kc == 0), stop=(kc == KC - 1)),
                         reads=[("sq", i2), "ones_b"], writes=psk(b))
                S.act(lambda e, b=b, i2=i2, nt=nt: e.activation(out=rstd[i2][:, 0:nt], in_=ps[:, b, 0:nt], func=AF.Sqrt, bias=EPS, scale=1.0 / D),
                      reads=psk(b), writes=[("rstd", i2)])
                S.dve(lambda e, i2=i2, nt=nt: e.reciprocal(out=rstd[i2][:, 0:nt], in_=rstd[i2][:, 0:nt]), reads=[("rstd", i2)], writes=[("rstd", i2)])
                S.dve(lambda e, i2=i2, t0=t0, nt=nt: e.tensor_tensor(out=tmp[i2][:, :, 0:nt], in0=xT[:, :, t0:t0 + nt], in1=rstd[i2][:, 0:nt].unsqueeze(1).to_broadcast([128, KC, nt]), op=ALU.mult),
                      reads=[("rstd", i2)], writes=[("tmp", i2)])
                for kc in range(KC):
                    dst = hT[:, kc, t0:t0 + nt] if out_f32 is None else out_f32(bi, kc)
                    if t0 < TP:
                        S.act(lambda e, dst=dst, i2=i2, kc=kc, nt=nt: e.activation(out=dst, in_=tmp[i2][:, kc, 0:nt], func=AF.Identity, scale=Amod[:, n, kc, 0:1], bias=modT[:, shb + kc, 0:1]),
                              reads=[("tmp", i2)], writes=[("h", bi, kc)])
                    else:
                        S.dve(lambda e, i2=i2, kc=kc: e.tensor_tensor(out=tmp[i2][:, kc, 0:TS].rearrange("p (s j) -> p s j", j=4), in0=tmp[i2][:, kc, 0:TS].rearrange("p (s j) -> p s j", j=4),
                                                                     in1=Amod[:, n, kc, 1:17].unsqueeze(2).to_broadcast([128, NSEQ, 4]), op=ALU.mult),
                              reads=[("tmp", i2)], writes=[("tmp", i2)])
                        S.dve(lambda e, dst=dst, i2=i2, kc=kc: e.tensor_tensor(out=dst.rearrange("p (s j) -> p s j", j=4), in0=tmp[i2][:, kc, 0:TS].rearrange("p (s j) -> p s j", j=4),
                                                                              in1=modT[:, shb + kc, 1:17].unsqueeze(2).to_broadcast([128, NSEQ, 4]), op=ALU.add),
                              reads=[("tmp", i2)], writes=[("h", bi, kc)])
            S.barrier()

        def gate_ap(gbase, kc, bi, nt):
            if bi != "S":
                return modT[:, gbase + kc, 0:1].to_broadcast([128, nt])
            return modT[:, gbase + kc, 1:17].unsqueeze(2).to_broadcast([128, NSEQ, 4])

        def resid_update(psrc, gbase, kc, bi, t0, nt, reads):
            if t0 < TP:
                S.dve(lambda e: e.scalar_tensor_tensor(out=xT[:, kc, t0:t0 + nt], in0=psrc, scalar=modT[:, gbase + kc, 0:1], in1=xT[:, kc, t0:t0 + nt], op0=ALU.mult, op1=ALU.add),
                      reads=list(reads) + [("x", kc, t0)], writes=[("x", kc, t0)])
            else:
                tmpg = K.tmpg
                S.dve(lambda e: e.tensor_tensor(out=tmpg.rearrange("p (s j) -> p s j", j=4), in0=psrc.rearrange("p (s j) -> p s j", j=4), in1=gate_ap(gbase, kc, "S", nt), op=ALU.mult),
                      reads=list(reads), writes=["tmpg"])
                S.dve(lambda e: e.tensor_tensor(out=xT[:, kc, t0:t0 + nt], in0=xT[:, kc, t0:t0 + nt], in1=tmpg, op=ALU.add),
                      reads=["tmpg", ("x", kc, t0)], writes=[("x", kc, t0)])

        def do_ffn(l):
            a_reset()
            gbase = l * 48 + 40
            K.tmpg = a_f32(TS)
            actb = a_bf16(NF, 704)
            gx = [a_f32(2 + 512) for _ in range(2)]
            gxs = a_f32(NSEQ, 6)
            cv = [a_f32(512) for _ in range(2)]
            tailP = a_f32(NF, 2)
            tailS = a_f32(NF, NSEQ, 2)
            sstT = a_f32(NF, 32)
            sstg = [a_f32(512, parts=32) for _ in range(2)]
            ostg = [a_f32(512, parts=32) for _ in range(2)]
            ostgP = [a_f32(512, parts=2) for _ in range(2)]
            for gi, g4 in enumerate(range(0, NF, 4)):
                b = ps_next()
                n = min(4, NF - g4)
                s2 = gi % 2
                S.dma("sp", "fst%d" % s2, lambda e, s2=s2, g4=g4, n=n: e.dma_start(out=sstg[s2][:, 0:n * 128], in_=s_fconv[l][:, g4 * 128:(g4 + n) * 128]), writes=[("sstg", s2)])
                for q in range(n):
                    S.pe(lambda e, b=b, q=q, s2=s2: e.transpose(ps[:, b, q * 32:(q + 1) * 32], sstg[s2][:, q * 128:(q + 1) * 128], ident_f[0:32, 0:32]),
                         reads=[("sstg", s2), "ident_f"], writes=psk(b))
                S.dve(lambda e, b=b, n=n, g4=g4: e.tensor_copy(out=sstT[:, g4:g4 + n, :].rearrange("p a b -> p (a b)"), in_=ps[:, b, 0:n * 32]), reads=psk(b), writes=["sstT"])
            S.dve(lambda e: e.memset(fcarry[:], 0.0), writes=[("fcarry", fc_) for fc_ in range(NF)])
            wcol = l * 3
            passes = [[(0, 0, 352, 0), (1, 352, 352, 352)], [(2, 704, 352, 0), (3, 1056, 352, 352)], [(4, 1408, 320, 0), (5, 1728, 320, 320), (6, TP, TS, 640)]]
            for pi, blks in enumerate(passes):
                for f0 in range(0, NF, 4):
                    nf = min(4, NF - f0)
                    wg, wgk = load_w_bf16(w_up[l], KC, nf * 128, col0=f0 * 128)
                    wv, wvk = load_w_bf16(w_up[l], KC, nf * 128, col0=DFF + f0 * 128)
                    for q in range(nf):
                        fc = f0 + q
                        for (bi, t0, nt, a0) in blks:
                            bg = ps_next()
                            for kc in range(KC):
                                S.pe(lambda e, bg=bg, kc=kc, q=q, t0=t0, nt=nt, wg=wg: e.matmul(ps[:, bg, 0:nt], lhsT=wg[:, kc, q * 128:(q + 1) * 128], rhs=hT[:, kc, t0:t0 + nt], start=(kc == 0), stop=(kc == KC - 1)),
                                     reads=[wgk], writes=psk(bg))
                            bv = ps_next()
                            for kc in range(KC):
                                S.pe(lambda e, bv=bv, kc=kc, q=q, t0=t0, nt=nt, wv=wv: e.matmul(ps[:, bv, 0:nt], lhsT=wv[:, kc, q * 128:(q + 1) * 128], rhs=hT[:, kc, t0:t0 + nt], start=(kc == 0), stop=(kc == KC - 1)),
                                     reads=[wvk], writes=psk(bv))
                            w0 = ffnvT[:, fc, wcol + 0:wcol + 1]
                            w1 = ffnvT[:, fc, wcol + 1:wcol + 2]
                            w2 = ffnvT[:, fc, wcol + 2:wcol + 3]
                            bb = ffnvT[:, fc, 6 + l:7 + l]
                            if bi < 6:
                                i2 = bi % 2
                                G = gx[i2]
                                c_ = cv[i2]
                                S.dve(lambda e, G=G, fc=fc: e.tensor_copy(out=G[:, 0:2], in_=fcarry[:, fc, :]), reads=[("fcarry", fc)], writes=[("gx", i2)])
                                S.act(lambda e, G=G, bg=bg, nt=nt: e.copy(out=G[:, 2:2 + nt], in_=ps[:, bg, 0:nt]), reads=psk(bg), writes=[("gx", i2)])
                                S.dve(lambda e, G=G, fc=fc, nt=nt: e.tensor_copy(out=fcarry[:, fc, :], in_=G[:, nt:nt + 2]), reads=[("gx", i2)], writes=[("fcarry", fc)])
                                if bi == 5:
                                    S.dve(lambda e, G=G, fc=fc, nt=nt: e.tensor_copy(out=tailP[:, fc, :], in_=G[:, nt:nt + 2]), reads=[("gx", i2)], writes=["tailP"])
                                S.act(lambda e, G=G, c_=c_, nt=nt, w2=w2: e.activation(out=c_[:, 0:nt], in_=G[:, 2:2 + nt], func=AF.Identity, scale=w2), reads=[("gx", i2), "ffnvT"], writes=[("cv", i2)])
                                S.dve(lambda e, G=G, c_=c_, nt=nt, w1=w1: e.scalar_tensor_tensor(out=c_[:, 0:nt], in0=G[:, 1:1 + nt], scalar=w1, in1=c_[:, 0:nt], op0=ALU.mult, op1=ALU.add), reads=[("gx", i2), ("cv", i2)], writes=[("cv", i2)])
                                S.dve(lambda e, G=G, c_=c_, nt=nt, w0=w0: e.scalar_tensor_tensor(out=c_[:, 0:nt], in0=G[:, 0:nt], scalar=w0, in1=c_[:, 0:nt], op0=ALU.mult, op1=ALU.add), reads=[("gx", i2), ("cv", i2)], writes=[("cv", i2)])
                                S.act(lambda e, c_=c_, nt=nt, bb=bb: e.activation(out=c_[:, 0:nt], in_=c_[:, 0:nt], func=AF.Silu, bias=bb), reads=[("cv", i2)], writes=[("cv", i2)])
                                S.dve(lambda e, c_=c_, nt=nt, bv=bv, fc=fc, a0=a0: e.tensor_tensor(out=actb[:, fc, a0:a0 + nt], in0=c_[:, 0:nt], in1=ps[:, bv, 0:nt], op=ALU.mult), reads=[("cv", i2)] + psk(bv), writes=[("act", fc, bi)])
                            else:
                                c_ = cv[0][:, 0:TS].rearrange("p (s j) -> p s j", j=4)
                                S.dve(lambda e, fc=fc: e.tensor_copy(out=gxs[:, :, 0:2], in_=sstT[:, fc, :].rearrange("p (s r) -> p s r", r=2)), reads=["sstT"], writes=["gxs"])
                                S.act(lambda e, bg=bg: e.copy(out=gxs[:, :, 2:6], in_=ps[:, bg, 0:TS].rearrange("p (s j) -> p s j", j=4)), reads=psk(bg), writes=["gxs"])
                                S.dve(lambda e, fc=fc: e.tensor_copy(out=tailS[:, fc, :, :], in_=gxs[:, :, 4:6]), reads=["gxs"], writes=["tailS"])
                                S.act(lambda e, c_=c_, w2=w2: e.activation(out=c_, in_=gxs[:, :, 2:6], func=AF.Identity, scale=w2), reads=["gxs", "ffnvT"], writes=[("cv", 0)])
                                S.dve(lambda e, c_=c_, w1=w1: e.scalar_tensor_tensor(out=c_, in0=gxs[:, :, 1:5], scalar=w1, in1=c_, op0=ALU.mult, op1=ALU.add), reads=["gxs", ("cv", 0)], writes=[("cv", 0)])
                                S.dve(lambda e, c_=c_, w0=w0: e.scalar_tensor_tensor(out=c_, in0=gxs[:, :, 0:4], scalar=w0, in1=c_, op0=ALU.mult, op1=ALU.add), reads=["gxs", ("cv", 0)], writes=[("cv", 0)])
                                S.act(lambda e, bb=bb: e.activation(out=cv[0][:, 0:TS], in_=cv[0][:, 0:TS], func=AF.Silu, bias=bb), reads=[("cv", 0)], writes=[("cv", 0)])
                                S.dve(lambda e, bv=bv, fc=fc, a0=a0: e.tensor_tensor(out=actb[:, fc, a0:a0 + TS], in0=cv[0][:, 0:TS], in1=ps[:, bv, 0:TS], op=ALU.mult), reads=[("cv", 0)] + psk(bv), writes=[("act", fc, bi)])
                for dh in range(2):
                    banks = {}
                    for dc in range(4):
                        for (bi, t0, nt, a0) in blks:
                            if len(blks) * 4 > 8 and bi == 6:
                                continue
                            banks[(dc, bi)] = ps_next()
                    for f0 in range(0, NF, 4):
                        nf = min(4, NF - f0)
                        wd, wdk = load_w_bf16(w_down[l], nf, 512, row0=f0 * 128, col0=dh * 512)
                        for (dc, bi), b in banks.items():
                            t0, nt, a0 = [(x[1], x[2], x[3]) for x in blks if x[0] == bi][0]
                            for q in range(nf):
                                fc = f0 + q
                                S.pe(lambda e, b=b, q=q, dc=dc, fc=fc, a0=a0, nt=nt, wd=wd: e.matmul(ps[:, b, 0:nt], lhsT=wd[:, q, dc * 128:(dc + 1) * 128], rhs=actb[:, fc, a0:a0 + nt], start=(fc == 0), stop=(fc == NF - 1)),
                                     reads=[wdk, ("act", fc, bi)], writes=psk(b))
                    for (dc, bi), b in banks.items():
                        t0, nt, a0 = [(x[1], x[2], x[3]) for x in blks if x[0] == bi][0]
                        resid_update(ps[:, b, 0:nt], gbase, dh * 4 + dc, bi, t0, nt, psk(b))
                if len(blks) * 4 > 8:
                    (bi, t0, nt, a0) = blks[2]
                    for dh in range(2):
                        banks = {dc: ps_next() for dc in range(4)}
                        for f0 in range(0, NF, 4):
                            nf = min(4, NF - f0)
                            wd, wdk = load_w_bf16(w_down[l], nf, 512, row0=f0 * 128, col0=dh * 512)
                            for dc, b in banks.items():
                                for q in range(nf):
                                    fc = f0 + q
                                    S.pe(lambda e, b=b, q=q, dc=dc, fc=fc, wd=wd, nt=nt, a0=a0: e.matmul(ps[:, b, 0:nt], lhsT=wd[:, q, dc * 128:(dc + 1) * 128], rhs=actb[:, fc, a0:a0 + nt], start=(fc == 0), stop=(fc == NF - 1)),
                                         reads=[wdk, ("act", fc, bi)], writes=psk(b))
                        for dc, b in banks.items():
                            resid_update(ps[:, b, 0:nt], gbase, dh * 4 + dc, bi, t0, nt, psk(b))
            for gi, g4 in enumerate(range(0, NF, 4)):
                n = min(4, NF - g4)
                o2 = gi % 2
                b = ps_next()
                b2 = ps_next()
                for q in range(n):
                    fc = g4 + q
                    S.pe(lambda e, b=b, q=q, fc=fc: e.transpose(ps[0:2, b, q * 128:(q + 1) * 128], tailP[:, fc, :], ident_f[:]), reads=["tailP", "ident_f"], writes=psk(b))
                    S.pe(lambda e, b2=b2, q=q, fc=fc: e.transpose(ps[0:32, b2, q * 128:(q + 1) * 128], tailS[:, fc, :, :].rearrange("p s r -> p (s r)"), ident_f[:]), reads=["tailS", "ident_f"], writes=psk(b2))
                S.dve(lambda e, b=b, n=n, o2=o2: e.tensor_copy(out=ostgP[o2][:, 0:n * 128], in_=ps[0:2, b, 0:n * 128]), reads=psk(b), writes=[("ostgP", o2)])
                S.dve(lambda e, b2=b2, n=n, o2=o2: e.tensor_copy(out=ostg[o2][:, 0:n * 128], in_=ps[0:32, b2, 0:n * 128]), reads=psk(b2), writes=[("ostg", o2)])
                S.dma("sp", "foutP%d" % o2, lambda e, o2=o2, n=n, g4=g4: e.dma_start(out=fconv_p[l][:, g4 * 128:(g4 + n) * 128], in_=ostgP[o2][:, 0:n * 128]), reads=[("ostgP", o2)])
                S.dma("sp", "foutS%d" % o2, lambda e, o2=o2, n=n, g4=g4: e.dma_start(out=fconv_s[l][:, g4 * 128:(g4 + n) * 128], in_=ostg[o2][:, 0:n * 128]), reads=[("ostg", o2)])
            S.barrier()

        def do_final():
            a_reset()
            yT = a_f32(KC, 512)
            ytok = [a_f32(D) for _ in range(2)]
            n = 4
            sq = a_bf16(KC, 512)
            rstd = a_f32(512)
            tmp = a_f32(KC, 512)
            shb = norm_shift[n]
            oi = 0
            for bi, (t0, nt) in enumerate(ALLBLOCKS):
                S.act(lambda e, t0=t0, nt=nt: e.activation(out=sq[:, :, 0:nt], in_=xT[:, :, t0:t0 + nt], func=AF.Square), reads=[], writes=["sq"])
                b = ps_next()
                for kc in range(KC):
                    S.pe(lambda e, b=b, kc=kc, nt=nt: e.matmul(ps[:, b, 0:nt], lhsT=ones_b[:], rhs=sq[:, kc, 0:nt], start=(kc == 0), stop=(kc == KC - 1)), reads=["sq", "ones_b"], writes=psk(b))
                S.act(lambda e, b=b, nt=nt: e.activation(out=rstd[:, 0:nt], in_=ps[:, b, 0:nt], func=AF.Sqrt, bias=EPS, scale=1.0 / D), reads=psk(b), writes=["rstd"])
                S.dve(lambda e, nt=nt: e.reciprocal(out=rstd[:, 0:nt], in_=rstd[:, 0:nt]), reads=["rstd"], writes=["rstd"])
                S.dve(lambda e, t0=t0, nt=nt: e.tensor_tensor(out=tmp[:, :, 0:nt], in0=xT[:, :, t0:t0 + nt], in1=rstd[:, 0:nt].unsqueeze(1).to_broadcast([128, KC, nt]), op=ALU.mult), reads=["rstd"], writes=["tmp"])
                for kc in range(KC):
                    if t0 < TP:
                        S.act(lambda e, kc=kc, nt=nt: e.activation(out=yT[:, kc, 0:nt], in_=tmp[:, kc, 0:nt], func=AF.Identity, scale=Amod[:, n, kc, 0:1], bias=modT[:, shb + kc, 0:1]), reads=["tmp"], writes=["yT"])
                    else:
                        S.dve(lambda e, kc=kc: e.tensor_tensor(out=tmp[:, kc, 0:TS].rearrange("p (s j) -> p s j", j=4), in0=tmp[:, kc, 0:TS].rearrange("p (s j) -> p s j", j=4), in1=Amod[:, n, kc, 1:17].unsqueeze(2).to_broadcast([128, NSEQ, 4]), op=ALU.mult), reads=["tmp"], writes=["tmp"])
                        S.dve(lambda e, kc=kc: e.tensor_tensor(out=yT[:, kc, 0:TS].rearrange("p (s j) -> p s j", j=4), in0=tmp[:, kc, 0:TS].rearrange("p (s j) -> p s j", j=4), in1=modT[:, shb + kc, 1:17].unsqueeze(2).to_broadcast([128, NSEQ, 4]), op=ALU.add), reads=["tmp"], writes=["yT"])
                for sub in range(0, nt, 128):
                    rows = min(128, nt - sub)
                    o2 = oi % 2
                    oi += 1
                    for half in range(2):
                        b = ps_next()
                        for q in range(4):
                            kc = half * 4 + q
                            S.pe(lambda e, b=b, q=q, kc=kc, sub=sub, rows=rows: e.transpose(ps[0:rows, b, q * 128:(q + 1) * 128], yT[:, kc, sub:sub + rows], ident_f[:]), reads=["yT", "ident_f"], writes=psk(b))
                        if half == 0:
                            S.act(lambda e, b=b, o2=o2, rows=rows, half=half: e.copy(out=ytok[o2][0:rows, half * 512:(half + 1) * 512], in_=ps[0:rows, b, :]), reads=psk(b), writes=[("ytok", o2, half)])
                        else:
                            S.dve(lambda e, b=b, o2=o2, rows=rows, half=half: e.tensor_copy(out=ytok[o2][0:rows, half * 512:(half + 1) * 512], in_=ps[0:rows, b, :]), reads=psk(b), writes=[("ytok", o2, half)])
                    if t0 < TP:
                        dst = y_p[t0 + sub:t0 + sub + rows, :]
                    else:
                        dst = y_s[sub:sub + rows, :]
                    S.dma("sp", "yo%d" % o2, lambda e, dst=dst, o2=o2, rows=rows: e.dma_start(out=dst, in_=ytok[o2][0:rows, :]), reads=[("ytok", o2, 0), ("ytok", o2, 1)])
            S.barrier()

        K.__dict__.update(locals())
        G_ = globals()
        do_norm(0)
        import os
        if "do_retention" in G_ and not os.environ.get("SKIP_RET"):
            G_["do_retention"](K)
        do_norm(1)
        do_ffn(0)
        do_norm(2)
        if "do_gdn" in G_:
            G_["do_gdn"](K)
        do_norm(3)
        do_ffn(1)
        do_final()
        if "modT" in dbg_out:
            S.dma("sp", "dbg", lambda e: e.dma_start(out=dbg_out["modT"], in_=modT[:].rearrange("p a b -> p (a b)")))
        if "Amod" in dbg_out:
            S.dma("sp", "dbg", lambda e: e.dma_start(out=dbg_out["Amod"], in_=Amod[:].rearrange("p a b c -> p (a b c)")))
        S.emit(st)
    return nc


def do_retention(K):
    S = K.S; ps = K.ps; nc = K.nc
    a_f32 = K.a_f32; a_bf16 = K.a_bf16; ps_next = K.ps_next; psk = K.psk
    hT = K.hT; xT = K.xT; modT = K.modT
    ident_b = K.ident_b
    g = _gammas()
    K.a_reset()
    K.tmpg = a_f32(TS)
    tabs = a_f32(4, 512)
    qb = a_bf16(2, 512)
    kb = a_bf16(2, 512)
    t1 = a_f32(512)
    t2 = a_f32(512)
    vb = a_bf16(512)
    sg = a_f32(512)
    kh = a_bf16(256)
    im = a_bf16(128)
    og = a_bf16(512)
    ogT = a_bf16(4, 512)
    ss = a_f32(2)
    maskP = a_f32(128)
    maskS = a_f32(128)
    ksc = a_f32(2 * RET_H)
    segcol = a_f32(NSEQ)
    S32 = a_f32(2, 512)
    Sbf = a_bf16(2, 512)
    qf = a_f32(2, TS)
    qX = a_f32(2, NSEQ, TS)
    khm = [a_bf16(256) for _ in range(2)]
    Sin = [a_f32(2, 512) for _ in range(2)]
    Sout = a_f32(2, 512)
    segmask = a_f32(NSEQ, TS)
    S.dma("sp", "rc", lambda e: e.dma_start(out=ksc, in_=K.c_kscale), writes=["ksc"])
    S.dma("sp", "rc", lambda e: e.dma_start(out=segcol, in_=K.c_segcol), writes=["segcol"])
    S.dma("sp", "rc", lambda e: e.dma_start(out=segmask.rearrange("p a b -> p (a b)"), in_=K.c_segmask), writes=["segmask"])
    gbase = 16
    w_in = K.w_ret_in
    w_out = K.w_ret_out
    sctr = [0]
    for h in range(RET_H):
        wq, wqk = K.load_w_bf16(w_in, KC, 256, col0=h * 256, slot=0, off=0, key=("w", 0, "q"))
        wk, wkk = K.load_w_bf16(w_in, KC, 256, col0=1024 + h * 256, slot=0, off=2048, key=("w", 0, "k"))
        wv, wvk = K.load_w_bf16(w_in, KC, 512, col0=2048 + h * 512, slot=1)
        wg, wgk = K.load_w_bf16(w_in, KC, 512, col0=4096 + h * 512, slot=2)
        wo, wok = K.load_w_bf16(w_out, 4, 1024, row0=h * 512, slot=3)
        S.dma("sp", "rm", lambda e, h=h: e.dma_start(out=maskP, in_=K.c_retmask[h]), writes=["maskP"])
        S.dma("sp", "rm", lambda e, h=h: e.dma_start(out=maskS, in_=K.c_retmask[RET_H + h]), writes=["maskS"])
        for bi, (t0, nt) in enumerate(K.ALLBLOCKS):
            isS = t0 >= TP
            C = 64 if isS else 128
            for ti, (hh, cs_) in enumerate([(h, 0), (h, 1), (RET_H, 0), (RET_H, 1)]):
                S.dma("sp", "rt", lambda e, ti=ti, hh=hh, cs_=cs_, t0=t0, nt=nt: e.dma_start(out=tabs[:, ti, 0:nt], in_=K.c_rope[hh, cs_, :, t0:t0 + nt]), writes=[("tabs", ti)])
            for which, (wt, wkey, dst, tb) in enumerate([(wq, wqk, qb, 0), (wk, wkk, kb, 2)]):
                pb = [ps_next(), ps_next()]
                for dc in range(2):
                    for kc in range(KC):
                        S.pe(lambda e, b=pb[dc], kc=kc, dc=dc, wt=wt, t0=t0, nt=nt: e.matmul(ps[:, b, 0:nt], lhsT=wt[:, kc, dc * 128:(dc + 1) * 128], rhs=hT[:, kc, t0:t0 + nt], start=(kc == 0), stop=(kc == KC - 1)),
                             reads=[wkey], writes=psk(pb[dc]))
                p1 = ps[:, pb[0], 0:nt]
                p2 = ps[:, pb[1], 0:nt]
                cc = tabs[:, tb, 0:nt]
                sn = tabs[:, tb + 1, 0:nt]
                to_f = isS and which == 0
                d1 = qf[:, 0, :] if to_f else dst[:, 0, 0:nt]
                d2 = qf[:, 1, :] if to_f else dst[:, 1, 0:nt]
                S.dve(lambda e, p1=p1, cc=cc, nt=nt: e.tensor_tensor(out=t1[:, 0:nt], in0=p1, in1=cc, op=ALU.mult), reads=psk(pb[0]) + [("tabs", tb)], writes=["t1"])
                S.dve(lambda e, p2=p2, sn=sn, nt=nt: e.tensor_tensor(out=t2[:, 0:nt], in0=p2, in1=sn, op=ALU.mult), reads=psk(pb[1]) + [("tabs", tb + 1)], writes=["t2"])
                S.pool(lambda e, d1=d1, nt=nt: e.tensor_tensor(out=d1, in0=t1[:, 0:nt], in1=t2[:, 0:nt], op=ALU.subtract), reads=["t1", "t2"], writes=[("rd", which, 0)])
                S.dve(lambda e, p1=p1, sn=sn, nt=nt: e.tensor_tensor(out=t1[:, 0:nt], in0=p1, in1=sn, op=ALU.mult), reads=psk(pb[0]) + [("tabs", tb + 1)], writes=["t1"])
                S.dve(lambda e, p2=p2, cc=cc, nt=nt: e.tensor_tensor(out=t2[:, 0:nt], in0=p2, in1=cc, op=ALU.mult), reads=psk(pb[1]) + [("tabs", tb)], writes=["t2"])
                S.pool(lambda e, d2=d2, nt=nt: e.tensor_tensor(out=d2, in0=t1[:, 0:nt], in1=t2[:, 0:nt], op=ALU.add), reads=["t1", "t2"], writes=[("rd", which, 1)])
                if to_f:
                    S.act(lambda e: e.copy(out=qb[:, :, 0:TS], in_=qf[:, :, :]), reads=[("rd", 0, 0), ("rd", 0, 1)], writes=["qbS"])
            qkeys = [("rd", 0, 0), ("rd", 0, 1)] + (["qbS"] if isS else [])
            kkeys = [("rd", 1, 0), ("rd", 1, 1)]
            for c0 in range(0, nt, C):
                first = (not isS) and t0 == 0 and c0 == 0
                last = (not isS) and (t0 + c0 + C == TP)
                bv = ps_next()
                for kc in range(KC):
                    S.pe(lambda e, bv=bv, kc=kc, t0=t0, c0=c0, C=C: e.matmul(ps[0:C, bv, :], lhsT=hT[:, kc, t0 + c0:t0 + c0 + C], rhs=wv[:, kc, :], start=(kc == 0), stop=(kc == KC - 1)), reads=[wvk], writes=psk(bv))
                S.act(lambda e, bv=bv, C=C: e.copy(out=vb[0:C, :], in_=ps[0:C, bv, :]), reads=psk(bv), writes=["vb"])
                bg = ps_next()
                for kc in range(KC):
                    S.pe(lambda e, bg=bg, kc=kc, t0=t0, c0=c0, C=C: e.matmul(ps[0:C, bg, :], lhsT=hT[:, kc, t0 + c0:t0 + c0 + C], rhs=wg[:, kc, :], start=(kc == 0), stop=(kc == KC - 1)), reads=[wgk], writes=psk(bg))
                S.act(lambda e, bg=bg, C=C: e.activation(out=sg[0:C, :], in_=ps[0:C, bg, :], func=AF.Silu), reads=psk(bg), writes=["sg"])
                bk = ps_next()
                psb = ps[:, bk, :].bitcast(BF16)
                for dc in range(2):
                    S.pe(lambda e, psb=psb, dc=dc, c0=c0, C=C: e.transpose(psb[0:C, dc * 128:(dc + 1) * 128], kb[:, dc, c0:c0 + C], ident_b[:]), reads=kkeys + ["ident_b"], writes=psk(bk))
                kcol = (RET_H + h) if isS else h
                S.act(lambda e, psb=psb, C=C, kcol=kcol: e.activation(out=kh[0:C, :], in_=psb[0:C, 0:256], func=AF.Identity, scale=ksc[0:C, kcol:kcol + 1]), reads=psk(bk) + ["ksc"], writes=["kh"])
                bi_ = ps_next()
                for dc in range(2):
                    S.pe(lambda e, bi_=bi_, dc=dc, c0=c0, C=C: e.matmul(ps[0:C, bi_, 0:C], lhsT=kb[:, dc, c0:c0 + C], rhs=qb[:, dc, c0:c0 + C], start=(dc == 0), stop=(dc == 1)), reads=kkeys + qkeys, writes=psk(bi_))
                mk = maskS if isS else maskP
                S.dve(lambda e, bi_=bi_, C=C, mk=mk: e.tensor_tensor(out=im[0:C, 0:C], in0=ps[0:C, bi_, 0:C], in1=mk[0:C, 0:C], op=ALU.mult), reads=psk(bi_) + ["maskP", "maskS"], writes=["im"])
                bo = ps_next()
                has_inter = isS or not first
                S.pe(lambda e, bo=bo, C=C, has_inter=has_inter: e.matmul(ps[0:C, bo, :], lhsT=im[0:C, 0:C], rhs=vb[0:C, :], start=True, stop=not has_inter), reads=["im", "vb"], writes=psk(bo))
                if not isS:
                    if not first:
                        for dc in range(2):
                            S.pe(lambda e, bo=bo, dc=dc, c0=c0, C=C: e.matmul(ps[0:C, bo, :], lhsT=qb[:, dc, c0:c0 + C], rhs=Sbf[:, dc, :], start=False, stop=(dc == 1)), reads=qkeys + [("Sbf", dc)], writes=psk(bo))
                    for dc in range(2):
                        bs = ps_next()
                        S.pe(lambda e, bs=bs, dc=dc, C=C: e.matmul(ps[:, bs, :], lhsT=kh[0:C, dc * 128:(dc + 1) * 128], rhs=vb[0:C, :], start=True, stop=True), reads=["kh", "vb"], writes=psk(bs))
                        if first:
                            S.dve(lambda e, bs=bs, dc=dc: e.tensor_copy(out=S32[:, dc, :], in_=ps[:, bs, :]), reads=psk(bs), writes=[("S32", dc)])
                        else:
                            S.dve(lambda e, bs=bs, dc=dc, gc=float(g[h] ** 128): e.scalar_tensor_tensor(out=S32[:, dc, :], in0=S32[:, dc, :], scalar=gc, in1=ps[:, bs, :], op0=ALU.mult, op1=ALU.add), reads=psk(bs) + [("S32", dc)], writes=[("S32", dc)])
                        if last:
                            S.dma("sp", "rpo", lambda e, dc=dc, h=h: e.dma_start(out=K.ret_p[h, dc * 128:(dc + 1) * 128, :], in_=S32[:, dc, :]), reads=[("S32", dc)])
                        else:
                            S.act(lambda e, dc=dc: e.copy(out=Sbf[:, dc, :], in_=S32[:, dc, :]), reads=[("S32", dc)], writes=[("Sbf", dc)])
                else:
                    K.P.reserved = {bo}
                    for dc in range(2):
                        S.dve(lambda e, dc=dc: e.tensor_tensor(out=qX[:, dc, :, :], in0=qf[:, dc, :].unsqueeze(1).to_broadcast([128, NSEQ, TS]), in1=segmask[:, :, :], op=ALU.mult), reads=[("rd", 0, dc), "segmask"], writes=[("qX", dc)])
                    def _ldr(n, h=h):
                        i3 = n % 2
                        S.dma("sp", "sin%d" % i3, lambda e, i3=i3, n=n, h=h: e.dma_start(out=Sin[i3], in_=K.s_ret[n, h].rearrange("(dc p) v -> p dc v", p=128)), writes=[("Sin", i3)])
                    _ldr(0)
                    for s_ in range(NSEQ):
                        i2 = s_ % 2
                        if s_ + 1 < NSEQ:
                            _ldr(s_ + 1)
                        for dc in range(2):
                            S.pe(lambda e, bo=bo, dc=dc, s_=s_, i2=i2: e.matmul(ps[0:TS, bo, :], lhsT=qX[:, dc, s_, :], rhs=Sin[i2][:, dc, :], start=False, stop=(s_ == NSEQ - 1 and dc == 1)), reads=[("qX", dc), ("Sin", i2)], writes=psk(bo))
                        S.dve(lambda e, i2=i2, s_=s_: e.tensor_scalar(out=khm[i2][0:TS, :], in0=kh[0:TS, :], scalar1=segcol[0:TS, s_:s_ + 1], scalar2=None, op0=ALU.mult), reads=["kh", "segcol"], writes=[("khm", i2)])
                        for dc in range(2):
                            bs = ps_next()
                            S.pe(lambda e, bs=bs, dc=dc, i2=i2: e.matmul(ps[:, bs, :], lhsT=khm[i2][0:TS, dc * 128:(dc + 1) * 128], rhs=vb[0:TS, :], start=True, stop=True), reads=[("khm", i2), "vb"], writes=psk(bs))
                            S.dve(lambda e, bs=bs, dc=dc, i2=i2, gc=float(g[h] ** 4): e.scalar_tensor_tensor(out=Sout[:, dc, :], in0=Sin[i2][:, dc, :], scalar=gc, in1=ps[:, bs, :], op0=ALU.mult, op1=ALU.add), reads=psk(bs) + [("Sin", i2)], writes=[("Sout", dc)])
                        S.dma("sp", "sout", lambda e, s_=s_, h=h: e.dma_start(out=K.ret_s[s_, h].rearrange("(dc p) v -> p dc v", p=128), in_=Sout), reads=[("Sout", 0), ("Sout", 1)], writes=[])
                    K.P.reserved = set()
                _ret_tail(K, h, bo, C, c0, sg, og, ogT, ss, t1)
            for dc8 in range(KC):
                b = ps_next()
                for ec in range(4):
                    S.pe(lambda e, b=b, ec=ec, dc8=dc8, nt=nt: e.matmul(ps[:, b, 0:nt], lhsT=wo[:, ec, dc8 * 128:(dc8 + 1) * 128], rhs=ogT[:, ec, 0:nt], start=(ec == 0), stop=(ec == 3)), reads=[wok, "ogT"], writes=psk(b))
                K.resid_update(ps[:, b, 0:nt], gbase, dc8, ("S" if isS else bi), t0, nt, psk(b))
        S.barrier()
    S.barrier()


def _ret_tail(K, h, bo, C, c0, sg, og, ogT, ss, junk):
    S = K.S; ps = K.ps
    _f = lambda e: e.activation(out=junk[0:C, :], in_=ps[0:C, bo, :], func=AF.Square, accum_out=ss[0:C, 0:1])
    _f._multi = True
    S.act(_f, reads=K.psk(bo), writes=["t1", "ss"])
    S.act(lambda e: e.activation(out=ss[0:C, 1:2], in_=ss[0:C, 0:1], func=AF.Sqrt, bias=EPS, scale=1.0 / 512), reads=["ss"], writes=["ss2"])
    S.dve(lambda e: e.reciprocal(out=ss[0:C, 1:2], in_=ss[0:C, 1:2]), reads=["ss2"], writes=["ss2"])
    S.dve(lambda e: e.scalar_tensor_tensor(out=og[0:C, :], in0=ps[0:C, bo, :], scalar=ss[0:C, 1:2], in1=sg[0:C, :], op0=ALU.mult, op1=ALU.mult), reads=K.psk(bo) + ["ss2", "sg"], writes=["og"])
    bt = K.ps_next()
    psb = ps[:, bt, :].bitcast(BF16)
    for ec in range(4):
        S.pe(lambda e, ec=ec: e.transpose(psb[:, ec * C:(ec + 1) * C], og[0:C, ec * 128:(ec + 1) * 128], K.ident_b[0:C, 0:C]), reads=["og", "ident_b"], writes=K.psk(bt))
    S.act(lambda e: e.copy(out=ogT[:, :, c0:c0 + C], in_=psb[:, 0:4 * C].rearrange("p (a b) -> p a b", a=4)), reads=K.psk(bt), writes=["ogT"])


def do_gdn(K):
    S = K.S; ps = K.ps
    a_f32 = K.a_f32; a_bf16 = K.a_bf16; ps_next = K.ps_next; psk = K.psk
    hT = K.hT; ident_f = K.ident_f; ident_b = K.ident_b; ones_b = K.ones_b
    w_in = K.w_gdn_in; w_out = K.w_gdn_out
    K.a_reset()
    K.tmpg = a_f32(TS)
    gm = a_f32(8, 128)
    segmask = a_f32(NSEQ, TS)
    segcol = a_f32(NSEQ)
    beta_all = a_f32(17, 16); g_all = a_f32(17, 16); negeG_all = a_f32(17, 16); kdec_all = a_f32(17, 16)
    gv = a_f32(32); negA = a_f32(16); gnw = a_f32(128)
    wcT = a_f32(32, 4)
    ones128 = a_f32(128)
    wba = a_bf16(KC, 32)
    Gx = a_f32(3 + 512); acc = a_f32(512); acc2 = a_f32(512); cch = a_f32(4, 3); gsT = a_f32(4, 48); Gxs = a_f32(NSEQ, 7)
    tailP = a_f32(4, 3); tailS = a_f32(4, NSEQ, 3)
    vs = a_f32(2, 512)
    sqb = a_bf16(512); rs = a_f32(512)
    qn = a_bf16(512); kn = a_bf16(512); qnf = a_f32(TS); knf = a_f32(TS)
    Bm = a_f32(2, 128); E = a_f32(2, 128); E2 = a_f32(2, 128); DTm = a_f32(2, 128)
    U = a_bf16(2, 128); UT = a_bf16(2, 128)
    UoM = [a_bf16(2, 128) for _ in range(3)]; UoTM = [a_bf16(2, 128) for _ in range(3)]
    Nb = [a_bf16(2, 128) for _ in range(2)]; Pb = [a_bf16(2, 128) for _ in range(2)]; PTb = [a_bf16(2, 128) for _ in range(2)]
    NTb = [a_bf16(2, 128) for _ in range(2)]; bm = a_bf16(4, 128)
    ktok = a_bf16(128); xv = a_bf16(2, 128); vnew = a_bf16(2, 128)
    Sbf = a_bf16(2, 128); og = a_bf16(256); ogT = a_bf16(2, 512)
    S32 = a_f32(2, 128); ss = a_f32(8)
    X = a_f32(NSEQ, TS); qgf = a_f32(TS)
    kdm = a_bf16(128); Sin = [a_f32(128) for _ in range(2)]; Sout = a_f32(128)
    ostg = acc
    eGl = [a_f32(2), a_f32(2)]
    w3f = K.wring[:, 3, :].bitcast(F32)
    w3b = K.wring[:, 3, :]
    szn2 = [w3f[:, 0:256], w3f[:, 256:512]]
    vtok2 = [w3f[:, 512:768].rearrange("p (a b) -> p a b", a=2), w3f[:, 768:1024].rearrange("p (a b) -> p a b", a=2)]

    def _b3(i):
        return w3b[:, 2048 + i * 256:2048 + (i + 1) * 256].rearrange("p (a b) -> p a b", a=2)
    Nfin = [_b3(0), _b3(1)]; attnT2 = [_b3(2), _b3(3)]; qg2 = [_b3(4), _b3(5)]; kd2 = [_b3(6), _b3(7)]

    def _f32v(b):
        return b.rearrange("p a b -> p (a b)").bitcast(F32)
    SinR = [Sin[0], Sin[1]] + [_f32v(x) for x in UoM + UoTM]
    SoutR = [Sout, _f32v(NTb[0]), _f32v(NTb[1])]
    RDEPTH = 7
    NLD = 2 * NSEQ
    mark = K.A.off

    S.dma("sp", "gc", lambda e: e.dma_start(out=gm.rearrange("p a b -> p (a b)"), in_=K.c_gmask), writes=["gm"])
    S.dma("sp", "gc", lambda e: e.dma_start(out=segmask.rearrange("p a b -> p (a b)"), in_=K.c_segmask), writes=["segmask"])
    S.dma("sp", "gc", lambda e: e.dma_start(out=segcol, in_=K.c_segcol), writes=["segcol"])
    S.dma("pool", "gcb", lambda e: e.dma_start(out=bm.rearrange("p a b -> p (a b)"), in_=K.c_bmask), writes=["bm"])
    S.dma("sp", "gc", lambda e: e.dma_start(out=gv, in_=K.gdn_vec.partition_broadcast(128)), writes=["gv"])
    S.dma("sp", "gc", lambda e: e.dma_start(out=gnw, in_=K.gdn_norm.partition_broadcast(128)), writes=["gnw"])
    S.dve(lambda e: e.memset(ones128, 1.0), writes=["ones128"])
    S.act(lambda e: e.activation(out=negA, in_=gv[:, 0:16], func=AF.Exp), reads=["gv"], writes=["negA"])
    S.dve(lambda e: e.tensor_scalar(out=negA, in0=negA, scalar1=-1.0, scalar2=None, op0=ALU.mult), reads=["negA"], writes=["negA"])
    TRIU = {False: gm[:, 0, :], True: gm[:, 4, :]}
    SU = {False: gm[:, 1, :], True: gm[:, 5, :]}
    INCL = {False: gm[:, 2, :], True: gm[:, 6, :]}
    STRICT = {False: gm[:, 3, :], True: gm[:, 7, :]}
    import os
    STG = int(os.environ.get('GDN_STAGE', 99))
    if STG == 0:
        S.barrier(); return
    Xflat = X.rearrange("p a b -> p (a b)")
    wst = [Xflat[0:4, 0:512], Xflat[0:4, 512:1024]]
    bw = ps_next()
    for pc in range(8):
        S.dma("sp", "wst%d" % (pc % 2), lambda e, pc=pc: e.dma_start(out=wst[pc % 2], in_=K.w_gconv[:, pc * 512:(pc + 1) * 512]), writes=[("wst", pc % 2)])
        for q in range(4):
            cidx = pc * 4 + q
            S.pe(lambda e, pc=pc, q=q, cidx=cidx: e.transpose(ps[:, bw, cidx * 4:(cidx + 1) * 4], wst[pc % 2][:, q * 128:(q + 1) * 128], ident_f[0:4, 0:4]), reads=[("wst", pc % 2), "ident_f"], writes=psk(bw))
    S.dve(lambda e: e.tensor_copy(out=wcT.rearrange("p a b -> p (a b)"), in_=ps[:, bw, 0:128]), reads=psk(bw), writes=["wcT"])
    if STG == 1:
        S.barrier(); return
    src = w_in[:, 6144:6176].rearrange("(k p) c -> p k c", p=128)
    S.dma("pool", "wba", lambda e: e.dma_start(out=wba, in_=src), writes=["wba"])
    if STG == 2:
        S.barrier(); return
    tiles = [(t * 128, 128, False) for t in range(16)] + [(TP, TS, True)]
    for tl, (t0, C, isS) in enumerate(tiles):
        pb = ps_next()
        for kc in range(KC):
            S.pe(lambda e, pb=pb, kc=kc, t0=t0, C=C: e.matmul(ps[0:C, pb, 0:32], lhsT=hT[:, kc, t0:t0 + C], rhs=wba[:, kc, :], start=(kc == 0), stop=(kc == KC - 1)), reads=["wba"], writes=psk(pb))
        S.act(lambda e, pb=pb, C=C, tl=tl: e.activation(out=beta_all[0:C, tl, :], in_=ps[0:C, pb, 0:16], func=AF.Sigmoid), reads=psk(pb), writes=[("beta", tl)])
        S.dve(lambda e, pb=pb, C=C, tl=tl: e.tensor_tensor(out=g_all[0:C, tl, :], in0=ps[0:C, pb, 16:32], in1=gv[0:C, 16:32], op=ALU.add), reads=psk(pb) + ["gv"], writes=[("g", tl)])
        S.act(lambda e, C=C, tl=tl: e.activation(out=g_all[0:C, tl, :], in_=g_all[0:C, tl, :], func=AF.Exp), reads=[("g", tl)], writes=[("g", tl)])
        S.act(lambda e, C=C, tl=tl: e.activation(out=g_all[0:C, tl, :], in_=g_all[0:C, tl, :], func=AF.Ln, bias=1.0), reads=[("g", tl)], writes=[("g", tl)])
        S.dve(lambda e, C=C, tl=tl: e.tensor_tensor(out=g_all[0:C, tl, :], in0=g_all[0:C, tl, :], in1=negA[0:C, :], op=ALU.mult), reads=[("g", tl), "negA"], writes=[("g", tl)])
        pg = ps_next()
        S.pe(lambda e, pg=pg, C=C, tl=tl, isS=isS: e.matmul(ps[0:C, pg, 0:16], lhsT=TRIU[isS][0:C, 0:C], rhs=g_all[0:C, tl, :], start=True, stop=True), reads=[("g", tl), "gm"], writes=psk(pg))
        S.pe(lambda e, pg=pg, C=C, tl=tl, isS=isS: e.matmul(ps[0:C, pg, 16:32], lhsT=SU[isS][0:C, 0:C], rhs=g_all[0:C, tl, :], start=True, stop=True), reads=[("g", tl), "gm"], writes=psk(pg))
        S.act(lambda e, pg=pg, C=C, tl=tl: e.activation(out=negeG_all[0:C, tl, :], in_=ps[0:C, pg, 0:16], func=AF.Exp), reads=psk(pg), writes=[("negeG", tl)])
        S.dve(lambda e, C=C, tl=tl: e.tensor_scalar(out=negeG_all[0:C, tl, :], in0=negeG_all[0:C, tl, :], scalar1=-1.0, scalar2=None, op0=ALU.mult), reads=[("negeG", tl)], writes=[("negeG", tl)])
        S.act(lambda e, pg=pg, C=C, tl=tl: e.activation(out=kdec_all[0:C, tl, :], in_=ps[0:C, pg, 16:32], func=AF.Exp), reads=psk(pg), writes=[("kdec", tl)])
    S.barrier()
    K.A.off = mark
    gbase = 48 + 16

    import os
    for hk in range(int(os.environ.get("GDN_HK0", 0)), int(os.environ.get("GDN_NHK", GDN_HK))):
        hv0 = 2 * hk
        wq, wqk = K.load_w_bf16(w_in, KC, 128, col0=hk * 128, slot=0, off=0, key=("w", 0, "q"))
        wk, wkk = K.load_w_bf16(w_in, KC, 128, col0=1024 + hk * 128, slot=0, off=1024, key=("w", 0, "k"))
        wv, wvk = K.load_w_bf16(w_in, KC, 256, col0=2048 + hk * 256, slot=1, off=0, key=("w", 1, "v"))
        wz, wzk = K.load_w_bf16(w_in, KC, 256, col0=4096 + hk * 256, slot=1, off=2048, key=("w", 1, "z"))
        wo, wok = K.load_w_bf16(w_out, 2, 1024, row0=hk * 256, slot=2)
        cids = [hk, 8 + hk, 16 + 2 * hk, 17 + 2 * hk]
        CUT = os.environ.get('GDN_CUT', '')
        if hk >= 1 and 'D' in CUT:
            S.barrier(); continue
        gst = a_f32(512, parts=48)
        K.A.off = mark
        if not (hk >= 1 and 'H' in CUT):
            for ci, cid in enumerate(cids):
                S.dma("sp", "gst", lambda e, ci=ci, cid=cid, gst=gst: e.dma_start(out=gst[:, ci * 128:(ci + 1) * 128], in_=K.s_gconv[:, cid * 128:(cid + 1) * 128]), writes=["gst"])
            bq = ps_next()
            for ci in range(4):
                S.pe(lambda e, ci=ci, bq=bq, gst=gst: e.transpose(ps[:, bq, ci * 48:(ci + 1) * 48], gst[:, ci * 128:(ci + 1) * 128], ident_f[0:48, 0:48]), reads=["gst", "ident_f"], writes=psk(bq))
            S.dve(lambda e, bq=bq: e.tensor_copy(out=gsT.rearrange("p a b -> p (a b)"), in_=ps[:, bq, 0:192]), reads=psk(bq), writes=["gsT"])
        S.dve(lambda e: e.memset(cch, 0.0), writes=["cch"])
        if hk >= 1 and 'E' in CUT:
            S.barrier(); continue
        for bi, (t0, nt) in enumerate(K.ALLBLOCKS):
            isS = t0 >= TP
            C = 64 if isS else 128
            lastblk = (t0 + nt == TP)
            if isS:
                S.barrier()
            for ci, (wt, wkey, col) in enumerate([(wq, wqk, 0), (wk, wkk, 0), (wv, wvk, 0), (wv, wvk, 128)]):
                pp = ps_next()
                for kc in range(KC):
                    S.pe(lambda e, pp=pp, kc=kc, wt=wt, col=col, t0=t0, nt=nt: e.matmul(ps[:, pp, 0:nt], lhsT=wt[:, kc, col:col + 128], rhs=hT[:, kc, t0:t0 + nt], start=(kc == 0), stop=(kc == KC - 1)), reads=[wkey], writes=psk(pp))
                cid = cids[ci]
                wc = [wcT[:, cid, i:i + 1] for i in range(4)]
                accb = acc if ci % 2 == 0 else acc2
                ak = ("acc", ci % 2)
                dst = accb[:, 0:nt] if ci < 2 else vs[:, ci - 2, 0:nt]
                if not isS:
                    S.dve(lambda e, ci=ci: e.tensor_copy(out=Gx[:, 0:3], in_=cch[:, ci, :]), reads=["cch"], writes=["Gx"])
                    S.act(lambda e, pp=pp, nt=nt: e.copy(out=Gx[:, 3:3 + nt], in_=ps[:, pp, 0:nt]), reads=psk(pp), writes=["Gx"])
                    S.dve(lambda e, ci=ci, nt=nt: e.tensor_copy(out=cch[:, ci, :], in_=Gx[:, nt:nt + 3]), reads=["Gx"], writes=["cch"])
                    if lastblk:
                        S.dve(lambda e, ci=ci, nt=nt: e.tensor_copy(out=tailP[:, ci, :], in_=Gx[:, nt:nt + 3]), reads=["Gx"], writes=["tailP"])
                    x3 = [Gx[:, i:i + nt] for i in range(4)]
                    a_ = accb[:, 0:nt]
                    d_ = dst
                    gk = ["Gx"]
                else:
                    S.dve(lambda e, ci=ci: e.tensor_copy(out=Gxs[:, :, 0:3], in_=gsT[:, ci, :].rearrange("p (s r) -> p s r", r=3)), reads=["gsT"], writes=["Gxs"])
                    S.act(lambda e, pp=pp: e.copy(out=Gxs[:, :, 3:7], in_=ps[:, pp, 0:TS].rearrange("p (s j) -> p s j", j=4)), reads=psk(pp), writes=["Gxs"])
                    S.dve(lambda e, ci=ci: e.tensor_copy(out=tailS[:, ci, :, :], in_=Gxs[:, :, 4:7]), reads=["Gxs"], writes=["tailS"])
                    x3 = [Gxs[:, :, i:i + 4] for i in range(4)]
                    a_ = accb[:, 0:TS].rearrange("p (s j) -> p s j", j=4)
                    d_ = dst.rearrange("p (s j) -> p s j", j=4)
                    gk = ["Gxs"]
                S.act(lambda e, a_=a_, x3=x3, wc=wc: e.activation(out=a_, in_=x3[3], func=AF.Identity, scale=wc[3]), reads=gk + ["wcT"], writes=[ak])
                for i in (2, 1, 0):
                    S.dve(lambda e, a_=a_, x3=x3, wc=wc, i=i: e.scalar_tensor_tensor(out=a_, in0=x3[i], scalar=wc[i], in1=a_, op0=ALU.mult, op1=ALU.add), reads=gk + [ak], writes=[ak])
                S.act(lambda e, a_=a_, d_=d_: e.activation(out=d_, in_=a_, func=AF.Silu), reads=[ak], writes=([ak, ("cv", ci)] if ci < 2 else [("cv", ci)]))
                if ci < 2:
                    S.act(lambda e, nt=nt, accb=accb: e.activation(out=sqb[:, 0:nt], in_=accb[:, 0:nt], func=AF.Square), reads=[ak], writes=["sqb"])
                    pn = ps_next()
                    S.pe(lambda e, pn=pn, nt=nt: e.matmul(ps[:, pn, 0:nt], lhsT=ones_b[:], rhs=sqb[:, 0:nt], start=True, stop=True), reads=["sqb", "ones_b"], writes=psk(pn))
                    S.act(lambda e, pn=pn, nt=nt: e.activation(out=rs[:, 0:nt], in_=ps[:, pn, 0:nt], func=AF.Sqrt, bias=EPS, scale=1.0), reads=psk(pn), writes=["rs"])
                    S.dve(lambda e, nt=nt: e.reciprocal(out=rs[:, 0:nt], in_=rs[:, 0:nt]), reads=["rs"], writes=["rs"])
                    dn = qn if ci == 0 else kn
                    dnf = qnf if ci == 0 else knf
                    sc = (128.0 ** -0.5) if ci == 0 else 1.0
                    if isS:
                        S.dve(lambda e, dnf=dnf, sc=sc, accb=accb: e.scalar_tensor_tensor(out=dnf[:, :], in0=accb[:, 0:TS], scalar=sc, in1=rs[:, 0:TS], op0=ALU.mult, op1=ALU.mult), reads=[ak, "rs"], writes=[("nf", ci)])
                        S.act(lambda e, dn=dn, dnf=dnf: e.copy(out=dn[:, 0:TS], in_=dnf[:, :]), reads=[("nf", ci)], writes=[("n", ci)])
                    else:
                        S.dve(lambda e, dn=dn, sc=sc, nt=nt, accb=accb: e.scalar_tensor_tensor(out=dn[:, 0:nt], in0=accb[:, 0:nt], scalar=sc, in1=rs[:, 0:nt], op0=ALU.mult, op1=ALU.mult), reads=[ak, "rs"], writes=[("n", ci)])
            def chunk_gen(c0, pb):
                tl = (t0 + c0) // 128
                first = (not isS) and t0 == 0 and c0 == 0
                last = (not isS) and (t0 + c0 + C == TP)
                L = 1 if isS else 6
                bvt = ps_next()
                for e_ in range(2):
                    S.pe(lambda e, e_=e_, bvt=bvt, c0=c0, C=C: e.transpose(ps[0:C, bvt, e_ * 128:(e_ + 1) * 128], vs[:, e_, c0:c0 + C], ident_f[:]), reads=[("cv", 2 + e_), "ident_f"], writes=psk(bvt))
                    yield
                S.act(lambda e, bvt=bvt, C=C: e.copy(out=vtok2[pb][0:C].rearrange("p a b -> p (a b)"), in_=ps[0:C, bvt, 0:256]), reads=psk(bvt), writes=[("vtok", pb)])
                yield
                bz = ps_next()
                for kc in range(KC):
                    S.pe(lambda e, bz=bz, kc=kc, t0=t0, c0=c0, C=C: e.matmul(ps[0:C, bz, 0:256], lhsT=hT[:, kc, t0 + c0:t0 + c0 + C], rhs=wz[:, kc, :], start=(kc == 0), stop=(kc == KC - 1)), reads=[wzk], writes=psk(bz))
                    yield
                S.act(lambda e, bz=bz, C=C: e.activation(out=szn2[pb][0:C, :], in_=ps[0:C, bz, 0:256], func=AF.Silu), reads=psk(bz), writes=[("szn", pb)])
                yield
                S.dve(lambda e, C=C: e.tensor_tensor(out=szn2[pb][0:C, :].rearrange("p (a b) -> p a b", a=2), in0=szn2[pb][0:C, :].rearrange("p (a b) -> p a b", a=2), in1=gnw[0:C, :].unsqueeze(1).to_broadcast([C, 2, 128]), op=ALU.mult), reads=[("szn", pb), "gnw"], writes=[("szn", pb)])
                yield
                bkt = ps_next()
                pkb = ps[:, bkt, :].bitcast(BF16)
                S.pe(lambda e, pkb=pkb, c0=c0, C=C: e.transpose(pkb[0:C, 0:128], kn[:, c0:c0 + C], ident_b[:]), reads=[("n", 1), "ident_b"], writes=psk(bkt))
                yield
                S.act(lambda e, pkb=pkb, C=C: e.copy(out=ktok[0:C, :], in_=pkb[0:C, 0:128]), reads=psk(bkt), writes=["ktok"])
                yield
                bkk = ps_next()
                S.pe(lambda e, bkk=bkk, c0=c0, C=C: e.matmul(ps[0:C, bkk, 0:C], lhsT=kn[:, c0:c0 + C], rhs=kn[:, c0:c0 + C], start=True, stop=True), reads=[("n", 1)], writes=psk(bkk))
                yield
                S.pe(lambda e, bkk=bkk, c0=c0, C=C: e.matmul(ps[0:C, bkk, 128:128 + C], lhsT=kn[:, c0:c0 + C], rhs=qn[:, c0:c0 + C], start=True, stop=True), reads=[("n", 0), ("n", 1)], writes=psk(bkk))
                yield
                bd = ps_next()
                bd2 = ps_next()
                for e_ in range(2):
                    hv = hv0 + e_
                    S.dve(lambda e, e_=e_, hv=hv, C=C, tl=tl, isS=isS: e.tensor_scalar(out=Bm[0:C, e_, 0:C], in0=TRIU[isS][0:C, 0:C], scalar1=g_all[0:C, tl, hv:hv + 1], scalar2=None, op0=ALU.mult), reads=["gm"], writes=[("Bm", e_)])
                    yield
                    S.pe(lambda e, e_=e_, bd=bd, C=C, isS=isS: e.matmul(ps[0:C, bd, e_ * 128:e_ * 128 + C], lhsT=SU[isS][0:C, 0:C], rhs=Bm[0:C, e_, 0:C], start=True, stop=True), reads=[("Bm", e_), "gm"], writes=psk(bd))
                    yield
                    S.pe(lambda e, e_=e_, bd2=bd2, C=C: e.matmul(ps[:, bd2, e_ * 128:e_ * 128 + C], lhsT=ones128[0:C, :], rhs=Bm[0:C, e_, 0:C], start=True, stop=True), reads=[("Bm", e_), "ones128"], writes=psk(bd2))
                    yield
                S.act(lambda e, bd=bd, C=C: e.activation(out=E[0:C, :, 0:C], in_=ps[0:C, bd, 0:256].rearrange("p (a b) -> p a b", a=2)[:, :, 0:C], func=AF.Exp), reads=psk(bd), writes=["E"])
                yield
                S.act(lambda e, bd2=bd2, C=C: e.activation(out=E2[:, :, 0:C], in_=ps[:, bd2, 0:256].rearrange("p (a b) -> p a b", a=2)[:, :, 0:C], func=AF.Exp), reads=psk(bd2), writes=["E2"])
                yield
                S.dve(lambda e, C=C, isS=isS: e.tensor_tensor(out=DTm[0:C, :, 0:C], in0=E[0:C, :, 0:C], in1=INCL[isS][0:C, 0:C].unsqueeze(1).to_broadcast([C, 2, C]), op=ALU.mult), reads=["E", "gm"], writes=["DTm"])
                yield
                S.dve(lambda e, C=C, isS=isS: e.tensor_tensor(out=E[0:C, :, 0:C], in0=E[0:C, :, 0:C], in1=STRICT[isS][0:C, 0:C].unsqueeze(1).to_broadcast([C, 2, C]), op=ALU.mult), reads=["E", "gm", "DTm"], writes=["E"])
                yield
                for e_ in range(2):
                    hv = hv0 + e_
                    S.dve(lambda e, e_=e_, hv=hv, bkk=bkk, C=C, tl=tl: e.scalar_tensor_tensor(out=U[0:C, e_, 0:C], in0=ps[0:C, bkk, 0:C], scalar=beta_all[0:C, tl, hv:hv + 1], in1=E[0:C, e_, 0:C], op0=ALU.mult, op1=ALU.mult), reads=psk(bkk) + ["E"], writes=[("U", e_)])
                    yield
                    S.dve(lambda e, e_=e_, bkk=bkk, C=C: e.tensor_tensor(out=attnT2[pb][0:C, e_, 0:C], in0=ps[0:C, bkk, 128:128 + C], in1=DTm[0:C, e_, 0:C], op=ALU.mult), reads=psk(bkk) + ["DTm"], writes=[("attnT", e_, pb)])
                    yield
                    if isS:
                        pass
                    else:
                        S.pool(lambda e, e_=e_, c0=c0, C=C: e.tensor_tensor(out=qg2[pb][:, e_, 0:C], in0=qn[:, c0:c0 + C], in1=E2[:, e_, 0:C], op=ALU.mult), reads=[("n", 0), "E2"], writes=[("qg", e_, pb)])
                        yield
                    S.dve(lambda e, e_=e_, hv=hv, C=C, tl=tl: e.tensor_scalar(out=kd2[pb][0:C, e_, :], in0=ktok[0:C, :], scalar1=kdec_all[0:C, tl, hv:hv + 1], scalar2=None, op0=ALU.mult), reads=["ktok"], writes=[("kd", e_, pb)])
                    yield
                but = ps_next()
                pub = ps[:, but, :].bitcast(BF16)
                for e_ in range(2):
                    S.pe(lambda e, e_=e_, pub=pub, C=C: e.transpose(pub[0:C, e_ * 128:e_ * 128 + C], U[0:C, e_, 0:C], ident_b[0:C, 0:C]), reads=[("U", e_), "ident_b"], writes=psk(but))
                    yield
                S.act(lambda e, pub=pub, C=C: e.copy(out=UT[0:C, :, 0:C], in_=pub[0:C, 0:256].rearrange("p (a b) -> p a b", a=2)[:, :, 0:C]), reads=psk(but), writes=["UT"])
                yield
                if isS:
                    S.dve(lambda e, C=C: e.tensor_tensor(out=Nb[0][0:C, :, 0:C], in0=ident_f[0:C, 0:C].unsqueeze(1).to_broadcast([C, 2, C]), in1=U[0:C, :, 0:C], op=ALU.subtract), reads=[("U", 0), ("U", 1), "ident_f"], writes=[("N", 0)])
                    yield
                    Pprev, PTprev, Pk_, PTk_ = U, UT, ["U0", "U1"], ["UT"]
                    Pkeys_prev = [("U", 0), ("U", 1)]
                    PTkeys_prev = ["UT"]
                    ni = 0
                    for lv in range(1, L + 1):
                        pi = lv % 2
                        need_P = lv < L
                        if need_P:
                            b1 = ps_next()
                            for e_ in range(2):
                                S.pe(lambda e, e_=e_, b1=b1, C=C, PTprev=PTprev, Pprev=Pprev: e.matmul(ps[0:C, b1, e_ * 128:e_ * 128 + C], lhsT=PTprev[0:C, e_, 0:C], rhs=Pprev[0:C, e_, 0:C], start=True, stop=True), reads=Pkeys_prev + PTkeys_prev, writes=psk(b1))
                                yield
                            S.act(lambda e, b1=b1, C=C, pi=pi: e.copy(out=Pb[pi][0:C, :, 0:C], in_=ps[0:C, b1, 0:256].rearrange("p (a b) -> p a b", a=2)[:, :, 0:C]), reads=psk(b1), writes=[("P", pi)])
                            yield
                        b2 = ps_next()
                        for e_ in range(2):
                            S.pe(lambda e, e_=e_, b2=b2, C=C, PTprev=PTprev, Pprev=Pprev: e.matmul(ps[0:C, b2, e_ * 128:e_ * 128 + C], lhsT=Pprev[0:C, e_, 0:C], rhs=PTprev[0:C, e_, 0:C], start=True, stop=True), reads=Pkeys_prev + PTkeys_prev, writes=psk(b2))
                            yield
                        S.act(lambda e, b2=b2, C=C, pi=pi: e.copy(out=PTb[pi][0:C, :, 0:C], in_=ps[0:C, b2, 0:256].rearrange("p (a b) -> p a b", a=2)[:, :, 0:C]), reads=psk(b2), writes=[("PT", pi)])
                        yield
                        b3 = ps_next()
                        for e_ in range(2):
                            S.pe(lambda e, e_=e_, b3=b3, C=C, pi=pi, ni=ni: e.matmul(ps[0:C, b3, e_ * 128:e_ * 128 + C], lhsT=PTb[pi][0:C, e_, 0:C], rhs=Nb[ni][0:C, e_, 0:C], start=True, stop=True), reads=[("PT", pi), ("N", ni)], writes=psk(b3))
                            yield
                        S.dve(lambda e, b3=b3, C=C, ni=ni: e.tensor_tensor(out=Nb[1 - ni][0:C, :, 0:C], in0=ps[0:C, b3, 0:256].rearrange("p (a b) -> p a b", a=2)[:, :, 0:C], in1=Nb[ni][0:C, :, 0:C], op=ALU.add), reads=psk(b3) + [("N", ni)], writes=[("N", 1 - ni)])
                        yield
                        ni = 1 - ni
                        Pprev, PTprev = Pb[pi], PTb[pi]
                        Pkeys_prev = [("P", pi)]
                        PTkeys_prev = [("PT", pi)]

                else:
                    def _ev2(b, C=C):
                        return ps[0:C, b, 0:256].rearrange("p (a b) -> p a b", a=2)[:, :, 0:C]
                    idb = ident_f[0:C, 0:C].unsqueeze(1).to_broadcast([C, 2, C])
                    S.dve(lambda e: e.tensor_tensor(out=Pb[0][:, :, :], in0=U[:, :, :], in1=bm[:, 0, :].unsqueeze(1).to_broadcast([128, 2, 128]), op=ALU.mult), reads=[("U", 0), ("U", 1), "bm"], writes=[("P", 0)])
                    yield
                    S.dve(lambda e: e.tensor_tensor(out=PTb[0][:, :, :], in0=UT[:, :, :], in1=bm[:, 0, :].unsqueeze(1).to_broadcast([128, 2, 128]), op=ALU.mult), reads=["UT", "bm"], writes=[("PT", 0)])
                    yield
                    S.dve(lambda e, idb=idb: e.tensor_tensor(out=Nb[0][:, :, :], in0=idb, in1=Pb[0][:, :, :], op=ALU.subtract), reads=[("P", 0), "ident_f"], writes=[("N", 0)])
                    yield
                    S.dve(lambda e, idb=idb: e.tensor_tensor(out=NTb[0][:, :, :], in0=idb, in1=PTb[0][:, :, :], op=ALU.subtract), reads=[("PT", 0), "ident_f"], writes=[("NT", 0)])
                    yield
                    for mi in range(3):
                        S.pool(lambda e, mi=mi: e.tensor_tensor(out=UoM[mi][:, :, :], in0=U[:, :, :], in1=bm[:, mi + 1, :].unsqueeze(1).to_broadcast([128, 2, 128]), op=ALU.mult), reads=[("U", 0), ("U", 1), "bm"], writes=[("UoM", mi)])
                        yield
                        S.pool(lambda e, mi=mi: e.tensor_tensor(out=UoTM[mi][:, :, :], in0=UT[:, :, :], in1=bm[:, mi + 1, :].unsqueeze(1).to_broadcast([128, 2, 128]), op=ALU.mult), reads=["UT", "bm"], writes=[("UoTM", mi)])
                        yield
                    ni = 0
                    pprev = 0
                    for lv in range(1, 4):
                        pi = lv % 2
                        b1 = ps_next(); b2 = ps_next(); b3 = ps_next(); b4 = ps_next()
                        for e_ in range(2):
                            S.pe(lambda e, e_=e_, b1=b1, pprev=pprev: e.matmul(ps[:, b1, e_ * 128:(e_ + 1) * 128], lhsT=PTb[pprev][:, e_, :], rhs=Pb[pprev][:, e_, :], start=True, stop=True), reads=[("P", pprev), ("PT", pprev)], writes=psk(b1))
                            yield
                        for e_ in range(2):
                            S.pe(lambda e, e_=e_, b2=b2, pprev=pprev: e.matmul(ps[:, b2, e_ * 128:(e_ + 1) * 128], lhsT=Pb[pprev][:, e_, :], rhs=PTb[pprev][:, e_, :], start=True, stop=True), reads=[("P", pprev), ("PT", pprev)], writes=psk(b2))
                            yield
                        S.act(lambda e, b1=b1, pi=pi: e.copy(out=Pb[pi][:, :, :], in_=_ev2(b1)), reads=psk(b1), writes=[("P", pi)])
                        yield
                        S.act(lambda e, b2=b2, pi=pi: e.copy(out=PTb[pi][:, :, :], in_=_ev2(b2)), reads=psk(b2), writes=[("PT", pi)])
                        yield
                        for e_ in range(2):
                            S.pe(lambda e, e_=e_, b3=b3, pi=pi, ni=ni: e.matmul(ps[:, b3, e_ * 128:(e_ + 1) * 128], lhsT=PTb[pi][:, e_, :], rhs=Nb[ni][:, e_, :], start=True, stop=True), reads=[("PT", pi), ("N", ni)], writes=psk(b3))
                            yield
                        for e_ in range(2):
                            S.pe(lambda e, e_=e_, b4=b4, pi=pi, ni=ni: e.matmul(ps[:, b4, e_ * 128:(e_ + 1) * 128], lhsT=Pb[pi][:, e_, :], rhs=NTb[ni][:, e_, :], start=True, stop=True), reads=[("P", pi), ("NT", ni)], writes=psk(b4))
                            yield
                        S.dve(lambda e, b3=b3, ni=ni: e.tensor_tensor(out=Nb[1 - ni][:, :, :], in0=_ev2(b3), in1=Nb[ni][:, :, :], op=ALU.add), reads=psk(b3) + [("N", ni)], writes=[("N", 1 - ni)])
                        yield
                        S.dve(lambda e, b4=b4, ni=ni: e.tensor_tensor(out=NTb[1 - ni][:, :, :], in0=_ev2(b4), in1=NTb[ni][:, :, :], op=ALU.add), reads=psk(b4) + [("NT", ni)], writes=[("NT", 1 - ni)])
                        yield
                        ni = 1 - ni
                        pprev = pi
                    for mi in range(3):
                        lastm = (mi == 2)
                        b1 = ps_next()
                        for e_ in range(2):
                            S.pe(lambda e, e_=e_, b1=b1, ni=ni, mi=mi: e.matmul(ps[:, b1, e_ * 128:(e_ + 1) * 128], lhsT=UoTM[mi][:, e_, :], rhs=Nb[ni][:, e_, :], start=True, stop=True), reads=[("UoTM", mi), ("N", ni)], writes=psk(b1))
                            yield
                        S.act(lambda e, b1=b1: e.copy(out=Pb[1][:, :, :], in_=_ev2(b1)), reads=psk(b1), writes=[("P", 1)])
                        yield
                        if not lastm:
                            b3 = ps_next()
                            for e_ in range(2):
                                S.pe(lambda e, e_=e_, b3=b3, ni=ni, mi=mi: e.matmul(ps[:, b3, e_ * 128:(e_ + 1) * 128], lhsT=UoM[mi][:, e_, :], rhs=NTb[ni][:, e_, :], start=True, stop=True), reads=[("UoM", mi), ("NT", ni)], writes=psk(b3))
                                yield
                            S.act(lambda e, b3=b3: e.copy(out=PTb[1][:, :, :], in_=_ev2(b3)), reads=psk(b3), writes=[("PT", 1)])
                            yield
                        b2 = ps_next()
                        for e_ in range(2):
                            S.pe(lambda e, e_=e_, b2=b2, ni=ni: e.matmul(ps[:, b2, e_ * 128:(e_ + 1) * 128], lhsT=NTb[ni][:, e_, :], rhs=Pb[1][:, e_, :], start=True, stop=True), reads=[("NT", ni), ("P", 1)], writes=psk(b2))
                            yield
                        S.dve(lambda e, b2=b2, ni=ni, lastm=lastm: e.tensor_tensor(out=(Nfin[pb] if lastm else Nb[1 - ni])[:, :, :], in0=Nb[ni][:, :, :], in1=_ev2(b2), op=ALU.subtract), reads=psk(b2) + [("N", ni)], writes=[(("Nfin", pb) if lastm else ("N", 1 - ni))])
                        yield
                        if not lastm:
                            b4 = ps_next()
                            for e_ in range(2):
                                S.pe(lambda e, e_=e_, b4=b4, ni=ni: e.matmul(ps[:, b4, e_ * 128:(e_ + 1) * 128], lhsT=Nb[ni][:, e_, :], rhs=PTb[1][:, e_, :], start=True, stop=True), reads=[("N", ni), ("PT", 1)], writes=psk(b4))
                                yield
                            S.dve(lambda e, b4=b4, ni=ni: e.tensor_tensor(out=NTb[1 - ni][:, :, :], in0=NTb[ni][:, :, :], in1=_ev2(b4), op=ALU.subtract), reads=psk(b4) + [("NT", ni)], writes=[("NT", 1 - ni)])
                            yield
                        ni = 1 - ni
                if isS:
                    S.pool(lambda e, ni=ni, C=C: e.tensor_copy(out=Nfin[pb][0:C, :, 0:C], in_=Nb[ni][0:C, :, 0:C]), reads=[("N", ni)], writes=[("Nfin", pb)])
                    yield
                Nf = Nfin[pb]
                Nkey = ("Nfin", pb)
                if not isS:
                    S.pool(lambda e, C=C: e.tensor_copy(out=eGl[pb][:, 0:2], in_=E2[:, :, C - 1:C].rearrange("p a b -> p (a b)")), reads=["E2"], writes=[("eGl", pb)])
                    yield
                yield "SPLIT"
                if not isS:
                    if first:
                        S.act(lambda e, C=C: e.copy(out=xv[0:C].rearrange("p a b -> p (a b)"), in_=vtok2[pb][0:C].rearrange("p a b -> p (a b)")), reads=[("vtok", pb)], writes=["xv"])
                        yield
                    else:
                        pk_ = ps_next()
                        S.pe(lambda e, pk_=pk_, c0=c0, C=C: e.matmul(ps[0:C, pk_, 0:256], lhsT=kn[:, c0:c0 + C], rhs=Sbf[:].rearrange("p a b -> p (a b)"), start=True, stop=True), reads=[("n", 1), "Sbf"], writes=psk(pk_))
                        yield
                        for e_ in range(2):
                            hv = hv0 + e_
                            S.dve(lambda e, e_=e_, hv=hv, pk_=pk_, C=C, tl=tl: e.scalar_tensor_tensor(out=xv[0:C, e_, :], in0=ps[0:C, pk_, e_ * 128:(e_ + 1) * 128], scalar=negeG_all[0:C, tl, hv:hv + 1], in1=vtok2[pb][0:C, e_, :], op0=ALU.mult, op1=ALU.add), reads=psk(pk_) + [("vtok", pb)], writes=["xv"])
                            yield
                else:
                    S.dve(lambda e: e.tensor_tensor(out=X[:, :, :], in0=knf[:, :].unsqueeze(1).to_broadcast([128, NSEQ, TS]), in1=segmask[:, :, :], op=ALU.mult), reads=[("nf", 1), "segmask"], writes=["X"])
                    yield
                    pks = [ps_next(), ps_next()]
                    K.P.reserved = set(pks)
                    def _ldA(n):
                        k_ = n % 8
                        S.dma("sp", "gsin%d" % k_, lambda e, k_=k_, s2=n % NSEQ, hv2=hv0 + n // NSEQ: e.dma_start(out=SinR[k_], in_=K.s_gdn[s2, hv2]), writes=[("SinR", k_)])
                    for n_ in range(RDEPTH):
                        _ldA(n_)
                        yield
                    for e_ in range(2):
                        hv = hv0 + e_
                        for s_ in range(NSEQ):
                            n_ = e_ * NSEQ + s_
                            k_ = n_ % 8
                            S.pe(lambda e, e_=e_, s_=s_, k_=k_, pks=pks: e.matmul(ps[0:TS, pks[e_], 0:128], lhsT=X[:, s_, :], rhs=SinR[k_], start=(s_ == 0), stop=(s_ == NSEQ - 1)), reads=["X", ("SinR", k_)], writes=psk(pks[e_]))
                            yield
                            if n_ + RDEPTH < NLD:
                                _ldA(n_ + RDEPTH)
                                yield
                        S.dve(lambda e, e_=e_, hv=hv, tl=tl, pks=pks: e.scalar_tensor_tensor(out=xv[0:TS, e_, :], in0=ps[0:TS, pks[e_], 0:128], scalar=negeG_all[0:TS, tl, hv:hv + 1], in1=vtok2[pb][0:TS, e_, :], op0=ALU.mult, op1=ALU.add), reads=psk(pks[e_]) + [("vtok", pb)], writes=["xv"])
                        yield
                    K.P.reserved = set()
                pv = ps_next()
                for e_ in range(2):
                    S.pe(lambda e, e_=e_, pv=pv, C=C, Nf=Nf: e.matmul(ps[0:C, pv, e_ * 128:(e_ + 1) * 128], lhsT=Nf[0:C, e_, 0:C], rhs=xv[0:C, e_, :], start=True, stop=True), reads=[Nkey, "xv"], writes=psk(pv))
                    yield
                for e_ in range(2):
                    hv = hv0 + e_
                    S.act(lambda e, e_=e_, hv=hv, pv=pv, C=C, tl=tl: e.activation(out=vnew[0:C, e_, :], in_=ps[0:C, pv, e_ * 128:(e_ + 1) * 128], func=AF.Identity, scale=beta_all[0:C, tl, hv:hv + 1]), reads=psk(pv), writes=[("vnew", e_)])
                    yield
                if not isS:
                    po = ps_next()
                    po_aps = [ps[0:C, po, 0:128], ps[0:C, po, 128:256]]
                    po_keys = [psk(po), psk(po)]
                    for e_ in range(2):
                        if not first:
                            S.pe(lambda e, e_=e_, C=C, po_aps=po_aps: e.matmul(po_aps[e_], lhsT=qg2[pb][:, e_, 0:C], rhs=Sbf[:, e_, :], start=True, stop=False), reads=[("qg", e_, pb), "Sbf"], writes=po_keys[e_])
                            yield
                        S.pe(lambda e, e_=e_, C=C, po_aps=po_aps, first=first: e.matmul(po_aps[e_], lhsT=attnT2[pb][0:C, e_, 0:C], rhs=vnew[0:C, e_, :], start=first, stop=True), reads=[("attnT", e_, pb), ("vnew", e_)], writes=po_keys[e_])
                        yield
                    pS_ = ps_next()
                    for e_ in range(2):
                        S.pe(lambda e, e_=e_, pS_=pS_, C=C: e.matmul(ps[:, pS_, e_ * 128:(e_ + 1) * 128], lhsT=kd2[pb][0:C, e_, :], rhs=vnew[0:C, e_, :], start=True, stop=True), reads=[("kd", e_, pb), ("vnew", e_)], writes=psk(pS_))
                        yield
                    for e_ in range(2):
                        hv = hv0 + e_
                        if first:
                            S.dve(lambda e, e_=e_, pS_=pS_: e.tensor_copy(out=S32[:, e_, :], in_=ps[:, pS_, e_ * 128:(e_ + 1) * 128]), reads=psk(pS_), writes=[("S32", e_)])
                            yield
                        else:
                            S.dve(lambda e, e_=e_, pS_=pS_, C=C: e.scalar_tensor_tensor(out=S32[:, e_, :], in0=S32[:, e_, :], scalar=eGl[pb][:, e_:e_ + 1], in1=ps[:, pS_, e_ * 128:(e_ + 1) * 128], op0=ALU.mult, op1=ALU.add), reads=psk(pS_) + [("S32", e_), ("eGl", pb)], writes=[("S32", e_)])
                            yield
                        if last:
                            S.dma("sp", "gpo", lambda e, e_=e_, hv=hv: e.dma_start(out=K.gdn_p[hv], in_=S32[:, e_, :]), reads=[("S32", e_)])
                            yield
                    if not last:
                        S.act(lambda e: e.copy(out=Sbf[:].rearrange("p a b -> p (a b)"), in_=S32[:].rearrange("p a b -> p (a b)")), reads=[("S32", 0), ("S32", 1)], writes=["Sbf"])
                        yield
                else:
                    pos = [ps_next(), ps_next()]
                    K.P.reserved = set(pos)
                    po_aps = [ps[0:TS, pos[0], 0:128], ps[0:TS, pos[1], 0:128]]
                    po_keys = [psk(pos[0]), psk(pos[1])]
                    def _ldB(n):
                        k_ = n % 8
                        S.dma("sp", "gsin%d" % k_, lambda e, k_=k_, s2=n % NSEQ, hv2=hv0 + n // NSEQ: e.dma_start(out=SinR[k_], in_=K.s_gdn[s2, hv2]), writes=[("SinR", k_)])
                    for n_ in range(RDEPTH):
                        _ldB(n_)
                        yield
                    for e_ in range(2):
                        hv = hv0 + e_
                        S.pe(lambda e, e_=e_, po_aps=po_aps: e.matmul(po_aps[e_], lhsT=attnT2[pb][0:TS, e_, 0:TS], rhs=vnew[0:TS, e_, :], start=True, stop=False), reads=[("attnT", e_, pb), ("vnew", e_)], writes=po_keys[e_])
                        yield
                        S.pool(lambda e, e_=e_: e.tensor_tensor(out=qgf[:, :], in0=qnf[:, :], in1=E2[:, e_, 0:TS], op=ALU.mult), reads=[("nf", 0), "E2"], writes=["qgf"])
                        yield
                        S.dve(lambda e: e.tensor_tensor(out=X[:, :, :], in0=qgf[:, :].unsqueeze(1).to_broadcast([128, NSEQ, TS]), in1=segmask[:, :, :], op=ALU.mult), reads=["qgf", "segmask"], writes=["X"])
                        yield
                        for s_ in range(NSEQ):
                            n_ = e_ * NSEQ + s_
                            k_ = n_ % 8
                            j_ = n_ % 3
                            S.pe(lambda e, e_=e_, s_=s_, k_=k_, po_aps=po_aps: e.matmul(po_aps[e_], lhsT=X[:, s_, :], rhs=SinR[k_], start=False, stop=(s_ == NSEQ - 1)), reads=["X", ("SinR", k_)], writes=po_keys[e_])
                            yield
                            S.dve(lambda e, e_=e_, s_=s_: e.tensor_scalar(out=kdm[0:TS, :], in0=kd2[pb][0:TS, e_, :], scalar1=segcol[0:TS, s_:s_ + 1], scalar2=None, op0=ALU.mult), reads=[("kd", e_, pb), "segcol"], writes=["kdm"])
                            yield
                            pS_ = ps_next()
                            S.pe(lambda e, e_=e_, pS_=pS_: e.matmul(ps[:, pS_, 0:128], lhsT=kdm[0:TS, :], rhs=vnew[0:TS, e_, :], start=True, stop=True), reads=["kdm", ("vnew", e_)], writes=psk(pS_))
                            yield
                            S.dve(lambda e, e_=e_, s_=s_, k_=k_, j_=j_, pS_=pS_: e.scalar_tensor_tensor(out=SoutR[j_], in0=SinR[k_], scalar=E2[:, e_, 4 * s_ + 3:4 * s_ + 4], in1=ps[:, pS_, 0:128], op0=ALU.mult, op1=ALU.add), reads=psk(pS_) + [("SinR", k_), "E2"], writes=[("SoutR", j_)])
                            yield
                            if n_ + RDEPTH < NLD:
                                _ldB(n_ + RDEPTH)
                                yield
                            S.dma("sp", "gsout%d" % j_, lambda e, s_=s_, hv=hv, j_=j_: e.dma_start(out=K.gdn_s[s_, hv], in_=SoutR[j_]), reads=[("SoutR", j_)])
                            yield
                    K.P.reserved = set()
                for e_ in range(2):
                    _f = lambda e, e_=e_, C=C, po_aps=po_aps: e.activation(out=acc[0:C, e_ * 128:(e_ + 1) * 128], in_=po_aps[e_], func=AF.Square, accum_out=ss[0:C, e_:e_ + 1])
                    _f._multi = True
                    S.act(_f, reads=po_keys[e_], writes=[("acc", 0), ("ss", e_)])
                    yield
                S.act(lambda e, C=C: e.activation(out=ss[0:C, 2:4], in_=ss[0:C, 0:2], func=AF.Sqrt, bias=EPS, scale=1.0 / 128), reads=[("ss", 0), ("ss", 1)], writes=["ss2"])
                yield
                S.dve(lambda e, C=C: e.reciprocal(out=ss[0:C, 2:4], in_=ss[0:C, 2:4]), reads=["ss2"], writes=["ss2"])
                yield
                for e_ in range(2):
                    S.dve(lambda e, e_=e_, C=C, po_aps=po_aps: e.scalar_tensor_tensor(out=og[0:C, e_ * 128:(e_ + 1) * 128], in0=po_aps[e_], scalar=ss[0:C, 2 + e_:3 + e_], in1=szn2[pb][0:C, e_ * 128:(e_ + 1) * 128], op0=ALU.mult, op1=ALU.mult), reads=po_keys[e_] + ["ss2", ("szn", pb)], writes=[("og", e_)])
                    yield
                bt = ps_next()
                ptb = ps[:, bt, :].bitcast(BF16)
                for e_ in range(2):
                    S.pe(lambda e, e_=e_, ptb=ptb, C=C: e.transpose(ptb[:, e_ * C:(e_ + 1) * C], og[0:C, e_ * 128:(e_ + 1) * 128], ident_b[0:C, 0:C]), reads=[("og", e_), "ident_b"], writes=psk(bt))
                    yield
                S.act(lambda e, ptb=ptb, C=C, c0=c0: e.copy(out=ogT[:, :, c0:c0 + C], in_=ptb[:, 0:2 * C].rearrange("p (a b) -> p a b", a=2)), reads=psk(bt), writes=["ogT"])
                yield
            chunks_ = list(range(0, nt, C))
            gens = [chunk_gen(c0_, i_ % 2) for i_, c0_ in enumerate(chunks_)]
            PIPE = (not isS) and os.environ.get("GDN_PIPE", "1") == "1"

            def _adv_split(g):
                for x_ in g:
                    if x_ == "SPLIT":
                        return
            if not PIPE:
                for g in gens:
                    for _ in g:
                        pass
            else:
                _adv_split(gens[0])
                for i_ in range(len(gens)):
                    nxt = gens[i_ + 1] if i_ + 1 < len(gens) else None
                    a_done = False
                    b_done = nxt is None
                    while not (a_done and b_done):
                        if not a_done:
                            try:
                                next(gens[i_])
                            except StopIteration:
                                a_done = True
                        if not b_done:
                            try:
                                if next(nxt) == "SPLIT":
                                    b_done = True
                            except StopIteration:
                                b_done = True
            for dc8 in range(KC):
                if hk >= 1 and 'F' in CUT:
                    continue
                b = ps_next()
                for e_ in range(2):
                    S.pe(lambda e, b=b, e_=e_, dc8=dc8, nt=nt: e.matmul(ps[:, b, 0:nt], lhsT=wo[:, e_, dc8 * 128:(dc8 + 1) * 128], rhs=ogT[:, e_, 0:nt], start=(e_ == 0), stop=(e_ == 1)), reads=[wok, "ogT"], writes=psk(b))
                K.resid_update(ps[:, b, 0:nt], gbase, dc8, ("S" if isS else bi), t0, nt, psk(b))
        if hk >= 1 and 'G' in CUT:
            S.barrier(); continue
        bo1 = ps_next()
        for ci in range(4):
            S.pe(lambda e, ci=ci, bo1=bo1: e.transpose(ps[0:3, bo1, ci * 128:(ci + 1) * 128], tailP[:, ci, :], ident_f[:]), reads=["tailP", "ident_f"], writes=psk(bo1))
        S.dve(lambda e, bo1=bo1: e.tensor_copy(out=ostg[0:3, :], in_=ps[0:3, bo1, :]), reads=psk(bo1), writes=[("acc", 0)])
        for ci, cid in enumerate(cids):
            S.dma("sp", "gco", lambda e, ci=ci, cid=cid: e.dma_start(out=K.gconv_p[:, cid * 128:(cid + 1) * 128], in_=ostg[0:3, ci * 128:(ci + 1) * 128]), reads=[("acc", 0)])
        bo2 = ps_next()
        for ci in range(4):
            S.pe(lambda e, ci=ci, bo2=bo2: e.transpose(ps[0:48, bo2, ci * 128:(ci + 1) * 128], tailS[:, ci, :, :].rearrange("p s r -> p (s r)"), ident_f[:]), reads=["tailS", "ident_f"], writes=psk(bo2))
        S.dve(lambda e, bo2=bo2: e.tensor_copy(out=ostg[0:48, :], in_=ps[0:48, bo2, :]), reads=psk(bo2), writes=[("acc", 0)])
        for ci, cid in enumerate(cids):
            S.dma("sp", "gco", lambda e, ci=ci, cid=cid: e.dma_start(out=K.gconv_s[:, cid * 128:(cid + 1) * 128], in_=ostg[0:48, ci * 128:(ci + 1) * 128]), reads=[("acc", 0)])
        S.barrier()
    S.barrier()

_CACHE = {}


def make_in_maps(inp):
    f = lambda a: np.ascontiguousarray(np.asarray(a, dtype=np.float32))
    consts = host_consts()
    shared = {
        "w_ada": f(inp["w_ada"]), "b_ada": f(inp["b_ada"]),
        "w_ada_final": f(inp["w_ada_final"]), "b_ada_final": f(inp["b_ada_final"]).reshape(1, -1),
        "w_ret_in": f(inp["w_ret_in"][0]), "w_ret_out": f(inp["w_ret_out"][0]),
        "w_gdn_in": f(inp["w_gdn_in"][0]), "w_gdn_out": f(inp["w_gdn_out"][0]),
        "w_gdn_conv": f(inp["w_gdn_conv"][0]),
        "gdn_vec": f(np.concatenate([inp["gdn_a_log"][0], inp["gdn_dt_bias"][0]])).reshape(1, 32),
        "gdn_norm": f(inp["gdn_norm"]).reshape(1, 128),
        "w_ffn_up": f(inp["w_ffn_up"]), "w_ffn_down": f(inp["w_ffn_down"]),
        "ffn_vec": f(np.concatenate([np.asarray(inp["w_ffn_dw"]).reshape(6, DFF), np.asarray(inp["b_ffn_dw"]).reshape(2, DFF)], axis=0)),
    }
    shared.update(consts)
    maps = []
    for c in range(NCORES):
        sl = slice(NSEQ * c, NSEQ * (c + 1))
        m = dict(shared)
        m["xp"] = f(inp["x_prompt"][c])
        m["xs"] = f(np.asarray(inp["x_sample"][sl]).reshape(TS, D))
        m["s_ret"] = f(inp["state_ret"][0, sl])
        m["s_gdn"] = f(inp["state_gdn"][0, sl])
        m["s_gconv"] = f(np.asarray(inp["state_gdn_conv"][0, sl]).reshape(NSEQ * 3, 4096))
        m["s_fconv"] = f(np.asarray(inp["state_ffn_conv"][:, sl]).reshape(2, NSEQ * 2, DFF))
        m["vec22"] = f(np.concatenate([np.asarray(inp["c_prompt"][c:c + 1]), np.asarray(inp["c_sample"][sl]),
                                        np.asarray(inp["norm_mix"]), np.asarray(inp["norm_ffn"]), np.asarray(inp["norm_final"]).reshape(1, D)], axis=0))
        maps.append(m)
    return maps


def kernel(**inp):
    if "nc" not in _CACHE:
        _CACHE["nc"] = build_program()
    nc = _CACHE["nc"]
    maps = make_in_maps(inp)
    res = run_bass_kernel_spmd(nc, maps, core_ids=list(range(NCORES)))
    R = res.results
    cat = lambda k: np.stack([np.asarray(R[c][k]) for c in range(NCORES)], axis=0)
    y_prompt = cat("y_p")
    y_sample = cat("y_s").reshape(128, 4, D)
    ret_p = cat("ret_p")[None]
    gdn_p = cat("gdn_p")[None]
    gconv_p = cat("gconv_p")[None]
    fconv_p = np.transpose(cat("fconv_p"), (1, 0, 2, 3))
    ret_s = cat("ret_s").reshape(1, 128, RET_H, 256, 512)
    gdn_s = cat("gdn_s").reshape(1, 128, GDN_HV, 128, 128)
    gconv_s = cat("gconv_s").reshape(1, 128, 3, 4096)
    fconv_s = np.transpose(cat("fconv_s").reshape(NCORES, 2, NSEQ, 2, DFF), (1, 0, 2, 3, 4)).reshape(2, 128, 2, DFF)
    return (y_prompt.astype(np.float32), y_sample.astype(np.float32), ret_p.astype(np.float32), gdn_p.astype(np.float32),
            gconv_p.astype(np.float32), fconv_p.astype(np.float32), ret_s.astype(np.float32), gdn_s.astype(np.float32),
            gconv_s.astype(np.float32), fconv_s.astype(np.float32))
```

```python
import math
import bisect
import numpy as np
from contextlib import ExitStack
import concourse.bass as bass
import concourse.mybir as mybir
from concourse.bass_utils import run_bass_kernel_spmd

F32 = mybir.dt.float32
BF16 = mybir.dt.bfloat16
AF = mybir.ActivationFunctionType
ALU = mybir.AluOpType

NCORES = 8
D = 1024
KC = 8
TP = 2048
NSEQ = 16
TS = 64
TT = TP + TS
DFF = 2816
NF = 22
EPS = 1e-6
RET_H = 4
GDN_HV = 16
GDN_HK = 8
PAST = 16384
SAME_ENGINE_SYNC = True
ATTACH_WAITS = True
DEBUG = {}


class _Op:
    __slots__ = ("eng", "fn", "deps", "dma_key", "sig", "cnt", "idx")

    def __init__(self, eng, fn, deps, dma_key, idx):
        self.eng = eng
        self.fn = fn
        self.deps = deps
        self.dma_key = dma_key
        self.sig = False
        self.cnt = 0
        self.idx = idx


class Sched:
    ENGS = ("pe", "act", "dve", "pool", "sp")

    def __init__(self, nc):
        self.nc = nc
        self.ops = []
        self.last_w = {}
        self.readers = {}
        self.last_eng = {}
        self.dmas_since_bar = []

    def op(self, eng, fn, reads=(), writes=(), dma_key=None, extra=()):
        deps = set(extra)
        for r in reads:
            w = self.last_w.get(r)
            if w is not None:
                deps.add(w)
        for w_ in writes:
            w = self.last_w.get(w_)
            if w is not None:
                deps.add(w)
            deps |= self.readers.get(w_, set())
        idx = len(self.ops)
        deps.discard(idx)
        self.ops.append(_Op(eng, fn, deps, dma_key, idx))
        for r in reads:
            self.readers.setdefault(r, set()).add(idx)
        for w_ in writes:
            self.last_w[w_] = idx
            self.readers[w_] = set()
        self.last_eng[eng] = idx
        if dma_key is not None:
            self.dmas_since_bar.append(idx)
        return idx

    def pe(self, fn, reads=(), writes=()):
        return self.op("pe", fn, reads, writes)

    def act(self, fn, reads=(), writes=()):
        return self.op("act", fn, reads, writes)

    def dve(self, fn, reads=(), writes=()):
        return self.op("dve", fn, reads, writes)

    def pool(self, fn, reads=(), writes=()):
        return self.op("pool", fn, reads, writes)

    def dma(self, q, key, fn, reads=(), writes=()):
        return self.op(q, fn, reads, writes, dma_key=key)

    def barrier(self):
        deps = set(self.last_eng.values()) | set(self.dmas_since_bar)
        self.dmas_since_bar = []
        for e in self.ENGS:
            self.op(e, None, extra=deps)
        self.last_w = {}
        self.readers = {}

    def emit(self, stack):
        nc = self.nc
        ops = self.ops

        def needs(c, p):
            if p.fn is None:
                return False
            if p.dma_key is not None:
                return True
            if p.eng == c.eng:
                if p.eng == "pe":
                    return False
                return SAME_ENGINE_SYNC
            return True

        for c in ops:
            for d in c.deps:
                p = ops[d]
                if needs(c, p):
                    p.sig = True
        eng_cnt = {e: 0 for e in self.ENGS}
        dma_cnt = {}
        dma_keys = []
        dma_issue_idx = {}
        for o in ops:
            if o.dma_key is not None:
                if o.dma_key not in dma_cnt:
                    dma_cnt[o.dma_key] = 0
                    dma_keys.append(o.dma_key)
                    dma_issue_idx[o.dma_key] = []
                dma_cnt[o.dma_key] += 1
                dma_issue_idx[o.dma_key].append(o.idx)
            elif o.sig:
                eng_cnt[o.eng] += 1
                o.cnt = eng_cnt[o.eng]
        sems = {}
        for e in self.ENGS:
            sems[("e", e)] = stack.enter_context(nc.semaphore("s_" + e))
        for k in dma_keys:
            sems[("d", k)] = stack.enter_context(nc.semaphore("d_" + str(k)))
        per_eng = {e: [o for o in ops if o.eng == e] for e in self.ENGS}
        block = stack.enter_context(nc.Block())

        def run_engine(ename, eobj):
            waited = {}
            for o in per_eng[ename]:
                need = {}
                for d in o.deps:
                    p = ops[d]
                    if not needs(o, p):
                        continue
                    if p.dma_key is not None:
                        key = ("d", p.dma_key)
                        val = 16 * bisect.bisect_left(dma_issue_idx[p.dma_key], o.idx)
                    else:
                        key = ("e", p.eng)
                        val = p.cnt
                    if need.get(key, 0) < val:
                        need[key] = val
                pend = [(key, val) for key, val in need.items() if waited.get(key, 0) < val]
                for key, val in pend:
                    waited[key] = val
                fuse = (ATTACH_WAITS and o.fn is not None and o.dma_key is None and ename in ("act", "dve", "pool", "pe")
                        and not getattr(o.fn, "_multi", False) and len(pend) > 0)
                for key, val in (pend[:-1] if fuse else pend):
                    eobj.wait_ge(sems[key], val)
                if o.fn is None:
                    continue
                ins = o.fn(eobj)
                if fuse:
                    ins._wait_ge(sems[pend[-1][0]], pend[-1][1])
                if o.dma_key is not None:
                    ins.then_inc(sems[("d", o.dma_key)], 16)
                elif o.sig:
                    ins.then_inc(sems[("e", ename)], 1)
            for k in dma_keys:
                if any(o.dma_key == k for o in per_eng[ename]):
                    eobj.wait_ge(sems[("d", k)], 16 * dma_cnt[k])

        @block.tensor
        def _(e):
            run_engine("pe", e)

        @block.scalar
        def _(e):
            run_engine("act", e)

        @block.vector
        def _(e):
            run_engine("dve", e)

        @block.gpsimd
        def _(e):
            run_engine("pool", e)

        @block.sync
        def _(e):
            run_engine("sp", e)


def _gammas():
    return (1.0 - 2.0 ** (-5.0 - np.arange(RET_H, dtype=np.float64)))


def host_consts():
    c = {}
    c["c_ident"] = np.eye(128, dtype=np.float32)
    half = 128
    inv_freq = (np.float32(10000.0) ** (-(np.arange(half, dtype=np.float32)) / np.float32(half))).astype(np.float32)
    pos = np.concatenate([np.arange(TP, dtype=np.float32), (PAST + (np.arange(TS) % 4)).astype(np.float32)])
    ang = (pos[None, :] * inv_freq[:, None]).astype(np.float32)
    cos = np.cos(ang.astype(np.float64))
    sin = np.sin(ang.astype(np.float64))
    g = _gammas()
    pin = np.concatenate([np.arange(TP) % 128, np.arange(TS) % 4]).astype(np.float64)
    rope = np.zeros((RET_H + 1, 2, 128, TT), np.float32)
    for h in range(RET_H):
        dec = g[h] ** (pin + 1.0)
        rope[h, 0] = cos * dec[None, :]
        rope[h, 1] = sin * dec[None, :]
    rope[RET_H, 0] = cos * (256.0 ** -0.5)
    rope[RET_H, 1] = sin * (256.0 ** -0.5)
    c["rope"] = rope
    mP = np.zeros((RET_H, 128, 128), np.float32)
    mS = np.zeros((RET_H, 128, 128), np.float32)
    ks = np.zeros((128, 2 * RET_H), np.float32)
    jj = np.arange(128)
    for h in range(RET_H):
        mP[h] = np.where(jj[None, :] >= jj[:, None], g[h] ** (-(jj[:, None] + 1.0)), 0.0)
        j4 = jj[:64] % 4
        same = (jj[:64, None] // 4) == (jj[None, :64] // 4)
        mS[h, :64, :64] = np.where(same & (jj[None, :64] >= jj[:64, None]), g[h] ** (-(j4[:, None] + 1.0)), 0.0)
        ks[:, h] = g[h] ** (127.0 - jj)
        ks[:64, 4 + h] = g[h] ** (3.0 - j4)
    c["retmask"] = np.concatenate([mP, mS], axis=0)
    c["kscale"] = ks
    seg = np.zeros((128, NSEQ, 64), np.float32)
    for s in range(NSEQ):
        seg[:, s, 4 * s:4 * s + 4] = 1.0
    c["segmask"] = seg.reshape(128, NSEQ * 64)
    segcol = np.zeros((128, NSEQ), np.float32)
    for s in range(NSEQ):
        segcol[4 * s:4 * s + 4, s] = 1.0
    c["segcol"] = segcol
    gm = np.zeros((8, 128, 128), np.float32)
    a = np.arange(128)
    gm[0] = (a[:, None] <= a[None, :])
    gm[1] = (a[:, None] > a[None, :])
    gm[2] = (a[None, :] >= a[:, None])
    gm[3] = (a[None, :] > a[:, None])
    b = np.arange(64)
    same = (b[:, None] // 4) == (b[None, :] // 4)
    gm[4, :64, :64] = (b[:, None] <= b[None, :]) & same
    gm[5, :64, :64] = (b[:, None] > b[None, :]) & same
    gm[6, :64, :64] = (b[None, :] >= b[:, None]) & same
    gm[7, :64, :64] = (b[None, :] > b[:, None]) & same
    c["gmask"] = np.ascontiguousarray(gm.transpose(1, 0, 2)).reshape(128, 8 * 128)
    bmk = np.zeros((4, 128, 128), np.float32)
    bmk[0] = (a[:, None] // 16) == (a[None, :] // 16)
    for mi, m in enumerate([32, 64, 128]):
        bmk[mi + 1] = ((a[:, None] // m) == (a[None, :] // m)) & ((a[:, None] // (m // 2)) != (a[None, :] // (m // 2)))
    c["bmask"] = np.ascontiguousarray(bmk.transpose(1, 0, 2)).reshape(128, 4 * 128)
    return c


class Ctx:
    pass


def build_program(dbg=()):
    nc = bass.Bass("TRN2", target_bir_lowering=False)
    K = Ctx()
    K.nc = nc
    ins = {}

    def din(name, shape):
        ins[name] = nc.dram_tensor(name, list(shape), F32, kind="ExternalInput").ap()
        return ins[name]

    def dout(name, shape):
        return nc.dram_tensor(name, list(shape), F32, kind="ExternalOutput").ap()

    xp = din("xp", [TP, D]); xs = din("xs", [TS, D])
    s_ret = din("s_ret", [NSEQ, RET_H, 256, 512])
    s_gdn = din("s_gdn", [NSEQ, GDN_HV, 128, 128])
    s_gconv = din("s_gconv", [NSEQ * 3, 4096])
    s_fconv = din("s_fconv", [2, NSEQ * 2, DFF])
    vec22 = din("vec22", [22, D])
    w_ada = din("w_ada", [2, D, 6 * D]); b_ada = din("b_ada", [2, 6 * D])
    w_adaf = din("w_ada_final", [D, 2 * D]); b_adaf = din("b_ada_final", [1, 2 * D])
    w_ret_in = din("w_ret_in", [D, 6144]); w_ret_out = din("w_ret_out", [2048, D])
    w_gdn_in = din("w_gdn_in", [D, 6176]); w_gdn_out = din("w_gdn_out", [2048, D])
    w_gconv = din("w_gdn_conv", [4, 4096])
    gdn_vec = din("gdn_vec", [1, 32])
    gdn_norm = din("gdn_norm", [1, 128])
    w_up = din("w_ffn_up", [2, D, 2 * DFF]); w_down = din("w_ffn_down", [2, DFF, D])
    ffn_vec = din("ffn_vec", [8, DFF])
    c_ident = din("c_ident", [128, 128])
    c_rope = din("rope", [RET_H + 1, 2, 128, TT])
    c_retmask = din("retmask", [2 * RET_H, 128, 128])
    c_kscale = din("kscale", [128, 2 * RET_H])
    c_segmask = din("segmask", [128, NSEQ * 64])
    c_segcol = din("segcol", [128, NSEQ])
    c_gmask = din("gmask", [128, 8 * 128])
    c_bmask = din("bmask", [128, 4 * 128])

    y_p = dout("y_p", [TP, D]); y_s = dout("y_s", [TS, D])
    ret_p = dout("ret_p", [RET_H, 256, 512]); gdn_p = dout("gdn_p", [GDN_HV, 128, 128])
    gconv_p = dout("gconv_p", [3, 4096]); fconv_p = dout("fconv_p", [2, 2, DFF])
    ret_s = dout("ret_s", [NSEQ, RET_H, 256, 512]); gdn_s = dout("gdn_s", [NSEQ, GDN_HV, 128, 128])
    gconv_s = dout("gconv_s", [NSEQ * 3, 4096]); fconv_s = dout("fconv_s", [2, NSEQ * 2, DFF])
    dbg_out = {n: dout("dbg_" + n, shp) for n, shp in dbg}

    with ExitStack() as st:
        def T(name, shape, dt):
            return st.enter_context(nc.sbuf_tensor(name, list(shape), dt))

        S = Sched(nc)
        xT = T("xT", [128, KC, TT], F32)
        hT = T("hT", [128, KC, TT], BF16)
        wring = T("wring", [128, 4, 4096], BF16)
        modT = T("modT", [128, 112, 17], F32)
        vecT = T("vecT", [128, KC, 22], F32)
        Amod = T("Amod", [128, 5, KC, 17], F32)
        csT = T("csT", [128, KC, 17], F32)
        ident_f = T("ident_f", [128, 128], F32)
        ident_b = T("ident_b", [128, 128], BF16)
        ones_b = T("ones_b", [128, 128], BF16)
        ones_f = T("ones_f", [128, 32], F32)
        ffnvT = T("ffnvT", [128, NF, 8], F32)
        fcarry = T("fcarry", [128, NF, 2], F32)
        ARENA = 15000
        arena = T("arena", [128, ARENA], F32)
        ps = st.enter_context(nc.psum_tensor("ps", [128, 8, 512], F32))

        A = Ctx()
        A.off = 0

        def a_reset():
            A.off = 0

        def a_f32(*free, parts=128):
            n = int(np.prod(free))
            assert A.off + n <= ARENA, ("arena overflow", A.off, n)
            ap = arena[0:parts, A.off:A.off + n]
            A.off += n
            if len(free) == 2:
                ap = ap.rearrange("p (a b) -> p a b", a=free[0])
            elif len(free) == 3:
                ap = ap.rearrange("p (a b c) -> p a b c", a=free[0], b=free[1])
            return ap

        def a_bf16(*free, parts=128):
            n = int(np.prod(free))
            nf = (n + 1) // 2
            assert A.off + nf <= ARENA, ("arena overflow", A.off, nf)
            ap = arena[0:parts, A.off:A.off + nf].bitcast(BF16)[:, 0:n]
            A.off += nf
            if len(free) == 2:
                ap = ap.rearrange("p (a b) -> p a b", a=free[0])
            elif len(free) == 3:
                ap = ap.rearrange("p (a b c) -> p a b c", a=free[0], b=free[1])
            return ap

        P = Ctx()
        P.i = 0

        P.reserved = set()

        def ps_next(n=1):
            while True:
                if P.i + n > 8:
                    P.i = 0
                b = P.i
                P.i = (P.i + n) % 8
                if not any((b + i) in P.reserved for i in range(n)):
                    return b

        def psk(b, n=1):
            return [("ps", b + i) for i in range(n)]

        W = Ctx()
        W.i = 0

        def wslot():
            i = W.i
            W.i = (W.i + 1) % 4
            return i

        def load_w_bf16(src2d, kc, ncols, row0=0, col0=0, slot=None, off=0, key=None):
            i = wslot() if slot is None else slot
            view = wring[:, i, off:off + kc * ncols].rearrange("p (k c) -> p k c", k=kc)
            src = src2d[row0:row0 + kc * 128, col0:col0 + ncols].rearrange("(k p) c -> p k c", p=128)
            k_ = ("w", i) if key is None else key
            S.dma("pool", "w%d" % i, lambda e: e.dma_start(out=view, in_=src), writes=[k_])
            return view, k_

        def load_w_f32(src2d, kc, ncols, row0=0, col0=0):
            i = wslot()
            view = wring[:, i, :].bitcast(F32)[:, 0:kc * ncols].rearrange("p (k c) -> p k c", k=kc)
            src = src2d[row0:row0 + kc * 128, col0:col0 + ncols].rearrange("(k p) c -> p k c", p=128)
            S.dma("sp", "wf%d" % i, lambda e: e.dma_start(out=view, in_=src), writes=[("w", i)])
            return view, ("w", i)

        BLOCKS_P = [(0, 512), (512, 512), (1024, 512), (1536, 512)]
        BLOCK_S = (TP, TS)
        ALLBLOCKS = BLOCKS_P + [BLOCK_S]

        S.dma("sp", "c_id", lambda e: e.dma_start(out=ident_f[:], in_=c_ident), writes=["ident_f"])
        S.dve(lambda e: e.tensor_copy(out=ident_b[:], in_=ident_f[:]), reads=["ident_f"], writes=["ident_b"])
        S.dve(lambda e: e.memset(ones_b[:], 1.0), writes=["ones_b"])
        S.dve(lambda e: e.memset(ones_f[:], 1.0), writes=["ones_f"])
        a_reset()
        stage22 = a_f32(D, parts=22)
        S.dma("sp", "c_s22", lambda e: e.dma_start(out=stage22, in_=vec22), writes=["stage22"])
        b0 = ps_next()
        for kc in range(KC):
            S.pe(lambda e, kc=kc: e.transpose(ps[:, b0, kc * 22:(kc + 1) * 22], stage22[:, kc * 128:(kc + 1) * 128], ident_f[0:22, 0:22]),
                 reads=["stage22", "ident_f"], writes=psk(b0))
        S.dve(lambda e: e.tensor_copy(out=vecT[:].rearrange("p a b -> p (a b)"), in_=ps[:, b0, 0:KC * 22]), reads=psk(b0), writes=["vecT"])
        S.act(lambda e: e.activation(out=csT[:], in_=vecT[:, :, 0:17], func=AF.Silu), reads=["vecT"], writes=["csT"])
        stage8 = a_f32(DFF, parts=8)
        S.dma("sp", "c_s8", lambda e: e.dma_start(out=stage8, in_=ins["ffn_vec"]), writes=["stage8"])
        b1 = ps_next()
        for fc in range(NF):
            S.pe(lambda e, fc=fc: e.transpose(ps[:, b1, fc * 8:(fc + 1) * 8], stage8[:, fc * 128:(fc + 1) * 128], ident_f[0:8, 0:8]),
                 reads=["stage8", "ident_f"], writes=psk(b1))
        S.dve(lambda e: e.tensor_copy(out=ffnvT[:].rearrange("p a b -> p (a b)"), in_=ps[:, b1, 0:NF * 8]), reads=psk(b1), writes=["ffnvT"])

        browb = [a_bf16(512, parts=1) for _ in range(3)]
        mstage = [a_f32(512, parts=17) for _ in range(2)]
        csb = a_bf16(KC, 17)
        S.dve(lambda e: e.tensor_copy(out=csb, in_=csT[:]), reads=["csT"], writes=["csb"])
        bri = [0]

        def ada_layer(wsrc, bsrc, ncols, mod_base):
            for pcs in range(ncols // 512):
                wv, wk = load_w_bf16(wsrc, KC, 512, col0=pcs * 512)
                bi = bri[0] % 3
                m2 = bri[0] % 2
                bri[0] += 1
                S.dma("pool", "brb%d" % bi, lambda e, bi=bi, pcs=pcs: e.dma_start(out=browb[bi], in_=bsrc[0:1, pcs * 512:(pcs + 1) * 512]), writes=[("browb", bi)])
                bank = ps_next()
                for kc in range(KC):
                    S.pe(lambda e, bank=bank, kc=kc, wv=wv: e.matmul(ps[0:17, bank, :], lhsT=csb[:, kc, :], rhs=wv[:, kc, :], start=(kc == 0), stop=False), reads=[wk, "csb"], writes=psk(bank))
                S.pe(lambda e, bank=bank, bi=bi: e.matmul(ps[0:17, bank, :], lhsT=ones_b[0:1, 0:17], rhs=browb[bi][0:1, :], start=False, stop=True), reads=[("browb", bi), "ones_b"], writes=psk(bank))
                S.act(lambda e, bank=bank, m2=m2: e.copy(out=mstage[m2], in_=ps[0:17, bank, :]), reads=psk(bank), writes=[("mstage", m2)])
                bank2 = ps_next()
                for q in range(4):
                    S.pe(lambda e, bank2=bank2, q=q, m2=m2: e.transpose(ps[:, bank2, q * 17:(q + 1) * 17], mstage[m2][:, q * 128:(q + 1) * 128], ident_f[0:17, 0:17]), reads=[("mstage", m2), "ident_f"], writes=psk(bank2))
                m0 = mod_base + 4 * pcs
                S.dve(lambda e, bank2=bank2, m0=m0: e.tensor_copy(out=modT[:, m0:m0 + 4, :].rearrange("p a b -> p (a b)"), in_=ps[:, bank2, 0:68]), reads=psk(bank2), writes=["modT"])

        ada_layer(w_ada[0], b_ada[0:1, :], 6 * D, 0)
        ada_layer(w_ada[1], b_ada[1:2, :], 6 * D, 48)
        ada_layer(w_adaf, b_adaf, 2 * D, 96)
        norm_specs = [(0 * 48 + 8, 17), (0 * 48 + 32, 19), (1 * 48 + 8, 18), (1 * 48 + 32, 20), (96 + 8, 21)]
        for n, (scb, col) in enumerate(norm_specs):
            for kc in range(KC):
                S.dve(lambda e, n=n, kc=kc, scb=scb, col=col: e.tensor_scalar(out=Amod[:, n, kc, :], in0=modT[:, scb + kc, :], scalar1=1.0, scalar2=vecT[:, kc, col:col + 1], op0=ALU.add, op1=ALU.mult),
                      reads=["modT", "vecT"], writes=["Amod"])
        norm_shift = [0, 24, 48, 72, 96]
        S.barrier()

        a_reset()
        xst = [a_f32(D) for _ in range(2)]
        tiles = [(xp, t * 128, 128, t * 128) for t in range(16)] + [(xs, 0, TS, TP)]
        for ti, (src, r0, rows, t0) in enumerate(tiles):
            sb = ti % 2
            S.dma("sp", "xst%d" % sb, lambda e, sb=sb, src=src, r0=r0, rows=rows: e.dma_start(out=xst[sb][0:rows, :], in_=src[r0:r0 + rows, :]), writes=[("xst", sb)])
            for half in range(2):
                b = ps_next()
                for q in range(4):
                    kc = half * 4 + q
                    S.pe(lambda e, b=b, q=q, kc=kc, sb=sb, rows=rows: e.transpose(ps[:, b, q * 128:q * 128 + rows], xst[sb][0:rows, kc * 128:(kc + 1) * 128], ident_f[0:rows, 0:rows]),
                         reads=[("xst", sb), "ident_f"], writes=psk(b))
                src_ap = ps[:, b, :].rearrange("p (q c) -> p q c", q=4)[:, :, 0:rows]
                dst_ap = xT[:, half * 4:half * 4 + 4, t0:t0 + rows]
                if half == 0:
                    S.act(lambda e, dst_ap=dst_ap, src_ap=src_ap: e.copy(out=dst_ap, in_=src_ap), reads=psk(b), writes=[("xT", ti)])
                else:
                    S.dve(lambda e, dst_ap=dst_ap, src_ap=src_ap: e.tensor_copy(out=dst_ap, in_=src_ap), reads=psk(b), writes=[("xT", ti)])
        S.barrier()

        def do_norm(n, out_hT=True, out_f32=None):
            a_reset()
            sq = [a_bf16(KC, 512) for _ in range(2)]
            rstd = [a_f32(512) for _ in range(2)]
            tmp = [a_f32(KC, 512) for _ in range(2)]
            shb = norm_shift[n]
            for bi, (t0, nt) in enumerate(ALLBLOCKS):
                i2 = bi % 2
                S.act(lambda e, i2=i2, t0=t0, nt=nt: e.activation(out=sq[i2][:, :, 0:nt], in_=xT[:, :, t0:t0 + nt], func=AF.Square),
                      reads=[], writes=[("sq", i2)])
                b = ps_next()
                for kc in range(KC):
                    S.pe(lambda e, b=b, kc=kc, i2=i2, nt=nt: e.matmul(ps[:, b, 0:nt], lhsT=ones_b[:], rhs=sq[i2][:, kc, 0:nt], start=(kc == 0), stop=(kc == KC - 1)),
                         reads=[("sq", i2), "ones_b"], writes=psk(b))
                S.act(lambda e, b=b, i2=i2, nt=nt: e.activation(out=rstd[i2][:, 0:nt], in_=ps[:, b, 0:nt], func=AF.Sqrt, bias=EPS, scale=1.0 / D),
                      reads=psk(b), writes=[("rstd", i2)])
                S.dve(lambda e, i2=i2, nt=nt: e.reciprocal(out=rstd[i2][:, 0:nt], in_=rstd[i2][:, 0:nt]), reads=[("rstd", i2)], writes=[("rstd", i2)])
                S.dve(lambda e, i2=i2, t0=t0, nt=nt: e.tensor_tensor(out=tmp[i2][:, :, 0:nt], in0=xT[:, :, t0:t0 + nt], in1=rstd[i2][:, 0:nt].unsqueeze(1).to_broadcast([128, KC, nt]), op=ALU.mult),
                      reads=[("rstd", i2)], writes=[("tmp", i2)])
                for kc in range(KC):
                    dst = hT[:, kc, t0:t0 + nt] if out_f32 is None else out_f32(bi, kc)
                    if t0 < TP:
                        S.act(lambda e, dst=dst, i2=i2, kc=kc, nt=nt: e.activation(out=dst, in_=tmp[i2][:, kc, 0:nt], func=AF.Identity, scale=Amod[:, n, kc, 0:1], bias=modT[:, shb + kc, 0:1]),
                              reads=[("tmp", i2)], writes=[("h", bi, kc)])
                    else:
                        S.dve(lambda e, i2=i2, kc=kc: e.tensor_tensor(out=tmp[i2][:, kc, 0:TS].rearrange("p (s j) -> p s j", j=4), in0=tmp[i2][:, kc, 0:TS].rearrange("p (s j) -> p s j", j=4),
                                                                     in1=Amod[:, n, kc, 1:17].unsqueeze(2).to_broadcast([128, NSEQ, 4]), op=ALU.mult),
                              reads=[("tmp", i2)], writes=[("tmp", i2)])
                        S.dve(lambda e, dst=dst, i2=i2, kc=kc: e.tensor_tensor(out=dst.rearrange("p (s j) -> p s j", j=4), in0=tmp[i2][:, kc, 0:TS].rearrange("p (s j) -> p s j", j=4),
                                                                              in1=modT[:, shb + kc, 1:17].unsqueeze(2).to_broadcast([128, NSEQ, 4]), op=ALU.add),
                              reads=[("tmp", i2)], writes=[("h", bi, kc)])
            S.barrier()

        def gate_ap(gbase, kc, bi, nt):
            if bi != "S":
                return modT[:, gbase + kc, 0:1].to_broadcast([128, nt])
            return modT[:, gbase + kc, 1:17].unsqueeze(2).to_broadcast([128, NSEQ, 4])

        def resid_update(psrc, gbase, kc, bi, t0, nt, reads):
            if t0 < TP:
                S.dve(lambda e: e.scalar_tensor_tensor(out=xT[:, kc, t0:t0 + nt], in0=psrc, scalar=modT[:, gbase + kc, 0:1], in1=xT[:, kc, t0:t0 + nt], op0=ALU.mult, op1=ALU.add),
                      reads=list(reads) + [("x", kc, t0)], writes=[("x", kc, t0)])
            else:
                tmpg = K.tmpg
                S.dve(lambda e: e.tensor_tensor(out=tmpg.rearrange("p (s j) -> p s j", j=4), in0=psrc.rearrange("p (s j) -> p s j", j=4), in1=gate_ap(gbase, kc, "S", nt), op=ALU.mult),
                      reads=list(reads), writes=["tmpg"])
                S.dve(lambda e: e.tensor_tensor(out=xT[:, kc, t0:t0 + nt], in0=xT[:, kc, t0:t0 + nt], in1=tmpg, op=ALU.add),
                      reads=["tmpg", ("x", kc, t0)], writes=[("x", kc, t0)])

        def do_ffn(l):
            a_reset()
            gbase = l * 48 + 40
            K.tmpg = a_f32(TS)
            actb = a_bf16(NF, 704)
            gx = [a_f32(2 + 512) for _ in range(2)]
            gxs = a_f32(NSEQ, 6)
            cv = [a_f32(512) for _ in range(2)]
            tailP = a_f32(NF, 2)
            tailS = a_f32(NF, NSEQ, 2)
            sstT = a_f32(NF, 32)
            sstg = [a_f32(512, parts=32) for _ in range(2)]
            ostg = [a_f32(512, parts=32) for _ in range(2)]
            ostgP = [a_f32(512, parts=2) for _ in range(2)]
            for gi, g4 in enumerate(range(0, NF, 4)):
                b = ps_next()
                n = min(4, NF - g4)
                s2 = gi % 2
                S.dma("sp", "fst%d" % s2, lambda e, s2=s2, g4=g4, n=n: e.dma_start(out=sstg[s2][:, 0:n * 128], in_=s_fconv[l][:, g4 * 128:(g4 + n) * 128]), writes=[("sstg", s2)])
                for q in range(n):
                    S.pe(lambda e, b=b, q=q, s2=s2: e.transpose(ps[:, b, q * 32:(q + 1) * 32], sstg[s2][:, q * 128:(q + 1) * 128], ident_f[0:32, 0:32]),
                         reads=[("sstg", s2), "ident_f"], writes=psk(b))
                S.dve(lambda e, b=b, n=n, g4=g4: e.tensor_copy(out=sstT[:, g4:g4 + n, :].rearrange("p a b -> p (a b)"), in_=ps[:, b, 0:n * 32]), reads=psk(b), writes=["sstT"])
            S.dve(lambda e: e.memset(fcarry[:], 0.0), writes=[("fcarry", fc_) for fc_ in range(NF)])
            wcol = l * 3
            passes = [[(0, 0, 352, 0), (1, 352, 352, 352)], [(2, 704, 352, 0), (3, 1056, 352, 352)], [(4, 1408, 320, 0), (5, 1728, 320, 320), (6, TP, TS, 640)]]
            for pi, blks in enumerate(passes):
                for f0 in range(0, NF, 4):
                    nf = min(4, NF - f0)
                    wg, wgk = load_w_bf16(w_up[l], KC, nf * 128, col0=f0 * 128)
                    wv, wvk = load_w_bf16(w_up[l], KC, nf * 128, col0=DFF + f0 * 128)
                    for q in range(nf):
                        fc = f0 + q
                        for (bi, t0, nt, a0) in blks:
                            bg = ps_next()
                            for kc in range(KC):
                                S.pe(lambda e, bg=bg, kc=kc, q=q, t0=t0, nt=nt, wg=wg: e.matmul(ps[:, bg, 0:nt], lhsT=wg[:, kc, q * 128:(q + 1) * 128], rhs=hT[:, kc, t0:t0 + nt], start=(kc == 0), stop=(kc == KC - 1)),
                                     reads=[wgk], writes=psk(bg))
                            bv = ps_next()
                            for kc in range(KC):
                                S.pe(lambda e, bv=bv, kc=kc, q=q, t0=t0, nt=nt, wv=wv: e.matmul(ps[:, bv, 0:nt], lhsT=wv[:, kc, q * 128:(q + 1) * 128], rhs=hT[:, kc, t0:t0 + nt], start=(kc == 0), stop=(kc == KC - 1)),
                                     reads=[wvk], writes=psk(bv))
                            w0 = ffnvT[:, fc, wcol + 0:wcol + 1]
                            w1 = ffnvT[:, fc, wcol + 1:wcol + 2]
                            w2 = ffnvT[:, fc, wcol + 2:wcol + 3]
                            bb = ffnvT[:, fc, 6 + l:7 + l]
                            if bi < 6:
                                i2 = bi % 2
                                G = gx[i2]
                                c_ = cv[i2]
                                S.dve(lambda e, G=G, fc=fc: e.tensor_copy(out=G[:, 0:2], in_=fcarry[:, fc, :]), reads=[("fcarry", fc)], writes=[("gx", i2)])
                                S.act(lambda e, G=G, bg=bg, nt=nt: e.copy(out=G[:, 2:2 + nt], in_=ps[:, bg, 0:nt]), reads=psk(bg), writes=[("gx", i2)])
                                S.dve(lambda e, G=G, fc=fc, nt=nt: e.tensor_copy(out=fcarry[:, fc, :], in_=G[:, nt:nt + 2]), reads=[("gx", i2)], writes=[("fcarry", fc)])
                                if bi == 5:
                                    S.dve(lambda e, G=G, fc=fc, nt=nt: e.tensor_copy(out=tailP[:, fc, :], in_=G[:, nt:nt + 2]), reads=[("gx", i2)], writes=["tailP"])
                                S.act(lambda e, G=G, c_=c_, nt=nt, w2=w2: e.activation(out=c_[:, 0:nt], in_=G[:, 2:2 + nt], func=AF.Identity, scale=w2), reads=[("gx", i2), "ffnvT"], writes=[("cv", i2)])
                                S.dve(lambda e, G=G, c_=c_, nt=nt, w1=w1: e.scalar_tensor_tensor(out=c_[:, 0:nt], in0=G[:, 1:1 + nt], scalar=w1, in1=c_[:, 0:nt], op0=ALU.mult, op1=ALU.add), reads=[("gx", i2), ("cv", i2)], writes=[("cv", i2)])
                                S.dve(lambda e, G=G, c_=c_, nt=nt, w0=w0: e.scalar_tensor_tensor(out=c_[:, 0:nt], in0=G[:, 0:nt], scalar=w0, in1=c_[:, 0:nt], op0=ALU.mult, op1=ALU.add), reads=[("gx", i2), ("cv", i2)], writes=[("cv", i2)])
                                S.act(lambda e, c_=c_, nt=nt, bb=bb: e.activation(out=c_[:, 0:nt], in_=c_[:, 0:nt], func=AF.Silu, bias=bb), reads=[("cv", i2)], writes=[("cv", i2)])
                                S.dve(lambda e, c_=c_, nt=nt, bv=bv, fc=fc, a0=a0: e.tensor_tensor(out=actb[:, fc, a0:a0 + nt], in0=c_[:, 0:nt], in1=ps[:, bv, 0:nt], op=ALU.mult), reads=[("cv", i2)] + psk(bv), writes=[("act", fc, bi)])
                            else:
                                c_ = cv[0][:, 0:TS].rearrange("p (s j) -> p s j", j=4)
                                S.dve(lambda e, fc=fc: e.tensor_copy(out=gxs[:, :, 0:2], in_=sstT[:, fc, :].rearrange("p (s r) -> p s r", r=2)), reads=["sstT"], writes=["gxs"])
                                S.act(lambda e, bg=bg: e.copy(out=gxs[:, :, 2:6], in_=ps[:, bg, 0:TS].rearrange("p (s j) -> p s j", j=4)), reads=psk(bg), writes=["gxs"])
                                S.dve(lambda e, fc=fc: e.tensor_copy(out=tailS[:, fc, :, :], in_=gxs[:, :, 4:6]), reads=["gxs"], writes=["tailS"])
                                S.act(lambda e, c_=c_, w2=w2: e.activation(out=c_, in_=gxs[:, :, 2:6], func=AF.Identity, scale=w2), reads=["gxs", "ffnvT"], writes=[("cv", 0)])
                                S.dve(lambda e, c_=c_, w1=w1: e.scalar_tensor_tensor(out=c_, in0=gxs[:, :, 1:5], scalar=w1, in1=c_, op0=ALU.mult, op1=ALU.add), reads=["gxs", ("cv", 0)], writes=[("cv", 0)])
                                S.dve(lambda e, c_=c_, w0=w0: e.scalar_tensor_tensor(out=c_, in0=gxs[:, :, 0:4], scalar=w0, in1=c_, op0=ALU.mult, op1=ALU.add), reads=["gxs", ("cv", 0)], writes=[("cv", 0)])
                                S.act(lambda e, bb=bb: e.activation(out=cv[0][:, 0:TS], in_=cv[0][:, 0:TS], func=AF.Silu, bias=bb), reads=[("cv", 0)], writes=[("cv", 0)])
                                S.dve(lambda e, bv=bv, fc=fc, a0=a0: e.tensor_tensor(out=actb[:, fc, a0:a0 + TS], in0=cv[0][:, 0:TS], in1=ps[:, bv, 0:TS], op=ALU.mult), reads=[("cv", 0)] + psk(bv), writes=[("act", fc, bi)])
                for dh in range(2):
                    banks = {}
                    for dc in range(4):
                        for (bi, t0, nt, a0) in blks:
                            if len(blks) * 4 > 8 and bi == 6:
                                continue
                            banks[(dc, bi)] = ps_next()
                    for f0 in range(0, NF, 4):
                        nf = min(4, NF - f0)
                        wd, wdk = load_w_bf16(w_down[l], nf, 512, row0=f0 * 128, col0=dh * 512)
                        for (dc, bi), b in banks.items():
                            t0, nt, a0 = [(x[1], x[2], x[3]) for x in blks if x[0] == bi][0]
                            for q in range(nf):
                                fc = f0 + q
                                S.pe(lambda e, b=b, q=q, dc=dc, fc=fc, a0=a0, nt=nt, wd=wd: e.matmul(ps[:, b, 0:nt], lhsT=wd[:, q, dc * 128:(dc + 1) * 128], rhs=actb[:, fc, a0:a0 + nt], start=(fc == 0), stop=(fc == NF - 1)),
                                     reads=[wdk, ("act", fc, bi)], writes=psk(b))
                    for (dc, bi), b in banks.items():
                        t0, nt, a0 = [(x[1], x[2], x[3]) for x in blks if x[0] == bi][0]
                        resid_update(ps[:, b, 0:nt], gbase, dh * 4 + dc, bi, t0, nt, psk(b))
                if len(blks) * 4 > 8:
                    (bi, t0, nt, a0) = blks[2]
                    for dh in range(2):
                        banks = {dc: ps_next() for dc in range(4)}
                        for f0 in range(0, NF, 4):
                            nf = min(4, NF - f0)
                            wd, wdk = load_w_bf16(w_down[l], nf, 512, row0=f0 * 128, col0=dh * 512)
                            for dc, b in banks.items():
                                for q in range(nf):
                                    fc = f0 + q
                                    S.pe(lambda e, b=b, q=q, dc=dc, fc=fc, wd=wd, nt=nt, a0=a0: e.matmul(ps[:, b, 0:nt], lhsT=wd[:, q, dc * 128:(dc + 1) * 128], rhs=actb[:, fc, a0:a0 + nt], start=(fc == 0), stop=(fc == NF - 1)),
                                         reads=[wdk, ("act", fc, bi)], writes=psk(b))
                        for dc, b in banks.items():
                            resid_update(ps[:, b, 0:nt], gbase, dh * 4 + dc, bi, t0, nt, psk(b))
            for gi, g4 in enumerate(range(0, NF, 4)):
                n = min(4, NF - g4)
                o2 = gi % 2
                b = ps_next()
                b2 = ps_next()
                for q in range(n):
                    fc = g4 + q
                    S.pe(lambda e, b=b, q=q, fc=fc: e.transpose(ps[0:2, b, q * 128:(q + 1) * 128], tailP[:, fc, :], ident_f[:]), reads=["tailP", "ident_f"], writes=psk(b))
                    S.pe(lambda e, b2=b2, q=q, fc=fc: e.transpose(ps[0:32, b2, q * 128:(q + 1) * 128], tailS[:, fc, :, :].rearrange("p s r -> p (s r)"), ident_f[:]), reads=["tailS", "ident_f"], writes=psk(b2))
                S.dve(lambda e, b=b, n=n, o2=o2: e.tensor_copy(out=ostgP[o2][:, 0:n * 128], in_=ps[0:2, b, 0:n * 128]), reads=psk(b), writes=[("ostgP", o2)])
                S.dve(lambda e, b2=b2, n=n, o2=o2: e.tensor_copy(out=ostg[o2][:, 0:n * 128], in_=ps[0:32, b2, 0:n * 128]), reads=psk(b2), writes=[("ostg", o2)])
                S.dma("sp", "foutP%d" % o2, lambda e, o2=o2, n=n, g4=g4: e.dma_start(out=fconv_p[l][:, g4 * 128:(g4 + n) * 128], in_=ostgP[o2][:, 0:n * 128]), reads=[("ostgP", o2)])
                S.dma("sp", "foutS%d" % o2, lambda e, o2=o2, n=n, g4=g4: e.dma_start(out=fconv_s[l][:, g4 * 128:(g4 + n) * 128], in_=ostg[o2][:, 0:n * 128]), reads=[("ostg", o2)])
            S.barrier()

        def do_final():
            a_reset()
            yT = a_f32(KC, 512)
            ytok = [a_f32(D) for _ in range(2)]
            n = 4
            sq = a_bf16(KC, 512)
            rstd = a_f32(512)
            tmp = a_f32(KC, 512)
            shb = norm_shift[n]
            oi = 0
            for bi, (t0, nt) in enumerate(ALLBLOCKS):
                S.act(lambda e, t0=t0, nt=nt: e.activation(out=sq[:, :, 0:nt], in_=xT[:, :, t0:t0 + nt], func=AF.Square), reads=[], writes=["sq"])
                b = ps_next()
                for kc in range(KC):
                    S.pe(lambda e, b=b, kc=kc, nt=nt: e.matmul(ps[:, b, 0:nt], lhsT=ones_b[:], rhs=sq[:, kc, 0:nt], start=(kc == 0), stop=(kc == KC - 1)), reads=["sq", "ones_b"], writes=psk(b))
                S.act(lambda e, b=b, nt=nt: e.activation(out=rstd[:, 0:nt], in_=ps[:, b, 0:nt], func=AF.Sqrt, bias=EPS, scale=1.0 / D), reads=psk(b), writes=["rstd"])
                S.dve(lambda e, nt=nt: e.reciprocal(out=rstd[:, 0:nt], in_=rstd[:, 0:nt]), reads=["rstd"], writes=["rstd"])
                S.dve(lambda e, t0=t0, nt=nt: e.tensor_tensor(out=tmp[:, :, 0:nt], in0=xT[:, :, t0:t0 + nt], in1=rstd[:, 0:nt].unsqueeze(1).to_broadcast([128, KC, nt]), op=ALU.mult), reads=["rstd"], writes=["tmp"])
                for kc in range(KC):
                    if t0 < TP:
                        S.act(lambda e, kc=kc, nt=nt: e.activation(out=yT[:, kc, 0:nt], in_=tmp[:, kc, 0:nt], func=AF.Identity, scale=Amod[:, n, kc, 0:1], bias=modT[:, shb + kc, 0:1]), reads=["tmp"], writes=["yT"])
                    else:
                        S.dve(lambda e, kc=kc: e.tensor_tensor(out=tmp[:, kc, 0:TS].rearrange("p (s j) -> p s j", j=4), in0=tmp[:, kc, 0:TS].rearrange("p (s j) -> p s j", j=4), in1=Amod[:, n, kc, 1:17].unsqueeze(2).to_broadcast([128, NSEQ, 4]), op=ALU.mult), reads=["tmp"], writes=["tmp"])
                        S.dve(lambda e, kc=kc: e.tensor_tensor(out=yT[:, kc, 0:TS].rearrange("p (s j) -> p s j", j=4), in0=tmp[:, kc, 0:TS].rearrange("p (s j) -> p s j", j=4), in1=modT[:, shb + kc, 1:17].unsqueeze(2).to_broadcast([128, NSEQ, 4]), op=ALU.add), reads=["tmp"], writes=["yT"])
                for sub in range(0, nt, 128):
                    rows = min(128, nt - sub)
                    o2 = oi % 2
                    oi += 1
                    for half in range(2):
                        b = ps_next()
                        for q in range(4):
                            kc = half * 4 + q
                            S.pe(lambda e, b=b, q=q, kc=kc, sub=sub, rows=rows: e.transpose(ps[0:rows, b, q * 128:(q + 1) * 128], yT[:, kc, sub:sub + rows], ident_f[:]), reads=["yT", "ident_f"], writes=psk(b))
                        if half == 0:
                            S.act(lambda e, b=b, o2=o2, rows=rows, half=half: e.copy(out=ytok[o2][0:rows, half * 512:(half + 1) * 512], in_=ps[0:rows, b, :]), reads=psk(b), writes=[("ytok", o2, half)])
                        else:
                            S.dve(lambda e, b=b, o2=o2, rows=rows, half=half: e.tensor_copy(out=ytok[o2][0:rows, half * 512:(half + 1) * 512], in_=ps[0:rows, b, :]), reads=psk(b), writes=[("ytok", o2, half)])
                    if t0 < TP:
                        dst = y_p[t0 + sub:t0 + sub + rows, :]
                    else:
                        dst = y_s[sub:sub + rows, :]
                    S.dma("sp", "yo%d" % o2, lambda e, dst=dst, o2=o2, rows=rows: e.dma_start(out=dst, in_=ytok[o2][0:rows, :]), reads=[("ytok", o2, 0), ("ytok", o2, 1)])
            S.barrier()

        K.__dict__.update(locals())
        G_ = globals()
        do_norm(0)
        import os
        if "do_retention" in G_ and not os.environ.get("SKIP_RET"):
            G_["do_retention"](K)
        do_norm(1)
        do_ffn(0)
        do_norm(2)
        if "do_gdn" in G_:
            G_["do_gdn"](K)
        do_norm(3)
        do_ffn(1)
        do_final()
        if "modT" in dbg_out:
            S.dma("sp", "dbg", lambda e: e.dma_start(out=dbg_out["modT"], in_=modT[:].rearrange("p a b -> p (a b)")))
        if "Amod" in dbg_out:
            S.dma("sp", "dbg", lambda e: e.dma_start(out=dbg_out["Amod"], in_=Amod[:].rearrange("p a b c -> p (a b c)")))
        S.emit(st)
    return nc


def do_retention(K):
    S = K.S; ps = K.ps; nc = K.nc
    a_f32 = K.a_f32; a_bf16 = K.a_bf16; ps_next = K.ps_next; psk = K.psk
    hT = K.hT; xT = K.xT; modT = K.modT
    ident_b = K.ident_b
    g = _gammas()
    K.a_reset()
    K.tmpg = a_f32(TS)
    tabs = a_f32(4, 512)
    qb = a_bf16(2, 512)
    kb = a_bf16(2, 512)
    t1 = a_f32(512)
    t2 = a_f32(512)
    vb = a_bf16(512)
    sg = a_f32(512)
    kh = a_bf16(256)
    im = a_bf16(128)
    og = a_bf16(512)
    ogT = a_bf16(4, 512)
    ss = a_f32(2)
    maskP = a_f32(128)
    maskS = a_f32(128)
    ksc = a_f32(2 * RET_H)
    segcol = a_f32(NSEQ)
    S32 = a_f32(2, 512)
    Sbf = a_bf16(2, 512)
    qf = a_f32(2, TS)
    qX = a_f32(2, NSEQ, TS)
    khm = [a_bf16(256) for _ in range(2)]
    Sin = [a_f32(2, 512) for _ in range(2)]
    Sout = a_f32(2, 512)
    segmask = a_f32(NSEQ, TS)
    S.dma("sp", "rc", lambda e: e.dma_start(out=ksc, in_=K.c_kscale), writes=["ksc"])
    S.dma("sp", "rc", lambda e: e.dma_start(out=segcol, in_=K.c_segcol), writes=["segcol"])
    S.dma("sp", "rc", lambda e: e.dma_start(out=segmask.rearrange("p a b -> p (a b)"), in_=K.c_segmask), writes=["segmask"])
    gbase = 16
    w_in = K.w_ret_in
    w_out = K.w_ret_out
    sctr = [0]
    for h in range(RET_H):
        wq, wqk = K.load_w_bf16(w_in, KC, 256, col0=h * 256, slot=0, off=0, key=("w", 0, "q"))
        wk, wkk = K.load_w_bf16(w_in, KC, 256, col0=1024 + h * 256, slot=0, off=2048, key=("w", 0, "k"))
        wv, wvk = K.load_w_bf16(w_in, KC, 512, col0=2048 + h * 512, slot=1)
        wg, wgk = K.load_w_bf16(w_in, KC, 512, col0=4096 + h * 512, slot=2)
        wo, wok = K.load_w_bf16(w_out, 4, 1024, row0=h * 512, slot=3)
        S.dma("sp", "rm", lambda e, h=h: e.dma_start(out=maskP, in_=K.c_retmask[h]), writes=["maskP"])
        S.dma("sp", "rm", lambda e, h=h: e.dma_start(out=maskS, in_=K.c_retmask[RET_H + h]), writes=["maskS"])
        for bi, (t0, nt) in enumerate(K.ALLBLOCKS):
            isS = t0 >= TP
            C = 64 if isS else 128
            for ti, (hh, cs_) in enumerate([(h, 0), (h, 1), (RET_H, 0), (RET_H, 1)]):
                S.dma("sp", "rt", lambda e, ti=ti, hh=hh, cs_=cs_, t0=t0, nt=nt: e.dma_start(out=tabs[:, ti, 0:nt], in_=K.c_rope[hh, cs_, :, t0:t0 + nt]), writes=[("tabs", ti)])
            for which, (wt, wkey, dst, tb) in enumerate([(wq, wqk, qb, 0), (wk, wkk, kb, 2)]):
                pb = [ps_next(), ps_next()]
                for dc in range(2):
                    for kc in range(KC):
                        S.pe(lambda e, b=pb[dc], kc=kc, dc=dc, wt=wt, t0=t0, nt=nt: e.matmul(ps[:, b, 0:nt], lhsT=wt[:, kc, dc * 128:(dc + 1) * 128], rhs=hT[:, kc, t0:t0 + nt], start=(kc == 0), stop=(kc == KC - 1)),
                             reads=[wkey], writes=psk(pb[dc]))
                p1 = ps[:, pb[0], 0:nt]
                p2 = ps[:, pb[1], 0:nt]
                cc = tabs[:, tb, 0:nt]
                sn = tabs[:, tb + 1, 0:nt]
                to_f = isS and which == 0
                d1 = qf[:, 0, :] if to_f else dst[:, 0, 0:nt]
                d2 = qf[:, 1, :] if to_f else dst[:, 1, 0:nt]
                S.dve(lambda e, p1=p1, cc=cc, nt=nt: e.tensor_tensor(out=t1[:, 0:nt], in0=p1, in1=cc, op=ALU.mult), reads=psk(pb[0]) + [("tabs", tb)], writes=["t1"])
                S.dve(lambda e, p2=p2, sn=sn, nt=nt: e.tensor_tensor(out=t2[:, 0:nt], in0=p2, in1=sn, op=ALU.mult), reads=psk(pb[1]) + [("tabs", tb + 1)], writes=["t2"])
                S.pool(lambda e, d1=d1, nt=nt: e.tensor_tensor(out=d1, in0=t1[:, 0:nt], in1=t2[:, 0:nt], op=ALU.subtract), reads=["t1", "t2"], writes=[("rd", which, 0)])
                S.dve(lambda e, p1=p1, sn=sn, nt=nt: e.tensor_tensor(out=t1[:, 0:nt], in0=p1, in1=sn, op=ALU.mult), reads=psk(pb[0]) + [("tabs", tb + 1)], writes=["t1"])
                S.dve(lambda e, p2=p2, cc=cc, nt=nt: e.tensor_tensor(out=t2[:, 0:nt], in0=p2, in1=cc, op=ALU.mult), reads=psk(pb[1]) + [("tabs", tb)], writes=["t2"])
                S.pool(lambda e, d2=d2, nt=nt: e.tensor_tensor(out=d2, in0=t1[:, 0:nt], in1=t2[:, 0:nt], op=ALU.add), reads=["t1", "t2"], writes=[("rd", which, 1)])
                if to_f:
                    S.act(lambda e: e.copy(out=qb[:, :, 0:TS], in_=qf[:, :, :]), reads=[("rd", 0, 0), ("rd", 0, 1)], writes=["qbS"])
            qkeys = [("rd", 0, 0), ("rd", 0, 1)] + (["qbS"] if isS else [])
            kkeys = [("rd", 1, 0), ("rd", 1, 1)]
            for c0 in range(0, nt, C):
                first = (not isS) and t0 == 0 and c0 == 0
                last = (not isS) and (t0 + c0 + C == TP)
                bv = ps_next()
                for kc in range(KC):
                    S.pe(lambda e, bv=bv, kc=kc, t0=t0, c0=c0, C=C: e.matmul(ps[0:C, bv, :], lhsT=hT[:, kc, t0 + c0:t0 + c0 + C], rhs=wv[:, kc, :], start=(kc == 0), stop=(kc == KC - 1)), reads=[wvk], writes=psk(bv))
                S.act(lambda e, bv=bv, C=C: e.copy(out=vb[0:C, :], in_=ps[0:C, bv, :]), reads=psk(bv), writes=["vb"])
                bg = ps_next()
                for kc in range(KC):
                    S.pe(lambda e, bg=bg, kc=kc, t0=t0, c0=c0, C=C: e.matmul(ps[0:C, bg, :], lhsT=hT[:, kc, t0 + c0:t0 + c0 + C], rhs=wg[:, kc, :], start=(kc == 0), stop=(kc == KC - 1)), reads=[wgk], writes=psk(bg))
                S.act(lambda e, bg=bg, C=C: e.activation(out=sg[0:C, :], in_=ps[0:C, bg, :], func=AF.Silu), reads=psk(bg), writes=["sg"])
                bk = ps_next()
                psb = ps[:, bk, :].bitcast(BF16)
                for dc in range(2):
                    S.pe(lambda e, psb=psb, dc=dc, c0=c0, C=C: e.transpose(psb[0:C, dc * 128:(dc + 1) * 128], kb[:, dc, c0:c0 + C], ident_b[:]), reads=kkeys + ["ident_b"], writes=psk(bk))
                kcol = (RET_H + h) if isS else h
                S.act(lambda e, psb=psb, C=C, kcol=kcol: e.activation(out=kh[0:C, :], in_=psb[0:C, 0:256], func=AF.Identity, scale=ksc[0:C, kcol:kcol + 1]), reads=psk(bk) + ["ksc"], writes=["kh"])
                bi_ = ps_next()
                for dc in range(2):
                    S.pe(lambda e, bi_=bi_, dc=dc, c0=c0, C=C: e.matmul(ps[0:C, bi_, 0:C], lhsT=kb[:, dc, c0:c0 + C], rhs=qb[:, dc, c0:c0 + C], start=(dc == 0), stop=(dc == 1)), reads=kkeys + qkeys, writes=psk(bi_))
                mk = maskS if isS else maskP
                S.dve(lambda e, bi_=bi_, C=C, mk=mk: e.tensor_tensor(out=im[0:C, 0:C], in0=ps[0:C, bi_, 0:C], in1=mk[0:C, 0:C], op=ALU.mult), reads=psk(bi_) + ["maskP", "maskS"], writes=["im"])
                bo = ps_next()
                has_inter = isS or not first
                S.pe(lambda e, bo=bo, C=C, has_inter=has_inter: e.matmul(ps[0:C, bo, :], lhsT=im[0:C, 0:C], rhs=vb[0:C, :], start=True, stop=not has_inter), reads=["im", "vb"], writes=psk(bo))
                if not isS:
                    if not first:
                        for dc in range(2):
                            S.pe(lambda e, bo=bo, dc=dc, c0=c0, C=C: e.matmul(ps[0:C, bo, :], lhsT=qb[:, dc, c0:c0 + C], rhs=Sbf[:, dc, :], start=False, stop=(dc == 1)), reads=qkeys + [("Sbf", dc)], writes=psk(bo))
                    for dc in range(2):
                        bs = ps_next()
                        S.pe(lambda e, bs=bs, dc=dc, C=C: e.matmul(ps[:, bs, :], lhsT=kh[0:C, dc * 128:(dc + 1) * 128], rhs=vb[0:C, :], start=True, stop=True), reads=["kh", "vb"], writes=psk(bs))
                        if first:
                            S.dve(lambda e, bs=bs, dc=dc: e.tensor_copy(out=S32[:, dc, :], in_=ps[:, bs, :]), reads=psk(bs), writes=[("S32", dc)])
                        else:
                            S.dve(lambda e, bs=bs, dc=dc, gc=float(g[h] ** 128): e.scalar_tensor_tensor(out=S32[:, dc, :], in0=S32[:, dc, :], scalar=gc, in1=ps[:, bs, :], op0=ALU.mult, op1=ALU.add), reads=psk(bs) + [("S32", dc)], writes=[("S32", dc)])
                        if last:
                            S.dma("sp", "rpo", lambda e, dc=dc, h=h: e.dma_start(out=K.ret_p[h, dc * 128:(dc + 1) * 128, :], in_=S32[:, dc, :]), reads=[("S32", dc)])
                        else:
                            S.act(lambda e, dc=dc: e.copy(out=Sbf[:, dc, :], in_=S32[:, dc, :]), reads=[("S32", dc)], writes=[("Sbf", dc)])
                else:
                    K.P.reserved = {bo}
                    for dc in range(2):
                        S.dve(lambda e, dc=dc: e.tensor_tensor(out=qX[:, dc, :, :], in0=qf[:, dc, :].unsqueeze(1).to_broadcast([128, NSEQ, TS]), in1=segmask[:, :, :], op=ALU.mult), reads=[("rd", 0, dc), "segmask"], writes=[("qX", dc)])
                    def _ldr(n, h=h):
                        i3 = n % 2
                        S.dma("sp", "sin%d" % i3, lambda e, i3=i3, n=n, h=h: e.dma_start(out=Sin[i3], in_=K.s_ret[n, h].rearrange("(dc p) v -> p dc v", p=128)), writes=[("Sin", i3)])
                    _ldr(0)
                    for s_ in range(NSEQ):
                        i2 = s_ % 2
                        if s_ + 1 < NSEQ:
                            _ldr(s_ + 1)
                        for dc in range(2):
                            S.pe(lambda e, bo=bo, dc=dc, s_=s_, i2=i2: e.matmul(ps[0:TS, bo, :], lhsT=qX[:, dc, s_, :], rhs=Sin[i2][:, dc, :], start=False, stop=(s_ == NSEQ - 1 and dc == 1)), reads=[("qX", dc), ("Sin", i2)], writes=psk(bo))
                        S.dve(lambda e, i2=i2, s_=s_: e.tensor_scalar(out=khm[i2][0:TS, :], in0=kh[0:TS, :], scalar1=segcol[0:TS, s_:s_ + 1], scalar2=None, op0=ALU.mult), reads=["kh", "segcol"], writes=[("khm", i2)])
                        for dc in range(2):
                            bs = ps_next()
                            S.pe(lambda e, bs=bs, dc=dc, i2=i2: e.matmul(ps[:, bs, :], lhsT=khm[i2][0:TS, dc * 128:(dc + 1) * 128], rhs=vb[0:TS, :], start=True, stop=True), reads=[("khm", i2), "vb"], writes=psk(bs))
                            S.dve(lambda e, bs=bs, dc=dc, i2=i2, gc=float(g[h] ** 4): e.scalar_tensor_tensor(out=Sout[:, dc, :], in0=Sin[i2][:, dc, :], scalar=gc, in1=ps[:, bs, :], op0=ALU.mult, op1=ALU.add), reads=psk(bs) + [("Sin", i2)], writes=[("Sout", dc)])
                        S.dma("sp", "sout", lambda e, s_=s_, h=h: e.dma_start(out=K.ret_s[s_, h].rearrange("(dc p) v -> p dc v", p=128), in_=Sout), reads=[("Sout", 0), ("Sout", 1)], writes=[])
                    K.P.reserved = set()
                _ret_tail(K, h, bo, C, c0, sg, og, ogT, ss, t1)
            for dc8 in range(KC):
                b = ps_next()
                for ec in range(4):
                    S.pe(lambda e, b=b, ec=ec, dc8=dc8, nt=nt: e.matmul(ps[:, b, 0:nt], lhsT=wo[:, ec, dc8 * 128:(dc8 + 1) * 128], rhs=ogT[:, ec, 0:nt], start=(ec == 0), stop=(ec == 3)), reads=[wok, "ogT"], writes=psk(b))
                K.resid_update(ps[:, b, 0:nt], gbase, dc8, ("S" if isS else bi), t0, nt, psk(b))
        S.barrier()
    S.barrier()


def _ret_tail(K, h, bo, C, c0, sg, og, ogT, ss, junk):
    S = K.S; ps = K.ps
    _f = lambda e: e.activation(out=junk[0:C, :], in_=ps[0:C, bo, :], func=AF.Square, accum_out=ss[0:C, 0:1])
    _f._multi = True
    S.act(_f, reads=K.psk(bo), writes=["t1", "ss"])
    S.act(lambda e: e.activation(out=ss[0:C, 1:2], in_=ss[0:C, 0:1], func=AF.Sqrt, bias=EPS, scale=1.0 / 512), reads=["ss"], writes=["ss2"])
    S.dve(lambda e: e.reciprocal(out=ss[0:C, 1:2], in_=ss[0:C, 1:2]), reads=["ss2"], writes=["ss2"])
    S.dve(lambda e: e.scalar_tensor_tensor(out=og[0:C, :], in0=ps[0:C, bo, :], scalar=ss[0:C, 1:2], in1=sg[0:C, :], op0=ALU.mult, op1=ALU.mult), reads=K.psk(bo) + ["ss2", "sg"], writes=["og"])
    bt = K.ps_next()
    psb = ps[:, bt, :].bitcast(BF16)
    for ec in range(4):
        S.pe(lambda e, ec=ec: e.transpose(psb[:, ec * C:(ec + 1) * C], og[0:C, ec * 128:(ec + 1) * 128], K.ident_b[0:C, 0:C]), reads=["og", "ident_b"], writes=K.psk(bt))
    S.act(lambda e: e.copy(out=ogT[:, :, c0:c0 + C], in_=psb[:, 0:4 * C].rearrange("p (a b) -> p a b", a=4)), reads=K.psk(bt), writes=["ogT"])


def do_gdn(K):
    S = K.S; ps = K.ps
    a_f32 = K.a_f32; a_bf16 = K.a_bf16; ps_next = K.ps_next; psk = K.psk
    hT = K.hT; ident_f = K.ident_f; ident_b = K.ident_b; ones_b = K.ones_b
    w_in = K.w_gdn_in; w_out = K.w_gdn_out
    K.a_reset()
    K.tmpg = a_f32(TS)
    gm = a_f32(8, 128)
    segmask = a_f32(NSEQ, TS)
    segcol = a_f32(NSEQ)
    beta_all = a_f32(17, 16); g_all = a_f32(17, 16); negeG_all = a_f32(17, 16); kdec_all = a_f32(17, 16)
    gv = a_f32(32); negA = a_f32(16); gnw = a_f32(128)
    wcT = a_f32(32, 4)
    ones128 = a_f32(128)
    wba = a_bf16(KC, 32)
    Gx = a_f32(3 + 512); acc = a_f32(512); acc2 = a_f32(512); cch = a_f32(4, 3); gsT = a_f32(4, 48); Gxs = a_f32(NSEQ, 7)
    tailP = a_f32(4, 3); tailS = a_f32(4, NSEQ, 3)
    vs = a_f32(2, 512)
    sqb = a_bf16(512); rs = a_f32(512)
    qn = a_bf16(512); kn = a_bf16(512); qnf = a_f32(TS); knf = a_f32(TS)
    Bm = a_f32(2, 128); E = a_f32(2, 128); E2 = a_f32(2, 128); DTm = a_f32(2, 128)
    U = a_bf16(2, 128); UT = a_bf16(2, 128)
    UoM = [a_bf16(2, 128) for _ in range(3)]; UoTM = [a_bf16(2, 128) for _ in range(3)]
    Nb = [a_bf16(2, 128) for _ in range(2)]; Pb = [a_bf16(2, 128) for _ in range(2)]; PTb = [a_bf16(2, 128) for _ in range(2)]
    NTb = [a_bf16(2, 128) for _ in range(2)]; bm = a_bf16(4, 128)
    ktok = a_bf16(128); xv = a_bf16(2, 128); vnew = a_bf16(2, 128)
    Sbf = a_bf16(2, 128); og = a_bf16(256); ogT = a_bf16(2, 512)
    S32 = a_f32(2, 128); ss = a_f32(8)
    X = a_f32(NSEQ, TS); qgf = a_f32(TS)
    kdm = a_bf16(128); Sin = [a_f32(128) for _ in range(2)]; Sout = a_f32(128)
    ostg = acc
    eGl = [a_f32(2), a_f32(2)]
    w3f = K.wring[:, 3, :].bitcast(F32)
    w3b = K.wring[:, 3, :]
    szn2 = [w3f[:, 0:256], w3f[:, 256:512]]
    vtok2 = [w3f[:, 512:768].rearrange("p (a b) -> p a b", a=2), w3f[:, 768:1024].rearrange("p (a b) -> p a b", a=2)]

    def _b3(i):
        return w3b[:, 2048 + i * 256:2048 + (i + 1) * 256].rearrange("p (a b) -> p a b", a=2)
    Nfin = [_b3(0), _b3(1)]; attnT2 = [_b3(2), _b3(3)]; qg2 = [_b3(4), _b3(5)]; kd2 = [_b3(6), _b3(7)]

    def _f32v(b):
        return b.rearrange("p a b -> p (a b)").bitcast(F32)
    SinR = [Sin[0], Sin[1]] + [_f32v(x) for x in UoM + UoTM]
    SoutR = [Sout, _f32v(NTb[0]), _f32v(NTb[1])]
    RDEPTH = 7
    NLD = 2 * NSEQ
    mark = K.A.off

    S.dma("sp", "gc", lambda e: e.dma_start(out=gm.rearrange("p a b -> p (a b)"), in_=K.c_gmask), writes=["gm"])
    S.dma("sp", "gc", lambda e: e.dma_start(out=segmask.rearrange("p a b -> p (a b)"), in_=K.c_segmask), writes=["segmask"])
    S.dma("sp", "gc", lambda e: e.dma_start(out=segcol, in_=K.c_segcol), writes=["segcol"])
    S.dma("pool", "gcb", lambda e: e.dma_start(out=bm.rearrange("p a b -> p (a b)"), in_=K.c_bmask), writes=["bm"])
    S.dma("sp", "gc", lambda e: e.dma_start(out=gv, in_=K.gdn_vec.partition_broadcast(128)), writes=["gv"])
    S.dma("sp", "gc", lambda e: e.dma_start(out=gnw, in_=K.gdn_norm.partition_broadcast(128)), writes=["gnw"])
    S.dve(lambda e: e.memset(ones128, 1.0), writes=["ones128"])
    S.act(lambda e: e.activation(out=negA, in_=gv[:, 0:16], func=AF.Exp), reads=["gv"], writes=["negA"])
    S.dve(lambda e: e.tensor_scalar(out=negA, in0=negA, scalar1=-1.0, scalar2=None, op0=ALU.mult), reads=["negA"], writes=["negA"])
    TRIU = {False: gm[:, 0, :], True: gm[:, 4, :]}
    SU = {False: gm[:, 1, :], True: gm[:, 5, :]}
    INCL = {False: gm[:, 2, :], True: gm[:, 6, :]}
    STRICT = {False: gm[:, 3, :], True: gm[:, 7, :]}
    import os
    STG = int(os.environ.get('GDN_STAGE', 99))
    if STG == 0:
        S.barrier(); return
    Xflat = X.rearrange("p a b -> p (a b)")
    wst = [Xflat[0:4, 0:512], Xflat[0:4, 512:1024]]
    bw = ps_next()
    for pc in range(8):
        S.dma("sp", "wst%d" % (pc % 2), lambda e, pc=pc: e.dma_start(out=wst[pc % 2], in_=K.w_gconv[:, pc * 512:(pc + 1) * 512]), writes=[("wst", pc % 2)])
        for q in range(4):
            cidx = pc * 4 + q
            S.pe(lambda e, pc=pc, q=q, cidx=cidx: e.transpose(ps[:, bw, cidx * 4:(cidx + 1) * 4], wst[pc % 2][:, q * 128:(q + 1) * 128], ident_f[0:4, 0:4]), reads=[("wst", pc % 2), "ident_f"], writes=psk(bw))
    S.dve(lambda e: e.tensor_copy(out=wcT.rearrange("p a b -> p (a b)"), in_=ps[:, bw, 0:128]), reads=psk(bw), writes=["wcT"])
    if STG == 1:
        S.barrier(); return
    src = w_in[:, 6144:6176].rearrange("(k p) c -> p k c", p=128)
    S.dma("pool", "wba", lambda e: e.dma_start(out=wba, in_=src), writes=["wba"])
    if STG == 2:
        S.barrier(); return
    tiles = [(t * 128, 128, False) for t in range(16)] + [(TP, TS, True)]
    for tl, (t0, C, isS) in enumerate(tiles):
        pb = ps_next()
        for kc in range(KC):
            S.pe(lambda e, pb=pb, kc=kc, t0=t0, C=C: e.matmul(ps[0:C, pb, 0:32], lhsT=hT[:, kc, t0:t0 + C], rhs=wba[:, kc, :], start=(kc == 0), stop=(kc == KC - 1)), reads=["wba"], writes=psk(pb))
        S.act(lambda e, pb=pb, C=C, tl=tl: e.activation(out=beta_all[0:C, tl, :], in_=ps[0:C, pb, 0:16], func=AF.Sigmoid), reads=psk(pb), writes=[("beta", tl)])
        S.dve(lambda e, pb=pb, C=C, tl=tl: e.tensor_tensor(out=g_all[0:C, tl, :], in0=ps[0:C, pb, 16:32], in1=gv[0:C, 16:32], op=ALU.add), reads=psk(pb) + ["gv"], writes=[("g", tl)])
        S.act(lambda e, C=C, tl=tl: e.activation(out=g_all[0:C, tl, :], in_=g_all[0:C, tl, :], func=AF.Exp), reads=[("g", tl)], writes=[("g", tl)])
        S.act(lambda e, C=C, tl=tl: e.activation(out=g_all[0:C, tl, :], in_=g_all[0:C, tl, :], func=AF.Ln, bias=1.0), reads=[("g", tl)], writes=[("g", tl)])
        S.dve(lambda e, C=C, tl=tl: e.tensor_tensor(out=g_all[0:C, tl, :], in0=g_all[0:C, tl, :], in1=negA[0:C, :], op=ALU.mult), reads=[("g", tl), "negA"], writes=[("g", tl)])
        pg = ps_next()
        S.pe(lambda e, pg=pg, C=C, tl=tl, isS=isS: e.matmul(ps[0:C, pg, 0:16], lhsT=TRIU[isS][0:C, 0:C], rhs=g_all[0:C, tl, :], start=True, stop=True), reads=[("g", tl), "gm"], writes=psk(pg))
        S.pe(lambda e, pg=pg, C=C, tl=tl, isS=isS: e.matmul(ps[0:C, pg, 16:32], lhsT=SU[isS][0:C, 0:C], rhs=g_all[0:C, tl, :], start=True, stop=True), reads=[("g", tl), "gm"], writes=psk(pg))
        S.act(lambda e, pg=pg, C=C, tl=tl: e.activation(out=negeG_all[0:C, tl, :], in_=ps[0:C, pg, 0:16], func=AF.Exp), reads=psk(pg), writes=[("negeG", tl)])
        S.dve(lambda e, C=C, tl=tl: e.tensor_scalar(out=negeG_all[0:C, tl, :], in0=negeG_all[0:C, tl, :], scalar1=-1.0, scalar2=None, op0=ALU.mult), reads=[("negeG", tl)], writes=[("negeG", tl)])
        S.act(lambda e, pg=pg, C=C, tl=tl: e.activation(out=kdec_all[0:C, tl, :], in_=ps[0:C, pg, 16:32], func=AF.Exp), reads=psk(pg), writes=[("kdec", tl)])
    S.barrier()
    K.A.off = mark
    gbase = 48 + 16

    import os
    for hk in range(int(os.environ.get("GDN_HK0", 0)), int(os.environ.get("GDN_NHK", GDN_HK))):
        hv0 = 2 * hk
        wq, wqk = K.load_w_bf16(w_in, KC, 128, col0=hk * 128, slot=0, off=0, key=("w", 0, "q"))
        wk, wkk = K.load_w_bf16(w_in, KC, 128, col0=1024 + hk * 128, slot=0, off=1024, key=("w", 0, "k"))
        wv, wvk = K.load_w_bf16(w_in, KC, 256, col0=2048 + hk * 256, slot=1, off=0, key=("w", 1, "v"))
        wz, wzk = K.load_w_bf16(w_in, KC, 256, col0=4096 + hk * 256, slot=1, off=2048, key=("w", 1, "z"))
        wo, wok = K.load_w_bf16(w_out, 2, 1024, row0=hk * 256, slot=2)
        cids = [hk, 8 + hk, 16 + 2 * hk, 17 + 2 * hk]
        CUT = os.environ.get('GDN_CUT', '')
        if hk >= 1 and 'D' in CUT:
            S.barrier(); continue
        gst = a_f32(512, parts=48)
        K.A.off = mark
        if not (hk >= 1 and 'H' in CUT):
            for ci, cid in enumerate(cids):
                S.dma("sp", "gst", lambda e, ci=ci, cid=cid, gst=gst: e.dma_start(out=gst[:, ci * 128:(ci + 1) * 128], in_=K.s_gconv[:, cid * 128:(cid + 1) * 128]), writes=["gst"])
            bq = ps_next()
            for ci in range(4):
                S.pe(lambda e, ci=ci, bq=bq, gst=gst: e.transpose(ps[:, bq, ci * 48:(ci + 1) * 48], gst[:, ci * 128:(ci + 1) * 128], ident_f[0:48, 0:48]), reads=["gst", "ident_f"], writes=psk(bq))
            S.dve(lambda e, bq=bq: e.tensor_copy(out=gsT.rearrange("p a b -> p (a b)"), in_=ps[:, bq, 0:192]), reads=psk(bq), writes=["gsT"])
        S.dve(lambda e: e.memset(cch, 0.0), writes=["cch"])
        if hk >= 1 and 'E' in CUT:
            S.barrier(); continue
        for bi, (t0, nt) in enumerate(K.ALLBLOCKS):
            isS = t0 >= TP
            C = 64 if isS else 128
            lastblk = (t0 + nt == TP)
            if isS:
                S.barrier()
            for ci, (wt, wkey, col) in enumerate([(wq, wqk, 0), (wk, wkk, 0), (wv, wvk, 0), (wv, wvk, 128)]):
                pp = ps_next()
                for kc in range(KC):
                    S.pe(lambda e, pp=pp, kc=kc, wt=wt, col=col, t0=t0, nt=nt: e.matmul(ps[:, pp, 0:nt], lhsT=wt[:, kc, col:col + 128], rhs=hT[:, kc, t0:t0 + nt], start=(kc == 0), stop=(kc == KC - 1)), reads=[wkey], writes=psk(pp))
                cid = cids[ci]
                wc = [wcT[:, cid, i:i + 1] for i in range(4)]
                accb = acc if ci % 2 == 0 else acc2
                ak = ("acc", ci % 2)
                dst = accb[:, 0:nt] if ci < 2 else vs[:, ci - 2, 0:nt]
                if not isS:
                    S.dve(lambda e, ci=ci: e.tensor_copy(out=Gx[:, 0:3], in_=cch[:, ci, :]), reads=["cch"], writes=["Gx"])
                    S.act(lambda e, pp=pp, nt=nt: e.copy(out=Gx[:, 3:3 + nt], in_=ps[:, pp, 0:nt]), reads=psk(pp), writes=["Gx"])
                    S.dve(lambda e, ci=ci, nt=nt: e.tensor_copy(out=cch[:, ci, :], in_=Gx[:, nt:nt + 3]), reads=["Gx"], writes=["cch"])
                    if lastblk:
                        S.dve(lambda e, ci=ci, nt=nt: e.tensor_copy(out=tailP[:, ci, :], in_=Gx[:, nt:nt + 3]), reads=["Gx"], writes=["tailP"])
                    x3 = [Gx[:, i:i + nt] for i in range(4)]
                    a_ = accb[:, 0:nt]
                    d_ = dst
                    gk = ["Gx"]
                else:
                    S.dve(lambda e, ci=ci: e.tensor_copy(out=Gxs[:, :, 0:3], in_=gsT[:, ci, :].rearrange("p (s r) -> p s r", r=3)), reads=["gsT"], writes=["Gxs"])
                    S.act(lambda e, pp=pp: e.copy(out=Gxs[:, :, 3:7], in_=ps[:, pp, 0:TS].rearrange("p (s j) -> p s j", j=4)), reads=psk(pp), writes=["Gxs"])
                    S.dve(lambda e, ci=ci: e.tensor_copy(out=tailS[:, ci, :, :], in_=Gxs[:, :, 4:7]), reads=["Gxs"], writes=["tailS"])
                    x3 = [Gxs[:, :, i:i + 4] for i in range(4)]
                    a_ = accb[:, 0:TS].rearrange("p (s j) -> p s j", j=4)
                    d_ = dst.rearrange("p (s j) -> p s j", j=4)
                    gk = ["Gxs"]
                S.act(lambda e, a_=a_, x3=x3, wc=wc: e.activation(out=a_, in_=x3[3], func=AF.Identity, scale=wc[3]), reads=gk + ["wcT"], writes=[ak])
                for i in (2, 1, 0):
                    S.dve(lambda e, a_=a_, x3=x3, wc=wc, i=i: e.scalar_tensor_tensor(out=a_, in0=x3[i], scalar=wc[i], in1=a_, op0=ALU.mult, op1=ALU.add), reads=gk + [ak], writes=[ak])
                S.act(lambda e, a_=a_, d_=d_: e.activation(out=d_, in_=a_, func=AF.Silu), reads=[ak], writes=([ak, ("cv", ci)] if ci < 2 else [("cv", ci)]))
                if ci < 2:
                    S.act(lambda e, nt=nt, accb=accb: e.activation(out=sqb[:, 0:nt], in_=accb[:, 0:nt], func=AF.Square), reads=[ak], writes=["sqb"])
                    pn = ps_next()
                    S.pe(lambda e, pn=pn, nt=nt: e.matmul(ps[:, pn, 0:nt], lhsT=ones_b[:], rhs=sqb[:, 0:nt], start=True, stop=True), reads=["sqb", "ones_b"], writes=psk(pn))
                    S.act(lambda e, pn=pn, nt=nt: e.activation(out=rs[:, 0:nt], in_=ps[:, pn, 0:nt], func=AF.Sqrt, bias=EPS, scale=1.0), reads=psk(pn), writes=["rs"])
                    S.dve(lambda e, nt=nt: e.reciprocal(out=rs[:, 0:nt], in_=rs[:, 0:nt]), reads=["rs"], writes=["rs"])
                    dn = qn if ci == 0 else kn
                    dnf = qnf if ci == 0 else knf
                    sc = (128.0 ** -0.5) if ci == 0 else 1.0
                    if isS:
                        S.dve(lambda e, dnf=dnf, sc=sc, accb=accb: e.scalar_tensor_tensor(out=dnf[:, :], in0=accb[:, 0:TS], scalar=sc, in1=rs[:, 0:TS], op0=ALU.mult, op1=ALU.mult), reads=[ak, "rs"], writes=[("nf", ci)])
                        S.act(lambda e, dn=dn, dnf=dnf: e.copy(out=dn[:, 0:TS], in_=dnf[:, :]), reads=[("nf", ci)], writes=[("n", ci)])
                    else:
                        S.dve(lambda e, dn=dn, sc=sc, nt=nt, accb=accb: e.scalar_tensor_tensor(out=dn[:, 0:nt], in0=accb[:, 0:nt], scalar=sc, in1=rs[:, 0:nt], op0=ALU.mult, op1=ALU.mult), reads=[ak, "rs"], writes=[("n", ci)])
            def chunk_gen(c0, pb):
                tl = (t0 + c0) // 128
                first = (not isS) and t0 == 0 and c0 == 0
                last = (not isS) and (t0 + c0 + C == TP)
                L = 1 if isS else 6
                bvt = ps_next()
                for e_ in range(2):
                    S.pe(lambda e, e_=e_, bvt=bvt, c0=c0, C=C: e.transpose(ps[0:C, bvt, e_ * 128:(e_ + 1) * 128], vs[:, e_, c0:c0 + C], ident_f[:]), reads=[("cv", 2 + e_), "ident_f"], writes=psk(bvt))
                    yield
                S.act(lambda e, bvt=bvt, C=C: e.copy(out=vtok2[pb][0:C].rearrange("p a b -> p (a b)"), in_=ps[0:C, bvt, 0:256]), reads=psk(bvt), writes=[("vtok", pb)])
                yield
                bz = ps_next()
                for kc in range(KC):
                    S.pe(lambda e, bz=bz, kc=kc, t0=t0, c0=c0, C=C: e.matmul(ps[0:C, bz, 0:256], lhsT=hT[:, kc, t0 + c0:t0 + c0 + C], rhs=wz[:, kc, :], start=(kc == 0), stop=(kc == KC - 1)), reads=[wzk], writes=psk(bz))
                    yield
                S.act(lambda e, bz=bz, C=C: e.activation(out=szn2[pb][0:C, :], in_=ps[0:C, bz, 0:256], func=AF.Silu), reads=psk(bz), writes=[("szn", pb)])
                yield
                S.dve(lambda e, C=C: e.tensor_tensor(out=szn2[pb][0:C, :].rearrange("p (a b) -> p a b", a=2), in0=szn2[pb][0:C, :].rearrange("p (a b) -> p a b", a=2), in1=gnw[0:C, :].unsqueeze(1).to_broadcast([C, 2, 128]), op=ALU.mult), reads=[("szn", pb), "gnw"], writes=[("szn", pb)])
                yield
                bkt = ps_next()
                pkb = ps[:, bkt, :].bitcast(BF16)
                S.pe(lambda e, pkb=pkb, c0=c0, C=C: e.transpose(pkb[0:C, 0:128], kn[:, c0:c0 + C], ident_b[:]), reads=[("n", 1), "ident_b"], writes=psk(bkt))
                yield
                S.act(lambda e, pkb=pkb, C=C: e.copy(out=ktok[0:C, :], in_=pkb[0:C, 0:128]), reads=psk(bkt), writes=["ktok"])
                yield
                bkk = ps_next()
                S.pe(lambda e, bkk=bkk, c0=c0, C=C: e.matmul(ps[0:C, bkk, 0:C], lhsT=kn[:, c0:c0 + C], rhs=kn[:, c0:c0 + C], start=True, stop=True), reads=[("n", 1)], writes=psk(bkk))
                yield
                S.pe(lambda e, bkk=bkk, c0=c0, C=C: e.matmul(ps[0:C, bkk, 128:128 + C], lhsT=kn[:, c0:c0 + C], rhs=qn[:, c0:c0 + C], start=True, stop=True), reads=[("n", 0), ("n", 1)], writes=psk(bkk))
                yield
                bd = ps_next()
                bd2 = ps_next()
                for e_ in range(2):
                    hv = hv0 + e_
                    S.dve(lambda e, e_=e_, hv=hv, C=C, tl=tl, isS=isS: e.tensor_scalar(out=Bm[0:C, e_, 0:C], in0=TRIU[isS][0:C, 0:C], scalar1=g_all[0:C, tl, hv:hv + 1], scalar2=None, op0=ALU.mult), reads=["gm"], writes=[("Bm", e_)])
                    yield
                    S.pe(lambda e, e_=e_, bd=bd, C=C, isS=isS: e.matmul(ps[0:C, bd, e_ * 128:e_ * 128 + C], lhsT=SU[isS][0:C, 0:C], rhs=Bm[0:C, e_, 0:C], start=True, stop=True), reads=[("Bm", e_), "gm"], writes=psk(bd))
                    yield
                    S.pe(lambda e, e_=e_, bd2=bd2, C=C: e.matmul(ps[:, bd2, e_ * 128:e_ * 128 + C], lhsT=ones128[0:C, :], rhs=Bm[0:C, e_, 0:C], start=True, stop=True), reads=[("Bm", e_), "ones128"], writes=psk(bd2))
                    yield
                S.act(lambda e, bd=bd, C=C: e.activation(out=E[0:C, :, 0:C], in_=ps[0:C, bd, 0:256].rearrange("p (a b) -> p a b", a=2)[:, :, 0:C], func=AF.Exp), reads=psk(bd), writes=["E"])
                yield
                S.act(lambda e, bd2=bd2, C=C: e.activation(out=E2[:, :, 0:C], in_=ps[:, bd2, 0:256].rearrange("p (a b) -> p a b", a=2)[:, :, 0:C], func=AF.Exp), reads=psk(bd2), writes=["E2"])
                yield
                S.dve(lambda e, C=C, isS=isS: e.tensor_tensor(out=DTm[0:C, :, 0:C], in0=E[0:C, :, 0:C], in1=INCL[isS][0:C, 0:C].unsqueeze(1).to_broadcast([C, 2, C]), op=ALU.mult), reads=["E", "gm"], writes=["DTm"])
                yield
                S.dve(lambda e, C=C, isS=isS: e.tensor_tensor(out=E[0:C, :, 0:C], in0=E[0:C, :, 0:C], in1=STRICT[isS][0:C, 0:C].unsqueeze(1).to_broadcast([C, 2, C]), op=ALU.mult), reads=["E", "gm", "DTm"], writes=["E"])
                yield
                for e_ in range(2):
                    hv = hv0 + e_
                    S.dve(lambda e, e_=e_, hv=hv, bkk=bkk, C=C, tl=tl: e.scalar_tensor_tensor(out=U[0:C, e_, 0:C], in0=ps[0:C, bkk, 0:C], scalar=beta_all[0:C, tl, hv:hv + 1], in1=E[0:C, e_, 0:C], op0=ALU.mult, op1=ALU.mult), reads=psk(bkk) + ["E"], writes=[("U", e_)])
                    yield
                    S.dve(lambda e, e_=e_, bkk=bkk, C=C: e.tensor_tensor(out=attnT2[pb][0:C, e_, 0:C], in0=ps[0:C, bkk, 128:128 + C], in1=DTm[0:C, e_, 0:C], op=ALU.mult), reads=psk(bkk) + ["DTm"], writes=[("attnT", e_, pb)])
                    yield
                    if isS:
                        pass
                    else:
                        S.pool(lambda e, e_=e_, c0=c0, C=C: e.tensor_tensor(out=qg2[pb][:, e_, 0:C], in0=qn[:, c0:c0 + C], in1=E2[:, e_, 0:C], op=ALU.mult), reads=[("n", 0), "E2"], writes=[("qg", e_, pb)])
                        yield
                    S.dve(lambda e, e_=e_, hv=hv, C=C, tl=tl: e.tensor_scalar(out=kd2[pb][0:C, e_, :], in0=ktok[0:C, :], scalar1=kdec_all[0:C, tl, hv:hv + 1], scalar2=None, op0=ALU.mult), reads=["ktok"], writes=[("kd", e_, pb)])
                    yield
                but = ps_next()
                pub = ps[:, but, :].bitcast(BF16)
                for e_ in range(2):
                    S.pe(lambda e, e_=e_, pub=pub, C=C: e.transpose(pub[0:C, e_ * 128:e_ * 128 + C], U[0:C, e_, 0:C], ident_b[0:C, 0:C]), reads=[("U", e_), "ident_b"], writes=psk(but))
                    yield
                S.act(lambda e, pub=pub, C=C: e.copy(out=UT[0:C, :, 0:C], in_=pub[0:C, 0:256].rearrange("p (a b) -> p a b", a=2)[:, :, 0:C]), reads=psk(but), writes=["UT"])
                yield
                if isS:
                    S.dve(lambda e, C=C: e.tensor_tensor(out=Nb[0][0:C, :, 0:C], in0=ident_f[0:C, 0:C].unsqueeze(1).to_broadcast([C, 2, C]), in1=U[0:C, :, 0:C], op=ALU.subtract), reads=[("U", 0), ("U", 1), "ident_f"], writes=[("N", 0)])
                    yield
                    Pprev, PTprev, Pk_, PTk_ = U, UT, ["U0", "U1"], ["UT"]
                    Pkeys_prev = [("U", 0), ("U", 1)]
                    PTkeys_prev = ["UT"]
                    ni = 0
                    for lv in range(1, L + 1):
                        pi = lv % 2
                        need_P = lv < L
                        if need_P:
                            b1 = ps_next()
                            for e_ in range(2):
                                S.pe(lambda e, e_=e_, b1=b1, C=C, PTprev=PTprev, Pprev=Pprev: e.matmul(ps[0:C, b1, e_ * 128:e_ * 128 + C], lhsT=PTprev[0:C, e_, 0:C], rhs=Pprev[0:C, e_, 0:C], start=True, stop=True), reads=Pkeys_prev + PTkeys_prev, writes=psk(b1))
                                yield
                            S.act(lambda e, b1=b1, C=C, pi=pi: e.copy(out=Pb[pi][0:C, :, 0:C], in_=ps[0:C, b1, 0:256].rearrange("p (a b) -> p a b", a=2)[:, :, 0:C]), reads=psk(b1), writes=[("P", pi)])
                            yield
                        b2 = ps_next()
                        for e_ in range(2):
                            S.pe(lambda e, e_=e_, b2=b2, C=C, PTprev=PTprev, Pprev=Pprev: e.matmul(ps[0:C, b2, e_ * 128:e_ * 128 + C], lhsT=Pprev[0:C, e_, 0:C], rhs=PTprev[0:C, e_, 0:C], start=True, stop=True), reads=Pkeys_prev + PTkeys_prev, writes=psk(b2))
                            yield
                        S.act(lambda e, b2=b2, C=C, pi=pi: e.copy(out=PTb[pi][0:C, :, 0:C], in_=ps[0:C, b2, 0:256].rearrange("p (a b) -> p a b", a=2)[:, :, 0:C]), reads=psk(b2), writes=[("PT", pi)])
                        yield
                        b3 = ps_next()
                        for e_ in range(2):
                            S.pe(lambda e, e_=e_, b3=b3, C=C, pi=pi, ni=ni: e.matmul(ps[0:C, b3, e_ * 128:e_ * 128 + C], lhsT=PTb[pi][0:C, e_, 0:C], rhs=Nb[ni][0:C, e_, 0:C], start=True, stop=True), reads=[("PT", pi), ("N", ni)], writes=psk(b3))
                            yield
                        S.dve(lambda e, b3=b3, C=C, ni=ni: e.tensor_tensor(out=Nb[1 - ni][0:C, :, 0:C], in0=ps[0:C, b3, 0:256].rearrange("p (a b) -> p a b", a=2)[:, :, 0:C], in1=Nb[ni][0:C, :, 0:C], op=ALU.add), reads=psk(b3) + [("N", ni)], writes=[("N", 1 - ni)])
                        yield
                        ni = 1 - ni
                        Pprev, PTprev = Pb[pi], PTb[pi]
                        Pkeys_prev = [("P", pi)]
                        PTkeys_prev = [("PT", pi)]

                else:
                    def _ev2(b, C=C):
                        return ps[0:C, b, 0:256].rearrange("p (a b) -> p a b", a=2)[:, :, 0:C]
                    idb = ident_f[0:C, 0:C].unsqueeze(1).to_broadcast([C, 2, C])
                    S.dve(lambda e: e.tensor_tensor(out=Pb[0][:, :, :], in0=U[:, :, :], in1=bm[:, 0, :].unsqueeze(1).to_broadcast([128, 2, 128]), op=ALU.mult), reads=[("U", 0), ("U", 1), "bm"], writes=[("P", 0)])
                    yield
                    S.dve(lambda e: e.tensor_tensor(out=PTb[0][:, :, :], in0=UT[:, :, :], in1=bm[:, 0, :].unsqueeze(1).to_broadcast([128, 2, 128]), op=ALU.mult), reads=["UT", "bm"], writes=[("PT", 0)])
                    yield
                    S.dve(lambda e, idb=idb: e.tensor_tensor(out=Nb[0][:, :, :], in0=idb, in1=Pb[0][:, :, :], op=ALU.subtract), reads=[("P", 0), "ident_f"], writes=[("N", 0)])
                    yield
                    S.dve(lambda e, idb=idb: e.tensor_tensor(out=NTb[0][:, :, :], in0=idb, in1=PTb[0][:, :, :], op=ALU.subtract), reads=[("PT", 0), "ident_f"], writes=[("NT", 0)])
                    yield
                    for mi in range(3):
                        S.pool(lambda e, mi=mi: e.tensor_tensor(out=UoM[mi][:, :, :], in0=U[:, :, :], in1=bm[:, mi + 1, :].unsqueeze(1).to_broadcast([128, 2, 128]), op=ALU.mult), reads=[("U", 0), ("U", 1), "bm"], writes=[("UoM", mi)])
                        yield
                        S.pool(lambda e, mi=mi: e.tensor_tensor(out=UoTM[mi][:, :, :], in0=UT[:, :, :], in1=bm[:, mi + 1, :].unsqueeze(1).to_broadcast([128, 2, 128]), op=ALU.mult), reads=["UT", "bm"], writes=[("UoTM", mi)])
                        yield
                    ni = 0
                    pprev = 0
                    for lv in range(1, 4):
                        pi = lv % 2
                        b1 = ps_next(); b2 = ps_next(); b3 = ps_next(); b4 = ps_next()
                        for e_ in range(2):
                            S.pe(lambda e, e_=e_, b1=b1, pprev=pprev: e.matmul(ps[:, b1, e_ * 128:(e_ + 1) * 128], lhsT=PTb[pprev][:, e_, :], rhs=Pb[pprev][:, e_, :], start=True, stop=True), reads=[("P", pprev), ("PT", pprev)], writes=psk(b1))
                            yield
                        for e_ in range(2):
                            S.pe(lambda e, e_=e_, b2=b2, pprev=pprev: e.matmul(ps[:, b2, e_ * 128:(e_ + 1) * 128], lhsT=Pb[pprev][:, e_, :], rhs=PTb[pprev][:, e_, :], start=True, stop=True), reads=[("P", pprev), ("PT", pprev)], writes=psk(b2))
                            yield
                        S.act(lambda e, b1=b1, pi=pi: e.copy(out=Pb[pi][:, :, :], in_=_ev2(b1)), reads=psk(b1), writes=[("P", pi)])
                        yield
                        S.act(lambda e, b2=b2, pi=pi: e.copy(out=PTb[pi][:, :, :], in_=_ev2(b2)), reads=psk(b2), writes=[("PT", pi)])
                        yield
                        for e_ in range(2):
                            S.pe(lambda e, e_=e_, b3=b3, pi=pi, ni=ni: e.matmul(ps[:, b3, e_ * 128:(e_ + 1) * 128], lhsT=PTb[pi][:, e_, :], rhs=Nb[ni][:, e_, :], start=True, stop=True), reads=[("PT", pi), ("N", ni)], writes=psk(b3))
                            yield
                        for e_ in range(2):
                            S.pe(lambda e, e_=e_, b4=b4, pi=pi, ni=ni: e.matmul(ps[:, b4, e_ * 128:(e_ + 1) * 128], lhsT=Pb[pi][:, e_, :], rhs=NTb[ni][:, e_, :], start=True, stop=True), reads=[("P", pi), ("NT", ni)], writes=psk(b4))
                            yield
                        S.dve(lambda e, b3=b3, ni=ni: e.tensor_tensor(out=Nb[1 - ni][:, :, :], in0=_ev2(b3), in1=Nb[ni][:, :, :], op=ALU.add), reads=psk(b3) + [("N", ni)], writes=[("N", 1 - ni)])
                        yield
                        S.dve(lambda e, b4=b4, ni=ni: e.tensor_tensor(out=NTb[1 - ni][:, :, :], in0=_ev2(b4), in1=NTb[ni][:, :, :], op=ALU.add), reads=psk(b4) + [("NT", ni)], writes=[("NT", 1 - ni)])
                        yield
                        ni = 1 - ni
                        pprev = pi
                    for mi in range(3):
                        lastm = (mi == 2)
                        b1 = ps_next()
                        for e_ in range(2):
                            S.pe(lambda e, e_=e_, b1=b1, ni=ni, mi=mi: e.matmul(ps[:, b1, e_ * 128:(e_ + 1) * 128], lhsT=UoTM[mi][:, e_, :], rhs=Nb[ni][:, e_, :], start=True, stop=True), reads=[("UoTM", mi), ("N", ni)], writes=psk(b1))
                            yield
                        S.act(lambda e, b1=b1: e.copy(out=Pb[1][:, :, :], in_=_ev2(b1)), reads=psk(b1), writes=[("P", 1)])
                        yield
                        if not lastm:
                            b3 = ps_next()
                            for e_ in range(2):
                                S.pe(lambda e, e_=e_, b3=b3, ni=ni, mi=mi: e.matmul(ps[:, b3, e_ * 128:(e_ + 1) * 128], lhsT=UoM[mi][:, e_, :], rhs=NTb[ni][:, e_, :], start=True, stop=True), reads=[("UoM", mi), ("NT", ni)], writes=psk(b3))
                                yield
                            S.act(lambda e, b3=b3: e.copy(out=PTb[1][:, :, :], in_=_ev2(b3)), reads=psk(b3), writes=[("PT", 1)])
                            yield
                        b2 = ps_next()
                        for e_ in range(2):
                            S.pe(lambda e, e_=e_, b2=b2, ni=ni: e.matmul(ps[:, b2, e_ * 128:(e_ + 1) * 128], lhsT=NTb[ni][:, e_, :], rhs=Pb[1][:, e_, :], start=True, stop=True), reads=[("NT", ni), ("P", 1)], writes=psk(b2))
                            yield
                        S.dve(lambda e, b2=b2, ni=ni, lastm=lastm: e.tensor_tensor(out=(Nfin[pb] if lastm else Nb[1 - ni])[:, :, :], in0=Nb[ni][:, :, :], in1=_ev2(b2), op=ALU.subtract), reads=psk(b2) + [("N", ni)], writes=[(("Nfin", pb) if lastm else ("N", 1 - ni))])
                        yield
                        if not lastm:
                            b4 = ps_next()
                            for e_ in range(2):
                                S.pe(lambda e, e_=e_, b4=b4, ni=ni: e.matmul(ps[:, b4, e_ * 128:(e_ + 1) * 128], lhsT=Nb[ni][:, e_, :], rhs=PTb[1][:, e_, :], start=True, stop=True), reads=[("N", ni), ("PT", 1)], writes=psk(b4))
                                yield
                            S.dve(lambda e, b4=b4, ni=ni: e.tensor_tensor(out=NTb[1 - ni][:, :, :], in0=NTb[ni][:, :, :], in1=_ev2(b4), op=ALU.subtract), reads=psk(b4) + [("NT", ni)], writes=[("NT", 1 - ni)])
                            yield
                        ni = 1 - ni
                if isS:
                    S.pool(lambda e, ni=ni, C=C: e.tensor_copy(out=Nfin[pb][0:C, :, 0:C], in_=Nb[ni][0:C, :, 0:C]), reads=[("N", ni)], writes=[("Nfin", pb)])
                    yield
                Nf = Nfin[pb]
                Nkey = ("Nfin", pb)
                if not isS:
                    S.pool(lambda e, C=C: e.tensor_copy(out=eGl[pb][:, 0:2], in_=E2[:, :, C - 1:C].rearrange("p a b -> p (a b)")), reads=["E2"], writes=[("eGl", pb)])
                    yield
                yield "SPLIT"
                if not isS:
                    if first:
                        S.act(lambda e, C=C: e.copy(out=xv[0:C].rearrange("p a b -> p (a b)"), in_=vtok2[pb][0:C].rearrange("p a b -> p (a b)")), reads=[("vtok", pb)], writes=["xv"])
                        yield
                    else:
                        pk_ = ps_next()
                        S.pe(lambda e, pk_=pk_, c0=c0, C=C: e.matmul(ps[0:C, pk_, 0:256], lhsT=kn[:, c0:c0 + C], rhs=Sbf[:].rearrange("p a b -> p (a b)"), start=True, stop=True), reads=[("n", 1), "Sbf"], writes=psk(pk_))
                        yield
                        for e_ in range(2):
                            hv = hv0 + e_
                            S.dve(lambda e, e_=e_, hv=hv, pk_=pk_, C=C, tl=tl: e.scalar_tensor_tensor(out=xv[0:C, e_, :], in0=ps[0:C, pk_, e_ * 128:(e_ + 1) * 128], scalar=negeG_all[0:C, tl, hv:hv + 1], in1=vtok2[pb][0:C, e_, :], op0=ALU.mult, op1=ALU.add), reads=psk(pk_) + [("vtok", pb)], writes=["xv"])
                            yield
                else:
                    S.dve(lambda e: e.tensor_tensor(out=X[:, :, :], in0=knf[:, :].unsqueeze(1).to_broadcast([128, NSEQ, TS]), in1=segmask[:, :, :], op=ALU.mult), reads=[("nf", 1), "segmask"], writes=["X"])
                    yield
                    pks = [ps_next(), ps_next()]
                    K.P.reserved = set(pks)
                    def _ldA(n):
                        k_ = n % 8
                        S.dma("sp", "gsin%d" % k_, lambda e, k_=k_, s2=n % NSEQ, hv2=hv0 + n // NSEQ: e.dma_start(out=SinR[k_], in_=K.s_gdn[s2, hv2]), writes=[("SinR", k_)])
                    for n_ in range(RDEPTH):
                        _ldA(n_)
                        yield
                    for e_ in range(2):
                        hv = hv0 + e_
                        for s_ in range(NSEQ):
                            n_ = e_ * NSEQ + s_
                            k_ = n_ % 8
                            S.pe(lambda e, e_=e_, s_=s_, k_=k_, pks=pks: e.matmul(ps[0:TS, pks[e_], 0:128], lhsT=X[:, s_, :], rhs=SinR[k_], start=(s_ == 0), stop=(s_ == NSEQ - 1)), reads=["X", ("SinR", k_)], writes=psk(pks[e_]))
                            yield
                            if n_ + RDEPTH < NLD:
                                _ldA(n_ + RDEPTH)
                                yield
                        S.dve(lambda e, e_=e_, hv=hv, tl=tl, pks=pks: e.scalar_tensor_tensor(out=xv[0:TS, e_, :], in0=ps[0:TS, pks[e_], 0:128], scalar=negeG_all[0:TS, tl, hv:hv + 1], in1=vtok2[pb][0:TS, e_, :], op0=ALU.mult, op1=ALU.add), reads=psk(pks[e_]) + [("vtok", pb)], writes=["xv"])
                        yield
                    K.P.reserved = set()
                pv = ps_next()
                for e_ in range(2):
                    S.pe(lambda e, e_=e_, pv=pv, C=C, Nf=Nf: e.matmul(ps[0:C, pv, e_ * 128:(e_ + 1) * 128], lhsT=Nf[0:C, e_, 0:C], rhs=xv[0:C, e_, :], start=True, stop=True), reads=[Nkey, "xv"], writes=psk(pv))
                    yield
                for e_ in range(2):
                    hv = hv0 + e_
                    S.act(lambda e, e_=e_, hv=hv, pv=pv, C=C, tl=tl: e.activation(out=vnew[0:C, e_, :], in_=ps[0:C, pv, e_ * 128:(e_ + 1) * 128], func=AF.Identity, scale=beta_all[0:C, tl, hv:hv + 1]), reads=psk(pv), writes=[("vnew", e_)])
                    yield
                if not isS:
                    po = ps_next()
                    po_aps = [ps[0:C, po, 0:128], ps[0:C, po, 128:256]]
                    po_keys = [psk(po), psk(po)]
                    for e_ in range(2):
                        if not first:
                            S.pe(lambda e, e_=e_, C=C, po_aps=po_aps: e.matmul(po_aps[e_], lhsT=qg2[pb][:, e_, 0:C], rhs=Sbf[:, e_, :], start=True, stop=False), reads=[("qg", e_, pb), "Sbf"], writes=po_keys[e_])
                            yield
                        S.pe(lambda e, e_=e_, C=C, po_aps=po_aps, first=first: e.matmul(po_aps[e_], lhsT=attnT2[pb][0:C, e_, 0:C], rhs=vnew[0:C, e_, :], start=first, stop=True), reads=[("attnT", e_, pb), ("vnew", e_)], writes=po_keys[e_])
                        yield
                    pS_ = ps_next()
                    for e_ in range(2):
                        S.pe(lambda e, e_=e_, pS_=pS_, C=C: e.matmul(ps[:, pS_, e_ * 128:(e_ + 1) * 128], lhsT=kd2[pb][0:C, e_, :], rhs=vnew[0:C, e_, :], start=True, stop=True), reads=[("kd", e_, pb), ("vnew", e_)], writes=psk(pS_))
                        yield
                    for e_ in range(2):
                        hv = hv0 + e_
                        if first:
                            S.dve(lambda e, e_=e_, pS_=pS_: e.tensor_copy(out=S32[:, e_, :], in_=ps[:, pS_, e_ * 128:(e_ + 1) * 128]), reads=psk(pS_), writes=[("S32", e_)])
                            yield
                        else:
                            S.dve(lambda e, e_=e_, pS_=pS_, C=C: e.scalar_tensor_tensor(out=S32[:, e_, :], in0=S32[:, e_, :], scalar=eGl[pb][:, e_:e_ + 1], in1=ps[:, pS_, e_ * 128:(e_ + 1) * 128], op0=ALU.mult, op1=ALU.add), reads=psk(pS_) + [("S32", e_), ("eGl", pb)], writes=[("S32", e_)])
                            yield
                        if last:
                            S.dma("sp", "gpo", lambda e, e_=e_, hv=hv: e.dma_start(out=K.gdn_p[hv], in_=S32[:, e_, :]), reads=[("S32", e_)])
                            yield
                    if not last:
                        S.act(lambda e: e.copy(out=Sbf[:].rearrange("p a b -> p (a b)"), in_=S32[:].rearrange("p a b -> p (a b)")), reads=[("S32", 0), ("S32", 1)], writes=["Sbf"])
                        yield
                else:
                    pos = [ps_next(), ps_next()]
                    K.P.reserved = set(pos)
                    po_aps = [ps[0:TS, pos[0], 0:128], ps[0:TS, pos[1], 0:128]]
                    po_keys = [psk(pos[0]), psk(pos[1])]
                    def _ldB(n):
                        k_ = n % 8
                        S.dma("sp", "gsin%d" % k_, lambda e, k_=k_, s2=n % NSEQ, hv2=hv0 + n // NSEQ: e.dma_start(out=SinR[k_], in_=K.s_gdn[s2, hv2]), writes=[("SinR", k_)])
                    for n_ in range(RDEPTH):
                        _ldB(n_)
                        yield
                    for e_ in range(2):
                        hv = hv0 + e_
                        S.pe(lambda e, e_=e_, po_aps=po_aps: e.matmul(po_aps[e_], lhsT=attnT2[pb][0:TS, e_, 0:TS], rhs=vnew[0:TS, e_, :], start=True, stop=False), reads=[("attnT", e_, pb), ("vnew", e_)], writes=po_keys[e_])
                        yield
                        S.pool(lambda e, e_=e_: e.tensor_tensor(out=qgf[:, :], in0=qnf[:, :], in1=E2[:, e_, 0:TS], op=ALU.mult), reads=[("nf", 0), "E2"], writes=["qgf"])
                        yield
                        S.dve(lambda e: e.tensor_tensor(out=X[:, :, :], in0=qgf[:, :].unsqueeze(1).to_broadcast([128, NSEQ, TS]), in1=segmask[:, :, :], op=ALU.mult), reads=["qgf", "segmask"], writes=["X"])
                        yield
                        for s_ in range(NSEQ):
                            n_ = e_ * NSEQ + s_
                            k_ = n_ % 8
                            j_ = n_ % 3
                            S.pe(lambda e, e_=e_, s_=s_, k_=k_, po_aps=po_aps: e.matmul(po_aps[e_], lhsT=X[:, s_, :], rhs=SinR[k_], start=False, stop=(s_ == NSEQ - 1)), reads=["X", ("SinR", k_)], writes=po_keys[e_])
                            yield
                            S.dve(lambda e, e_=e_, s_=s_: e.tensor_scalar(out=kdm[0:TS, :], in0=kd2[pb][0:TS, e_, :], scalar1=segcol[0:TS, s_:s_ + 1], scalar2=None, op0=ALU.mult), reads=[("kd", e_, pb), "segcol"], writes=["kdm"])
                            yield
                            pS_ = ps_next()
                            S.pe(lambda e, e_=e_, pS_=pS_: e.matmul(ps[:, pS_, 0:128], lhsT=kdm[0:TS, :], rhs=vnew[0:TS, e_, :], start=True, stop=True), reads=["kdm", ("vnew", e_)], writes=psk(pS_))
                            yield
                            S.dve(lambda e, e_=e_, s_=s_, k_=k_, j_=j_, pS_=pS_: e.scalar_tensor_tensor(out=SoutR[j_], in0=SinR[k_], scalar=E2[:, e_, 4 * s_ + 3:4 * s_ + 4], in1=ps[:, pS_, 0:128], op0=ALU.mult, op1=ALU.add), reads=psk(pS_) + [("SinR", k_), "E2"], writes=[("SoutR", j_)])
                            yield
                            if n_ + RDEPTH < NLD:
                                _ldB(n_ + RDEPTH)
                                yield
                            S.dma("sp", "gsout%d" % j_, lambda e, s_=s_, hv=hv, j_=j_: e.dma_start(out=K.gdn_s[s_, hv], in_=SoutR[j_]), reads=[("SoutR", j_)])
                            yield
                    K.P.reserved = set()
                for e_ in range(2):
                    _f = lambda e, e_=e_, C=C, po_aps=po_aps: e.activation(out=acc[0:C, e_ * 128:(e_ + 1) * 128], in_=po_aps[e_], func=AF.Square, accum_out=ss[0:C, e_:e_ + 1])
                    _f._multi = True
                    S.act(_f, reads=po_keys[e_], writes=[("acc", 0), ("ss", e_)])
                    yield
                S.act(lambda e, C=C: e.activation(out=ss[0:C, 2:4], in_=ss[0:C, 0:2], func=AF.Sqrt, bias=EPS, scale=1.0 / 128), reads=[("ss", 0), ("ss", 1)], writes=["ss2"])
                yield
                S.dve(lambda e, C=C: e.reciprocal(out=ss[0:C, 2:4], in_=ss[0:C, 2:4]), reads=["ss2"], writes=["ss2"])
                yield
                for e_ in range(2):
                    S.dve(lambda e, e_=e_, C=C, po_aps=po_aps: e.scalar_tensor_tensor(out=og[0:C, e_ * 128:(e_ + 1) * 128], in0=po_aps[e_], scalar=ss[0:C, 2 + e_:3 + e_], in1=szn2[pb][0:C, e_ * 128:(e_ + 1) * 128], op0=ALU.mult, op1=ALU.mult), reads=po_keys[e_] + ["ss2", ("szn", pb)], writes=[("og", e_)])
                    yield
                bt = ps_next()
                ptb = ps[:, bt, :].bitcast(BF16)
                for e_ in range(2):
                    S.pe(lambda e, e_=e_, ptb=ptb, C=C: e.transpose(ptb[:, e_ * C:(e_ + 1) * C], og[0:C, e_ * 128:(e_ + 1) * 128], ident_b[0:C, 0:C]), reads=[("og", e_), "ident_b"], writes=psk(bt))
                    yield
                S.act(lambda e, ptb=ptb, C=C, c0=c0: e.copy(out=ogT[:, :, c0:c0 + C], in_=ptb[:, 0:2 * C].rearrange("p (a b) -> p a b", a=2)), reads=psk(bt), writes=["ogT"])
                yield
            chunks_ = list(range(0, nt, C))
            gens = [chunk_gen(c0_, i_ % 2) for i_, c0_ in enumerate(chunks_)]
            PIPE = (not isS) and os.environ.get("GDN_PIPE", "1") == "1"

            def _adv_split(g):
                for x_ in g:
                    if x_ == "SPLIT":
                        return
            if not PIPE:
                for g in gens:
                    for _ in g:
                        pass
            else:
                _adv_split(gens[0])
                for i_ in range(len(gens)):
                    nxt = gens[i_ + 1] if i_ + 1 < len(gens) else None
                    a_done = False
                    b_done = nxt is None
                    while not (a_done and b_done):
                        if not a_done:
                            try:
                                next(gens[i_])
                            except StopIteration:
                                a_done = True
                        if not b_done:
                            try:
                                if next(nxt) == "SPLIT":
                                    b_done = True
                            except StopIteration:
                                b_done = True
            for dc8 in range(KC):
                if hk >= 1 and 'F' in CUT:
                    continue
                b = ps_next()
                for e_ in range(2):
                    S.pe(lambda e, b=b, e_=e_, dc8=dc8, nt=nt: e.matmul(ps[:, b, 0:nt], lhsT=wo[:, e_, dc8 * 128:(dc8 + 1) * 128], rhs=ogT[:, e_, 0:nt], start=(e_ == 0), stop=(e_ == 1)), reads=[wok, "ogT"], writes=psk(b))
                K.resid_update(ps[:, b, 0:nt], gbase, dc8, ("S" if isS else bi), t0, nt, psk(b))
        if hk >= 1 and 'G' in CUT:
            S.barrier(); continue
        bo1 = ps_next()
        for ci in range(4):
            S.pe(lambda e, ci=ci, bo1=bo1: e.transpose(ps[0:3, bo1, ci * 128:(ci + 1) * 128], tailP[:, ci, :], ident_f[:]), reads=["tailP", "ident_f"], writes=psk(bo1))
        S.dve(lambda e, bo1=bo1: e.tensor_copy(out=ostg[0:3, :], in_=ps[0:3, bo1, :]), reads=psk(bo1), writes=[("acc", 0)])
        for ci, cid in enumerate(cids):
            S.dma("sp", "gco", lambda e, ci=ci, cid=cid: e.dma_start(out=K.gconv_p[:, cid * 128:(cid + 1) * 128], in_=ostg[0:3, ci * 128:(ci + 1) * 128]), reads=[("acc", 0)])
        bo2 = ps_next()
        for ci in range(4):
            S.pe(lambda e, ci=ci, bo2=bo2: e.transpose(ps[0:48, bo2, ci * 128:(ci + 1) * 128], tailS[:, ci, :, :].rearrange("p s r -> p (s r)"), ident_f[:]), reads=["tailS", "ident_f"], writes=psk(bo2))
        S.dve(lambda e, bo2=bo2: e.tensor_copy(out=ostg[0:48, :], in_=ps[0:48, bo2, :]), reads=psk(bo2), writes=[("acc", 0)])
        for ci, cid in enumerate(cids):
            S.dma("sp", "gco", lambda e, ci=ci, cid=cid: e.dma_start(out=K.gconv_s[:, cid * 128:(cid + 1) * 128], in_=ostg[0:48, ci * 128:(ci + 1) * 128]), reads=[("acc", 0)])
        S.barrier()
    S.barrier()

_CACHE = {}


def make_in_maps(inp):
    f = lambda a: np.ascontiguousarray(np.asarray(a, dtype=np.float32))
    consts = host_consts()
    shared = {
        "w_ada": f(inp["w_ada"]), "b_ada": f(inp["b_ada"]),
        "w_ada_final": f(inp["w_ada_final"]), "b_ada_final": f(inp["b_ada_final"]).reshape(1, -1),
        "w_ret_in": f(inp["w_ret_in"][0]), "w_ret_out": f(inp["w_ret_out"][0]),
        "w_gdn_in": f(inp["w_gdn_in"][0]), "w_gdn_out": f(inp["w_gdn_out"][0]),
        "w_gdn_conv": f(inp["w_gdn_conv"][0]),
        "gdn_vec": f(np.concatenate([inp["gdn_a_log"][0], inp["gdn_dt_bias"][0]])).reshape(1, 32),
        "gdn_norm": f(inp["gdn_norm"]).reshape(1, 128),
        "w_ffn_up": f(inp["w_ffn_up"]), "w_ffn_down": f(inp["w_ffn_down"]),
        "ffn_vec": f(np.concatenate([np.asarray(inp["w_ffn_dw"]).reshape(6, DFF), np.asarray(inp["b_ffn_dw"]).reshape(2, DFF)], axis=0)),
    }
    shared.update(consts)
    maps = []
    for c in range(NCORES):
        sl = slice(NSEQ * c, NSEQ * (c + 1))
        m = dict(shared)
        m["xp"] = f(inp["x_prompt"][c])
        m["xs"] = f(np.asarray(inp["x_sample"][sl]).reshape(TS, D))
        m["s_ret"] = f(inp["state_ret"][0, sl])
        m["s_gdn"] = f(inp["state_gdn"][0, sl])
        m["s_gconv"] = f(np.asarray(inp["state_gdn_conv"][0, sl]).reshape(NSEQ * 3, 4096))
        m["s_fconv"] = f(np.asarray(inp["state_ffn_conv"][:, sl]).reshape(2, NSEQ * 2, DFF))
        m["vec22"] = f(np.concatenate([np.asarray(inp["c_prompt"][c:c + 1]), np.asarray(inp["c_sample"][sl]),
                                        np.asarray(inp["norm_mix"]), np.asarray(inp["norm_ffn"]), np.asarray(inp["norm_final"]).reshape(1, D)], axis=0))
        maps.append(m)
    return maps


def kernel(**inp):
    if "nc" not in _CACHE:
        _CACHE["nc"] = build_program()
    nc = _CACHE["nc"]
    maps = make_in_maps(inp)
    res = run_bass_kernel_spmd(nc, maps, core_ids=list(range(NCORES)))
    R = res.results
    cat = lambda k: np.stack([np.asarray(R[c][k]) for c in range(NCORES)], axis=0)
    y_prompt = cat("y_p")
    y_sample = cat("y_s").reshape(128, 4, D)
    ret_p = cat("ret_p")[None]
    gdn_p = cat("gdn_p")[None]
    gconv_p = cat("gconv_p")[None]
    fconv_p = np.transpose(cat("fconv_p"), (1, 0, 2, 3))
    ret_s = cat("ret_s").reshape(1, 128, RET_H, 256, 512)
    gdn_s = cat("gdn_s").reshape(1, 128, GDN_HV, 128, 128)
    gconv_s = cat("gconv_s").reshape(1, 128, 3, 4096)
    fconv_s = np.transpose(cat("fconv_s").reshape(NCORES, 2, NSEQ, 2, DFF), (1, 0, 2, 3, 4)).reshape(2, 128, 2, DFF)
    return (y_prompt.astype(np.float32), y_sample.astype(np.float32), ret_p.astype(np.float32), gdn_p.astype(np.float32),
            gconv_p.astype(np.float32), fconv_p.astype(np.float32), ret_s.astype(np.float32), gdn_s.astype(np.float32),
            gconv_s.astype(np.float32), fconv_s.astype(np.float32))
```

```python
import math
import bisect
import numpy as np
from contextlib import ExitStack
import concourse.bass as bass
import concourse.mybir as mybir
from concourse.bass_utils import run_bass_kernel_spmd

F32 = mybir.dt.float32
BF16 = mybir.dt.bfloat16
AF = mybir.ActivationFunctionType
ALU = mybir.AluOpType

NCORES = 8
D = 1024
KC = 8
TP = 2048
NSEQ = 16
TS = 64
TT = TP + TS
DFF = 2816
NF = 22
EPS = 1e-6
RET_H = 4
GDN_HV = 16
GDN_HK = 8
PAST = 16384
SAME_ENGINE_SYNC = True
ATTACH_WAITS = True
DEBUG = {}


class _Op:
    __slots__ = ("eng", "fn", "deps", "dma_key", "sig", "cnt", "idx")

    def __init__(self, eng, fn, deps, dma_key, idx):
        self.eng = eng
        self.fn = fn
        self.deps = deps
        self.dma_key = dma_key
        self.sig = False
        self.cnt = 0
        self.idx = idx


class Sched:
    ENGS = ("pe", "act", "dve", "pool", "sp")

    def __init__(self, nc):
        self.nc = nc
        self.ops = []
        self.last_w = {}
        self.readers = {}
        self.last_eng = {}
        self.dmas_since_bar = []

    def op(self, eng, fn, reads=(), writes=(), dma_key=None, extra=()):
        deps = set(extra)
        for r in reads:
            w = self.last_w.get(r)
            if w is not None:
                deps.add(w)
        for w_ in writes:
            w = self.last_w.get(w_)
            if w is not None:
                deps.add(w)
            deps |= self.readers.get(w_, set())
        idx = len(self.ops)
        deps.discard(idx)
        self.ops.append(_Op(eng, fn, deps, dma_key, idx))
        for r in reads:
            self.readers.setdefault(r, set()).add(idx)
        for w_ in writes:
            self.last_w[w_] = idx
            self.readers[w_] = set()
        self.last_eng[eng] = idx
        if dma_key is not None:
            self.dmas_since_bar.append(idx)
        return idx

    def pe(self, fn, reads=(), writes=()):
        return self.op("pe", fn, reads, writes)

    def act(self, fn, reads=(), writes=()):
        return self.op("act", fn, reads, writes)

    def dve(self, fn, reads=(), writes=()):
        return self.op("dve", fn, reads, writes)

    def pool(self, fn, reads=(), writes=()):
        return self.op("pool", fn, reads, writes)

    def dma(self, q, key, fn, reads=(), writes=()):
        return self.op(q, fn, reads, writes, dma_key=key)

    def barrier(self):
        deps = set(self.last_eng.values()) | set(self.dmas_since_bar)
        self.dmas_since_bar = []
        for e in self.ENGS:
            self.op(e, None, extra=deps)
        self.last_w = {}
        self.readers = {}

    def emit(self, stack):
        nc = self.nc
        ops = self.ops

        def needs(c, p):
            if p.fn is None:
                return False
            if p.dma_key is not None:
                return True
            if p.eng == c.eng:
                if p.eng == "pe":
                    return False
                return SAME_ENGINE_SYNC
            return True

        for c in ops:
            for d in c.deps:
                p = ops[d]
                if needs(c, p):
                    p.sig = True
        eng_cnt = {e: 0 for e in self.ENGS}
        dma_cnt = {}
        dma_keys = []
        dma_issue_idx = {}
        for o in ops:
            if o.dma_key is not None:
                if o.dma_key not in dma_cnt:
                    dma_cnt[o.dma_key] = 0
                    dma_keys.append(o.dma_key)
                    dma_issue_idx[o.dma_key] = []
                dma_cnt[o.dma_key] += 1
                dma_issue_idx[o.dma_key].append(o.idx)
            elif o.sig:
                eng_cnt[o.eng] += 1
                o.cnt = eng_cnt[o.eng]
        sems = {}
        for e in self.ENGS:
            sems[("e", e)] = stack.enter_context(nc.semaphore("s_" + e))
        for k in dma_keys:
            sems[("d", k)] = stack.enter_context(nc.semaphore("d_" + str(k)))
        per_eng = {e: [o for o in ops if o.eng == e] for e in self.ENGS}
        block = stack.enter_context(nc.Block())

        def run_engine(ename, eobj):
            waited = {}
            for o in per_eng[ename]:
                need = {}
                for d in o.deps:
                    p = ops[d]
                    if not needs(o, p):
                        continue
                    if p.dma_key is not None:
                        key = ("d", p.dma_key)
                        val = 16 * bisect.bisect_left(dma_issue_idx[p.dma_key], o.idx)
                    else:
                        key = ("e", p.eng)
                        val = p.cnt
                    if need.get(key, 0) < val:
                        need[key] = val
                pend = [(key, val) for key, val in need.items() if waited.get(key, 0) < val]
                for key, val in pend:
                    waited[key] = val
                fuse = (ATTACH_WAITS and o.fn is not None and o.dma_key is None and ename in ("act", "dve", "pool", "pe")
                        and not getattr(o.fn, "_multi", False) and len(pend) > 0)
                for key, val in (pend[:-1] if fuse else pend):
                    eobj.wait_ge(sems[key], val)
                if o.fn is None:
                    continue
                ins = o.fn(eobj)
                if fuse:
                    ins._wait_ge(sems[pend[-1][0]], pend[-1][1])
                if o.dma_key is not None:
                    ins.then_inc(sems[("d", o.dma_key)], 16)
                elif o.sig:
                    ins.then_inc(sems[("e", ename)], 1)
            for k in dma_keys:
                if any(o.dma_key == k for o in per_eng[ename]):
                    eobj.wait_ge(sems[("d", k)], 16 * dma_cnt[k])

        @block.tensor
        def _(e):
            run_engine("pe", e)

        @block.scalar
        def _(e):
            run_engine("act", e)

        @block.vector
        def _(e):
            run_engine("dve", e)

        @block.gpsimd
        def _(e):
            run_engine("pool", e)

        @block.sync
        def _(e):
            run_engine("sp", e)


def _gammas():
    return (1.0 - 2.0 ** (-5.0 - np.arange(RET_H, dtype=np.float64)))


def host_consts():
    c = {}
    c["c_ident"] = np.eye(128, dtype=np.float32)
    half = 128
    inv_freq = (np.float32(10000.0) ** (-(np.arange(half, dtype=np.float32)) / np.float32(half))).astype(np.float32)
    pos = np.concatenate([np.arange(TP, dtype=np.float32), (PAST + (np.arange(TS) % 4)).astype(np.float32)])
    ang = (pos[None, :] * inv_freq[:, None]).astype(np.float32)
    cos = np.cos(ang.astype(np.float64))
    sin = np.sin(ang.astype(np.float64))
    g = _gammas()
    pin = np.concatenate([np.arange(TP) % 128, np.arange(TS) % 4]).astype(np.float64)
    rope = np.zeros((RET_H + 1, 2, 128, TT), np.float32)
    for h in range(RET_H):
        dec = g[h] ** (pin + 1.0)
        rope[h, 0] = cos * dec[None, :]
        rope[h, 1] = sin * dec[None, :]
    rope[RET_H, 0] = cos * (256.0 ** -0.5)
    rope[RET_H, 1] = sin * (256.0 ** -0.5)
    c["rope"] = rope
    mP = np.zeros((RET_H, 128, 128), np.float32)
    mS = np.zeros((RET_H, 128, 128), np.float32)
    ks = np.zeros((128, 2 * RET_H), np.float32)
    jj = np.arange(128)
    for h in range(RET_H):
        mP[h] = np.where(jj[None, :] >= jj[:, None], g[h] ** (-(jj[:, None] + 1.0)), 0.0)
        j4 = jj[:64] % 4
        same = (jj[:64, None] // 4) == (jj[None, :64] // 4)
        mS[h, :64, :64] = np.where(same & (jj[None, :64] >= jj[:64, None]), g[h] ** (-(j4[:, None] + 1.0)), 0.0)
        ks[:, h] = g[h] ** (127.0 - jj)
        ks[:64, 4 + h] = g[h] ** (3.0 - j4)
    c["retmask"] = np.concatenate([mP, mS], axis=0)
    c["kscale"] = ks
    seg = np.zeros((128, NSEQ, 64), np.float32)
    for s in range(NSEQ):
        seg[:, s, 4 * s:4 * s + 4] = 1.0
    c["segmask"] = seg.reshape(128, NSEQ * 64)
    segcol = np.zeros((128, NSEQ), np.float32)
    for s in range(NSEQ):
        segcol[4 * s:4 * s + 4, s] = 1.0
    c["segcol"] = segcol
    gm = np.zeros((8, 128, 128), np.float32)
    a = np.arange(128)
    gm[0] = (a[:, None] <= a[None, :])
    gm[1] = (a[:, None] > a[None, :])
    gm[2] = (a[None, :] >= a[:, None])
    gm[3] = (a[None, :] > a[:, None])
    b = np.arange(64)
    same = (b[:, None] // 4) == (b[None, :] // 4)
    gm[4, :64, :64] = (b[:, None] <= b[None, :]) & same
    gm[5, :64, :64] = (b[:, None] > b[None, :]) & same
    gm[6, :64, :64] = (b[None, :] >= b[:, None]) & same
    gm[7, :64, :64] = (b[None, :] > b[:, None]) & same
    c["gmask"] = np.ascontiguousarray(gm.transpose(1, 0, 2)).reshape(128, 8 * 128)
    bmk = np.zeros((4, 128, 128), np.float32)
    bmk[0] = (a[:, None] // 16) == (a[None, :] // 16)
    for mi, m in enumerate([32, 64, 128]):
        bmk[mi + 1] = ((a[:, None] // m) == (a[None, :] // m)) & ((a[:, None] // (m // 2)) != (a[None, :] // (m // 2)))
    c["bmask"] = np.ascontiguousarray(bmk.transpose(1, 0, 2)).reshape(128, 4 * 128)
    return c


class Ctx:
    pass


def build_program(dbg=()):
    nc = bass.Bass("TRN2", target_bir_lowering=False)
    K = Ctx()
    K.nc = nc
    ins = {}

    def din(name, shape):
        ins[name] = nc.dram_tensor(name, list(shape), F32, kind="ExternalInput").ap()
        return ins[name]

    def dout(name, shape):
        return nc.dram_tensor(name, list(shape), F32, kind="ExternalOutput").ap()

    xp = din("xp", [TP, D]); xs = din("xs", [TS, D])
    s_ret = din("s_ret", [NSEQ, RET_H, 256, 512])
    s_gdn = din("s_gdn", [NSEQ, GDN_HV, 128, 128])
    s_gconv = din("s_gconv", [NSEQ * 3, 4096])
    s_fconv = din("s_fconv", [2, NSEQ * 2, DFF])
    vec22 = din("vec22", [22, D])
    w_ada = din("w_ada", [2, D, 6 * D]); b_ada = din("b_ada", [2, 6 * D])
    w_adaf = din("w_ada_final", [D, 2 * D]); b_adaf = din("b_ada_final", [1, 2 * D])
    w_ret_in = din("w_ret_in", [D, 6144]); w_ret_out = din("w_ret_out", [2048, D])
    w_gdn_in = din("w_gdn_in", [D, 6176]); w_gdn_out = din("w_gdn_out", [2048, D])
    w_gconv = din("w_gdn_conv", [4, 4096])
    gdn_vec = din("gdn_vec", [1, 32])
    gdn_norm = din("gdn_norm", [1, 128])
    w_up = din("w_ffn_up", [2, D, 2 * DFF]); w_down = din("w_ffn_down", [2, DFF, D])
    ffn_vec = din("ffn_vec", [8, DFF])
    c_ident = din("c_ident", [128, 128])
    c_rope = din("rope", [RET_H + 1, 2, 128, TT])
    c_retmask = din("retmask", [2 * RET_H, 128, 128])
    c_kscale = din("kscale", [128, 2 * RET_H])
    c_segmask = din("segmask", [128, NSEQ * 64])
    c_segcol = din("segcol", [128, NSEQ])
    c_gmask = din("gmask", [128, 8 * 128])
    c_bmask = din("bmask", [128, 4 * 128])

    y_p = dout("y_p", [TP, D]); y_s = dout("y_s", [TS, D])
    ret_p = dout("ret_p", [RET_H, 256, 512]); gdn_p = dout("gdn_p", [GDN_HV, 128, 128])
    gconv_p = dout("gconv_p", [3, 4096]); fconv_p = dout("fconv_p", [2, 2, DFF])
    ret_s = dout("ret_s", [NSEQ, RET_H, 256, 512]); gdn_s = dout("gdn_s", [NSEQ, GDN_HV, 128, 128])
    gconv_s = dout("gconv_s", [NSEQ * 3, 4096]); fconv_s = dout("fconv_s", [2, NSEQ * 2, DFF])
    dbg_out = {n: dout("dbg_" + n, shp) for n, shp in dbg}

    with ExitStack() as st:
        def T(name, shape, dt):
            return st.enter_context(nc.sbuf_tensor(name, list(shape), dt))

        S = Sched(nc)
        xT = T("xT", [128, KC, TT], F32)
        hT = T("hT", [128, KC, TT], BF16)
        wring = T("wring", [128, 4, 4096], BF16)
        modT = T("modT", [128, 112, 17], F32)
        vecT = T("vecT", [128, KC, 22], F32)
        Amod = T("Amod", [128, 5, KC, 17], F32)
        csT = T("csT", [128, KC, 17], F32)
        ident_f = T("ident_f", [128, 128], F32)
        ident_b = T("ident_b", [128, 128], BF16)
        ones_b = T("ones_b", [128, 128], BF16)
        ones_f = T("ones_f", [128, 32], F32)
        ffnvT = T("ffnvT", [128, NF, 8], F32)
        fcarry = T("fcarry", [128, NF, 2], F32)
        ARENA = 15000
        arena = T("arena", [128, ARENA], F32)
        ps = st.enter_context(nc.psum_tensor("ps", [128, 8, 512], F32))

        A = Ctx()
        A.off = 0

        def a_reset():
            A.off = 0

        def a_f32(*free, parts=128):
            n = int(np.prod(free))
            assert A.off + n <= ARENA, ("arena overflow", A.off, n)
            ap = arena[0:parts, A.off:A.off + n]
            A.off += n
            if len(free) == 2:
                ap = ap.rearrange("p (a b) -> p a b", a=free[0])
            elif len(free) == 3:
                ap = ap.rearrange("p (a b c) -> p a b c", a=free[0], b=free[1])
            return ap

        def a_bf16(*free, parts=128):
            n = int(np.prod(free))
            nf = (n + 1) // 2
            assert A.off + nf <= ARENA, ("arena overflow", A.off, nf)
            ap = arena[0:parts, A.off:A.off + nf].bitcast(BF16)[:, 0:n]
            A.off += nf
            if len(free) == 2:
                ap = ap.rearrange("p (a b) -> p a b", a=free[0])
            elif len(free) == 3:
                ap = ap.rearrange("p (a b c) -> p a b c", a=free[0], b=free[1])
            return ap

        P = Ctx()
        P.i = 0

        P.reserved = set()

        def ps_next(n=1):
            while True:
                if P.i + n > 8:
                    P.i = 0
                b = P.i
                P.i = (P.i + n) % 8
                if not any((b + i) in P.reserved for i in range(n)):
                    return b

        def psk(b, n=1):
            return [("ps", b + i) for i in range(n)]

        W = Ctx()
        W.i = 0

        def wslot():
            i = W.i
            W.i = (W.i + 1) % 4
            return i

        def load_w_bf16(src2d, kc, ncols, row0=0, col0=0, slot=None, off=0, key=None):
            i = wslot() if slot is None else slot
            view = wring[:, i, off:off + kc * ncols].rearrange("p (k c) -> p k c", k=kc)
            src = src2d[row0:row0 + kc * 128, col0:col0 + ncols].rearrange("(k p) c -> p k c", p=128)
            k_ = ("w", i) if key is None else key
            S.dma("pool", "w%d" % i, lambda e: e.dma_start(out=view, in_=src), writes=[k_])
            return view, k_

        def load_w_f32(src2d, kc, ncols, row0=0, col0=0):
            i = wslot()
            view = wring[:, i, :].bitcast(F32)[:, 0:kc * ncols].rearrange("p (k c) -> p k c", k=kc)
            src = src2d[row0:row0 + kc * 128, col0:col0 + ncols].rearrange("(k p) c -> p k c", p=128)
            S.dma("sp", "wf%d" % i, lambda e: e.dma_start(out=view, in_=src), writes=[("w", i)])
            return view, ("w", i)

        BLOCKS_P = [(0, 512), (512, 512), (1024, 512), (1536, 512)]
        BLOCK_S = (TP, TS)
        ALLBLOCKS = BLOCKS_P + [BLOCK_S]

        S.dma("sp", "c_id", lambda e: e.dma_start(out=ident_f[:], in_=c_ident), writes=["ident_f"])
        S.dve(lambda e: e.tensor_copy(out=ident_b[:], in_=ident_f[:]), reads=["ident_f"], writes=["ident_b"])
        S.dve(lambda e: e.memset(ones_b[:], 1.0), writes=["ones_b"])
        S.dve(lambda e: e.memset(ones_f[:], 1.0), writes=["ones_f"])
        a_reset()
        stage22 = a_f32(D, parts=22)
        S.dma("sp", "c_s22", lambda e: e.dma_start(out=stage22, in_=vec22), writes=["stage22"])
        b0 = ps_next()
        for kc in range(KC):
            S.pe(lambda e, kc=kc: e.transpose(ps[:, b0, kc * 22:(kc + 1) * 22], stage22[:, kc * 128:(kc + 1) * 128], ident_f[0:22, 0:22]),
                 reads=["stage22", "ident_f"], writes=psk(b0))
        S.dve(lambda e: e.tensor_copy(out=vecT[:].rearrange("p a b -> p (a b)"), in_=ps[:, b0, 0:KC * 22]), reads=psk(b0), writes=["vecT"])
        S.act(lambda e: e.activation(out=csT[:], in_=vecT[:, :, 0:17], func=AF.Silu), reads=["vecT"], writes=["csT"])
        stage8 = a_f32(DFF, parts=8)
        S.dma("sp", "c_s8", lambda e: e.dma_start(out=stage8, in_=ins["ffn_vec"]), writes=["stage8"])
        b1 = ps_next()
        for fc in range(NF):
            S.pe(lambda e, fc=fc: e.transpose(ps[:, b1, fc * 8:(fc + 1) * 8], stage8[:, fc * 128:(fc + 1) * 128], ident_f[0:8, 0:8]),
                 reads=["stage8", "ident_f"], writes=psk(b1))
        S.dve(lambda e: e.tensor_copy(out=ffnvT[:].rearrange("p a b -> p (a b)"), in_=ps[:, b1, 0:NF * 8]), reads=psk(b1), writes=["ffnvT"])

        browb = [a_bf16(512, parts=1) for _ in range(3)]
        mstage = [a_f32(512, parts=17) for _ in range(2)]
        csb = a_bf16(KC, 17)
        S.dve(lambda e: e.tensor_copy(out=csb, in_=csT[:]), reads=["csT"], writes=["csb"])
        bri = [0]

        def ada_layer(wsrc, bsrc, ncols, mod_base):
            for pcs in range(ncols // 512):
                wv, wk = load_w_bf16(wsrc, KC, 512, col0=pcs * 512)
                bi = bri[0] % 3
                m2 = bri[0] % 2
                bri[0] += 1
                S.dma("pool", "brb%d" % bi, lambda e, bi=bi, pcs=pcs: e.dma_start(out=browb[bi], in_=bsrc[0:1, pcs * 512:(pcs + 1) * 512]), writes=[("browb", bi)])
                bank = ps_next()
                for kc in range(KC):
                    S.pe(lambda e, bank=bank, kc=kc, wv=wv: e.matmul(ps[0:17, bank, :], lhsT=csb[:, kc, :], rhs=wv[:, kc, :], start=(kc == 0), stop=False), reads=[wk, "csb"], writes=psk(bank))
                S.pe(lambda e, bank=bank, bi=bi: e.matmul(ps[0:17, bank, :], lhsT=ones_b[0:1, 0:17], rhs=browb[bi][0:1, :], start=False, stop=True), reads=[("browb", bi), "ones_b"], writes=psk(bank))
                S.act(lambda e, bank=bank, m2=m2: e.copy(out=mstage[m2], in_=ps[0:17, bank, :]), reads=psk(bank), writes=[("mstage", m2)])
                bank2 = ps_next()
                for q in range(4):
                    S.pe(lambda e, bank2=bank2, q=q, m2=m2: e.transpose(ps[:, bank2, q * 17:(q + 1) * 17], mstage[m2][:, q * 128:(q + 1) * 128], ident_f[0:17, 0:17]), reads=[("mstage", m2), "ident_f"], writes=psk(bank2))
                m0 = mod_base + 4 * pcs
                S.dve(lambda e, bank2=bank2, m0=m0: e.tensor_copy(out=modT[:, m0:m0 + 4, :].rearrange("p a b -> p (a b)"), in_=ps[:, bank2, 0:68]), reads=psk(bank2), writes=["modT"])

        ada_layer(w_ada[0], b_ada[0:1, :], 6 * D, 0)
        ada_layer(w_ada[1], b_ada[1:2, :], 6 * D, 48)
        ada_layer(w_adaf, b_adaf, 2 * D, 96)
        norm_specs = [(0 * 48 + 8, 17), (0 * 48 + 32, 19), (1 * 48 + 8, 18), (1 * 48 + 32, 20), (96 + 8, 21)]
        for n, (scb, col) in enumerate(norm_specs):
            for kc in range(KC):
                S.dve(lambda e, n=n, kc=kc, scb=scb, col=col: e.tensor_scalar(out=Amod[:, n, kc, :], in0=modT[:, scb + kc, :], scalar1=1.0, scalar2=vecT[:, kc, col:col + 1], op0=ALU.add, op1=ALU.mult),
                      reads=["modT", "vecT"], writes=["Amod"])
        norm_shift = [0, 24, 48, 72, 96]
        S.barrier()

        a_reset()
        xst = [a_f32(D) for _ in range(2)]
        tiles = [(xp, t * 128, 128, t * 128) for t in range(16)] + [(xs, 0, TS, TP)]
        for ti, (src, r0, rows, t0) in enumerate(tiles):
            sb = ti % 2
            S.dma("sp", "xst%d" % sb, lambda e, sb=sb, src=src, r0=r0, rows=rows: e.dma_start(out=xst[sb][0:rows, :], in_=src[r0:r0 + rows, :]), writes=[("xst", sb)])
            for half in range(2):
                b = ps_next()
                for q in range(4):
                    kc = half * 4 + q
                    S.pe(lambda e, b=b, q=q, kc=kc, sb=sb, rows=rows: e.transpose(ps[:, b, q * 128:q * 128 + rows], xst[sb][0:rows, kc * 128:(kc + 1) * 128], ident_f[0:rows, 0:rows]),
                         reads=[("xst", sb), "ident_f"], writes=psk(b))
                src_ap = ps[:, b, :].rearrange("p (q c) -> p q c", q=4)[:, :, 0:rows]
                dst_ap = xT[:, half * 4:half * 4 + 4, t0:t0 + rows]
                if half == 0:
                    S.act(lambda e, dst_ap=dst_ap, src_ap=src_ap: e.copy(out=dst_ap, in_=src_ap), reads=psk(b), writes=[("xT", ti)])
                else:
                    S.dve(lambda e, dst_ap=dst_ap, src_ap=src_ap: e.tensor_copy(out=dst_ap, in_=src_ap), reads=psk(b), writes=[("xT", ti)])
        S.barrier()

        def do_norm(n, out_hT=True, out_f32=None):
            a_reset()
            sq = [a_bf16(KC, 512) for _ in range(2)]
            rstd = [a_f32(512) for _ in range(2)]
            tmp = [a_f32(KC, 512) for _ in range(2)]
            shb = norm_shift[n]
            for bi, (t0, nt) in enumerate(ALLBLOCKS):
                i2 = bi % 2
                S.act(lambda e, i2=i2, t0=t0, nt=nt: e.activation(out=sq[i2][:, :, 0:nt], in_=xT[:, :, t0:t0 + nt], func=AF.Square),
                      reads=[], writes=[("sq", i2)])
                b = ps_next()
                for kc in range(KC):
                    S.pe(lambda e, b=b, kc=kc, i2=i2, nt=nt: e.matmul(ps[:, b, 0:nt], lhsT=ones_b[:], rhs=sq[i2][:, kc, 0:nt], start=(kc == 0), stop=(kc == KC - 1)),
                         reads=[("sq", i2), "ones_b"], writes=psk(b))
                S.act(lambda e, b=b, i2=i2, nt=nt: e.activation(out=rstd[i2][:, 0:nt], in_=ps[:, b, 0:nt], func=AF.Sqrt, bias=EPS, scale=1.0 / D),
                      reads=psk(b), writes=[("rstd", i2)])
                S.dve(lambda e, i2=i2, nt=nt: e.reciprocal(out=rstd[i2][:, 0:nt], in_=rstd[i2][:, 0:nt]), reads=[("rstd", i2)], writes=[("rstd", i2)])
                S.dve(lambda e, i2=i2, t0=t0, nt=nt: e.tensor_tensor(out=tmp[i2][:, :, 0:nt], in0=xT[:, :, t0:t0 + nt], in1=rstd[i2][:, 0:nt].unsqueeze(1).to_broadcast([128, KC, nt]), op=ALU.mult),
                      reads=[("rstd", i2)], writes=[("tmp", i2)])
                for kc in range(KC):
                    dst = hT[:, kc, t0:t0 + nt] if out_f32 is None else out_f32(bi, kc)
                    if t0 < TP:
                        S.act(lambda e, dst=dst, i2=i2, kc=kc, nt=nt: e.activation(out=dst, in_=tmp[i2][:, kc, 0:nt], func=AF.Identity, scale=Amod[:, n, kc, 0:1], bias=modT[:, shb + kc, 0:1]),
                              reads=[("tmp", i2)], writes=[("h", bi, kc)])
                    else:
                        S.dve(lambda e, i2=i2, kc=kc: e.tensor_tensor(out=tmp[i2][:, kc, 0:TS].rearrange("p (s j) -> p s j", j=4), in0=tmp[i2][:, kc, 0:TS].rearrange("p (s j) -> p s j", j=4),
                                                                     in1=Amod[:, n, kc, 1:17].unsqueeze(2).to_broadcast([128, NSEQ, 4]), op=ALU.mult),
                              reads=[("tmp", i2)], writes=[("tmp", i2)])
                        S.dve(lambda e, dst=dst, i2=i2, kc=kc: e.tensor_tensor(out=dst.rearrange("p (s j) -> p s j", j=4), in0=tmp[i2][:, kc, 0:TS].rearrange("p (s j) -> p s j", j=4),
                                                                              in1=modT[:, shb + kc, 1:17].unsqueeze(2).to_broadcast([128, NSEQ, 4]), op=ALU.add),
                              reads=[("tmp", i2)], writes=[("h", bi, kc)])
            S.barrier()

        def gate_ap(gbase, kc, bi, nt):
            if bi != "S":
                return modT[:, gbase + kc, 0:1].to_broadcast([128, nt])
            return modT[:, gbase + kc, 1:17].unsqueeze(2).to_broadcast([128, NSEQ, 4])

        def resid_update(psrc, gbase, kc, bi, t0, nt, reads):
            if t0 < TP:
                S.dve(lambda e: e.scalar_tensor_tensor(out=xT[:, kc, t0:t0 + nt], in0=psrc, scalar=modT[:, gbase + kc, 0:1], in1=xT[:, kc, t0:t0 + nt], op0=ALU.mult, op1=ALU.add),
                      reads=list(reads) + [("x", kc, t0)], writes=[("x", kc, t0)])
            else:
                tmpg = K.tmpg
                S.dve(lambda e: e.tensor_tensor(out=tmpg.rearrange("p (s j) -> p s j", j=4), in0=psrc.rearrange("p (s j) -> p s j", j=4), in1=gate_ap(gbase, kc, "S", nt), op=ALU.mult),
                      reads=list(reads), writes=["tmpg"])
                S.dve(lambda e: e.tensor_tensor(out=xT[:, kc, t0:t0 + nt], in0=xT[:, kc, t0:t0 + nt], in1=tmpg, op=ALU.add),
                      reads=["tmpg", ("x", kc, t0)], writes=[("x", kc, t0)])

        def do_ffn(l):
            a_reset()
            gbase = l * 48 + 40
            K.tmpg = a_f32(TS)
            actb = a_bf16(NF, 704)
            gx = [a_f32(2 + 512) for _ in range(2)]
            gxs = a_f32(NSEQ, 6)
            cv = [a_f32(512) for _ in range(2)]
            tailP = a_f32(NF, 2)
            tailS = a_f32(NF, NSEQ, 2)
            sstT = a_f32(NF, 32)
            sstg = [a_f32(512, parts=32) for _ in range(2)]
            ostg = [a_f32(512, parts=32) for _ in range(2)]
            ostgP = [a_f32(512, parts=2) for _ in range(2)]
            for gi, g4 in enumerate(range(0, NF, 4)):
                b = ps_next()
                n = min(4, NF - g4)
                s2 = gi % 2
                S.dma("sp", "fst%d" % s2, lambda e, s2=s2, g4=g4, n=n: e.dma_start(out=sstg[s2][:, 0:n * 128], in_=s_fconv[l][:, g4 * 128:(g4 + n) * 128]), writes=[("sstg", s2)])
                for q in range(n):
                    S.pe(lambda e, b=b, q=q, s2=s2: e.transpose(ps[:, b, q * 32:(q + 1) * 32], sstg[s2][:, q * 128:(q + 1) * 128], ident_f[0:32, 0:32]),
                         reads=[("sstg", s2), "ident_f"], writes=psk(b))
                S.dve(lambda e, b=b, n=n, g4=g4: e.tensor_copy(out=sstT[:, g4:g4 + n, :].rearrange("p a b -> p (a b)"), in_=ps[:, b, 0:n * 32]), reads=psk(b), writes=["sstT"])
            S.dve(lambda e: e.memset(fcarry[:], 0.0), writes=[("fcarry", fc_) for fc_ in range(NF)])
            wcol = l * 3
            passes = [[(0, 0, 352, 0), (1, 352, 352, 352)], [(2, 704, 352, 0), (3, 1056, 352, 352)], [(4, 1408, 320, 0), (5, 1728, 320, 320), (6, TP, TS, 640)]]
            for pi, blks in enumerate(passes):
                for f0 in range(0, NF, 4):
                    nf = min(4, NF - f0)
                    wg, wgk = load_w_bf16(w_up[l], KC, nf * 128, col0=f0 * 128)
                    wv, wvk = load_w_bf16(w_up[l], KC, nf * 128, col0=DFF + f0 * 128)
                    for q in range(nf):
                        fc = f0 + q
                        for (bi, t0, nt, a0) in blks:
                            bg = ps_next()
                            for kc in range(KC):
                                S.pe(lambda e, bg=bg, kc=kc, q=q, t0=t0, nt=nt, wg=wg: e.matmul(ps[:, bg, 0:nt], lhsT=wg[:, kc, q * 128:(q + 1) * 128], rhs=hT[:, kc, t0:t0 + nt], start=(kc == 0), stop=(kc == KC - 1)),
                                     reads=[wgk], writes=psk(bg))
                            bv = ps_next()
                            for kc in range(KC):
                                S.pe(lambda e, bv=bv, kc=kc, q=q, t0=t0, nt=nt, wv=wv: e.matmul(ps[:, bv, 0:nt], lhsT=wv[:, kc, q * 128:(q + 1) * 128], rhs=hT[:, kc, t0:t0 + nt], start=(kc == 0), stop=(kc == KC - 1)),
                                     reads=[wvk], writes=psk(bv))
                            w0 = ffnvT[:, fc, wcol + 0:wcol + 1]
                            w1 = ffnvT[:, fc, wcol + 1:wcol + 2]
                            w2 = ffnvT[:, fc, wcol + 2:wcol + 3]
                            bb = ffnvT[:, fc, 6 + l:7 + l]
                            if bi < 6:
                                i2 = bi % 2
                                G = gx[i2]
                                c_ = cv[i2]
                                S.dve(lambda e, G=G, fc=fc: e.tensor_copy(out=G[:, 0:2], in_=fcarry[:, fc, :]), reads=[("fcarry", fc)], writes=[("gx", i2)])
                                S.act(lambda e, G=G, bg=bg, nt=nt: e.copy(out=G[:, 2:2 + nt], in_=ps[:, bg, 0:nt]), reads=psk(bg), writes=[("gx", i2)])
                                S.dve(lambda e, G=G, fc=fc, nt=nt: e.tensor_copy(out=fcarry[:, fc, :], in_=G[:, nt:nt + 2]), reads=[("gx", i2)], writes=[("fcarry", fc)])
                                if bi == 5:
                                    S.dve(lambda e, G=G, fc=fc, nt=nt: e.tensor_copy(out=tailP[:, fc, :], in_=G[:, nt:nt + 2]), reads=[("gx", i2)], writes=["tailP"])
                                S.act(lambda e, G=G, c_=c_, nt=nt, w2=w2: e.activation(out=c_[:, 0:nt], in_=G[:, 2:2 + nt], func=AF.Identity, scale=w2), reads=[("gx", i2), "ffnvT"], writes=[("cv", i2)])
                                S.dve(lambda e, G=G, c_=c_, nt=nt, w1=w1: e.scalar_tensor_tensor(out=c_[:, 0:nt], in0=G[:, 1:1 + nt], scalar=w1, in1=c_[:, 0:nt], op0=ALU.mult, op1=ALU.add), reads=[("gx", i2), ("cv", i2)], writes=[("cv", i2)])
                                S.dve(lambda e, G=G, c_=c_, nt=nt, w0=w0: e.scalar_tensor_tensor(out=c_[:, 0:nt], in0=G[:, 0:nt], scalar=w0, in1=c_[:, 0:nt], op0=ALU.mult, op1=ALU.add), reads=[("gx", i2), ("cv", i2)], writes=[("cv", i2)])
                                S.act(lambda e, c_=c_, nt=nt, bb=bb: e.activation(out=c_[:, 0:nt], in_=c_[:, 0:nt], func=AF.Silu, bias=bb), reads=[("cv", i2)], writes=[("cv", i2)])
                                S.dve(lambda e, c_=c_, nt=nt, bv=bv, fc=fc, a0=a0: e.tensor_tensor(out=actb[:, fc, a0:a0 + nt], in0=c_[:, 0:nt], in1=ps[:, bv, 0:nt], op=ALU.mult), reads=[("cv", i2)] + psk(bv), writes=[("act", fc, bi)])
                            else:
                                c_ = cv[0][:, 0:TS].rearrange("p (s j) -> p s j", j=4)
                                S.dve(lambda e, fc=fc: e.tensor_copy(out=gxs[:, :, 0:2], in_=sstT[:, fc, :].rearrange("p (s r) -> p s r", r=2)), reads=["sstT"], writes=["gxs"])
                                S.act(lambda e, bg=bg: e.copy(out=gxs[:, :, 2:6], in_=ps[:, bg, 0:TS].rearrange("p (s j) -> p s j", j=4)), reads=psk(bg), writes=["gxs"])
                                S.dve(lambda e, fc=fc: e.tensor_copy(out=tailS[:, fc, :, :], in_=gxs[:, :, 4:6]), reads=["gxs"], writes=["tailS"])
                                S.act(lambda e, c_=c_, w2=w2: e.activation(out=c_, in_=gxs[:, :, 2:6], func=AF.Identity, scale=w2), reads=["gxs", "ffnvT"], writes=[("cv", 0)])
                                S.dve(lambda e, c_=c_, w1=w1: e.scalar_tensor_tensor(out=c_, in0=gxs[:, :, 1:5], scalar=w1, in1=c_, op0=ALU.mult, op1=ALU.add), reads=["gxs", ("cv", 0)], writes=[("cv", 0)])
                                S.dve(lambda e, c_=c_, w0=w0: e.scalar_tensor_tensor(out=c_, in0=gxs[:, :, 0:4], scalar=w0, in1=c_, op0=ALU.mult, op1=ALU.add), reads=["gxs", ("cv", 0)], writes=[("cv", 0)])
                                S.act(lambda e, bb=bb: e.activation(out=cv[0][:, 0:TS], in_=cv[0][:, 0:TS], func=AF.Silu, bias=bb), reads=[("cv", 0)], writes=[("cv", 0)])
                                S.dve(lambda e, bv=bv, fc=fc, a0=a0: e.tensor_tensor(out=actb[:, fc, a0:a0 + TS], in0=cv[0][:, 0:TS], in1=ps[:, bv, 0:TS], op=ALU.mult), reads=[("cv", 0)] + psk(bv), writes=[("act", fc, bi)])
                for dh in range(2):
                    banks = {}
                    for dc in range(4):
                        for (bi, t0, nt, a0) in blks:
                            if len(blks) * 4 > 8 and bi == 6:
                                continue
                            banks[(dc, bi)] = ps_next()
                    for f0 in range(0, NF, 4):
                        nf = min(4, NF - f0)
                        wd, wdk = load_w_bf16(w_down[l], nf, 512, row0=f0 * 128, col0=dh * 512)
                        for (dc, bi), b in banks.items():
                            t0, nt, a0 = [(x[1], x[2], x[3]) for x in blks if x[0] == bi][0]
                            for q in range(nf):
                                fc = f0 + q
                                S.pe(lambda e, b=b, q=q, dc=dc, fc=fc, a0=a0, nt=nt, wd=wd: e.matmul(ps[:, b, 0:nt], lhsT=wd[:, q, dc * 128:(dc + 1) * 128], rhs=actb[:, fc, a0:a0 + nt], start=(fc == 0), stop=(fc == NF - 1)),
                                     reads=[wdk, ("act", fc, bi)], writes=psk(b))
                    for (dc, bi), b in banks.items():
                        t0, nt, a0 = [(x[1], x[2], x[3]) for x in blks if x[0] == bi][0]
                        resid_update(ps[:, b, 0:nt], gbase, dh * 4 + dc, bi, t0, nt, psk(b))
                if len(blks) * 4 > 8:
                    (bi, t0, nt, a0) = blks[2]
                    for dh in range(2):
                        banks = {dc: ps_next() for dc in range(4)}
                        for f0 in range(0, NF, 4):
                            nf = min(4, NF - f0)
                            wd, wdk = load_w_bf16(w_down[l], nf, 512, row0=f0 * 128, col0=dh * 512)
                            for dc, b in banks.items():
                                for q in range(nf):
                                    fc = f0 + q
                                    S.pe(lambda e, b=b, q=q, dc=dc, fc=fc, wd=wd, nt=nt, a0=a0: e.matmul(ps[:, b, 0:nt], lhsT=wd[:, q, dc * 128:(dc + 1) * 128], rhs=actb[:, fc, a0:a0 + nt], start=(fc == 0), stop=(fc == NF - 1)),
                                         reads=[wdk, ("act", fc, bi)], writes=psk(b))
                        for dc, b in banks.items():
                            resid_update(ps[:, b, 0:nt], gbase, dh * 4 + dc, bi, t0, nt, psk(b))
            for gi, g4 in enumerate(range(0, NF, 4)):
                n = min(4, NF - g4)
                o2 = gi % 2
                b = ps_next()
                b2 = ps_next()
                for q in range(n):
                    fc = g4 + q
                    S.pe(lambda e, b=b, q=q, fc=fc: e.transpose(ps[0:2, b, q * 128:(q + 1) * 128], tailP[:, fc, :], ident_f[:]), reads=["tailP", "ident_f"], writes=psk(b))
                    S.pe(lambda e, b2=b2, q=q, fc=fc: e.transpose(ps[0:32, b2, q * 128:(q + 1) * 128], tailS[:, fc, :, :].rearrange("p s r -> p (s r)"), ident_f[:]), reads=["tailS", "ident_f"], writes=psk(b2))
                S.dve(lambda e, b=b, n=n, o2=o2: e.tensor_copy(out=ostgP[o2][:, 0:n * 128], in_=ps[0:2, b, 0:n * 128]), reads=psk(b), writes=[("ostgP", o2)])
                S.dve(lambda e, b2=b2, n=n, o2=o2: e.tensor_copy(out=ostg[o2][:, 0:n * 128], in_=ps[0:32, b2, 0:n * 128]), reads=psk(b2), writes=[("ostg", o2)])
                S.dma("sp", "foutP%d" % o2, lambda e, o2=o2, n=n, g4=g4: e.dma_start(out=fconv_p[l][:, g4 * 128:(g4 + n) * 128], in_=ostgP[o2][:, 0:n * 128]), reads=[("ostgP", o2)])
                S.dma("sp", "foutS%d" % o2, lambda e, o2=o2, n=n, g4=g4: e.dma_start(out=fconv_s[l][:, g4 * 128:(g4 + n) * 128], in_=ostg[o2][:, 0:n * 128]), reads=[("ostg", o2)])
            S.barrier()

        def do_final():
            a_reset()
            yT = a_f32(KC, 512)
            ytok = [a_f32(D) for _ in range(2)]
            n = 4
            sq = a_bf16(KC, 512)
            rstd = a_f32(512)
            tmp = a_f32(KC, 512)
            shb = norm_shift[n]
            oi = 0
            for bi, (t0, nt) in enumerate(ALLBLOCKS):
                S.act(lambda e, t0=t0, nt=nt: e.activation(out=sq[:, :, 0:nt], in_=xT[:, :, t0:t0 + nt], func=AF.Square), reads=[], writes=["sq"])
                b = ps_next()
                for kc in range(KC):
                    S.pe(lambda e, b=b, kc=kc, nt=nt: e.matmul(ps[:, b, 0:nt], lhsT=ones_b[:], rhs=sq[:, kc, 0:nt], start=(kc == 0), stop=(kc == KC - 1)), reads=["sq", "ones_b"], writes=psk(b))
                S.act(lambda e, b=b, nt=nt: e.activation(out=rstd[:, 0:nt], in_=ps[:, b, 0:nt], func=AF.Sqrt, bias=EPS, scale=1.0 / D), reads=psk(b), writes=["rstd"])
                S.dve(lambda e, nt=nt: e.reciprocal(out=rstd[:, 0:nt], in_=rstd[:, 0:nt]), reads=["rstd"], writes=["rstd"])
                S.dve(lambda e, t0=t0, nt=nt: e.tensor_tensor(out=tmp[:, :, 0:nt], in0=xT[:, :, t0:t0 + nt], in1=rstd[:, 0:nt].unsqueeze(1).to_broadcast([128, KC, nt]), op=ALU.mult), reads=["rstd"], writes=["tmp"])
                for kc in range(KC):
                    if t0 < TP:
                        S.act(lambda e, kc=kc, nt=nt: e.activation(out=yT[:, kc, 0:nt], in_=tmp[:, kc, 0:nt], func=AF.Identity, scale=Amod[:, n, kc, 0:1], bias=modT[:, shb + kc, 0:1]), reads=["tmp"], writes=["yT"])
                    else:
                        S.dve(lambda e, kc=kc: e.tensor_tensor(out=tmp[:, kc, 0:TS].rearrange("p (s j) -> p s j", j=4), in0=tmp[:, kc, 0:TS].rearrange("p (s j) -> p s j", j=4), in1=Amod[:, n, kc, 1:17].unsqueeze(2).to_broadcast([128, NSEQ, 4]), op=ALU.mult), reads=["tmp"], writes=["tmp"])
                        S.dve(lambda e, kc=kc: e.tensor_tensor(out=yT[:, kc, 0:TS].rearrange("p (s j) -> p s j", j=4), in0=tmp[:, kc, 0:TS].rearrange("p (s j) -> p s j", j=4), in1=modT[:, shb + kc, 1:17].unsqueeze(2).to_broadcast([128, NSEQ, 4]), op=ALU.add), reads=["tmp"], writes=["yT"])
                for sub in range(0, nt, 128):
                    rows = min(128, nt - sub)
                    o2 = oi % 2
                    oi += 1
                    for half in range(2):
                        b = ps_next()
                        for q in range(4):
                            kc = half * 4 + q
                            S.pe(lambda e, b=b, q=q, kc=kc, sub=sub, rows=rows: e.transpose(ps[0:rows, b, q * 128:(q + 1) * 128], yT[:, kc, sub:sub + rows], ident_f[:]), reads=["yT", "ident_f"], writes=psk(b))
                        if half == 0:
                            S.act(lambda e, b=b, o2=o2, rows=rows, half=half: e.copy(out=ytok[o2][0:rows, half * 512:(half + 1) * 512], in_=ps[0:rows, b, :]), reads=psk(b), writes=[("ytok", o2, half)])
                        else:
                            S.dve(lambda e, b=b, o2=o2, rows=rows, half=half: e.tensor_copy(out=ytok[o2][0:rows, half * 512:(half + 1) * 512], in_=ps[0:rows, b, :]), reads=psk(b), writes=[("ytok", o2, half)])
                    if t0 < TP:
                        dst = y_p[t0 + sub:t0 + sub + rows, :]
                    else:
                        dst = y_s[sub:sub + rows, :]
                    S.dma("sp", "yo%d" % o2, lambda e, dst=dst, o2=o2, rows=rows: e.dma_start(out=dst, in_=ytok[o2][0:rows, :]), reads=[("ytok", o2, 0), ("ytok", o2, 1)])
            S.barrier()

        K.__dict__.update(locals())
        G_ = globals()
        do_norm(0)
        import os
        if "do_retention" in G_ and not os.environ.get("SKIP_RET"):
            G_["do_retention"](K)
        do_norm(1)
        do_ffn(0)
        do_norm(2)
        if "do_gdn" in G_:
            G_["do_gdn"](K)
        do_norm(3)
        do_ffn(1)
        do_final()
        if "modT" in dbg_out:
            S.dma("sp", "dbg", lambda e: e.dma_start(out=dbg_out["modT"], in_=modT[:].rearrange("p a b -> p (a b)")))
        if "Amod" in dbg_out:
            S.dma("sp", "dbg", lambda e: e.dma_start(out=dbg_out["Amod"], in_=Amod[:].rearrange("p a b c -> p (a b c)")))
        S.emit(st)
    return nc


def do_retention(K):
    S = K.S; ps = K.ps; nc = K.nc
    a_f32 = K.a_f32; a_bf16 = K.a_bf16; ps_next = K.ps_next; psk = K.psk
    hT = K.hT; xT = K.xT; modT = K.modT
    ident_b = K.ident_b
    g = _gammas()
    K.a_reset()
    K.tmpg = a_f32(TS)
    tabs = a_f32(4, 512)
    qb = a_bf16(2, 512)
    kb = a_bf16(2, 512)
    t1 = a_f32(512)
    t2 = a_f32(512)
    vb = a_bf16(512)
    sg = a_f32(512)
    kh = a_bf16(256)
    im = a_bf16(128)
    og = a_bf16(512)
    ogT = a_bf16(4, 512)
    ss = a_f32(2)
    maskP = a_f32(128)
    maskS = a_f32(128)
    ksc = a_f32(2 * RET_H)
    segcol = a_f32(NSEQ)
    S32 = a_f32(2, 512)
    Sbf = a_bf16(2, 512)
    qf = a_f32(2, TS)
    qX = a_f32(2, NSEQ, TS)
    khm = [a_bf16(256) for _ in range(2)]
    Sin = [a_f32(2, 512) for _ in range(2)]
    Sout = a_f32(2, 512)
    segmask = a_f32(NSEQ, TS)
    S.dma("sp", "rc", lambda e: e.dma_start(out=ksc, in_=K.c_kscale), writes=["ksc"])
    S.dma("sp", "rc", lambda e: e.dma_start(out=segcol, in_=K.c_segcol), writes=["segcol"])
    S.dma("sp", "rc", lambda e: e.dma_start(out=segmask.rearrange("p a b -> p (a b)"), in_=K.c_segmask), writes=["segmask"])
    gbase = 16
    w_in = K.w_ret_in
    w_out = K.w_ret_out
    sctr = [0]
    for h in range(RET_H):
        wq, wqk = K.load_w_bf16(w_in, KC, 256, col0=h * 256, slot=0, off=0, key=("w", 0, "q"))
        wk, wkk = K.load_w_bf16(w_in, KC, 256, col0=1024 + h * 256, slot=0, off=2048, key=("w", 0, "k"))
        wv, wvk = K.load_w_bf16(w_in, KC, 512, col0=2048 + h * 512, slot=1)
        wg, wgk = K.load_w_bf16(w_in, KC, 512, col0=4096 + h * 512, slot=2)
        wo, wok = K.load_w_bf16(w_out, 4, 1024, row0=h * 512, slot=3)
        S.dma("sp", "rm", lambda e, h=h: e.dma_start(out=maskP, in_=K.c_retmask[h]), writes=["maskP"])
        S.dma("sp", "rm", lambda e, h=h: e.dma_start(out=maskS, in_=K.c_retmask[RET_H + h]), writes=["maskS"])
        for bi, (t0, nt) in enumerate(K.ALLBLOCKS):
            isS = t0 >= TP
            C = 64 if isS else 128
            for ti, (hh, cs_) in enumerate([(h, 0), (h, 1), (RET_H, 0), (RET_H, 1)]):
                S.dma("sp", "rt", lambda e, ti=ti, hh=hh, cs_=cs_, t0=t0, nt=nt: e.dma_start(out=tabs[:, ti, 0:nt], in_=K.c_rope[hh, cs_, :, t0:t0 + nt]), writes=[("tabs", ti)])
            for which, (wt, wkey, dst, tb) in enumerate([(wq, wqk, qb, 0), (wk, wkk, kb, 2)]):
                pb = [ps_next(), ps_next()]
                for dc in range(2):
                    for kc in range(KC):
                        S.pe(lambda e, b=pb[dc], kc=kc, dc=dc, wt=wt, t0=t0, nt=nt: e.matmul(ps[:, b, 0:nt], lhsT=wt[:, kc, dc * 128:(dc + 1) * 128], rhs=hT[:, kc, t0:t0 + nt], start=(kc == 0), stop=(kc == KC - 1)),
                             reads=[wkey], writes=psk(pb[dc]))
                p1 = ps[:, pb[0], 0:nt]
                p2 = ps[:, pb[1], 0:nt]
                cc = tabs[:, tb, 0:nt]
                sn = tabs[:, tb + 1, 0:nt]
                to_f = isS and which == 0
                d1 = qf[:, 0, :] if to_f else dst[:, 0, 0:nt]
                d2 = qf[:, 1, :] if to_f else dst[:, 1, 0:nt]
                S.dve(lambda e, p1=p1, cc=cc, nt=nt: e.tensor_tensor(out=t1[:, 0:nt], in0=p1, in1=cc, op=ALU.mult), reads=psk(pb[0]) + [("tabs", tb)], writes=["t1"])
                S.dve(lambda e, p2=p2, sn=sn, nt=nt: e.tensor_tensor(out=t2[:, 0:nt], in0=p2, in1=sn, op=ALU.mult), reads=psk(pb[1]) + [("tabs", tb + 1)], writes=["t2"])
                S.pool(lambda e, d1=d1, nt=nt: e.tensor_tensor(out=d1, in0=t1[:, 0:nt], in1=t2[:, 0:nt], op=ALU.subtract), reads=["t1", "t2"], writes=[("rd", which, 0)])
                S.dve(lambda e, p1=p1, sn=sn, nt=nt: e.tensor_tensor(out=t1[:, 0:nt], in0=p1, in1=sn, op=ALU.mult), reads=psk(pb[0]) + [("tabs", tb + 1)], writes=["t1"])
                S.dve(lambda e, p2=p2, cc=cc, nt=nt: e.tensor_tensor(out=t2[:, 0:nt], in0=p2, in1=cc, op=ALU.mult), reads=psk(pb[1]) + [("tabs", tb)], writes=["t2"])
                S.pool(lambda e, d2=d2, nt=nt: e.tensor_tensor(out=d2, in0=t1[:, 0:nt], in1=t2[:, 0:nt], op=ALU.add), reads=["t1", "t2"], writes=[("rd", which, 1)])
                if to_f:
                    S.act(lambda e: e.copy(out=qb[:, :, 0:TS], in_=qf[:, :, :]), reads=[("rd", 0, 0), ("rd", 0, 1)], writes=["qbS"])
            qkeys = [("rd", 0, 0), ("rd", 0, 1)] + (["qbS"] if isS else [])
            kkeys = [("rd", 1, 0), ("rd", 1, 1)]
            for c0 in range(0, nt, C):
                first = (not isS) and t0 == 0 and c0 == 0
                last = (not isS) and (t0 + c0 + C == TP)
                bv = ps_next()
                for kc in range(KC):
                    S.pe(lambda e, bv=bv, kc=kc, t0=t0, c0=c0, C=C: e.matmul(ps[0:C, bv, :], lhsT=hT[:, kc, t0 + c0:t0 + c0 + C], rhs=wv[:, kc, :], start=(kc == 0), stop=(kc == KC - 1)), reads=[wvk], writes=psk(bv))
                S.act(lambda e, bv=bv, C=C: e.copy(out=vb[0:C, :], in_=ps[0:C, bv, :]), reads=psk(bv), writes=["vb"])
                bg = ps_next()
                for kc in range(KC):
                    S.pe(lambda e, bg=bg, kc=kc, t0=t0, c0=c0, C=C: e.matmul(ps[0:C, bg, :], lhsT=hT[:, kc, t0 + c0:t0 + c0 + C], rhs=wg[:, kc, :], start=(kc == 0), stop=(kc == KC - 1)), reads=[wgk], writes=psk(bg))
                S.act(lambda e, bg=bg, C=C: e.activation(out=sg[0:C, :], in_=ps[0:C, bg, :], func=AF.Silu), reads=psk(bg), writes=["sg"])
                bk = ps_next()
                psb = ps[:, bk, :].bitcast(BF16)
                for dc in range(2):
                    S.pe(lambda e, psb=psb, dc=dc, c0=c0, C=C: e.transpose(psb[0:C, dc * 128:(dc + 1) * 128], kb[:, dc, c0:c0 + C], ident_b[:]), reads=kkeys + ["ident_b"], writes=psk(bk))
                kcol = (RET_H + h) if isS else h
                S.act(lambda e, psb=psb, C=C, kcol=kcol: e.activation(out=kh[0:C, :], in_=psb[0:C, 0:256], func=AF.Identity, scale=ksc[0:C, kcol:kcol + 1]), reads=psk(bk) + ["ksc"], writes=["kh"])
                bi_ = ps_next()
                for dc in range(2):
                    S.pe(lambda e, bi_=bi_, dc=dc, c0=c0, C=C: e.matmul(ps[0:C, bi_, 0:C], lhsT=kb[:, dc, c0:c0 + C], rhs=qb[:, dc, c0:c0 + C], start=(dc == 0), stop=(dc == 1)), reads=kkeys + qkeys, writes=psk(bi_))
                mk = maskS if isS else maskP
                S.dve(lambda e, bi_=bi_, C=C, mk=mk: e.tensor_tensor(out=im[0:C, 0:C], in0=ps[0:C, bi_, 0:C], in1=mk[0:C, 0:C], op=ALU.mult), reads=psk(bi_) + ["maskP", "maskS"], writes=["im"])
                bo = ps_next()
                has_inter = isS or not first
                S.pe(lambda e, bo=bo, C=C, has_inter=has_inter: e.matmul(ps[0:C, bo, :], lhsT=im[0:C, 0:C], rhs=vb[0:C, :], start=True, stop=not has_inter), reads=["im", "vb"], writes=psk(bo))
                if not isS:
                    if not first:
                        for dc in range(2):
                            S.pe(lambda e, bo=bo, dc=dc, c0=c0, C=C: e.matmul(ps[0:C, bo, :], lhsT=qb[:, dc, c0:c0 + C], rhs=Sbf[:, dc, :], start=False, stop=(dc == 1)), reads=qkeys + [("Sbf", dc)], writes=psk(bo))
                    for dc in range(2):
                        bs = ps_next()
                        S.pe(lambda e, bs=bs, dc=dc, C=C: e.matmul(ps[:, bs, :], lhsT=kh[0:C, dc * 128:(dc + 1) * 128], rhs=vb[0:C, :], start=True, stop=True), reads=["kh", "vb"], writes=psk(bs))
                        if first:
                            S.dve(lambda e, bs=bs, dc=dc: e.tensor_copy(out=S32[:, dc, :], in_=ps[:, bs, :]), reads=psk(bs), writes=[("S32", dc)])
                        else:
                            S.dve(lambda e, bs=bs, dc=dc, gc=float(g[h] ** 128): e.scalar_tensor_tensor(out=S32[:, dc, :], in0=S32[:, dc, :], scalar=gc, in1=ps[:, bs, :], op0=ALU.mult, op1=ALU.add), reads=psk(bs) + [("S32", dc)], writes=[("S32", dc)])
                        if last:
                            S.dma("sp", "rpo", lambda e, dc=dc, h=h: e.dma_start(out=K.ret_p[h, dc * 128:(dc + 1) * 128, :], in_=S32[:, dc, :]), reads=[("S32", dc)])
                        else:
                            S.act(lambda e, dc=dc: e.copy(out=Sbf[:, dc, :], in_=S32[:, dc, :]), reads=[("S32", dc)], writes=[("Sbf", dc)])
                else:
                    K.P.reserved = {bo}
                    for dc in range(2):
                        S.dve(lambda e, dc=dc: e.tensor_tensor(out=qX[:, dc, :, :], in0=qf[:, dc, :].unsqueeze(1).to_broadcast([128, NSEQ, TS]), in1=segmask[:, :, :], op=ALU.mult), reads=[("rd", 0, dc), "segmask"], writes=[("qX", dc)])
                    def _ldr(n, h=h):
                        i3 = n % 2
                        S.dma("sp", "sin%d" % i3, lambda e, i3=i3, n=n, h=h: e.dma_start(out=Sin[i3], in_=K.s_ret[n, h].rearrange("(dc p) v -> p dc v", p=128)), writes=[("Sin", i3)])
                    _ldr(0)
                    for s_ in range(NSEQ):
                        i2 = s_ % 2
                        if s_ + 1 < NSEQ:
                            _ldr(s_ + 1)
                        for dc in range(2):
                            S.pe(lambda e, bo=bo, dc=dc, s_=s_, i2=i2: e.matmul(ps[0:TS, bo, :], lhsT=qX[:, dc, s_, :], rhs=Sin[i2][:, dc, :], start=False, stop=(s_ == NSEQ - 1 and dc == 1)), reads=[("qX", dc), ("Sin", i2)], writes=psk(bo))
                        S.dve(lambda e, i2=i2, s_=s_: e.tensor_scalar(out=khm[i2][0:TS, :], in0=kh[0:TS, :], scalar1=segcol[0:TS, s_:s_ + 1], scalar2=None, op0=ALU.mult), reads=["kh", "segcol"], writes=[("khm", i2)])
                        for dc in range(2):
                            bs = ps_next()
                            S.pe(lambda e, bs=bs, dc=dc, i2=i2: e.matmul(ps[:, bs, :], lhsT=khm[i2][0:TS, dc * 128:(dc + 1) * 128], rhs=vb[0:TS, :], start=True, stop=True), reads=[("khm", i2), "vb"], writes=psk(bs))
                            S.dve(lambda e, bs=bs, dc=dc, i2=i2, gc=float(g[h] ** 4): e.scalar_tensor_tensor(out=Sout[:, dc, :], in0=Sin[i2][:, dc, :], scalar=gc, in1=ps[:, bs, :], op0=ALU.mult, op1=ALU.add), reads=psk(bs) + [("Sin", i2)], writes=[("Sout", dc)])
                        S.dma("sp", "sout", lambda e, s_=s_, h=h: e.dma_start(out=K.ret_s[s_, h].rearrange("(dc p) v -> p dc v", p=128), in_=Sout), reads=[("Sout", 0), ("Sout", 1)], writes=[])
                    K.P.reserved = set()
                _ret_tail(K, h, bo, C, c0, sg, og, ogT, ss, t1)
            for dc8 in range(KC):
                b = ps_next()
                for ec in range(4):
                    S.pe(lambda e, b=b, ec=ec, dc8=dc8, nt=nt: e.matmul(ps[:, b, 0:nt], lhsT=wo[:, ec, dc8 * 128:(dc8 + 1) * 128], rhs=ogT[:, ec, 0:nt], start=(ec == 0), stop=(ec == 3)), reads=[wok, "ogT"], writes=psk(b))
                K.resid_update(ps[:, b, 0:nt], gbase, dc8, ("S" if isS else bi), t0, nt, psk(b))
        S.barrier()
    S.barrier()


def _ret_tail(K, h, bo, C, c0, sg, og, ogT, ss, junk):
    S = K.S; ps = K.ps
    _f = lambda e: e.activation(out=junk[0:C, :], in_=ps[0:C, bo, :], func=AF.Square, accum_out=ss[0:C, 0:1])
    _f._multi = True
    S.act(_f, reads=K.psk(bo), writes=["t1", "ss"])
    S.act(lambda e: e.activation(out=ss[0:C, 1:2], in_=ss[0:C, 0:1], func=AF.Sqrt, bias=EPS, scale=1.0 / 512), reads=["ss"], writes=["ss2"])
    S.dve(lambda e: e.reciprocal(out=ss[0:C, 1:2], in_=ss[0:C, 1:2]), reads=["ss2"], writes=["ss2"])
    S.dve(lambda e: e.scalar_tensor_tensor(out=og[0:C, :], in0=ps[0:C, bo, :], scalar=ss[0:C, 1:2], in1=sg[0:C, :], op0=ALU.mult, op1=ALU.mult), reads=K.psk(bo) + ["ss2", "sg"], writes=["og"])
    bt = K.ps_next()
    psb = ps[:, bt, :].bitcast(BF16)
    for ec in range(4):
        S.pe(lambda e, ec=ec: e.transpose(psb[:, ec * C:(ec + 1) * C], og[0:C, ec * 128:(ec + 1) * 128], K.ident_b[0:C, 0:C]), reads=["og", "ident_b"], writes=K.psk(bt))
    S.act(lambda e: e.copy(out=ogT[:, :, c0:c0 + C], in_=psb[:, 0:4 * C].rearrange("p (a b) -> p a b", a=4)), reads=K.psk(bt), writes=["ogT"])


def do_gdn(K):
    S = K.S; ps = K.ps
    a_f32 = K.a_f32; a_bf16 = K.a_bf16; ps_next = K.ps_next; psk = K.psk
    hT = K.hT; ident_f = K.ident_f; ident_b = K.ident_b; ones_b = K.ones_b
    w_in = K.w_gdn_in; w_out = K.w_gdn_out
    K.a_reset()
    K.tmpg = a_f32(TS)
    gm = a_f32(8, 128)
    segmask = a_f32(NSEQ, TS)
    segcol = a_f32(NSEQ)
    beta_all = a_f32(17, 16); g_all = a_f32(17, 16); negeG_all = a_f32(17, 16); kdec_all = a_f32(17, 16)
    gv = a_f32(32); negA = a_f32(16); gnw = a_f32(128)
    wcT = a_f32(32, 4)
    ones128 = a_f32(128)
    wba = a_bf16(KC, 32)
    Gx = a_f32(3 + 512); acc = a_f32(512); acc2 = a_f32(512); cch = a_f32(4, 3); gsT = a_f32(4, 48); Gxs = a_f32(NSEQ, 7)
    tailP = a_f32(4, 3); tailS = a_f32(4, NSEQ, 3)
    vs = a_f32(2, 512)
    sqb = a_bf16(512); rs = a_f32(512)
    qn = a_bf16(512); kn = a_bf16(512); qnf = a_f32(TS); knf = a_f32(TS)
    Bm = a_f32(2, 128); E = a_f32(2, 128); E2 = a_f32(2, 128); DTm = a_f32(2, 128)
    U = a_bf16(2, 128); UT = a_bf16(2, 128)
    UoM = [a_bf16(2, 128) for _ in range(3)]; UoTM = [a_bf16(2, 128) for _ in range(3)]
    Nb = [a_bf16(2, 128) for _ in range(2)]; Pb = [a_bf16(2, 128) for _ in range(2)]; PTb = [a_bf16(2, 128) for _ in range(2)]
    NTb = [a_bf16(2, 128) for _ in range(2)]; bm = a_bf16(4, 128)
    ktok = a_bf16(128); xv = a_bf16(2, 128); vnew = a_bf16(2, 128)
    Sbf = a_bf16(2, 128); og = a_bf16(256); ogT = a_bf16(2, 512)
    S32 = a_f32(2, 128); ss = a_f32(8)
    X = a_f32(NSEQ, TS); qgf = a_f32(TS)
    kdm = a_bf16(128); Sin = [a_f32(128) for _ in range(2)]; Sout = a_f32(128)
    ostg = acc
    eGl = [a_f32(2), a_f32(2)]
    w3f = K.wring[:, 3, :].bitcast(F32)
    w3b = K.wring[:, 3, :]
    szn2 = [w3f[:, 0:256], w3f[:, 256:512]]
    vtok2 = [w3f[:, 512:768].rearrange("p (a b) -> p a b", a=2), w3f[:, 768:1024].rearrange("p (a b) -> p a b", a=2)]

    def _b3(i):
        return w3b[:, 2048 + i * 256:2048 + (i + 1) * 256].rearrange("p (a b) -> p a b", a=2)
    Nfin = [_b3(0), _b3(1)]; attnT2 = [_b3(2), _b3(3)]; qg2 = [_b3(4), _b3(5)]; kd2 = [_b3(6), _b3(7)]

    def _f32v(b):
        return b.rearrange("p a b -> p (a b)").bitcast(F32)
    SinR = [Sin[0], Sin[1]] + [_f32v(x) for x in UoM + UoTM]
    SoutR = [Sout, _f32v(NTb[0]), _f32v(NTb[1])]
    RDEPTH = 7
    NLD = 2 * NSEQ
    mark = K.A.off

    S.dma("sp", "gc", lambda e: e.dma_start(out=gm.rearrange("p a b -> p (a b)"), in_=K.c_gmask), writes=["gm"])
    S.dma("sp", "gc", lambda e: e.dma_start(out=segmask.rearrange("p a b -> p (a b)"), in_=K.c_segmask), writes=["segmask"])
    S.dma("sp", "gc", lambda e: e.dma_start(out=segcol, in_=K.c_segcol), writes=["segcol"])
    S.dma("pool", "gcb", lambda e: e.dma_start(out=bm.rearrange("p a b -> p (a b)"), in_=K.c_bmask), writes=["bm"])
    S.dma("sp", "gc", lambda e: e.dma_start(out=gv, in_=K.gdn_vec.partition_broadcast(128)), writes=["gv"])
    S.dma("sp", "gc", lambda e: e.dma_start(out=gnw, in_=K.gdn_norm.partition_broadcast(128)), writes=["gnw"])
    S.dve(lambda e: e.memset(ones128, 1.0), writes=["ones128"])
    S.dve(lambda e: e.memset(ss[:, 4:6], -0.5), writes=["mhalf"])
    S.act(lambda e: e.activation(out=negA, in_=gv[:, 0:16], func=AF.Exp), reads=["gv"], writes=["negA"])
    S.dve(lambda e: e.tensor_scalar(out=negA, in0=negA, scalar1=-1.0, scalar2=None, op0=ALU.mult), reads=["negA"], writes=["negA"])
    TRIU = {False: gm[:, 0, :], True: gm[:, 4, :]}
    SU = {False: gm[:, 1, :], True: gm[:, 5, :]}
    INCL = {False: gm[:, 2, :], True: gm[:, 6, :]}
    STRICT = {False: gm[:, 3, :], True: gm[:, 7, :]}
    import os
    STG = int(os.environ.get('GDN_STAGE', 99))
    if STG == 0:
        S.barrier(); return
    Xflat = X.rearrange("p a b -> p (a b)")
    wst = [Xflat[0:4, 0:512], Xflat[0:4, 512:1024]]
    bw = ps_next()
    for pc in range(8):
        S.dma("sp", "wst%d" % (pc % 2), lambda e, pc=pc: e.dma_start(out=wst[pc % 2], in_=K.w_gconv[:, pc * 512:(pc + 1) * 512]), writes=[("wst", pc % 2)])
        for q in range(4):
            cidx = pc * 4 + q
            S.pe(lambda e, pc=pc, q=q, cidx=cidx: e.transpose(ps[:, bw, cidx * 4:(cidx + 1) * 4], wst[pc % 2][:, q * 128:(q + 1) * 128], ident_f[0:4, 0:4]), reads=[("wst", pc % 2), "ident_f"], writes=psk(bw))
    S.dve(lambda e: e.tensor_copy(out=wcT.rearrange("p a b -> p (a b)"), in_=ps[:, bw, 0:128]), reads=psk(bw), writes=["wcT"])
    if STG == 1:
        S.barrier(); return
    src = w_in[:, 6144:6176].rearrange("(k p) c -> p k c", p=128)
    S.dma("pool", "wba", lambda e: e.dma_start(out=wba, in_=src), writes=["wba"])
    if STG == 2:
        S.barrier(); return
    tiles = [(t * 128, 128, False) for t in range(16)] + [(TP, TS, True)]
    for tl, (t0, C, isS) in enumerate(tiles):
        pb = ps_next()
        for kc in range(KC):
            S.pe(lambda e, pb=pb, kc=kc, t0=t0, C=C: e.matmul(ps[0:C, pb, 0:32], lhsT=hT[:, kc, t0:t0 + C], rhs=wba[:, kc, :], start=(kc == 0), stop=(kc == KC - 1)), reads=["wba"], writes=psk(pb))
        S.act(lambda e, pb=pb, C=C, tl=tl: e.activation(out=beta_all[0:C, tl, :], in_=ps[0:C, pb, 0:16], func=AF.Sigmoid), reads=psk(pb), writes=[("beta", tl)])
        S.dve(lambda e, pb=pb, C=C, tl=tl: e.tensor_tensor(out=g_all[0:C, tl, :], in0=ps[0:C, pb, 16:32], in1=gv[0:C, 16:32], op=ALU.add), reads=psk(pb) + ["gv"], writes=[("g", tl)])
        S.act(lambda e, C=C, tl=tl: e.activation(out=g_all[0:C, tl, :], in_=g_all[0:C, tl, :], func=AF.Exp), reads=[("g", tl)], writes=[("g", tl)])
        S.act(lambda e, C=C, tl=tl: e.activation(out=g_all[0:C, tl, :], in_=g_all[0:C, tl, :], func=AF.Ln, bias=1.0), reads=[("g", tl)], writes=[("g", tl)])
        S.dve(lambda e, C=C, tl=tl: e.tensor_tensor(out=g_all[0:C, tl, :], in0=g_all[0:C, tl, :], in1=negA[0:C, :], op=ALU.mult), reads=[("g", tl), "negA"], writes=[("g", tl)])
        pg = ps_next()
        S.pe(lambda e, pg=pg, C=C, tl=tl, isS=isS: e.matmul(ps[0:C, pg, 0:16], lhsT=TRIU[isS][0:C, 0:C], rhs=g_all[0:C, tl, :], start=True, stop=True), reads=[("g", tl), "gm"], writes=psk(pg))
        S.pe(lambda e, pg=pg, C=C, tl=tl, isS=isS: e.matmul(ps[0:C, pg, 16:32], lhsT=SU[isS][0:C, 0:C], rhs=g_all[0:C, tl, :], start=True, stop=True), reads=[("g", tl), "gm"], writes=psk(pg))
        S.act(lambda e, pg=pg, C=C, tl=tl: e.activation(out=negeG_all[0:C, tl, :], in_=ps[0:C, pg, 0:16], func=AF.Exp), reads=psk(pg), writes=[("negeG", tl)])
        S.dve(lambda e, C=C, tl=tl: e.tensor_scalar(out=negeG_all[0:C, tl, :], in0=negeG_all[0:C, tl, :], scalar1=-1.0, scalar2=None, op0=ALU.mult), reads=[("negeG", tl)], writes=[("negeG", tl)])
        S.act(lambda e, pg=pg, C=C, tl=tl: e.activation(out=kdec_all[0:C, tl, :], in_=ps[0:C, pg, 16:32], func=AF.Exp), reads=psk(pg), writes=[("kdec", tl)])
    S.barrier()
    K.A.off = mark
    gbase = 48 + 16

    import os
    for hk in range(int(os.environ.get("GDN_HK0", 0)), int(os.environ.get("GDN_NHK", GDN_HK))):
        hv0 = 2 * hk
        wq, wqk = K.load_w_bf16(w_in, KC, 128, col0=hk * 128, slot=0, off=0, key=("w", 0, "q"))
        wk, wkk = K.load_w_bf16(w_in, KC, 128, col0=1024 + hk * 128, slot=0, off=1024, key=("w", 0, "k"))
        wv, wvk = K.load_w_bf16(w_in, KC, 256, col0=2048 + hk * 256, slot=1, off=0, key=("w", 1, "v"))
        wz, wzk = K.load_w_bf16(w_in, KC, 256, col0=4096 + hk * 256, slot=1, off=2048, key=("w", 1, "z"))
        wo, wok = K.load_w_bf16(w_out, 2, 1024, row0=hk * 256, slot=2)
        cids = [hk, 8 + hk, 16 + 2 * hk, 17 + 2 * hk]
        CUT = os.environ.get('GDN_CUT', '')
        if hk >= 1 and 'D' in CUT:
            S.barrier(); continue
        gst = a_f32(512, parts=48)
        K.A.off = mark
        if not (hk >= 1 and 'H' in CUT):
            for ci, cid in enumerate(cids):
                S.dma("sp", "gst", lambda e, ci=ci, cid=cid, gst=gst: e.dma_start(out=gst[:, ci * 128:(ci + 1) * 128], in_=K.s_gconv[:, cid * 128:(cid + 1) * 128]), writes=["gst"])
            bq = ps_next()
            for ci in range(4):
                S.pe(lambda e, ci=ci, bq=bq, gst=gst: e.transpose(ps[:, bq, ci * 48:(ci + 1) * 48], gst[:, ci * 128:(ci + 1) * 128], ident_f[0:48, 0:48]), reads=["gst", "ident_f"], writes=psk(bq))
            S.dve(lambda e, bq=bq: e.tensor_copy(out=gsT.rearrange("p a b -> p (a b)"), in_=ps[:, bq, 0:192]), reads=psk(bq), writes=["gsT"])
        S.dve(lambda e: e.memset(cch, 0.0), writes=["cch"])
        if hk >= 1 and 'E' in CUT:
            S.barrier(); continue
        for bi, (t0, nt) in enumerate(K.ALLBLOCKS):
            isS = t0 >= TP
            C = 64 if isS else 128
            lastblk = (t0 + nt == TP)
            if isS:
                S.barrier()
            for ci, (wt, wkey, col) in enumerate([(wq, wqk, 0), (wk, wkk, 0), (wv, wvk, 0), (wv, wvk, 128)]):
                pp = ps_next()
                for kc in range(KC):
                    S.pe(lambda e, pp=pp, kc=kc, wt=wt, col=col, t0=t0, nt=nt: e.matmul(ps[:, pp, 0:nt], lhsT=wt[:, kc, col:col + 128], rhs=hT[:, kc, t0:t0 + nt], start=(kc == 0), stop=(kc == KC - 1)), reads=[wkey], writes=psk(pp))
                cid = cids[ci]
                wc = [wcT[:, cid, i:i + 1] for i in range(4)]
                accb = acc if ci % 2 == 0 else acc2
                ak = ("acc", ci % 2)
                dst = accb[:, 0:nt] if ci < 2 else vs[:, ci - 2, 0:nt]
                if not isS:
                    S.dve(lambda e, ci=ci: e.tensor_copy(out=Gx[:, 0:3], in_=cch[:, ci, :]), reads=["cch"], writes=["Gx"])
                    S.act(lambda e, pp=pp, nt=nt: e.copy(out=Gx[:, 3:3 + nt], in_=ps[:, pp, 0:nt]), reads=psk(pp), writes=["Gx"])
                    S.dve(lambda e, ci=ci, nt=nt: e.tensor_copy(out=cch[:, ci, :], in_=Gx[:, nt:nt + 3]), reads=["Gx"], writes=["cch"])
                    if lastblk:
                        S.dve(lambda e, ci=ci, nt=nt: e.tensor_copy(out=tailP[:, ci, :], in_=Gx[:, nt:nt + 3]), reads=["Gx"], writes=["tailP"])
                    x3 = [Gx[:, i:i + nt] for i in range(4)]
                    a_ = accb[:, 0:nt]
                    d_ = dst
                    gk = ["Gx"]
                else:
                    S.dve(lambda e, ci=ci: e.tensor_copy(out=Gxs[:, :, 0:3], in_=gsT[:, ci, :].rearrange("p (s r) -> p s r", r=3)), reads=["gsT"], writes=["Gxs"])
                    S.act(lambda e, pp=pp: e.copy(out=Gxs[:, :, 3:7], in_=ps[:, pp, 0:TS].rearrange("p (s j) -> p s j", j=4)), reads=psk(pp), writes=["Gxs"])
                    S.dve(lambda e, ci=ci: e.tensor_copy(out=tailS[:, ci, :, :], in_=Gxs[:, :, 4:7]), reads=["Gxs"], writes=["tailS"])
                    x3 = [Gxs[:, :, i:i + 4] for i in range(4)]
                    a_ = accb[:, 0:TS].rearrange("p (s j) -> p s j", j=4)
                    d_ = dst.rearrange("p (s j) -> p s j", j=4)
                    gk = ["Gxs"]
                S.act(lambda e, a_=a_, x3=x3, wc=wc: e.activation(out=a_, in_=x3[3], func=AF.Identity, scale=wc[3]), reads=gk + ["wcT"], writes=[ak])
                for i in (2, 1, 0):
                    S.dve(lambda e, a_=a_, x3=x3, wc=wc, i=i: e.scalar_tensor_tensor(out=a_, in0=x3[i], scalar=wc[i], in1=a_, op0=ALU.mult, op1=ALU.add), reads=gk + [ak], writes=[ak])
                S.act(lambda e, a_=a_, d_=d_: e.activation(out=d_, in_=a_, func=AF.Silu), reads=[ak], writes=([ak, ("cv", ci)] if ci < 2 else [("cv", ci)]))
                if ci < 2:
                    S.act(lambda e, nt=nt, accb=accb: e.activation(out=sqb[:, 0:nt], in_=accb[:, 0:nt], func=AF.Square), reads=[ak], writes=["sqb"])
                    pn = ps_next()
                    S.pe(lambda e, pn=pn, nt=nt: e.matmul(ps[:, pn, 0:nt], lhsT=ones_b[:], rhs=sqb[:, 0:nt], start=True, stop=True), reads=["sqb", "ones_b"], writes=psk(pn))
                    S.act(lambda e, pn=pn, nt=nt: e.activation(out=rs[:, 0:nt], in_=ps[:, pn, 0:nt], func=AF.Sqrt, bias=EPS, scale=1.0), reads=psk(pn), writes=["rs"])
                    S.dve(lambda e, nt=nt: e.reciprocal(out=rs[:, 0:nt], in_=rs[:, 0:nt]), reads=["rs"], writes=["rs"])
                    dn = qn if ci == 0 else kn
                    dnf = qnf if ci == 0 else knf
                    sc = (128.0 ** -0.5) if ci == 0 else 1.0
                    if isS:
                        S.dve(lambda e, dnf=dnf, sc=sc, accb=accb: e.scalar_tensor_tensor(out=dnf[:, :], in0=accb[:, 0:TS], scalar=sc, in1=rs[:, 0:TS], op0=ALU.mult, op1=ALU.mult), reads=[ak, "rs"], writes=[("nf", ci)])
                        S.act(lambda e, dn=dn, dnf=dnf: e.copy(out=dn[:, 0:TS], in_=dnf[:, :]), reads=[("nf", ci)], writes=[("n", ci)])
                    else:
                        S.dve(lambda e, dn=dn, sc=sc, nt=nt, accb=accb: e.scalar_tensor_tensor(out=dn[:, 0:nt], in0=accb[:, 0:nt], scalar=sc, in1=rs[:, 0:nt], op0=ALU.mult, op1=ALU.mult), reads=[ak, "rs"], writes=[("n", ci)])
            def chunk_gen(c0, pb):
                tl = (t0 + c0) // 128
                first = (not isS) and t0 == 0 and c0 == 0
                last = (not isS) and (t0 + c0 + C == TP)
                L = 1 if isS else 6
                bvt = ps_next()
                for e_ in range(2):
                    S.pe(lambda e, e_=e_, bvt=bvt, c0=c0, C=C: e.transpose(ps[0:C, bvt, e_ * 128:(e_ + 1) * 128], vs[:, e_, c0:c0 + C], ident_f[:]), reads=[("cv", 2 + e_), "ident_f"], writes=psk(bvt))
                    yield
                S.act(lambda e, bvt=bvt, C=C: e.copy(out=vtok2[pb][0:C].rearrange("p a b -> p (a b)"), in_=ps[0:C, bvt, 0:256]), reads=psk(bvt), writes=[("vtok", pb)])
                yield
                bz = ps_next()
                for kc in range(KC):
                    S.pe(lambda e, bz=bz, kc=kc, t0=t0, c0=c0, C=C: e.matmul(ps[0:C, bz, 0:256], lhsT=hT[:, kc, t0 + c0:t0 + c0 + C], rhs=wz[:, kc, :], start=(kc == 0), stop=(kc == KC - 1)), reads=[wzk], writes=psk(bz))
                    yield
                S.act(lambda e, bz=bz, C=C: e.activation(out=szn2[pb][0:C, :], in_=ps[0:C, bz, 0:256], func=AF.Silu), reads=psk(bz), writes=[("szn", pb)])
                yield
                S.dve(lambda e, C=C: e.tensor_tensor(out=szn2[pb][0:C, :].rearrange("p (a b) -> p a b", a=2), in0=szn2[pb][0:C, :].rearrange("p (a b) -> p a b", a=2), in1=gnw[0:C, :].unsqueeze(1).to_broadcast([C, 2, 128]), op=ALU.mult), reads=[("szn", pb), "gnw"], writes=[("szn", pb)])
                yield
                bkt = ps_next()
                pkb = ps[:, bkt, :].bitcast(BF16)
                S.pe(lambda e, pkb=pkb, c0=c0, C=C: e.transpose(pkb[0:C, 0:128], kn[:, c0:c0 + C], ident_b[:]), reads=[("n", 1), "ident_b"], writes=psk(bkt))
                yield
                S.act(lambda e, pkb=pkb, C=C: e.copy(out=ktok[0:C, :], in_=pkb[0:C, 0:128]), reads=psk(bkt), writes=["ktok"])
                yield
                bkk = ps_next()
                S.pe(lambda e, bkk=bkk, c0=c0, C=C: e.matmul(ps[0:C, bkk, 0:C], lhsT=kn[:, c0:c0 + C], rhs=kn[:, c0:c0 + C], start=True, stop=True), reads=[("n", 1)], writes=psk(bkk))
                yield
                S.pe(lambda e, bkk=bkk, c0=c0, C=C: e.matmul(ps[0:C, bkk, 128:128 + C], lhsT=kn[:, c0:c0 + C], rhs=qn[:, c0:c0 + C], start=True, stop=True), reads=[("n", 0), ("n", 1)], writes=psk(bkk))
                yield
                bd = ps_next()
                bd2 = ps_next()
                for e_ in range(2):
                    hv = hv0 + e_
                    S.dve(lambda e, e_=e_, hv=hv, C=C, tl=tl, isS=isS: e.tensor_scalar(out=Bm[0:C, e_, 0:C], in0=TRIU[isS][0:C, 0:C], scalar1=g_all[0:C, tl, hv:hv + 1], scalar2=None, op0=ALU.mult), reads=["gm"], writes=[("Bm", e_)])
                    yield
                    S.pe(lambda e, e_=e_, bd=bd, C=C, isS=isS: e.matmul(ps[0:C, bd, e_ * 128:e_ * 128 + C], lhsT=SU[isS][0:C, 0:C], rhs=Bm[0:C, e_, 0:C], start=True, stop=True), reads=[("Bm", e_), "gm"], writes=psk(bd))
                    yield
                    S.pe(lambda e, e_=e_, bd2=bd2, C=C: e.matmul(ps[:, bd2, e_ * 128:e_ * 128 + C], lhsT=ones128[0:C, :], rhs=Bm[0:C, e_, 0:C], start=True, stop=True), reads=[("Bm", e_), "ones128"], writes=psk(bd2))
                    yield
                S.act(lambda e, bd=bd, C=C: e.activation(out=E[0:C, :, 0:C], in_=ps[0:C, bd, 0:256].rearrange("p (a b) -> p a b", a=2)[:, :, 0:C], func=AF.Exp), reads=psk(bd), writes=["E"])
                yield
                S.act(lambda e, bd2=bd2, C=C: e.activation(out=E2[:, :, 0:C], in_=ps[:, bd2, 0:256].rearrange("p (a b) -> p a b", a=2)[:, :, 0:C], func=AF.Exp), reads=psk(bd2), writes=["E2"])
                yield
                S.dve(lambda e, C=C, isS=isS: e.tensor_tensor(out=DTm[0:C, :, 0:C], in0=E[0:C, :, 0:C], in1=INCL[isS][0:C, 0:C].unsqueeze(1).to_broadcast([C, 2, C]), op=ALU.mult), reads=["E", "gm"], writes=["DTm"])
                yield
                S.dve(lambda e, C=C, isS=isS: e.tensor_tensor(out=E[0:C, :, 0:C], in0=E[0:C, :, 0:C], in1=STRICT[isS][0:C, 0:C].unsqueeze(1).to_broadcast([C, 2, C]), op=ALU.mult), reads=["E", "gm", "DTm"], writes=["E"])
                yield
                for e_ in range(2):
                    hv = hv0 + e_
                    S.dve(lambda e, e_=e_, hv=hv, bkk=bkk, C=C, tl=tl: e.scalar_tensor_tensor(out=U[0:C, e_, 0:C], in0=ps[0:C, bkk, 0:C], scalar=beta_all[0:C, tl, hv:hv + 1], in1=E[0:C, e_, 0:C], op0=ALU.mult, op1=ALU.mult), reads=psk(bkk) + ["E"], writes=[("U", e_)])
                    yield
                    S.dve(lambda e, e_=e_, bkk=bkk, C=C: e.tensor_tensor(out=attnT2[pb][0:C, e_, 0:C], in0=ps[0:C, bkk, 128:128 + C], in1=DTm[0:C, e_, 0:C], op=ALU.mult), reads=psk(bkk) + ["DTm"], writes=[("attnT", e_, pb)])
                    yield
                    if isS:
                        pass
                    else:
                        S.pool(lambda e, e_=e_, c0=c0, C=C: e.tensor_tensor(out=qg2[pb][:, e_, 0:C], in0=qn[:, c0:c0 + C], in1=E2[:, e_, 0:C], op=ALU.mult), reads=[("n", 0), "E2"], writes=[("qg", e_, pb)])
                        yield
                    S.dve(lambda e, e_=e_, hv=hv, C=C, tl=tl: e.tensor_scalar(out=kd2[pb][0:C, e_, :], in0=ktok[0:C, :], scalar1=kdec_all[0:C, tl, hv:hv + 1], scalar2=None, op0=ALU.mult), reads=["ktok"], writes=[("kd", e_, pb)])
                    yield
                but = ps_next()
                pub = ps[:, but, :].bitcast(BF16)
                for e_ in range(2):
                    S.pe(lambda e, e_=e_, pub=pub, C=C: e.transpose(pub[0:C, e_ * 128:e_ * 128 + C], U[0:C, e_, 0:C], ident_b[0:C, 0:C]), reads=[("U", e_), "ident_b"], writes=psk(but))
                    yield
                S.act(lambda e, pub=pub, C=C: e.copy(out=UT[0:C, :, 0:C], in_=pub[0:C, 0:256].rearrange("p (a b) -> p a b", a=2)[:, :, 0:C]), reads=psk(but), writes=["UT"])
                yield
                if isS:
                    S.dve(lambda e, C=C: e.tensor_tensor(out=Nb[0][0:C, :, 0:C], in0=ident_f[0:C, 0:C].unsqueeze(1).to_broadcast([C, 2, C]), in1=U[0:C, :, 0:C], op=ALU.subtract), reads=[("U", 0), ("U", 1), "ident_f"], writes=[("N", 0)])
                    yield
                    Pprev, PTprev, Pk_, PTk_ = U, UT, ["U0", "U1"], ["UT"]
                    Pkeys_prev = [("U", 0), ("U", 1)]
                    PTkeys_prev = ["UT"]
                    ni = 0
                    for lv in range(1, L + 1):
                        pi = lv % 2
                        need_P = lv < L
                        if need_P:
                            b1 = ps_next()
                            for e_ in range(2):
                                S.pe(lambda e, e_=e_, b1=b1, C=C, PTprev=PTprev, Pprev=Pprev: e.matmul(ps[0:C, b1, e_ * 128:e_ * 128 + C], lhsT=PTprev[0:C, e_, 0:C], rhs=Pprev[0:C, e_, 0:C], start=True, stop=True), reads=Pkeys_prev + PTkeys_prev, writes=psk(b1))
                                yield
                            S.act(lambda e, b1=b1, C=C, pi=pi: e.copy(out=Pb[pi][0:C, :, 0:C], in_=ps[0:C, b1, 0:256].rearrange("p (a b) -> p a b", a=2)[:, :, 0:C]), reads=psk(b1), writes=[("P", pi)])
                            yield
                        b2 = ps_next()
                        for e_ in range(2):
                            S.pe(lambda e, e_=e_, b2=b2, C=C, PTprev=PTprev, Pprev=Pprev: e.matmul(ps[0:C, b2, e_ * 128:e_ * 128 + C], lhsT=Pprev[0:C, e_, 0:C], rhs=PTprev[0:C, e_, 0:C], start=True, stop=True), reads=Pkeys_prev + PTkeys_prev, writes=psk(b2))
                            yield
                        S.act(lambda e, b2=b2, C=C, pi=pi: e.copy(out=PTb[pi][0:C, :, 0:C], in_=ps[0:C, b2, 0:256].rearrange("p (a b) -> p a b", a=2)[:, :, 0:C]), reads=psk(b2), writes=[("PT", pi)])
                        yield
                        b3 = ps_next()
                        for e_ in range(2):
                            S.pe(lambda e, e_=e_, b3=b3, C=C, pi=pi, ni=ni: e.matmul(ps[0:C, b3, e_ * 128:e_ * 128 + C], lhsT=PTb[pi][0:C, e_, 0:C], rhs=Nb[ni][0:C, e_, 0:C], start=True, stop=True), reads=[("PT", pi), ("N", ni)], writes=psk(b3))
                            yield
                        S.dve(lambda e, b3=b3, C=C, ni=ni: e.tensor_tensor(out=Nb[1 - ni][0:C, :, 0:C], in0=ps[0:C, b3, 0:256].rearrange("p (a b) -> p a b", a=2)[:, :, 0:C], in1=Nb[ni][0:C, :, 0:C], op=ALU.add), reads=psk(b3) + [("N", ni)], writes=[("N", 1 - ni)])
                        yield
                        ni = 1 - ni
                        Pprev, PTprev = Pb[pi], PTb[pi]
                        Pkeys_prev = [("P", pi)]
                        PTkeys_prev = [("PT", pi)]

                else:
                    def _ev2(b, C=C):
                        return ps[0:C, b, 0:256].rearrange("p (a b) -> p a b", a=2)[:, :, 0:C]
                    idb = ident_f[0:C, 0:C].unsqueeze(1).to_broadcast([C, 2, C])
                    S.dve(lambda e: e.tensor_tensor(out=Pb[0][:, :, :], in0=U[:, :, :], in1=bm[:, 0, :].unsqueeze(1).to_broadcast([128, 2, 128]), op=ALU.mult), reads=[("U", 0), ("U", 1), "bm"], writes=[("P", 0)])
                    yield
                    S.dve(lambda e: e.tensor_tensor(out=PTb[0][:, :, :], in0=UT[:, :, :], in1=bm[:, 0, :].unsqueeze(1).to_broadcast([128, 2, 128]), op=ALU.mult), reads=["UT", "bm"], writes=[("PT", 0)])
                    yield
                    S.dve(lambda e, idb=idb: e.tensor_tensor(out=Nb[0][:, :, :], in0=idb, in1=Pb[0][:, :, :], op=ALU.subtract), reads=[("P", 0), "ident_f"], writes=[("N", 0)])
                    yield
                    S.dve(lambda e, idb=idb: e.tensor_tensor(out=NTb[0][:, :, :], in0=idb, in1=PTb[0][:, :, :], op=ALU.subtract), reads=[("PT", 0), "ident_f"], writes=[("NT", 0)])
                    yield
                    for mi in range(3):
                        S.pool(lambda e, mi=mi: e.tensor_tensor(out=UoM[mi][:, :, :], in0=U[:, :, :], in1=bm[:, mi + 1, :].unsqueeze(1).to_broadcast([128, 2, 128]), op=ALU.mult), reads=[("U", 0), ("U", 1), "bm"], writes=[("UoM", mi)])
                        yield
                        S.pool(lambda e, mi=mi: e.tensor_tensor(out=UoTM[mi][:, :, :], in0=UT[:, :, :], in1=bm[:, mi + 1, :].unsqueeze(1).to_broadcast([128, 2, 128]), op=ALU.mult), reads=["UT", "bm"], writes=[("UoTM", mi)])
                        yield
                    ni = 0
                    pprev = 0
                    for lv in range(1, 4):
                        pi = lv % 2
                        b1 = ps_next(); b2 = ps_next(); b3 = ps_next(); b4 = ps_next()
                        for e_ in range(2):
                            S.pe(lambda e, e_=e_, b1=b1, pprev=pprev: e.matmul(ps[:, b1, e_ * 128:(e_ + 1) * 128], lhsT=PTb[pprev][:, e_, :], rhs=Pb[pprev][:, e_, :], start=True, stop=True), reads=[("P", pprev), ("PT", pprev)], writes=psk(b1))
                            yield
                        for e_ in range(2):
                            S.pe(lambda e, e_=e_, b2=b2, pprev=pprev: e.matmul(ps[:, b2, e_ * 128:(e_ + 1) * 128], lhsT=Pb[pprev][:, e_, :], rhs=PTb[pprev][:, e_, :], start=True, stop=True), reads=[("P", pprev), ("PT", pprev)], writes=psk(b2))
                            yield
                        S.act(lambda e, b1=b1, pi=pi: e.copy(out=Pb[pi][:, :, :], in_=_ev2(b1)), reads=psk(b1), writes=[("P", pi)])
                        yield
                        S.act(lambda e, b2=b2, pi=pi: e.copy(out=PTb[pi][:, :, :], in_=_ev2(b2)), reads=psk(b2), writes=[("PT", pi)])
                        yield
                        for e_ in range(2):
                            S.pe(lambda e, e_=e_, b3=b3, pi=pi, ni=ni: e.matmul(ps[:, b3, e_ * 128:(e_ + 1) * 128], lhsT=PTb[pi][:, e_, :], rhs=Nb[ni][:, e_, :], start=True, stop=True), reads=[("PT", pi), ("N", ni)], writes=psk(b3))
                            yield
                        for e_ in range(2):
                            S.pe(lambda e, e_=e_, b4=b4, pi=pi, ni=ni: e.matmul(ps[:, b4, e_ * 128:(e_ + 1) * 128], lhsT=Pb[pi][:, e_, :], rhs=NTb[ni][:, e_, :], start=True, stop=True), reads=[("P", pi), ("NT", ni)], writes=psk(b4))
                            yield
                        S.dve(lambda e, b3=b3, ni=ni: e.tensor_tensor(out=Nb[1 - ni][:, :, :], in0=_ev2(b3), in1=Nb[ni][:, :, :], op=ALU.add), reads=psk(b3) + [("N", ni)], writes=[("N", 1 - ni)])
                        yield
                        S.dve(lambda e, b4=b4, ni=ni: e.tensor_tensor(out=NTb[1 - ni][:, :, :], in0=_ev2(b4), in1=NTb[ni][:, :, :], op=ALU.add), reads=psk(b4) + [("NT", ni)], writes=[("NT", 1 - ni)])
                        yield
                        ni = 1 - ni
                        pprev = pi
                    for mi in range(3):
                        lastm = (mi == 2)
                        b1 = ps_next()
                        for e_ in range(2):
                            S.pe(lambda e, e_=e_, b1=b1, ni=ni, mi=mi: e.matmul(ps[:, b1, e_ * 128:(e_ + 1) * 128], lhsT=UoTM[mi][:, e_, :], rhs=Nb[ni][:, e_, :], start=True, stop=True), reads=[("UoTM", mi), ("N", ni)], writes=psk(b1))
                            yield
                        S.act(lambda e, b1=b1: e.copy(out=Pb[1][:, :, :], in_=_ev2(b1)), reads=psk(b1), writes=[("P", 1)])
                        yield
                        if not lastm:
                            b3 = ps_next()
                            for e_ in range(2):
                                S.pe(lambda e, e_=e_, b3=b3, ni=ni, mi=mi: e.matmul(ps[:, b3, e_ * 128:(e_ + 1) * 128], lhsT=UoM[mi][:, e_, :], rhs=NTb[ni][:, e_, :], start=True, stop=True), reads=[("UoM", mi), ("NT", ni)], writes=psk(b3))
                                yield
                            S.act(lambda e, b3=b3: e.copy(out=PTb[1][:, :, :], in_=_ev2(b3)), reads=psk(b3), writes=[("PT", 1)])
                            yield
                        b2 = ps_next()
                        for e_ in range(2):
                            S.pe(lambda e, e_=e_, b2=b2, ni=ni: e.matmul(ps[:, b2, e_ * 128:(e_ + 1) * 128], lhsT=NTb[ni][:, e_, :], rhs=Pb[1][:, e_, :], start=True, stop=True), reads=[("NT", ni), ("P", 1)], writes=psk(b2))
                            yield
                        S.dve(lambda e, b2=b2, ni=ni, lastm=lastm: e.tensor_tensor(out=(Nfin[pb] if lastm else Nb[1 - ni])[:, :, :], in0=Nb[ni][:, :, :], in1=_ev2(b2), op=ALU.subtract), reads=psk(b2) + [("N", ni)], writes=[(("Nfin", pb) if lastm else ("N", 1 - ni))])
                        yield
                        if not lastm:
                            b4 = ps_next()
                            for e_ in range(2):
                                S.pe(lambda e, e_=e_, b4=b4, ni=ni: e.matmul(ps[:, b4, e_ * 128:(e_ + 1) * 128], lhsT=Nb[ni][:, e_, :], rhs=PTb[1][:, e_, :], start=True, stop=True), reads=[("N", ni), ("PT", 1)], writes=psk(b4))
                                yield
                            S.dve(lambda e, b4=b4, ni=ni: e.tensor_tensor(out=NTb[1 - ni][:, :, :], in0=NTb[ni][:, :, :], in1=_ev2(b4), op=ALU.subtract), reads=psk(b4) + [("NT", ni)], writes=[("NT", 1 - ni)])
                            yield
                        ni = 1 - ni
                if isS:
                    S.pool(lambda e, ni=ni, C=C: e.tensor_copy(out=Nfin[pb][0:C, :, 0:C], in_=Nb[ni][0:C, :, 0:C]), reads=[("N", ni)], writes=[("Nfin", pb)])
                    yield
                Nf = Nfin[pb]
                Nkey = ("Nfin", pb)
                if not isS:
                    S.pool(lambda e, C=C: e.tensor_copy(out=eGl[pb][:, 0:2], in_=E2[:, :, C - 1:C].rearrange("p a b -> p (a b)")), reads=["E2"], writes=[("eGl", pb)])
                    yield
                yield "SPLIT"
                if not isS:
                    if first:
                        S.act(lambda e, C=C: e.copy(out=xv[0:C].rearrange("p a b -> p (a b)"), in_=vtok2[pb][0:C].rearrange("p a b -> p (a b)")), reads=[("vtok", pb)], writes=["xv"])
                        yield
                    else:
                        pk_ = ps_next()
                        S.pe(lambda e, pk_=pk_, c0=c0, C=C: e.matmul(ps[0:C, pk_, 0:256], lhsT=kn[:, c0:c0 + C], rhs=Sbf[:].rearrange("p a b -> p (a b)"), start=True, stop=True), reads=[("n", 1), "Sbf"], writes=psk(pk_))
                        yield
                        for e_ in range(2):
                            hv = hv0 + e_
                            S.dve(lambda e, e_=e_, hv=hv, pk_=pk_, C=C, tl=tl: e.scalar_tensor_tensor(out=xv[0:C, e_, :], in0=ps[0:C, pk_, e_ * 128:(e_ + 1) * 128], scalar=negeG_all[0:C, tl, hv:hv + 1], in1=vtok2[pb][0:C, e_, :], op0=ALU.mult, op1=ALU.add), reads=psk(pk_) + [("vtok", pb)], writes=["xv"])
                            yield
                else:
                    S.dve(lambda e: e.tensor_tensor(out=X[:, :, :], in0=knf[:, :].unsqueeze(1).to_broadcast([128, NSEQ, TS]), in1=segmask[:, :, :], op=ALU.mult), reads=[("nf", 1), "segmask"], writes=["X"])
                    yield
                    pks = [ps_next(), ps_next()]
                    K.P.reserved = set(pks)
                    def _ldA(n):
                        k_ = n % 8
                        S.dma("sp", "gsin%d" % k_, lambda e, k_=k_, s2=n % NSEQ, hv2=hv0 + n // NSEQ: e.dma_start(out=SinR[k_], in_=K.s_gdn[s2, hv2]), writes=[("SinR", k_)])
                    for n_ in range(RDEPTH):
                        _ldA(n_)
                        yield
                    for e_ in range(2):
                        hv = hv0 + e_
                        for s_ in range(NSEQ):
                            n_ = e_ * NSEQ + s_
                            k_ = n_ % 8
                            S.pe(lambda e, e_=e_, s_=s_, k_=k_, pks=pks: e.matmul(ps[0:TS, pks[e_], 0:128], lhsT=X[:, s_, :], rhs=SinR[k_], start=(s_ == 0), stop=(s_ == NSEQ - 1)), reads=["X", ("SinR", k_)], writes=psk(pks[e_]))
                            yield
                            if n_ + RDEPTH < NLD:
                                _ldA(n_ + RDEPTH)
                                yield
                        S.dve(lambda e, e_=e_, hv=hv, tl=tl, pks=pks: e.scalar_tensor_tensor(out=xv[0:TS, e_, :], in0=ps[0:TS, pks[e_], 0:128], scalar=negeG_all[0:TS, tl, hv:hv + 1], in1=vtok2[pb][0:TS, e_, :], op0=ALU.mult, op1=ALU.add), reads=psk(pks[e_]) + [("vtok", pb)], writes=["xv"])
                        yield
                    K.P.reserved = set()
                pv = ps_next()
                for e_ in range(2):
                    S.pe(lambda e, e_=e_, pv=pv, C=C, Nf=Nf: e.matmul(ps[0:C, pv, e_ * 128:(e_ + 1) * 128], lhsT=Nf[0:C, e_, 0:C], rhs=xv[0:C, e_, :], start=True, stop=True), reads=[Nkey, "xv"], writes=psk(pv))
                    yield
                for e_ in range(2):
                    hv = hv0 + e_
                    S.act(lambda e, e_=e_, hv=hv, pv=pv, C=C, tl=tl: e.activation(out=vnew[0:C, e_, :], in_=ps[0:C, pv, e_ * 128:(e_ + 1) * 128], func=AF.Identity, scale=beta_all[0:C, tl, hv:hv + 1]), reads=psk(pv), writes=[("vnew", e_)])
                    yield
                if not isS:
                    po = ps_next()
                    po_aps = [ps[0:C, po, 0:128], ps[0:C, po, 128:256]]
                    po_keys = [psk(po), psk(po)]
                    for e_ in range(2):
                        if not first:
                            S.pe(lambda e, e_=e_, C=C, po_aps=po_aps: e.matmul(po_aps[e_], lhsT=qg2[pb][:, e_, 0:C], rhs=Sbf[:, e_, :], start=True, stop=False), reads=[("qg", e_, pb), "Sbf"], writes=po_keys[e_])
                            yield
                        S.pe(lambda e, e_=e_, C=C, po_aps=po_aps, first=first: e.matmul(po_aps[e_], lhsT=attnT2[pb][0:C, e_, 0:C], rhs=vnew[0:C, e_, :], start=first, stop=True), reads=[("attnT", e_, pb), ("vnew", e_)], writes=po_keys[e_])
                        yield
                    pS_ = ps_next()
                    for e_ in range(2):
                        S.pe(lambda e, e_=e_, pS_=pS_, C=C: e.matmul(ps[:, pS_, e_ * 128:(e_ + 1) * 128], lhsT=kd2[pb][0:C, e_, :], rhs=vnew[0:C, e_, :], start=True, stop=True), reads=[("kd", e_, pb), ("vnew", e_)], writes=psk(pS_))
                        yield
                    for e_ in range(2):
                        hv = hv0 + e_
                        if first:
                            S.dve(lambda e, e_=e_, pS_=pS_: e.tensor_copy(out=S32[:, e_, :], in_=ps[:, pS_, e_ * 128:(e_ + 1) * 128]), reads=psk(pS_), writes=[("S32", e_)])
                            yield
                        else:
                            S.dve(lambda e, e_=e_, pS_=pS_, C=C: e.scalar_tensor_tensor(out=S32[:, e_, :], in0=S32[:, e_, :], scalar=eGl[pb][:, e_:e_ + 1], in1=ps[:, pS_, e_ * 128:(e_ + 1) * 128], op0=ALU.mult, op1=ALU.add), reads=psk(pS_) + [("S32", e_), ("eGl", pb)], writes=[("S32", e_)])
                            yield
                        if last:
                            S.dma("sp", "gpo", lambda e, e_=e_, hv=hv: e.dma_start(out=K.gdn_p[hv], in_=S32[:, e_, :]), reads=[("S32", e_)])
                            yield
                    if not last:
                        S.act(lambda e: e.copy(out=Sbf[:].rearrange("p a b -> p (a b)"), in_=S32[:].rearrange("p a b -> p (a b)")), reads=[("S32", 0), ("S32", 1)], writes=["Sbf"])
                        yield
                else:
                    pos = [ps_next(), ps_next()]
                    K.P.reserved = set(pos)
                    po_aps = [ps[0:TS, pos[0], 0:128], ps[0:TS, pos[1], 0:128]]
                    po_keys = [psk(pos[0]), psk(pos[1])]
                    def _ldB(n):
                        k_ = n % 8
                        S.dma("sp", "gsin%d" % k_, lambda e, k_=k_, s2=n % NSEQ, hv2=hv0 + n // NSEQ: e.dma_start(out=SinR[k_], in_=K.s_gdn[s2, hv2]), writes=[("SinR", k_)])
                    for n_ in range(RDEPTH):
                        _ldB(n_)
                        yield
                    for e_ in range(2):
                        hv = hv0 + e_
                        S.pe(lambda e, e_=e_, po_aps=po_aps: e.matmul(po_aps[e_], lhsT=attnT2[pb][0:TS, e_, 0:TS], rhs=vnew[0:TS, e_, :], start=True, stop=False), reads=[("attnT", e_, pb), ("vnew", e_)], writes=po_keys[e_])
                        yield
                        S.pool(lambda e, e_=e_: e.tensor_tensor(out=qgf[:, :], in0=qnf[:, :], in1=E2[:, e_, 0:TS], op=ALU.mult), reads=[("nf", 0), "E2"], writes=["qgf"])
                        yield
                        S.dve(lambda e: e.tensor_tensor(out=X[:, :, :], in0=qgf[:, :].unsqueeze(1).to_broadcast([128, NSEQ, TS]), in1=segmask[:, :, :], op=ALU.mult), reads=["qgf", "segmask"], writes=["X"])
                        yield
                        for s_ in range(NSEQ):
                            n_ = e_ * NSEQ + s_
                            k_ = n_ % 8
                            j_ = n_ % 3
                            S.pe(lambda e, e_=e_, s_=s_, k_=k_, po_aps=po_aps: e.matmul(po_aps[e_], lhsT=X[:, s_, :], rhs=SinR[k_], start=False, stop=(s_ == NSEQ - 1)), reads=["X", ("SinR", k_)], writes=po_keys[e_])
                            yield
                            S.dve(lambda e, e_=e_, s_=s_: e.tensor_scalar(out=kdm[0:TS, :], in0=kd2[pb][0:TS, e_, :], scalar1=segcol[0:TS, s_:s_ + 1], scalar2=None, op0=ALU.mult), reads=[("kd", e_, pb), "segcol"], writes=["kdm"])
                            yield
                            pS_ = ps_next()
                            S.pe(lambda e, e_=e_, pS_=pS_: e.matmul(ps[:, pS_, 0:128], lhsT=kdm[0:TS, :], rhs=vnew[0:TS, e_, :], start=True, stop=True), reads=["kdm", ("vnew", e_)], writes=psk(pS_))
                            yield
                            S.dve(lambda e, e_=e_, s_=s_, k_=k_, j_=j_, pS_=pS_: e.scalar_tensor_tensor(out=SoutR[j_], in0=SinR[k_], scalar=E2[:, e_, 4 * s_ + 3:4 * s_ + 4], in1=ps[:, pS_, 0:128], op0=ALU.mult, op1=ALU.add), reads=psk(pS_) + [("SinR", k_), "E2"], writes=[("SoutR", j_)])
                            yield
                            if n_ + RDEPTH < NLD:
                                _ldB(n_ + RDEPTH)
                                yield
                            S.dma("sp", "gsout%d" % j_, lambda e, s_=s_, hv=hv, j_=j_: e.dma_start(out=K.gdn_s[s_, hv], in_=SoutR[j_]), reads=[("SoutR", j_)])
                            yield
                    K.P.reserved = set()
                for e_ in range(2):
                    _f = lambda e, e_=e_, C=C, po_aps=po_aps: e.activation(out=acc[0:C, e_ * 128:(e_ + 1) * 128], in_=po_aps[e_], func=AF.Square, accum_out=ss[0:C, e_:e_ + 1])
                    _f._multi = True
                    S.act(_f, reads=po_keys[e_], writes=[("acc", 0), ("ss", e_)])
                    yield
                S.dve(lambda e, C=C: e.tensor_scalar(out=ss[0:C, 2:4], in0=ss[0:C, 0:2], scalar1=1.0 / 128, scalar2=EPS, op0=ALU.mult, op1=ALU.add), reads=[("ss", 0), ("ss", 1)], writes=["ss2"])
                yield
                S.pool(lambda e, C=C: e.tensor_tensor(out=ss[0:C, 2:4], in0=ss[0:C, 2:4], in1=ss[0:C, 4:6], op=ALU.pow), reads=["ss2", "mhalf"], writes=["ss2"])
                yield
                for e_ in range(2):
                    S.dve(lambda e, e_=e_, C=C, po_aps=po_aps: e.scalar_tensor_tensor(out=og[0:C, e_ * 128:(e_ + 1) * 128], in0=po_aps[e_], scalar=ss[0:C, 2 + e_:3 + e_], in1=szn2[pb][0:C, e_ * 128:(e_ + 1) * 128], op0=ALU.mult, op1=ALU.mult), reads=po_keys[e_] + ["ss2", ("szn", pb)], writes=[("og", e_)])
                    yield
                bt = ps_next()
                ptb = ps[:, bt, :].bitcast(BF16)
                for e_ in range(2):
                    S.pe(lambda e, e_=e_, ptb=ptb, C=C: e.transpose(ptb[:, e_ * C:(e_ + 1) * C], og[0:C, e_ * 128:(e_ + 1) * 128], ident_b[0:C, 0:C]), reads=[("og", e_), "ident_b"], writes=psk(bt))
                    yield
                S.act(lambda e, ptb=ptb, C=C, c0=c0: e.copy(out=ogT[:, :, c0:c0 + C], in_=ptb[:, 0:2 * C].rearrange("p (a b) -> p a b", a=2)), reads=psk(bt), writes=["ogT"])
                yield
            chunks_ = list(range(0, nt, C))
            gens = [chunk_gen(c0_, i_ % 2) for i_, c0_ in enumerate(chunks_)]
            PIPE = (not isS) and os.environ.get("GDN_PIPE", "1") == "1"

            def _adv_split(g):
                for x_ in g:
                    if x_ == "SPLIT":
                        return
            if not PIPE:
                for g in gens:
                    for _ in g:
                        pass
            else:
                _adv_split(gens[0])
                for i_ in range(len(gens)):
                    nxt = gens[i_ + 1] if i_ + 1 < len(gens) else None
                    a_done = False
                    b_done = nxt is None
                    while not (a_done and b_done):
                        if not a_done:
                            try:
                                next(gens[i_])
                            except StopIteration:
                                a_done = True
                        if not b_done:
                            try:
                                if next(nxt) == "SPLIT":
                                    b_done = True
                            except StopIteration:
                                b_done = True
            for dc8 in range(KC):
                if hk >= 1 and 'F' in CUT:
                    continue
                b = ps_next()
                for e_ in range(2):
                    S.pe(lambda e, b=b, e_=e_, dc8=dc8, nt=nt: e.matmul(ps[:, b, 0:nt], lhsT=wo[:, e_, dc8 * 128:(dc8 + 1) * 128], rhs=ogT[:, e_, 0:nt], start=(e_ == 0), stop=(e_ == 1)), reads=[wok, "ogT"], writes=psk(b))
                K.resid_update(ps[:, b, 0:nt], gbase, dc8, ("S" if isS else bi), t0, nt, psk(b))
        if hk >= 1 and 'G' in CUT:
            S.barrier(); continue
        bo1 = ps_next()
        for ci in range(4):
            S.pe(lambda e, ci=ci, bo1=bo1: e.transpose(ps[0:3, bo1, ci * 128:(ci + 1) * 128], tailP[:, ci, :], ident_f[:]), reads=["tailP", "ident_f"], writes=psk(bo1))
        S.dve(lambda e, bo1=bo1: e.tensor_copy(out=ostg[0:3, :], in_=ps[0:3, bo1, :]), reads=psk(bo1), writes=[("acc", 0)])
        for ci, cid in enumerate(cids):
            S.dma("sp", "gco", lambda e, ci=ci, cid=cid: e.dma_start(out=K.gconv_p[:, cid * 128:(cid + 1) * 128], in_=ostg[0:3, ci * 128:(ci + 1) * 128]), reads=[("acc", 0)])
        bo2 = ps_next()
        for ci in range(4):
            S.pe(lambda e, ci=ci, bo2=bo2: e.transpose(ps[0:48, bo2, ci * 128:(ci + 1) * 128], tailS[:, ci, :, :].rearrange("p s r -> p (s r)"), ident_f[:]), reads=["tailS", "ident_f"], writes=psk(bo2))
        S.dve(lambda e, bo2=bo2: e.tensor_copy(out=ostg[0:48, :], in_=ps[0:48, bo2, :]), reads=psk(bo2), writes=[("acc", 0)])
        for ci, cid in enumerate(cids):
            S.dma("sp", "gco", lambda e, ci=ci, cid=cid: e.dma_start(out=K.gconv_s[:, cid * 128:(cid + 1) * 128], in_=ostg[0:48, ci * 128:(ci + 1) * 128]), reads=[("acc", 0)])
        S.barrier()
    S.barrier()

_CACHE = {}


def make_in_maps(inp):
    f = lambda a: np.ascontiguousarray(np.asarray(a, dtype=np.float32))
    consts = host_consts()
    shared = {
        "w_ada": f(inp["w_ada"]), "b_ada": f(inp["b_ada"]),
        "w_ada_final": f(inp["w_ada_final"]), "b_ada_final": f(inp["b_ada_final"]).reshape(1, -1),
        "w_ret_in": f(inp["w_ret_in"][0]), "w_ret_out": f(inp["w_ret_out"][0]),
        "w_gdn_in": f(inp["w_gdn_in"][0]), "w_gdn_out": f(inp["w_gdn_out"][0]),
        "w_gdn_conv": f(inp["w_gdn_conv"][0]),
        "gdn_vec": f(np.concatenate([inp["gdn_a_log"][0], inp["gdn_dt_bias"][0]])).reshape(1, 32),
        "gdn_norm": f(inp["gdn_norm"]).reshape(1, 128),
        "w_ffn_up": f(inp["w_ffn_up"]), "w_ffn_down": f(inp["w_ffn_down"]),
        "ffn_vec": f(np.concatenate([np.asarray(inp["w_ffn_dw"]).reshape(6, DFF), np.asarray(inp["b_ffn_dw"]).reshape(2, DFF)], axis=0)),
    }
    shared.update(consts)
    maps = []
    for c in range(NCORES):
        sl = slice(NSEQ * c, NSEQ * (c + 1))
        m = dict(shared)
        m["xp"] = f(inp["x_prompt"][c])
        m["xs"] = f(np.asarray(inp["x_sample"][sl]).reshape(TS, D))
        m["s_ret"] = f(inp["state_ret"][0, sl])
        m["s_gdn"] = f(inp["state_gdn"][0, sl])
        m["s_gconv"] = f(np.asarray(inp["state_gdn_conv"][0, sl]).reshape(NSEQ * 3, 4096))
        m["s_fconv"] = f(np.asarray(inp["state_ffn_conv"][:, sl]).reshape(2, NSEQ * 2, DFF))
        m["vec22"] = f(np.concatenate([np.asarray(inp["c_prompt"][c:c + 1]), np.asarray(inp["c_sample"][sl]),
                                        np.asarray(inp["norm_mix"]), np.asarray(inp["norm_ffn"]), np.asarray(inp["norm_final"]).reshape(1, D)], axis=0))
        maps.append(m)
    return maps


def kernel(**inp):
    if "nc" not in _CACHE:
        _CACHE["nc"] = build_program()
    nc = _CACHE["nc"]
    maps = make_in_maps(inp)
    res = run_bass_kernel_spmd(nc, maps, core_ids=list(range(NCORES)))
    R = res.results
    cat = lambda k: np.stack([np.asarray(R[c][k]) for c in range(NCORES)], axis=0)
    y_prompt = cat("y_p")
    y_sample = cat("y_s").reshape(128, 4, D)
    ret_p = cat("ret_p")[None]
    gdn_p = cat("gdn_p")[None]
    gconv_p = cat("gconv_p")[None]
    fconv_p = np.transpose(cat("fconv_p"), (1, 0, 2, 3))
    ret_s = cat("ret_s").reshape(1, 128, RET_H, 256, 512)
    gdn_s = cat("gdn_s").reshape(1, 128, GDN_HV, 128, 128)
    gconv_s = cat("gconv_s").reshape(1, 128, 3, 4096)
    fconv_s = np.transpose(cat("fconv_s").reshape(NCORES, 2, NSEQ, 2, DFF), (1, 0, 2, 3, 4)).reshape(2, 128, 2, DFF)
    return (y_prompt.astype(np.float32), y_sample.astype(np.float32), ret_p.astype(np.float32), gdn_p.astype(np.float32),
            gconv_p.astype(np.float32), fconv_p.astype(np.float32), ret_s.astype(np.float32), gdn_s.astype(np.float32),
            gconv_s.astype(np.float32), fconv_s.astype(np.float32))
```

```python
import math
import bisect
import numpy as np
from contextlib import ExitStack
import concourse.bass as bass
import concourse.mybir as mybir
from concourse.bass_utils import run_bass_kernel_spmd

F32 = mybir.dt.float32
BF16 = mybir.dt.bfloat16
AF = mybir.ActivationFunctionType
ALU = mybir.AluOpType

NCORES = 8
D = 1024
KC = 8
TP = 2048
NSEQ = 16
TS = 64
TT = TP + TS
DFF = 2816
NF = 22
EPS = 1e-6
RET_H = 4
GDN_HV = 16
GDN_HK = 8
PAST = 16384
SAME_ENGINE_SYNC = True
ATTACH_WAITS = True
DEBUG = {}


class _Op:
    __slots__ = ("eng", "fn", "deps", "dma_key", "sig", "cnt", "idx")

    def __init__(self, eng, fn, deps, dma_key, idx):
        self.eng = eng
        self.fn = fn
        self.deps = deps
        self.dma_key = dma_key
        self.sig = False
        self.cnt = 0
        self.idx = idx


class Sched:
    ENGS = ("pe", "act", "dve", "pool", "sp")

    def __init__(self, nc):
        self.nc = nc
        self.ops = []
        self.last_w = {}
        self.readers = {}
        self.last_eng = {}
        self.dmas_since_bar = []

    def op(self, eng, fn, reads=(), writes=(), dma_key=None, extra=()):
        deps = set(extra)
        for r in reads:
            w = self.last_w.get(r)
            if w is not None:
                deps.add(w)
        for w_ in writes:
            w = self.last_w.get(w_)
            if w is not None:
                deps.add(w)
            deps |= self.readers.get(w_, set())
        idx = len(self.ops)
        deps.discard(idx)
        self.ops.append(_Op(eng, fn, deps, dma_key, idx))
        for r in reads:
            self.readers.setdefault(r, set()).add(idx)
        for w_ in writes:
            self.last_w[w_] = idx
            self.readers[w_] = set()
        self.last_eng[eng] = idx
        if dma_key is not None:
            self.dmas_since_bar.append(idx)
        return idx

    def pe(self, fn, reads=(), writes=()):
        return self.op("pe", fn, reads, writes)

    def act(self, fn, reads=(), writes=()):
        return self.op("act", fn, reads, writes)

    def dve(self, fn, reads=(), writes=()):
        return self.op("dve", fn, reads, writes)

    def pool(self, fn, reads=(), writes=()):
        return self.op("pool", fn, reads, writes)

    def dma(self, q, key, fn, reads=(), writes=()):
        return self.op(q, fn, reads, writes, dma_key=key)

    def barrier(self):
        deps = set(self.last_eng.values()) | set(self.dmas_since_bar)
        self.dmas_since_bar = []
        for e in self.ENGS:
            self.op(e, None, extra=deps)
        self.last_w = {}
        self.readers = {}

    def emit(self, stack):
        nc = self.nc
        ops = self.ops

        def needs(c, p):
            if p.fn is None:
                return False
            if p.dma_key is not None:
                return True
            if p.eng == c.eng:
                if p.eng == "pe":
                    return False
                return SAME_ENGINE_SYNC
            return True

        for c in ops:
            for d in c.deps:
                p = ops[d]
                if needs(c, p):
                    p.sig = True
        eng_cnt = {e: 0 for e in self.ENGS}
        dma_cnt = {}
        dma_keys = []
        dma_issue_idx = {}
        for o in ops:
            if o.dma_key is not None:
                if o.dma_key not in dma_cnt:
                    dma_cnt[o.dma_key] = 0
                    dma_keys.append(o.dma_key)
                    dma_issue_idx[o.dma_key] = []
                dma_cnt[o.dma_key] += 1
                dma_issue_idx[o.dma_key].append(o.idx)
            elif o.sig:
                eng_cnt[o.eng] += 1
                o.cnt = eng_cnt[o.eng]
        sems = {}
        for e in self.ENGS:
            sems[("e", e)] = stack.enter_context(nc.semaphore("s_" + e))
        for k in dma_keys:
            sems[("d", k)] = stack.enter_context(nc.semaphore("d_" + str(k)))
        per_eng = {e: [o for o in ops if o.eng == e] for e in self.ENGS}
        block = stack.enter_context(nc.Block())

        def run_engine(ename, eobj):
            waited = {}
            for o in per_eng[ename]:
                need = {}
                for d in o.deps:
                    p = ops[d]
                    if not needs(o, p):
                        continue
                    if p.dma_key is not None:
                        key = ("d", p.dma_key)
                        val = 16 * bisect.bisect_left(dma_issue_idx[p.dma_key], o.idx)
                    else:
                        key = ("e", p.eng)
                        val = p.cnt
                    if need.get(key, 0) < val:
                        need[key] = val
                pend = [(key, val) for key, val in need.items() if waited.get(key, 0) < val]
                for key, val in pend:
                    waited[key] = val
                fuse = (ATTACH_WAITS and o.fn is not None and o.dma_key is None and ename in ("act", "dve", "pool", "pe")
                        and not getattr(o.fn, "_multi", False) and len(pend) > 0)
                for key, val in (pend[:-1] if fuse else pend):
                    eobj.wait_ge(sems[key], val)
                if o.fn is None:
                    continue
                ins = o.fn(eobj)
                if fuse:
                    ins._wait_ge(sems[pend[-1][0]], pend[-1][1])
                if o.dma_key is not None:
                    ins.then_inc(sems[("d", o.dma_key)], 16)
                elif o.sig:
                    ins.then_inc(sems[("e", ename)], 1)
            for k in dma_keys:
                if any(o.dma_key == k for o in per_eng[ename]):
                    eobj.wait_ge(sems[("d", k)], 16 * dma_cnt[k])

        @block.tensor
        def _(e):
            run_engine("pe", e)

        @block.scalar
        def _(e):
            run_engine("act", e)

        @block.vector
        def _(e):
            run_engine("dve", e)

        @block.gpsimd
        def _(e):
            run_engine("pool", e)

        @block.sync
        def _(e):
            run_engine("sp", e)


def _gammas():
    return (1.0 - 2.0 ** (-5.0 - np.arange(RET_H, dtype=np.float64)))


def host_consts():
    c = {}
    c["c_ident"] = np.eye(128, dtype=np.float32)
    half = 128
    inv_freq = (np.float32(10000.0) ** (-(np.arange(half, dtype=np.float32)) / np.float32(half))).astype(np.float32)
    pos = np.concatenate([np.arange(TP, dtype=np.float32), (PAST + (np.arange(TS) % 4)).astype(np.float32)])
    ang = (pos[None, :] * inv_freq[:, None]).astype(np.float32)
    cos = np.cos(ang.astype(np.float64))
    sin = np.sin(ang.astype(np.float64))
    g = _gammas()
    pin = np.concatenate([np.arange(TP) % 128, np.arange(TS) % 4]).astype(np.float64)
    rope = np.zeros((RET_H + 1, 2, 128, TT), np.float32)
    for h in range(RET_H):
        dec = g[h] ** (pin + 1.0)
        rope[h, 0] = cos * dec[None, :]
        rope[h, 1] = sin * dec[None, :]
    rope[RET_H, 0] = cos * (256.0 ** -0.5)
    rope[RET_H, 1] = sin * (256.0 ** -0.5)
    c["rope"] = rope
    mP = np.zeros((RET_H, 128, 128), np.float32)
    mS = np.zeros((RET_H, 128, 128), np.float32)
    ks = np.zeros((128, 2 * RET_H), np.float32)
    jj = np.arange(128)
    for h in range(RET_H):
        mP[h] = np.where(jj[None, :] >= jj[:, None], g[h] ** (-(jj[:, None] + 1.0)), 0.0)
        j4 = jj[:64] % 4
        same = (jj[:64, None] // 4) == (jj[None, :64] // 4)
        mS[h, :64, :64] = np.where(same & (jj[None, :64] >= jj[:64, None]), g[h] ** (-(j4[:, None] + 1.0)), 0.0)
        ks[:, h] = g[h] ** (127.0 - jj)
        ks[:64, 4 + h] = g[h] ** (3.0 - j4)
    c["retmask"] = np.concatenate([mP, mS], axis=0)
    c["kscale"] = ks
    seg = np.zeros((128, NSEQ, 64), np.float32)
    for s in range(NSEQ):
        seg[:, s, 4 * s:4 * s + 4] = 1.0
    c["segmask"] = seg.reshape(128, NSEQ * 64)
    segcol = np.zeros((128, NSEQ), np.float32)
    for s in range(NSEQ):
        segcol[4 * s:4 * s + 4, s] = 1.0
    c["segcol"] = segcol
    gm = np.zeros((8, 128, 128), np.float32)
    a = np.arange(128)
    gm[0] = (a[:, None] <= a[None, :])
    gm[1] = (a[:, None] > a[None, :])
    gm[2] = (a[None, :] >= a[:, None])
    gm[3] = (a[None, :] > a[:, None])
    b = np.arange(64)
    same = (b[:, None] // 4) == (b[None, :] // 4)
    gm[4, :64, :64] = (b[:, None] <= b[None, :]) & same
    gm[5, :64, :64] = (b[:, None] > b[None, :]) & same
    gm[6, :64, :64] = (b[None, :] >= b[:, None]) & same
    gm[7, :64, :64] = (b[None, :] > b[:, None]) & same
    c["gmask"] = np.ascontiguousarray(gm.transpose(1, 0, 2)).reshape(128, 8 * 128)
    bmk = np.zeros((4, 128, 128), np.float32)
    bmk[0] = (a[:, None] // 16) == (a[None, :] // 16)
    for mi, m in enumerate([32, 64, 128]):
        bmk[mi + 1] = ((a[:, None] // m) == (a[None, :] // m)) & ((a[:, None] // (m // 2)) != (a[None, :] // (m // 2)))
    c["bmask"] = np.ascontiguousarray(bmk.transpose(1, 0, 2)).reshape(128, 4 * 128)
    return c


class Ctx:
    pass


def build_program(dbg=()):
    nc = bass.Bass("TRN2", target_bir_lowering=False)
    K = Ctx()
    K.nc = nc
    ins = {}

    def din(name, shape):
        ins[name] = nc.dram_tensor(name, list(shape), F32, kind="ExternalInput").ap()
        return ins[name]

    def dout(name, shape):
        return nc.dram_tensor(name, list(shape), F32, kind="ExternalOutput").ap()

    xp = din("xp", [TP, D]); xs = din("xs", [TS, D])
    s_ret = din("s_ret", [NSEQ, RET_H, 256, 512])
    s_gdn = din("s_gdn", [NSEQ, GDN_HV, 128, 128])
    s_gconv = din("s_gconv", [NSEQ * 3, 4096])
    s_fconv = din("s_fconv", [2, NSEQ * 2, DFF])
    vec22 = din("vec22", [22, D])
    w_ada = din("w_ada", [2, D, 6 * D]); b_ada = din("b_ada", [2, 6 * D])
    w_adaf = din("w_ada_final", [D, 2 * D]); b_adaf = din("b_ada_final", [1, 2 * D])
    w_ret_in = din("w_ret_in", [D, 6144]); w_ret_out = din("w_ret_out", [2048, D])
    w_gdn_in = din("w_gdn_in", [D, 6176]); w_gdn_out = din("w_gdn_out", [2048, D])
    w_gconv = din("w_gdn_conv", [4, 4096])
    gdn_vec = din("gdn_vec", [1, 32])
    gdn_norm = din("gdn_norm", [1, 128])
    w_up = din("w_ffn_up", [2, D, 2 * DFF]); w_down = din("w_ffn_down", [2, DFF, D])
    ffn_vec = din("ffn_vec", [8, DFF])
    c_ident = din("c_ident", [128, 128])
    c_rope = din("rope", [RET_H + 1, 2, 128, TT])
    c_retmask = din("retmask", [2 * RET_H, 128, 128])
    c_kscale = din("kscale", [128, 2 * RET_H])
    c_segmask = din("segmask", [128, NSEQ * 64])
    c_segcol = din("segcol", [128, NSEQ])
    c_gmask = din("gmask", [128, 8 * 128])
    c_bmask = din("bmask", [128, 4 * 128])

    y_p = dout("y_p", [TP, D]); y_s = dout("y_s", [TS, D])
    ret_p = dout("ret_p", [RET_H, 256, 512]); gdn_p = dout("gdn_p", [GDN_HV, 128, 128])
    gconv_p = dout("gconv_p", [3, 4096]); fconv_p = dout("fconv_p", [2, 2, DFF])
    ret_s = dout("ret_s", [NSEQ, RET_H, 256, 512]); gdn_s = dout("gdn_s", [NSEQ, GDN_HV, 128, 128])
    gconv_s = dout("gconv_s", [NSEQ * 3, 4096]); fconv_s = dout("fconv_s", [2, NSEQ * 2, DFF])
    dbg_out = {n: dout("dbg_" + n, shp) for n, shp in dbg}

    with ExitStack() as st:
        def T(name, shape, dt):
            return st.enter_context(nc.sbuf_tensor(name, list(shape), dt))

        S = Sched(nc)
        xT = T("xT", [128, KC, TT], F32)
        hT = T("hT", [128, KC, TT], BF16)
        wring = T("wring", [128, 4, 4096], BF16)
        modT = T("modT", [128, 112, 17], F32)
        vecT = T("vecT", [128, KC, 22], F32)
        Amod = T("Amod", [128, 5, KC, 17], F32)
        csT = T("csT", [128, KC, 17], F32)
        ident_f = T("ident_f", [128, 128], F32)
        ident_b = T("ident_b", [128, 128], BF16)
        ones_b = T("ones_b", [128, 128], BF16)
        ones_f = T("ones_f", [128, 32], F32)
        ffnvT = T("ffnvT", [128, NF, 8], F32)
        fcarry = T("fcarry", [128, NF, 2], F32)
        ARENA = 15000
        arena = T("arena", [128, ARENA], F32)
        ps = st.enter_context(nc.psum_tensor("ps", [128, 8, 512], F32))

        A = Ctx()
        A.off = 0

        def a_reset():
            A.off = 0

        def a_f32(*free, parts=128):
            n = int(np.prod(free))
            assert A.off + n <= ARENA, ("arena overflow", A.off, n)
            ap = arena[0:parts, A.off:A.off + n]
            A.off += n
            if len(free) == 2:
                ap = ap.rearrange("p (a b) -> p a b", a=free[0])
            elif len(free) == 3:
                ap = ap.rearrange("p (a b c) -> p a b c", a=free[0], b=free[1])
            return ap

        def a_bf16(*free, parts=128):
            n = int(np.prod(free))
            nf = (n + 1) // 2
            assert A.off + nf <= ARENA, ("arena overflow", A.off, nf)
            ap = arena[0:parts, A.off:A.off + nf].bitcast(BF16)[:, 0:n]
            A.off += nf
            if len(free) == 2:
                ap = ap.rearrange("p (a b) -> p a b", a=free[0])
            elif len(free) == 3:
                ap = ap.rearrange("p (a b c) -> p a b c", a=free[0], b=free[1])
            return ap

        P = Ctx()
        P.i = 0

        P.reserved = set()

        def ps_next(n=1):
            while True:
                if P.i + n > 8:
                    P.i = 0
                b = P.i
                P.i = (P.i + n) % 8
                if not any((b + i) in P.reserved for i in range(n)):
                    return b

        def psk(b, n=1):
            return [("ps", b + i) for i in range(n)]

        W = Ctx()
        W.i = 0

        def wslot():
            i = W.i
            W.i = (W.i + 1) % 4
            return i

        def load_w_bf16(src2d, kc, ncols, row0=0, col0=0, slot=None, off=0, key=None):
            i = wslot() if slot is None else slot
            view = wring[:, i, off:off + kc * ncols].rearrange("p (k c) -> p k c", k=kc)
            src = src2d[row0:row0 + kc * 128, col0:col0 + ncols].rearrange("(k p) c -> p k c", p=128)
            k_ = ("w", i) if key is None else key
            S.dma("pool", "w%d" % i, lambda e: e.dma_start(out=view, in_=src), writes=[k_])
            return view, k_

        def load_w_f32(src2d, kc, ncols, row0=0, col0=0):
            i = wslot()
            view = wring[:, i, :].bitcast(F32)[:, 0:kc * ncols].rearrange("p (k c) -> p k c", k=kc)
            src = src2d[row0:row0 + kc * 128, col0:col0 + ncols].rearrange("(k p) c -> p k c", p=128)
            S.dma("sp", "wf%d" % i, lambda e: e.dma_start(out=view, in_=src), writes=[("w", i)])
            return view, ("w", i)

        BLOCKS_P = [(0, 512), (512, 512), (1024, 512), (1536, 512)]
        BLOCK_S = (TP, TS)
        ALLBLOCKS = BLOCKS_P + [BLOCK_S]

        S.dma("sp", "c_id", lambda e: e.dma_start(out=ident_f[:], in_=c_ident), writes=["ident_f"])
        S.dve(lambda e: e.tensor_copy(out=ident_b[:], in_=ident_f[:]), reads=["ident_f"], writes=["ident_b"])
        S.dve(lambda e: e.memset(ones_b[:], 1.0), writes=["ones_b"])
        S.dve(lambda e: e.memset(ones_f[:], 1.0), writes=["ones_f"])
        a_reset()
        stage22 = a_f32(D, parts=22)
        S.dma("sp", "c_s22", lambda e: e.dma_start(out=stage22, in_=vec22), writes=["stage22"])
        b0 = ps_next()
        for kc in range(KC):
            S.pe(lambda e, kc=kc: e.transpose(ps[:, b0, kc * 22:(kc + 1) * 22], stage22[:, kc * 128:(kc + 1) * 128], ident_f[0:22, 0:22]),
                 reads=["stage22", "ident_f"], writes=psk(b0))
        S.dve(lambda e: e.tensor_copy(out=vecT[:].rearrange("p a b -> p (a b)"), in_=ps[:, b0, 0:KC * 22]), reads=psk(b0), writes=["vecT"])
        S.act(lambda e: e.activation(out=csT[:], in_=vecT[:, :, 0:17], func=AF.Silu), reads=["vecT"], writes=["csT"])
        stage8 = a_f32(DFF, parts=8)
        S.dma("sp", "c_s8", lambda e: e.dma_start(out=stage8, in_=ins["ffn_vec"]), writes=["stage8"])
        b1 = ps_next()
        for fc in range(NF):
            S.pe(lambda e, fc=fc: e.transpose(ps[:, b1, fc * 8:(fc + 1) * 8], stage8[:, fc * 128:(fc + 1) * 128], ident_f[0:8, 0:8]),
                 reads=["stage8", "ident_f"], writes=psk(b1))
        S.dve(lambda e: e.tensor_copy(out=ffnvT[:].rearrange("p a b -> p (a b)"), in_=ps[:, b1, 0:NF * 8]), reads=psk(b1), writes=["ffnvT"])

        browb = [a_bf16(512, parts=1) for _ in range(3)]
        mstage = [a_f32(512, parts=17) for _ in range(2)]
        csb = a_bf16(KC, 17)
        S.dve(lambda e: e.tensor_copy(out=csb, in_=csT[:]), reads=["csT"], writes=["csb"])
        bri = [0]

        def ada_layer(wsrc, bsrc, ncols, mod_base):
            for pcs in range(ncols // 512):
                wv, wk = load_w_bf16(wsrc, KC, 512, col0=pcs * 512)
                bi = bri[0] % 3
                m2 = bri[0] % 2
                bri[0] += 1
                S.dma("pool", "brb%d" % bi, lambda e, bi=bi, pcs=pcs: e.dma_start(out=browb[bi], in_=bsrc[0:1, pcs * 512:(pcs + 1) * 512]), writes=[("browb", bi)])
                bank = ps_next()
                for kc in range(KC):
                    S.pe(lambda e, bank=bank, kc=kc, wv=wv: e.matmul(ps[0:17, bank, :], lhsT=csb[:, kc, :], rhs=wv[:, kc, :], start=(kc == 0), stop=False), reads=[wk, "csb"], writes=psk(bank))
                S.pe(lambda e, bank=bank, bi=bi: e.matmul(ps[0:17, bank, :], lhsT=ones_b[0:1, 0:17], rhs=browb[bi][0:1, :], start=False, stop=True), reads=[("browb", bi), "ones_b"], writes=psk(bank))
                S.act(lambda e, bank=bank, m2=m2: e.copy(out=mstage[m2], in_=ps[0:17, bank, :]), reads=psk(bank), writes=[("mstage", m2)])
                bank2 = ps_next()
                for q in range(4):
                    S.pe(lambda e, bank2=bank2, q=q, m2=m2: e.transpose(ps[:, bank2, q * 17:(q + 1) * 17], mstage[m2][:, q * 128:(q + 1) * 128], ident_f[0:17, 0:17]), reads=[("mstage", m2), "ident_f"], writes=psk(bank2))
                m0 = mod_base + 4 * pcs
                S.dve(lambda e, bank2=bank2, m0=m0: e.tensor_copy(out=modT[:, m0:m0 + 4, :].rearrange("p a b -> p (a b)"), in_=ps[:, bank2, 0:68]), reads=psk(bank2), writes=["modT"])

        ada_layer(w_ada[0], b_ada[0:1, :], 6 * D, 0)
        ada_layer(w_ada[1], b_ada[1:2, :], 6 * D, 48)
        ada_layer(w_adaf, b_adaf, 2 * D, 96)
        norm_specs = [(0 * 48 + 8, 17), (0 * 48 + 32, 19), (1 * 48 + 8, 18), (1 * 48 + 32, 20), (96 + 8, 21)]
        for n, (scb, col) in enumerate(norm_specs):
            for kc in range(KC):
                S.dve(lambda e, n=n, kc=kc, scb=scb, col=col: e.tensor_scalar(out=Amod[:, n, kc, :], in0=modT[:, scb + kc, :], scalar1=1.0, scalar2=vecT[:, kc, col:col + 1], op0=ALU.add, op1=ALU.mult),
                      reads=["modT", "vecT"], writes=["Amod"])
        norm_shift = [0, 24, 48, 72, 96]
        S.barrier()

        a_reset()
        xst = [a_f32(D) for _ in range(2)]
        tiles = [(xp, t * 128, 128, t * 128) for t in range(16)] + [(xs, 0, TS, TP)]
        for ti, (src, r0, rows, t0) in enumerate(tiles):
            sb = ti % 2
            S.dma("sp", "xst%d" % sb, lambda e, sb=sb, src=src, r0=r0, rows=rows: e.dma_start(out=xst[sb][0:rows, :], in_=src[r0:r0 + rows, :]), writes=[("xst", sb)])
            for half in range(2):
                b = ps_next()
                for q in range(4):
                    kc = half * 4 + q
                    S.pe(lambda e, b=b, q=q, kc=kc, sb=sb, rows=rows: e.transpose(ps[:, b, q * 128:q * 128 + rows], xst[sb][0:rows, kc * 128:(kc + 1) * 128], ident_f[0:rows, 0:rows]),
                         reads=[("xst", sb), "ident_f"], writes=psk(b))
                src_ap = ps[:, b, :].rearrange("p (q c) -> p q c", q=4)[:, :, 0:rows]
                dst_ap = xT[:, half * 4:half * 4 + 4, t0:t0 + rows]
                if half == 0:
                    S.act(lambda e, dst_ap=dst_ap, src_ap=src_ap: e.copy(out=dst_ap, in_=src_ap), reads=psk(b), writes=[("xT", ti)])
                else:
                    S.dve(lambda e, dst_ap=dst_ap, src_ap=src_ap: e.tensor_copy(out=dst_ap, in_=src_ap), reads=psk(b), writes=[("xT", ti)])
        S.barrier()

        def do_norm(n, out_hT=True, out_f32=None):
            a_reset()
            sq = [a_bf16(KC, 512) for _ in range(2)]
            rstd = [a_f32(512) for _ in range(2)]
            tmp = [a_f32(KC, 512) for _ in range(2)]
            shb = norm_shift[n]
            for bi, (t0, nt) in enumerate(ALLBLOCKS):
                i2 = bi % 2
                S.act(lambda e, i2=i2, t0=t0, nt=nt: e.activation(out=sq[i2][:, :, 0:nt], in_=xT[:, :, t0:t0 + nt], func=AF.Square),
                      reads=[], writes=[("sq", i2)])
                b = ps_next()
                for kc in range(KC):
                    S.pe(lambda e, b=b, kc=kc, i2=i2, nt=nt: e.matmul(ps[:, b, 0:nt], lhsT=ones_b[:], rhs=sq[i2][:, kc, 0:nt], start=(kc == 0), stop=(kc == KC - 1)),
                         reads=[("sq", i2), "ones_b"], writes=psk(b))
                S.act(lambda e, b=b, i2=i2, nt=nt: e.activation(out=rstd[i2][:, 0:nt], in_=ps[:, b, 0:nt], func=AF.Sqrt, bias=EPS, scale=1.0 / D),
                      reads=psk(b), writes=[("rstd", i2)])
                S.dve(lambda e, i2=i2, nt=nt: e.reciprocal(out=rstd[i2][:, 0:nt], in_=rstd[i2][:, 0:nt]), reads=[("rstd", i2)], writes=[("rstd", i2)])
                S.dve(lambda e, i2=i2, t0=t0, nt=nt: e.tensor_tensor(out=tmp[i2][:, :, 0:nt], in0=xT[:, :, t0:t0 + nt], in1=rstd[i2][:, 0:nt].unsqueeze(1).to_broadcast([128, KC, nt]), op=ALU.mult),
                      reads=[("rstd", i2)], writes=[("tmp", i2)])
                for kc in range(KC):
                    dst = hT[:, kc, t0:t0 + nt] if out_f32 is None else out_f32(bi, kc)
                    if t0 < TP:
                        S.act(lambda e, dst=dst, i2=i2, kc=kc, nt=nt: e.activation(out=dst, in_=tmp[i2][:, kc, 0:nt], func=AF.Identity, scale=Amod[:, n, kc, 0:1], bias=modT[:, shb + kc, 0:1]),
                              reads=[("tmp", i2)], writes=[("h", bi, kc)])
                    else:
                        S.dve(lambda e, i2=i2, kc=kc: e.tensor_tensor(out=tmp[i2][:, kc, 0:TS].rearrange("p (s j) -> p s j", j=4), in0=tmp[i2][:, kc, 0:TS].rearrange("p (s j) -> p s j", j=4),
                                                                     in1=Amod[:, n, kc, 1:17].unsqueeze(2).to_broadcast([128, NSEQ, 4]), op=ALU.mult),
                              reads=[("tmp", i2)], writes=[("tmp", i2)])
                        S.dve(lambda e, dst=dst, i2=i2, kc=kc: e.tensor_tensor(out=dst.rearrange("p (s j) -> p s j", j=4), in0=tmp[i2][:, kc, 0:TS].rearrange("p (s j) -> p s j", j=4),
                                                                              in1=modT[:, shb + kc, 1:17].unsqueeze(2).to_broadcast([128, NSEQ, 4]), op=ALU.add),
                              reads=[("tmp", i2)], writes=[("h", bi, kc)])
            S.barrier()

        def gate_ap(gbase, kc, bi, nt):
            if bi != "S":
                return modT[:, gbase + kc, 0:1].to_broadcast([128, nt])
            return modT[:, gbase + kc, 1:17].unsqueeze(2).to_broadcast([128, NSEQ, 4])

        def resid_update(psrc, gbase, kc, bi, t0, nt, reads):
            if t0 < TP:
                S.dve(lambda e: e.scalar_tensor_tensor(out=xT[:, kc, t0:t0 + nt], in0=psrc, scalar=modT[:, gbase + kc, 0:1], in1=xT[:, kc, t0:t0 + nt], op0=ALU.mult, op1=ALU.add),
                      reads=list(reads) + [("x", kc, t0)], writes=[("x", kc, t0)])
            else:
                tmpg = K.tmpg
                S.dve(lambda e: e.tensor_tensor(out=tmpg.rearrange("p (s j) -> p s j", j=4), in0=psrc.rearrange("p (s j) -> p s j", j=4), in1=gate_ap(gbase, kc, "S", nt), op=ALU.mult),
                      reads=list(reads), writes=["tmpg"])
                S.dve(lambda e: e.tensor_tensor(out=xT[:, kc, t0:t0 + nt], in0=xT[:, kc, t0:t0 + nt], in1=tmpg, op=ALU.add),
                      reads=["tmpg", ("x", kc, t0)], writes=[("x", kc, t0)])

        def do_ffn(l):
            a_reset()
            gbase = l * 48 + 40
            K.tmpg = a_f32(TS)
            actb = a_bf16(NF, 704)
            gx = [a_f32(2 + 512) for _ in range(2)]
            gxs = a_f32(NSEQ, 6)
            cv = [a_f32(512) for _ in range(2)]
            tailP = a_f32(NF, 2)
            tailS = a_f32(NF, NSEQ, 2)
            sstT = a_f32(NF, 32)
            sstg = [a_f32(512, parts=32) for _ in range(2)]
            ostg = [a_f32(512, parts=32) for _ in range(2)]
            ostgP = [a_f32(512, parts=2) for _ in range(2)]
            for gi, g4 in enumerate(range(0, NF, 4)):
                b = ps_next()
                n = min(4, NF - g4)
                s2 = gi % 2
                S.dma("sp", "fst%d" % s2, lambda e, s2=s2, g4=g4, n=n: e.dma_start(out=sstg[s2][:, 0:n * 128], in_=s_fconv[l][:, g4 * 128:(g4 + n) * 128]), writes=[("sstg", s2)])
                for q in range(n):
                    S.pe(lambda e, b=b, q=q, s2=s2: e.transpose(ps[:, b, q * 32:(q + 1) * 32], sstg[s2][:, q * 128:(q + 1) * 128], ident_f[0:32, 0:32]),
                         reads=[("sstg", s2), "ident_f"], writes=psk(b))
                S.dve(lambda e, b=b, n=n, g4=g4: e.tensor_copy(out=sstT[:, g4:g4 + n, :].rearrange("p a b -> p (a b)"), in_=ps[:, b, 0:n * 32]), reads=psk(b), writes=["sstT"])
            S.dve(lambda e: e.memset(fcarry[:], 0.0), writes=[("fcarry", fc_) for fc_ in range(NF)])
            wcol = l * 3
            passes = [[(0, 0, 352, 0), (1, 352, 352, 352)], [(2, 704, 352, 0), (3, 1056, 352, 352)], [(4, 1408, 320, 0), (5, 1728, 320, 320), (6, TP, TS, 640)]]
            for pi, blks in enumerate(passes):
                for f0 in range(0, NF, 4):
                    nf = min(4, NF - f0)
                    wg, wgk = load_w_bf16(w_up[l], KC, nf * 128, col0=f0 * 128)
                    wv, wvk = load_w_bf16(w_up[l], KC, nf * 128, col0=DFF + f0 * 128)
                    for q in range(nf):
                        fc = f0 + q
                        for (bi, t0, nt, a0) in blks:
                            bg = ps_next()
                            for kc in range(KC):
                                S.pe(lambda e, bg=bg, kc=kc, q=q, t0=t0, nt=nt, wg=wg: e.matmul(ps[:, bg, 0:nt], lhsT=wg[:, kc, q * 128:(q + 1) * 128], rhs=hT[:, kc, t0:t0 + nt], start=(kc == 0), stop=(kc == KC - 1)),
                                     reads=[wgk], writes=psk(bg))
                            bv = ps_next()
                            for kc in range(KC):
                                S.pe(lambda e, bv=bv, kc=kc, q=q, t0=t0, nt=nt, wv=wv: e.matmul(ps[:, bv, 0:nt], lhsT=wv[:, kc, q * 128:(q + 1) * 128], rhs=hT[:, kc, t0:t0 + nt], start=(kc == 0), stop=(kc == KC - 1)),
                                     reads=[wvk], writes=psk(bv))
                            w0 = ffnvT[:, fc, wcol + 0:wcol + 1]
                            w1 = ffnvT[:, fc, wcol + 1:wcol + 2]
                            w2 = ffnvT[:, fc, wcol + 2:wcol + 3]
                            bb = ffnvT[:, fc, 6 + l:7 + l]
                            if bi < 6:
                                i2 = bi % 2
                                G = gx[i2]
                                c_ = cv[i2]
                                S.dve(lambda e, G=G, fc=fc: e.tensor_copy(out=G[:, 0:2], in_=fcarry[:, fc, :]), reads=[("fcarry", fc)], writes=[("gx", i2)])
                                S.act(lambda e, G=G, bg=bg, nt=nt: e.copy(out=G[:, 2:2 + nt], in_=ps[:, bg, 0:nt]), reads=psk(bg), writes=[("gx", i2)])
                                S.dve(lambda e, G=G, fc=fc, nt=nt: e.tensor_copy(out=fcarry[:, fc, :], in_=G[:, nt:nt + 2]), reads=[("gx", i2)], writes=[("fcarry", fc)])
                                if bi == 5:
                                    S.dve(lambda e, G=G, fc=fc, nt=nt: e.tensor_copy(out=tailP[:, fc, :], in_=G[:, nt:nt + 2]), reads=[("gx", i2)], writes=["tailP"])
                                S.act(lambda e, G=G, c_=c_, nt=nt, w2=w2: e.activation(out=c_[:, 0:nt], in_=G[:, 2:2 + nt], func=AF.Identity, scale=w2), reads=[("gx", i2), "ffnvT"], writes=[("cv", i2)])
                                S.dve(lambda e, G=G, c_=c_, nt=nt, w1=w1: e.scalar_tensor_tensor(out=c_[:, 0:nt], in0=G[:, 1:1 + nt], scalar=w1, in1=c_[:, 0:nt], op0=ALU.mult, op1=ALU.add), reads=[("gx", i2), ("cv", i2)], writes=[("cv", i2)])
                                S.dve(lambda e, G=G, c_=c_, nt=nt, w0=w0: e.scalar_tensor_tensor(out=c_[:, 0:nt], in0=G[:, 0:nt], scalar=w0, in1=c_[:, 0:nt], op0=ALU.mult, op1=ALU.add), reads=[("gx", i2), ("cv", i2)], writes=[("cv", i2)])
                                S.act(lambda e, c_=c_, nt=nt, bb=bb: e.activation(out=c_[:, 0:nt], in_=c_[:, 0:nt], func=AF.Silu, bias=bb), reads=[("cv", i2)], writes=[("cv", i2)])
                                S.dve(lambda e, c_=c_, nt=nt, bv=bv, fc=fc, a0=a0: e.tensor_tensor(out=actb[:, fc, a0:a0 + nt], in0=c_[:, 0:nt], in1=ps[:, bv, 0:nt], op=ALU.mult), reads=[("cv", i2)] + psk(bv), writes=[("act", fc, bi)])
                            else:
                                c_ = cv[0][:, 0:TS].rearrange("p (s j) -> p s j", j=4)
                                S.dve(lambda e, fc=fc: e.tensor_copy(out=gxs[:, :, 0:2], in_=sstT[:, fc, :].rearrange("p (s r) -> p s r", r=2)), reads=["sstT"], writes=["gxs"])
                                S.act(lambda e, bg=bg: e.copy(out=gxs[:, :, 2:6], in_=ps[:, bg, 0:TS].rearrange("p (s j) -> p s j", j=4)), reads=psk(bg), writes=["gxs"])
                                S.dve(lambda e, fc=fc: e.tensor_copy(out=tailS[:, fc, :, :], in_=gxs[:, :, 4:6]), reads=["gxs"], writes=["tailS"])
                                S.act(lambda e, c_=c_, w2=w2: e.activation(out=c_, in_=gxs[:, :, 2:6], func=AF.Identity, scale=w2), reads=["gxs", "ffnvT"], writes=[("cv", 0)])
                                S.dve(lambda e, c_=c_, w1=w1: e.scalar_tensor_tensor(out=c_, in0=gxs[:, :, 1:5], scalar=w1, in1=c_, op0=ALU.mult, op1=ALU.add), reads=["gxs", ("cv", 0)], writes=[("cv", 0)])
                                S.dve(lambda e, c_=c_, w0=w0: e.scalar_tensor_tensor(out=c_, in0=gxs[:, :, 0:4], scalar=w0, in1=c_, op0=ALU.mult, op1=ALU.add), reads=["gxs", ("cv", 0)], writes=[("cv", 0)])
                                S.act(lambda e, bb=bb: e.activation(out=cv[0][:, 0:TS], in_=cv[0][:, 0:TS], func=AF.Silu, bias=bb), reads=[("cv", 0)], writes=[("cv", 0)])
                                S.dve(lambda e, bv=bv, fc=fc, a0=a0: e.tensor_tensor(out=actb[:, fc, a0:a0 + TS], in0=cv[0][:, 0:TS], in1=ps[:, bv, 0:TS], op=ALU.mult), reads=[("cv", 0)] + psk(bv), writes=[("act", fc, bi)])
                for dh in range(2):
                    banks = {}
                    for dc in range(4):
                        for (bi, t0, nt, a0) in blks:
                            if len(blks) * 4 > 8 and bi == 6:
                                continue
                            banks[(dc, bi)] = ps_next()
                    for f0 in range(0, NF, 4):
                        nf = min(4, NF - f0)
                        wd, wdk = load_w_bf16(w_down[l], nf, 512, row0=f0 * 128, col0=dh * 512)
                        for (dc, bi), b in banks.items():
                            t0, nt, a0 = [(x[1], x[2], x[3]) for x in blks if x[0] == bi][0]
                            for q in range(nf):
                                fc = f0 + q
                                S.pe(lambda e, b=b, q=q, dc=dc, fc=fc, a0=a0, nt=nt, wd=wd: e.matmul(ps[:, b, 0:nt], lhsT=wd[:, q, dc * 128:(dc + 1) * 128], rhs=actb[:, fc, a0:a0 + nt], start=(fc == 0), stop=(fc == NF - 1)),
                                     reads=[wdk, ("act", fc, bi)], writes=psk(b))
                    for (dc, bi), b in banks.items():
                        t0, nt, a0 = [(x[1], x[2], x[3]) for x in blks if x[0] == bi][0]
                        resid_update(ps[:, b, 0:nt], gbase, dh * 4 + dc, bi, t0, nt, psk(b))
                if len(blks) * 4 > 8:
                    (bi, t0, nt, a0) = blks[2]
                    for dh in range(2):
                        banks = {dc: ps_next() for dc in range(4)}
                        for f0 in range(0, NF, 4):
                            nf = min(4, NF - f0)
                            wd, wdk = load_w_bf16(w_down[l], nf, 512, row0=f0 * 128, col0=dh * 512)
                            for dc, b in banks.items():
                                for q in range(nf):
                                    fc = f0 + q
                                    S.pe(lambda e, b=b, q=q, dc=dc, fc=fc, wd=wd, nt=nt, a0=a0: e.matmul(ps[:, b, 0:nt], lhsT=wd[:, q, dc * 128:(dc + 1) * 128], rhs=actb[:, fc, a0:a0 + nt], start=(fc == 0), stop=(fc == NF - 1)),
                                         reads=[wdk, ("act", fc, bi)], writes=psk(b))
                        for dc, b in banks.items():
                            resid_update(ps[:, b, 0:nt], gbase, dh * 4 + dc, bi, t0, nt, psk(b))
            for gi, g4 in enumerate(range(0, NF, 4)):
                n = min(4, NF - g4)
                o2 = gi % 2
                b = ps_next()
                b2 = ps_next()
                for q in range(n):
                    fc = g4 + q
                    S.pe(lambda e, b=b, q=q, fc=fc: e.transpose(ps[0:2, b, q * 128:(q + 1) * 128], tailP[:, fc, :], ident_f[:]), reads=["tailP", "ident_f"], writes=psk(b))
                    S.pe(lambda e, b2=b2, q=q, fc=fc: e.transpose(ps[0:32, b2, q * 128:(q + 1) * 128], tailS[:, fc, :, :].rearrange("p s r -> p (s r)"), ident_f[:]), reads=["tailS", "ident_f"], writes=psk(b2))
                S.dve(lambda e, b=b, n=n, o2=o2: e.tensor_copy(out=ostgP[o2][:, 0:n * 128], in_=ps[0:2, b, 0:n * 128]), reads=psk(b), writes=[("ostgP", o2)])
                S.dve(lambda e, b2=b2, n=n, o2=o2: e.tensor_copy(out=ostg[o2][:, 0:n * 128], in_=ps[0:32, b2, 0:n * 128]), reads=psk(b2), writes=[("ostg", o2)])
                S.dma("sp", "foutP%d" % o2, lambda e, o2=o2, n=n, g4=g4: e.dma_start(out=fconv_p[l][:, g4 * 128:(g4 + n) * 128], in_=ostgP[o2][:, 0:n * 128]), reads=[("ostgP", o2)])
                S.dma("sp", "foutS%d" % o2, lambda e, o2=o2, n=n, g4=g4: e.dma_start(out=fconv_s[l][:, g4 * 128:(g4 + n) * 128], in_=ostg[o2][:, 0:n * 128]), reads=[("ostg", o2)])
            S.barrier()

        def do_final():
            a_reset()
            yT = a_f32(KC, 512)
            ytok = [a_f32(D) for _ in range(2)]
            n = 4
            sq = a_bf16(KC, 512)
            rstd = a_f32(512)
            tmp = a_f32(KC, 512)
            shb = norm_shift[n]
            oi = 0
            for bi, (t0, nt) in enumerate(ALLBLOCKS):
                S.act(lambda e, t0=t0, nt=nt: e.activation(out=sq[:, :, 0:nt], in_=xT[:, :, t0:t0 + nt], func=AF.Square), reads=[], writes=["sq"])
                b = ps_next()
                for kc in range(KC):
                    S.pe(lambda e, b=b, kc=kc, nt=nt: e.matmul(ps[:, b, 0:nt], lhsT=ones_b[:], rhs=sq[:, kc, 0:nt], start=(kc == 0), stop=(kc == KC - 1)), reads=["sq", "ones_b"], writes=psk(b))
                S.act(lambda e, b=b, nt=nt: e.activation(out=rstd[:, 0:nt], in_=ps[:, b, 0:nt], func=AF.Sqrt, bias=EPS, scale=1.0 / D), reads=psk(b), writes=["rstd"])
                S.dve(lambda e, nt=nt: e.reciprocal(out=rstd[:, 0:nt], in_=rstd[:, 0:nt]), reads=["rstd"], writes=["rstd"])
                S.dve(lambda e, t0=t0, nt=nt: e.tensor_tensor(out=tmp[:, :, 0:nt], in0=xT[:, :, t0:t0 + nt], in1=rstd[:, 0:nt].unsqueeze(1).to_broadcast([128, KC, nt]), op=ALU.mult), reads=["rstd"], writes=["tmp"])
                for kc in range(KC):
                    if t0 < TP:
                        S.act(lambda e, kc=kc, nt=nt: e.activation(out=yT[:, kc, 0:nt], in_=tmp[:, kc, 0:nt], func=AF.Identity, scale=Amod[:, n, kc, 0:1], bias=modT[:, shb + kc, 0:1]), reads=["tmp"], writes=["yT"])
                    else:
                        S.dve(lambda e, kc=kc: e.tensor_tensor(out=tmp[:, kc, 0:TS].rearrange("p (s j) -> p s j", j=4), in0=tmp[:, kc, 0:TS].rearrange("p (s j) -> p s j", j=4), in1=Amod[:, n, kc, 1:17].unsqueeze(2).to_broadcast([128, NSEQ, 4]), op=ALU.mult), reads=["tmp"], writes=["tmp"])
                        S.dve(lambda e, kc=kc: e.tensor_tensor(out=yT[:, kc, 0:TS].rearrange("p (s j) -> p s j", j=4), in0=tmp[:, kc, 0:TS].rearrange("p (s j) -> p s j", j=4), in1=modT[:, shb + kc, 1:17].unsqueeze(2).to_broadcast([128, NSEQ, 4]), op=ALU.add), reads=["tmp"], writes=["yT"])
                for sub in range(0, nt, 128):
                    rows = min(128, nt - sub)
                    o2 = oi % 2
                    oi += 1
                    for half in range(2):
                        b = ps_next()
                        for q in range(4):
                            kc = half * 4 + q
                            S.pe(lambda e, b=b, q=q, kc=kc, sub=sub, rows=rows: e.transpose(ps[0:rows, b, q * 128:(q + 1) * 128], yT[:, kc, sub:sub + rows], ident_f[:]), reads=["yT", "ident_f"], writes=psk(b))
                        if half == 0:
                            S.act(lambda e, b=b, o2=o2, rows=rows, half=half: e.copy(out=ytok[o2][0:rows, half * 512:(half + 1) * 512], in_=ps[0:rows, b, :]), reads=psk(b), writes=[("ytok", o2, half)])
                        else:
                            S.dve(lambda e, b=b, o2=o2, rows=rows, half=half: e.tensor_copy(out=ytok[o2][0:rows, half * 512:(half + 1) * 512], in_=ps[0:rows, b, :]), reads=psk(b), writes=[("ytok", o2, half)])
                    if t0 < TP:
                        dst = y_p[t0 + sub:t0 + sub + rows, :]
                    else:
                        dst = y_s[sub:sub + rows, :]
                    S.dma("sp", "yo%d" % o2, lambda e, dst=dst, o2=o2, rows=rows: e.dma_start(out=dst, in_=ytok[o2][0:rows, :]), reads=[("ytok", o2, 0), ("ytok", o2, 1)])
            S.barrier()

        K.__dict__.update(locals())
        G_ = globals()
        do_norm(0)
        import os
        if "do_retention" in G_ and not os.environ.get("SKIP_RET"):
            G_["do_retention"](K)
        do_norm(1)
        do_ffn(0)
        do_norm(2)
        if "do_gdn" in G_:
            G_["do_gdn"](K)
        do_norm(3)
        do_ffn(1)
        do_final()
        if "modT" in dbg_out:
            S.dma("sp", "dbg", lambda e: e.dma_start(out=dbg_out["modT"], in_=modT[:].rearrange("p a b -> p (a b)")))
        if "Amod" in dbg_out:
            S.dma("sp", "dbg", lambda e: e.dma_start(out=dbg_out["Amod"], in_=Amod[:].rearrange("p a b c -> p (a b c)")))
        S.emit(st)
    return nc


def do_retention(K):
    S = K.S; ps = K.ps; nc = K.nc
    a_f32 = K.a_f32; a_bf16 = K.a_bf16; ps_next = K.ps_next; psk = K.psk
    hT = K.hT; xT = K.xT; modT = K.modT
    ident_b = K.ident_b
    g = _gammas()
    K.a_reset()
    K.tmpg = a_f32(TS)
    tabs = a_f32(4, 512)
    qb = a_bf16(2, 512)
    kb = a_bf16(2, 512)
    t1 = a_f32(512)
    t2 = a_f32(512)
    vb = a_bf16(512)
    sg = a_f32(512)
    kh = a_bf16(256)
    im = a_bf16(128)
    og = a_bf16(512)
    ogT = a_bf16(4, 512)
    ss = a_f32(4)
    maskP = a_f32(128)
    maskS = a_f32(128)
    ksc = a_f32(2 * RET_H)
    segcol = a_f32(NSEQ)
    S32 = a_f32(2, 512)
    Sbf = a_bf16(2, 512)
    qf = a_f32(2, TS)
    qX = a_f32(2, NSEQ, TS)
    khm = [a_bf16(256) for _ in range(2)]
    Sin = [a_f32(2, 512) for _ in range(2)]
    Sout = a_f32(2, 512)
    segmask = a_f32(NSEQ, TS)
    S.dve(lambda e: e.memset(ss[:, 2:4], -0.5), writes=["mhalf"])
    S.dma("sp", "rc", lambda e: e.dma_start(out=ksc, in_=K.c_kscale), writes=["ksc"])
    S.dma("sp", "rc", lambda e: e.dma_start(out=segcol, in_=K.c_segcol), writes=["segcol"])
    S.dma("sp", "rc", lambda e: e.dma_start(out=segmask.rearrange("p a b -> p (a b)"), in_=K.c_segmask), writes=["segmask"])
    gbase = 16
    w_in = K.w_ret_in
    w_out = K.w_ret_out
    sctr = [0]
    for h in range(RET_H):
        wq, wqk = K.load_w_bf16(w_in, KC, 256, col0=h * 256, slot=0, off=0, key=("w", 0, "q"))
        wk, wkk = K.load_w_bf16(w_in, KC, 256, col0=1024 + h * 256, slot=0, off=2048, key=("w", 0, "k"))
        wv, wvk = K.load_w_bf16(w_in, KC, 512, col0=2048 + h * 512, slot=1)
        wg, wgk = K.load_w_bf16(w_in, KC, 512, col0=4096 + h * 512, slot=2)
        wo, wok = K.load_w_bf16(w_out, 4, 1024, row0=h * 512, slot=3)
        S.dma("sp", "rm", lambda e, h=h: e.dma_start(out=maskP, in_=K.c_retmask[h]), writes=["maskP"])
        S.dma("sp", "rm", lambda e, h=h: e.dma_start(out=maskS, in_=K.c_retmask[RET_H + h]), writes=["maskS"])
        for bi, (t0, nt) in enumerate(K.ALLBLOCKS):
            isS = t0 >= TP
            C = 64 if isS else 128
            for ti, (hh, cs_) in enumerate([(h, 0), (h, 1), (RET_H, 0), (RET_H, 1)]):
                S.dma("sp", "rt", lambda e, ti=ti, hh=hh, cs_=cs_, t0=t0, nt=nt: e.dma_start(out=tabs[:, ti, 0:nt], in_=K.c_rope[hh, cs_, :, t0:t0 + nt]), writes=[("tabs", ti)])
            for which, (wt, wkey, dst, tb) in enumerate([(wq, wqk, qb, 0), (wk, wkk, kb, 2)]):
                pb = [ps_next(), ps_next()]
                for dc in range(2):
                    for kc in range(KC):
                        S.pe(lambda e, b=pb[dc], kc=kc, dc=dc, wt=wt, t0=t0, nt=nt: e.matmul(ps[:, b, 0:nt], lhsT=wt[:, kc, dc * 128:(dc + 1) * 128], rhs=hT[:, kc, t0:t0 + nt], start=(kc == 0), stop=(kc == KC - 1)),
                             reads=[wkey], writes=psk(pb[dc]))
                p1 = ps[:, pb[0], 0:nt]
                p2 = ps[:, pb[1], 0:nt]
                cc = tabs[:, tb, 0:nt]
                sn = tabs[:, tb + 1, 0:nt]
                to_f = isS and which == 0
                d1 = qf[:, 0, :] if to_f else dst[:, 0, 0:nt]
                d2 = qf[:, 1, :] if to_f else dst[:, 1, 0:nt]
                S.dve(lambda e, p1=p1, cc=cc, nt=nt: e.tensor_tensor(out=t1[:, 0:nt], in0=p1, in1=cc, op=ALU.mult), reads=psk(pb[0]) + [("tabs", tb)], writes=["t1"])
                S.dve(lambda e, p2=p2, sn=sn, nt=nt: e.tensor_tensor(out=t2[:, 0:nt], in0=p2, in1=sn, op=ALU.mult), reads=psk(pb[1]) + [("tabs", tb + 1)], writes=["t2"])
                S.pool(lambda e, d1=d1, nt=nt: e.tensor_tensor(out=d1, in0=t1[:, 0:nt], in1=t2[:, 0:nt], op=ALU.subtract), reads=["t1", "t2"], writes=[("rd", which, 0)])
                S.dve(lambda e, p1=p1, sn=sn, nt=nt: e.tensor_tensor(out=t1[:, 0:nt], in0=p1, in1=sn, op=ALU.mult), reads=psk(pb[0]) + [("tabs", tb + 1)], writes=["t1"])
                S.dve(lambda e, p2=p2, cc=cc, nt=nt: e.tensor_tensor(out=t2[:, 0:nt], in0=p2, in1=cc, op=ALU.mult), reads=psk(pb[1]) + [("tabs", tb)], writes=["t2"])
                S.pool(lambda e, d2=d2, nt=nt: e.tensor_tensor(out=d2, in0=t1[:, 0:nt], in1=t2[:, 0:nt], op=ALU.add), reads=["t1", "t2"], writes=[("rd", which, 1)])
                if to_f:
                    S.act(lambda e: e.copy(out=qb[:, :, 0:TS], in_=qf[:, :, :]), reads=[("rd", 0, 0), ("rd", 0, 1)], writes=["qbS"])
            qkeys = [("rd", 0, 0), ("rd", 0, 1)] + (["qbS"] if isS else [])
            kkeys = [("rd", 1, 0), ("rd", 1, 1)]
            for c0 in range(0, nt, C):
                first = (not isS) and t0 == 0 and c0 == 0
                last = (not isS) and (t0 + c0 + C == TP)
                bv = ps_next()
                for kc in range(KC):
                    S.pe(lambda e, bv=bv, kc=kc, t0=t0, c0=c0, C=C: e.matmul(ps[0:C, bv, :], lhsT=hT[:, kc, t0 + c0:t0 + c0 + C], rhs=wv[:, kc, :], start=(kc == 0), stop=(kc == KC - 1)), reads=[wvk], writes=psk(bv))
                S.act(lambda e, bv=bv, C=C: e.copy(out=vb[0:C, :], in_=ps[0:C, bv, :]), reads=psk(bv), writes=["vb"])
                bg = ps_next()
                for kc in range(KC):
                    S.pe(lambda e, bg=bg, kc=kc, t0=t0, c0=c0, C=C: e.matmul(ps[0:C, bg, :], lhsT=hT[:, kc, t0 + c0:t0 + c0 + C], rhs=wg[:, kc, :], start=(kc == 0), stop=(kc == KC - 1)), reads=[wgk], writes=psk(bg))
                S.act(lambda e, bg=bg, C=C: e.activation(out=sg[0:C, :], in_=ps[0:C, bg, :], func=AF.Silu), reads=psk(bg), writes=["sg"])
                bk = ps_next()
                psb = ps[:, bk, :].bitcast(BF16)
                for dc in range(2):
                    S.pe(lambda e, psb=psb, dc=dc, c0=c0, C=C: e.transpose(psb[0:C, dc * 128:(dc + 1) * 128], kb[:, dc, c0:c0 + C], ident_b[:]), reads=kkeys + ["ident_b"], writes=psk(bk))
                kcol = (RET_H + h) if isS else h
                S.act(lambda e, psb=psb, C=C, kcol=kcol: e.activation(out=kh[0:C, :], in_=psb[0:C, 0:256], func=AF.Identity, scale=ksc[0:C, kcol:kcol + 1]), reads=psk(bk) + ["ksc"], writes=["kh"])
                bi_ = ps_next()
                for dc in range(2):
                    S.pe(lambda e, bi_=bi_, dc=dc, c0=c0, C=C: e.matmul(ps[0:C, bi_, 0:C], lhsT=kb[:, dc, c0:c0 + C], rhs=qb[:, dc, c0:c0 + C], start=(dc == 0), stop=(dc == 1)), reads=kkeys + qkeys, writes=psk(bi_))
                mk = maskS if isS else maskP
                S.dve(lambda e, bi_=bi_, C=C, mk=mk: e.tensor_tensor(out=im[0:C, 0:C], in0=ps[0:C, bi_, 0:C], in1=mk[0:C, 0:C], op=ALU.mult), reads=psk(bi_) + ["maskP", "maskS"], writes=["im"])
                bo = ps_next()
                has_inter = isS or not first
                S.pe(lambda e, bo=bo, C=C, has_inter=has_inter: e.matmul(ps[0:C, bo, :], lhsT=im[0:C, 0:C], rhs=vb[0:C, :], start=True, stop=not has_inter), reads=["im", "vb"], writes=psk(bo))
                if not isS:
                    if not first:
                        for dc in range(2):
                            S.pe(lambda e, bo=bo, dc=dc, c0=c0, C=C: e.matmul(ps[0:C, bo, :], lhsT=qb[:, dc, c0:c0 + C], rhs=Sbf[:, dc, :], start=False, stop=(dc == 1)), reads=qkeys + [("Sbf", dc)], writes=psk(bo))
                    for dc in range(2):
                        bs = ps_next()
                        S.pe(lambda e, bs=bs, dc=dc, C=C: e.matmul(ps[:, bs, :], lhsT=kh[0:C, dc * 128:(dc + 1) * 128], rhs=vb[0:C, :], start=True, stop=True), reads=["kh", "vb"], writes=psk(bs))
                        if first:
                            S.dve(lambda e, bs=bs, dc=dc: e.tensor_copy(out=S32[:, dc, :], in_=ps[:, bs, :]), reads=psk(bs), writes=[("S32", dc)])
                        else:
                            S.dve(lambda e, bs=bs, dc=dc, gc=float(g[h] ** 128): e.scalar_tensor_tensor(out=S32[:, dc, :], in0=S32[:, dc, :], scalar=gc, in1=ps[:, bs, :], op0=ALU.mult, op1=ALU.add), reads=psk(bs) + [("S32", dc)], writes=[("S32", dc)])
                        if last:
                            S.dma("sp", "rpo", lambda e, dc=dc, h=h: e.dma_start(out=K.ret_p[h, dc * 128:(dc + 1) * 128, :], in_=S32[:, dc, :]), reads=[("S32", dc)])
                        else:
                            S.act(lambda e, dc=dc: e.copy(out=Sbf[:, dc, :], in_=S32[:, dc, :]), reads=[("S32", dc)], writes=[("Sbf", dc)])
                else:
                    K.P.reserved = {bo}
                    for dc in range(2):
                        S.dve(lambda e, dc=dc: e.tensor_tensor(out=qX[:, dc, :, :], in0=qf[:, dc, :].unsqueeze(1).to_broadcast([128, NSEQ, TS]), in1=segmask[:, :, :], op=ALU.mult), reads=[("rd", 0, dc), "segmask"], writes=[("qX", dc)])
                    def _ldr(n, h=h):
                        i3 = n % 2
                        S.dma("sp", "sin%d" % i3, lambda e, i3=i3, n=n, h=h: e.dma_start(out=Sin[i3], in_=K.s_ret[n, h].rearrange("(dc p) v -> p dc v", p=128)), writes=[("Sin", i3)])
                    _ldr(0)
                    for s_ in range(NSEQ):
                        i2 = s_ % 2
                        if s_ + 1 < NSEQ:
                            _ldr(s_ + 1)
                        for dc in range(2):
                            S.pe(lambda e, bo=bo, dc=dc, s_=s_, i2=i2: e.matmul(ps[0:TS, bo, :], lhsT=qX[:, dc, s_, :], rhs=Sin[i2][:, dc, :], start=False, stop=(s_ == NSEQ - 1 and dc == 1)), reads=[("qX", dc), ("Sin", i2)], writes=psk(bo))
                        S.dve(lambda e, i2=i2, s_=s_: e.tensor_scalar(out=khm[i2][0:TS, :], in0=kh[0:TS, :], scalar1=segcol[0:TS, s_:s_ + 1], scalar2=None, op0=ALU.mult), reads=["kh", "segcol"], writes=[("khm", i2)])
                        for dc in range(2):
                            bs = ps_next()
                            S.pe(lambda e, bs=bs, dc=dc, i2=i2: e.matmul(ps[:, bs, :], lhsT=khm[i2][0:TS, dc * 128:(dc + 1) * 128], rhs=vb[0:TS, :], start=True, stop=True), reads=[("khm", i2), "vb"], writes=psk(bs))
                            S.dve(lambda e, bs=bs, dc=dc, i2=i2, gc=float(g[h] ** 4): e.scalar_tensor_tensor(out=Sout[:, dc, :], in0=Sin[i2][:, dc, :], scalar=gc, in1=ps[:, bs, :], op0=ALU.mult, op1=ALU.add), reads=psk(bs) + [("Sin", i2)], writes=[("Sout", dc)])
                        S.dma("sp", "sout", lambda e, s_=s_, h=h: e.dma_start(out=K.ret_s[s_, h].rearrange("(dc p) v -> p dc v", p=128), in_=Sout), reads=[("Sout", 0), ("Sout", 1)], writes=[])
                    K.P.reserved = set()
                _ret_tail(K, h, bo, C, c0, sg, og, ogT, ss, t1)
            for dc8 in range(KC):
                b = ps_next()
                for ec in range(4):
                    S.pe(lambda e, b=b, ec=ec, dc8=dc8, nt=nt: e.matmul(ps[:, b, 0:nt], lhsT=wo[:, ec, dc8 * 128:(dc8 + 1) * 128], rhs=ogT[:, ec, 0:nt], start=(ec == 0), stop=(ec == 3)), reads=[wok, "ogT"], writes=psk(b))
                K.resid_update(ps[:, b, 0:nt], gbase, dc8, ("S" if isS else bi), t0, nt, psk(b))
        S.barrier()
    S.barrier()


def _ret_tail(K, h, bo, C, c0, sg, og, ogT, ss, junk):
    S = K.S; ps = K.ps
    _f = lambda e: e.activation(out=junk[0:C, :], in_=ps[0:C, bo, :], func=AF.Square, accum_out=ss[0:C, 0:1])
    _f._multi = True
    S.act(_f, reads=K.psk(bo), writes=["t1", "ss"])
    S.dve(lambda e: e.tensor_scalar(out=ss[0:C, 1:2], in0=ss[0:C, 0:1], scalar1=1.0 / 512, scalar2=EPS, op0=ALU.mult, op1=ALU.add), reads=["ss"], writes=["ss2"])
    S.pool(lambda e: e.tensor_tensor(out=ss[0:C, 1:2], in0=ss[0:C, 1:2], in1=ss[0:C, 2:3], op=ALU.pow), reads=["ss2", "mhalf"], writes=["ss2"])
    S.dve(lambda e: e.scalar_tensor_tensor(out=og[0:C, :], in0=ps[0:C, bo, :], scalar=ss[0:C, 1:2], in1=sg[0:C, :], op0=ALU.mult, op1=ALU.mult), reads=K.psk(bo) + ["ss2", "sg"], writes=["og"])
    bt = K.ps_next()
    psb = ps[:, bt, :].bitcast(BF16)
    for ec in range(4):
        S.pe(lambda e, ec=ec: e.transpose(psb[:, ec * C:(ec + 1) * C], og[0:C, ec * 128:(ec + 1) * 128], K.ident_b[0:C, 0:C]), reads=["og", "ident_b"], writes=K.psk(bt))
    S.act(lambda e: e.copy(out=ogT[:, :, c0:c0 + C], in_=psb[:, 0:4 * C].rearrange("p (a b) -> p a b", a=4)), reads=K.psk(bt), writes=["ogT"])


def do_gdn(K):
    S = K.S; ps = K.ps
    a_f32 = K.a_f32; a_bf16 = K.a_bf16; ps_next = K.ps_next; psk = K.psk
    hT = K.hT; ident_f = K.ident_f; ident_b = K.ident_b; ones_b = K.ones_b
    w_in = K.w_gdn_in; w_out = K.w_gdn_out
    K.a_reset()
    K.tmpg = a_f32(TS)
    gm = a_f32(8, 128)
    segmask = a_f32(NSEQ, TS)
    segcol = a_f32(NSEQ)
    beta_all = a_f32(17, 16); g_all = a_f32(17, 16); negeG_all = a_f32(17, 16); kdec_all = a_f32(17, 16)
    gv = a_f32(32); negA = a_f32(16); gnw = a_f32(128)
    wcT = a_f32(32, 4)
    ones128 = a_f32(128)
    wba = a_bf16(KC, 32)
    Gx = a_f32(3 + 512); acc = a_f32(512); acc2 = a_f32(512); cch = a_f32(4, 3); gsT = a_f32(4, 48); Gxs = a_f32(NSEQ, 7)
    tailP = a_f32(4, 3); tailS = a_f32(4, NSEQ, 3)
    vs = a_f32(2, 512)
    sqb = a_bf16(512); rs = a_f32(512)
    qn = a_bf16(512); kn = a_bf16(512); qnf = a_f32(TS); knf = a_f32(TS)
    Bm = a_f32(2, 128); E = a_f32(2, 128); E2 = a_f32(2, 128); DTm = a_f32(2, 128)
    U = a_bf16(2, 128); UT = a_bf16(2, 128)
    UoM = [a_bf16(2, 128) for _ in range(3)]; UoTM = [a_bf16(2, 128) for _ in range(3)]
    Nb = [a_bf16(2, 128) for _ in range(2)]; Pb = [a_bf16(2, 128) for _ in range(2)]; PTb = [a_bf16(2, 128) for _ in range(2)]
    NTb = [a_bf16(2, 128) for _ in range(2)]; bm = a_bf16(4, 128)
    ktok = a_bf16(128); xv = a_bf16(2, 128); vnew = a_bf16(2, 128)
    Sbf = a_bf16(2, 128); og = a_bf16(256); ogT = a_bf16(2, 512)
    S32 = a_f32(2, 128); ss = a_f32(8)
    X = a_f32(NSEQ, TS); qgf = a_f32(TS)
    kdm = a_bf16(128); Sin = [a_f32(128) for _ in range(2)]; Sout = a_f32(128)
    ostg = acc
    eGl = [a_f32(2), a_f32(2)]
    w3f = K.wring[:, 3, :].bitcast(F32)
    w3b = K.wring[:, 3, :]
    szn2 = [w3f[:, 0:256], w3f[:, 256:512]]
    vtok2 = [w3f[:, 512:768].rearrange("p (a b) -> p a b", a=2), w3f[:, 768:1024].rearrange("p (a b) -> p a b", a=2)]

    def _b3(i):
        return w3b[:, 2048 + i * 256:2048 + (i + 1) * 256].rearrange("p (a b) -> p a b", a=2)
    Nfin = [_b3(0), _b3(1)]; attnT2 = [_b3(2), _b3(3)]; qg2 = [_b3(4), _b3(5)]; kd2 = [_b3(6), _b3(7)]

    def _f32v(b):
        return b.rearrange("p a b -> p (a b)").bitcast(F32)
    SinR = [Sin[0], Sin[1]] + [_f32v(x) for x in UoM + UoTM]
    SoutR = [Sout, _f32v(NTb[0]), _f32v(NTb[1])]
    RDEPTH = 7
    NLD = 2 * NSEQ
    mark = K.A.off

    S.dma("sp", "gc", lambda e: e.dma_start(out=gm.rearrange("p a b -> p (a b)"), in_=K.c_gmask), writes=["gm"])
    S.dma("sp", "gc", lambda e: e.dma_start(out=segmask.rearrange("p a b -> p (a b)"), in_=K.c_segmask), writes=["segmask"])
    S.dma("sp", "gc", lambda e: e.dma_start(out=segcol, in_=K.c_segcol), writes=["segcol"])
    S.dma("pool", "gcb", lambda e: e.dma_start(out=bm.rearrange("p a b -> p (a b)"), in_=K.c_bmask), writes=["bm"])
    S.dma("sp", "gc", lambda e: e.dma_start(out=gv, in_=K.gdn_vec.partition_broadcast(128)), writes=["gv"])
    S.dma("sp", "gc", lambda e: e.dma_start(out=gnw, in_=K.gdn_norm.partition_broadcast(128)), writes=["gnw"])
    S.dve(lambda e: e.memset(ones128, 1.0), writes=["ones128"])
    S.dve(lambda e: e.memset(ss[:, 4:6], -0.5), writes=["mhalf"])
    S.act(lambda e: e.activation(out=negA, in_=gv[:, 0:16], func=AF.Exp), reads=["gv"], writes=["negA"])
    S.dve(lambda e: e.tensor_scalar(out=negA, in0=negA, scalar1=-1.0, scalar2=None, op0=ALU.mult), reads=["negA"], writes=["negA"])
    TRIU = {False: gm[:, 0, :], True: gm[:, 4, :]}
    SU = {False: gm[:, 1, :], True: gm[:, 5, :]}
    INCL = {False: gm[:, 2, :], True: gm[:, 6, :]}
    STRICT = {False: gm[:, 3, :], True: gm[:, 7, :]}
    import os
    STG = int(os.environ.get('GDN_STAGE', 99))
    if STG == 0:
        S.barrier(); return
    Xflat = X.rearrange("p a b -> p (a b)")
    wst = [Xflat[0:4, 0:512], Xflat[0:4, 512:1024]]
    bw = ps_next()
    for pc in range(8):
        S.dma("sp", "wst%d" % (pc % 2), lambda e, pc=pc: e.dma_start(out=wst[pc % 2], in_=K.w_gconv[:, pc * 512:(pc + 1) * 512]), writes=[("wst", pc % 2)])
        for q in range(4):
            cidx = pc * 4 + q
            S.pe(lambda e, pc=pc, q=q, cidx=cidx: e.transpose(ps[:, bw, cidx * 4:(cidx + 1) * 4], wst[pc % 2][:, q * 128:(q + 1) * 128], ident_f[0:4, 0:4]), reads=[("wst", pc % 2), "ident_f"], writes=psk(bw))
    S.dve(lambda e: e.tensor_copy(out=wcT.rearrange("p a b -> p (a b)"), in_=ps[:, bw, 0:128]), reads=psk(bw), writes=["wcT"])
    if STG == 1:
        S.barrier(); return
    src = w_in[:, 6144:6176].rearrange("(k p) c -> p k c", p=128)
    S.dma("pool", "wba", lambda e: e.dma_start(out=wba, in_=src), writes=["wba"])
    if STG == 2:
        S.barrier(); return
    tiles = [(t * 128, 128, False) for t in range(16)] + [(TP, TS, True)]
    for tl, (t0, C, isS) in enumerate(tiles):
        pb = ps_next()
        for kc in range(KC):
            S.pe(lambda e, pb=pb, kc=kc, t0=t0, C=C: e.matmul(ps[0:C, pb, 0:32], lhsT=hT[:, kc, t0:t0 + C], rhs=wba[:, kc, :], start=(kc == 0), stop=(kc == KC - 1)), reads=["wba"], writes=psk(pb))
        S.act(lambda e, pb=pb, C=C, tl=tl: e.activation(out=beta_all[0:C, tl, :], in_=ps[0:C, pb, 0:16], func=AF.Sigmoid), reads=psk(pb), writes=[("beta", tl)])
        S.dve(lambda e, pb=pb, C=C, tl=tl: e.tensor_tensor(out=g_all[0:C, tl, :], in0=ps[0:C, pb, 16:32], in1=gv[0:C, 16:32], op=ALU.add), reads=psk(pb) + ["gv"], writes=[("g", tl)])
        S.act(lambda e, C=C, tl=tl: e.activation(out=g_all[0:C, tl, :], in_=g_all[0:C, tl, :], func=AF.Exp), reads=[("g", tl)], writes=[("g", tl)])
        S.act(lambda e, C=C, tl=tl: e.activation(out=g_all[0:C, tl, :], in_=g_all[0:C, tl, :], func=AF.Ln, bias=1.0), reads=[("g", tl)], writes=[("g", tl)])
        S.dve(lambda e, C=C, tl=tl: e.tensor_tensor(out=g_all[0:C, tl, :], in0=g_all[0:C, tl, :], in1=negA[0:C, :], op=ALU.mult), reads=[("g", tl), "negA"], writes=[("g", tl)])
        pg = ps_next()
        S.pe(lambda e, pg=pg, C=C, tl=tl, isS=isS: e.matmul(ps[0:C, pg, 0:16], lhsT=TRIU[isS][0:C, 0:C], rhs=g_all[0:C, tl, :], start=True, stop=True), reads=[("g", tl), "gm"], writes=psk(pg))
        S.pe(lambda e, pg=pg, C=C, tl=tl, isS=isS: e.matmul(ps[0:C, pg, 16:32], lhsT=SU[isS][0:C, 0:C], rhs=g_all[0:C, tl, :], start=True, stop=True), reads=[("g", tl), "gm"], writes=psk(pg))
        S.act(lambda e, pg=pg, C=C, tl=tl: e.activation(out=negeG_all[0:C, tl, :], in_=ps[0:C, pg, 0:16], func=AF.Exp), reads=psk(pg), writes=[("negeG", tl)])
        S.dve(lambda e, C=C, tl=tl: e.tensor_scalar(out=negeG_all[0:C, tl, :], in0=negeG_all[0:C, tl, :], scalar1=-1.0, scalar2=None, op0=ALU.mult), reads=[("negeG", tl)], writes=[("negeG", tl)])
        S.act(lambda e, pg=pg, C=C, tl=tl: e.activation(out=kdec_all[0:C, tl, :], in_=ps[0:C, pg, 16:32], func=AF.Exp), reads=psk(pg), writes=[("kdec", tl)])
    S.barrier()
    K.A.off = mark
    gbase = 48 + 16

    import os
    for hk in range(int(os.environ.get("GDN_HK0", 0)), int(os.environ.get("GDN_NHK", GDN_HK))):
        hv0 = 2 * hk
        wq, wqk = K.load_w_bf16(w_in, KC, 128, col0=hk * 128, slot=0, off=0, key=("w", 0, "q"))
        wk, wkk = K.load_w_bf16(w_in, KC, 128, col0=1024 + hk * 128, slot=0, off=1024, key=("w", 0, "k"))
        wv, wvk = K.load_w_bf16(w_in, KC, 256, col0=2048 + hk * 256, slot=1, off=0, key=("w", 1, "v"))
        wz, wzk = K.load_w_bf16(w_in, KC, 256, col0=4096 + hk * 256, slot=1, off=2048, key=("w", 1, "z"))
        wo, wok = K.load_w_bf16(w_out, 2, 1024, row0=hk * 256, slot=2)
        cids = [hk, 8 + hk, 16 + 2 * hk, 17 + 2 * hk]
        CUT = os.environ.get('GDN_CUT', '')
        if hk >= 1 and 'D' in CUT:
            S.barrier(); continue
        gst = a_f32(512, parts=48)
        K.A.off = mark
        if not (hk >= 1 and 'H' in CUT):
            for ci, cid in enumerate(cids):
                S.dma("sp", "gst", lambda e, ci=ci, cid=cid, gst=gst: e.dma_start(out=gst[:, ci * 128:(ci + 1) * 128], in_=K.s_gconv[:, cid * 128:(cid + 1) * 128]), writes=["gst"])
            bq = ps_next()
            for ci in range(4):
                S.pe(lambda e, ci=ci, bq=bq, gst=gst: e.transpose(ps[:, bq, ci * 48:(ci + 1) * 48], gst[:, ci * 128:(ci + 1) * 128], ident_f[0:48, 0:48]), reads=["gst", "ident_f"], writes=psk(bq))
            S.dve(lambda e, bq=bq: e.tensor_copy(out=gsT.rearrange("p a b -> p (a b)"), in_=ps[:, bq, 0:192]), reads=psk(bq), writes=["gsT"])
        S.dve(lambda e: e.memset(cch, 0.0), writes=["cch"])
        if hk >= 1 and 'E' in CUT:
            S.barrier(); continue
        for bi, (t0, nt) in enumerate(K.ALLBLOCKS):
            isS = t0 >= TP
            C = 64 if isS else 128
            lastblk = (t0 + nt == TP)
            if isS:
                S.barrier()
            for ci, (wt, wkey, col) in enumerate([(wq, wqk, 0), (wk, wkk, 0), (wv, wvk, 0), (wv, wvk, 128)]):
                pp = ps_next()
                for kc in range(KC):
                    S.pe(lambda e, pp=pp, kc=kc, wt=wt, col=col, t0=t0, nt=nt: e.matmul(ps[:, pp, 0:nt], lhsT=wt[:, kc, col:col + 128], rhs=hT[:, kc, t0:t0 + nt], start=(kc == 0), stop=(kc == KC - 1)), reads=[wkey], writes=psk(pp))
                cid = cids[ci]
                wc = [wcT[:, cid, i:i + 1] for i in range(4)]
                accb = acc if ci % 2 == 0 else acc2
                ak = ("acc", ci % 2)
                dst = accb[:, 0:nt] if ci < 2 else vs[:, ci - 2, 0:nt]
                if not isS:
                    S.dve(lambda e, ci=ci: e.tensor_copy(out=Gx[:, 0:3], in_=cch[:, ci, :]), reads=["cch"], writes=["Gx"])
                    S.act(lambda e, pp=pp, nt=nt: e.copy(out=Gx[:, 3:3 + nt], in_=ps[:, pp, 0:nt]), reads=psk(pp), writes=["Gx"])
                    S.dve(lambda e, ci=ci, nt=nt: e.tensor_copy(out=cch[:, ci, :], in_=Gx[:, nt:nt + 3]), reads=["Gx"], writes=["cch"])
                    if lastblk:
                        S.dve(lambda e, ci=ci, nt=nt: e.tensor_copy(out=tailP[:, ci, :], in_=Gx[:, nt:nt + 3]), reads=["Gx"], writes=["tailP"])
                    x3 = [Gx[:, i:i + nt] for i in range(4)]
                    a_ = accb[:, 0:nt]
                    d_ = dst
                    gk = ["Gx"]
                else:
                    S.dve(lambda e, ci=ci: e.tensor_copy(out=Gxs[:, :, 0:3], in_=gsT[:, ci, :].rearrange("p (s r) -> p s r", r=3)), reads=["gsT"], writes=["Gxs"])
                    S.act(lambda e, pp=pp: e.copy(out=Gxs[:, :, 3:7], in_=ps[:, pp, 0:TS].rearrange("p (s j) -> p s j", j=4)), reads=psk(pp), writes=["Gxs"])
                    S.dve(lambda e, ci=ci: e.tensor_copy(out=tailS[:, ci, :, :], in_=Gxs[:, :, 4:7]), reads=["Gxs"], writes=["tailS"])
                    x3 = [Gxs[:, :, i:i + 4] for i in range(4)]
                    a_ = accb[:, 0:TS].rearrange("p (s j) -> p s j", j=4)
                    d_ = dst.rearrange("p (s j) -> p s j", j=4)
                    gk = ["Gxs"]
                S.act(lambda e, a_=a_, x3=x3, wc=wc: e.activation(out=a_, in_=x3[3], func=AF.Identity, scale=wc[3]), reads=gk + ["wcT"], writes=[ak])
                for i in (2, 1, 0):
                    S.dve(lambda e, a_=a_, x3=x3, wc=wc, i=i: e.scalar_tensor_tensor(out=a_, in0=x3[i], scalar=wc[i], in1=a_, op0=ALU.mult, op1=ALU.add), reads=gk + [ak], writes=[ak])
                S.act(lambda e, a_=a_, d_=d_: e.activation(out=d_, in_=a_, func=AF.Silu), reads=[ak], writes=([ak, ("cv", ci)] if ci < 2 else [("cv", ci)]))
                if ci < 2:
                    S.act(lambda e, nt=nt, accb=accb: e.activation(out=sqb[:, 0:nt], in_=accb[:, 0:nt], func=AF.Square), reads=[ak], writes=["sqb"])
                    pn = ps_next()
                    S.pe(lambda e, pn=pn, nt=nt: e.matmul(ps[:, pn, 0:nt], lhsT=ones_b[:], rhs=sqb[:, 0:nt], start=True, stop=True), reads=["sqb", "ones_b"], writes=psk(pn))
                    S.act(lambda e, pn=pn, nt=nt: e.activation(out=rs[:, 0:nt], in_=ps[:, pn, 0:nt], func=AF.Sqrt, bias=EPS, scale=1.0), reads=psk(pn), writes=["rs"])
                    S.dve(lambda e, nt=nt: e.reciprocal(out=rs[:, 0:nt], in_=rs[:, 0:nt]), reads=["rs"], writes=["rs"])
                    dn = qn if ci == 0 else kn
                    dnf = qnf if ci == 0 else knf
                    sc = (128.0 ** -0.5) if ci == 0 else 1.0
                    if isS:
                        S.dve(lambda e, dnf=dnf, sc=sc, accb=accb: e.scalar_tensor_tensor(out=dnf[:, :], in0=accb[:, 0:TS], scalar=sc, in1=rs[:, 0:TS], op0=ALU.mult, op1=ALU.mult), reads=[ak, "rs"], writes=[("nf", ci)])
                        S.act(lambda e, dn=dn, dnf=dnf: e.copy(out=dn[:, 0:TS], in_=dnf[:, :]), reads=[("nf", ci)], writes=[("n", ci)])
                    else:
                        S.dve(lambda e, dn=dn, sc=sc, nt=nt, accb=accb: e.scalar_tensor_tensor(out=dn[:, 0:nt], in0=accb[:, 0:nt], scalar=sc, in1=rs[:, 0:nt], op0=ALU.mult, op1=ALU.mult), reads=[ak, "rs"], writes=[("n", ci)])
            def chunk_gen(c0, pb):
                tl = (t0 + c0) // 128
                first = (not isS) and t0 == 0 and c0 == 0
                last = (not isS) and (t0 + c0 + C == TP)
                L = 1 if isS else 6
                bvt = ps_next()
                for e_ in range(2):
                    S.pe(lambda e, e_=e_, bvt=bvt, c0=c0, C=C: e.transpose(ps[0:C, bvt, e_ * 128:(e_ + 1) * 128], vs[:, e_, c0:c0 + C], ident_f[:]), reads=[("cv", 2 + e_), "ident_f"], writes=psk(bvt))
                    yield
                S.act(lambda e, bvt=bvt, C=C: e.copy(out=vtok2[pb][0:C].rearrange("p a b -> p (a b)"), in_=ps[0:C, bvt, 0:256]), reads=psk(bvt), writes=[("vtok", pb)])
                yield
                bz = ps_next()
                for kc in range(KC):
                    S.pe(lambda e, bz=bz, kc=kc, t0=t0, c0=c0, C=C: e.matmul(ps[0:C, bz, 0:256], lhsT=hT[:, kc, t0 + c0:t0 + c0 + C], rhs=wz[:, kc, :], start=(kc == 0), stop=(kc == KC - 1)), reads=[wzk], writes=psk(bz))
                    yield
                S.act(lambda e, bz=bz, C=C: e.activation(out=szn2[pb][0:C, :], in_=ps[0:C, bz, 0:256], func=AF.Silu), reads=psk(bz), writes=[("szn", pb)])
                yield
                S.dve(lambda e, C=C: e.tensor_tensor(out=szn2[pb][0:C, :].rearrange("p (a b) -> p a b", a=2), in0=szn2[pb][0:C, :].rearrange("p (a b) -> p a b", a=2), in1=gnw[0:C, :].unsqueeze(1).to_broadcast([C, 2, 128]), op=ALU.mult), reads=[("szn", pb), "gnw"], writes=[("szn", pb)])
                yield
                bkt = ps_next()
                pkb = ps[:, bkt, :].bitcast(BF16)
                S.pe(lambda e, pkb=pkb, c0=c0, C=C: e.transpose(pkb[0:C, 0:128], kn[:, c0:c0 + C], ident_b[:]), reads=[("n", 1), "ident_b"], writes=psk(bkt))
                yield
                S.act(lambda e, pkb=pkb, C=C: e.copy(out=ktok[0:C, :], in_=pkb[0:C, 0:128]), reads=psk(bkt), writes=["ktok"])
                yield
                bkk = ps_next()
                S.pe(lambda e, bkk=bkk, c0=c0, C=C: e.matmul(ps[0:C, bkk, 0:C], lhsT=kn[:, c0:c0 + C], rhs=kn[:, c0:c0 + C], start=True, stop=True), reads=[("n", 1)], writes=psk(bkk))
                yield
                S.pe(lambda e, bkk=bkk, c0=c0, C=C: e.matmul(ps[0:C, bkk, 128:128 + C], lhsT=kn[:, c0:c0 + C], rhs=qn[:, c0:c0 + C], start=True, stop=True), reads=[("n", 0), ("n", 1)], writes=psk(bkk))
                yield
                bd = ps_next()
                bd2 = ps_next()
                for e_ in range(2):
                    hv = hv0 + e_
                    S.dve(lambda e, e_=e_, hv=hv, C=C, tl=tl, isS=isS: e.tensor_scalar(out=Bm[0:C, e_, 0:C], in0=TRIU[isS][0:C, 0:C], scalar1=g_all[0:C, tl, hv:hv + 1], scalar2=None, op0=ALU.mult), reads=["gm"], writes=[("Bm", e_)])
                    yield
                    S.pe(lambda e, e_=e_, bd=bd, C=C, isS=isS: e.matmul(ps[0:C, bd, e_ * 128:e_ * 128 + C], lhsT=SU[isS][0:C, 0:C], rhs=Bm[0:C, e_, 0:C], start=True, stop=True), reads=[("Bm", e_), "gm"], writes=psk(bd))
                    yield
                    S.pe(lambda e, e_=e_, bd2=bd2, C=C: e.matmul(ps[:, bd2, e_ * 128:e_ * 128 + C], lhsT=ones128[0:C, :], rhs=Bm[0:C, e_, 0:C], start=True, stop=True), reads=[("Bm", e_), "ones128"], writes=psk(bd2))
                    yield
                S.act(lambda e, bd=bd, C=C: e.activation(out=E[0:C, :, 0:C], in_=ps[0:C, bd, 0:256].rearrange("p (a b) -> p a b", a=2)[:, :, 0:C], func=AF.Exp), reads=psk(bd), writes=["E"])
                yield
                S.act(lambda e, bd2=bd2, C=C: e.activation(out=E2[:, :, 0:C], in_=ps[:, bd2, 0:256].rearrange("p (a b) -> p a b", a=2)[:, :, 0:C], func=AF.Exp), reads=psk(bd2), writes=["E2"])
                yield
                S.dve(lambda e, C=C, isS=isS: e.tensor_tensor(out=DTm[0:C, :, 0:C], in0=E[0:C, :, 0:C], in1=INCL[isS][0:C, 0:C].unsqueeze(1).to_broadcast([C, 2, C]), op=ALU.mult), reads=["E", "gm"], writes=["DTm"])
                yield
                S.dve(lambda e, C=C, isS=isS: e.tensor_tensor(out=E[0:C, :, 0:C], in0=E[0:C, :, 0:C], in1=STRICT[isS][0:C, 0:C].unsqueeze(1).to_broadcast([C, 2, C]), op=ALU.mult), reads=["E", "gm", "DTm"], writes=["E"])
                yield
                for e_ in range(2):
                    hv = hv0 + e_
                    S.dve(lambda e, e_=e_, hv=hv, bkk=bkk, C=C, tl=tl: e.scalar_tensor_tensor(out=U[0:C, e_, 0:C], in0=ps[0:C, bkk, 0:C], scalar=beta_all[0:C, tl, hv:hv + 1], in1=E[0:C, e_, 0:C], op0=ALU.mult, op1=ALU.mult), reads=psk(bkk) + ["E"], writes=[("U", e_)])
                    yield
                    S.dve(lambda e, e_=e_, bkk=bkk, C=C: e.tensor_tensor(out=attnT2[pb][0:C, e_, 0:C], in0=ps[0:C, bkk, 128:128 + C], in1=DTm[0:C, e_, 0:C], op=ALU.mult), reads=psk(bkk) + ["DTm"], writes=[("attnT", e_, pb)])
                    yield
                    if isS:
                        pass
                    else:
                        S.pool(lambda e, e_=e_, c0=c0, C=C: e.tensor_tensor(out=qg2[pb][:, e_, 0:C], in0=qn[:, c0:c0 + C], in1=E2[:, e_, 0:C], op=ALU.mult), reads=[("n", 0), "E2"], writes=[("qg", e_, pb)])
                        yield
                    S.dve(lambda e, e_=e_, hv=hv, C=C, tl=tl: e.tensor_scalar(out=kd2[pb][0:C, e_, :], in0=ktok[0:C, :], scalar1=kdec_all[0:C, tl, hv:hv + 1], scalar2=None, op0=ALU.mult), reads=["ktok"], writes=[("kd", e_, pb)])
                    yield
                but = ps_next()
                pub = ps[:, but, :].bitcast(BF16)
                for e_ in range(2):
                    S.pe(lambda e, e_=e_, pub=pub, C=C: e.transpose(pub[0:C, e_ * 128:e_ * 128 + C], U[0:C, e_, 0:C], ident_b[0:C, 0:C]), reads=[("U", e_), "ident_b"], writes=psk(but))
                    yield
                S.act(lambda e, pub=pub, C=C: e.copy(out=UT[0:C, :, 0:C], in_=pub[0:C, 0:256].rearrange("p (a b) -> p a b", a=2)[:, :, 0:C]), reads=psk(but), writes=["UT"])
                yield
                if isS:
                    S.dve(lambda e, C=C: e.tensor_tensor(out=Nb[0][0:C, :, 0:C], in0=ident_f[0:C, 0:C].unsqueeze(1).to_broadcast([C, 2, C]), in1=U[0:C, :, 0:C], op=ALU.subtract), reads=[("U", 0), ("U", 1), "ident_f"], writes=[("N", 0)])
                    yield
                    Pprev, PTprev, Pk_, PTk_ = U, UT, ["U0", "U1"], ["UT"]
                    Pkeys_prev = [("U", 0), ("U", 1)]
                    PTkeys_prev = ["UT"]
                    ni = 0
                    for lv in range(1, L + 1):
                        pi = lv % 2
                        need_P = lv < L
                        if need_P:
                            b1 = ps_next()
                            for e_ in range(2):
                                S.pe(lambda e, e_=e_, b1=b1, C=C, PTprev=PTprev, Pprev=Pprev: e.matmul(ps[0:C, b1, e_ * 128:e_ * 128 + C], lhsT=PTprev[0:C, e_, 0:C], rhs=Pprev[0:C, e_, 0:C], start=True, stop=True), reads=Pkeys_prev + PTkeys_prev, writes=psk(b1))
                                yield
                            S.act(lambda e, b1=b1, C=C, pi=pi: e.copy(out=Pb[pi][0:C, :, 0:C], in_=ps[0:C, b1, 0:256].rearrange("p (a b) -> p a b", a=2)[:, :, 0:C]), reads=psk(b1), writes=[("P", pi)])
                            yield
                        b2 = ps_next()
                        for e_ in range(2):
                            S.pe(lambda e, e_=e_, b2=b2, C=C, PTprev=PTprev, Pprev=Pprev: e.matmul(ps[0:C, b2, e_ * 128:e_ * 128 + C], lhsT=Pprev[0:C, e_, 0:C], rhs=PTprev[0:C, e_, 0:C], start=True, stop=True), reads=Pkeys_prev + PTkeys_prev, writes=psk(b2))
                            yield
                        S.act(lambda e, b2=b2, C=C, pi=pi: e.copy(out=PTb[pi][0:C, :, 0:C], in_=ps[0:C, b2, 0:256].rearrange("p (a b) -> p a b", a=2)[:, :, 0:C]), reads=psk(b2), writes=[("PT", pi)])
                        yield
                        b3 = ps_next()
                        for e_ in range(2):
                            S.pe(lambda e, e_=e_, b3=b3, C=C, pi=pi, ni=ni: e.matmul(ps[0:C, b3, e_ * 128:e_ * 128 + C], lhsT=PTb[pi][0:C, e_, 0:C], rhs=Nb[ni][0:C, e_, 0:C], start=True, stop=True), reads=[("PT", pi), ("N", ni)], writes=psk(b3))
                            yield
                        S.dve(lambda e, b3=b3, C=C, ni=ni: e.tensor_tensor(out=Nb[1 - ni][0:C, :, 0:C], in0=ps[0:C, b3, 0:256].rearrange("p (a b) -> p a b", a=2)[:, :, 0:C], in1=Nb[ni][0:C, :, 0:C], op=ALU.add), reads=psk(b3) + [("N", ni)], writes=[("N", 1 - ni)])
                        yield
                        ni = 1 - ni
                        Pprev, PTprev = Pb[pi], PTb[pi]
                        Pkeys_prev = [("P", pi)]
                        PTkeys_prev = [("PT", pi)]

                else:
                    def _ev2(b, C=C):
                        return ps[0:C, b, 0:256].rearrange("p (a b) -> p a b", a=2)[:, :, 0:C]
                    idb = ident_f[0:C, 0:C].unsqueeze(1).to_broadcast([C, 2, C])
                    S.dve(lambda e: e.tensor_tensor(out=Pb[0][:, :, :], in0=U[:, :, :], in1=bm[:, 0, :].unsqueeze(1).to_broadcast([128, 2, 128]), op=ALU.mult), reads=[("U", 0), ("U", 1), "bm"], writes=[("P", 0)])
                    yield
                    S.dve(lambda e: e.tensor_tensor(out=PTb[0][:, :, :], in0=UT[:, :, :], in1=bm[:, 0, :].unsqueeze(1).to_broadcast([128, 2, 128]), op=ALU.mult), reads=["UT", "bm"], writes=[("PT", 0)])
                    yield
                    S.dve(lambda e, idb=idb: e.tensor_tensor(out=Nb[0][:, :, :], in0=idb, in1=Pb[0][:, :, :], op=ALU.subtract), reads=[("P", 0), "ident_f"], writes=[("N", 0)])
                    yield
                    S.dve(lambda e, idb=idb: e.tensor_tensor(out=NTb[0][:, :, :], in0=idb, in1=PTb[0][:, :, :], op=ALU.subtract), reads=[("PT", 0), "ident_f"], writes=[("NT", 0)])
                    yield
                    for mi in range(3):
                        S.pool(lambda e, mi=mi: e.tensor_tensor(out=UoM[mi][:, :, :], in0=U[:, :, :], in1=bm[:, mi + 1, :].unsqueeze(1).to_broadcast([128, 2, 128]), op=ALU.mult), reads=[("U", 0), ("U", 1), "bm"], writes=[("UoM", mi)])
                        yield
                        S.pool(lambda e, mi=mi: e.tensor_tensor(out=UoTM[mi][:, :, :], in0=UT[:, :, :], in1=bm[:, mi + 1, :].unsqueeze(1).to_broadcast([128, 2, 128]), op=ALU.mult), reads=["UT", "bm"], writes=[("UoTM", mi)])
                        yield
                    ni = 0
                    pprev = 0
                    for lv in range(1, 4):
                        pi = lv % 2
                        b1 = ps_next(); b2 = ps_next(); b3 = ps_next(); b4 = ps_next()
                        for e_ in range(2):
                            S.pe(lambda e, e_=e_, b1=b1, pprev=pprev: e.matmul(ps[:, b1, e_ * 128:(e_ + 1) * 128], lhsT=PTb[pprev][:, e_, :], rhs=Pb[pprev][:, e_, :], start=True, stop=True), reads=[("P", pprev), ("PT", pprev)], writes=psk(b1))
                            yield
                        for e_ in range(2):
                            S.pe(lambda e, e_=e_, b2=b2, pprev=pprev: e.matmul(ps[:, b2, e_ * 128:(e_ + 1) * 128], lhsT=Pb[pprev][:, e_, :], rhs=PTb[pprev][:, e_, :], start=True, stop=True), reads=[("P", pprev), ("PT", pprev)], writes=psk(b2))
                            yield
                        S.act(lambda e, b1=b1, pi=pi: e.copy(out=Pb[pi][:, :, :], in_=_ev2(b1)), reads=psk(b1), writes=[("P", pi)])
                        yield
                        S.act(lambda e, b2=b2, pi=pi: e.copy(out=PTb[pi][:, :, :], in_=_ev2(b2)), reads=psk(b2), writes=[("PT", pi)])
                        yield
                        for e_ in range(2):
                            S.pe(lambda e, e_=e_, b3=b3, pi=pi, ni=ni: e.matmul(ps[:, b3, e_ * 128:(e_ + 1) * 128], lhsT=PTb[pi][:, e_, :], rhs=Nb[ni][:, e_, :], start=True, stop=True), reads=[("PT", pi), ("N", ni)], writes=psk(b3))
                            yield
                        for e_ in range(2):
                            S.pe(lambda e, e_=e_, b4=b4, pi=pi, ni=ni: e.matmul(ps[:, b4, e_ * 128:(e_ + 1) * 128], lhsT=Pb[pi][:, e_, :], rhs=NTb[ni][:, e_, :], start=True, stop=True), reads=[("P", pi), ("NT", ni)], writes=psk(b4))
                            yield
                        S.dve(lambda e, b3=b3, ni=ni: e.tensor_tensor(out=Nb[1 - ni][:, :, :], in0=_ev2(b3), in1=Nb[ni][:, :, :], op=ALU.add), reads=psk(b3) + [("N", ni)], writes=[("N", 1 - ni)])
                        yield
                        S.dve(lambda e, b4=b4, ni=ni: e.tensor_tensor(out=NTb[1 - ni][:, :, :], in0=_ev2(b4), in1=NTb[ni][:, :, :], op=ALU.add), reads=psk(b4) + [("NT", ni)], writes=[("NT", 1 - ni)])
                        yield
                        ni = 1 - ni
                        pprev = pi
                    for mi in range(3):
                        lastm = (mi == 2)
                        b1 = ps_next()
                        for e_ in range(2):
                            S.pe(lambda e, e_=e_, b1=b1, ni=ni, mi=mi: e.matmul(ps[:, b1, e_ * 128:(e_ + 1) * 128], lhsT=UoTM[mi][:, e_, :], rhs=Nb[ni][:, e_, :], start=True, stop=True), reads=[("UoTM", mi), ("N", ni)], writes=psk(b1))
                            yield
                        S.act(lambda e, b1=b1: e.copy(out=Pb[1][:, :, :], in_=_ev2(b1)), reads=psk(b1), writes=[("P", 1)])
                        yield
                        if not lastm:
                            b3 = ps_next()
                            for e_ in range(2):
                                S.pe(lambda e, e_=e_, b3=b3, ni=ni, mi=mi: e.matmul(ps[:, b3, e_ * 128:(e_ + 1) * 128], lhsT=UoM[mi][:, e_, :], rhs=NTb[ni][:, e_, :], start=True, stop=True), reads=[("UoM", mi), ("NT", ni)], writes=psk(b3))
                                yield
                            S.act(lambda e, b3=b3: e.copy(out=PTb[1][:, :, :], in_=_ev2(b3)), reads=psk(b3), writes=[("PT", 1)])
                            yield
                        b2 = ps_next()
                        for e_ in range(2):
                            S.pe(lambda e, e_=e_, b2=b2, ni=ni: e.matmul(ps[:, b2, e_ * 128:(e_ + 1) * 128], lhsT=NTb[ni][:, e_, :], rhs=Pb[1][:, e_, :], start=True, stop=True), reads=[("NT", ni), ("P", 1)], writes=psk(b2))
                            yield
                        S.dve(lambda e, b2=b2, ni=ni, lastm=lastm: e.tensor_tensor(out=(Nfin[pb] if lastm else Nb[1 - ni])[:, :, :], in0=Nb[ni][:, :, :], in1=_ev2(b2), op=ALU.subtract), reads=psk(b2) + [("N", ni)], writes=[(("Nfin", pb) if lastm else ("N", 1 - ni))])
                        yield
                        if not lastm:
                            b4 = ps_next()
                            for e_ in range(2):
                                S.pe(lambda e, e_=e_, b4=b4, ni=ni: e.matmul(ps[:, b4, e_ * 128:(e_ + 1) * 128], lhsT=Nb[ni][:, e_, :], rhs=PTb[1][:, e_, :], start=True, stop=True), reads=[("N", ni), ("PT", 1)], writes=psk(b4))
                                yield
                            S.dve(lambda e, b4=b4, ni=ni: e.tensor_tensor(out=NTb[1 - ni][:, :, :], in0=NTb[ni][:, :, :], in1=_ev2(b4), op=ALU.subtract), reads=psk(b4) + [("NT", ni)], writes=[("NT", 1 - ni)])
                            yield
                        ni = 1 - ni
                if isS:
                    S.pool(lambda e, ni=ni, C=C: e.tensor_copy(out=Nfin[pb][0:C, :, 0:C], in_=Nb[ni][0:C, :, 0:C]), reads=[("N", ni)], writes=[("Nfin", pb)])
                    yield
                Nf = Nfin[pb]
                Nkey = ("Nfin", pb)
                if not isS:
                    S.pool(lambda e, C=C: e.tensor_copy(out=eGl[pb][:, 0:2], in_=E2[:, :, C - 1:C].rearrange("p a b -> p (a b)")), reads=["E2"], writes=[("eGl", pb)])
                    yield
                yield "SPLIT"
                if not isS:
                    if first:
                        S.act(lambda e, C=C: e.copy(out=xv[0:C].rearrange("p a b -> p (a b)"), in_=vtok2[pb][0:C].rearrange("p a b -> p (a b)")), reads=[("vtok", pb)], writes=["xv"])
                        yield
                    else:
                        pk_ = ps_next()
                        S.pe(lambda e, pk_=pk_, c0=c0, C=C: e.matmul(ps[0:C, pk_, 0:256], lhsT=kn[:, c0:c0 + C], rhs=Sbf[:].rearrange("p a b -> p (a b)"), start=True, stop=True), reads=[("n", 1), "Sbf"], writes=psk(pk_))
                        yield
                        for e_ in range(2):
                            hv = hv0 + e_
                            S.dve(lambda e, e_=e_, hv=hv, pk_=pk_, C=C, tl=tl: e.scalar_tensor_tensor(out=xv[0:C, e_, :], in0=ps[0:C, pk_, e_ * 128:(e_ + 1) * 128], scalar=negeG_all[0:C, tl, hv:hv + 1], in1=vtok2[pb][0:C, e_, :], op0=ALU.mult, op1=ALU.add), reads=psk(pk_) + [("vtok", pb)], writes=["xv"])
                            yield
                else:
                    S.dve(lambda e: e.tensor_tensor(out=X[:, :, :], in0=knf[:, :].unsqueeze(1).to_broadcast([128, NSEQ, TS]), in1=segmask[:, :, :], op=ALU.mult), reads=[("nf", 1), "segmask"], writes=["X"])
                    yield
                    pks = [ps_next(), ps_next()]
                    K.P.reserved = set(pks)
                    def _ldA(n):
                        k_ = n % 8
                        S.dma("sp", "gsin%d" % k_, lambda e, k_=k_, s2=n % NSEQ, hv2=hv0 + n // NSEQ: e.dma_start(out=SinR[k_], in_=K.s_gdn[s2, hv2]), writes=[("SinR", k_)])
                    for n_ in range(RDEPTH):
                        _ldA(n_)
                        yield
                    for e_ in range(2):
                        hv = hv0 + e_
                        for s_ in range(NSEQ):
                            n_ = e_ * NSEQ + s_
                            k_ = n_ % 8
                            S.pe(lambda e, e_=e_, s_=s_, k_=k_, pks=pks: e.matmul(ps[0:TS, pks[e_], 0:128], lhsT=X[:, s_, :], rhs=SinR[k_], start=(s_ == 0), stop=(s_ == NSEQ - 1)), reads=["X", ("SinR", k_)], writes=psk(pks[e_]))
                            yield
                            if n_ + RDEPTH < NLD:
                                _ldA(n_ + RDEPTH)
                                yield
                        S.dve(lambda e, e_=e_, hv=hv, tl=tl, pks=pks: e.scalar_tensor_tensor(out=xv[0:TS, e_, :], in0=ps[0:TS, pks[e_], 0:128], scalar=negeG_all[0:TS, tl, hv:hv + 1], in1=vtok2[pb][0:TS, e_, :], op0=ALU.mult, op1=ALU.add), reads=psk(pks[e_]) + [("vtok", pb)], writes=["xv"])
                        yield
                    K.P.reserved = set()
                pv = ps_next()
                for e_ in range(2):
                    S.pe(lambda e, e_=e_, pv=pv, C=C, Nf=Nf: e.matmul(ps[0:C, pv, e_ * 128:(e_ + 1) * 128], lhsT=Nf[0:C, e_, 0:C], rhs=xv[0:C, e_, :], start=True, stop=True), reads=[Nkey, "xv"], writes=psk(pv))
                    yield
                for e_ in range(2):
                    hv = hv0 + e_
                    S.act(lambda e, e_=e_, hv=hv, pv=pv, C=C, tl=tl: e.activation(out=vnew[0:C, e_, :], in_=ps[0:C, pv, e_ * 128:(e_ + 1) * 128], func=AF.Identity, scale=beta_all[0:C, tl, hv:hv + 1]), reads=psk(pv), writes=[("vnew", e_)])
                    yield
                if not isS:
                    po = ps_next()
                    po_aps = [ps[0:C, po, 0:128], ps[0:C, po, 128:256]]
                    po_keys = [psk(po), psk(po)]
                    for e_ in range(2):
                        if not first:
                            S.pe(lambda e, e_=e_, C=C, po_aps=po_aps: e.matmul(po_aps[e_], lhsT=qg2[pb][:, e_, 0:C], rhs=Sbf[:, e_, :], start=True, stop=False), reads=[("qg", e_, pb), "Sbf"], writes=po_keys[e_])
                            yield
                        S.pe(lambda e, e_=e_, C=C, po_aps=po_aps, first=first: e.matmul(po_aps[e_], lhsT=attnT2[pb][0:C, e_, 0:C], rhs=vnew[0:C, e_, :], start=first, stop=True), reads=[("attnT", e_, pb), ("vnew", e_)], writes=po_keys[e_])
                        yield
                    pS_ = ps_next()
                    for e_ in range(2):
                        S.pe(lambda e, e_=e_, pS_=pS_, C=C: e.matmul(ps[:, pS_, e_ * 128:(e_ + 1) * 128], lhsT=kd2[pb][0:C, e_, :], rhs=vnew[0:C, e_, :], start=True, stop=True), reads=[("kd", e_, pb), ("vnew", e_)], writes=psk(pS_))
                        yield
                    for e_ in range(2):
                        hv = hv0 + e_
                        if first:
                            S.dve(lambda e, e_=e_, pS_=pS_: e.tensor_copy(out=S32[:, e_, :], in_=ps[:, pS_, e_ * 128:(e_ + 1) * 128]), reads=psk(pS_), writes=[("S32", e_)])
                            yield
                        else:
                            S.dve(lambda e, e_=e_, pS_=pS_, C=C: e.scalar_tensor_tensor(out=S32[:, e_, :], in0=S32[:, e_, :], scalar=eGl[pb][:, e_:e_ + 1], in1=ps[:, pS_, e_ * 128:(e_ + 1) * 128], op0=ALU.mult, op1=ALU.add), reads=psk(pS_) + [("S32", e_), ("eGl", pb)], writes=[("S32", e_)])
                            yield
                        if last:
                            S.dma("sp", "gpo", lambda e, e_=e_, hv=hv: e.dma_start(out=K.gdn_p[hv], in_=S32[:, e_, :]), reads=[("S32", e_)])
                            yield
                    if not last:
                        S.act(lambda e: e.copy(out=Sbf[:].rearrange("p a b -> p (a b)"), in_=S32[:].rearrange("p a b -> p (a b)")), reads=[("S32", 0), ("S32", 1)], writes=["Sbf"])
                        yield
                else:
                    pos = [ps_next(), ps_next()]
                    K.P.reserved = set(pos)
                    po_aps = [ps[0:TS, pos[0], 0:128], ps[0:TS, pos[1], 0:128]]
                    po_keys = [psk(pos[0]), psk(pos[1])]
                    def _ldB(n):
                        k_ = n % 8
                        S.dma("sp", "gsin%d" % k_, lambda e, k_=k_, s2=n % NSEQ, hv2=hv0 + n // NSEQ: e.dma_start(out=SinR[k_], in_=K.s_gdn[s2, hv2]), writes=[("SinR", k_)])
                    for n_ in range(RDEPTH):
                        _ldB(n_)
                        yield
                    for e_ in range(2):
                        hv = hv0 + e_
                        S.pe(lambda e, e_=e_, po_aps=po_aps: e.matmul(po_aps[e_], lhsT=attnT2[pb][0:TS, e_, 0:TS], rhs=vnew[0:TS, e_, :], start=True, stop=False), reads=[("attnT", e_, pb), ("vnew", e_)], writes=po_keys[e_])
                        yield
                        S.pool(lambda e, e_=e_: e.tensor_tensor(out=qgf[:, :], in0=qnf[:, :], in1=E2[:, e_, 0:TS], op=ALU.mult), reads=[("nf", 0), "E2"], writes=["qgf"])
                        yield
                        S.dve(lambda e: e.tensor_tensor(out=X[:, :, :], in0=qgf[:, :].unsqueeze(1).to_broadcast([128, NSEQ, TS]), in1=segmask[:, :, :], op=ALU.mult), reads=["qgf", "segmask"], writes=["X"])
                        yield
                        for s_ in range(NSEQ):
                            n_ = e_ * NSEQ + s_
                            k_ = n_ % 8
                            j_ = n_ % 3
                            S.pe(lambda e, e_=e_, s_=s_, k_=k_, po_aps=po_aps: e.matmul(po_aps[e_], lhsT=X[:, s_, :], rhs=SinR[k_], start=False, stop=(s_ == NSEQ - 1)), reads=["X", ("SinR", k_)], writes=po_keys[e_])
                            yield
                            S.dve(lambda e, e_=e_, s_=s_: e.tensor_scalar(out=kdm[0:TS, :], in0=kd2[pb][0:TS, e_, :], scalar1=segcol[0:TS, s_:s_ + 1], scalar2=None, op0=ALU.mult), reads=[("kd", e_, pb), "segcol"], writes=["kdm"])
                            yield
                            pS_ = ps_next()
                            S.pe(lambda e, e_=e_, pS_=pS_: e.matmul(ps[:, pS_, 0:128], lhsT=kdm[0:TS, :], rhs=vnew[0:TS, e_, :], start=True, stop=True), reads=["kdm", ("vnew", e_)], writes=psk(pS_))
                            yield
                            S.dve(lambda e, e_=e_, s_=s_, k_=k_, j_=j_, pS_=pS_: e.scalar_tensor_tensor(out=SoutR[j_], in0=SinR[k_], scalar=E2[:, e_, 4 * s_ + 3:4 * s_ + 4], in1=ps[:, pS_, 0:128], op0=ALU.mult, op1=ALU.add), reads=psk(pS_) + [("SinR", k_), "E2"], writes=[("SoutR", j_)])
                            yield
                            if n_ + RDEPTH < NLD:
                                _ldB(n_ + RDEPTH)
                                yield
                            S.dma("sp", "gsout%d" % j_, lambda e, s_=s_, hv=hv, j_=j_: e.dma_start(out=K.gdn_s[s_, hv], in_=SoutR[j_]), reads=[("SoutR", j_)])
                            yield
                    K.P.reserved = set()
                for e_ in range(2):
                    _f = lambda e, e_=e_, C=C, po_aps=po_aps: e.activation(out=acc[0:C, e_ * 128:(e_ + 1) * 128], in_=po_aps[e_], func=AF.Square, accum_out=ss[0:C, e_:e_ + 1])
                    _f._multi = True
                    S.act(_f, reads=po_keys[e_], writes=[("acc", 0), ("ss", e_)])
                    yield
                S.dve(lambda e, C=C: e.tensor_scalar(out=ss[0:C, 2:4], in0=ss[0:C, 0:2], scalar1=1.0 / 128, scalar2=EPS, op0=ALU.mult, op1=ALU.add), reads=[("ss", 0), ("ss", 1)], writes=["ss2"])
                yield
                S.pool(lambda e, C=C: e.tensor_tensor(out=ss[0:C, 2:4], in0=ss[0:C, 2:4], in1=ss[0:C, 4:6], op=ALU.pow), reads=["ss2", "mhalf"], writes=["ss2"])
                yield
                for e_ in range(2):
                    S.dve(lambda e, e_=e_, C=C, po_aps=po_aps: e.scalar_tensor_tensor(out=og[0:C, e_ * 128:(e_ + 1) * 128], in0=po_aps[e_], scalar=ss[0:C, 2 + e_:3 + e_], in1=szn2[pb][0:C, e_ * 128:(e_ + 1) * 128], op0=ALU.mult, op1=ALU.mult), reads=po_keys[e_] + ["ss2", ("szn", pb)], writes=[("og", e_)])
                    yield
                bt = ps_next()
                ptb = ps[:, bt, :].bitcast(BF16)
                for e_ in range(2):
                    S.pe(lambda e, e_=e_, ptb=ptb, C=C: e.transpose(ptb[:, e_ * C:(e_ + 1) * C], og[0:C, e_ * 128:(e_ + 1) * 128], ident_b[0:C, 0:C]), reads=[("og", e_), "ident_b"], writes=psk(bt))
                    yield
                S.act(lambda e, ptb=ptb, C=C, c0=c0: e.copy(out=ogT[:, :, c0:c0 + C], in_=ptb[:, 0:2 * C].rearrange("p (a b) -> p a b", a=2)), reads=psk(bt), writes=["ogT"])
                yield
            chunks_ = list(range(0, nt, C))
            gens = [chunk_gen(c0_, i_ % 2) for i_, c0_ in enumerate(chunks_)]
            PIPE = (not isS) and os.environ.get("GDN_PIPE", "1") == "1"

            def _adv_split(g):
                for x_ in g:
                    if x_ == "SPLIT":
                        return
            if not PIPE:
                for g in gens:
                    for _ in g:
                        pass
            else:
                _adv_split(gens[0])
                for i_ in range(len(gens)):
                    nxt = gens[i_ + 1] if i_ + 1 < len(gens) else None
                    a_done = False
                    b_done = nxt is None
                    while not (a_done and b_done):
                        if not a_done:
                            try:
                                next(gens[i_])
                            except StopIteration:
                                a_done = True
                        if not b_done:
                            try:
                                if next(nxt) == "SPLIT":
                                    b_done = True
                            except StopIteration:
                                b_done = True
            for dc8 in range(KC):
                if hk >= 1 and 'F' in CUT:
                    continue
                b = ps_next()
                for e_ in range(2):
                    S.pe(lambda e, b=b, e_=e_, dc8=dc8, nt=nt: e.matmul(ps[:, b, 0:nt], lhsT=wo[:, e_, dc8 * 128:(dc8 + 1) * 128], rhs=ogT[:, e_, 0:nt], start=(e_ == 0), stop=(e_ == 1)), reads=[wok, "ogT"], writes=psk(b))
                K.resid_update(ps[:, b, 0:nt], gbase, dc8, ("S" if isS else bi), t0, nt, psk(b))
        if hk >= 1 and 'G' in CUT:
            S.barrier(); continue
        bo1 = ps_next()
        for ci in range(4):
            S.pe(lambda e, ci=ci, bo1=bo1: e.transpose(ps[0:3, bo1, ci * 128:(ci + 1) * 128], tailP[:, ci, :], ident_f[:]), reads=["tailP", "ident_f"], writes=psk(bo1))
        S.dve(lambda e, bo1=bo1: e.tensor_copy(out=ostg[0:3, :], in_=ps[0:3, bo1, :]), reads=psk(bo1), writes=[("acc", 0)])
        for ci, cid in enumerate(cids):
            S.dma("sp", "gco", lambda e, ci=ci, cid=cid: e.dma_start(out=K.gconv_p[:, cid * 128:(cid + 1) * 128], in_=ostg[0:3, ci * 128:(ci + 1) * 128]), reads=[("acc", 0)])
        bo2 = ps_next()
        for ci in range(4):
            S.pe(lambda e, ci=ci, bo2=bo2: e.transpose(ps[0:48, bo2, ci * 128:(ci + 1) * 128], tailS[:, ci, :, :].rearrange("p s r -> p (s r)"), ident_f[:]), reads=["tailS", "ident_f"], writes=psk(bo2))
        S.dve(lambda e, bo2=bo2: e.tensor_copy(out=ostg[0:48, :], in_=ps[0:48, bo2, :]), reads=psk(bo2), writes=[("acc", 0)])
        for ci, cid in enumerate(cids):
            S.dma("sp", "gco", lambda e, ci=ci, cid=cid: e.dma_start(out=K.gconv_s[:, cid * 128:(cid + 1) * 128], in_=ostg[0:48, ci * 128:(ci + 1) * 128]), reads=[("acc", 0)])
        S.barrier()
    S.barrier()

_CACHE = {}


def make_in_maps(inp):
    f = lambda a: np.ascontiguousarray(np.asarray(a, dtype=np.float32))
    consts = host_consts()
    shared = {
        "w_ada": f(inp["w_ada"]), "b_ada": f(inp["b_ada"]),
        "w_ada_final": f(inp["w_ada_final"]), "b_ada_final": f(inp["b_ada_final"]).reshape(1, -1),
        "w_ret_in": f(inp["w_ret_in"][0]), "w_ret_out": f(inp["w_ret_out"][0]),
        "w_gdn_in": f(inp["w_gdn_in"][0]), "w_gdn_out": f(inp["w_gdn_out"][0]),
        "w_gdn_conv": f(inp["w_gdn_conv"][0]),
        "gdn_vec": f(np.concatenate([inp["gdn_a_log"][0], inp["gdn_dt_bias"][0]])).reshape(1, 32),
        "gdn_norm": f(inp["gdn_norm"]).reshape(1, 128),
        "w_ffn_up": f(inp["w_ffn_up"]), "w_ffn_down": f(inp["w_ffn_down"]),
        "ffn_vec": f(np.concatenate([np.asarray(inp["w_ffn_dw"]).reshape(6, DFF), np.asarray(inp["b_ffn_dw"]).reshape(2, DFF)], axis=0)),
    }
    shared.update(consts)
    maps = []
    for c in range(NCORES):
        sl = slice(NSEQ * c, NSEQ * (c + 1))
        m = dict(shared)
        m["xp"] = f(inp["x_prompt"][c])
        m["xs"] = f(np.asarray(inp["x_sample"][sl]).reshape(TS, D))
        m["s_ret"] = f(inp["state_ret"][0, sl])
        m["s_gdn"] = f(inp["state_gdn"][0, sl])
        m["s_gconv"] = f(np.asarray(inp["state_gdn_conv"][0, sl]).reshape(NSEQ * 3, 4096))
        m["s_fconv"] = f(np.asarray(inp["state_ffn_conv"][:, sl]).reshape(2, NSEQ * 2, DFF))
        m["vec22"] = f(np.concatenate([np.asarray(inp["c_prompt"][c:c + 1]), np.asarray(inp["c_sample"][sl]),
                                        np.asarray(inp["norm_mix"]), np.asarray(inp["norm_ffn"]), np.asarray(inp["norm_final"]).reshape(1, D)], axis=0))
        maps.append(m)
    return maps


def kernel(**inp):
    if "nc" not in _CACHE:
        _CACHE["nc"] = build_program()
    nc = _CACHE["nc"]
    maps = make_in_maps(inp)
    res = run_bass_kernel_spmd(nc, maps, core_ids=list(range(NCORES)))
    R = res.results
    cat = lambda k: np.stack([np.asarray(R[c][k]) for c in range(NCORES)], axis=0)
    y_prompt = cat("y_p")
    y_sample = cat("y_s").reshape(128, 4, D)
    ret_p = cat("ret_p")[None]
    gdn_p = cat("gdn_p")[None]
    gconv_p = cat("gconv_p")[None]
    fconv_p = np.transpose(cat("fconv_p"), (1, 0, 2, 3))
    ret_s = cat("ret_s").reshape(1, 128, RET_H, 256, 512)
    gdn_s = cat("gdn_s").reshape(1, 128, GDN_HV, 128, 128)
    gconv_s = cat("gconv_s").reshape(1, 128, 3, 4096)
    fconv_s = np.transpose(cat("fconv_s").reshape(NCORES, 2, NSEQ, 2, DFF), (1, 0, 2, 3, 4)).reshape(2, 128, 2, DFF)
    return (y_prompt.astype(np.float32), y_sample.astype(np.float32), ret_p.astype(np.float32), gdn_p.astype(np.float32),
            gconv_p.astype(np.float32), fconv_p.astype(np.float32), ret_s.astype(np.float32), gdn_s.astype(np.float32),
            gconv_s.astype(np.float32), fconv_s.astype(np.float32))
```
